# Optimizing a Trainium2 kernel written in Bass

```python
import math
import jax
import jax.numpy as jnp
from jax import lax
import numpy as np

D_MODEL = 1024
BATCH = 4
SEQ = 4096
DEPTH = 1
DEC_BATCH = 8
DEC_SEQ = 16
PAST_LEN = 2048

CHUNK = 64
N_PREV_CHUNKS = 8
HEAD_DIM = 64
A_HEADS = 8
A_WIDTH = A_HEADS * HEAD_DIM
A_MAX_REL = 64
B_HEADS = 8
B_KV_HEADS = 2
B_GROUP = B_HEADS // B_KV_HEADS
B_WIDTH = B_HEADS * HEAD_DIM
B_KV_WIDTH = B_KV_HEADS * HEAD_DIM
MIX_WIDTH = A_WIDTH + B_WIDTH
IDX_HEADS = 8
IDX_DIM = 64
TOPK_MAX = 256
N_BUCKETS = 32
T5_MAX_DIST = 128
MEM_LEN = 256
MEM_HEADS = 4
MEM_HEAD_DIM = 128
MEM_WIDTH = MEM_HEADS * MEM_HEAD_DIM
D_FF = 2816
CONV_W = 3
Q_BLOCK = 128
IN_SIZES = (A_WIDTH, A_WIDTH, A_WIDTH, B_WIDTH, B_KV_WIDTH, B_KV_WIDTH, IDX_HEADS * IDX_DIM, IDX_DIM, IDX_HEADS)
IN_WIDTH = sum(IN_SIZES)
ALPHA = (2 * DEPTH) ** 0.25
BETA = (8 * DEPTH) ** -0.25
LN_EPS = 1e-5
ATTN_SCALE = HEAD_DIM ** -0.5
NEG = -1e30

kernel_name = 'hybrid_streaming_encoder_step'


def layer_norm(x, g, b):
    xf = x.astype(jnp.float32)
    mu = jnp.mean(xf, axis=-1, keepdims=True)
    var = jnp.mean(jnp.square(xf - mu), axis=-1, keepdims=True)
    return ((xf - mu) * lax.rsqrt(var + LN_EPS) * g + b).astype(x.dtype)


def split_in(z):
    b, t = z.shape[:2]
    parts, off = [], 0
    for n in IN_SIZES:
        parts.append(z[..., off:off + n])
        off += n
    qa, ka, va, qb, kb, vb, qi, ki, wi = parts
    hd = lambda a, h, d: a.reshape(b, t, h, d)
    return (hd(qa, A_HEADS, HEAD_DIM), hd(ka, A_HEADS, HEAD_DIM), hd(va, A_HEADS, HEAD_DIM),
            hd(qb, B_HEADS, HEAD_DIM), hd(kb, B_KV_HEADS, HEAD_DIM), hd(vb, B_KV_HEADS, HEAD_DIM),
            hd(qi, IDX_HEADS, IDX_DIM), ki, wi * IDX_HEADS ** -0.5)


def clipped_rel_bias(table, rel):
    return table[jnp.clip(rel, -A_MAX_REL, A_MAX_REL) + A_MAX_REL]


def t5_bucket(rel):
    half = N_BUCKETS // 2
    max_exact = half // 2
    n = jnp.abs(rel)
    log_ratio = jnp.log(jnp.maximum(n, 1).astype(jnp.float32) / max_exact) / math.log(T5_MAX_DIST / max_exact)
    large = jnp.minimum(max_exact + (log_ratio * (half - max_exact)).astype(jnp.int32), half - 1)
    return jnp.where(rel < 0, half, 0) + jnp.where(n < max_exact, n, large)


def band_attend_prompt(q, k, v, table):
    b, t = q.shape[:2]
    n_c = t // CHUNK
    band = (N_PREV_CHUNKS + 1) * CHUNK
    pad = ((0, 0), (N_PREV_CHUNKS * CHUNK, 0), (0, 0), (0, 0))
    kc = jnp.pad(k, pad).reshape(b, n_c + N_PREV_CHUNKS, CHUNK, A_HEADS, HEAD_DIM)
    vc = jnp.pad(v, pad).reshape(b, n_c + N_PREV_CHUNKS, CHUNK, A_HEADS, HEAD_DIM)
    idx = jnp.arange(n_c)[:, None] + jnp.arange(N_PREV_CHUNKS + 1)[None, :]
    kb = kc[:, idx].reshape(b, n_c, band, A_HEADS, HEAD_DIM)
    vb = vc[:, idx].reshape(b, n_c, band, A_HEADS, HEAD_DIM)
    qc = q.reshape(b, n_c, CHUNK, A_HEADS, HEAD_DIM)
    rel = jnp.arange(CHUNK)[:, None] + N_PREV_CHUNKS * CHUNK - jnp.arange(band)[None, :]
    bias = clipped_rel_bias(table, rel).transpose(2, 0, 1)
    kpos = (jnp.arange(n_c)[:, None] - N_PREV_CHUNKS) * CHUNK + jnp.arange(band)[None, :]
    s = jnp.einsum('bcqhd,bckhd->bchqk', qc, kb).astype(jnp.float32) * ATTN_SCALE + bias
    s = jnp.where((kpos >= 0)[None, :, None, None, :], s, NEG)
    p = jax.nn.softmax(s, axis=-1).astype(v.dtype)
    return jnp.einsum('bchqk,bckhd->bcqhd', p, vb).reshape(b, t, A_WIDTH)


def band_attend_sample(q, k, v, table):
    b, s_len = q.shape[:2]
    n_keys = k.shape[1]
    rel = (n_keys - s_len + jnp.arange(s_len))[:, None] - jnp.arange(n_keys)[None, :]
    bias = clipped_rel_bias(table, rel).transpose(2, 0, 1)
    s = jnp.einsum('bqhd,bkhd->bhqk', q, k).astype(jnp.float32) * ATTN_SCALE + bias
    p = jax.nn.softmax(s, axis=-1).astype(v.dtype)
    return jnp.einsum('bhqk,bkhd->bqhd', p, v).reshape(b, s_len, A_WIDTH)


def dsa_attend(q, qi, wi, qpos, k, v, ki, admissible, top_k, t5_table):
    b, nq = q.shape[:2]
    logits = jnp.einsum('bqhe,ble->bqhl', qi, ki).astype(jnp.float32) * IDX_DIM ** -0.5
    score = jnp.einsum('bqh,bqhl->bql', wi.astype(jnp.float32), jax.nn.relu(logits))
    score = jnp.where(admissible[None], score, -jnp.inf)
    top_val, sel = lax.top_k(score, top_k)
    gather = jax.vmap(lambda rows, ids: rows[ids])
    ks, vs = gather(k, sel), gather(v, sel)
    bias = t5_table[t5_bucket(qpos[None, :, None] - sel)]
    bias = bias.reshape(b, nq, top_k, B_KV_HEADS, B_GROUP).transpose(0, 1, 3, 4, 2)
    qg = q.reshape(b, nq, B_KV_HEADS, B_GROUP, HEAD_DIM)
    s = jnp.einsum('bqgrd,bqkgd->bqgrk', qg, ks).astype(jnp.float32) * ATTN_SCALE + bias
    s = jnp.where(jnp.isfinite(top_val)[:, :, None, None, :], s, NEG)
    p = jax.nn.softmax(s, axis=-1).astype(v.dtype)
    return jnp.einsum('bqgrk,bqkgd->bqgrd', p, vs).reshape(b, nq, B_WIDTH)


def dsa_prompt(q, k, v, qi, ki, wi, t5_table):
    b, t = q.shape[:2]
    top_k = min(TOPK_MAX, t // 4)
    key_idx = jnp.arange(t)

    def block(i):
        q0 = i * Q_BLOCK
        sl = lambda a: lax.dynamic_slice_in_dim(a, q0, Q_BLOCK, axis=1)
        qpos = q0 + jnp.arange(Q_BLOCK)
        adm = key_idx[None, :] < ((qpos // CHUNK + 1) * CHUNK)[:, None]
        return dsa_attend(sl(q), sl(qi), sl(wi), qpos, k, v, ki, adm, top_k, t5_table)

    out = lax.map(block, jnp.arange(t // Q_BLOCK))
    return out.transpose(1, 0, 2, 3).reshape(b, t, B_WIDTH)


def mem_kv(mem, w_k, w_v):
    b = mem.shape[0]
    return ((mem @ w_k).reshape(b, MEM_LEN, MEM_HEADS, MEM_HEAD_DIM),
            (mem @ w_v).reshape(b, MEM_LEN, MEM_HEADS, MEM_HEAD_DIM))


def mem_attend(x, mk, mv, w_q, w_o):
    b, t = x.shape[:2]
    q = (x @ w_q).reshape(b, t, MEM_HEADS, MEM_HEAD_DIM)
    s = jnp.einsum('bthd,bmhd->bhtm', q, mk).astype(jnp.float32) * MEM_HEAD_DIM ** -0.5
    p = jax.nn.softmax(s, axis=-1).astype(mv.dtype)
    return jnp.einsum('bhtm,bmhd->bthd', p, mv).reshape(b, t, MEM_WIDTH) @ w_o


def conv_ffn(x, g_hist, w_up, w_conv, b_conv, w_down):
    t = x.shape[1]
    u, g = jnp.split(x @ w_up, 2, axis=-1)
    gp = jnp.concatenate([g_hist, g], axis=1)
    gc = b_conv + sum(w_conv[j] * gp[:, j:j + t] for j in range(CONV_W))
    h = u * jax.nn.gelu(gc)
    return h @ w_down, gp[:, t:]


def setup_inputs(seed: int = 0) -> dict:
    key = jax.random.key(seed)
    ks = iter(jax.random.split(key, 32))
    nrm = lambda shape, scale=1.0: scale * jax.random.normal(next(ks), shape, jnp.float32)
    a_cache = min(N_PREV_CHUNKS * CHUNK, PAST_LEN)
    return {
        'x_prompt': nrm((BATCH, SEQ, D_MODEL)),
        'x_sample': nrm((DEC_BATCH, DEC_SEQ, D_MODEL)),
        'cache_a_k': nrm((DEPTH, DEC_BATCH, a_cache, A_HEADS, HEAD_DIM)),
        'cache_a_v': nrm((DEPTH, DEC_BATCH, a_cache, A_HEADS, HEAD_DIM)),
        'cache_b_k': nrm((DEPTH, DEC_BATCH, PAST_LEN, B_KV_HEADS, HEAD_DIM)),
        'cache_b_v': nrm((DEPTH, DEC_BATCH, PAST_LEN, B_KV_HEADS, HEAD_DIM)),
        'cache_b_kidx': nrm((DEPTH, DEC_BATCH, PAST_LEN, IDX_DIM)),
        'cache_mem_k': nrm((DEPTH, DEC_BATCH, MEM_LEN, MEM_HEADS, MEM_HEAD_DIM)),
        'cache_mem_v': nrm((DEPTH, DEC_BATCH, MEM_LEN, MEM_HEADS, MEM_HEAD_DIM)),
        'state_ffn_conv': nrm((DEPTH, DEC_BATCH, CONV_W - 1, D_FF)),
        'mem_prompt': nrm((BATCH, MEM_LEN, D_MODEL)),
        'w_in': nrm((DEPTH, D_MODEL, IN_WIDTH), D_MODEL ** -0.5),
        'a_rel_bias': nrm((DEPTH, 2 * A_MAX_REL + 1, A_HEADS), 0.1),
        't5_bias': nrm((N_BUCKETS, B_HEADS), 0.1),
        'w_o': nrm((DEPTH, MIX_WIDTH, D_MODEL), BETA * MIX_WIDTH ** -0.5),
        'ln1_g': 1.0 + nrm((DEPTH, D_MODEL), 0.01),
        'ln1_b': nrm((DEPTH, D_MODEL), 0.01),
        'w_mq': nrm((DEPTH, D_MODEL, MEM_WIDTH), D_MODEL ** -0.5),
        'w_mk': nrm((DEPTH, D_MODEL, MEM_WIDTH), D_MODEL ** -0.5),
        'w_mv': nrm((DEPTH, D_MODEL, MEM_WIDTH), D_MODEL ** -0.5),
        'w_mo': nrm((DEPTH, MEM_WIDTH, D_MODEL), BETA * MEM_WIDTH ** -0.5),
        'ln2_g': 1.0 + nrm((DEPTH, D_MODEL), 0.01),
        'ln2_b': nrm((DEPTH, D_MODEL), 0.01),
        'w_up': nrm((DEPTH, D_MODEL, 2 * D_FF), D_MODEL ** -0.5),
        'w_conv': nrm((DEPTH, CONV_W, D_FF), CONV_W ** -0.5),
        'b_conv': nrm((DEPTH, D_FF), 0.01),
        'w_down': nrm((DEPTH, D_FF, D_MODEL), BETA * D_FF ** -0.5),
        'ln3_g': 1.0 + nrm((DEPTH, D_MODEL), 0.01),
        'ln3_b': nrm((DEPTH, D_MODEL), 0.01),
    }


def reference(x_prompt, x_sample, cache_a_k, cache_a_v, cache_b_k, cache_b_v, cache_b_kidx,
              cache_mem_k, cache_mem_v, state_ffn_conv, mem_prompt, w_in, a_rel_bias, t5_bias,
              w_o, ln1_g, ln1_b, w_mq, w_mk, w_mv, w_mo, ln2_g, ln2_b, w_up, w_conv, b_conv,
              w_down, ln3_g, ln3_b):
    xp, xs = x_prompt, x_sample
    b_p, t_p = xp.shape[:2]
    s_len = xs.shape[1]
    a_keep = min(N_PREV_CHUNKS * CHUNK, t_p)
    new_p = [[] for _ in range(8)]
    new_s = [[] for _ in range(6)]
    for l in range(DEPTH):
        qa, ka, va, qb, kb, vb, qi, ki, wi = split_in(xp @ w_in[l])
        mix = jnp.concatenate([band_attend_prompt(qa, ka, va, a_rel_bias[l]),
                               dsa_prompt(qb, kb, vb, qi, ki, wi, t5_bias)], axis=-1)
        h = layer_norm(ALPHA * xp + mix @ w_o[l], ln1_g[l], ln1_b[l])
        mk, mv = mem_kv(mem_prompt, w_mk[l], w_mv[l])
        h = layer_norm(ALPHA * h + mem_attend(h, mk, mv, w_mq[l], w_mo[l]), ln2_g[l], ln2_b[l])
        f, g_tail = conv_ffn(h, jnp.zeros((b_p, CONV_W - 1, D_FF), h.dtype), w_up[l], w_conv[l], b_conv[l], w_down[l])
        xp = layer_norm(ALPHA * h + f, ln3_g[l], ln3_b[l])
        for lst, arr in zip(new_p, (ka[:, t_p - a_keep:], va[:, t_p - a_keep:], kb, vb, ki, mk, mv, g_tail)):
            lst.append(arr)

        qa, ka, va, qb, kb, vb, qi, ki, wi = split_in(xs @ w_in[l])
        oa = band_attend_sample(qa, jnp.concatenate([cache_a_k[l], ka], axis=1),
                                jnp.concatenate([cache_a_v[l], va], axis=1), a_rel_bias[l])
        kb_all = jnp.concatenate([cache_b_k[l], kb], axis=1)
        vb_all = jnp.concatenate([cache_b_v[l], vb], axis=1)
        ki_all = jnp.concatenate([cache_b_kidx[l], ki], axis=1)
        n_keys = kb_all.shape[1]
        qpos = n_keys - s_len + jnp.arange(s_len)
        adm = jnp.ones((s_len, n_keys), dtype=bool)
        ob = dsa_attend(qb, qi, wi, qpos, kb_all, vb_all, ki_all, adm, min(TOPK_MAX, n_keys // 4), t5_bias)
        h = layer_norm(ALPHA * xs + jnp.concatenate([oa, ob], axis=-1) @ w_o[l], ln1_g[l], ln1_b[l])
        h = layer_norm(ALPHA * h + mem_attend(h, cache_mem_k[l], cache_mem_v[l], w_mq[l], w_mo[l]), ln2_g[l], ln2_b[l])
        f, g_tail_s = conv_ffn(h, state_ffn_conv[l], w_up[l], w_conv[l], b_conv[l], w_down[l])
        xs = layer_norm(ALPHA * h + f, ln3_g[l], ln3_b[l])
        for lst, arr in zip(new_s, (ka, va, kb, vb, ki, g_tail_s)):
            lst.append(arr)

    pak, pav, pbk, pbv, pbi, pmk, pmv, pfc = [jnp.stack(a) for a in new_p]
    sak, sav, sbk, sbv, sbi, sfc = [jnp.stack(a) for a in new_s]
    return (xp, xs, pak, pav, pbk, pbv, pbi, pmk, pmv, pfc, sak, sav, sbk, sbv, sbi, sfc)
```

```python
import math
from contextlib import ExitStack

import numpy as np
import concourse.bass as bass
import concourse.mybir as mybir
from concourse.bass_utils import run_bass_kernel_spmd

F32 = mybir.dt.float32
BF16 = mybir.dt.bfloat16
AF = mybir.ActivationFunctionType
ALU = mybir.AluOpType

D = 1024
KC = 8
NT = 32
NCOL = 2952
C_QA, C_KA, C_QB, C_KB, C_QI, C_KI, C_VA, C_VB, C_WI = 0, 512, 1024, 1536, 1664, 2176, 2304, 2816, 2944
DFF = 2816
NFC = 22
ALPHA = 2.0 ** 0.25
LN_EPS = 1e-5
NEGM = -30000.0
NIT = 17
BIS_W0 = 16.0
ABW = 640
BNW = 256
NBLK = 18


class Res:
    __slots__ = ("lw", "rd", "name", "excl")

    def __init__(self, name="", excl=False):
        self.lw = None
        self.rd = {}
        self.name = name
        self.excl = excl


def _call(name, *args, **kw):
    return lambda e: getattr(e, name)(*args, **kw)


class Prog:
    ENG = ("pe", "act", "dve", "pool", "sp")

    def __init__(self, nc, sems, dma_sems):
        self.nc = nc
        self.streams = {e: [] for e in self.ENG}
        self.sem = sems
        self.cnt = {e: 0 for e in self.ENG}
        self.seen = {e: {} for e in self.ENG}
        self.dsems = dma_sems
        self.dval = [0] * len(dma_sems)
        self.dnext = 0
        self.semh = dict(sems)
        for i, h in enumerate(dma_sems):
            self.semh[("d", i)] = h
        self.ninst = 0
        self.dead = False
        self.deferred = []
        self.defer_lag = 48

    def _deps(self, reads, writes, eng=None):
        d = {}
        for r in reads:
            if r.lw is not None:
                k, v = r.lw
                if d.get(k, 0) < v:
                    d[k] = v
            if r.excl:
                for k, v in r.rd.items():
                    if k != eng and d.get(k, 0) < v:
                        d[k] = v
        for w in writes:
            if w.lw is not None:
                k, v = w.lw
                if d.get(k, 0) < v:
                    d[k] = v
            for k, v in w.rd.items():
                if d.get(k, 0) < v:
                    d[k] = v
        return d

    def _wait(self, eng, deps):
        for k, v in deps.items():
            if k == "pe" and eng == "pe":
                continue
            if self.seen[eng].get(k, 0) >= v:
                continue
            self.seen[eng][k] = v
            h = self.semh[k]
            self.streams[eng].append(lambda e, h=h, v=v: e.wait_ge(h, v))

    def _flush_deferred(self, force=False, reads=(), writes=()):
        if not self.deferred:
            return
        conflict = force
        if not conflict:
            ws = set(id(w) for w in writes)
            rs = set(id(r) for r in reads)
            for d in self.deferred:
                dr = set(id(x) for x in d[3])
                dw = set(id(x) for x in d[4])
                if (ws & dr) or (ws & dw) or (rs & dw):
                    conflict = True
                    break
        if conflict:
            pend, self.deferred = self.deferred, []
            for d in pend:
                self._dma_now(d[0], d[1], d[2], d[3], d[4], d[5])
            return
        while self.deferred and self.ninst - self.deferred[0][6] >= self.defer_lag:
            d = self.deferred.pop(0)
            self._dma_now(d[0], d[1], d[2], d[3], d[4], d[5])

    def op(self, eng, fn, reads=(), writes=()):
        if self.dead:
            return
        self._flush_deferred(False, reads, writes)
        self._wait(eng, self._deps(reads, writes, eng))
        self.cnt[eng] += 1
        n = self.cnt[eng]
        h = self.sem[eng]
        self.streams[eng].append(lambda e, fn=fn, h=h: fn(e).then_inc(h, 1))
        self.ninst += 1
        for r in reads:
            if r.rd.get(eng, 0) < n:
                r.rd[eng] = n
        for w in writes:
            w.lw = (eng, n)
            w.rd = {}

    def dma(self, q, out, in_, reads=(), writes=(), slow=False, defer=False):
        if self.dead:
            return
        if defer:
            self._flush_deferred(False, reads, writes)
            self.deferred.append((q, out, in_, list(reads), list(writes), slow, self.ninst))
            return
        self._flush_deferred(False, reads, writes)
        self._dma_now(q, out, in_, reads, writes, slow)

    def _dma_now(self, q, out, in_, reads=(), writes=(), slow=False):
        deps = self._deps(reads, writes)
        i = self.dnext
        self.dnext = (i + 1) % len(self.dsems)
        k = ("d", i)
        if self.dval[i] > 0 and deps.get(k, 0) < self.dval[i]:
            deps[k] = self.dval[i]
        self._wait(q, deps)
        self.dval[i] += 16
        v = self.dval[i]
        h = self.dsems[i]
        if slow:
            self.streams[q].append(
                lambda e, out=out, in_=in_, h=h: e.dma_start(out=out, in_=in_, allow_slow_non_contiguous=True).then_inc(h, 16))
        else:
            self.streams[q].append(lambda e, out=out, in_=in_, h=h: e.dma_start(out=out, in_=in_).then_inc(h, 16))
        self.ninst += 1
        for r in reads:
            if r.rd.get(k, 0) < v:
                r.rd[k] = v
        for w in writes:
            w.lw = (k, v)
            w.rd = {}

    def finish(self):
        self._flush_deferred(True)
        deps = {("d", i): v for i, v in enumerate(self.dval) if v > 0}
        self._wait("sp", deps)

    def flush(self, block):
        self._flush_deferred(True)
        s = self.streams
        self.streams = {e: [] for e in self.ENG}

        def mk(lst):
            def body(e):
                for f in lst:
                    f(e)
            return body

        block.tensor(mk(s["pe"]))
        block.scalar(mk(s["act"]))
        block.vector(mk(s["dve"]))
        block.gpsimd(mk(s["pool"]))
        block.sync(mk(s["sp"]))


def build_program(debug=False, stop_at=None):
    nc = bass.Bass("TRN2", target_bir_lowering=False)

    def din(name, shape, dt=F32):
        return nc.dram_tensor(name, list(shape), dt, kind="ExternalInput").ap()

    def dout(name, shape, dt=F32):
        return nc.dram_tensor(name, list(shape), dt, kind="ExternalOutput").ap()

    def dscr(name, shape, dt):
        return nc.dram_tensor(name, list(shape), dt, kind="Internal").ap()

    I = {}
    I["xkT"] = din("xkT", [NT, 128, 1024])
    I["xsT"] = din("xsT", [128, 8 * 16])
    I["xres"] = din("xres", [NBLK * 128, 1024])
    I["win"] = din("win", [128, KC * NCOL])
    I["wo"] = din("wo", [128, 8 * 1024])
    I["wmq"] = din("wmq", [128, 8 * 512])
    I["wmk"] = din("wmk", [128, 8 * 512])
    I["wmv"] = din("wmv", [128, 8 * 512])
    I["wmo"] = din("wmo", [128, 4 * 1024])
    I["wup"] = din("wup", [NFC, 128, 8 * 256])
    I["wdown"] = din("wdown", [128, NFC * 1024])
    I["lnp"] = din("lnp", [6, 1024])
    I["wconvT"] = din("wconvT", [128, NFC * 3])
    I["bconvT"] = din("bconvT", [128, NFC])
    I["memT"] = din("memT", [128, 8 * 256])
    I["cmkT"] = din("cmkT", [128, 4 * 256])
    I["cmv"] = din("cmv", [256, 512])
    I["cakT"] = din("cakT", [128, 4 * 512])
    I["cav"] = din("cav", [512, 512])
    I["cbkT"] = din("cbkT", [128, 2048])
    I["cbv"] = din("cbv", [2048, 128])
    I["cbiT"] = din("cbiT", [128, 2048])
    I["sconvT"] = din("sconvT", [128, NFC * 2])
    I["ident"] = din("ident", [128, 128])
    I["AB"] = din("AB", [128, 8 * ABW])
    I["ABs"] = din("ABs", [16, 8 * 528])
    I["Bn"] = din("Bn", [128, 8 * BNW])
    I["Bns"] = din("Bns", [16, 8 * 144])
    I["C15"] = din("C15", [128, 8])
    I["colmask"] = din("colmask", [128, 1])
    I["diagmask"] = din("diagmask", [128, 128])
    I["kvalid"] = din("kvalid", [128, NT])
    I["flag"] = din("flag", [128, 1])

    O = {}
    O["y"] = dout("y", [2048, 1024])
    O["ys"] = dout("ys", [16, 1024])
    O["akT"] = dout("akT", [128, 4 * 512])
    O["av"] = dout("av", [512, 512])
    O["bkT"] = dout("bkT", [128, 4096])
    O["bv"] = dout("bv", [4096, 128])
    O["biT"] = dout("biT", [64, 4096])
    O["mkT"] = dout("mkT", [128, 4 * 256])
    O["mv"] = dout("mv", [256, 512])
    O["fcT"] = dout("fcT", [128, NFC * 2])
    O["sakT"] = dout("sakT", [128, 4 * 16])
    O["sav"] = dout("sav", [16, 512])
    O["sbkT"] = dout("sbkT", [128, 16])
    O["sbv"] = dout("sbv", [16, 128])
    O["sbiT"] = dout("sbiT", [64, 16])
    O["sfcT"] = dout("sfcT", [128, NFC * 2])
    if debug:
        O["dbg_mix"] = dout("dbg_mix", [NBLK * 128, 1024], BF16)
        O["dbg_h2"] = dout("dbg_h2", [NBLK * 128, 1024])
        mixD = O["dbg_mix"]
        h2D = O["dbg_h2"]
    else:
        mixD = dscr("mixD", [NBLK * 128, 1024], BF16)
        h2D = dscr("h2D", [NBLK * 128, 1024], F32)
    h2TD = dscr("h2TD", [NBLK, 128, 1024], BF16)
    R_mixD = [Res("mixD%d" % i) for i in range(NBLK)]
    R_h2D = [Res("h2D%d" % i) for i in range(NBLK)]
    R_h2TD = [Res("h2TD%d" % i) for i in range(NBLK)]

    es = ExitStack()
    with es:
        sems = {e: es.enter_context(nc.semaphore("s_" + e)) for e in Prog.ENG}
        dsems = [es.enter_context(nc.semaphore("d%d" % i)) for i in range(32)]
        P = Prog(nc, sems, dsems)
        block = es.enter_context(nc.Block())

        def checkpoint(name):
            if stop_at is not None and name == stop_at and not P.dead:
                P.finish()
                P.flush(block)
                P.dead = True

        pb = [es.enter_context(nc.psum_tensor("pb%d" % i, [128, 512], F32)) for i in range(8)]
        R_pb = [Res("pb%d" % i, excl=True) for i in range(8)]

        class Rot:
            def __init__(self, idxs):
                self.idxs = idxs
                self.i = 0

            def next(self):
                k = self.idxs[self.i % len(self.idxs)]
                self.i += 1
                return k

        def sb(stack, name, shape, dt):
            return stack.enter_context(nc.sbuf_tensor("sb_" + name, list(shape), dt))

        ident_f = sb(es, "ident_f", [128, 128], F32)
        ident = sb(es, "ident", [128, 512], BF16)
        R_ident = Res("ident")
        P.dma("sp", ident_f[:, :], I["ident"][:, :], writes=[R_ident])
        for r in range(4):
            P.op("act", _call("activation", out=ident[:, r * 128:(r + 1) * 128], in_=ident_f[:, :], func=AF.Copy),
                 reads=[R_ident], writes=[R_ident])

        def run_interleaved(gens):
            gens = list(gens)
            while gens:
                for g in list(gens):
                    try:
                        next(g)
                    except StopIteration:
                        gens.remove(g)

        with ExitStack() as sa:
            winb = sb(sa, "winb", [128, KC * NCOL], BF16)
            R_win = Res("win")
            kbi = sb(sa, "kbi", [128, 2 * 4096], BF16)
            R_kbi = [Res("kbi%d" % r) for r in range(NT)]
            vb_aug = sb(sa, "vb_aug", [128, NT * 2 * 65], BF16)
            R_vb = [Res("vb%d" % r) for r in range(NT)]
            kaT = sb(sa, "kaT", [128, 6 * 512], BF16)
            R_ka = [Res("ka%d" % s) for s in range(6)]
            va_aug = sb(sa, "va_aug", [128, 6 * 8 * 65], BF16)
            R_va = [Res("va%d" % s) for s in range(6)]
            ABb = sb(sa, "ABb", [128, 8 * ABW], BF16)
            R_AB = Res("AB")
            Bnb = sb(sa, "Bnb", [128, 8 * BNW], BF16)
            R_Bn = Res("Bn")
            Mnear = [sb(sa, "Mnear%d" % k, [128, 8 * BNW], BF16) for k in range(2)]
            R_Mnear = [Res("Mnear%d" % k) for k in range(2)]
            score = [sb(sa, "score%d" % k, [128, 4096], F32) for k in range(2)]
            R_score = [Res("score%d" % k) for k in range(2)]
            Mb = [sb(sa, "Mb%d" % k, [128, 4096], BF16) for k in range(2)]
            R_M = [Res("M%d" % k) for k in range(2)]
            relu = [sb(sa, "relu%d" % k, [128, 512], BF16) for k in range(3)]
            R_relu = [Res("relu%d" % k) for k in range(3)]
            xstg = sb(sa, "xstg", [128, 1024], F32)
            R_xstg = Res("xstg")
            xTb = [sb(sa, "xTb%d" % k, [128, 1024], BF16) for k in range(2)]
            R_xT = [Res("xT%d" % k) for k in range(2)]
            qaz = [sb(sa, "qaz%d" % k, [128, 1024], BF16) for k in range(2)]
            qbz = [sb(sa, "qbz%d" % k, [128, 1024], BF16) for k in range(2)]
            qiz = [sb(sa, "qiz%d" % k, [128, 1024], BF16) for k in range(2)]
            R_qa = [Res("qa%d" % k) for k in range(2)]
            R_qb = [Res("qb%d" % k) for k in range(2)]
            R_qi = [Res("qi%d" % k) for k in range(2)]
            coef = [sb(sa, "coef%d" % k, [128, 8], F32) for k in range(2)]
            R_coef = [Res("coef%d" % k) for k in range(2)]
            dg = [sb(sa, "dg%d" % k, [128, 1024], BF16) for k in range(2)]
            R_dg = [Res("dg%d" % k) for k in range(2)]
            PTA = [sb(sa, "PTA%d" % k, [128, 512], BF16) for k in range(3)]
            R_PTA = [Res("PTA%d" % k) for k in range(3)]
            PTB = [sb(sa, "PTB%d" % k, [128, 512], BF16) for k in range(3)]
            R_PTB = [Res("PTB%d" % k) for k in range(3)]
            mixb = [sb(sa, "mixb%d" % k, [128, 1024], BF16) for k in range(2)]
            R_mix = [Res("mix%d" % k) for k in range(2)]
            ostg = [sb(sa, "ostg%d" % k, [128, 256], F32) for k in range(2)]
            R_ostg = [Res("ostg%d" % k) for k in range(2)]
            vbstg = [sb(sa, "vbstg%d" % k, [128, 128], F32) for k in range(2)]
            R_vbstg = [Res("vbstg%d" % k) for k in range(2)]
            astg = sb(sa, "astg", [128, 1024], F32)
            R_astg = Res("astg")
            small = [sb(sa, "small%d" % k, [128, 16], F32) for k in range(2)]
            R_small = [Res("small%d" % k) for k in range(2)]
            recA = [sb(sa, "recA%d" % k, [128, 8], F32) for k in range(2)]
            R_recA = [Res("recA%d" % k) for k in range(2)]
            recB = [sb(sa, "recB%d" % k, [128, 8], F32) for k in range(2)]
            R_recB = [Res("recB%d" % k) for k in range(2)]
            colmask = sb(sa, "colmask", [128, 1], F32)
            diagm = sb(sa, "diagm", [128, 128], F32)
            kvalid = sb(sa, "kvalid", [128, NT], F32)
            c15 = sb(sa, "c15", [128, 8], F32)
            ones8 = sb(sa, "ones8", [128, 8], F32)
            R_cst = Res("cst")

            wrot = Rot([0, 1, 2])

            P.dma("sp", colmask[:, :], I["colmask"][:, :], writes=[R_cst])
            P.dma("sp", diagm[:, :], I["diagmask"][:, :], writes=[R_cst])
            P.dma("sp", kvalid[:, :], I["kvalid"][:, :], writes=[R_cst])
            P.dma("sp", c15[:, :], I["C15"][:, :], writes=[R_cst])
            P.op("pool", _call("memset", ones8[:, :], 1.0), writes=[R_cst])
            for k in range(2):
                P.op("pool", _call("memset", qaz[k][:, :], 0.0), writes=[R_qa[k]])
                P.op("pool", _call("memset", qbz[k][:, :], 0.0), writes=[R_qb[k]])
                P.op("pool", _call("memset", qiz[k][:, :], 0.0), writes=[R_qi[k]])

            HW = NCOL // 2
            for kc in range(KC):
                for hh in range(2):
                    stg, R_stg = score[hh], R_score[hh]
                    P.dma("sp", stg[:, 0:HW], I["win"][:, kc * NCOL + hh * HW: kc * NCOL + (hh + 1) * HW], writes=[R_stg])
                    if hh == 0:
                        P.op("act", _call("activation", out=winb[:, kc * NCOL + hh * HW: kc * NCOL + (hh + 1) * HW], in_=stg[:, 0:HW], func=AF.Copy),
                             reads=[R_stg], writes=[R_win])
                    else:
                        P.op("pool", _call("tensor_copy", out=winb[:, kc * NCOL + hh * HW: kc * NCOL + (hh + 1) * HW], in_=stg[:, 0:HW]),
                             reads=[R_stg], writes=[R_win])
            for hh in range(2):
                w = 4 * ABW
                P.dma("sp", score[hh][:, 0:w], I["AB"][:, hh * w:(hh + 1) * w], writes=[R_score[hh]])
                P.op("act", _call("activation", out=ABb[:, hh * w:(hh + 1) * w], in_=score[hh][:, 0:w], func=AF.Copy),
                     reads=[R_score[hh]], writes=[R_AB])
            P.dma("sp", score[0][:, 0:8 * BNW], I["Bn"][:, :], writes=[R_score[0]])
            for h in range(8):
                P.op("dve", _call("tensor_scalar", out=Bnb[:, h * BNW:(h + 1) * BNW], in0=score[0][:, h * BNW:(h + 1) * BNW],
                                  scalar1=c15[:, h:h + 1], scalar2=None, op0=ALU.subtract),
                     reads=[R_score[0], R_cst], writes=[R_Bn])
            checkpoint('consts')

            def win_cols(kc, c0, n):
                return winb[:, kc * NCOL + c0: kc * NCOL + c0 + n]

            def fm_proj(bank, xT, R_x, N, col0, nchunks, ocol=0):
                for j in range(nchunks):
                    for kc in range(KC):
                        P.op("pe", _call("matmul", out=pb[bank][:, ocol + j * N: ocol + (j + 1) * N], lhsT=win_cols(kc, col0 + j * 128, 128),
                                         rhs=xT[:, kc * N:(kc + 1) * N], start=(kc == 0), stop=(kc == KC - 1)),
                             reads=[R_win, R_x], writes=[R_pb[bank]])

            def tm_proj(bank, xT, R_x, N, col0, ncols, ocol=0):
                for kc in range(KC):
                    P.op("pe", _call("matmul", out=pb[bank][0:N, ocol:ocol + ncols], lhsT=xT[:, kc * N:(kc + 1) * N],
                                     rhs=win_cols(kc, col0, ncols), start=(kc == 0), stop=(kc == KC - 1)),
                         reads=[R_win, R_x], writes=[R_pb[bank]])

            def load_xT(r):
                s = r % 2
                P.dma("sp", xstg[:, :], I["xkT"][r], writes=[R_xstg])
                P.op("pool", _call("tensor_copy", out=xTb[s][:, :], in_=xstg[:, :]), reads=[R_xstg], writes=[R_xT[s]])

            def kside(r, full):
                s = r % 2
                xT, R_x = xTb[s], R_xT[s]
                so = r % 2
                bk = wrot.next()
                fm_proj(bk, xT, R_x, 128, C_KB, 1)
                fm_proj(bk, xT, R_x, 128, C_KI, 1, ocol=128)
                P.op("act", _call("activation", out=ostg[so][:, :], in_=pb[bk][:, 0:256], func=AF.Copy), reads=[R_pb[bk]], writes=[R_ostg[so]])
                P.op("pool", _call("tensor_copy", out=kbi[:, :].rearrange("p (a c) -> p a c", a=2)[:, :, r * 128:(r + 1) * 128],
                                   in_=ostg[so][:, :].rearrange("p (a c) -> p a c", a=2)),
                     reads=[R_ostg[so]], writes=[R_kbi[r]])
                P.dma("sp", O["bkT"][:, r * 128:(r + 1) * 128], ostg[so][:, 0:128], reads=[R_ostg[so]], defer=True)
                P.dma("sp", O["biT"][:, r * 128:(r + 1) * 128], ostg[so][0:64, 128:256], reads=[R_ostg[so]], defer=True)
                yield
                bv_ = wrot.next()
                tm_proj(bv_, xT, R_x, 128, C_VB, 128)
                vbv = vb_aug[:, r * 130:(r + 1) * 130].rearrange("p (g d) -> p g d", d=65)
                P.op("act", _call("activation", out=vbstg[so][:, :], in_=pb[bv_][:, 0:128], func=AF.Copy), reads=[R_pb[bv_]], writes=[R_vbstg[so]])
                P.op("pool", _call("tensor_copy", out=vbv[:, :, 0:64], in_=vbstg[so][:, :].rearrange("p (g d) -> p g d", d=64)),
                     reads=[R_vbstg[so]], writes=[R_vb[r]])
                P.op("pool", _call("tensor_scalar", out=vbv[:, :, 64:65], in0=ones8[:, 0:2].rearrange("p (g o) -> p g o", o=1),
                                   scalar1=kvalid[:, r:r + 1], scalar2=None, op0=ALU.mult),
                     reads=[R_cst], writes=[R_vb[r]])
                P.dma("sp", O["bv"][r * 128:(r + 1) * 128, :], vbstg[so][:, :], reads=[R_vbstg[so]], defer=True)
                yield
                if not full:
                    return
                slot = r % 6
                ba = wrot.next()
                fm_proj(ba, xT, R_x, 128, C_KA, 4)
                P.op("act", _call("activation", out=kaT[:, slot * 512:(slot + 1) * 512], in_=pb[ba][:, :], func=AF.Copy),
                     reads=[R_pb[ba]], writes=[R_ka[slot]])
                if r >= 28:
                    P.op("dve", _call("tensor_copy", out=astg[:, 0:512], in_=pb[ba][:, :]), reads=[R_pb[ba]], writes=[R_astg])
                    P.dma("sp", O["akT"].rearrange("p (j t) -> p j t", t=512)[:, :, (r - 28) * 128:(r - 27) * 128],
                          astg[:, 0:512].rearrange("p (j t) -> p j t", t=128), reads=[R_astg], defer=True)
                yield
                bva = wrot.next()
                tm_proj(bva, xT, R_x, 128, C_VA, 512)
                vav = va_aug[:, slot * 520:(slot + 1) * 520].rearrange("p (h d) -> p h d", d=65)
                P.op("act", _call("activation", out=vav[:, :, 0:64], in_=pb[bva][:, :].rearrange("p (h d) -> p h d", d=64), func=AF.Copy),
                     reads=[R_pb[bva]], writes=[R_va[slot]])
                P.op("pool", _call("tensor_scalar", out=vav[:, :, 64:65], in0=ones8[:, :].rearrange("p (h o) -> p h o", o=1),
                                   scalar1=kvalid[:, r:r + 1], scalar2=None, op0=ALU.mult),
                     reads=[R_cst], writes=[R_va[slot]])
                if r >= 28:
                    P.op("dve", _call("tensor_copy", out=astg[:, 512:1024], in_=pb[bva][:, :]), reads=[R_pb[bva]], writes=[R_astg])
                    P.dma("sp", O["av"][(r - 28) * 128:(r - 27) * 128, :], astg[:, 512:1024], reads=[R_astg], defer=True)
                yield

            def qside(xT, R_x, qs, st):
                b1 = wrot.next()
                fm_proj(b1, xT, R_x, qs, C_QA, 4)
                for hf in range(2):
                    P.op("act", _call("activation",
                                      out=qaz[st][hf * 64:(hf + 1) * 64, 0:8 * qs].rearrange("p (j two q) -> p j two q", two=2, q=qs)[:, :, hf, :],
                                      in_=pb[b1][hf * 64:(hf + 1) * 64, 0:4 * qs].rearrange("p (j q) -> p j q", q=qs), func=AF.Copy, scale=0.125),
                         reads=[R_pb[b1]], writes=[R_qa[st]])
                yield
                b2 = wrot.next()
                fm_proj(b2, xT, R_x, qs, C_QB, 4)
                for g in range(2):
                    P.op("act", _call("activation", out=qbz[st][g * 64:(g + 1) * 64, g * 4 * qs:(g + 1) * 4 * qs],
                                      in_=pb[b2][g * 64:(g + 1) * 64, 0:4 * qs], func=AF.Copy, scale=0.125),
                         reads=[R_pb[b2]], writes=[R_qb[st]])
                yield
                b3 = wrot.next()
                fm_proj(b3, xT, R_x, qs, C_QI, 4)
                for hf in range(2):
                    P.op("act", _call("activation",
                                      out=qiz[st][hf * 64:(hf + 1) * 64, 0:8 * qs].rearrange("p (j two q) -> p j two q", two=2, q=qs)[:, :, hf, :],
                                      in_=pb[b3][hf * 64:(hf + 1) * 64, 0:4 * qs].rearrange("p (j q) -> p j q", q=qs), func=AF.Copy),
                         reads=[R_pb[b3]], writes=[R_qi[st]])
                b4 = wrot.next()
                tm_proj(b4, xT, R_x, qs, C_WI, 8)
                P.op("dve", _call("tensor_scalar", out=coef[st][0:qs, :], in0=pb[b4][0:qs, 0:8], scalar1=float(8.0 ** -1.5), scalar2=None, op0=ALU.mult),
                     reads=[R_pb[b4]], writes=[R_coef[st]])
                for h in range(8):
                    P.op("pool", _call("tensor_scalar", out=dg[st][0:qs, h * 128: h * 128 + qs], in0=ident_f[0:qs, 0:qs],
                                       scalar1=coef[st][0:qs, h:h + 1], scalar2=None, op0=ALU.mult),
                         reads=[R_coef[st], R_ident], writes=[R_dg[st]])
                yield

            def normalize(bank, qs, mixt, R_m, col0, rec, R_rec):
                ov = pb[bank][0:qs, 0:260].rearrange("p (h d) -> p h d", d=65)
                P.op("dve", _call("tensor_scalar", out=rec[0:qs, 0:4].rearrange("p (h o) -> p h o", o=1), in0=ov[:, :, 64:65],
                                  scalar1=1e-30, scalar2=None, op0=ALU.max),
                     reads=[R_pb[bank]], writes=[R_rec])
                P.op("dve", _call("reciprocal", out=rec[0:qs, 0:4], in_=rec[0:qs, 0:4]), reads=[R_rec], writes=[R_rec])
                for hh in range(4):
                    P.op("dve", _call("tensor_scalar", out=mixt[0:qs, col0 + hh * 64: col0 + (hh + 1) * 64],
                                      in0=pb[bank][0:qs, hh * 65: hh * 65 + 64],
                                      scalar1=rec[0:qs, hh:hh + 1], scalar2=None, op0=ALU.mult),
                         reads=[R_pb[bank], R_rec], writes=[R_m])

            def pipe3(items, s1, s2, s3, D):
                pend = []
                for it in items:
                    s1(it)
                    s2(it)
                    pend.append(it)
                    if len(pend) > D:
                        s3(pend.pop(0))
                    yield
                while pend:
                    s3(pend.pop(0))
                    yield

            pta_rot = Rot([0, 1, 2])
            relu_rot = Rot([0, 1, 2])
            ptb_rot = Rot([0, 1, 2])
            brot = Rot([3, 7])

            def front_attn(sn, qs, wins, btiles, prompt_masks, abw):
                st = sn % 2
                mixt, R_m = mixb[st], R_mix[st]
                nw = len(wins)

                units = []
                for h in range(8):
                    units.append({"h": h, "t0": 0, "tiles": wins[0:4]})
                    if nw > 4:
                        units.append({"h": h, "t0": 4, "tiles": wins[4:5]})

                def a1(u):
                    h = u["h"]
                    j = h // 2
                    bank = wrot.next()
                    u["bank"] = bank
                    for i, (slot, ts) in enumerate(u["tiles"]):
                        t = u["t0"] + i
                        c0 = i * qs
                        P.op("pe", _call("matmul", out=pb[bank][0:ts, c0:c0 + qs], lhsT=kaT[:, slot * 512 + j * 128: slot * 512 + j * 128 + ts],
                                         rhs=qaz[st][:, h * qs:(h + 1) * qs], start=True, stop=False),
                             reads=[R_ka[slot], R_qa[st]], writes=[R_pb[bank]])
                        P.op("pe", _call("matmul", out=pb[bank][0:ts, c0:c0 + qs], lhsT=ABb[0:qs, h * abw + t * 128: h * abw + t * 128 + ts],
                                         rhs=ident[0:qs, 0:qs], start=False, stop=True),
                             reads=[R_AB, R_ident], writes=[R_pb[bank]])

                def a2(u):
                    k = pta_rot.next()
                    u["pt"], u["R_pt"] = PTA[k], R_PTA[k]
                    bank = u["bank"]
                    tsm = max(ts for (_, ts) in u["tiles"])
                    n = len(u["tiles"])
                    P.op("act", _call("activation", out=u["pt"][0:tsm, 0:n * qs], in_=pb[bank][0:tsm, 0:n * qs], func=AF.Exp),
                         reads=[R_pb[bank]], writes=[u["R_pt"]])

                def a3(u):
                    h = u["h"]
                    last_unit = (u["t0"] + len(u["tiles"]) == nw)
                    for i, (slot, ts) in enumerate(u["tiles"]):
                        t = u["t0"] + i
                        P.op("pe", _call("matmul", out=pb[4][0:qs, (h % 4) * 65:(h % 4) * 65 + 65], lhsT=u["pt"][0:ts, i * qs:(i + 1) * qs],
                                         rhs=va_aug[0:ts, slot * 520 + h * 65: slot * 520 + h * 65 + 65],
                                         start=(h % 4 == 0 and t == 0), stop=(t == nw - 1), skip_group_check=True),
                             reads=[u["R_pt"], R_va[slot]], writes=[R_pb[4]])
                    if last_unit and h % 4 == 3:
                        normalize(4, qs, mixt, R_m, (h // 4) * 256, recA[st], R_recA[st])

                yield from pipe3(units, a1, a2, a3, 2)

                L = btiles[-1][1] + btiles[-1][2]
                items = []
                cc = 0
                for c0 in range(0, L, 512):
                    w = min(512, L - c0)
                    rk = [R_kbi[tt[0]] for tt in btiles if tt[1] >= c0 - 127 and tt[1] < c0 + w]
                    for h in range(8):
                        items.append({"c0": c0, "w": w, "h": h, "sc": (5, 4)[cc % 2], "rk": rk})
                    cc += 1

                def i1(it):
                    bank = wrot.next()
                    it["bank"] = bank
                    h, c0, w = it["h"], it["c0"], it["w"]
                    P.op("pe", _call("matmul", out=pb[bank][0:qs, 0:w], lhsT=qiz[st][:, h * qs:(h + 1) * qs],
                                     rhs=kbi[:, 4096 + c0: 4096 + c0 + w], start=True, stop=True),
                         reads=[R_qi[st]] + it["rk"], writes=[R_pb[bank]])

                def i2(it):
                    k = relu_rot.next()
                    it["rl"], it["R_rl"] = relu[k], R_relu[k]
                    w = it["w"]
                    P.op("act", _call("activation", out=it["rl"][0:qs, 0:w], in_=pb[it["bank"]][0:qs, 0:w], func=AF.Relu),
                         reads=[R_pb[it["bank"]]], writes=[it["R_rl"]])

                def i3(it):
                    h, c0, w, sc = it["h"], it["c0"], it["w"], it["sc"]
                    P.op("pe", _call("matmul", out=pb[sc][0:qs, 0:w], lhsT=dg[st][0:qs, h * 128: h * 128 + qs], rhs=it["rl"][0:qs, 0:w],
                                     start=(h == 0), stop=(h == 7)),
                         reads=[R_dg[st], it["R_rl"]], writes=[R_pb[sc]])
                    if h == 7:
                        if prompt_masks and c0 < 2048:
                            wm = min(w, 2048 - c0)
                            P.op("act", _call("activation", out=score[st][0:qs, c0:c0 + wm], in_=pb[sc][0:qs, 0:wm], func=AF.Identity,
                                              bias=colmask[0:qs, 0:1]),
                                 reads=[R_pb[sc], R_cst], writes=[R_score[st]])
                            if wm < w:
                                P.op("act", _call("activation", out=score[st][0:qs, c0 + wm:c0 + w], in_=pb[sc][0:qs, wm:w], func=AF.Copy),
                                     reads=[R_pb[sc]], writes=[R_score[st]])
                        else:
                            P.op("act", _call("activation", out=score[st][0:qs, c0:c0 + w], in_=pb[sc][0:qs, 0:w], func=AF.Copy),
                                 reads=[R_pb[sc]], writes=[R_score[st]])

                yield from pipe3(items, i1, i2, i3, 2)
                if prompt_masks:
                    P.op("dve", _call("tensor_tensor", out=score[st][0:qs, L - 128:L], in0=score[st][0:qs, L - 128:L], in1=diagm[0:qs, :], op=ALU.add),
                         reads=[R_score[st], R_cst], writes=[R_score[st]])
                yield

            def back_attn(sn, qs, btiles, blk, bnw):
                st = sn % 2
                mixt, R_m = mixb[st], R_mix[st]
                sm, R_sm = small[st], R_small[st]
                L = btiles[-1][1] + btiles[-1][2]
                P.op("dve", _call("memset", sm[0:qs, 1:2], 0.0), writes=[R_sm])
                for k in range(NIT):
                    wk = BIS_W0 / (2.0 ** k)
                    P.op("dve", _call("tensor_scalar", out=Mb[st][0:qs, 0:L], in0=score[st][0:qs, 0:L], scalar1=sm[0:qs, 1:2], scalar2=None,
                                      op0=ALU.is_ge, op1=ALU.add, accum_out=sm[0:qs, 0:1]),
                         reads=[R_score[st], R_sm], writes=[R_M[st], R_sm])
                    P.op("dve", _call("tensor_scalar", out=sm[0:qs, 2:3], in0=sm[0:qs, 0:1], scalar1=255.5, scalar2=wk,
                                      op0=ALU.is_ge, op1=ALU.mult),
                         reads=[R_sm], writes=[R_sm])
                    P.op("dve", _call("scalar_tensor_tensor", out=sm[0:qs, 1:2], in0=sm[0:qs, 2:3], scalar=-wk / 2.0,
                                      in1=sm[0:qs, 1:2], op0=ALU.add, op1=ALU.add),
                         reads=[R_sm], writes=[R_sm])
                    yield
                wl = BIS_W0 / (2.0 ** (NIT - 1)) / 2.0
                P.op("dve", _call("tensor_scalar", out=sm[0:qs, 3:4], in0=sm[0:qs, 1:2], scalar1=-wl, scalar2=None, op0=ALU.add),
                     reads=[R_sm], writes=[R_sm])
                P.op("dve", _call("tensor_scalar", out=Mb[st][0:qs, 0:L], in0=score[st][0:qs, 0:L], scalar1=sm[0:qs, 3:4], scalar2=NEGM,
                                  op0=ALU.is_lt, op1=ALU.mult),
                     reads=[R_score[st], R_sm], writes=[R_M[st]])
                nearw = btiles[-2][2] + btiles[-1][2]
                for h in range(8):
                    P.op("dve", _call("tensor_tensor", out=Mnear[st][0:qs, h * bnw: h * bnw + nearw], in0=Bnb[0:qs, h * bnw: h * bnw + nearw],
                                      in1=Mb[st][0:qs, L - nearw:L], op=ALU.add),
                         reads=[R_Bn, R_M[st]], writes=[R_Mnear[st]])
                yield
                nb = len(btiles)
                items = [{"g": g, "t": t, "vt": vt, "c0": c0, "ts": ts} for g in range(2) for t, (vt, c0, ts) in enumerate(btiles)]

                def b1(it):
                    g, t, vt, c0, ts = it["g"], it["t"], it["vt"], it["c0"], it["ts"]
                    bank = brot.next()
                    it["bank"] = bank
                    P.op("pe", _call("matmul", out=pb[bank][0:ts, 0:4 * qs], lhsT=kbi[:, c0:c0 + ts],
                                     rhs=qbz[st][:, g * 4 * qs:(g + 1) * 4 * qs], start=True, stop=False),
                         reads=[R_kbi[vt], R_qb[st]], writes=[R_pb[bank]])
                    if t < nb - 2 and qs == 128:
                        P.op("pe", _call("matmul", out=pb[bank][0:ts, 0:512], lhsT=Mb[st][0:qs, c0:c0 + ts], rhs=ident[0:128, 0:512],
                                         start=False, stop=True),
                             reads=[R_M[st], R_ident], writes=[R_pb[bank]])
                    elif t < nb - 2:
                        for r in range(4):
                            P.op("pe", _call("matmul", out=pb[bank][0:ts, r * qs:(r + 1) * qs], lhsT=Mb[st][0:qs, c0:c0 + ts],
                                             rhs=ident[0:qs, 0:qs], start=False, stop=(r == 3)),
                                 reads=[R_M[st], R_ident], writes=[R_pb[bank]])
                    else:
                        tt = t - (nb - 2)
                        for r in range(4):
                            hh = g * 4 + r
                            P.op("pe", _call("matmul", out=pb[bank][0:ts, r * qs:(r + 1) * qs],
                                             lhsT=Mnear[st][0:qs, hh * bnw + tt * 128: hh * bnw + tt * 128 + ts], rhs=ident[0:qs, 0:qs],
                                             start=False, stop=(r == 3)),
                                 reads=[R_Mnear[st], R_ident], writes=[R_pb[bank]])

                def b2(it):
                    k = ptb_rot.next()
                    it["ptb"], it["R_ptb"] = PTB[k], R_PTB[k]
                    ts = it["ts"]
                    P.op("act", _call("activation", out=it["ptb"][0:ts, 0:4 * qs], in_=pb[it["bank"]][0:ts, 0:4 * qs], func=AF.Exp),
                         reads=[R_pb[it["bank"]]], writes=[it["R_ptb"]])

                def b3(it):
                    g, t, vt, ts = it["g"], it["t"], it["vt"], it["ts"]
                    for r in range(4):
                        P.op("pe", _call("matmul", out=pb[6][0:qs, r * 65: r * 65 + 65], lhsT=it["ptb"][0:ts, r * qs:(r + 1) * qs],
                                         rhs=vb_aug[0:ts, (vt * 2 + g) * 65:(vt * 2 + g) * 65 + 65],
                                         start=(t == 0 and r == 0), stop=(t == nb - 1), skip_group_check=True),
                             reads=[it["R_ptb"], R_vb[vt]], writes=[R_pb[6]])
                    if t == nb - 1:
                        normalize(6, qs, mixt, R_m, 512 + g * 256, recB[st], R_recB[st])

                yield from pipe3(items, b1, b2, b3, 1)
                P.dma("sp", mixD[blk * 128: blk * 128 + qs, :], mixt[0:qs, :], reads=[R_m], writes=[R_mixD[blk]], defer=True)
                yield

            for r in range(16):
                load_xT(r)
                for _ in kside(r, full=(r >= 11)):
                    pass
            checkpoint('phase0')

            def prompt_front(sn, T):
                if T >= 16:
                    load_xT(T)
                    yield from kside(T, full=True)
                s = T % 2
                yield from qside(xTb[s], R_xT[s], 128, sn % 2)
                wins = [((T - 4 + t) % 6, 128) for t in range(5)]
                btiles = [(t, t * 128, 128) for t in range(T + 1)]
                yield from front_attn(sn, 128, wins, btiles, True, ABW)

            def prompt_back(sn, T, blk):
                btiles = [(t, t * 128, 128) for t in range(T + 1)]
                yield from back_attn(sn, 128, btiles, blk, BNW)

            steps = [(0, 15, 16)] + [(1 + i, 16 + i, i) for i in range(16)]
            run_interleaved([prompt_front(*steps[0][0:2])])
            for si in range(len(steps)):
                sn, T, blk = steps[si]
                gens = [prompt_back(sn, T, blk)]
                if si + 1 < len(steps):
                    gens.append(prompt_front(*steps[si + 1][0:2]))
                run_interleaved(gens)
            checkpoint('steps')

            SN = len(steps)
            sst = SN % 2
            P.dma("sp", score[0][:, 0:2048], I["cbkT"][:, :], writes=[R_score[0]])
            P.op("act", _call("activation", out=kbi[:, 0:2048], in_=score[0][:, 0:2048], func=AF.Copy),
                 reads=[R_score[0]], writes=R_kbi[0:16])
            P.dma("sp", score[0][:, 2048:4096], I["cbiT"][:, :], writes=[R_score[0]])
            P.op("act", _call("activation", out=kbi[:, 4096:4096 + 2048], in_=score[0][:, 2048:4096], func=AF.Copy),
                 reads=[R_score[0]], writes=R_kbi[0:16])
            P.dma("sp", score[1][:, 0:2048].rearrange("p (t c) -> p t c", c=128), I["cbv"].rearrange("(t p) c -> p t c", p=128), writes=[R_score[1]])
            vball = vb_aug[:, 0:16 * 130].rearrange("p (t d) -> p t d", d=65)
            P.op("act", _call("activation", out=vball[:, :, 0:64], in_=score[1][:, 0:2048].rearrange("p (t d) -> p t d", d=64), func=AF.Copy),
                 reads=[R_score[1]], writes=R_vb[0:17])
            P.op("pool", _call("memset", vb_aug[:, 0:17 * 130].rearrange("p (t d) -> p t d", d=65)[:, :, 64:65], 1.0), writes=R_vb[0:17])
            P.dma("sp", score[0][:, 0:2048], I["cakT"][:, :], writes=[R_score[0]])
            for s4 in range(4):
                P.op("act", _call("activation", out=kaT[:, s4 * 512:(s4 + 1) * 512].rearrange("p (j t) -> p j t", t=128),
                                  in_=score[0][:, 0:2048].rearrange("p (j t) -> p j t", t=512)[:, :, s4 * 128:(s4 + 1) * 128], func=AF.Copy),
                     reads=[R_score[0]], writes=[R_ka[s4]])
            P.dma("sp", score[1][:, 2048:4096].rearrange("p (t c) -> p t c", c=512), I["cav"].rearrange("(t p) c -> p t c", p=128), writes=[R_score[1]])
            vaall = va_aug[:, 0:4 * 520].rearrange("p (t d) -> p t d", d=65)
            P.op("act", _call("activation", out=vaall[:, :, 0:64], in_=score[1][:, 2048:4096].rearrange("p (t d) -> p t d", d=64), func=AF.Copy),
                 reads=[R_score[1]], writes=R_va[0:5])
            P.op("pool", _call("memset", va_aug[:, 0:5 * 520].rearrange("p (t d) -> p t d", d=65)[:, :, 64:65], 1.0), writes=R_va[0:5])
            for hh in range(2):
                w = 4 * 528
                P.dma("sp", score[0][0:16, 0:w], I["ABs"][:, hh * w:(hh + 1) * w], writes=[R_score[0]])
                P.op("act", _call("activation", out=ABb[0:16, hh * w:(hh + 1) * w], in_=score[0][0:16, 0:w], func=AF.Copy),
                     reads=[R_score[0]], writes=[R_AB])
            P.dma("sp", score[1][0:16, 0:8 * 144], I["Bns"][:, :], writes=[R_score[1]])
            for h in range(8):
                P.op("dve", _call("tensor_scalar", out=Bnb[0:16, h * 144:(h + 1) * 144], in0=score[1][0:16, h * 144:(h + 1) * 144],
                                  scalar1=c15[0:16, h:h + 1], scalar2=None, op0=ALU.subtract),
                     reads=[R_score[1], R_cst], writes=[R_Bn])
            P.op("pool", _call("memset", qaz[sst][:, :], 0.0), writes=[R_qa[sst]])
            P.op("pool", _call("memset", qbz[sst][:, :], 0.0), writes=[R_qb[sst]])
            P.op("pool", _call("memset", qiz[sst][:, :], 0.0), writes=[R_qi[sst]])
            P.dma("sp", xstg[:, 0:128], I["xsT"][:, :], writes=[R_xstg])
            P.op("pool", _call("tensor_copy", out=xTb[0][:, 0:128], in_=xstg[:, 0:128]), reads=[R_xstg], writes=[R_xT[0]])
            xs_, R_xs = xTb[0], R_xT[0]
            bk = wrot.next()
            fm_proj(bk, xs_, R_xs, 16, C_KB, 1)
            fm_proj(bk, xs_, R_xs, 16, C_KI, 1, ocol=16)
            P.op("act", _call("activation", out=kbi[:, 2048:2064], in_=pb[bk][:, 0:16], func=AF.Copy), reads=[R_pb[bk]], writes=[R_kbi[16]])
            P.op("act", _call("activation", out=kbi[:, 4096 + 2048:4096 + 2064], in_=pb[bk][:, 16:32], func=AF.Copy), reads=[R_pb[bk]], writes=[R_kbi[16]])
            P.op("dve", _call("tensor_copy", out=ostg[0][:, 0:32], in_=pb[bk][:, 0:32]), reads=[R_pb[bk]], writes=[R_ostg[0]])
            P.dma("sp", O["sbkT"][:, :], ostg[0][:, 0:16], reads=[R_ostg[0]], defer=True)
            P.dma("sp", O["sbiT"][:, :], ostg[0][0:64, 16:32], reads=[R_ostg[0]], defer=True)
            bv_ = wrot.next()
            tm_proj(bv_, xs_, R_xs, 16, C_VB, 128)
            vbv = vb_aug[0:16, 16 * 130:17 * 130].rearrange("p (g d) -> p g d", d=65)
            P.op("act", _call("activation", out=vbv[:, :, 0:64], in_=pb[bv_][0:16, 0:128].rearrange("p (g d) -> p g d", d=64), func=AF.Copy),
                 reads=[R_pb[bv_]], writes=[R_vb[16]])
            P.op("dve", _call("tensor_copy", out=vbstg[0][0:16, :], in_=pb[bv_][0:16, 0:128]), reads=[R_pb[bv_]], writes=[R_vbstg[0]])
            P.dma("sp", O["sbv"][:, :], vbstg[0][0:16, :], reads=[R_vbstg[0]], defer=True)
            ba = wrot.next()
            fm_proj(ba, xs_, R_xs, 16, C_KA, 4)
            P.op("act", _call("activation", out=kaT[:, 4 * 512:5 * 512].rearrange("p (j t) -> p j t", t=128)[:, :, 0:16],
                              in_=pb[ba][:, 0:64].rearrange("p (j t) -> p j t", t=16), func=AF.Copy),
                 reads=[R_pb[ba]], writes=[R_ka[4]])
            P.op("dve", _call("tensor_copy", out=astg[:, 0:64], in_=pb[ba][:, 0:64]), reads=[R_pb[ba]], writes=[R_astg])
            P.dma("sp", O["sakT"][:, :], astg[:, 0:64], reads=[R_astg], defer=True)
            bva = wrot.next()
            tm_proj(bva, xs_, R_xs, 16, C_VA, 512)
            vav = va_aug[0:16, 4 * 520:5 * 520].rearrange("p (h d) -> p h d", d=65)
            P.op("act", _call("activation", out=vav[:, :, 0:64], in_=pb[bva][0:16, :].rearrange("p (h d) -> p h d", d=64), func=AF.Copy),
                 reads=[R_pb[bva]], writes=[R_va[4]])
            P.op("dve", _call("tensor_copy", out=astg[0:16, 512:1024], in_=pb[bva][0:16, :]), reads=[R_pb[bva]], writes=[R_astg])
            P.dma("sp", O["sav"][:, :], astg[0:16, 512:1024], reads=[R_astg], defer=True)
            checkpoint('sample_pre')
            wins = [(0, 128), (1, 128), (2, 128), (3, 128), (4, 16)]
            btiles = [(t, t * 128, 128) for t in range(16)] + [(16, 2048, 16)]

            def sample_all():
                yield from qside(xs_, R_xs, 16, sst)
                yield from front_attn(SN, 16, wins, btiles, False, 528)
                yield from back_attn(SN, 16, btiles, 17, 144)

            run_interleaved([sample_all()])
            checkpoint('phaseA')
            P.flush(block)

        with ExitStack() as sbk:
            wob = sb(sbk, "wob", [128, 8 * 1024], BF16)
            wmqb = sb(sbk, "wmqb", [128, 8 * 512], BF16)
            wmob = sb(sbk, "wmob", [128, 4 * 1024], BF16)
            wtmp = sb(sbk, "wtmp", [128, 8 * 512], BF16)
            R_wo, R_wmq, R_wmo, R_wtmp = Res("wo"), Res("wmq"), Res("wmo"), Res("wtmp")
            wst = [sb(sbk, "wst%d" % k, [128, 2048], F32) for k in range(2)]
            R_wst = [Res("wst%d" % k) for k in range(2)]
            lnt = sb(sbk, "lnt", [128, 4 * 1024], F32)
            R_ln = Res("ln")
            memTb = sb(sbk, "memTb", [128, 8 * 256], BF16)
            R_memT = Res("memT")
            mkT = [sb(sbk, "mkT%d" % k, [128, 4 * 256], BF16) for k in range(2)]
            mva = [sb(sbk, "mva%d" % k, [128, 2 * 4 * 129], BF16) for k in range(2)]
            R_mk = [Res("mk%d" % k) for k in range(2)]
            R_mv = [Res("mv%d" % k) for k in range(2)]
            mixl = [sb(sbk, "mixl%d" % k, [128, 1024], BF16) for k in range(2)]
            R_mixl = [Res("mixl%d" % k) for k in range(2)]
            xr = [sb(sbk, "xr%d" % k, [128, 1024], F32) for k in range(2)]
            R_xr = [Res("xr%d" % k) for k in range(2)]
            tT_l = [sb(sbk, "tT%d" % k, [128, 1024], BF16) for k in range(2)]
            hA_l = [sb(sbk, "hA%d" % k, [128, 1024], F32) for k in range(2)]
            hB_l = [sb(sbk, "hB%d" % k, [128, 1024], F32) for k in range(2)]
            h16_l = [sb(sbk, "h16%d" % k, [128, 1024], BF16) for k in range(2)]
            qmT_l = [sb(sbk, "qmT%d" % k, [128, 512], BF16) for k in range(2)]
            PTm_l = [sb(sbk, "PTm%d" % k, [128, 1024], BF16) for k in range(2)]
            o16_l = [sb(sbk, "o16%d" % k, [128, 512], BF16) for k in range(2)]
            oT_l = [sb(sbk, "oT%d" % k, [128, 512], BF16) for k in range(2)]
            stat_l = [sb(sbk, "stat%d" % k, [128, 32], F32) for k in range(2)]
            RB = [{n: Res(n + str(k)) for n in ("tT", "hA", "hB", "h16", "qm", "PTm", "o16", "oT", "stat")} for k in range(2)]
            h2T = [sb(sbk, "h2T%d" % k, [128, 1024], BF16) for k in range(2)]
            R_h2T = [Res("h2T%d" % k) for k in range(2)]
            mstg = sb(sbk, "mstg", [128, 1024], F32)
            R_mstg = Res("mstg")
            wrot = Rot([0, 1, 2, 3, 4, 5, 6, 7])

            def load_cast(dst, R_dst, src, ncols, engs=("act", "pool")):
                k = 0
                for c0 in range(0, ncols, 2048):
                    w = min(2048, ncols - c0)
                    s = k % 2
                    P.dma("sp", wst[s][:, 0:w], src[:, c0:c0 + w], writes=[R_wst[s]])
                    eng = engs[k % len(engs)]
                    if eng == "act":
                        P.op("act", _call("activation", out=dst[:, c0:c0 + w], in_=wst[s][:, 0:w], func=AF.Copy),
                             reads=[R_wst[s]], writes=[R_dst])
                    else:
                        P.op(eng, _call("tensor_copy", out=dst[:, c0:c0 + w], in_=wst[s][:, 0:w]),
                             reads=[R_wst[s]], writes=[R_dst])
                    k += 1

            load_cast(wob, R_wo, I["wo"], 8192)
            load_cast(wmqb, R_wmq, I["wmq"], 4096)
            load_cast(wmob, R_wmo, I["wmo"], 4096)
            for k in range(4):
                P.dma("sp", lnt[:, k * 1024:(k + 1) * 1024], I["lnp"][k:k + 1, :].to_broadcast([128, 1024]), writes=[R_ln])
            load_cast(memTb, R_memT, I["memT"], 2048)
            load_cast(wtmp, R_wtmp, I["wmk"], 4096)
            for h in range(4):
                bank = wrot.next()
                for kc in range(KC):
                    P.op("pe", _call("matmul",
                        out=pb[bank][:, 0:256], lhsT=wtmp[:, kc * 512 + h * 128: kc * 512 + (h + 1) * 128],
                        rhs=memTb[:, kc * 256:(kc + 1) * 256], start=(kc == 0), stop=(kc == KC - 1)),
                        reads=[R_wtmp, R_memT], writes=[R_pb[bank]])
                P.op("act", _call("activation", out=mkT[0][:, h * 256:(h + 1) * 256], in_=pb[bank][:, 0:256], func=AF.Copy),
                     reads=[R_pb[bank]], writes=[R_mk[0]])
                P.op("dve", _call("tensor_copy", out=mstg[:, h * 256:(h + 1) * 256], in_=pb[bank][:, 0:256]),
                     reads=[R_pb[bank]], writes=[R_mstg])
            P.dma("sp", O["mkT"][:, :], mstg[:, :], reads=[R_mstg], defer=True)
            load_cast(wtmp, R_wtmp, I["wmv"], 4096)
            for mt in range(2):
                bank = wrot.next()
                for kc in range(KC):
                    P.op("pe", _call("matmul",
                        out=pb[bank][:, 0:512], lhsT=memTb[:, kc * 256 + mt * 128: kc * 256 + (mt + 1) * 128],
                        rhs=wtmp[:, kc * 512:(kc + 1) * 512], start=(kc == 0), stop=(kc == KC - 1)),
                        reads=[R_wtmp, R_memT], writes=[R_pb[bank]])
                mvv = mva[0][:, mt * 516:(mt + 1) * 516].rearrange("p (h d) -> p h d", d=129)
                P.op("act", _call("activation", out=mvv[:, :, 0:128], in_=pb[bank][:, :].rearrange("p (h d) -> p h d", d=128), func=AF.Copy),
                     reads=[R_pb[bank]], writes=[R_mv[0]])
                P.op("dve", _call("tensor_copy", out=mstg[:, mt * 512:(mt + 1) * 512], in_=pb[bank][:, :]),
                     reads=[R_pb[bank]], writes=[R_mstg])
                P.dma("sp", O["mv"][mt * 128:(mt + 1) * 128, :], mstg[:, mt * 512:(mt + 1) * 512], reads=[R_mstg], defer=True)
            for k in range(2):
                P.op("pool", _call("memset", mva[k][:, :].rearrange("p (t d) -> p t d", d=129)[:, :, 128:129], 1.0), writes=[R_mv[k]])
            load_cast(mkT[1], R_mk[1], I["cmkT"], 1024)
            P.dma("sp", wst[0][:, 0:1024].rearrange("p (t c) -> p t c", c=512), I["cmv"].rearrange("(t p) c -> p t c", p=128), writes=[R_wst[0]])
            P.op("act", _call("activation", out=mva[1][:, :].rearrange("p (t d) -> p t d", d=129)[:, :, 0:128],
                                               in_=wst[0][:, 0:1024].rearrange("p (t d) -> p t d", d=128), func=AF.Copy),
                 reads=[R_wst[0]], writes=[R_mv[1]])

            checkpoint('phaseB_pre')
            def transpose_to(src16, R_src, qs, nchunk, dst, R_dst):
                bank = wrot.next()
                pbf = pb[bank][:, :].bitcast(BF16)
                for c in range(nchunk):
                    P.op("pe", _call("transpose", out=pbf[:, c * qs:(c + 1) * qs], in_=src16[0:qs, c * 128:(c + 1) * 128],
                                                                   identity=ident[0:qs, 0:qs]),
                         reads=[R_src, R_ident], writes=[R_pb[bank]])
                P.op("act", _call("activation", out=dst[:, 0:nchunk * qs], in_=pbf[:, 0:nchunk * qs], func=AF.Copy),
                     reads=[R_pb[bank]], writes=[R_dst])

            def layer_norm(hin, R_hin, qs, gcol, hout, R_hout, stat, R_stat):
                for c in range(2):
                    P.op("dve", _call("bn_stats", out=stat[0:qs, c * 6:(c + 1) * 6], in_=hin[0:qs, c * 512:(c + 1) * 512]),
                         reads=[R_hin], writes=[R_stat])
                P.op("dve", _call("bn_aggr", out=stat[0:qs, 12:14], in_=stat[0:qs, 0:12]), reads=[R_stat], writes=[R_stat])
                P.op("dve", _call("tensor_scalar", out=stat[0:qs, 14:15], in0=stat[0:qs, 13:14], scalar1=LN_EPS, scalar2=None, op0=ALU.add),
                     reads=[R_stat], writes=[R_stat])
                P.op("act", _call("activation", out=stat[0:qs, 15:16], in_=stat[0:qs, 14:15], func=AF.Sqrt), reads=[R_stat], writes=[R_stat])
                P.op("dve", _call("reciprocal", out=stat[0:qs, 16:17], in_=stat[0:qs, 15:16]), reads=[R_stat], writes=[R_stat])
                P.op("dve", _call("scalar_tensor_tensor", out=stat[0:qs, 17:18], in0=stat[0:qs, 12:13], scalar=-1.0, in1=stat[0:qs, 16:17],
                                                             op0=ALU.mult, op1=ALU.mult),
                     reads=[R_stat], writes=[R_stat])
                P.op("act", _call("activation", out=hout[0:qs, :], in_=hin[0:qs, :], func=AF.Identity, scale=stat[0:qs, 16:17], bias=stat[0:qs, 17:18]),
                     reads=[R_hin, R_stat], writes=[R_hout])
                P.op("dve", _call("tensor_tensor", out=hout[0:qs, :], in0=hout[0:qs, :], in1=lnt[0:qs, gcol * 1024:(gcol + 1) * 1024], op=ALU.mult),
                     reads=[R_hout, R_ln], writes=[R_hout])
                P.op("dve", _call("tensor_tensor", out=hout[0:qs, :], in0=hout[0:qs, :], in1=lnt[0:qs, (gcol + 1) * 1024:(gcol + 2) * 1024], op=ALU.add),
                     reads=[R_hout, R_ln], writes=[R_hout])

            def phaseB_block(blk, qs, row0, mi, k2):
                s = k2
                tT, hA, hB, h16, qmT, PTm, o16, oT, stat = (tT_l[k2], hA_l[k2], hB_l[k2], h16_l[k2], qmT_l[k2], PTm_l[k2], o16_l[k2],
                                                             oT_l[k2], stat_l[k2])
                R_tT, R_hA, R_hB, R_h16, R_qm, R_PTm, R_o16, R_oT, R_stat = (RB[k2][n] for n in ("tT", "hA", "hB", "h16", "qm", "PTm", "o16", "oT", "stat"))
                P.dma("sp", mixl[s][0:qs, :], mixD[blk * 128 + row0: blk * 128 + row0 + qs, :], reads=[R_mixD[blk]], writes=[R_mixl[s]])
                P.dma("sp", xr[s][0:qs, :], I["xres"][blk * 128: blk * 128 + qs, :], writes=[R_xr[s]])
                transpose_to(mixl[s], R_mixl[s], qs, 8, tT, R_tT)
                yield
                b0, b1 = wrot.next(), wrot.next()
                for n, bank in enumerate((b0, b1)):
                    for kc in range(KC):
                        P.op("pe", _call("matmul",
                            out=pb[bank][0:qs, :], lhsT=tT[:, kc * qs:(kc + 1) * qs], rhs=wob[:, kc * 1024 + n * 512: kc * 1024 + (n + 1) * 512],
                            start=(kc == 0), stop=(kc == KC - 1)),
                            reads=[R_tT, R_wo], writes=[R_pb[bank]])
                    P.op("dve", _call("scalar_tensor_tensor",
                        out=hA[0:qs, n * 512:(n + 1) * 512], in0=xr[s][0:qs, n * 512:(n + 1) * 512], scalar=ALPHA, in1=pb[bank][0:qs, :],
                        op0=ALU.mult, op1=ALU.add),
                        reads=[R_xr[s], R_pb[bank]], writes=[R_hA])
                yield
                layer_norm(hA, R_hA, qs, 0, hB, R_hB, stat, R_stat)
                yield
                P.op("pool", _call("tensor_copy", out=h16[0:qs, :], in_=hB[0:qs, :]), reads=[R_hB], writes=[R_h16])
                transpose_to(h16, R_h16, qs, 8, tT, R_tT)
                yield
                bq = wrot.next()
                for h in range(4):
                    for kc in range(KC):
                        P.op("pe", _call("matmul",
                            out=pb[bq][:, h * qs:(h + 1) * qs], lhsT=wmqb[:, kc * 512 + h * 128: kc * 512 + (h + 1) * 128],
                            rhs=tT[:, kc * qs:(kc + 1) * qs], start=(kc == 0), stop=(kc == KC - 1)),
                            reads=[R_wmq, R_tT], writes=[R_pb[bq]])
                P.op("act", _call("activation", out=qmT[:, 0:4 * qs], in_=pb[bq][:, 0:4 * qs], func=AF.Copy, scale=float(128.0 ** -0.5)),
                     reads=[R_pb[bq]], writes=[R_qm])
                yield
                bs0, bs1 = wrot.next(), wrot.next()
                for h in range(4):
                    for mt in range(2):
                        idx = h * 2 + mt
                        bank = bs0 if idx < 4 else bs1
                        c0 = (idx % 4) * qs
                        P.op("pe", _call("matmul",
                            out=pb[bank][:, c0:c0 + qs], lhsT=mkT[mi][:, h * 256 + mt * 128: h * 256 + (mt + 1) * 128],
                            rhs=qmT[:, h * qs:(h + 1) * qs], start=True, stop=True),
                            reads=[R_mk[mi], R_qm], writes=[R_pb[bank]])
                for k, bank in enumerate((bs0, bs1)):
                    P.op("act", _call("activation", out=PTm[:, k * 4 * qs:(k + 1) * 4 * qs], in_=pb[bank][:, 0:4 * qs], func=AF.Exp),
                         reads=[R_pb[bank]], writes=[R_PTm])
                yield
                bo0, bo1 = wrot.next(), wrot.next()
                for h in range(4):
                    bank = bo0 if h < 2 else bo1
                    for mt in range(2):
                        idx = h * 2 + mt
                        P.op("pe", _call("matmul",
                            out=pb[bank][0:qs, (h % 2) * 129:(h % 2) * 129 + 129], lhsT=PTm[:, idx * qs:(idx + 1) * qs],
                            rhs=mva[mi][:, (mt * 4 + h) * 129:(mt * 4 + h) * 129 + 129],
                            start=(h % 2 == 0 and mt == 0), stop=(mt == 1), skip_group_check=True),
                            reads=[R_PTm, R_mv[mi]], writes=[R_pb[bank]])
                for k, bank in enumerate((bo0, bo1)):
                    ov = pb[bank][0:qs, 0:258].rearrange("p (h d) -> p h d", d=129)
                    P.op("dve", _call("tensor_scalar", out=stat[0:qs, 20 + 2 * k:22 + 2 * k].rearrange("p (h o) -> p h o", o=1),
                                                                      in0=ov[:, :, 128:129], scalar1=1e-30, scalar2=None, op0=ALU.max),
                         reads=[R_pb[bank]], writes=[R_stat])
                    P.op("dve", _call("reciprocal", out=stat[0:qs, 20 + 2 * k:22 + 2 * k], in_=stat[0:qs, 20 + 2 * k:22 + 2 * k]),
                         reads=[R_stat], writes=[R_stat])
                    for hh in range(2):
                        h = k * 2 + hh
                        P.op("dve", _call("tensor_scalar",
                            out=o16[0:qs, h * 128:(h + 1) * 128], in0=pb[bank][0:qs, hh * 129: hh * 129 + 128],
                            scalar1=stat[0:qs, 20 + 2 * k + hh:21 + 2 * k + hh], scalar2=None, op0=ALU.mult),
                            reads=[R_pb[bank], R_stat], writes=[R_o16])
                yield
                transpose_to(o16, R_o16, qs, 4, oT, R_oT)
                yield
                b0, b1 = wrot.next(), wrot.next()
                for n, bank in enumerate((b0, b1)):
                    for c in range(4):
                        P.op("pe", _call("matmul",
                            out=pb[bank][0:qs, :], lhsT=oT[:, c * qs:(c + 1) * qs], rhs=wmob[:, c * 1024 + n * 512: c * 1024 + (n + 1) * 512],
                            start=(c == 0), stop=(c == 3)),
                            reads=[R_oT, R_wmo], writes=[R_pb[bank]])
                    P.op("dve", _call("scalar_tensor_tensor",
                        out=hA[0:qs, n * 512:(n + 1) * 512], in0=hB[0:qs, n * 512:(n + 1) * 512], scalar=ALPHA, in1=pb[bank][0:qs, :],
                        op0=ALU.mult, op1=ALU.add),
                        reads=[R_hB, R_pb[bank]], writes=[R_hA])
                yield
                layer_norm(hA, R_hA, qs, 2, hB, R_hB, stat, R_stat)
                yield
                P.dma("sp", h2D[blk * 128: blk * 128 + qs, :], hB[0:qs, :], reads=[R_hB], writes=[R_h2D[blk]], defer=True)
                P.op("pool", _call("tensor_copy", out=h16[0:qs, :], in_=hB[0:qs, :]), reads=[R_hB], writes=[R_h16])
                transpose_to(h16, R_h16, qs, 8, h2T[s], R_h2T[s])
                P.dma("sp", h2TD[blk][:, 0:8 * qs], h2T[s][:, 0:8 * qs], reads=[R_h2T[s]], writes=[R_h2TD[blk]], defer=True)
                yield

            def run_staggered(gens, lag):
                active = []
                pending = list(gens)
                tick = 0
                while active or pending:
                    if pending and (not active or tick >= lag):
                        active.append(pending.pop(0))
                        tick = 0
                    for g in list(active):
                        try:
                            next(g)
                        except StopIteration:
                            active.remove(g)
                    tick += 1

            blocks = [(16, 2, 126, 0), (17, 16, 0, 1)] + [(i, 128, 0, 0) for i in range(16)]
            run_staggered([phaseB_block(b_, q_, r_, m_, pos % 2) for pos, (b_, q_, r_, m_) in enumerate(blocks)], 7)
            checkpoint('phaseB')
            P.flush(block)

        with ExitStack() as sc:
            wdb = sb(sc, "wdb", [128, NFC * 1024], BF16)
            R_wd = Res("wd")
            wst = [sb(sc, "wstc%d" % k, [128, 2048], F32) for k in range(2)]
            R_wst = [Res("wstc%d" % k) for k in range(2)]
            wsl = [sb(sc, "wsl%d" % k, [128, 2048], BF16) for k in range(2)]
            R_wsl = [Res("wsl%d" % k) for k in range(2)]
            R_wslB = [Res("wslB%d" % k) for k in range(2)]
            hT2 = [sb(sc, "hT%d" % k, [128, NFC * 512], BF16) for k in range(2)]
            R_hT2 = [Res("hT%d" % k) for k in range(2)]
            hTm = sb(sc, "hTm", [128, NFC * 16], BF16)
            R_hTm = Res("hTm")
            h2Tg = [sb(sc, "h2Tg%d" % k, [128, 8 * 512], BF16) for k in range(2)]
            R_h2Tg = [Res("h2Tg%d" % k) for k in range(2)]
            h2Tm = sb(sc, "h2Tm", [128, 8 * 18], BF16)
            R_h2Tm = Res("h2Tm")
            Gb = [sb(sc, "Gb%d" % k, [128, 514], F32) for k in range(3)]
            R_Gb = [Res("Gb%d" % k) for k in range(3)]
            Gs = sb(sc, "Gs", [128, 18], F32)
            R_Gs = Res("Gs")
            t0b = [sb(sc, "t0b%d" % k, [128, 512], F32) for k in range(3)]
            R_t0 = [Res("t0%d" % k) for k in range(3)]
            geb = [sb(sc, "geb%d" % k, [128, 512], F32) for k in range(3)]
            R_ge = [Res("ge%d" % k) for k in range(3)]
            t1b = [sb(sc, "t1b%d" % k, [128, 512], F32) for k in range(3)]
            R_t1b = [Res("t1b%d" % k) for k in range(3)]
            t2b = [sb(sc, "t2b%d" % k, [128, 512], F32) for k in range(3)]
            R_t2b = [Res("t2b%d" % k) for k in range(3)]
            t0s = sb(sc, "t0s", [128, 16], F32)
            ges = sb(sc, "ges", [128, 16], F32)
            R_ts = Res("ts")
            carry = sb(sc, "carry", [128, NFC * 2], F32)
            R_carry = [Res("carry%d" % c) for c in range(NFC)]
            sfc = sb(sc, "sfc", [128, NFC * 2], F32)
            R_sfc = Res("sfc")
            sconv = sb(sc, "sconv", [128, NFC * 2], F32)
            wconv = sb(sc, "wconv", [128, NFC * 3], F32)
            bconv = sb(sc, "bconv", [128, NFC], F32)
            flag = sb(sc, "flag", [128, 1], F32)
            R_cc = Res("cc")
            ln3 = sb(sc, "ln3", [128, 2 * 1024], F32)
            R_ln3 = Res("ln3")
            h2r = [sb(sc, "h2r%d" % k, [128, 1024], F32) for k in range(2)]
            R_h2r = [Res("h2r%d" % k) for k in range(2)]
            yA = sb(sc, "yA", [128, 1024], F32)
            R_yA = Res("yA")
            yB = [sb(sc, "yB%d" % k, [128, 1024], F32) for k in range(2)]
            R_yB = [Res("yB%d" % k) for k in range(2)]
            stat = sb(sc, "statc", [128, 32], F32)
            R_stat = Res("statc")

            P.dma("sp", sconv[:, :], I["sconvT"][:, :], writes=[R_cc])
            P.dma("sp", wconv[:, :], I["wconvT"][:, :], writes=[R_cc])
            P.dma("sp", bconv[:, :], I["bconvT"][:, :], writes=[R_cc])
            P.dma("sp", flag[:, :], I["flag"][:, :], writes=[R_cc])
            for k in range(2):
                P.dma("sp", ln3[:, k * 1024:(k + 1) * 1024], I["lnp"][4 + k:5 + k, :].to_broadcast([128, 1024]), writes=[R_ln3])
            k = 0
            for c0 in range(0, NFC * 1024, 2048):
                s = k % 2
                P.dma("sp", wst[s][:, :], I["wdown"][:, c0:c0 + 2048], writes=[R_wst[s]])
                if k % 2 == 0:
                    P.op("act", _call("activation", out=wdb[:, c0:c0 + 2048], in_=wst[s][:, :], func=AF.Copy), reads=[R_wst[s]], writes=[R_wd])
                else:
                    P.op("pool", _call("tensor_copy", out=wdb[:, c0:c0 + 2048], in_=wst[s][:, :]), reads=[R_wst[s]], writes=[R_wd])
                k += 1
            P.dma("sp", h2Tm[:, :].rearrange("p (c q) -> p c q", q=18)[:, :, 0:2], h2TD[16][:, 0:16].rearrange("p (c q) -> p c q", q=2),
                  reads=[R_h2TD[16]], writes=[R_h2Tm], slow=True)
            P.dma("sp", h2Tm[:, :].rearrange("p (c q) -> p c q", q=18)[:, :, 2:18], h2TD[17][:, 0:128].rearrange("p (c q) -> p c q", q=16),
                  reads=[R_h2TD[17]], writes=[R_h2Tm], slow=True)

            checkpoint('phaseC_pre')
            UB = [0, 2, 4]
            GBK = [1, 3, 5]
            MB = 7
            YB = [6, 7]
            wk = [0]

            def ln3_out(pre_banks, qs, h2src, R_h2src, dst_ap, ys, R_ys):
                for n, bank in enumerate(pre_banks):
                    P.op("dve", _call("scalar_tensor_tensor",
                        out=yA[0:qs, n * 512:(n + 1) * 512], in0=h2src[0:qs, n * 512:(n + 1) * 512], scalar=ALPHA, in1=pb[bank][0:qs, :],
                        op0=ALU.mult, op1=ALU.add),
                        reads=[R_h2src, R_pb[bank]], writes=[R_yA])
                for c in range(2):
                    P.op("dve", _call("bn_stats", out=stat[0:qs, c * 6:(c + 1) * 6], in_=yA[0:qs, c * 512:(c + 1) * 512]),
                         reads=[R_yA], writes=[R_stat])
                P.op("dve", _call("bn_aggr", out=stat[0:qs, 12:14], in_=stat[0:qs, 0:12]), reads=[R_stat], writes=[R_stat])
                P.op("dve", _call("tensor_scalar", out=stat[0:qs, 14:15], in0=stat[0:qs, 13:14], scalar1=LN_EPS, scalar2=None, op0=ALU.add),
                     reads=[R_stat], writes=[R_stat])
                P.op("act", _call("activation", out=stat[0:qs, 15:16], in_=stat[0:qs, 14:15], func=AF.Sqrt), reads=[R_stat], writes=[R_stat])
                P.op("dve", _call("reciprocal", out=stat[0:qs, 16:17], in_=stat[0:qs, 15:16]), reads=[R_stat], writes=[R_stat])
                P.op("dve", _call("scalar_tensor_tensor", out=stat[0:qs, 17:18], in0=stat[0:qs, 12:13], scalar=-1.0, in1=stat[0:qs, 16:17],
                                                             op0=ALU.mult, op1=ALU.mult),
                     reads=[R_stat], writes=[R_stat])
                P.op("act", _call("activation", out=ys[0:qs, :], in_=yA[0:qs, :], func=AF.Identity, scale=stat[0:qs, 16:17], bias=stat[0:qs, 17:18]),
                     reads=[R_yA, R_stat], writes=[R_ys])
                P.op("pool", _call("tensor_tensor", out=ys[0:qs, :], in0=ys[0:qs, :], in1=ln3[0:qs, 0:1024], op=ALU.mult),
                     reads=[R_ys, R_ln3], writes=[R_ys])
                P.op("pool", _call("tensor_tensor", out=ys[0:qs, :], in0=ys[0:qs, :], in1=ln3[0:qs, 1024:2048], op=ALU.add),
                     reads=[R_ys, R_ln3], writes=[R_ys])
                P.dma("sp", dst_ap, ys[0:qs, :], reads=[R_ys], defer=True)

            def load_h2Tg(grp):
                gs = grp % 2
                for bi in range(4):
                    blk = grp * 4 + bi
                    P.dma("sp", h2Tg[gs][:, :].rearrange("p (c q) -> p c q", q=512)[:, :, bi * 128:(bi + 1) * 128],
                          h2TD[blk][:, :].rearrange("p (c q) -> p c q", q=128), reads=[R_h2TD[blk]], writes=[R_h2Tg[gs]])

            def c_s1(grp, c):
                s = (grp * NFC + c) % 2
                P.dma("sp", wst[s][:, :], I["wup"][c], writes=[R_wst[s]])
                P.op("pool", _call("tensor_copy", out=wsl[s][:, 0:1152], in_=wst[s][:, 0:1152]), reads=[R_wst[s]], writes=[R_wsl[s]])
                P.op("dve", _call("tensor_copy", out=wsl[s][:, 1152:2048], in_=wst[s][:, 1152:2048]), reads=[R_wst[s]], writes=[R_wslB[s]])

            def c_s2(grp, c):
                s = (grp * NFC + c) % 2
                gs = grp % 2
                mo = (c % 2) * 64
                if grp == 0:
                    for part, oc in ((0, mo), (1, mo + 32)):
                        for kc in range(KC):
                            P.op("pe", _call("matmul", out=pb[MB][:, oc:oc + 18], lhsT=wsl[s][:, kc * 256 + part * 128: kc * 256 + (part + 1) * 128],
                                             rhs=h2Tm[:, kc * 18:(kc + 1) * 18], start=(kc == 0), stop=(kc == KC - 1)),
                                 reads=[R_wsl[s], R_wslB[s], R_h2Tm], writes=[R_pb[MB]])
                k3 = (grp * NFC + c) % 3
                ub, gbk = UB[k3], GBK[k3]
                for part, bank in ((0, ub), (1, gbk)):
                    for kc in range(KC):
                        P.op("pe", _call("matmul", out=pb[bank][:, :], lhsT=wsl[s][:, kc * 256 + part * 128: kc * 256 + (part + 1) * 128],
                                         rhs=h2Tg[gs][:, kc * 512:(kc + 1) * 512], start=(kc == 0), stop=(kc == KC - 1)),
                             reads=[R_wsl[s], R_wslB[s], R_h2Tg[gs]], writes=[R_pb[bank]])

            def c_s3(grp, c):
                hTg, R_hTg = hT2[grp % 2], R_hT2[grp % 2]
                mo = (c % 2) * 64
                if grp == 0:
                    P.op("dve", _call("tensor_scalar", out=carry[:, c * 2:(c + 1) * 2], in0=pb[MB][:, mo + 32:mo + 34], scalar1=flag[:, 0:1],
                                      scalar2=None, op0=ALU.mult),
                         reads=[R_pb[MB], R_cc], writes=[R_carry[c]])
                    P.op("act", _call("activation", out=Gs[:, 0:2], in_=sconv[:, c * 2:(c + 1) * 2], func=AF.Copy), reads=[R_cc], writes=[R_Gs])
                    P.op("act", _call("activation", out=Gs[:, 2:18], in_=pb[MB][:, mo + 34:mo + 50], func=AF.Copy), reads=[R_pb[MB]], writes=[R_Gs])
                    P.op("act", _call("activation", out=t0s[:, :], in_=Gs[:, 2:18], func=AF.Identity, scale=wconv[:, c * 3 + 2:c * 3 + 3],
                                      bias=bconv[:, c:c + 1]),
                         reads=[R_Gs, R_cc], writes=[R_ts])
                    P.op("dve", _call("scalar_tensor_tensor", out=t0s[:, :], in0=Gs[:, 1:17], scalar=wconv[:, c * 3 + 1:c * 3 + 2], in1=t0s[:, :],
                                      op0=ALU.mult, op1=ALU.add),
                         reads=[R_Gs, R_cc, R_ts], writes=[R_ts])
                    P.op("dve", _call("scalar_tensor_tensor", out=t0s[:, :], in0=Gs[:, 0:16], scalar=wconv[:, c * 3:c * 3 + 1], in1=t0s[:, :],
                                      op0=ALU.mult, op1=ALU.add),
                         reads=[R_Gs, R_cc, R_ts], writes=[R_ts])
                    P.op("act", _call("activation", out=ges[:, :], in_=t0s[:, :], func=AF.Gelu_apprx_tanh), reads=[R_ts], writes=[R_ts])
                    P.op("dve", _call("tensor_tensor", out=hTm[:, c * 16:(c + 1) * 16], in0=pb[MB][:, mo + 2:mo + 18], in1=ges[:, :], op=ALU.mult),
                         reads=[R_pb[MB], R_ts], writes=[R_hTm])
                    P.op("act", _call("activation", out=sfc[:, c * 2:(c + 1) * 2], in_=Gs[:, 16:18], func=AF.Copy), reads=[R_Gs], writes=[R_sfc])
                k3 = (grp * NFC + c) % 3
                ub, gbk = UB[k3], GBK[k3]
                G, R_G = Gb[k3], R_Gb[k3]
                t0, R_t = t0b[k3], R_t0[k3]
                ge, R_g = geb[k3], R_ge[k3]
                t1, R_t1 = t1b[k3], R_t1b[k3]
                t2, R_t2 = t2b[k3], R_t2b[k3]
                P.op("act", _call("activation", out=G[:, 0:2], in_=carry[:, c * 2:(c + 1) * 2], func=AF.Copy),
                     reads=[R_carry[c]], writes=[R_G])
                P.op("act", _call("activation", out=G[:, 2:514], in_=pb[gbk][:, :], func=AF.Copy), reads=[R_pb[gbk]], writes=[R_G])
                P.op("act", _call("activation", out=carry[:, c * 2:(c + 1) * 2], in_=G[:, 512:514], func=AF.Copy),
                     reads=[R_G], writes=[R_carry[c]])
                P.op("act", _call("activation", out=t0[:, :], in_=G[:, 2:514], func=AF.Identity,
                                  scale=wconv[:, c * 3 + 2:c * 3 + 3], bias=bconv[:, c:c + 1]),
                     reads=[R_G, R_cc], writes=[R_t])
                P.op("act", _call("activation", out=t1[:, :], in_=G[:, 1:513], func=AF.Identity, scale=wconv[:, c * 3 + 1:c * 3 + 2]),
                     reads=[R_G, R_cc], writes=[R_t1])
                P.op("act", _call("activation", out=t2[:, :], in_=G[:, 0:512], func=AF.Identity, scale=wconv[:, c * 3:c * 3 + 1]),
                     reads=[R_G, R_cc], writes=[R_t2])
                P.op("dve", _call("tensor_tensor", out=t0[:, :], in0=t0[:, :], in1=t1[:, :], op=ALU.add), reads=[R_t, R_t1], writes=[R_t])
                P.op("dve", _call("tensor_tensor", out=t0[:, :], in0=t0[:, :], in1=t2[:, :], op=ALU.add), reads=[R_t, R_t2], writes=[R_t])
                P.op("act", _call("activation", out=ge[:, :], in_=t0[:, :], func=AF.Gelu_apprx_tanh), reads=[R_t], writes=[R_g])
                P.op("dve", _call("tensor_tensor", out=hTg[:, c * 512:(c + 1) * 512], in0=pb[ub][:, :], in1=ge[:, :], op=ALU.mult),
                     reads=[R_pb[ub], R_g], writes=[R_hTg])

            def c_down(grp):
                hTg, R_hTg = hT2[grp % 2], R_hT2[grp % 2]
                if grp == 0:
                    for n, bank in enumerate(YB):
                        for c in range(NFC):
                            P.op("pe", _call("matmul", out=pb[bank][0:16, :], lhsT=hTm[:, c * 16:(c + 1) * 16],
                                             rhs=wdb[:, c * 1024 + n * 512: c * 1024 + (n + 1) * 512], start=(c == 0), stop=(c == NFC - 1)),
                                 reads=[R_hTm, R_wd], writes=[R_pb[bank]])
                    P.dma("sp", h2r[0][0:16, :], h2D[17 * 128: 17 * 128 + 16, :], reads=[R_h2D[17]], writes=[R_h2r[0]])
                    ln3_out(YB, 16, h2r[0], R_h2r[0], O["ys"][:, :], yB[0], R_yB[0])
                    P.dma("sp", O["sfcT"][:, :], sfc[:, :], reads=[R_sfc], defer=True)
                for bi in range(4):
                    blk = grp * 4 + bi
                    hs = blk % 2
                    P.dma("sp", h2r[hs][:, :], h2D[blk * 128:(blk + 1) * 128, :], reads=[R_h2D[blk]], writes=[R_h2r[hs]])
                    for n, bank in enumerate(YB):
                        for c in range(NFC):
                            P.op("pe", _call("matmul", out=pb[bank][:, :], lhsT=hTg[:, c * 512 + bi * 128: c * 512 + (bi + 1) * 128],
                                             rhs=wdb[:, c * 1024 + n * 512: c * 1024 + (n + 1) * 512], start=(c == 0), stop=(c == NFC - 1)),
                                 reads=[R_hTg, R_wd], writes=[R_pb[bank]])
                    ln3_out(YB, 128, h2r[hs], R_h2r[hs], O["y"][blk * 128:(blk + 1) * 128, :], yB[hs], R_yB[hs])

            seq = [(grp, c) for grp in range(4) for c in range(NFC)]
            nseq = len(seq)
            load_h2Tg(0)
            load_h2Tg(1)
            for idx in range(nseq + 2):
                if idx < nseq:
                    c_s1(*seq[idx])
                if 1 <= idx <= nseq:
                    c_s2(*seq[idx - 1])
                if idx >= 2:
                    g3, c3 = seq[idx - 2]
                    c_s3(g3, c3)
                    if c3 == NFC - 1:
                        c_down(g3)
                        if g3 + 2 < 4:
                            load_h2Tg(g3 + 2)
            P.dma("sp", O["fcT"][:, :], carry[:, :], reads=R_carry, defer=True)
            P.finish()
            P.flush(block)
    return nc


def _t5_bucket(rel):
    half, max_exact = 16, 8
    n = np.abs(rel)
    log_ratio = np.log(np.maximum(n, 1).astype(np.float32) / max_exact) / math.log(128 / max_exact)
    large = np.minimum(max_exact + (log_ratio * (half - max_exact)).astype(np.int32), half - 1)
    return np.where(rel < 0, half, 0) + np.where(n < max_exact, n, large)


def _host_inputs(inp):
    f32 = np.float32
    x_prompt = np.asarray(inp["x_prompt"], f32)
    x_sample = np.asarray(inp["x_sample"], f32)
    w_in = np.asarray(inp["w_in"], f32)[0]
    qa, ka, va = w_in[:, 0:512], w_in[:, 512:1024], w_in[:, 1024:1536]
    qb, kb, vb = w_in[:, 1536:2048], w_in[:, 2048:2176], w_in[:, 2176:2304]
    qi, ki, wi = w_in[:, 2304:2816], w_in[:, 2816:2880], w_in[:, 2880:2888]
    qbp = np.concatenate([np.concatenate([qb[:, r * 64:(r + 1) * 64], qb[:, (4 + r) * 64:(5 + r) * 64]], axis=1) for r in range(4)], axis=1)
    winp = np.concatenate([qa, ka, qbp, kb, qi, ki, ki, va, vb, wi], axis=1)
    assert winp.shape[1] == NCOL

    def kc_layout(w):
        n = w.shape[1]
        return np.ascontiguousarray(w.reshape(8, 128, n).transpose(1, 0, 2).reshape(128, 8 * n))

    shared = {}
    shared["win"] = kc_layout(winp)
    shared["wo"] = kc_layout(np.asarray(inp["w_o"], f32)[0])
    shared["wmq"] = kc_layout(np.asarray(inp["w_mq"], f32)[0])
    shared["wmk"] = kc_layout(np.asarray(inp["w_mk"], f32)[0])
    shared["wmv"] = kc_layout(np.asarray(inp["w_mv"], f32)[0])
    wmo = np.asarray(inp["w_mo"], f32)[0]
    shared["wmo"] = np.ascontiguousarray(wmo.reshape(4, 128, 1024).transpose(1, 0, 2).reshape(128, 4096))
    w_up = np.asarray(inp["w_up"], f32)[0]
    wu = w_up[:, :DFF].reshape(8, 128, NFC, 128)
    wg = w_up[:, DFF:].reshape(8, 128, NFC, 128)
    wup = np.stack([wu, wg], axis=3)
    shared["wup"] = np.ascontiguousarray(wup.transpose(2, 1, 0, 3, 4).reshape(NFC, 128, 8 * 256))
    w_down = np.asarray(inp["w_down"], f32)[0]
    shared["wdown"] = np.ascontiguousarray(w_down.reshape(NFC, 128, 1024).transpose(1, 0, 2).reshape(128, NFC * 1024))
    shared["lnp"] = np.ascontiguousarray(np.stack([np.asarray(inp[k], f32)[0] for k in ("ln1_g", "ln1_b", "ln2_g", "ln2_b", "ln3_g", "ln3_b")]))
    w_conv = np.asarray(inp["w_conv"], f32)[0]
    shared["wconvT"] = np.ascontiguousarray(w_conv.reshape(3, NFC, 128).transpose(2, 1, 0).reshape(128, NFC * 3))
    shared["bconvT"] = np.ascontiguousarray(np.asarray(inp["b_conv"], f32)[0].reshape(NFC, 128).T)
    shared["ident"] = np.eye(128, dtype=f32)
    tabA = np.asarray(inp["a_rel_bias"], f32)[0]
    qq = np.arange(128)[:, None]
    kk = np.arange(640)[None, :]
    kpos = kk - 512
    rel = qq - kpos
    cq = qq // 64
    kch = np.floor_divide(kpos, 64)
    allowed = (kch >= cq - 8) & (kch <= cq)
    bias = tabA[np.clip(rel, -64, 64) + 64]
    AB = np.where(allowed[:, :, None], bias, f32(NEGM)).astype(f32)
    shared["AB"] = np.ascontiguousarray(AB.transpose(0, 2, 1).reshape(128, 8 * ABW))
    js = np.arange(16)[:, None]
    ks = np.arange(528)[None, :]
    ABs = tabA[np.clip(512 + js - ks, -64, 64) + 64]
    shared["ABs"] = np.ascontiguousarray(ABs.transpose(0, 2, 1).reshape(16, 8 * 528)).astype(f32)
    t5 = np.asarray(inp["t5_bias"], f32)
    relB = np.arange(128)[:, None] - np.arange(256)[None, :] + 128
    Bn = t5[_t5_bucket(relB)]
    shared["Bn"] = np.ascontiguousarray(Bn.transpose(0, 2, 1).reshape(128, 8 * BNW)).astype(f32)
    relBs = 128 + np.arange(16)[:, None] - np.arange(144)[None, :]
    Bns = t5[_t5_bucket(relBs)]
    shared["Bns"] = np.ascontiguousarray(Bns.transpose(0, 2, 1).reshape(16, 8 * 144)).astype(f32)
    shared["C15"] = np.ascontiguousarray(np.broadcast_to(t5[15][None, :], (128, 8))).astype(f32)
    dm = np.zeros((128, 128), f32)
    dm[0:64, 64:128] = NEGM
    shared["diagmask"] = dm

    mem_prompt = np.asarray(inp["mem_prompt"], f32)
    maps = []
    for c in range(8):
        b, half = c // 2, c % 2
        m = dict(shared)
        xk = np.zeros((4096, 1024), f32)
        if half == 1:
            xk[:] = x_prompt[b]
        else:
            xk[2048:] = x_prompt[b, :2048]
        m["xkT"] = np.ascontiguousarray(xk.reshape(32, 128, 8, 128).transpose(0, 3, 2, 1).reshape(32, 128, 1024))
        xs = x_sample[c]
        m["xsT"] = np.ascontiguousarray(xs.reshape(16, 8, 128).transpose(2, 1, 0).reshape(128, 128))
        xres = np.zeros((NBLK * 128, 1024), f32)
        xres[0:2048] = xk[2048:]
        xres[2048:2050] = xk[2046:2048]
        xres[17 * 128:17 * 128 + 16] = xs
        m["xres"] = xres
        m["memT"] = np.ascontiguousarray(mem_prompt[b].reshape(256, 8, 128).transpose(2, 1, 0).reshape(128, 2048))
        cmk = np.asarray(inp["cache_mem_k"], f32)[0, c]
        m["cmkT"] = np.ascontiguousarray(cmk.transpose(2, 1, 0).reshape(128, 1024))
        m["cmv"] = np.ascontiguousarray(np.asarray(inp["cache_mem_v"], f32)[0, c].reshape(256, 512))
        cak = np.asarray(inp["cache_a_k"], f32)[0, c]
        m["cakT"] = np.ascontiguousarray(cak.reshape(512, 4, 2, 64).transpose(2, 3, 1, 0).reshape(128, 2048))
        m["cav"] = np.ascontiguousarray(np.asarray(inp["cache_a_v"], f32)[0, c].reshape(512, 512))
        cbk = np.asarray(inp["cache_b_k"], f32)[0, c]
        m["cbkT"] = np.ascontiguousarray(cbk.reshape(2048, 128).T)
        m["cbv"] = np.ascontiguousarray(np.asarray(inp["cache_b_v"], f32)[0, c].reshape(2048, 128))
        cbi = np.asarray(inp["cache_b_kidx"], f32)[0, c]
        m["cbiT"] = np.ascontiguousarray(np.concatenate([cbi.T, cbi.T], axis=0))
        sc_ = np.asarray(inp["state_ffn_conv"], f32)[0, c]
        m["sconvT"] = np.ascontiguousarray(sc_.reshape(2, NFC, 128).transpose(2, 1, 0).reshape(128, NFC * 2))
        m["colmask"] = np.full((128, 1), NEGM if half == 0 else 0.0, f32)
        kv = np.ones((128, NT), f32)
        if half == 0:
            kv[:, 0:16] = 0.0
        m["kvalid"] = kv
        m["flag"] = np.full((128, 1), float(half), f32)
        maps.append(m)
    return maps


_NC_CACHE = {}


def _run(inputs, debug=False):
    key = bool(debug)
    if key not in _NC_CACHE:
        _NC_CACHE[key] = build_program(debug=debug)
    nc = _NC_CACHE[key]
    maps = _host_inputs(inputs)
    res = run_bass_kernel_spmd(nc, maps, core_ids=list(range(8)))
    return res.results


def kernel(**inputs):
    R = _run(inputs)
    f32 = np.float32
    y = np.zeros((4, 4096, 1024), f32)
    ys = np.zeros((8, 16, 1024), f32)
    pak = np.zeros((1, 4, 512, 8, 64), f32)
    pav = np.zeros((1, 4, 512, 8, 64), f32)
    pbk = np.zeros((1, 4, 4096, 2, 64), f32)
    pbv = np.zeros((1, 4, 4096, 2, 64), f32)
    pbi = np.zeros((1, 4, 4096, 64), f32)
    pmk = np.zeros((1, 4, 256, 4, 128), f32)
    pmv = np.zeros((1, 4, 256, 4, 128), f32)
    pfc = np.zeros((1, 4, 2, DFF), f32)
    sak = np.zeros((1, 8, 16, 8, 64), f32)
    sav = np.zeros((1, 8, 16, 8, 64), f32)
    sbk = np.zeros((1, 8, 16, 2, 64), f32)
    sbv = np.zeros((1, 8, 16, 2, 64), f32)
    sbi = np.zeros((1, 8, 16, 64), f32)
    sfc = np.zeros((1, 8, 2, DFF), f32)
    for c in range(8):
        b, half = c // 2, c % 2
        r = R[c]
        y[b, half * 2048:(half + 1) * 2048] = np.asarray(r["y"], f32)
        ys[c] = np.asarray(r["ys"], f32)
        if half == 1:
            akT = np.asarray(r["akT"], f32).reshape(2, 64, 4, 512)
            pak[0, b] = akT.transpose(3, 2, 0, 1).reshape(512, 8, 64)
            pav[0, b] = np.asarray(r["av"], f32).reshape(512, 8, 64)
            pbk[0, b] = np.asarray(r["bkT"], f32).T.reshape(4096, 2, 64)
            pbv[0, b] = np.asarray(r["bv"], f32).reshape(4096, 2, 64)
            pbi[0, b] = np.asarray(r["biT"], f32).T
            pmk[0, b] = np.asarray(r["mkT"], f32).reshape(128, 4, 256).transpose(2, 1, 0)
            pmv[0, b] = np.asarray(r["mv"], f32).reshape(256, 4, 128)
            pfc[0, b] = np.asarray(r["fcT"], f32).reshape(128, NFC, 2).transpose(2, 1, 0).reshape(2, DFF)
        sakT = np.asarray(r["sakT"], f32).reshape(2, 64, 4, 16)
        sak[0, c] = sakT.transpose(3, 2, 0, 1).reshape(16, 8, 64)
        sav[0, c] = np.asarray(r["sav"], f32).reshape(16, 8, 64)
        sbk[0, c] = np.asarray(r["sbkT"], f32).T.reshape(16, 2, 64)
        sbv[0, c] = np.asarray(r["sbv"], f32).reshape(16, 2, 64)
        sbi[0, c] = np.asarray(r["sbiT"], f32).T
        sfc[0, c] = np.asarray(r["sfcT"], f32).reshape(128, NFC, 2).transpose(2, 1, 0).reshape(2, DFF)
    return (y, ys, pak, pav, pbk, pbv, pbi, pmk, pmv, pfc, sak, sav, sbk, sbv, sbi, sfc)
```

```python
import math
from contextlib import ExitStack

import numpy as np
import concourse.bass as bass
import concourse.mybir as mybir
from concourse.bass_utils import run_bass_kernel_spmd

F32 = mybir.dt.float32
BF16 = mybir.dt.bfloat16
AF = mybir.ActivationFunctionType
ALU = mybir.AluOpType

D = 1024
KC = 8
NT = 32
NCOL = 2952
C_QA, C_KA, C_QB, C_KB, C_QI, C_KI, C_VA, C_VB, C_WI = 0, 512, 1024, 1536, 1664, 2176, 2304, 2816, 2944
DFF = 2816
NFC = 22
ALPHA = 2.0 ** 0.25
LN_EPS = 1e-5
NEGM = -30000.0
NIT = 17
BIS_W0 = 16.0
ABW = 640
BNW = 256
NBLK = 18


class Res:
    __slots__ = ("lw", "rd", "name", "excl")

    def __init__(self, name="", excl=False):
        self.lw = None
        self.rd = {}
        self.name = name
        self.excl = excl


def _call(name, *args, **kw):
    return lambda e: getattr(e, name)(*args, **kw)


class Prog:
    ENG = ("pe", "act", "dve", "pool", "sp")

    def __init__(self, nc, sems, dma_sems):
        self.nc = nc
        self.streams = {e: [] for e in self.ENG}
        self.sem = sems
        self.cnt = {e: 0 for e in self.ENG}
        self.seen = {e: {} for e in self.ENG}
        self.dsems = dma_sems
        self.dval = [0] * len(dma_sems)
        self.dnext = 0
        self.semh = dict(sems)
        for i, h in enumerate(dma_sems):
            self.semh[("d", i)] = h
        self.ninst = 0
        self.dead = False
        self.deferred = []
        self.defer_lag = 48

    def _deps(self, reads, writes, eng=None):
        d = {}
        for r in reads:
            if r.lw is not None:
                k, v = r.lw
                if d.get(k, 0) < v:
                    d[k] = v
            if r.excl:
                for k, v in r.rd.items():
                    if k != eng and d.get(k, 0) < v:
                        d[k] = v
        for w in writes:
            if w.lw is not None:
                k, v = w.lw
                if d.get(k, 0) < v:
                    d[k] = v
            for k, v in w.rd.items():
                if d.get(k, 0) < v:
                    d[k] = v
        return d

    def _wait(self, eng, deps):
        for k, v in deps.items():
            if k == "pe" and eng == "pe":
                continue
            if self.seen[eng].get(k, 0) >= v:
                continue
            self.seen[eng][k] = v
            h = self.semh[k]
            self.streams[eng].append(lambda e, h=h, v=v: e.wait_ge(h, v))

    def _flush_deferred(self, force=False, reads=(), writes=()):
        if not self.deferred:
            return
        conflict = force
        if not conflict:
            ws = set(id(w) for w in writes)
            rs = set(id(r) for r in reads)
            for d in self.deferred:
                dr = set(id(x) for x in d[3])
                dw = set(id(x) for x in d[4])
                if (ws & dr) or (ws & dw) or (rs & dw):
                    conflict = True
                    break
        if conflict:
            pend, self.deferred = self.deferred, []
            for d in pend:
                self._dma_now(d[0], d[1], d[2], d[3], d[4], d[5])
            return
        while self.deferred and self.ninst - self.deferred[0][6] >= self.defer_lag:
            d = self.deferred.pop(0)
            self._dma_now(d[0], d[1], d[2], d[3], d[4], d[5])

    def op(self, eng, fn, reads=(), writes=()):
        if self.dead:
            return
        self._flush_deferred(False, reads, writes)
        self._wait(eng, self._deps(reads, writes, eng))
        self.cnt[eng] += 1
        n = self.cnt[eng]
        h = self.sem[eng]
        self.streams[eng].append(lambda e, fn=fn, h=h: fn(e).then_inc(h, 1))
        self.ninst += 1
        for r in reads:
            if r.rd.get(eng, 0) < n:
                r.rd[eng] = n
        for w in writes:
            w.lw = (eng, n)
            w.rd = {}

    def dma(self, q, out, in_, reads=(), writes=(), slow=False, defer=False):
        if self.dead:
            return
        if defer:
            self._flush_deferred(False, reads, writes)
            self.deferred.append((q, out, in_, list(reads), list(writes), slow, self.ninst))
            return
        self._flush_deferred(False, reads, writes)
        self._dma_now(q, out, in_, reads, writes, slow)

    def _dma_now(self, q, out, in_, reads=(), writes=(), slow=False):
        deps = self._deps(reads, writes)
        i = self.dnext
        self.dnext = (i + 1) % len(self.dsems)
        k = ("d", i)
        if self.dval[i] > 0 and deps.get(k, 0) < self.dval[i]:
            deps[k] = self.dval[i]
        self._wait(q, deps)
        self.dval[i] += 16
        v = self.dval[i]
        h = self.dsems[i]
        if slow:
            self.streams[q].append(
                lambda e, out=out, in_=in_, h=h: e.dma_start(out=out, in_=in_, allow_slow_non_contiguous=True).then_inc(h, 16))
        else:
            self.streams[q].append(lambda e, out=out, in_=in_, h=h: e.dma_start(out=out, in_=in_).then_inc(h, 16))
        self.ninst += 1
        for r in reads:
            if r.rd.get(k, 0) < v:
                r.rd[k] = v
        for w in writes:
            w.lw = (k, v)
            w.rd = {}

    def finish(self):
        self._flush_deferred(True)
        deps = {("d", i): v for i, v in enumerate(self.dval) if v > 0}
        self._wait("sp", deps)

    def flush(self, block):
        self._flush_deferred(True)
        s = self.streams
        self.streams = {e: [] for e in self.ENG}

        def mk(lst):
            def body(e):
                for f in lst:
                    f(e)
            return body

        block.tensor(mk(s["pe"]))
        block.scalar(mk(s["act"]))
        block.vector(mk(s["dve"]))
        block.gpsimd(mk(s["pool"]))
        block.sync(mk(s["sp"]))


def build_program(debug=False, stop_at=None):
    nc = bass.Bass("TRN2", target_bir_lowering=False)

    def din(name, shape, dt=F32):
        return nc.dram_tensor(name, list(shape), dt, kind="ExternalInput").ap()

    def dout(name, shape, dt=F32):
        return nc.dram_tensor(name, list(shape), dt, kind="ExternalOutput").ap()

    def dscr(name, shape, dt):
        return nc.dram_tensor(name, list(shape), dt, kind="Internal").ap()

    I = {}
    I["xkT"] = din("xkT", [NT, 128, 1024])
    I["xsT"] = din("xsT", [128, 8 * 16])
    I["xres"] = din("xres", [NBLK * 128, 1024])
    I["win"] = din("win", [128, KC * NCOL])
    I["wo"] = din("wo", [128, 8 * 1024])
    I["wmq"] = din("wmq", [128, 8 * 512])
    I["wmk"] = din("wmk", [128, 8 * 512])
    I["wmv"] = din("wmv", [128, 8 * 512])
    I["wmo"] = din("wmo", [128, 4 * 1024])
    I["wup"] = din("wup", [NFC, 128, 8 * 256])
    I["wdown"] = din("wdown", [128, NFC * 1024])
    I["lnp"] = din("lnp", [6, 1024])
    I["wconvT"] = din("wconvT", [128, NFC * 3])
    I["bconvT"] = din("bconvT", [128, NFC])
    I["memT"] = din("memT", [128, 8 * 256])
    I["cmkT"] = din("cmkT", [128, 4 * 256])
    I["cmv"] = din("cmv", [256, 512])
    I["cakT"] = din("cakT", [128, 4 * 512])
    I["cav"] = din("cav", [512, 512])
    I["cbkT"] = din("cbkT", [128, 2048])
    I["cbv"] = din("cbv", [2048, 128])
    I["cbiT"] = din("cbiT", [128, 2048])
    I["sconvT"] = din("sconvT", [128, NFC * 2])
    I["ident"] = din("ident", [128, 128])
    I["AB"] = din("AB", [128, 8 * ABW])
    I["ABs"] = din("ABs", [16, 8 * 528])
    I["Bn"] = din("Bn", [128, 8 * BNW])
    I["Bns"] = din("Bns", [16, 8 * 144])
    I["C15"] = din("C15", [128, 8])
    I["colmask"] = din("colmask", [128, 1])
    I["diagmask"] = din("diagmask", [128, 128])
    I["kvalid"] = din("kvalid", [128, NT])
    I["flag"] = din("flag", [128, 1])

    O = {}
    O["y"] = dout("y", [2048, 1024])
    O["ys"] = dout("ys", [16, 1024])
    O["akT"] = dout("akT", [128, 4 * 512])
    O["av"] = dout("av", [512, 512])
    O["bkT"] = dout("bkT", [128, 4096])
    O["bv"] = dout("bv", [4096, 128])
    O["biT"] = dout("biT", [64, 4096])
    O["mkT"] = dout("mkT", [128, 4 * 256])
    O["mv"] = dout("mv", [256, 512])
    O["fcT"] = dout("fcT", [128, NFC * 2])
    O["sakT"] = dout("sakT", [128, 4 * 16])
    O["sav"] = dout("sav", [16, 512])
    O["sbkT"] = dout("sbkT", [128, 16])
    O["sbv"] = dout("sbv", [16, 128])
    O["sbiT"] = dout("sbiT", [64, 16])
    O["sfcT"] = dout("sfcT", [128, NFC * 2])
    if debug:
        O["dbg_mix"] = dout("dbg_mix", [NBLK * 128, 1024], BF16)
        O["dbg_h2"] = dout("dbg_h2", [NBLK * 128, 1024])
        mixD = O["dbg_mix"]
        h2D = O["dbg_h2"]
    else:
        mixD = dscr("mixD", [NBLK * 128, 1024], BF16)
        h2D = dscr("h2D", [NBLK * 128, 1024], F32)
    h2TD = dscr("h2TD", [NBLK, 128, 1024], BF16)
    R_mixD = [Res("mixD%d" % i) for i in range(NBLK)]
    R_h2D = [Res("h2D%d" % i) for i in range(NBLK)]
    R_h2TD = [Res("h2TD%d" % i) for i in range(NBLK)]

    es = ExitStack()
    with es:
        sems = {e: es.enter_context(nc.semaphore("s_" + e)) for e in Prog.ENG}
        dsems = [es.enter_context(nc.semaphore("d%d" % i)) for i in range(32)]
        P = Prog(nc, sems, dsems)
        block = es.enter_context(nc.Block())

        def checkpoint(name):
            if stop_at is not None and name == stop_at and not P.dead:
                P.finish()
                P.flush(block)
                P.dead = True

        pb = [es.enter_context(nc.psum_tensor("pb%d" % i, [128, 512], F32)) for i in range(8)]
        R_pb = [Res("pb%d" % i, excl=True) for i in range(8)]

        class Rot:
            def __init__(self, idxs):
                self.idxs = idxs
                self.i = 0

            def next(self):
                k = self.idxs[self.i % len(self.idxs)]
                self.i += 1
                return k

        def sb(stack, name, shape, dt):
            return stack.enter_context(nc.sbuf_tensor("sb_" + name, list(shape), dt))

        ident_f = sb(es, "ident_f", [128, 128], F32)
        ident = sb(es, "ident", [128, 512], BF16)
        R_ident = Res("ident")
        P.dma("sp", ident_f[:, :], I["ident"][:, :], writes=[R_ident])
        for r in range(4):
            P.op("act", _call("activation", out=ident[:, r * 128:(r + 1) * 128], in_=ident_f[:, :], func=AF.Copy),
                 reads=[R_ident], writes=[R_ident])

        def run_interleaved(gens):
            gens = [[0.0, i, g] for i, g in enumerate(gens)]
            while gens:
                gens.sort(key=lambda x: (x[0], x[1]))
                ent = gens[0]
                try:
                    c = next(ent[2])
                    ent[0] += (c if c else 1.0)
                except StopIteration:
                    gens.remove(ent)

        with ExitStack() as sa:
            winb = sb(sa, "winb", [128, KC * NCOL], BF16)
            R_win = Res("win")
            kbi = sb(sa, "kbi", [128, 2 * 4096], BF16)
            R_kbi = [Res("kbi%d" % r) for r in range(NT)]
            vb_aug = sb(sa, "vb_aug", [128, NT * 2 * 65], BF16)
            R_vb = [Res("vb%d" % r) for r in range(NT)]
            kaT = sb(sa, "kaT", [128, 6 * 512], BF16)
            R_ka = [Res("ka%d" % s) for s in range(6)]
            va_aug = sb(sa, "va_aug", [128, 6 * 8 * 65], BF16)
            R_va = [Res("va%d" % s) for s in range(6)]
            ABb = sb(sa, "ABb", [128, 8 * ABW], BF16)
            R_AB = Res("AB")
            Bnb = sb(sa, "Bnb", [128, 8 * BNW], BF16)
            R_Bn = Res("Bn")
            Mnear = [sb(sa, "Mnear%d" % k, [128, 8 * BNW], BF16) for k in range(2)]
            R_Mnear = [Res("Mnear%d" % k) for k in range(2)]
            score = [sb(sa, "score%d" % k, [128, 4096], F32) for k in range(2)]
            R_score = [Res("score%d" % k) for k in range(2)]
            Mb = [sb(sa, "Mb%d" % k, [128, 4096], BF16) for k in range(2)]
            R_M = [Res("M%d" % k) for k in range(2)]
            relu = [sb(sa, "relu%d" % k, [128, 512], BF16) for k in range(3)]
            R_relu = [Res("relu%d" % k) for k in range(3)]
            xstg = sb(sa, "xstg", [128, 1024], F32)
            R_xstg = Res("xstg")
            xTb = [sb(sa, "xTb%d" % k, [128, 1024], BF16) for k in range(2)]
            R_xT = [Res("xT%d" % k) for k in range(2)]
            qaz = [sb(sa, "qaz%d" % k, [128, 1024], BF16) for k in range(2)]
            qbz = [sb(sa, "qbz%d" % k, [128, 1024], BF16) for k in range(2)]
            qiz = [sb(sa, "qiz%d" % k, [128, 1024], BF16) for k in range(2)]
            R_qa = [Res("qa%d" % k) for k in range(2)]
            R_qb = [Res("qb%d" % k) for k in range(2)]
            R_qi = [Res("qi%d" % k) for k in range(2)]
            coef = [sb(sa, "coef%d" % k, [128, 8], F32) for k in range(2)]
            R_coef = [Res("coef%d" % k) for k in range(2)]
            dg = [sb(sa, "dg%d" % k, [128, 1024], BF16) for k in range(2)]
            R_dg = [Res("dg%d" % k) for k in range(2)]
            PTA = [sb(sa, "PTA%d" % k, [128, 512], BF16) for k in range(3)]
            R_PTA = [Res("PTA%d" % k) for k in range(3)]
            PTB = [sb(sa, "PTB%d" % k, [128, 512], BF16) for k in range(3)]
            R_PTB = [Res("PTB%d" % k) for k in range(3)]
            mixb = [sb(sa, "mixb%d" % k, [128, 1024], BF16) for k in range(2)]
            R_mix = [Res("mix%d" % k) for k in range(2)]
            ostg = [sb(sa, "ostg%d" % k, [128, 256], F32) for k in range(2)]
            R_ostg = [Res("ostg%d" % k) for k in range(2)]
            vbstg = [sb(sa, "vbstg%d" % k, [128, 128], F32) for k in range(2)]
            R_vbstg = [Res("vbstg%d" % k) for k in range(2)]
            astg = sb(sa, "astg", [128, 1024], F32)
            R_astg = Res("astg")
            small = [sb(sa, "small%d" % k, [128, 16], F32) for k in range(2)]
            R_small = [Res("small%d" % k) for k in range(2)]
            recA = [sb(sa, "recA%d" % k, [128, 8], F32) for k in range(2)]
            R_recA = [Res("recA%d" % k) for k in range(2)]
            recB = [sb(sa, "recB%d" % k, [128, 8], F32) for k in range(2)]
            R_recB = [Res("recB%d" % k) for k in range(2)]
            colmask = sb(sa, "colmask", [128, 1], F32)
            diagm = sb(sa, "diagm", [128, 128], F32)
            kvalid = sb(sa, "kvalid", [128, NT], F32)
            c15 = sb(sa, "c15", [128, 8], F32)
            ones8 = sb(sa, "ones8", [128, 8], F32)
            R_cst = Res("cst")

            wrot = Rot([0, 1, 2])

            P.dma("sp", colmask[:, :], I["colmask"][:, :], writes=[R_cst])
            P.dma("sp", diagm[:, :], I["diagmask"][:, :], writes=[R_cst])
            P.dma("sp", kvalid[:, :], I["kvalid"][:, :], writes=[R_cst])
            P.dma("sp", c15[:, :], I["C15"][:, :], writes=[R_cst])
            P.op("pool", _call("memset", ones8[:, :], 1.0), writes=[R_cst])
            for k in range(2):
                P.op("pool", _call("memset", qaz[k][:, :], 0.0), writes=[R_qa[k]])
                P.op("pool", _call("memset", qbz[k][:, :], 0.0), writes=[R_qb[k]])
                P.op("pool", _call("memset", qiz[k][:, :], 0.0), writes=[R_qi[k]])

            HW = NCOL // 2
            for kc in range(KC):
                for hh in range(2):
                    stg, R_stg = score[hh], R_score[hh]
                    P.dma("sp", stg[:, 0:HW], I["win"][:, kc * NCOL + hh * HW: kc * NCOL + (hh + 1) * HW], writes=[R_stg])
                    if hh == 0:
                        P.op("act", _call("activation", out=winb[:, kc * NCOL + hh * HW: kc * NCOL + (hh + 1) * HW], in_=stg[:, 0:HW], func=AF.Copy),
                             reads=[R_stg], writes=[R_win])
                    else:
                        P.op("pool", _call("tensor_copy", out=winb[:, kc * NCOL + hh * HW: kc * NCOL + (hh + 1) * HW], in_=stg[:, 0:HW]),
                             reads=[R_stg], writes=[R_win])
            for hh in range(2):
                w = 4 * ABW
                P.dma("sp", score[hh][:, 0:w], I["AB"][:, hh * w:(hh + 1) * w], writes=[R_score[hh]])
                P.op("act", _call("activation", out=ABb[:, hh * w:(hh + 1) * w], in_=score[hh][:, 0:w], func=AF.Copy),
                     reads=[R_score[hh]], writes=[R_AB])
            P.dma("sp", score[0][:, 0:8 * BNW], I["Bn"][:, :], writes=[R_score[0]])
            for h in range(8):
                P.op("dve", _call("tensor_scalar", out=Bnb[:, h * BNW:(h + 1) * BNW], in0=score[0][:, h * BNW:(h + 1) * BNW],
                                  scalar1=c15[:, h:h + 1], scalar2=None, op0=ALU.subtract),
                     reads=[R_score[0], R_cst], writes=[R_Bn])
            checkpoint('consts')

            def win_cols(kc, c0, n):
                return winb[:, kc * NCOL + c0: kc * NCOL + c0 + n]

            def fm_proj(bank, xT, R_x, N, col0, nchunks, ocol=0):
                for j in range(nchunks):
                    for kc in range(KC):
                        P.op("pe", _call("matmul", out=pb[bank][:, ocol + j * N: ocol + (j + 1) * N], lhsT=win_cols(kc, col0 + j * 128, 128),
                                         rhs=xT[:, kc * N:(kc + 1) * N], start=(kc == 0), stop=(kc == KC - 1)),
                             reads=[R_win, R_x], writes=[R_pb[bank]])

            def tm_proj(bank, xT, R_x, N, col0, ncols, ocol=0):
                for kc in range(KC):
                    P.op("pe", _call("matmul", out=pb[bank][0:N, ocol:ocol + ncols], lhsT=xT[:, kc * N:(kc + 1) * N],
                                     rhs=win_cols(kc, col0, ncols), start=(kc == 0), stop=(kc == KC - 1)),
                         reads=[R_win, R_x], writes=[R_pb[bank]])

            def load_xT(r):
                s = r % 2
                P.dma("sp", xstg[:, :], I["xkT"][r], writes=[R_xstg])
                P.op("pool", _call("tensor_copy", out=xTb[s][:, :], in_=xstg[:, :]), reads=[R_xstg], writes=[R_xT[s]])

            def kside(r, full):
                s = r % 2
                xT, R_x = xTb[s], R_xT[s]
                so = r % 2
                bk = wrot.next()
                fm_proj(bk, xT, R_x, 128, C_KB, 1)
                fm_proj(bk, xT, R_x, 128, C_KI, 1, ocol=128)
                P.op("act", _call("activation", out=ostg[so][:, :], in_=pb[bk][:, 0:256], func=AF.Copy), reads=[R_pb[bk]], writes=[R_ostg[so]])
                P.op("pool", _call("tensor_copy", out=kbi[:, :].rearrange("p (a c) -> p a c", a=2)[:, :, r * 128:(r + 1) * 128],
                                   in_=ostg[so][:, :].rearrange("p (a c) -> p a c", a=2)),
                     reads=[R_ostg[so]], writes=[R_kbi[r]])
                P.dma("sp", O["bkT"][:, r * 128:(r + 1) * 128], ostg[so][:, 0:128], reads=[R_ostg[so]], defer=True)
                P.dma("sp", O["biT"][:, r * 128:(r + 1) * 128], ostg[so][0:64, 128:256], reads=[R_ostg[so]], defer=True)
                yield 3.0
                bv_ = wrot.next()
                tm_proj(bv_, xT, R_x, 128, C_VB, 128)
                vbv = vb_aug[:, r * 130:(r + 1) * 130].rearrange("p (g d) -> p g d", d=65)
                P.op("act", _call("activation", out=vbstg[so][:, :], in_=pb[bv_][:, 0:128], func=AF.Copy), reads=[R_pb[bv_]], writes=[R_vbstg[so]])
                P.op("pool", _call("tensor_copy", out=vbv[:, :, 0:64], in_=vbstg[so][:, :].rearrange("p (g d) -> p g d", d=64)),
                     reads=[R_vbstg[so]], writes=[R_vb[r]])
                P.op("pool", _call("tensor_scalar", out=vbv[:, :, 64:65], in0=ones8[:, 0:2].rearrange("p (g o) -> p g o", o=1),
                                   scalar1=kvalid[:, r:r + 1], scalar2=None, op0=ALU.mult),
                     reads=[R_cst], writes=[R_vb[r]])
                P.dma("sp", O["bv"][r * 128:(r + 1) * 128, :], vbstg[so][:, :], reads=[R_vbstg[so]], defer=True)
                yield 3.0
                if not full:
                    return
                slot = r % 6
                ba = wrot.next()
                fm_proj(ba, xT, R_x, 128, C_KA, 4)
                P.op("act", _call("activation", out=kaT[:, slot * 512:(slot + 1) * 512], in_=pb[ba][:, :], func=AF.Copy),
                     reads=[R_pb[ba]], writes=[R_ka[slot]])
                if r >= 28:
                    P.op("dve", _call("tensor_copy", out=astg[:, 0:512], in_=pb[ba][:, :]), reads=[R_pb[ba]], writes=[R_astg])
                    P.dma("sp", O["akT"].rearrange("p (j t) -> p j t", t=512)[:, :, (r - 28) * 128:(r - 27) * 128],
                          astg[:, 0:512].rearrange("p (j t) -> p j t", t=128), reads=[R_astg], defer=True)
                yield 3.0
                bva = wrot.next()
                tm_proj(bva, xT, R_x, 128, C_VA, 512)
                vav = va_aug[:, slot * 520:(slot + 1) * 520].rearrange("p (h d) -> p h d", d=65)
                P.op("act", _call("activation", out=vav[:, :, 0:64], in_=pb[bva][:, :].rearrange("p (h d) -> p h d", d=64), func=AF.Copy),
                     reads=[R_pb[bva]], writes=[R_va[slot]])
                P.op("pool", _call("tensor_scalar", out=vav[:, :, 64:65], in0=ones8[:, :].rearrange("p (h o) -> p h o", o=1),
                                   scalar1=kvalid[:, r:r + 1], scalar2=None, op0=ALU.mult),
                     reads=[R_cst], writes=[R_va[slot]])
                if r >= 28:
                    P.op("dve", _call("tensor_copy", out=astg[:, 512:1024], in_=pb[bva][:, :]), reads=[R_pb[bva]], writes=[R_astg])
                    P.dma("sp", O["av"][(r - 28) * 128:(r - 27) * 128, :], astg[:, 512:1024], reads=[R_astg], defer=True)
                yield 3.0

            def qside(xT, R_x, qs, st):
                b1 = wrot.next()
                fm_proj(b1, xT, R_x, qs, C_QA, 4)
                for hf in range(2):
                    P.op("act", _call("activation",
                                      out=qaz[st][hf * 64:(hf + 1) * 64, 0:8 * qs].rearrange("p (j two q) -> p j two q", two=2, q=qs)[:, :, hf, :],
                                      in_=pb[b1][hf * 64:(hf + 1) * 64, 0:4 * qs].rearrange("p (j q) -> p j q", q=qs), func=AF.Copy, scale=0.125),
                         reads=[R_pb[b1]], writes=[R_qa[st]])
                yield 3.0
                b2 = wrot.next()
                fm_proj(b2, xT, R_x, qs, C_QB, 4)
                for g in range(2):
                    P.op("act", _call("activation", out=qbz[st][g * 64:(g + 1) * 64, g * 4 * qs:(g + 1) * 4 * qs],
                                      in_=pb[b2][g * 64:(g + 1) * 64, 0:4 * qs], func=AF.Copy, scale=0.125),
                         reads=[R_pb[b2]], writes=[R_qb[st]])
                yield 3.0
                b3 = wrot.next()
                fm_proj(b3, xT, R_x, qs, C_QI, 4)
                for hf in range(2):
                    P.op("act", _call("activation",
                                      out=qiz[st][hf * 64:(hf + 1) * 64, 0:8 * qs].rearrange("p (j two q) -> p j two q", two=2, q=qs)[:, :, hf, :],
                                      in_=pb[b3][hf * 64:(hf + 1) * 64, 0:4 * qs].rearrange("p (j q) -> p j q", q=qs), func=AF.Copy),
                         reads=[R_pb[b3]], writes=[R_qi[st]])
                b4 = wrot.next()
                tm_proj(b4, xT, R_x, qs, C_WI, 8)
                P.op("dve", _call("tensor_scalar", out=coef[st][0:qs, :], in0=pb[b4][0:qs, 0:8], scalar1=float(8.0 ** -1.5), scalar2=None, op0=ALU.mult),
                     reads=[R_pb[b4]], writes=[R_coef[st]])
                for h in range(8):
                    P.op("pool", _call("tensor_scalar", out=dg[st][0:qs, h * 128: h * 128 + qs], in0=ident_f[0:qs, 0:qs],
                                       scalar1=coef[st][0:qs, h:h + 1], scalar2=None, op0=ALU.mult),
                         reads=[R_coef[st], R_ident], writes=[R_dg[st]])
                yield 3.0

            def normalize(bank, qs, mixt, R_m, col0, rec, R_rec):
                ov = pb[bank][0:qs, 0:260].rearrange("p (h d) -> p h d", d=65)
                P.op("dve", _call("tensor_scalar", out=rec[0:qs, 0:4].rearrange("p (h o) -> p h o", o=1), in0=ov[:, :, 64:65],
                                  scalar1=1e-30, scalar2=None, op0=ALU.max),
                     reads=[R_pb[bank]], writes=[R_rec])
                P.op("dve", _call("reciprocal", out=rec[0:qs, 0:4], in_=rec[0:qs, 0:4]), reads=[R_rec], writes=[R_rec])
                for hh in range(4):
                    P.op("dve", _call("tensor_scalar", out=mixt[0:qs, col0 + hh * 64: col0 + (hh + 1) * 64],
                                      in0=pb[bank][0:qs, hh * 65: hh * 65 + 64],
                                      scalar1=rec[0:qs, hh:hh + 1], scalar2=None, op0=ALU.mult),
                         reads=[R_pb[bank], R_rec], writes=[R_m])

            def pipe3(items, s1, s2, s3, D, cost=1.0):
                pend = []
                for it in items:
                    s1(it)
                    s2(it)
                    pend.append(it)
                    if len(pend) > D:
                        s3(pend.pop(0))
                    yield cost
                while pend:
                    s3(pend.pop(0))
                    yield cost

            pta_rot = Rot([0, 1, 2])
            relu_rot = Rot([0, 1, 2])
            ptb_rot = Rot([0, 1, 2])
            brot = Rot([3, 7])

            def front_attn(sn, qs, wins, btiles, prompt_masks, abw):
                st = sn % 2
                mixt, R_m = mixb[st], R_mix[st]
                nw = len(wins)

                units = []
                for h in range(8):
                    units.append({"h": h, "t0": 0, "tiles": wins[0:4]})
                    if nw > 4:
                        units.append({"h": h, "t0": 4, "tiles": wins[4:5]})

                def a1(u):
                    h = u["h"]
                    j = h // 2
                    bank = wrot.next()
                    u["bank"] = bank
                    for i, (slot, ts) in enumerate(u["tiles"]):
                        t = u["t0"] + i
                        c0 = i * qs
                        P.op("pe", _call("matmul", out=pb[bank][0:ts, c0:c0 + qs], lhsT=kaT[:, slot * 512 + j * 128: slot * 512 + j * 128 + ts],
                                         rhs=qaz[st][:, h * qs:(h + 1) * qs], start=True, stop=False),
                             reads=[R_ka[slot], R_qa[st]], writes=[R_pb[bank]])
                        P.op("pe", _call("matmul", out=pb[bank][0:ts, c0:c0 + qs], lhsT=ABb[0:qs, h * abw + t * 128: h * abw + t * 128 + ts],
                                         rhs=ident[0:qs, 0:qs], start=False, stop=True),
                             reads=[R_AB, R_ident], writes=[R_pb[bank]])

                def a2(u):
                    k = pta_rot.next()
                    u["pt"], u["R_pt"] = PTA[k], R_PTA[k]
                    bank = u["bank"]
                    tsm = max(ts for (_, ts) in u["tiles"])
                    n = len(u["tiles"])
                    P.op("act", _call("activation", out=u["pt"][0:tsm, 0:n * qs], in_=pb[bank][0:tsm, 0:n * qs], func=AF.Exp),
                         reads=[R_pb[bank]], writes=[u["R_pt"]])

                def a3(u):
                    h = u["h"]
                    last_unit = (u["t0"] + len(u["tiles"]) == nw)
                    for i, (slot, ts) in enumerate(u["tiles"]):
                        t = u["t0"] + i
                        P.op("pe", _call("matmul", out=pb[4][0:qs, (h % 4) * 65:(h % 4) * 65 + 65], lhsT=u["pt"][0:ts, i * qs:(i + 1) * qs],
                                         rhs=va_aug[0:ts, slot * 520 + h * 65: slot * 520 + h * 65 + 65],
                                         start=(h % 4 == 0 and t == 0), stop=(t == nw - 1), skip_group_check=True),
                             reads=[u["R_pt"], R_va[slot]], writes=[R_pb[4]])
                    if last_unit and h % 4 == 3:
                        normalize(4, qs, mixt, R_m, (h // 4) * 256, recA[st], R_recA[st])

                yield from pipe3(units, a1, a2, a3, 2, 0.9)

                L = btiles[-1][1] + btiles[-1][2]
                items = []
                cc = 0
                for c0 in range(0, L, 512):
                    w = min(512, L - c0)
                    rk = [R_kbi[tt[0]] for tt in btiles if tt[1] >= c0 - 127 and tt[1] < c0 + w]
                    for h in range(8):
                        items.append({"c0": c0, "w": w, "h": h, "sc": (5, 4)[cc % 2], "rk": rk})
                    cc += 1

                def i1(it):
                    bank = wrot.next()
                    it["bank"] = bank
                    h, c0, w = it["h"], it["c0"], it["w"]
                    P.op("pe", _call("matmul", out=pb[bank][0:qs, 0:w], lhsT=qiz[st][:, h * qs:(h + 1) * qs],
                                     rhs=kbi[:, 4096 + c0: 4096 + c0 + w], start=True, stop=True),
                         reads=[R_qi[st]] + it["rk"], writes=[R_pb[bank]])

                def i2(it):
                    k = relu_rot.next()
                    it["rl"], it["R_rl"] = relu[k], R_relu[k]
                    w = it["w"]
                    P.op("act", _call("activation", out=it["rl"][0:qs, 0:w], in_=pb[it["bank"]][0:qs, 0:w], func=AF.Relu),
                         reads=[R_pb[it["bank"]]], writes=[it["R_rl"]])

                def i3(it):
                    h, c0, w, sc = it["h"], it["c0"], it["w"], it["sc"]
                    P.op("pe", _call("matmul", out=pb[sc][0:qs, 0:w], lhsT=dg[st][0:qs, h * 128: h * 128 + qs], rhs=it["rl"][0:qs, 0:w],
                                     start=(h == 0), stop=(h == 7)),
                         reads=[R_dg[st], it["R_rl"]], writes=[R_pb[sc]])
                    if h == 7:
                        if prompt_masks and c0 < 2048:
                            wm = min(w, 2048 - c0)
                            P.op("act", _call("activation", out=score[st][0:qs, c0:c0 + wm], in_=pb[sc][0:qs, 0:wm], func=AF.Identity,
                                              bias=colmask[0:qs, 0:1]),
                                 reads=[R_pb[sc], R_cst], writes=[R_score[st]])
                            if wm < w:
                                P.op("act", _call("activation", out=score[st][0:qs, c0 + wm:c0 + w], in_=pb[sc][0:qs, wm:w], func=AF.Copy),
                                     reads=[R_pb[sc]], writes=[R_score[st]])
                        else:
                            P.op("act", _call("activation", out=score[st][0:qs, c0:c0 + w], in_=pb[sc][0:qs, 0:w], func=AF.Copy),
                                 reads=[R_pb[sc]], writes=[R_score[st]])

                yield from pipe3(items, i1, i2, i3, 2, 0.65)
                if prompt_masks:
                    P.op("dve", _call("tensor_tensor", out=score[st][0:qs, L - 128:L], in0=score[st][0:qs, L - 128:L], in1=diagm[0:qs, :], op=ALU.add),
                         reads=[R_score[st], R_cst], writes=[R_score[st]])
                yield

            def back_attn(sn, qs, btiles, blk, bnw):
                st = sn % 2
                mixt, R_m = mixb[st], R_mix[st]
                sm, R_sm = small[st], R_small[st]
                L = btiles[-1][1] + btiles[-1][2]
                P.op("dve", _call("memset", sm[0:qs, 1:2], 0.0), writes=[R_sm])
                for k in range(NIT):
                    wk = BIS_W0 / (2.0 ** k)
                    P.op("dve", _call("tensor_scalar", out=Mb[st][0:qs, 0:L], in0=score[st][0:qs, 0:L], scalar1=sm[0:qs, 1:2], scalar2=None,
                                      op0=ALU.is_ge, op1=ALU.add, accum_out=sm[0:qs, 0:1]),
                         reads=[R_score[st], R_sm], writes=[R_M[st], R_sm])
                    P.op("dve", _call("tensor_scalar", out=sm[0:qs, 2:3], in0=sm[0:qs, 0:1], scalar1=255.5, scalar2=wk,
                                      op0=ALU.is_ge, op1=ALU.mult),
                         reads=[R_sm], writes=[R_sm])
                    P.op("dve", _call("scalar_tensor_tensor", out=sm[0:qs, 1:2], in0=sm[0:qs, 2:3], scalar=-wk / 2.0,
                                      in1=sm[0:qs, 1:2], op0=ALU.add, op1=ALU.add),
                         reads=[R_sm], writes=[R_sm])
                    yield L * 1.08e-3 + 0.5
                wl = BIS_W0 / (2.0 ** (NIT - 1)) / 2.0
                P.op("dve", _call("tensor_scalar", out=sm[0:qs, 3:4], in0=sm[0:qs, 1:2], scalar1=-wl, scalar2=None, op0=ALU.add),
                     reads=[R_sm], writes=[R_sm])
                P.op("dve", _call("tensor_scalar", out=Mb[st][0:qs, 0:L], in0=score[st][0:qs, 0:L], scalar1=sm[0:qs, 3:4], scalar2=NEGM,
                                  op0=ALU.is_lt, op1=ALU.mult),
                     reads=[R_score[st], R_sm], writes=[R_M[st]])
                nearw = btiles[-2][2] + btiles[-1][2]
                for h in range(8):
                    P.op("dve", _call("tensor_tensor", out=Mnear[st][0:qs, h * bnw: h * bnw + nearw], in0=Bnb[0:qs, h * bnw: h * bnw + nearw],
                                      in1=Mb[st][0:qs, L - nearw:L], op=ALU.add),
                         reads=[R_Bn, R_M[st]], writes=[R_Mnear[st]])
                yield
                nb = len(btiles)
                items = [{"g": g, "t": t, "vt": vt, "c0": c0, "ts": ts} for g in range(2) for t, (vt, c0, ts) in enumerate(btiles)]

                def b1(it):
                    g, t, vt, c0, ts = it["g"], it["t"], it["vt"], it["c0"], it["ts"]
                    bank = brot.next()
                    it["bank"] = bank
                    P.op("pe", _call("matmul", out=pb[bank][0:ts, 0:4 * qs], lhsT=kbi[:, c0:c0 + ts],
                                     rhs=qbz[st][:, g * 4 * qs:(g + 1) * 4 * qs], start=True, stop=False),
                         reads=[R_kbi[vt], R_qb[st]], writes=[R_pb[bank]])
                    if t < nb - 2 and qs == 128:
                        P.op("pe", _call("matmul", out=pb[bank][0:ts, 0:512], lhsT=Mb[st][0:qs, c0:c0 + ts], rhs=ident[0:128, 0:512],
                                         start=False, stop=True),
                             reads=[R_M[st], R_ident], writes=[R_pb[bank]])
                    elif t < nb - 2:
                        for r in range(4):
                            P.op("pe", _call("matmul", out=pb[bank][0:ts, r * qs:(r + 1) * qs], lhsT=Mb[st][0:qs, c0:c0 + ts],
                                             rhs=ident[0:qs, 0:qs], start=False, stop=(r == 3)),
                                 reads=[R_M[st], R_ident], writes=[R_pb[bank]])
                    else:
                        tt = t - (nb - 2)
                        for r in range(4):
                            hh = g * 4 + r
                            P.op("pe", _call("matmul", out=pb[bank][0:ts, r * qs:(r + 1) * qs],
                                             lhsT=Mnear[st][0:qs, hh * bnw + tt * 128: hh * bnw + tt * 128 + ts], rhs=ident[0:qs, 0:qs],
                                             start=False, stop=(r == 3)),
                                 reads=[R_Mnear[st], R_ident], writes=[R_pb[bank]])

                def b2(it):
                    k = ptb_rot.next()
                    it["ptb"], it["R_ptb"] = PTB[k], R_PTB[k]
                    ts = it["ts"]
                    P.op("act", _call("activation", out=it["ptb"][0:ts, 0:4 * qs], in_=pb[it["bank"]][0:ts, 0:4 * qs], func=AF.Exp),
                         reads=[R_pb[it["bank"]]], writes=[it["R_ptb"]])

                def b3(it):
                    g, t, vt, ts = it["g"], it["t"], it["vt"], it["ts"]
                    for r in range(4):
                        P.op("pe", _call("matmul", out=pb[6][0:qs, r * 65: r * 65 + 65], lhsT=it["ptb"][0:ts, r * qs:(r + 1) * qs],
                                         rhs=vb_aug[0:ts, (vt * 2 + g) * 65:(vt * 2 + g) * 65 + 65],
                                         start=(t == 0 and r == 0), stop=(t == nb - 1), skip_group_check=True),
                             reads=[it["R_ptb"], R_vb[vt]], writes=[R_pb[6]])
                    if t == nb - 1:
                        normalize(6, qs, mixt, R_m, 512 + g * 256, recB[st], R_recB[st])

                yield from pipe3(items, b1, b2, b3, 1, 0.8)
                P.dma("sp", mixD[blk * 128: blk * 128 + qs, :], mixt[0:qs, :], reads=[R_m], writes=[R_mixD[blk]], defer=True)
                yield

            for r in range(16):
                load_xT(r)
                for _ in kside(r, full=(r >= 11)):
                    pass
            checkpoint('phase0')

            def prompt_front(sn, T):
                if T >= 16:
                    load_xT(T)
                    yield from kside(T, full=True)
                s = T % 2
                yield from qside(xTb[s], R_xT[s], 128, sn % 2)
                wins = [((T - 4 + t) % 6, 128) for t in range(5)]
                btiles = [(t, t * 128, 128) for t in range(T + 1)]
                yield from front_attn(sn, 128, wins, btiles, True, ABW)

            def prompt_back(sn, T, blk):
                btiles = [(t, t * 128, 128) for t in range(T + 1)]
                yield from back_attn(sn, 128, btiles, blk, BNW)

            steps = [(0, 15, 16)] + [(1 + i, 16 + i, i) for i in range(16)]
            run_interleaved([prompt_front(*steps[0][0:2])])
            for si in range(len(steps)):
                sn, T, blk = steps[si]
                gens = [prompt_back(sn, T, blk)]
                if si + 1 < len(steps):
                    gens.append(prompt_front(*steps[si + 1][0:2]))
                run_interleaved(gens)
            checkpoint('steps')

            SN = len(steps)
            sst = SN % 2
            P.dma("sp", score[0][:, 0:2048], I["cbkT"][:, :], writes=[R_score[0]])
            P.op("act", _call("activation", out=kbi[:, 0:2048], in_=score[0][:, 0:2048], func=AF.Copy),
                 reads=[R_score[0]], writes=R_kbi[0:16])
            P.dma("sp", score[0][:, 2048:4096], I["cbiT"][:, :], writes=[R_score[0]])
            P.op("act", _call("activation", out=kbi[:, 4096:4096 + 2048], in_=score[0][:, 2048:4096], func=AF.Copy),
                 reads=[R_score[0]], writes=R_kbi[0:16])
            P.dma("sp", score[1][:, 0:2048].rearrange("p (t c) -> p t c", c=128), I["cbv"].rearrange("(t p) c -> p t c", p=128), writes=[R_score[1]])
            vball = vb_aug[:, 0:16 * 130].rearrange("p (t d) -> p t d", d=65)
            P.op("act", _call("activation", out=vball[:, :, 0:64], in_=score[1][:, 0:2048].rearrange("p (t d) -> p t d", d=64), func=AF.Copy),
                 reads=[R_score[1]], writes=R_vb[0:17])
            P.op("pool", _call("memset", vb_aug[:, 0:17 * 130].rearrange("p (t d) -> p t d", d=65)[:, :, 64:65], 1.0), writes=R_vb[0:17])
            P.dma("sp", score[0][:, 0:2048], I["cakT"][:, :], writes=[R_score[0]])
            for s4 in range(4):
                P.op("act", _call("activation", out=kaT[:, s4 * 512:(s4 + 1) * 512].rearrange("p (j t) -> p j t", t=128),
                                  in_=score[0][:, 0:2048].rearrange("p (j t) -> p j t", t=512)[:, :, s4 * 128:(s4 + 1) * 128], func=AF.Copy),
                     reads=[R_score[0]], writes=[R_ka[s4]])
            P.dma("sp", score[1][:, 2048:4096].rearrange("p (t c) -> p t c", c=512), I["cav"].rearrange("(t p) c -> p t c", p=128), writes=[R_score[1]])
            vaall = va_aug[:, 0:4 * 520].rearrange("p (t d) -> p t d", d=65)
            P.op("act", _call("activation", out=vaall[:, :, 0:64], in_=score[1][:, 2048:4096].rearrange("p (t d) -> p t d", d=64), func=AF.Copy),
                 reads=[R_score[1]], writes=R_va[0:5])
            P.op("pool", _call("memset", va_aug[:, 0:5 * 520].rearrange("p (t d) -> p t d", d=65)[:, :, 64:65], 1.0), writes=R_va[0:5])
            for hh in range(2):
                w = 4 * 528
                P.dma("sp", score[0][0:16, 0:w], I["ABs"][:, hh * w:(hh + 1) * w], writes=[R_score[0]])
                P.op("act", _call("activation", out=ABb[0:16, hh * w:(hh + 1) * w], in_=score[0][0:16, 0:w], func=AF.Copy),
                     reads=[R_score[0]], writes=[R_AB])
            P.dma("sp", score[1][0:16, 0:8 * 144], I["Bns"][:, :], writes=[R_score[1]])
            for h in range(8):
                P.op("dve", _call("tensor_scalar", out=Bnb[0:16, h * 144:(h + 1) * 144], in0=score[1][0:16, h * 144:(h + 1) * 144],
                                  scalar1=c15[0:16, h:h + 1], scalar2=None, op0=ALU.subtract),
                     reads=[R_score[1], R_cst], writes=[R_Bn])
            P.op("pool", _call("memset", qaz[sst][:, :], 0.0), writes=[R_qa[sst]])
            P.op("pool", _call("memset", qbz[sst][:, :], 0.0), writes=[R_qb[sst]])
            P.op("pool", _call("memset", qiz[sst][:, :], 0.0), writes=[R_qi[sst]])
            P.dma("sp", xstg[:, 0:128], I["xsT"][:, :], writes=[R_xstg])
            P.op("pool", _call("tensor_copy", out=xTb[0][:, 0:128], in_=xstg[:, 0:128]), reads=[R_xstg], writes=[R_xT[0]])
            xs_, R_xs = xTb[0], R_xT[0]
            bk = wrot.next()
            fm_proj(bk, xs_, R_xs, 16, C_KB, 1)
            fm_proj(bk, xs_, R_xs, 16, C_KI, 1, ocol=16)
            P.op("act", _call("activation", out=kbi[:, 2048:2064], in_=pb[bk][:, 0:16], func=AF.Copy), reads=[R_pb[bk]], writes=[R_kbi[16]])
            P.op("act", _call("activation", out=kbi[:, 4096 + 2048:4096 + 2064], in_=pb[bk][:, 16:32], func=AF.Copy), reads=[R_pb[bk]], writes=[R_kbi[16]])
            P.op("dve", _call("tensor_copy", out=ostg[0][:, 0:32], in_=pb[bk][:, 0:32]), reads=[R_pb[bk]], writes=[R_ostg[0]])
            P.dma("sp", O["sbkT"][:, :], ostg[0][:, 0:16], reads=[R_ostg[0]], defer=True)
            P.dma("sp", O["sbiT"][:, :], ostg[0][0:64, 16:32], reads=[R_ostg[0]], defer=True)
            bv_ = wrot.next()
            tm_proj(bv_, xs_, R_xs, 16, C_VB, 128)
            vbv = vb_aug[0:16, 16 * 130:17 * 130].rearrange("p (g d) -> p g d", d=65)
            P.op("act", _call("activation", out=vbv[:, :, 0:64], in_=pb[bv_][0:16, 0:128].rearrange("p (g d) -> p g d", d=64), func=AF.Copy),
                 reads=[R_pb[bv_]], writes=[R_vb[16]])
            P.op("dve", _call("tensor_copy", out=vbstg[0][0:16, :], in_=pb[bv_][0:16, 0:128]), reads=[R_pb[bv_]], writes=[R_vbstg[0]])
            P.dma("sp", O["sbv"][:, :], vbstg[0][0:16, :], reads=[R_vbstg[0]], defer=True)
            ba = wrot.next()
            fm_proj(ba, xs_, R_xs, 16, C_KA, 4)
            P.op("act", _call("activation", out=kaT[:, 4 * 512:5 * 512].rearrange("p (j t) -> p j t", t=128)[:, :, 0:16],
                              in_=pb[ba][:, 0:64].rearrange("p (j t) -> p j t", t=16), func=AF.Copy),
                 reads=[R_pb[ba]], writes=[R_ka[4]])
            P.op("dve", _call("tensor_copy", out=astg[:, 0:64], in_=pb[ba][:, 0:64]), reads=[R_pb[ba]], writes=[R_astg])
            P.dma("sp", O["sakT"][:, :], astg[:, 0:64], reads=[R_astg], defer=True)
            bva = wrot.next()
            tm_proj(bva, xs_, R_xs, 16, C_VA, 512)
            vav = va_aug[0:16, 4 * 520:5 * 520].rearrange("p (h d) -> p h d", d=65)
            P.op("act", _call("activation", out=vav[:, :, 0:64], in_=pb[bva][0:16, :].rearrange("p (h d) -> p h d", d=64), func=AF.Copy),
                 reads=[R_pb[bva]], writes=[R_va[4]])
            P.op("dve", _call("tensor_copy", out=astg[0:16, 512:1024], in_=pb[bva][0:16, :]), reads=[R_pb[bva]], writes=[R_astg])
            P.dma("sp", O["sav"][:, :], astg[0:16, 512:1024], reads=[R_astg], defer=True)
            checkpoint('sample_pre')
            wins = [(0, 128), (1, 128), (2, 128), (3, 128), (4, 16)]
            btiles = [(t, t * 128, 128) for t in range(16)] + [(16, 2048, 16)]

            def sample_all():
                yield from qside(xs_, R_xs, 16, sst)
                yield from front_attn(SN, 16, wins, btiles, False, 528)
                yield from back_attn(SN, 16, btiles, 17, 144)

            run_interleaved([sample_all()])
            checkpoint('phaseA')
            P.flush(block)

        with ExitStack() as sbk:
            wob = sb(sbk, "wob", [128, 8 * 1024], BF16)
            wmqb = sb(sbk, "wmqb", [128, 8 * 512], BF16)
            wmob = sb(sbk, "wmob", [128, 4 * 1024], BF16)
            wtmp = sb(sbk, "wtmp", [128, 8 * 512], BF16)
            R_wo, R_wmq, R_wmo, R_wtmp = Res("wo"), Res("wmq"), Res("wmo"), Res("wtmp")
            wst = [sb(sbk, "wst%d" % k, [128, 2048], F32) for k in range(2)]
            R_wst = [Res("wst%d" % k) for k in range(2)]
            lnt = sb(sbk, "lnt", [128, 4 * 1024], F32)
            R_ln = Res("ln")
            memTb = sb(sbk, "memTb", [128, 8 * 256], BF16)
            R_memT = Res("memT")
            mkT = [sb(sbk, "mkT%d" % k, [128, 4 * 256], BF16) for k in range(2)]
            mva = [sb(sbk, "mva%d" % k, [128, 2 * 4 * 129], BF16) for k in range(2)]
            R_mk = [Res("mk%d" % k) for k in range(2)]
            R_mv = [Res("mv%d" % k) for k in range(2)]
            mixl = [sb(sbk, "mixl%d" % k, [128, 1024], BF16) for k in range(3)]
            R_mixl = [Res("mixl%d" % k) for k in range(3)]
            xr = [sb(sbk, "xr%d" % k, [128, 1024], F32) for k in range(3)]
            R_xr = [Res("xr%d" % k) for k in range(3)]
            NB3 = 3
            tT_l = [sb(sbk, "tT%d" % k, [128, 1024], BF16) for k in range(NB3)]
            hA_l = [sb(sbk, "hA%d" % k, [128, 1024], F32) for k in range(NB3)]
            hB_l = [sb(sbk, "hB%d" % k, [128, 1024], F32) for k in range(NB3)]
            h16_l = [sb(sbk, "h16%d" % k, [128, 1024], BF16) for k in range(NB3)]
            qmT_l = [sb(sbk, "qmT%d" % k, [128, 512], BF16) for k in range(NB3)]
            PTm_l = [sb(sbk, "PTm%d" % k, [128, 1024], BF16) for k in range(NB3)]
            o16_l = [sb(sbk, "o16%d" % k, [128, 512], BF16) for k in range(NB3)]
            oT_l = [sb(sbk, "oT%d" % k, [128, 512], BF16) for k in range(NB3)]
            stat_l = [sb(sbk, "stat%d" % k, [128, 32], F32) for k in range(NB3)]
            RB = [{n: Res(n + str(k)) for n in ("tT", "hA", "hB", "h16", "qm", "PTm", "o16", "oT", "stat")} for k in range(NB3)]
            h2T = [sb(sbk, "h2T%d" % k, [128, 1024], BF16) for k in range(3)]
            R_h2T = [Res("h2T%d" % k) for k in range(3)]
            mstg = sb(sbk, "mstg", [128, 1024], F32)
            R_mstg = Res("mstg")
            wrot = Rot([0, 1, 2, 3, 4, 5, 6, 7])

            def load_cast(dst, R_dst, src, ncols, engs=("act", "pool")):
                k = 0
                for c0 in range(0, ncols, 2048):
                    w = min(2048, ncols - c0)
                    s = k % 2
                    P.dma("sp", wst[s][:, 0:w], src[:, c0:c0 + w], writes=[R_wst[s]])
                    eng = engs[k % len(engs)]
                    if eng == "act":
                        P.op("act", _call("activation", out=dst[:, c0:c0 + w], in_=wst[s][:, 0:w], func=AF.Copy),
                             reads=[R_wst[s]], writes=[R_dst])
                    else:
                        P.op(eng, _call("tensor_copy", out=dst[:, c0:c0 + w], in_=wst[s][:, 0:w]),
                             reads=[R_wst[s]], writes=[R_dst])
                    k += 1

            load_cast(wob, R_wo, I["wo"], 8192)
            load_cast(wmqb, R_wmq, I["wmq"], 4096)
            load_cast(wmob, R_wmo, I["wmo"], 4096)
            for k in range(4):
                P.dma("sp", lnt[:, k * 1024:(k + 1) * 1024], I["lnp"][k:k + 1, :].to_broadcast([128, 1024]), writes=[R_ln])
            load_cast(memTb, R_memT, I["memT"], 2048)
            load_cast(wtmp, R_wtmp, I["wmk"], 4096)
            for h in range(4):
                bank = wrot.next()
                for kc in range(KC):
                    P.op("pe", _call("matmul",
                        out=pb[bank][:, 0:256], lhsT=wtmp[:, kc * 512 + h * 128: kc * 512 + (h + 1) * 128],
                        rhs=memTb[:, kc * 256:(kc + 1) * 256], start=(kc == 0), stop=(kc == KC - 1)),
                        reads=[R_wtmp, R_memT], writes=[R_pb[bank]])
                P.op("act", _call("activation", out=mkT[0][:, h * 256:(h + 1) * 256], in_=pb[bank][:, 0:256], func=AF.Copy),
                     reads=[R_pb[bank]], writes=[R_mk[0]])
                P.op("dve", _call("tensor_copy", out=mstg[:, h * 256:(h + 1) * 256], in_=pb[bank][:, 0:256]),
                     reads=[R_pb[bank]], writes=[R_mstg])
            P.dma("sp", O["mkT"][:, :], mstg[:, :], reads=[R_mstg], defer=True)
            load_cast(wtmp, R_wtmp, I["wmv"], 4096)
            for mt in range(2):
                bank = wrot.next()
                for kc in range(KC):
                    P.op("pe", _call("matmul",
                        out=pb[bank][:, 0:512], lhsT=memTb[:, kc * 256 + mt * 128: kc * 256 + (mt + 1) * 128],
                        rhs=wtmp[:, kc * 512:(kc + 1) * 512], start=(kc == 0), stop=(kc == KC - 1)),
                        reads=[R_wtmp, R_memT], writes=[R_pb[bank]])
                mvv = mva[0][:, mt * 516:(mt + 1) * 516].rearrange("p (h d) -> p h d", d=129)
                P.op("act", _call("activation", out=mvv[:, :, 0:128], in_=pb[bank][:, :].rearrange("p (h d) -> p h d", d=128), func=AF.Copy),
                     reads=[R_pb[bank]], writes=[R_mv[0]])
                P.op("dve", _call("tensor_copy", out=mstg[:, mt * 512:(mt + 1) * 512], in_=pb[bank][:, :]),
                     reads=[R_pb[bank]], writes=[R_mstg])
                P.dma("sp", O["mv"][mt * 128:(mt + 1) * 128, :], mstg[:, mt * 512:(mt + 1) * 512], reads=[R_mstg], defer=True)
            for k in range(2):
                P.op("pool", _call("memset", mva[k][:, :].rearrange("p (t d) -> p t d", d=129)[:, :, 128:129], 1.0), writes=[R_mv[k]])
            load_cast(mkT[1], R_mk[1], I["cmkT"], 1024)
            P.dma("sp", wst[0][:, 0:1024].rearrange("p (t c) -> p t c", c=512), I["cmv"].rearrange("(t p) c -> p t c", p=128), writes=[R_wst[0]])
            P.op("act", _call("activation", out=mva[1][:, :].rearrange("p (t d) -> p t d", d=129)[:, :, 0:128],
                                               in_=wst[0][:, 0:1024].rearrange("p (t d) -> p t d", d=128), func=AF.Copy),
                 reads=[R_wst[0]], writes=[R_mv[1]])

            checkpoint('phaseB_pre')
            def transpose_to(src16, R_src, qs, nchunk, dst, R_dst):
                bank = wrot.next()
                pbf = pb[bank][:, :].bitcast(BF16)
                for c in range(nchunk):
                    P.op("pe", _call("transpose", out=pbf[:, c * qs:(c + 1) * qs], in_=src16[0:qs, c * 128:(c + 1) * 128],
                                                                   identity=ident[0:qs, 0:qs]),
                         reads=[R_src, R_ident], writes=[R_pb[bank]])
                P.op("act", _call("activation", out=dst[:, 0:nchunk * qs], in_=pbf[:, 0:nchunk * qs], func=AF.Copy),
                     reads=[R_pb[bank]], writes=[R_dst])

            def layer_norm(hin, R_hin, qs, gcol, hout, R_hout, stat, R_stat):
                for c in range(2):
                    P.op("dve", _call("bn_stats", out=stat[0:qs, c * 6:(c + 1) * 6], in_=hin[0:qs, c * 512:(c + 1) * 512]),
                         reads=[R_hin], writes=[R_stat])
                P.op("dve", _call("bn_aggr", out=stat[0:qs, 12:14], in_=stat[0:qs, 0:12]), reads=[R_stat], writes=[R_stat])
                P.op("dve", _call("tensor_scalar", out=stat[0:qs, 14:15], in0=stat[0:qs, 13:14], scalar1=LN_EPS, scalar2=None, op0=ALU.add),
                     reads=[R_stat], writes=[R_stat])
                P.op("act", _call("activation", out=stat[0:qs, 15:16], in_=stat[0:qs, 14:15], func=AF.Sqrt), reads=[R_stat], writes=[R_stat])
                P.op("dve", _call("reciprocal", out=stat[0:qs, 16:17], in_=stat[0:qs, 15:16]), reads=[R_stat], writes=[R_stat])
                P.op("dve", _call("scalar_tensor_tensor", out=stat[0:qs, 17:18], in0=stat[0:qs, 12:13], scalar=-1.0, in1=stat[0:qs, 16:17],
                                                             op0=ALU.mult, op1=ALU.mult),
                     reads=[R_stat], writes=[R_stat])
                P.op("act", _call("activation", out=hout[0:qs, :], in_=hin[0:qs, :], func=AF.Identity, scale=stat[0:qs, 16:17], bias=stat[0:qs, 17:18]),
                     reads=[R_hin, R_stat], writes=[R_hout])
                P.op("dve", _call("tensor_tensor", out=hout[0:qs, :], in0=hout[0:qs, :], in1=lnt[0:qs, gcol * 1024:(gcol + 1) * 1024], op=ALU.mult),
                     reads=[R_hout, R_ln], writes=[R_hout])
                P.op("dve", _call("tensor_tensor", out=hout[0:qs, :], in0=hout[0:qs, :], in1=lnt[0:qs, (gcol + 1) * 1024:(gcol + 2) * 1024], op=ALU.add),
                     reads=[R_hout, R_ln], writes=[R_hout])

            def phaseB_block(blk, qs, row0, mi, k2):
                s = k2
                tT, hA, hB, h16, qmT, PTm, o16, oT, stat = (tT_l[k2], hA_l[k2], hB_l[k2], h16_l[k2], qmT_l[k2], PTm_l[k2], o16_l[k2],
                                                             oT_l[k2], stat_l[k2])
                R_tT, R_hA, R_hB, R_h16, R_qm, R_PTm, R_o16, R_oT, R_stat = (RB[k2][n] for n in ("tT", "hA", "hB", "h16", "qm", "PTm", "o16", "oT", "stat"))
                P.dma("sp", mixl[s][0:qs, :], mixD[blk * 128 + row0: blk * 128 + row0 + qs, :], reads=[R_mixD[blk]], writes=[R_mixl[s]])
                P.dma("sp", xr[s][0:qs, :], I["xres"][blk * 128: blk * 128 + qs, :], writes=[R_xr[s]])
                transpose_to(mixl[s], R_mixl[s], qs, 8, tT, R_tT)
                yield
                b0, b1 = wrot.next(), wrot.next()
                for n, bank in enumerate((b0, b1)):
                    for kc in range(KC):
                        P.op("pe", _call("matmul",
                            out=pb[bank][0:qs, :], lhsT=tT[:, kc * qs:(kc + 1) * qs], rhs=wob[:, kc * 1024 + n * 512: kc * 1024 + (n + 1) * 512],
                            start=(kc == 0), stop=(kc == KC - 1)),
                            reads=[R_tT, R_wo], writes=[R_pb[bank]])
                    P.op("dve", _call("scalar_tensor_tensor",
                        out=hA[0:qs, n * 512:(n + 1) * 512], in0=xr[s][0:qs, n * 512:(n + 1) * 512], scalar=ALPHA, in1=pb[bank][0:qs, :],
                        op0=ALU.mult, op1=ALU.add),
                        reads=[R_xr[s], R_pb[bank]], writes=[R_hA])
                yield
                layer_norm(hA, R_hA, qs, 0, hB, R_hB, stat, R_stat)
                yield
                P.op("pool", _call("tensor_copy", out=h16[0:qs, :], in_=hB[0:qs, :]), reads=[R_hB], writes=[R_h16])
                transpose_to(h16, R_h16, qs, 8, tT, R_tT)
                yield
                bq = wrot.next()
                for h in range(4):
                    for kc in range(KC):
                        P.op("pe", _call("matmul",
                            out=pb[bq][:, h * qs:(h + 1) * qs], lhsT=wmqb[:, kc * 512 + h * 128: kc * 512 + (h + 1) * 128],
                            rhs=tT[:, kc * qs:(kc + 1) * qs], start=(kc == 0), stop=(kc == KC - 1)),
                            reads=[R_wmq, R_tT], writes=[R_pb[bq]])
                P.op("act", _call("activation", out=qmT[:, 0:4 * qs], in_=pb[bq][:, 0:4 * qs], func=AF.Copy, scale=float(128.0 ** -0.5)),
                     reads=[R_pb[bq]], writes=[R_qm])
                yield
                bs0, bs1 = wrot.next(), wrot.next()
                for h in range(4):
                    for mt in range(2):
                        idx = h * 2 + mt
                        bank = bs0 if idx < 4 else bs1
                        c0 = (idx % 4) * qs
                        P.op("pe", _call("matmul",
                            out=pb[bank][:, c0:c0 + qs], lhsT=mkT[mi][:, h * 256 + mt * 128: h * 256 + (mt + 1) * 128],
                            rhs=qmT[:, h * qs:(h + 1) * qs], start=True, stop=True),
                            reads=[R_mk[mi], R_qm], writes=[R_pb[bank]])
                for k, bank in enumerate((bs0, bs1)):
                    P.op("act", _call("activation", out=PTm[:, k * 4 * qs:(k + 1) * 4 * qs], in_=pb[bank][:, 0:4 * qs], func=AF.Exp),
                         reads=[R_pb[bank]], writes=[R_PTm])
                yield
                bo0, bo1 = wrot.next(), wrot.next()
                for h in range(4):
                    bank = bo0 if h < 2 else bo1
                    for mt in range(2):
                        idx = h * 2 + mt
                        P.op("pe", _call("matmul",
                            out=pb[bank][0:qs, (h % 2) * 129:(h % 2) * 129 + 129], lhsT=PTm[:, idx * qs:(idx + 1) * qs],
                            rhs=mva[mi][:, (mt * 4 + h) * 129:(mt * 4 + h) * 129 + 129],
                            start=(h % 2 == 0 and mt == 0), stop=(mt == 1), skip_group_check=True),
                            reads=[R_PTm, R_mv[mi]], writes=[R_pb[bank]])
                for k, bank in enumerate((bo0, bo1)):
                    ov = pb[bank][0:qs, 0:258].rearrange("p (h d) -> p h d", d=129)
                    P.op("dve", _call("tensor_scalar", out=stat[0:qs, 20 + 2 * k:22 + 2 * k].rearrange("p (h o) -> p h o", o=1),
                                                                      in0=ov[:, :, 128:129], scalar1=1e-30, scalar2=None, op0=ALU.max),
                         reads=[R_pb[bank]], writes=[R_stat])
                    P.op("dve", _call("reciprocal", out=stat[0:qs, 20 + 2 * k:22 + 2 * k], in_=stat[0:qs, 20 + 2 * k:22 + 2 * k]),
                         reads=[R_stat], writes=[R_stat])
                    for hh in range(2):
                        h = k * 2 + hh
                        P.op("dve", _call("tensor_scalar",
                            out=o16[0:qs, h * 128:(h + 1) * 128], in0=pb[bank][0:qs, hh * 129: hh * 129 + 128],
                            scalar1=stat[0:qs, 20 + 2 * k + hh:21 + 2 * k + hh], scalar2=None, op0=ALU.mult),
                            reads=[R_pb[bank], R_stat], writes=[R_o16])
                yield
                transpose_to(o16, R_o16, qs, 4, oT, R_oT)
                yield
                b0, b1 = wrot.next(), wrot.next()
                for n, bank in enumerate((b0, b1)):
                    for c in range(4):
                        P.op("pe", _call("matmul",
                            out=pb[bank][0:qs, :], lhsT=oT[:, c * qs:(c + 1) * qs], rhs=wmob[:, c * 1024 + n * 512: c * 1024 + (n + 1) * 512],
                            start=(c == 0), stop=(c == 3)),
                            reads=[R_oT, R_wmo], writes=[R_pb[bank]])
                    P.op("dve", _call("scalar_tensor_tensor",
                        out=hA[0:qs, n * 512:(n + 1) * 512], in0=hB[0:qs, n * 512:(n + 1) * 512], scalar=ALPHA, in1=pb[bank][0:qs, :],
                        op0=ALU.mult, op1=ALU.add),
                        reads=[R_hB, R_pb[bank]], writes=[R_hA])
                yield
                layer_norm(hA, R_hA, qs, 2, hB, R_hB, stat, R_stat)
                yield
                P.dma("sp", h2D[blk * 128: blk * 128 + qs, :], hB[0:qs, :], reads=[R_hB], writes=[R_h2D[blk]], defer=True)
                P.op("pool", _call("tensor_copy", out=h16[0:qs, :], in_=hB[0:qs, :]), reads=[R_hB], writes=[R_h16])
                transpose_to(h16, R_h16, qs, 8, h2T[s], R_h2T[s])
                P.dma("sp", h2TD[blk][:, 0:8 * qs], h2T[s][:, 0:8 * qs], reads=[R_h2T[s]], writes=[R_h2TD[blk]], defer=True)
                yield

            def run_staggered(gens, lag):
                active = []
                pending = list(gens)
                tick = 0
                while active or pending:
                    if pending and (not active or tick >= lag):
                        active.append(pending.pop(0))
                        tick = 0
                    for g in list(active):
                        try:
                            next(g)
                        except StopIteration:
                            active.remove(g)
                    tick += 1

            blocks = [(16, 2, 126, 0), (17, 16, 0, 1)] + [(i, 128, 0, 0) for i in range(16)]
            run_staggered([phaseB_block(b_, q_, r_, m_, pos % 3) for pos, (b_, q_, r_, m_) in enumerate(blocks)], 4)
            checkpoint('phaseB')
            P.flush(block)

        with ExitStack() as sc:
            wdb = sb(sc, "wdb", [128, NFC * 1024], BF16)
            R_wd = Res("wd")
            wst = [sb(sc, "wstc%d" % k, [128, 2048], F32) for k in range(2)]
            R_wst = [Res("wstc%d" % k) for k in range(2)]
            wsl = [sb(sc, "wsl%d" % k, [128, 2048], BF16) for k in range(2)]
            R_wsl = [Res("wsl%d" % k) for k in range(2)]
            R_wslB = [Res("wslB%d" % k) for k in range(2)]
            hT2 = [sb(sc, "hT%d" % k, [128, NFC * 512], BF16) for k in range(2)]
            R_hT2 = [Res("hT%d" % k) for k in range(2)]
            hTm = sb(sc, "hTm", [128, NFC * 16], BF16)
            R_hTm = Res("hTm")
            h2Tg = [sb(sc, "h2Tg%d" % k, [128, 8 * 512], BF16) for k in range(2)]
            R_h2Tg = [Res("h2Tg%d" % k) for k in range(2)]
            h2Tm = sb(sc, "h2Tm", [128, 8 * 18], BF16)
            R_h2Tm = Res("h2Tm")
            Gb = [sb(sc, "Gb%d" % k, [128, 514], F32) for k in range(3)]
            R_Gb = [Res("Gb%d" % k) for k in range(3)]
            Gs = sb(sc, "Gs", [128, 18], F32)
            R_Gs = Res("Gs")
            t0b = [sb(sc, "t0b%d" % k, [128, 512], F32) for k in range(3)]
            R_t0 = [Res("t0%d" % k) for k in range(3)]
            geb = [sb(sc, "geb%d" % k, [128, 512], F32) for k in range(3)]
            R_ge = [Res("ge%d" % k) for k in range(3)]
            t1b = [sb(sc, "t1b%d" % k, [128, 512], F32) for k in range(3)]
            R_t1b = [Res("t1b%d" % k) for k in range(3)]
            t2b = [sb(sc, "t2b%d" % k, [128, 512], F32) for k in range(3)]
            R_t2b = [Res("t2b%d" % k) for k in range(3)]
            t0s = sb(sc, "t0s", [128, 16], F32)
            ges = sb(sc, "ges", [128, 16], F32)
            R_ts = Res("ts")
            carry = sb(sc, "carry", [128, NFC * 2], F32)
            R_carry = [Res("carry%d" % c) for c in range(NFC)]
            sfc = sb(sc, "sfc", [128, NFC * 2], F32)
            R_sfc = Res("sfc")
            sconv = sb(sc, "sconv", [128, NFC * 2], F32)
            wconv = sb(sc, "wconv", [128, NFC * 3], F32)
            bconv = sb(sc, "bconv", [128, NFC], F32)
            flag = sb(sc, "flag", [128, 1], F32)
            R_cc = Res("cc")
            ln3 = sb(sc, "ln3", [128, 2 * 1024], F32)
            R_ln3 = Res("ln3")
            h2r = [sb(sc, "h2r%d" % k, [128, 1024], F32) for k in range(2)]
            R_h2r = [Res("h2r%d" % k) for k in range(2)]
            yA = sb(sc, "yA", [128, 1024], F32)
            R_yA = Res("yA")
            yB = [sb(sc, "yB%d" % k, [128, 1024], F32) for k in range(2)]
            R_yB = [Res("yB%d" % k) for k in range(2)]
            stat = sb(sc, "statc", [128, 32], F32)
            R_stat = Res("statc")

            P.dma("sp", sconv[:, :], I["sconvT"][:, :], writes=[R_cc])
            P.dma("sp", wconv[:, :], I["wconvT"][:, :], writes=[R_cc])
            P.dma("sp", bconv[:, :], I["bconvT"][:, :], writes=[R_cc])
            P.dma("sp", flag[:, :], I["flag"][:, :], writes=[R_cc])
            for k in range(2):
                P.dma("sp", ln3[:, k * 1024:(k + 1) * 1024], I["lnp"][4 + k:5 + k, :].to_broadcast([128, 1024]), writes=[R_ln3])
            k = 0
            for c0 in range(0, NFC * 1024, 2048):
                s = k % 2
                P.dma("sp", wst[s][:, :], I["wdown"][:, c0:c0 + 2048], writes=[R_wst[s]])
                if k % 2 == 0:
                    P.op("act", _call("activation", out=wdb[:, c0:c0 + 2048], in_=wst[s][:, :], func=AF.Copy), reads=[R_wst[s]], writes=[R_wd])
                else:
                    P.op("pool", _call("tensor_copy", out=wdb[:, c0:c0 + 2048], in_=wst[s][:, :]), reads=[R_wst[s]], writes=[R_wd])
                k += 1
            P.dma("sp", h2Tm[:, :].rearrange("p (c q) -> p c q", q=18)[:, :, 0:2], h2TD[16][:, 0:16].rearrange("p (c q) -> p c q", q=2),
                  reads=[R_h2TD[16]], writes=[R_h2Tm], slow=True)
            P.dma("sp", h2Tm[:, :].rearrange("p (c q) -> p c q", q=18)[:, :, 2:18], h2TD[17][:, 0:128].rearrange("p (c q) -> p c q", q=16),
                  reads=[R_h2TD[17]], writes=[R_h2Tm], slow=True)

            checkpoint('phaseC_pre')
            UB = [0, 2, 4]
            GBK = [1, 3, 5]
            MB = 7
            YB = [6, 7]
            wk = [0]

            def ln3_out(pre_banks, qs, h2src, R_h2src, dst_ap, ys, R_ys):
                for n, bank in enumerate(pre_banks):
                    P.op("dve", _call("scalar_tensor_tensor",
                        out=yA[0:qs, n * 512:(n + 1) * 512], in0=h2src[0:qs, n * 512:(n + 1) * 512], scalar=ALPHA, in1=pb[bank][0:qs, :],
                        op0=ALU.mult, op1=ALU.add),
                        reads=[R_h2src, R_pb[bank]], writes=[R_yA])
                for c in range(2):
                    P.op("dve", _call("bn_stats", out=stat[0:qs, c * 6:(c + 1) * 6], in_=yA[0:qs, c * 512:(c + 1) * 512]),
                         reads=[R_yA], writes=[R_stat])
                P.op("dve", _call("bn_aggr", out=stat[0:qs, 12:14], in_=stat[0:qs, 0:12]), reads=[R_stat], writes=[R_stat])
                P.op("dve", _call("tensor_scalar", out=stat[0:qs, 14:15], in0=stat[0:qs, 13:14], scalar1=LN_EPS, scalar2=None, op0=ALU.add),
                     reads=[R_stat], writes=[R_stat])
                P.op("act", _call("activation", out=stat[0:qs, 15:16], in_=stat[0:qs, 14:15], func=AF.Sqrt), reads=[R_stat], writes=[R_stat])
                P.op("dve", _call("reciprocal", out=stat[0:qs, 16:17], in_=stat[0:qs, 15:16]), reads=[R_stat], writes=[R_stat])
                P.op("dve", _call("scalar_tensor_tensor", out=stat[0:qs, 17:18], in0=stat[0:qs, 12:13], scalar=-1.0, in1=stat[0:qs, 16:17],
                                                             op0=ALU.mult, op1=ALU.mult),
                     reads=[R_stat], writes=[R_stat])
                P.op("act", _call("activation", out=ys[0:qs, :], in_=yA[0:qs, :], func=AF.Identity, scale=stat[0:qs, 16:17], bias=stat[0:qs, 17:18]),
                     reads=[R_yA, R_stat], writes=[R_ys])
                P.op("pool", _call("tensor_tensor", out=ys[0:qs, :], in0=ys[0:qs, :], in1=ln3[0:qs, 0:1024], op=ALU.mult),
                     reads=[R_ys, R_ln3], writes=[R_ys])
                P.op("pool", _call("tensor_tensor", out=ys[0:qs, :], in0=ys[0:qs, :], in1=ln3[0:qs, 1024:2048], op=ALU.add),
                     reads=[R_ys, R_ln3], writes=[R_ys])
                P.dma("sp", dst_ap, ys[0:qs, :], reads=[R_ys], defer=True)

            def load_h2Tg(grp):
                gs = grp % 2
                for bi in range(4):
                    blk = grp * 4 + bi
                    P.dma("sp", h2Tg[gs][:, :].rearrange("p (c q) -> p c q", q=512)[:, :, bi * 128:(bi + 1) * 128],
                          h2TD[blk][:, :].rearrange("p (c q) -> p c q", q=128), reads=[R_h2TD[blk]], writes=[R_h2Tg[gs]])

            def c_s1(grp, c):
                s = (grp * NFC + c) % 2
                P.dma("sp", wst[s][:, :], I["wup"][c], writes=[R_wst[s]])
                P.op("pool", _call("tensor_copy", out=wsl[s][:, 0:1152], in_=wst[s][:, 0:1152]), reads=[R_wst[s]], writes=[R_wsl[s]])
                P.op("dve", _call("tensor_copy", out=wsl[s][:, 1152:2048], in_=wst[s][:, 1152:2048]), reads=[R_wst[s]], writes=[R_wslB[s]])

            def c_s2(grp, c):
                s = (grp * NFC + c) % 2
                gs = grp % 2
                mo = (c % 2) * 64
                if grp == 0:
                    for part, oc in ((0, mo), (1, mo + 32)):
                        for kc in range(KC):
                            P.op("pe", _call("matmul", out=pb[MB][:, oc:oc + 18], lhsT=wsl[s][:, kc * 256 + part * 128: kc * 256 + (part + 1) * 128],
                                             rhs=h2Tm[:, kc * 18:(kc + 1) * 18], start=(kc == 0), stop=(kc == KC - 1)),
                                 reads=[R_wsl[s], R_wslB[s], R_h2Tm], writes=[R_pb[MB]])
                k3 = (grp * NFC + c) % 3
                ub, gbk = UB[k3], GBK[k3]
                for part, bank in ((0, ub), (1, gbk)):
                    for kc in range(KC):
                        P.op("pe", _call("matmul", out=pb[bank][:, :], lhsT=wsl[s][:, kc * 256 + part * 128: kc * 256 + (part + 1) * 128],
                                         rhs=h2Tg[gs][:, kc * 512:(kc + 1) * 512], start=(kc == 0), stop=(kc == KC - 1)),
                             reads=[R_wsl[s], R_wslB[s], R_h2Tg[gs]], writes=[R_pb[bank]])

            def c_s3(grp, c):
                hTg, R_hTg = hT2[grp % 2], R_hT2[grp % 2]
                mo = (c % 2) * 64
                if grp == 0:
                    P.op("dve", _call("tensor_scalar", out=carry[:, c * 2:(c + 1) * 2], in0=pb[MB][:, mo + 32:mo + 34], scalar1=flag[:, 0:1],
                                      scalar2=None, op0=ALU.mult),
                         reads=[R_pb[MB], R_cc], writes=[R_carry[c]])
                    P.op("act", _call("activation", out=Gs[:, 0:2], in_=sconv[:, c * 2:(c + 1) * 2], func=AF.Copy), reads=[R_cc], writes=[R_Gs])
                    P.op("act", _call("activation", out=Gs[:, 2:18], in_=pb[MB][:, mo + 34:mo + 50], func=AF.Copy), reads=[R_pb[MB]], writes=[R_Gs])
                    P.op("act", _call("activation", out=t0s[:, :], in_=Gs[:, 2:18], func=AF.Identity, scale=wconv[:, c * 3 + 2:c * 3 + 3],
                                      bias=bconv[:, c:c + 1]),
                         reads=[R_Gs, R_cc], writes=[R_ts])
                    P.op("dve", _call("scalar_tensor_tensor", out=t0s[:, :], in0=Gs[:, 1:17], scalar=wconv[:, c * 3 + 1:c * 3 + 2], in1=t0s[:, :],
                                      op0=ALU.mult, op1=ALU.add),
                         reads=[R_Gs, R_cc, R_ts], writes=[R_ts])
                    P.op("dve", _call("scalar_tensor_tensor", out=t0s[:, :], in0=Gs[:, 0:16], scalar=wconv[:, c * 3:c * 3 + 1], in1=t0s[:, :],
                                      op0=ALU.mult, op1=ALU.add),
                         reads=[R_Gs, R_cc, R_ts], writes=[R_ts])
                    P.op("act", _call("activation", out=ges[:, :], in_=t0s[:, :], func=AF.Gelu_apprx_tanh), reads=[R_ts], writes=[R_ts])
                    P.op("dve", _call("tensor_tensor", out=hTm[:, c * 16:(c + 1) * 16], in0=pb[MB][:, mo + 2:mo + 18], in1=ges[:, :], op=ALU.mult),
                         reads=[R_pb[MB], R_ts], writes=[R_hTm])
                    P.op("act", _call("activation", out=sfc[:, c * 2:(c + 1) * 2], in_=Gs[:, 16:18], func=AF.Copy), reads=[R_Gs], writes=[R_sfc])
                k3 = (grp * NFC + c) % 3
                ub, gbk = UB[k3], GBK[k3]
                G, R_G = Gb[k3], R_Gb[k3]
                t0, R_t = t0b[k3], R_t0[k3]
                ge, R_g = geb[k3], R_ge[k3]
                t1, R_t1 = t1b[k3], R_t1b[k3]
                t2, R_t2 = t2b[k3], R_t2b[k3]
                P.op("act", _call("activation", out=G[:, 0:2], in_=carry[:, c * 2:(c + 1) * 2], func=AF.Copy),
                     reads=[R_carry[c]], writes=[R_G])
                P.op("act", _call("activation", out=G[:, 2:514], in_=pb[gbk][:, :], func=AF.Copy), reads=[R_pb[gbk]], writes=[R_G])
                P.op("act", _call("activation", out=carry[:, c * 2:(c + 1) * 2], in_=G[:, 512:514], func=AF.Copy),
                     reads=[R_G], writes=[R_carry[c]])
                P.op("act", _call("activation", out=t0[:, :], in_=G[:, 2:514], func=AF.Identity,
                                  scale=wconv[:, c * 3 + 2:c * 3 + 3], bias=bconv[:, c:c + 1]),
                     reads=[R_G, R_cc], writes=[R_t])
                P.op("act", _call("activation", out=t1[:, :], in_=G[:, 1:513], func=AF.Identity, scale=wconv[:, c * 3 + 1:c * 3 + 2]),
                     reads=[R_G, R_cc], writes=[R_t1])
                P.op("act", _call("activation", out=t2[:, :], in_=G[:, 0:512], func=AF.Identity, scale=wconv[:, c * 3:c * 3 + 1]),
                     reads=[R_G, R_cc], writes=[R_t2])
                P.op("dve", _call("tensor_tensor", out=t0[:, :], in0=t0[:, :], in1=t1[:, :], op=ALU.add), reads=[R_t, R_t1], writes=[R_t])
                P.op("dve", _call("tensor_tensor", out=t0[:, :], in0=t0[:, :], in1=t2[:, :], op=ALU.add), reads=[R_t, R_t2], writes=[R_t])
                P.op("act", _call("activation", out=ge[:, :], in_=t0[:, :], func=AF.Gelu_apprx_tanh), reads=[R_t], writes=[R_g])
                P.op("dve", _call("tensor_tensor", out=hTg[:, c * 512:(c + 1) * 512], in0=pb[ub][:, :], in1=ge[:, :], op=ALU.mult),
                     reads=[R_pb[ub], R_g], writes=[R_hTg])

            def c_down(grp):
                hTg, R_hTg = hT2[grp % 2], R_hT2[grp % 2]
                if grp == 0:
                    for n, bank in enumerate(YB):
                        for c in range(NFC):
                            P.op("pe", _call("matmul", out=pb[bank][0:16, :], lhsT=hTm[:, c * 16:(c + 1) * 16],
                                             rhs=wdb[:, c * 1024 + n * 512: c * 1024 + (n + 1) * 512], start=(c == 0), stop=(c == NFC - 1)),
                                 reads=[R_hTm, R_wd], writes=[R_pb[bank]])
                    P.dma("sp", h2r[0][0:16, :], h2D[17 * 128: 17 * 128 + 16, :], reads=[R_h2D[17]], writes=[R_h2r[0]])
                    ln3_out(YB, 16, h2r[0], R_h2r[0], O["ys"][:, :], yB[0], R_yB[0])
                    P.dma("sp", O["sfcT"][:, :], sfc[:, :], reads=[R_sfc], defer=True)
                for bi in range(4):
                    blk = grp * 4 + bi
                    hs = blk % 2
                    P.dma("sp", h2r[hs][:, :], h2D[blk * 128:(blk + 1) * 128, :], reads=[R_h2D[blk]], writes=[R_h2r[hs]])
                    for n, bank in enumerate(YB):
                        for c in range(NFC):
                            P.op("pe", _call("matmul", out=pb[bank][:, :], lhsT=hTg[:, c * 512 + bi * 128: c * 512 + (bi + 1) * 128],
                                             rhs=wdb[:, c * 1024 + n * 512: c * 1024 + (n + 1) * 512], start=(c == 0), stop=(c == NFC - 1)),
                                 reads=[R_hTg, R_wd], writes=[R_pb[bank]])
                    ln3_out(YB, 128, h2r[hs], R_h2r[hs], O["y"][blk * 128:(blk + 1) * 128, :], yB[hs], R_yB[hs])

            seq = [(grp, c) for grp in range(4) for c in range(NFC)]
            nseq = len(seq)
            load_h2Tg(0)
            load_h2Tg(1)
            for idx in range(nseq + 2):
                if idx < nseq:
                    c_s1(*seq[idx])
                if 1 <= idx <= nseq:
                    c_s2(*seq[idx - 1])
                if idx >= 2:
                    g3, c3 = seq[idx - 2]
                    c_s3(g3, c3)
                    if c3 == NFC - 1:
                        c_down(g3)
                        if g3 + 2 < 4:
                            load_h2Tg(g3 + 2)
            P.dma("sp", O["fcT"][:, :], carry[:, :], reads=R_carry, defer=True)
            P.finish()
            P.flush(block)
    return nc


def _t5_bucket(rel):
    half, max_exact = 16, 8
    n = np.abs(rel)
    log_ratio = np.log(np.maximum(n, 1).astype(np.float32) / max_exact) / math.log(128 / max_exact)
    large = np.minimum(max_exact + (log_ratio * (half - max_exact)).astype(np.int32), half - 1)
    return np.where(rel < 0, half, 0) + np.where(n < max_exact, n, large)


def _host_inputs(inp):
    f32 = np.float32
    x_prompt = np.asarray(inp["x_prompt"], f32)
    x_sample = np.asarray(inp["x_sample"], f32)
    w_in = np.asarray(inp["w_in"], f32)[0]
    qa, ka, va = w_in[:, 0:512], w_in[:, 512:1024], w_in[:, 1024:1536]
    qb, kb, vb = w_in[:, 1536:2048], w_in[:, 2048:2176], w_in[:, 2176:2304]
    qi, ki, wi = w_in[:, 2304:2816], w_in[:, 2816:2880], w_in[:, 2880:2888]
    qbp = np.concatenate([np.concatenate([qb[:, r * 64:(r + 1) * 64], qb[:, (4 + r) * 64:(5 + r) * 64]], axis=1) for r in range(4)], axis=1)
    winp = np.concatenate([qa, ka, qbp, kb, qi, ki, ki, va, vb, wi], axis=1)
    assert winp.shape[1] == NCOL

    def kc_layout(w):
        n = w.shape[1]
        return np.ascontiguousarray(w.reshape(8, 128, n).transpose(1, 0, 2).reshape(128, 8 * n))

    shared = {}
    shared["win"] = kc_layout(winp)
    shared["wo"] = kc_layout(np.asarray(inp["w_o"], f32)[0])
    shared["wmq"] = kc_layout(np.asarray(inp["w_mq"], f32)[0])
    shared["wmk"] = kc_layout(np.asarray(inp["w_mk"], f32)[0])
    shared["wmv"] = kc_layout(np.asarray(inp["w_mv"], f32)[0])
    wmo = np.asarray(inp["w_mo"], f32)[0]
    shared["wmo"] = np.ascontiguousarray(wmo.reshape(4, 128, 1024).transpose(1, 0, 2).reshape(128, 4096))
    w_up = np.asarray(inp["w_up"], f32)[0]
    wu = w_up[:, :DFF].reshape(8, 128, NFC, 128)
    wg = w_up[:, DFF:].reshape(8, 128, NFC, 128)
    wup = np.stack([wu, wg], axis=3)
    shared["wup"] = np.ascontiguousarray(wup.transpose(2, 1, 0, 3, 4).reshape(NFC, 128, 8 * 256))
    w_down = np.asarray(inp["w_down"], f32)[0]
    shared["wdown"] = np.ascontiguousarray(w_down.reshape(NFC, 128, 1024).transpose(1, 0, 2).reshape(128, NFC * 1024))
    shared["lnp"] = np.ascontiguousarray(np.stack([np.asarray(inp[k], f32)[0] for k in ("ln1_g", "ln1_b", "ln2_g", "ln2_b", "ln3_g", "ln3_b")]))
    w_conv = np.asarray(inp["w_conv"], f32)[0]
    shared["wconvT"] = np.ascontiguousarray(w_conv.reshape(3, NFC, 128).transpose(2, 1, 0).reshape(128, NFC * 3))
    shared["bconvT"] = np.ascontiguousarray(np.asarray(inp["b_conv"], f32)[0].reshape(NFC, 128).T)
    shared["ident"] = np.eye(128, dtype=f32)
    tabA = np.asarray(inp["a_rel_bias"], f32)[0]
    qq = np.arange(128)[:, None]
    kk = np.arange(640)[None, :]
    kpos = kk - 512
    rel = qq - kpos
    cq = qq // 64
    kch = np.floor_divide(kpos, 64)
    allowed = (kch >= cq - 8) & (kch <= cq)
    bias = tabA[np.clip(rel, -64, 64) + 64]
    AB = np.where(allowed[:, :, None], bias, f32(NEGM)).astype(f32)
    shared["AB"] = np.ascontiguousarray(AB.transpose(0, 2, 1).reshape(128, 8 * ABW))
    js = np.arange(16)[:, None]
    ks = np.arange(528)[None, :]
    ABs = tabA[np.clip(512 + js - ks, -64, 64) + 64]
    shared["ABs"] = np.ascontiguousarray(ABs.transpose(0, 2, 1).reshape(16, 8 * 528)).astype(f32)
    t5 = np.asarray(inp["t5_bias"], f32)
    relB = np.arange(128)[:, None] - np.arange(256)[None, :] + 128
    Bn = t5[_t5_bucket(relB)]
    shared["Bn"] = np.ascontiguousarray(Bn.transpose(0, 2, 1).reshape(128, 8 * BNW)).astype(f32)
    relBs = 128 + np.arange(16)[:, None] - np.arange(144)[None, :]
    Bns = t5[_t5_bucket(relBs)]
    shared["Bns"] = np.ascontiguousarray(Bns.transpose(0, 2, 1).reshape(16, 8 * 144)).astype(f32)
    shared["C15"] = np.ascontiguousarray(np.broadcast_to(t5[15][None, :], (128, 8))).astype(f32)
    dm = np.zeros((128, 128), f32)
    dm[0:64, 64:128] = NEGM
    shared["diagmask"] = dm

    mem_prompt = np.asarray(inp["mem_prompt"], f32)
    maps = []
    for c in range(8):
        b, half = c // 2, c % 2
        m = dict(shared)
        xk = np.zeros((4096, 1024), f32)
        if half == 1:
            xk[:] = x_prompt[b]
        else:
            xk[2048:] = x_prompt[b, :2048]
        m["xkT"] = np.ascontiguousarray(xk.reshape(32, 128, 8, 128).transpose(0, 3, 2, 1).reshape(32, 128, 1024))
        xs = x_sample[c]
        m["xsT"] = np.ascontiguousarray(xs.reshape(16, 8, 128).transpose(2, 1, 0).reshape(128, 128))
        xres = np.zeros((NBLK * 128, 1024), f32)
        xres[0:2048] = xk[2048:]
        xres[2048:2050] = xk[2046:2048]
        xres[17 * 128:17 * 128 + 16] = xs
        m["xres"] = xres
        m["memT"] = np.ascontiguousarray(mem_prompt[b].reshape(256, 8, 128).transpose(2, 1, 0).reshape(128, 2048))
        cmk = np.asarray(inp["cache_mem_k"], f32)[0, c]
        m["cmkT"] = np.ascontiguousarray(cmk.transpose(2, 1, 0).reshape(128, 1024))
        m["cmv"] = np.ascontiguousarray(np.asarray(inp["cache_mem_v"], f32)[0, c].reshape(256, 512))
        cak = np.asarray(inp["cache_a_k"], f32)[0, c]
        m["cakT"] = np.ascontiguousarray(cak.reshape(512, 4, 2, 64).transpose(2, 3, 1, 0).reshape(128, 2048))
        m["cav"] = np.ascontiguousarray(np.asarray(inp["cache_a_v"], f32)[0, c].reshape(512, 512))
        cbk = np.asarray(inp["cache_b_k"], f32)[0, c]
        m["cbkT"] = np.ascontiguousarray(cbk.reshape(2048, 128).T)
        m["cbv"] = np.ascontiguousarray(np.asarray(inp["cache_b_v"], f32)[0, c].reshape(2048, 128))
        cbi = np.asarray(inp["cache_b_kidx"], f32)[0, c]
        m["cbiT"] = np.ascontiguousarray(np.concatenate([cbi.T, cbi.T], axis=0))
        sc_ = np.asarray(inp["state_ffn_conv"], f32)[0, c]
        m["sconvT"] = np.ascontiguousarray(sc_.reshape(2, NFC, 128).transpose(2, 1, 0).reshape(128, NFC * 2))
        m["colmask"] = np.full((128, 1), NEGM if half == 0 else 0.0, f32)
        kv = np.ones((128, NT), f32)
        if half == 0:
            kv[:, 0:16] = 0.0
        m["kvalid"] = kv
        m["flag"] = np.full((128, 1), float(half), f32)
        maps.append(m)
    return maps


_NC_CACHE = {}


def _run(inputs, debug=False):
    key = bool(debug)
    if key not in _NC_CACHE:
        _NC_CACHE[key] = build_program(debug=debug)
    nc = _NC_CACHE[key]
    maps = _host_inputs(inputs)
    res = run_bass_kernel_spmd(nc, maps, core_ids=list(range(8)))
    return res.results


def kernel(**inputs):
    R = _run(inputs)
    f32 = np.float32
    y = np.zeros((4, 4096, 1024), f32)
    ys = np.zeros((8, 16, 1024), f32)
    pak = np.zeros((1, 4, 512, 8, 64), f32)
    pav = np.zeros((1, 4, 512, 8, 64), f32)
    pbk = np.zeros((1, 4, 4096, 2, 64), f32)
    pbv = np.zeros((1, 4, 4096, 2, 64), f32)
    pbi = np.zeros((1, 4, 4096, 64), f32)
    pmk = np.zeros((1, 4, 256, 4, 128), f32)
    pmv = np.zeros((1, 4, 256, 4, 128), f32)
    pfc = np.zeros((1, 4, 2, DFF), f32)
    sak = np.zeros((1, 8, 16, 8, 64), f32)
    sav = np.zeros((1, 8, 16, 8, 64), f32)
    sbk = np.zeros((1, 8, 16, 2, 64), f32)
    sbv = np.zeros((1, 8, 16, 2, 64), f32)
    sbi = np.zeros((1, 8, 16, 64), f32)
    sfc = np.zeros((1, 8, 2, DFF), f32)
    for c in range(8):
        b, half = c // 2, c % 2
        r = R[c]
        y[b, half * 2048:(half + 1) * 2048] = np.asarray(r["y"], f32)
        ys[c] = np.asarray(r["ys"], f32)
        if half == 1:
            akT = np.asarray(r["akT"], f32).reshape(2, 64, 4, 512)
            pak[0, b] = akT.transpose(3, 2, 0, 1).reshape(512, 8, 64)
            pav[0, b] = np.asarray(r["av"], f32).reshape(512, 8, 64)
            pbk[0, b] = np.asarray(r["bkT"], f32).T.reshape(4096, 2, 64)
            pbv[0, b] = np.asarray(r["bv"], f32).reshape(4096, 2, 64)
            pbi[0, b] = np.asarray(r["biT"], f32).T
            pmk[0, b] = np.asarray(r["mkT"], f32).reshape(128, 4, 256).transpose(2, 1, 0)
            pmv[0, b] = np.asarray(r["mv"], f32).reshape(256, 4, 128)
            pfc[0, b] = np.asarray(r["fcT"], f32).reshape(128, NFC, 2).transpose(2, 1, 0).reshape(2, DFF)
        sakT = np.asarray(r["sakT"], f32).reshape(2, 64, 4, 16)
        sak[0, c] = sakT.transpose(3, 2, 0, 1).reshape(16, 8, 64)
        sav[0, c] = np.asarray(r["sav"], f32).reshape(16, 8, 64)
        sbk[0, c] = np.asarray(r["sbkT"], f32).T.reshape(16, 2, 64)
        sbv[0, c] = np.asarray(r["sbv"], f32).reshape(16, 2, 64)
        sbi[0, c] = np.asarray(r["sbiT"], f32).T
        sfc[0, c] = np.asarray(r["sfcT"], f32).reshape(128, NFC, 2).transpose(2, 1, 0).reshape(2, DFF)
    return (y, ys, pak, pav, pbk, pbv, pbi, pmk, pmv, pfc, sak, sav, sbk, sbv, sbi, sfc)
```

```python
import math
from contextlib import ExitStack

import numpy as np
import concourse.bass as bass
import concourse.mybir as mybir
from concourse.bass_utils import run_bass_kernel_spmd

F32 = mybir.dt.float32
BF16 = mybir.dt.bfloat16
AF = mybir.ActivationFunctionType
ALU = mybir.AluOpType

D = 1024
KC = 8
NT = 32
NCOL = 2952
C_QA, C_KA, C_QB, C_KB, C_QI, C_KI, C_VA, C_VB, C_WI = 0, 512, 1024, 1536, 1664, 2176, 2304, 2816, 2944
DFF = 2816
NFC = 22
ALPHA = 2.0 ** 0.25
LN_EPS = 1e-5
NEGM = -30000.0
NIT = 17
BIS_W0 = 16.0
ABW = 640
BNW = 256
NBLK = 18


class Res:
    __slots__ = ("lw", "rd", "name", "excl")

    def __init__(self, name="", excl=False):
        self.lw = None
        self.rd = {}
        self.name = name
        self.excl = excl


def _call(name, *args, **kw):
    return lambda e: getattr(e, name)(*args, **kw)


class Prog:
    ENG = ("pe", "act", "dve", "pool", "sp")

    def __init__(self, nc, sems, dma_sems):
        self.nc = nc
        self.streams = {e: [] for e in self.ENG}
        self.sem = sems
        self.cnt = {e: 0 for e in self.ENG}
        self.seen = {e: {} for e in self.ENG}
        self.dsems = dma_sems
        self.dval = [0] * len(dma_sems)
        self.dnext = 0
        self.semh = dict(sems)
        for i, h in enumerate(dma_sems):
            self.semh[("d", i)] = h
        self.ninst = 0
        self.dead = False
        self.deferred = []
        self.defer_lag = 48

    def _deps(self, reads, writes, eng=None):
        d = {}
        for r in reads:
            if r.lw is not None:
                k, v = r.lw
                if d.get(k, 0) < v:
                    d[k] = v
            if r.excl:
                for k, v in r.rd.items():
                    if k != eng and d.get(k, 0) < v:
                        d[k] = v
        for w in writes:
            if w.lw is not None:
                k, v = w.lw
                if d.get(k, 0) < v:
                    d[k] = v
            for k, v in w.rd.items():
                if d.get(k, 0) < v:
                    d[k] = v
        return d

    def _wait(self, eng, deps):
        for k, v in deps.items():
            if k == "pe" and eng == "pe":
                continue
            if self.seen[eng].get(k, 0) >= v:
                continue
            self.seen[eng][k] = v
            h = self.semh[k]
            self.streams[eng].append(lambda e, h=h, v=v: e.wait_ge(h, v))

    def _flush_deferred(self, force=False, reads=(), writes=()):
        if not self.deferred:
            return
        conflict = force
        if not conflict:
            ws = set(id(w) for w in writes)
            rs = set(id(r) for r in reads)
            for d in self.deferred:
                dr = set(id(x) for x in d[3])
                dw = set(id(x) for x in d[4])
                if (ws & dr) or (ws & dw) or (rs & dw):
                    conflict = True
                    break
        if conflict:
            pend, self.deferred = self.deferred, []
            for d in pend:
                self._dma_now(d[0], d[1], d[2], d[3], d[4], d[5])
            return
        while self.deferred and self.ninst - self.deferred[0][6] >= self.defer_lag:
            d = self.deferred.pop(0)
            self._dma_now(d[0], d[1], d[2], d[3], d[4], d[5])

    def op(self, eng, fn, reads=(), writes=()):
        if self.dead:
            return
        self._flush_deferred(False, reads, writes)
        self._wait(eng, self._deps(reads, writes, eng))
        self.cnt[eng] += 1
        n = self.cnt[eng]
        h = self.sem[eng]
        self.streams[eng].append(lambda e, fn=fn, h=h: fn(e).then_inc(h, 1))
        self.ninst += 1
        for r in reads:
            if r.rd.get(eng, 0) < n:
                r.rd[eng] = n
        for w in writes:
            w.lw = (eng, n)
            w.rd = {}

    def dma(self, q, out, in_, reads=(), writes=(), slow=False, defer=False):
        if self.dead:
            return
        if defer:
            self._flush_deferred(False, reads, writes)
            self.deferred.append((q, out, in_, list(reads), list(writes), slow, self.ninst))
            return
        self._flush_deferred(False, reads, writes)
        self._dma_now(q, out, in_, reads, writes, slow)

    def _dma_now(self, q, out, in_, reads=(), writes=(), slow=False):
        deps = self._deps(reads, writes)
        i = self.dnext
        self.dnext = (i + 1) % len(self.dsems)
        k = ("d", i)
        if self.dval[i] > 0 and deps.get(k, 0) < self.dval[i]:
            deps[k] = self.dval[i]
        self._wait(q, deps)
        self.dval[i] += 16
        v = self.dval[i]
        h = self.dsems[i]
        if slow:
            self.streams[q].append(
                lambda e, out=out, in_=in_, h=h: e.dma_start(out=out, in_=in_, allow_slow_non_contiguous=True).then_inc(h, 16))
        else:
            self.streams[q].append(lambda e, out=out, in_=in_, h=h: e.dma_start(out=out, in_=in_).then_inc(h, 16))
        self.ninst += 1
        for r in reads:
            if r.rd.get(k, 0) < v:
                r.rd[k] = v
        for w in writes:
            w.lw = (k, v)
            w.rd = {}

    def barrier(self):
        if self.dead:
            return
        self._flush_deferred(True)
        deps = {e: self.cnt[e] for e in self.ENG if self.cnt[e] > 0}
        for i, v in enumerate(self.dval):
            if v > 0:
                deps[("d", i)] = v
        for e in self.ENG:
            self._wait(e, dict(deps))

    def finish(self):
        self._flush_deferred(True)
        deps = {("d", i): v for i, v in enumerate(self.dval) if v > 0}
        self._wait("sp", deps)

    def flush(self, block):
        self._flush_deferred(True)
        s = self.streams
        self.streams = {e: [] for e in self.ENG}

        def mk(lst):
            def body(e):
                for f in lst:
                    f(e)
            return body

        block.tensor(mk(s["pe"]))
        block.scalar(mk(s["act"]))
        block.vector(mk(s["dve"]))
        block.gpsimd(mk(s["pool"]))
        block.sync(mk(s["sp"]))


def build_program(debug=False, stop_at=None):
    nc = bass.Bass("TRN2", target_bir_lowering=False)

    def din(name, shape, dt=F32):
        return nc.dram_tensor(name, list(shape), dt, kind="ExternalInput").ap()

    def dout(name, shape, dt=F32):
        return nc.dram_tensor(name, list(shape), dt, kind="ExternalOutput").ap()

    def dscr(name, shape, dt):
        return nc.dram_tensor(name, list(shape), dt, kind="Internal").ap()

    I = {}
    I["xkT"] = din("xkT", [NT, 128, 1024])
    I["xsT"] = din("xsT", [128, 8 * 16])
    I["xres"] = din("xres", [NBLK * 128, 1024])
    I["win"] = din("win", [128, KC * NCOL])
    I["wo"] = din("wo", [128, 8 * 1024])
    I["wmq"] = din("wmq", [128, 8 * 512])
    I["wmk"] = din("wmk", [128, 8 * 512])
    I["wmv"] = din("wmv", [128, 8 * 512])
    I["wmo"] = din("wmo", [128, 4 * 1024])
    I["wup"] = din("wup", [NFC, 128, 8 * 256])
    I["wdown"] = din("wdown", [128, NFC * 1024])
    I["lnp"] = din("lnp", [6, 1024])
    I["wconvT"] = din("wconvT", [128, NFC * 3])
    I["bconvT"] = din("bconvT", [128, NFC])
    I["memT"] = din("memT", [128, 8 * 256])
    I["cmkT"] = din("cmkT", [128, 4 * 256])
    I["cmv"] = din("cmv", [256, 512])
    I["cakT"] = din("cakT", [128, 4 * 512])
    I["cav"] = din("cav", [512, 512])
    I["cbkT"] = din("cbkT", [128, 2048])
    I["cbv"] = din("cbv", [2048, 128])
    I["cbiT"] = din("cbiT", [128, 2048])
    I["sconvT"] = din("sconvT", [128, NFC * 2])
    I["ident"] = din("ident", [128, 128])
    I["AB"] = din("AB", [128, 8 * ABW])
    I["ABs"] = din("ABs", [16, 8 * 528])
    I["Bn"] = din("Bn", [128, 8 * BNW])
    I["Bns"] = din("Bns", [16, 8 * 144])
    I["C15"] = din("C15", [128, 8])
    I["colmask"] = din("colmask", [128, 1])
    I["diagmask"] = din("diagmask", [128, 128])
    I["kvalid"] = din("kvalid", [128, NT])
    I["flag"] = din("flag", [128, 1])

    O = {}
    O["y"] = dout("y", [2048, 1024])
    O["ys"] = dout("ys", [16, 1024])
    O["akT"] = dout("akT", [128, 4 * 512])
    O["av"] = dout("av", [512, 512])
    O["bkT"] = dout("bkT", [128, 4096])
    O["bv"] = dout("bv", [4096, 128])
    O["biT"] = dout("biT", [64, 4096])
    O["mkT"] = dout("mkT", [128, 4 * 256])
    O["mv"] = dout("mv", [256, 512])
    O["fcT"] = dout("fcT", [128, NFC * 2])
    O["sakT"] = dout("sakT", [128, 4 * 16])
    O["sav"] = dout("sav", [16, 512])
    O["sbkT"] = dout("sbkT", [128, 16])
    O["sbv"] = dout("sbv", [16, 128])
    O["sbiT"] = dout("sbiT", [64, 16])
    O["sfcT"] = dout("sfcT", [128, NFC * 2])
    if debug:
        O["dbg_mix"] = dout("dbg_mix", [NBLK * 128, 1024], BF16)
        O["dbg_h2"] = dout("dbg_h2", [NBLK * 128, 1024])
        mixD = O["dbg_mix"]
        h2D = O["dbg_h2"]
    else:
        mixD = dscr("mixD", [NBLK * 128, 1024], BF16)
        h2D = dscr("h2D", [NBLK * 128, 1024], F32)
    h2TD = dscr("h2TD", [NBLK, 128, 1024], BF16)
    R_mixD = [Res("mixD%d" % i) for i in range(NBLK)]
    R_h2D = [Res("h2D%d" % i) for i in range(NBLK)]
    R_h2TD = [Res("h2TD%d" % i) for i in range(NBLK)]

    es = ExitStack()
    with es:
        sems = {e: es.enter_context(nc.semaphore("s_" + e)) for e in Prog.ENG}
        dsems = [es.enter_context(nc.semaphore("d%d" % i)) for i in range(32)]
        P = Prog(nc, sems, dsems)
        block = es.enter_context(nc.Block())

        def checkpoint(name):
            if stop_at is not None and name == stop_at and not P.dead:
                P.finish()
                P.flush(block)
                P.dead = True

        pb = [es.enter_context(nc.psum_tensor("pb%d" % i, [128, 512], F32)) for i in range(8)]
        R_pb = [Res("pb%d" % i, excl=True) for i in range(8)]

        class Rot:
            def __init__(self, idxs):
                self.idxs = idxs
                self.i = 0

            def next(self):
                k = self.idxs[self.i % len(self.idxs)]
                self.i += 1
                return k

        def sb(stack, name, shape, dt):
            return stack.enter_context(nc.sbuf_tensor("sb_" + name, list(shape), dt))

        ident_f = sb(es, "ident_f", [128, 128], F32)
        ident = sb(es, "ident", [128, 512], BF16)
        R_ident = Res("ident")
        P.dma("sp", ident_f[:, :], I["ident"][:, :], writes=[R_ident])
        for r in range(4):
            P.op("act", _call("activation", out=ident[:, r * 128:(r + 1) * 128], in_=ident_f[:, :], func=AF.Copy),
                 reads=[R_ident], writes=[R_ident])

        def run_interleaved(gens):
            gens = [[0.0, i, g] for i, g in enumerate(gens)]
            while gens:
                gens.sort(key=lambda x: (x[0], x[1]))
                ent = gens[0]
                try:
                    c = next(ent[2])
                    ent[0] += (c if c else 1.0)
                except StopIteration:
                    gens.remove(ent)

        with ExitStack() as sa:
            winb = sb(sa, "winb", [128, KC * NCOL], BF16)
            R_win = Res("win")
            kbi = sb(sa, "kbi", [128, 2 * 4096], BF16)
            R_kbi = [Res("kbi%d" % r) for r in range(NT)]
            vb_aug = sb(sa, "vb_aug", [128, NT * 2 * 65], BF16)
            R_vb = [Res("vb%d" % r) for r in range(NT)]
            kaT = sb(sa, "kaT", [128, 6 * 512], BF16)
            R_ka = [Res("ka%d" % s) for s in range(6)]
            va_aug = sb(sa, "va_aug", [128, 6 * 8 * 65], BF16)
            R_va = [Res("va%d" % s) for s in range(6)]
            ABb = sb(sa, "ABb", [128, 8 * ABW], BF16)
            R_AB = Res("AB")
            Bnb = sb(sa, "Bnb", [128, 8 * BNW], BF16)
            R_Bn = Res("Bn")
            Mnear = [sb(sa, "Mnear%d" % k, [128, 8 * BNW], BF16) for k in range(2)]
            R_Mnear = [Res("Mnear%d" % k) for k in range(2)]
            score = [sb(sa, "score%d" % k, [128, 4096], F32) for k in range(2)]
            R_score = [Res("score%d" % k) for k in range(2)]
            Mb = [sb(sa, "Mb%d" % k, [128, 4096], BF16) for k in range(2)]
            R_M = [Res("M%d" % k) for k in range(2)]
            relu = [sb(sa, "relu%d" % k, [128, 512], BF16) for k in range(3)]
            R_relu = [Res("relu%d" % k) for k in range(3)]
            xstg2 = [sb(sa, "xstg%d" % k, [128, 1024], F32) for k in range(2)]
            R_xstg2 = [Res("xstg%d" % k) for k in range(2)]
            xstg, R_xstg = xstg2[0], R_xstg2[0]
            xTb = [sb(sa, "xTb%d" % k, [128, 1024], BF16) for k in range(2)]
            R_xT = [Res("xT%d" % k) for k in range(2)]
            qaz = [sb(sa, "qaz%d" % k, [128, 1024], BF16) for k in range(2)]
            qbz = [sb(sa, "qbz%d" % k, [128, 1024], BF16) for k in range(2)]
            qiz = [sb(sa, "qiz%d" % k, [128, 1024], BF16) for k in range(2)]
            R_qa = [Res("qa%d" % k) for k in range(2)]
            R_qb = [Res("qb%d" % k) for k in range(2)]
            R_qi = [Res("qi%d" % k) for k in range(2)]
            coef = [sb(sa, "coef%d" % k, [128, 8], F32) for k in range(2)]
            R_coef = [Res("coef%d" % k) for k in range(2)]
            dg = [sb(sa, "dg%d" % k, [128, 1024], BF16) for k in range(2)]
            R_dg = [Res("dg%d" % k) for k in range(2)]
            PTA = [sb(sa, "PTA%d" % k, [128, 512], BF16) for k in range(3)]
            R_PTA = [Res("PTA%d" % k) for k in range(3)]
            PTB = [sb(sa, "PTB%d" % k, [128, 512], BF16) for k in range(3)]
            R_PTB = [Res("PTB%d" % k) for k in range(3)]
            mixb = [sb(sa, "mixb%d" % k, [128, 1024], BF16) for k in range(2)]
            R_mix = [Res("mix%d" % k) for k in range(2)]
            ostg = [sb(sa, "ostg%d" % k, [128, 256], F32) for k in range(2)]
            R_ostg = [Res("ostg%d" % k) for k in range(2)]
            vbstg = [sb(sa, "vbstg%d" % k, [128, 128], F32) for k in range(2)]
            R_vbstg = [Res("vbstg%d" % k) for k in range(2)]
            astg = sb(sa, "astg", [128, 1024], F32)
            R_astg = Res("astg")
            small = [sb(sa, "small%d" % k, [128, 16], F32) for k in range(2)]
            R_small = [Res("small%d" % k) for k in range(2)]
            recA = [sb(sa, "recA%d" % k, [128, 8], F32) for k in range(2)]
            R_recA = [Res("recA%d" % k) for k in range(2)]
            recB = [sb(sa, "recB%d" % k, [128, 8], F32) for k in range(2)]
            R_recB = [Res("recB%d" % k) for k in range(2)]
            colmask = sb(sa, "colmask", [128, 1], F32)
            diagm = sb(sa, "diagm", [128, 128], F32)
            kvalid = sb(sa, "kvalid", [128, NT], F32)
            c15 = sb(sa, "c15", [128, 8], F32)
            ones8 = sb(sa, "ones8", [128, 8], F32)
            R_cst = Res("cst")

            wrot = Rot([0, 1, 2])

            P.dma("sp", colmask[:, :], I["colmask"][:, :], writes=[R_cst])
            P.dma("sp", diagm[:, :], I["diagmask"][:, :], writes=[R_cst])
            P.dma("sp", kvalid[:, :], I["kvalid"][:, :], writes=[R_cst])
            P.dma("sp", c15[:, :], I["C15"][:, :], writes=[R_cst])
            P.op("pool", _call("memset", ones8[:, :], 1.0), writes=[R_cst])
            for k in range(2):
                P.op("pool", _call("memset", qaz[k][:, :], 0.0), writes=[R_qa[k]])
                P.op("pool", _call("memset", qbz[k][:, :], 0.0), writes=[R_qb[k]])
                P.op("pool", _call("memset", qiz[k][:, :], 0.0), writes=[R_qi[k]])

            HW = NCOL // 2
            for kc in range(KC):
                for hh in range(2):
                    stg, R_stg = score[hh], R_score[hh]
                    P.dma("sp", stg[:, 0:HW], I["win"][:, kc * NCOL + hh * HW: kc * NCOL + (hh + 1) * HW], writes=[R_stg])
                    if hh == 0:
                        P.op("act", _call("activation", out=winb[:, kc * NCOL + hh * HW: kc * NCOL + (hh + 1) * HW], in_=stg[:, 0:HW], func=AF.Copy),
                             reads=[R_stg], writes=[R_win])
                    else:
                        P.op("pool", _call("tensor_copy", out=winb[:, kc * NCOL + hh * HW: kc * NCOL + (hh + 1) * HW], in_=stg[:, 0:HW]),
                             reads=[R_stg], writes=[R_win])
            for hh in range(2):
                w = 4 * ABW
                P.dma("sp", score[hh][:, 0:w], I["AB"][:, hh * w:(hh + 1) * w], writes=[R_score[hh]])
                P.op("act", _call("activation", out=ABb[:, hh * w:(hh + 1) * w], in_=score[hh][:, 0:w], func=AF.Copy),
                     reads=[R_score[hh]], writes=[R_AB])
            P.dma("sp", score[0][:, 0:8 * BNW], I["Bn"][:, :], writes=[R_score[0]])
            for h in range(8):
                P.op("dve", _call("tensor_scalar", out=Bnb[:, h * BNW:(h + 1) * BNW], in0=score[0][:, h * BNW:(h + 1) * BNW],
                                  scalar1=c15[:, h:h + 1], scalar2=None, op0=ALU.subtract),
                     reads=[R_score[0], R_cst], writes=[R_Bn])
            checkpoint('consts')

            def win_cols(kc, c0, n):
                return winb[:, kc * NCOL + c0: kc * NCOL + c0 + n]

            def fm_proj(bank, xT, R_x, N, col0, nchunks, ocol=0):
                for j in range(nchunks):
                    for kc in range(KC):
                        P.op("pe", _call("matmul", out=pb[bank][:, ocol + j * N: ocol + (j + 1) * N], lhsT=win_cols(kc, col0 + j * 128, 128),
                                         rhs=xT[:, kc * N:(kc + 1) * N], start=(kc == 0), stop=(kc == KC - 1)),
                             reads=[R_win, R_x], writes=[R_pb[bank]])

            def tm_proj(bank, xT, R_x, N, col0, ncols, ocol=0):
                for kc in range(KC):
                    P.op("pe", _call("matmul", out=pb[bank][0:N, ocol:ocol + ncols], lhsT=xT[:, kc * N:(kc + 1) * N],
                                     rhs=win_cols(kc, col0, ncols), start=(kc == 0), stop=(kc == KC - 1)),
                         reads=[R_win, R_x], writes=[R_pb[bank]])

            def load_xT(r):
                s = r % 2
                P.dma("sp", xstg2[s][:, :], I["xkT"][r], writes=[R_xstg2[s]])
                P.op("pool", _call("tensor_copy", out=xTb[s][:, :], in_=xstg2[s][:, :]), reads=[R_xstg2[s]], writes=[R_xT[s]])

            def kside(r, full):
                s = r % 2
                xT, R_x = xTb[s], R_xT[s]
                so = r % 2
                bk = wrot.next()
                fm_proj(bk, xT, R_x, 128, C_KB, 1)
                fm_proj(bk, xT, R_x, 128, C_KI, 1, ocol=128)
                P.op("act", _call("activation", out=ostg[so][:, :], in_=pb[bk][:, 0:256], func=AF.Copy), reads=[R_pb[bk]], writes=[R_ostg[so]])
                P.op("pool", _call("tensor_copy", out=kbi[:, :].rearrange("p (a c) -> p a c", a=2)[:, :, r * 128:(r + 1) * 128],
                                   in_=ostg[so][:, :].rearrange("p (a c) -> p a c", a=2)),
                     reads=[R_ostg[so]], writes=[R_kbi[r]])
                P.dma("sp", O["bkT"][:, r * 128:(r + 1) * 128], ostg[so][:, 0:128], reads=[R_ostg[so]], defer=True)
                P.dma("sp", O["biT"][:, r * 128:(r + 1) * 128], ostg[so][0:64, 128:256], reads=[R_ostg[so]], defer=True)
                yield 3.0
                bv_ = wrot.next()
                tm_proj(bv_, xT, R_x, 128, C_VB, 128)
                vbv = vb_aug[:, r * 130:(r + 1) * 130].rearrange("p (g d) -> p g d", d=65)
                P.op("act", _call("activation", out=vbstg[so][:, :], in_=pb[bv_][:, 0:128], func=AF.Copy), reads=[R_pb[bv_]], writes=[R_vbstg[so]])
                P.op("pool", _call("tensor_copy", out=vbv[:, :, 0:64], in_=vbstg[so][:, :].rearrange("p (g d) -> p g d", d=64)),
                     reads=[R_vbstg[so]], writes=[R_vb[r]])
                P.op("pool", _call("tensor_scalar", out=vbv[:, :, 64:65], in0=ones8[:, 0:2].rearrange("p (g o) -> p g o", o=1),
                                   scalar1=kvalid[:, r:r + 1], scalar2=None, op0=ALU.mult),
                     reads=[R_cst], writes=[R_vb[r]])
                P.dma("sp", O["bv"][r * 128:(r + 1) * 128, :], vbstg[so][:, :], reads=[R_vbstg[so]], defer=True)
                yield 3.0
                if not full:
                    return
                slot = r % 6
                ba = wrot.next()
                fm_proj(ba, xT, R_x, 128, C_KA, 4)
                P.op("act", _call("activation", out=kaT[:, slot * 512:(slot + 1) * 512], in_=pb[ba][:, :], func=AF.Copy),
                     reads=[R_pb[ba]], writes=[R_ka[slot]])
                if r >= 28:
                    P.op("dve", _call("tensor_copy", out=astg[:, 0:512], in_=pb[ba][:, :]), reads=[R_pb[ba]], writes=[R_astg])
                    P.dma("sp", O["akT"].rearrange("p (j t) -> p j t", t=512)[:, :, (r - 28) * 128:(r - 27) * 128],
                          astg[:, 0:512].rearrange("p (j t) -> p j t", t=128), reads=[R_astg], defer=True)
                yield 3.0
                bva = wrot.next()
                tm_proj(bva, xT, R_x, 128, C_VA, 512)
                vav = va_aug[:, slot * 520:(slot + 1) * 520].rearrange("p (h d) -> p h d", d=65)
                P.op("act", _call("activation", out=vav[:, :, 0:64], in_=pb[bva][:, :].rearrange("p (h d) -> p h d", d=64), func=AF.Copy),
                     reads=[R_pb[bva]], writes=[R_va[slot]])
                P.op("pool", _call("tensor_scalar", out=vav[:, :, 64:65], in0=ones8[:, :].rearrange("p (h o) -> p h o", o=1),
                                   scalar1=kvalid[:, r:r + 1], scalar2=None, op0=ALU.mult),
                     reads=[R_cst], writes=[R_va[slot]])
                if r >= 28:
                    P.op("dve", _call("tensor_copy", out=astg[:, 512:1024], in_=pb[bva][:, :]), reads=[R_pb[bva]], writes=[R_astg])
                    P.dma("sp", O["av"][(r - 28) * 128:(r - 27) * 128, :], astg[:, 512:1024], reads=[R_astg], defer=True)
                yield 3.0

            def qside(xT, R_x, qs, st):
                b1 = wrot.next()
                fm_proj(b1, xT, R_x, qs, C_QA, 4)
                for hf in range(2):
                    P.op("act", _call("activation",
                                      out=qaz[st][hf * 64:(hf + 1) * 64, 0:8 * qs].rearrange("p (j two q) -> p j two q", two=2, q=qs)[:, :, hf, :],
                                      in_=pb[b1][hf * 64:(hf + 1) * 64, 0:4 * qs].rearrange("p (j q) -> p j q", q=qs), func=AF.Copy, scale=0.125),
                         reads=[R_pb[b1]], writes=[R_qa[st]])
                yield 3.0
                b2 = wrot.next()
                fm_proj(b2, xT, R_x, qs, C_QB, 4)
                for g in range(2):
                    P.op("act", _call("activation", out=qbz[st][g * 64:(g + 1) * 64, g * 4 * qs:(g + 1) * 4 * qs],
                                      in_=pb[b2][g * 64:(g + 1) * 64, 0:4 * qs], func=AF.Copy, scale=0.125),
                         reads=[R_pb[b2]], writes=[R_qb[st]])
                yield 3.0
                b3 = wrot.next()
                fm_proj(b3, xT, R_x, qs, C_QI, 4)
                for hf in range(2):
                    P.op("act", _call("activation",
                                      out=qiz[st][hf * 64:(hf + 1) * 64, 0:8 * qs].rearrange("p (j two q) -> p j two q", two=2, q=qs)[:, :, hf, :],
                                      in_=pb[b3][hf * 64:(hf + 1) * 64, 0:4 * qs].rearrange("p (j q) -> p j q", q=qs), func=AF.Copy),
                         reads=[R_pb[b3]], writes=[R_qi[st]])
                b4 = wrot.next()
                tm_proj(b4, xT, R_x, qs, C_WI, 8)
                P.op("dve", _call("tensor_scalar", out=coef[st][0:qs, :], in0=pb[b4][0:qs, 0:8], scalar1=float(8.0 ** -1.5), scalar2=None, op0=ALU.mult),
                     reads=[R_pb[b4]], writes=[R_coef[st]])
                for h in range(8):
                    P.op("pool", _call("tensor_scalar", out=dg[st][0:qs, h * 128: h * 128 + qs], in0=ident_f[0:qs, 0:qs],
                                       scalar1=coef[st][0:qs, h:h + 1], scalar2=None, op0=ALU.mult),
                         reads=[R_coef[st], R_ident], writes=[R_dg[st]])
                yield 3.0

            def normalize(bank, qs, mixt, R_m, col0, rec, R_rec):
                ov = pb[bank][0:qs, 0:260].rearrange("p (h d) -> p h d", d=65)
                P.op("dve", _call("tensor_scalar", out=rec[0:qs, 0:4].rearrange("p (h o) -> p h o", o=1), in0=ov[:, :, 64:65],
                                  scalar1=1e-30, scalar2=None, op0=ALU.max),
                     reads=[R_pb[bank]], writes=[R_rec])
                P.op("dve", _call("reciprocal", out=rec[0:qs, 0:4], in_=rec[0:qs, 0:4]), reads=[R_rec], writes=[R_rec])
                for hh in range(4):
                    P.op("dve", _call("tensor_scalar", out=mixt[0:qs, col0 + hh * 64: col0 + (hh + 1) * 64],
                                      in0=pb[bank][0:qs, hh * 65: hh * 65 + 64],
                                      scalar1=rec[0:qs, hh:hh + 1], scalar2=None, op0=ALU.mult),
                         reads=[R_pb[bank], R_rec], writes=[R_m])

            def pipe3(items, s1, s2, s3, D, cost=1.0):
                pend = []
                for it in items:
                    s1(it)
                    s2(it)
                    pend.append(it)
                    if len(pend) > D:
                        s3(pend.pop(0))
                    yield cost
                while pend:
                    s3(pend.pop(0))
                    yield cost

            pta_rot = Rot([0, 1, 2])
            relu_rot = Rot([0, 1, 2])
            ptb_rot = Rot([0, 1, 2])
            brot = Rot([3, 7])

            def front_attn(sn, qs, wins, btiles, prompt_masks, abw):
                st = sn % 2
                mixt, R_m = mixb[st], R_mix[st]
                nw = len(wins)

                units = []
                for h in range(8):
                    units.append({"h": h, "t0": 0, "tiles": wins[0:4]})
                    if nw > 4:
                        units.append({"h": h, "t0": 4, "tiles": wins[4:5]})

                def a1(u):
                    h = u["h"]
                    j = h // 2
                    bank = wrot.next()
                    u["bank"] = bank
                    for i, (slot, ts) in enumerate(u["tiles"]):
                        t = u["t0"] + i
                        c0 = i * qs
                        P.op("pe", _call("matmul", out=pb[bank][0:ts, c0:c0 + qs], lhsT=kaT[:, slot * 512 + j * 128: slot * 512 + j * 128 + ts],
                                         rhs=qaz[st][:, h * qs:(h + 1) * qs], start=True, stop=False),
                             reads=[R_ka[slot], R_qa[st]], writes=[R_pb[bank]])
                        P.op("pe", _call("matmul", out=pb[bank][0:ts, c0:c0 + qs], lhsT=ABb[0:qs, h * abw + t * 128: h * abw + t * 128 + ts],
                                         rhs=ident[0:qs, 0:qs], start=False, stop=True),
                             reads=[R_AB, R_ident], writes=[R_pb[bank]])

                def a2(u):
                    k = pta_rot.next()
                    u["pt"], u["R_pt"] = PTA[k], R_PTA[k]
                    bank = u["bank"]
                    tsm = max(ts for (_, ts) in u["tiles"])
                    n = len(u["tiles"])
                    P.op("act", _call("activation", out=u["pt"][0:tsm, 0:n * qs], in_=pb[bank][0:tsm, 0:n * qs], func=AF.Exp),
                         reads=[R_pb[bank]], writes=[u["R_pt"]])

                def a3(u):
                    h = u["h"]
                    last_unit = (u["t0"] + len(u["tiles"]) == nw)
                    for i, (slot, ts) in enumerate(u["tiles"]):
                        t = u["t0"] + i
                        P.op("pe", _call("matmul", out=pb[4][0:qs, (h % 4) * 65:(h % 4) * 65 + 65], lhsT=u["pt"][0:ts, i * qs:(i + 1) * qs],
                                         rhs=va_aug[0:ts, slot * 520 + h * 65: slot * 520 + h * 65 + 65],
                                         start=(h % 4 == 0 and t == 0), stop=(t == nw - 1), skip_group_check=True),
                             reads=[u["R_pt"], R_va[slot]], writes=[R_pb[4]])
                    if last_unit and h % 4 == 3:
                        normalize(4, qs, mixt, R_m, (h // 4) * 256, recA[st], R_recA[st])

                yield from pipe3(units, a1, a2, a3, 2, 0.9)

                L = btiles[-1][1] + btiles[-1][2]
                items = []
                cc = 0
                for c0 in range(0, L, 512):
                    w = min(512, L - c0)
                    rk = [R_kbi[tt[0]] for tt in btiles if tt[1] >= c0 - 127 and tt[1] < c0 + w]
                    for h in range(8):
                        items.append({"c0": c0, "w": w, "h": h, "sc": (5, 4)[cc % 2], "rk": rk})
                    cc += 1

                def i1(it):
                    bank = wrot.next()
                    it["bank"] = bank
                    h, c0, w = it["h"], it["c0"], it["w"]
                    P.op("pe", _call("matmul", out=pb[bank][0:qs, 0:w], lhsT=qiz[st][:, h * qs:(h + 1) * qs],
                                     rhs=kbi[:, 4096 + c0: 4096 + c0 + w], start=True, stop=True),
                         reads=[R_qi[st]] + it["rk"], writes=[R_pb[bank]])

                def i2(it):
                    k = relu_rot.next()
                    it["rl"], it["R_rl"] = relu[k], R_relu[k]
                    w = it["w"]
                    P.op("act", _call("activation", out=it["rl"][0:qs, 0:w], in_=pb[it["bank"]][0:qs, 0:w], func=AF.Relu),
                         reads=[R_pb[it["bank"]]], writes=[it["R_rl"]])

                def i3(it):
                    h, c0, w, sc = it["h"], it["c0"], it["w"], it["sc"]
                    P.op("pe", _call("matmul", out=pb[sc][0:qs, 0:w], lhsT=dg[st][0:qs, h * 128: h * 128 + qs], rhs=it["rl"][0:qs, 0:w],
                                     start=(h == 0), stop=(h == 7)),
                         reads=[R_dg[st], it["R_rl"]], writes=[R_pb[sc]])
                    if h == 7:
                        if prompt_masks and c0 < 2048:
                            wm = min(w, 2048 - c0)
                            P.op("act", _call("activation", out=score[st][0:qs, c0:c0 + wm], in_=pb[sc][0:qs, 0:wm], func=AF.Identity,
                                              bias=colmask[0:qs, 0:1]),
                                 reads=[R_pb[sc], R_cst], writes=[R_score[st]])
                            if wm < w:
                                P.op("act", _call("activation", out=score[st][0:qs, c0 + wm:c0 + w], in_=pb[sc][0:qs, wm:w], func=AF.Copy),
                                     reads=[R_pb[sc]], writes=[R_score[st]])
                        else:
                            P.op("act", _call("activation", out=score[st][0:qs, c0:c0 + w], in_=pb[sc][0:qs, 0:w], func=AF.Copy),
                                 reads=[R_pb[sc]], writes=[R_score[st]])

                yield from pipe3(items, i1, i2, i3, 2, 0.65)
                if prompt_masks:
                    P.op("dve", _call("tensor_tensor", out=score[st][0:qs, L - 128:L], in0=score[st][0:qs, L - 128:L], in1=diagm[0:qs, :], op=ALU.add),
                         reads=[R_score[st], R_cst], writes=[R_score[st]])
                yield

            def back_attn(sn, qs, btiles, blk, bnw):
                st = sn % 2
                mixt, R_m = mixb[st], R_mix[st]
                sm, R_sm = small[st], R_small[st]
                L = btiles[-1][1] + btiles[-1][2]
                P.op("dve", _call("memset", sm[0:qs, 1:2], 0.0), writes=[R_sm])
                for k in range(NIT):
                    wk = BIS_W0 / (2.0 ** k)
                    P.op("dve", _call("tensor_scalar", out=Mb[st][0:qs, 0:L], in0=score[st][0:qs, 0:L], scalar1=sm[0:qs, 1:2], scalar2=None,
                                      op0=ALU.is_ge, op1=ALU.add, accum_out=sm[0:qs, 0:1]),
                         reads=[R_score[st], R_sm], writes=[R_M[st], R_sm])
                    P.op("dve", _call("tensor_scalar", out=sm[0:qs, 2:3], in0=sm[0:qs, 0:1], scalar1=255.5, scalar2=wk,
                                      op0=ALU.is_ge, op1=ALU.mult),
                         reads=[R_sm], writes=[R_sm])
                    P.op("dve", _call("scalar_tensor_tensor", out=sm[0:qs, 1:2], in0=sm[0:qs, 2:3], scalar=-wk / 2.0,
                                      in1=sm[0:qs, 1:2], op0=ALU.add, op1=ALU.add),
                         reads=[R_sm], writes=[R_sm])
                    yield L * 1.08e-3 + 0.5
                wl = BIS_W0 / (2.0 ** (NIT - 1)) / 2.0
                P.op("dve", _call("tensor_scalar", out=sm[0:qs, 3:4], in0=sm[0:qs, 1:2], scalar1=-wl, scalar2=None, op0=ALU.add),
                     reads=[R_sm], writes=[R_sm])
                P.op("dve", _call("tensor_scalar", out=Mb[st][0:qs, 0:L], in0=score[st][0:qs, 0:L], scalar1=sm[0:qs, 3:4], scalar2=NEGM,
                                  op0=ALU.is_lt, op1=ALU.mult),
                     reads=[R_score[st], R_sm], writes=[R_M[st]])
                nearw = btiles[-2][2] + btiles[-1][2]
                for h in range(8):
                    P.op("dve", _call("tensor_tensor", out=Mnear[st][0:qs, h * bnw: h * bnw + nearw], in0=Bnb[0:qs, h * bnw: h * bnw + nearw],
                                      in1=Mb[st][0:qs, L - nearw:L], op=ALU.add),
                         reads=[R_Bn, R_M[st]], writes=[R_Mnear[st]])
                yield
                nb = len(btiles)
                items = [{"g": g, "t": t, "vt": vt, "c0": c0, "ts": ts} for g in range(2) for t, (vt, c0, ts) in enumerate(btiles)]

                def b1(it):
                    g, t, vt, c0, ts = it["g"], it["t"], it["vt"], it["c0"], it["ts"]
                    bank = brot.next()
                    it["bank"] = bank
                    P.op("pe", _call("matmul", out=pb[bank][0:ts, 0:4 * qs], lhsT=kbi[:, c0:c0 + ts],
                                     rhs=qbz[st][:, g * 4 * qs:(g + 1) * 4 * qs], start=True, stop=False),
                         reads=[R_kbi[vt], R_qb[st]], writes=[R_pb[bank]])
                    if t < nb - 2 and qs == 128:
                        P.op("pe", _call("matmul", out=pb[bank][0:ts, 0:512], lhsT=Mb[st][0:qs, c0:c0 + ts], rhs=ident[0:128, 0:512],
                                         start=False, stop=True),
                             reads=[R_M[st], R_ident], writes=[R_pb[bank]])
                    elif t < nb - 2:
                        for r in range(4):
                            P.op("pe", _call("matmul", out=pb[bank][0:ts, r * qs:(r + 1) * qs], lhsT=Mb[st][0:qs, c0:c0 + ts],
                                             rhs=ident[0:qs, 0:qs], start=False, stop=(r == 3)),
                                 reads=[R_M[st], R_ident], writes=[R_pb[bank]])
                    else:
                        tt = t - (nb - 2)
                        for r in range(4):
                            hh = g * 4 + r
                            P.op("pe", _call("matmul", out=pb[bank][0:ts, r * qs:(r + 1) * qs],
                                             lhsT=Mnear[st][0:qs, hh * bnw + tt * 128: hh * bnw + tt * 128 + ts], rhs=ident[0:qs, 0:qs],
                                             start=False, stop=(r == 3)),
                                 reads=[R_Mnear[st], R_ident], writes=[R_pb[bank]])

                def b2(it):
                    k = ptb_rot.next()
                    it["ptb"], it["R_ptb"] = PTB[k], R_PTB[k]
                    ts = it["ts"]
                    P.op("act", _call("activation", out=it["ptb"][0:ts, 0:4 * qs], in_=pb[it["bank"]][0:ts, 0:4 * qs], func=AF.Exp),
                         reads=[R_pb[it["bank"]]], writes=[it["R_ptb"]])

                def b3(it):
                    g, t, vt, ts = it["g"], it["t"], it["vt"], it["ts"]
                    for r in range(4):
                        P.op("pe", _call("matmul", out=pb[6][0:qs, r * 65: r * 65 + 65], lhsT=it["ptb"][0:ts, r * qs:(r + 1) * qs],
                                         rhs=vb_aug[0:ts, (vt * 2 + g) * 65:(vt * 2 + g) * 65 + 65],
                                         start=(t == 0 and r == 0), stop=(t == nb - 1), skip_group_check=True),
                             reads=[it["R_ptb"], R_vb[vt]], writes=[R_pb[6]])
                    if t == nb - 1:
                        normalize(6, qs, mixt, R_m, 512 + g * 256, recB[st], R_recB[st])

                yield from pipe3(items, b1, b2, b3, 1, 0.8)
                P.dma("sp", mixD[blk * 128: blk * 128 + qs, :], mixt[0:qs, :], reads=[R_m], writes=[R_mixD[blk]], defer=True)
                yield

            load_xT(0)
            for r in range(16):
                if r + 1 < 16:
                    load_xT(r + 1)
                for _ in kside(r, full=(r >= 11)):
                    pass
            checkpoint('phase0')

            def prompt_front(sn, T):
                if T >= 16:
                    load_xT(T)
                    yield from kside(T, full=True)
                s = T % 2
                yield from qside(xTb[s], R_xT[s], 128, sn % 2)
                wins = [((T - 4 + t) % 6, 128) for t in range(5)]
                btiles = [(t, t * 128, 128) for t in range(T + 1)]
                yield from front_attn(sn, 128, wins, btiles, True, ABW)

            def prompt_back(sn, T, blk):
                btiles = [(t, t * 128, 128) for t in range(T + 1)]
                yield from back_attn(sn, 128, btiles, blk, BNW)

            steps = [(0, 15, 16)] + [(1 + i, 16 + i, i) for i in range(16)]
            run_interleaved([prompt_front(*steps[0][0:2])])
            for si in range(len(steps)):
                sn, T, blk = steps[si]
                gens = [prompt_back(sn, T, blk)]
                if si + 1 < len(steps):
                    gens.append(prompt_front(*steps[si + 1][0:2]))
                run_interleaved(gens)
            checkpoint('steps')

            SN = len(steps)
            sst = SN % 2
            P.dma("sp", score[0][:, 0:2048], I["cbkT"][:, :], writes=[R_score[0]])
            P.op("act", _call("activation", out=kbi[:, 0:2048], in_=score[0][:, 0:2048], func=AF.Copy),
                 reads=[R_score[0]], writes=R_kbi[0:16])
            P.dma("sp", score[0][:, 2048:4096], I["cbiT"][:, :], writes=[R_score[0]])
            P.op("act", _call("activation", out=kbi[:, 4096:4096 + 2048], in_=score[0][:, 2048:4096], func=AF.Copy),
                 reads=[R_score[0]], writes=R_kbi[0:16])
            P.dma("sp", score[1][:, 0:2048].rearrange("p (t c) -> p t c", c=128), I["cbv"].rearrange("(t p) c -> p t c", p=128), writes=[R_score[1]])
            vball = vb_aug[:, 0:16 * 130].rearrange("p (t d) -> p t d", d=65)
            P.op("act", _call("activation", out=vball[:, :, 0:64], in_=score[1][:, 0:2048].rearrange("p (t d) -> p t d", d=64), func=AF.Copy),
                 reads=[R_score[1]], writes=R_vb[0:17])
            P.op("pool", _call("memset", vb_aug[:, 0:17 * 130].rearrange("p (t d) -> p t d", d=65)[:, :, 64:65], 1.0), writes=R_vb[0:17])
            P.dma("sp", score[0][:, 0:2048], I["cakT"][:, :], writes=[R_score[0]])
            for s4 in range(4):
                P.op("act", _call("activation", out=kaT[:, s4 * 512:(s4 + 1) * 512].rearrange("p (j t) -> p j t", t=128),
                                  in_=score[0][:, 0:2048].rearrange("p (j t) -> p j t", t=512)[:, :, s4 * 128:(s4 + 1) * 128], func=AF.Copy),
                     reads=[R_score[0]], writes=[R_ka[s4]])
            P.dma("sp", score[1][:, 2048:4096].rearrange("p (t c) -> p t c", c=512), I["cav"].rearrange("(t p) c -> p t c", p=128), writes=[R_score[1]])
            vaall = va_aug[:, 0:4 * 520].rearrange("p (t d) -> p t d", d=65)
            P.op("act", _call("activation", out=vaall[:, :, 0:64], in_=score[1][:, 2048:4096].rearrange("p (t d) -> p t d", d=64), func=AF.Copy),
                 reads=[R_score[1]], writes=R_va[0:5])
            P.op("pool", _call("memset", va_aug[:, 0:5 * 520].rearrange("p (t d) -> p t d", d=65)[:, :, 64:65], 1.0), writes=R_va[0:5])
            for hh in range(2):
                w = 4 * 528
                P.dma("sp", score[0][0:16, 0:w], I["ABs"][:, hh * w:(hh + 1) * w], writes=[R_score[0]])
                P.op("act", _call("activation", out=ABb[0:16, hh * w:(hh + 1) * w], in_=score[0][0:16, 0:w], func=AF.Copy),
                     reads=[R_score[0]], writes=[R_AB])
            P.dma("sp", score[1][0:16, 0:8 * 144], I["Bns"][:, :], writes=[R_score[1]])
            for h in range(8):
                P.op("dve", _call("tensor_scalar", out=Bnb[0:16, h * 144:(h + 1) * 144], in0=score[1][0:16, h * 144:(h + 1) * 144],
                                  scalar1=c15[0:16, h:h + 1], scalar2=None, op0=ALU.subtract),
                     reads=[R_score[1], R_cst], writes=[R_Bn])
            P.op("pool", _call("memset", qaz[sst][:, :], 0.0), writes=[R_qa[sst]])
            P.op("pool", _call("memset", qbz[sst][:, :], 0.0), writes=[R_qb[sst]])
            P.op("pool", _call("memset", qiz[sst][:, :], 0.0), writes=[R_qi[sst]])
            P.dma("sp", xstg[:, 0:128], I["xsT"][:, :], writes=[R_xstg])
            P.op("pool", _call("tensor_copy", out=xTb[0][:, 0:128], in_=xstg[:, 0:128]), reads=[R_xstg], writes=[R_xT[0]])
            xs_, R_xs = xTb[0], R_xT[0]
            bk = wrot.next()
            fm_proj(bk, xs_, R_xs, 16, C_KB, 1)
            fm_proj(bk, xs_, R_xs, 16, C_KI, 1, ocol=16)
            P.op("act", _call("activation", out=kbi[:, 2048:2064], in_=pb[bk][:, 0:16], func=AF.Copy), reads=[R_pb[bk]], writes=[R_kbi[16]])
            P.op("act", _call("activation", out=kbi[:, 4096 + 2048:4096 + 2064], in_=pb[bk][:, 16:32], func=AF.Copy), reads=[R_pb[bk]], writes=[R_kbi[16]])
            P.op("dve", _call("tensor_copy", out=ostg[0][:, 0:32], in_=pb[bk][:, 0:32]), reads=[R_pb[bk]], writes=[R_ostg[0]])
            P.dma("sp", O["sbkT"][:, :], ostg[0][:, 0:16], reads=[R_ostg[0]], defer=True)
            P.dma("sp", O["sbiT"][:, :], ostg[0][0:64, 16:32], reads=[R_ostg[0]], defer=True)
            bv_ = wrot.next()
            tm_proj(bv_, xs_, R_xs, 16, C_VB, 128)
            vbv = vb_aug[0:16, 16 * 130:17 * 130].rearrange("p (g d) -> p g d", d=65)
            P.op("act", _call("activation", out=vbv[:, :, 0:64], in_=pb[bv_][0:16, 0:128].rearrange("p (g d) -> p g d", d=64), func=AF.Copy),
                 reads=[R_pb[bv_]], writes=[R_vb[16]])
            P.op("dve", _call("tensor_copy", out=vbstg[0][0:16, :], in_=pb[bv_][0:16, 0:128]), reads=[R_pb[bv_]], writes=[R_vbstg[0]])
            P.dma("sp", O["sbv"][:, :], vbstg[0][0:16, :], reads=[R_vbstg[0]], defer=True)
            ba = wrot.next()
            fm_proj(ba, xs_, R_xs, 16, C_KA, 4)
            P.op("act", _call("activation", out=kaT[:, 4 * 512:5 * 512].rearrange("p (j t) -> p j t", t=128)[:, :, 0:16],
                              in_=pb[ba][:, 0:64].rearrange("p (j t) -> p j t", t=16), func=AF.Copy),
                 reads=[R_pb[ba]], writes=[R_ka[4]])
            P.op("dve", _call("tensor_copy", out=astg[:, 0:64], in_=pb[ba][:, 0:64]), reads=[R_pb[ba]], writes=[R_astg])
            P.dma("sp", O["sakT"][:, :], astg[:, 0:64], reads=[R_astg], defer=True)
            bva = wrot.next()
            tm_proj(bva, xs_, R_xs, 16, C_VA, 512)
            vav = va_aug[0:16, 4 * 520:5 * 520].rearrange("p (h d) -> p h d", d=65)
            P.op("act", _call("activation", out=vav[:, :, 0:64], in_=pb[bva][0:16, :].rearrange("p (h d) -> p h d", d=64), func=AF.Copy),
                 reads=[R_pb[bva]], writes=[R_va[4]])
            P.op("dve", _call("tensor_copy", out=astg[0:16, 512:1024], in_=pb[bva][0:16, :]), reads=[R_pb[bva]], writes=[R_astg])
            P.dma("sp", O["sav"][:, :], astg[0:16, 512:1024], reads=[R_astg], defer=True)
            checkpoint('sample_pre')
            wins = [(0, 128), (1, 128), (2, 128), (3, 128), (4, 16)]
            btiles = [(t, t * 128, 128) for t in range(16)] + [(16, 2048, 16)]

            def sample_all():
                yield from qside(xs_, R_xs, 16, sst)
                yield from front_attn(SN, 16, wins, btiles, False, 528)
                yield from back_attn(SN, 16, btiles, 17, 144)

            run_interleaved([sample_all()])
            checkpoint('phaseA')
            P.flush(block)

        P.barrier()
        with ExitStack() as sbk:
            wob = sb(sbk, "wob", [128, 8 * 1024], BF16)
            wmqb = sb(sbk, "wmqb", [128, 8 * 512], BF16)
            wmob = sb(sbk, "wmob", [128, 4 * 1024], BF16)
            wtmp = sb(sbk, "wtmp", [128, 8 * 512], BF16)
            R_wo, R_wmq, R_wmo, R_wtmp = Res("wo"), Res("wmq"), Res("wmo"), Res("wtmp")
            wst = [sb(sbk, "wst%d" % k, [128, 2048], F32) for k in range(2)]
            R_wst = [Res("wst%d" % k) for k in range(2)]
            lnt = sb(sbk, "lnt", [128, 4 * 1024], F32)
            R_ln = Res("ln")
            memTb = sb(sbk, "memTb", [128, 8 * 256], BF16)
            R_memT = Res("memT")
            mkT = [sb(sbk, "mkT%d" % k, [128, 4 * 256], BF16) for k in range(2)]
            mva = [sb(sbk, "mva%d" % k, [128, 2 * 4 * 129], BF16) for k in range(2)]
            R_mk = [Res("mk%d" % k) for k in range(2)]
            R_mv = [Res("mv%d" % k) for k in range(2)]
            mixl = [sb(sbk, "mixl%d" % k, [128, 1024], BF16) for k in range(4)]
            R_mixl = [Res("mixl%d" % k) for k in range(4)]
            xr = [sb(sbk, "xr%d" % k, [128, 1024], F32) for k in range(4)]
            R_xr = [Res("xr%d" % k) for k in range(4)]
            NB3 = 4
            tT_l = [sb(sbk, "tT%d" % k, [128, 1024], BF16) for k in range(NB3)]
            hA_l = [sb(sbk, "hA%d" % k, [128, 1024], F32) for k in range(NB3)]
            hB_l = [sb(sbk, "hB%d" % k, [128, 1024], F32) for k in range(NB3)]
            h16_l = [sb(sbk, "h16%d" % k, [128, 1024], BF16) for k in range(NB3)]
            qmT_l = [sb(sbk, "qmT%d" % k, [128, 512], BF16) for k in range(NB3)]
            PTm_l = [sb(sbk, "PTm%d" % k, [128, 1024], BF16) for k in range(NB3)]
            o16_l = [sb(sbk, "o16%d" % k, [128, 512], BF16) for k in range(NB3)]
            oT_l = [sb(sbk, "oT%d" % k, [128, 512], BF16) for k in range(NB3)]
            stat_l = [sb(sbk, "stat%d" % k, [128, 32], F32) for k in range(NB3)]
            RB = [{n: Res(n + str(k)) for n in ("tT", "hA", "hB", "h16", "qm", "PTm", "o16", "oT", "stat")} for k in range(NB3)]
            h2T = [sb(sbk, "h2T%d" % k, [128, 1024], BF16) for k in range(4)]
            R_h2T = [Res("h2T%d" % k) for k in range(4)]
            mstg = sb(sbk, "mstg", [128, 1024], F32)
            R_mstg = Res("mstg")
            wrot = Rot([0, 1, 2, 3, 4, 5, 6, 7])

            def load_cast(dst, R_dst, src, ncols, engs=("act", "pool")):
                k = 0
                for c0 in range(0, ncols, 2048):
                    w = min(2048, ncols - c0)
                    s = k % 2
                    P.dma("sp", wst[s][:, 0:w], src[:, c0:c0 + w], writes=[R_wst[s]])
                    eng = engs[k % len(engs)]
                    if eng == "act":
                        P.op("act", _call("activation", out=dst[:, c0:c0 + w], in_=wst[s][:, 0:w], func=AF.Copy),
                             reads=[R_wst[s]], writes=[R_dst])
                    else:
                        P.op(eng, _call("tensor_copy", out=dst[:, c0:c0 + w], in_=wst[s][:, 0:w]),
                             reads=[R_wst[s]], writes=[R_dst])
                    k += 1

            load_cast(wob, R_wo, I["wo"], 8192)
            load_cast(wmqb, R_wmq, I["wmq"], 4096)
            load_cast(wmob, R_wmo, I["wmo"], 4096)
            for k in range(4):
                P.dma("sp", lnt[:, k * 1024:(k + 1) * 1024], I["lnp"][k:k + 1, :].to_broadcast([128, 1024]), writes=[R_ln])
            load_cast(memTb, R_memT, I["memT"], 2048)
            load_cast(wtmp, R_wtmp, I["wmk"], 4096)
            for h in range(4):
                bank = wrot.next()
                for kc in range(KC):
                    P.op("pe", _call("matmul",
                        out=pb[bank][:, 0:256], lhsT=wtmp[:, kc * 512 + h * 128: kc * 512 + (h + 1) * 128],
                        rhs=memTb[:, kc * 256:(kc + 1) * 256], start=(kc == 0), stop=(kc == KC - 1)),
                        reads=[R_wtmp, R_memT], writes=[R_pb[bank]])
                P.op("act", _call("activation", out=mkT[0][:, h * 256:(h + 1) * 256], in_=pb[bank][:, 0:256], func=AF.Copy),
                     reads=[R_pb[bank]], writes=[R_mk[0]])
                P.op("dve", _call("tensor_copy", out=mstg[:, h * 256:(h + 1) * 256], in_=pb[bank][:, 0:256]),
                     reads=[R_pb[bank]], writes=[R_mstg])
            P.dma("sp", O["mkT"][:, :], mstg[:, :], reads=[R_mstg], defer=True)
            load_cast(wtmp, R_wtmp, I["wmv"], 4096)
            for mt in range(2):
                bank = wrot.next()
                for kc in range(KC):
                    P.op("pe", _call("matmul",
                        out=pb[bank][:, 0:512], lhsT=memTb[:, kc * 256 + mt * 128: kc * 256 + (mt + 1) * 128],
                        rhs=wtmp[:, kc * 512:(kc + 1) * 512], start=(kc == 0), stop=(kc == KC - 1)),
                        reads=[R_wtmp, R_memT], writes=[R_pb[bank]])
                mvv = mva[0][:, mt * 516:(mt + 1) * 516].rearrange("p (h d) -> p h d", d=129)
                P.op("act", _call("activation", out=mvv[:, :, 0:128], in_=pb[bank][:, :].rearrange("p (h d) -> p h d", d=128), func=AF.Copy),
                     reads=[R_pb[bank]], writes=[R_mv[0]])
                P.op("dve", _call("tensor_copy", out=mstg[:, mt * 512:(mt + 1) * 512], in_=pb[bank][:, :]),
                     reads=[R_pb[bank]], writes=[R_mstg])
                P.dma("sp", O["mv"][mt * 128:(mt + 1) * 128, :], mstg[:, mt * 512:(mt + 1) * 512], reads=[R_mstg], defer=True)
            for k in range(2):
                P.op("pool", _call("memset", mva[k][:, :].rearrange("p (t d) -> p t d", d=129)[:, :, 128:129], 1.0), writes=[R_mv[k]])
            load_cast(mkT[1], R_mk[1], I["cmkT"], 1024)
            P.dma("sp", wst[0][:, 0:1024].rearrange("p (t c) -> p t c", c=512), I["cmv"].rearrange("(t p) c -> p t c", p=128), writes=[R_wst[0]])
            P.op("act", _call("activation", out=mva[1][:, :].rearrange("p (t d) -> p t d", d=129)[:, :, 0:128],
                                               in_=wst[0][:, 0:1024].rearrange("p (t d) -> p t d", d=128), func=AF.Copy),
                 reads=[R_wst[0]], writes=[R_mv[1]])

            checkpoint('phaseB_pre')
            def transpose_to(src16, R_src, qs, nchunk, dst, R_dst):
                bank = wrot.next()
                pbf = pb[bank][:, :].bitcast(BF16)
                for c in range(nchunk):
                    P.op("pe", _call("transpose", out=pbf[:, c * qs:(c + 1) * qs], in_=src16[0:qs, c * 128:(c + 1) * 128],
                                                                   identity=ident[0:qs, 0:qs]),
                         reads=[R_src, R_ident], writes=[R_pb[bank]])
                P.op("act", _call("activation", out=dst[:, 0:nchunk * qs], in_=pbf[:, 0:nchunk * qs], func=AF.Copy),
                     reads=[R_pb[bank]], writes=[R_dst])

            def layer_norm(hin, R_hin, qs, gcol, hout, R_hout, stat, R_stat):
                for c in range(2):
                    P.op("dve", _call("bn_stats", out=stat[0:qs, c * 6:(c + 1) * 6], in_=hin[0:qs, c * 512:(c + 1) * 512]),
                         reads=[R_hin], writes=[R_stat])
                P.op("dve", _call("bn_aggr", out=stat[0:qs, 12:14], in_=stat[0:qs, 0:12]), reads=[R_stat], writes=[R_stat])
                P.op("dve", _call("tensor_scalar", out=stat[0:qs, 14:15], in0=stat[0:qs, 13:14], scalar1=LN_EPS, scalar2=None, op0=ALU.add),
                     reads=[R_stat], writes=[R_stat])
                P.op("act", _call("activation", out=stat[0:qs, 15:16], in_=stat[0:qs, 14:15], func=AF.Sqrt), reads=[R_stat], writes=[R_stat])
                P.op("dve", _call("reciprocal", out=stat[0:qs, 16:17], in_=stat[0:qs, 15:16]), reads=[R_stat], writes=[R_stat])
                P.op("dve", _call("scalar_tensor_tensor", out=stat[0:qs, 17:18], in0=stat[0:qs, 12:13], scalar=-1.0, in1=stat[0:qs, 16:17],
                                                             op0=ALU.mult, op1=ALU.mult),
                     reads=[R_stat], writes=[R_stat])
                P.op("act", _call("activation", out=hout[0:qs, :], in_=hin[0:qs, :], func=AF.Identity, scale=stat[0:qs, 16:17], bias=stat[0:qs, 17:18]),
                     reads=[R_hin, R_stat], writes=[R_hout])
                P.op("dve", _call("tensor_tensor", out=hout[0:qs, :], in0=hout[0:qs, :], in1=lnt[0:qs, gcol * 1024:(gcol + 1) * 1024], op=ALU.mult),
                     reads=[R_hout, R_ln], writes=[R_hout])
                P.op("dve", _call("tensor_tensor", out=hout[0:qs, :], in0=hout[0:qs, :], in1=lnt[0:qs, (gcol + 1) * 1024:(gcol + 2) * 1024], op=ALU.add),
                     reads=[R_hout, R_ln], writes=[R_hout])

            def phaseB_block(blk, qs, row0, mi, k2):
                s = k2
                tT, hA, hB, h16, qmT, PTm, o16, oT, stat = (tT_l[k2], hA_l[k2], hB_l[k2], h16_l[k2], qmT_l[k2], PTm_l[k2], o16_l[k2],
                                                             oT_l[k2], stat_l[k2])
                R_tT, R_hA, R_hB, R_h16, R_qm, R_PTm, R_o16, R_oT, R_stat = (RB[k2][n] for n in ("tT", "hA", "hB", "h16", "qm", "PTm", "o16", "oT", "stat"))
                P.dma("sp", mixl[s][0:qs, :], mixD[blk * 128 + row0: blk * 128 + row0 + qs, :], reads=[R_mixD[blk]], writes=[R_mixl[s]])
                P.dma("sp", xr[s][0:qs, :], I["xres"][blk * 128: blk * 128 + qs, :], writes=[R_xr[s]])
                transpose_to(mixl[s], R_mixl[s], qs, 8, tT, R_tT)
                yield
                b0, b1 = wrot.next(), wrot.next()
                for n, bank in enumerate((b0, b1)):
                    for kc in range(KC):
                        P.op("pe", _call("matmul",
                            out=pb[bank][0:qs, :], lhsT=tT[:, kc * qs:(kc + 1) * qs], rhs=wob[:, kc * 1024 + n * 512: kc * 1024 + (n + 1) * 512],
                            start=(kc == 0), stop=(kc == KC - 1)),
                            reads=[R_tT, R_wo], writes=[R_pb[bank]])
                    P.op("dve", _call("scalar_tensor_tensor",
                        out=hA[0:qs, n * 512:(n + 1) * 512], in0=xr[s][0:qs, n * 512:(n + 1) * 512], scalar=ALPHA, in1=pb[bank][0:qs, :],
                        op0=ALU.mult, op1=ALU.add),
                        reads=[R_xr[s], R_pb[bank]], writes=[R_hA])
                yield
                layer_norm(hA, R_hA, qs, 0, hB, R_hB, stat, R_stat)
                yield
                P.op("pool", _call("tensor_copy", out=h16[0:qs, :], in_=hB[0:qs, :]), reads=[R_hB], writes=[R_h16])
                transpose_to(h16, R_h16, qs, 8, tT, R_tT)
                yield
                bq = wrot.next()
                for h in range(4):
                    for kc in range(KC):
                        P.op("pe", _call("matmul",
                            out=pb[bq][:, h * qs:(h + 1) * qs], lhsT=wmqb[:, kc * 512 + h * 128: kc * 512 + (h + 1) * 128],
                            rhs=tT[:, kc * qs:(kc + 1) * qs], start=(kc == 0), stop=(kc == KC - 1)),
                            reads=[R_wmq, R_tT], writes=[R_pb[bq]])
                P.op("act", _call("activation", out=qmT[:, 0:4 * qs], in_=pb[bq][:, 0:4 * qs], func=AF.Copy, scale=float(128.0 ** -0.5)),
                     reads=[R_pb[bq]], writes=[R_qm])
                yield
                bs0, bs1 = wrot.next(), wrot.next()
                for h in range(4):
                    for mt in range(2):
                        idx = h * 2 + mt
                        bank = bs0 if idx < 4 else bs1
                        c0 = (idx % 4) * qs
                        P.op("pe", _call("matmul",
                            out=pb[bank][:, c0:c0 + qs], lhsT=mkT[mi][:, h * 256 + mt * 128: h * 256 + (mt + 1) * 128],
                            rhs=qmT[:, h * qs:(h + 1) * qs], start=True, stop=True),
                            reads=[R_mk[mi], R_qm], writes=[R_pb[bank]])
                for k, bank in enumerate((bs0, bs1)):
                    P.op("act", _call("activation", out=PTm[:, k * 4 * qs:(k + 1) * 4 * qs], in_=pb[bank][:, 0:4 * qs], func=AF.Exp),
                         reads=[R_pb[bank]], writes=[R_PTm])
                yield
                bo0, bo1 = wrot.next(), wrot.next()
                for h in range(4):
                    bank = bo0 if h < 2 else bo1
                    for mt in range(2):
                        idx = h * 2 + mt
                        P.op("pe", _call("matmul",
                            out=pb[bank][0:qs, (h % 2) * 129:(h % 2) * 129 + 129], lhsT=PTm[:, idx * qs:(idx + 1) * qs],
                            rhs=mva[mi][:, (mt * 4 + h) * 129:(mt * 4 + h) * 129 + 129],
                            start=(h % 2 == 0 and mt == 0), stop=(mt == 1), skip_group_check=True),
                            reads=[R_PTm, R_mv[mi]], writes=[R_pb[bank]])
                for k, bank in enumerate((bo0, bo1)):
                    ov = pb[bank][0:qs, 0:258].rearrange("p (h d) -> p h d", d=129)
                    P.op("dve", _call("tensor_scalar", out=stat[0:qs, 20 + 2 * k:22 + 2 * k].rearrange("p (h o) -> p h o", o=1),
                                                                      in0=ov[:, :, 128:129], scalar1=1e-30, scalar2=None, op0=ALU.max),
                         reads=[R_pb[bank]], writes=[R_stat])
                    P.op("dve", _call("reciprocal", out=stat[0:qs, 20 + 2 * k:22 + 2 * k], in_=stat[0:qs, 20 + 2 * k:22 + 2 * k]),
                         reads=[R_stat], writes=[R_stat])
                    for hh in range(2):
                        h = k * 2 + hh
                        P.op("dve", _call("tensor_scalar",
                            out=o16[0:qs, h * 128:(h + 1) * 128], in0=pb[bank][0:qs, hh * 129: hh * 129 + 128],
                            scalar1=stat[0:qs, 20 + 2 * k + hh:21 + 2 * k + hh], scalar2=None, op0=ALU.mult),
                            reads=[R_pb[bank], R_stat], writes=[R_o16])
                yield
                transpose_to(o16, R_o16, qs, 4, oT, R_oT)
                yield
                b0, b1 = wrot.next(), wrot.next()
                for n, bank in enumerate((b0, b1)):
                    for c in range(4):
                        P.op("pe", _call("matmul",
                            out=pb[bank][0:qs, :], lhsT=oT[:, c * qs:(c + 1) * qs], rhs=wmob[:, c * 1024 + n * 512: c * 1024 + (n + 1) * 512],
                            start=(c == 0), stop=(c == 3)),
                            reads=[R_oT, R_wmo], writes=[R_pb[bank]])
                    P.op("dve", _call("scalar_tensor_tensor",
                        out=hA[0:qs, n * 512:(n + 1) * 512], in0=hB[0:qs, n * 512:(n + 1) * 512], scalar=ALPHA, in1=pb[bank][0:qs, :],
                        op0=ALU.mult, op1=ALU.add),
                        reads=[R_hB, R_pb[bank]], writes=[R_hA])
                yield
                layer_norm(hA, R_hA, qs, 2, hB, R_hB, stat, R_stat)
                yield
                P.dma("sp", h2D[blk * 128: blk * 128 + qs, :], hB[0:qs, :], reads=[R_hB], writes=[R_h2D[blk]], defer=True)
                P.op("pool", _call("tensor_copy", out=h16[0:qs, :], in_=hB[0:qs, :]), reads=[R_hB], writes=[R_h16])
                transpose_to(h16, R_h16, qs, 8, h2T[s], R_h2T[s])
                P.dma("sp", h2TD[blk][:, 0:8 * qs], h2T[s][:, 0:8 * qs], reads=[R_h2T[s]], writes=[R_h2TD[blk]], defer=True)
                yield

            def run_staggered(gens, lag):
                active = []
                pending = list(gens)
                tick = 0
                while active or pending:
                    if pending and (not active or tick >= lag):
                        active.append(pending.pop(0))
                        tick = 0
                    for g in list(active):
                        try:
                            next(g)
                        except StopIteration:
                            active.remove(g)
                    tick += 1

            blocks = [(16, 2, 126, 0), (17, 16, 0, 1)] + [(i, 128, 0, 0) for i in range(16)]
            run_staggered([phaseB_block(b_, q_, r_, m_, pos % 4) for pos, (b_, q_, r_, m_) in enumerate(blocks)], 3)
            checkpoint('phaseB')
            P.flush(block)

        P.barrier()
        with ExitStack() as sc:
            wdb = sb(sc, "wdb", [128, NFC * 1024], BF16)
            R_wd = Res("wd")
            wst = [sb(sc, "wstc%d" % k, [128, 2048], F32) for k in range(2)]
            R_wst = [Res("wstc%d" % k) for k in range(2)]
            wsl = [sb(sc, "wsl%d" % k, [128, 2048], BF16) for k in range(2)]
            R_wsl = [Res("wsl%d" % k) for k in range(2)]
            R_wslB = [Res("wslB%d" % k) for k in range(2)]
            hT2 = [sb(sc, "hT%d" % k, [128, NFC * 512], BF16) for k in range(2)]
            R_hT2 = [Res("hT%d" % k) for k in range(2)]
            hTm = sb(sc, "hTm", [128, NFC * 16], BF16)
            R_hTm = Res("hTm")
            h2Tg = [sb(sc, "h2Tg%d" % k, [128, 8 * 512], BF16) for k in range(2)]
            R_h2Tg = [Res("h2Tg%d" % k) for k in range(2)]
            h2Tm = sb(sc, "h2Tm", [128, 8 * 18], BF16)
            R_h2Tm = Res("h2Tm")
            Gb = [sb(sc, "Gb%d" % k, [128, 514], F32) for k in range(3)]
            R_Gb = [Res("Gb%d" % k) for k in range(3)]
            Gs = sb(sc, "Gs", [128, 18], F32)
            R_Gs = Res("Gs")
            t0b = [sb(sc, "t0b%d" % k, [128, 512], F32) for k in range(3)]
            R_t0 = [Res("t0%d" % k) for k in range(3)]
            geb = [sb(sc, "geb%d" % k, [128, 512], F32) for k in range(3)]
            R_ge = [Res("ge%d" % k) for k in range(3)]
            t1b = [sb(sc, "t1b%d" % k, [128, 512], F32) for k in range(3)]
            R_t1b = [Res("t1b%d" % k) for k in range(3)]
            t2b = [sb(sc, "t2b%d" % k, [128, 512], F32) for k in range(3)]
            R_t2b = [Res("t2b%d" % k) for k in range(3)]
            t0s = sb(sc, "t0s", [128, 16], F32)
            ges = sb(sc, "ges", [128, 16], F32)
            R_ts = Res("ts")
            carry = sb(sc, "carry", [128, NFC * 2], F32)
            R_carry = [Res("carry%d" % c) for c in range(NFC)]
            sfc = sb(sc, "sfc", [128, NFC * 2], F32)
            R_sfc = Res("sfc")
            sconv = sb(sc, "sconv", [128, NFC * 2], F32)
            wconv = sb(sc, "wconv", [128, NFC * 3], F32)
            bconv = sb(sc, "bconv", [128, NFC], F32)
            flag = sb(sc, "flag", [128, 1], F32)
            R_cc = Res("cc")
            ln3 = sb(sc, "ln3", [128, 2 * 1024], F32)
            R_ln3 = Res("ln3")
            h2r = [sb(sc, "h2r%d" % k, [128, 1024], F32) for k in range(2)]
            R_h2r = [Res("h2r%d" % k) for k in range(2)]
            yA = sb(sc, "yA", [128, 1024], F32)
            R_yA = Res("yA")
            yB = [sb(sc, "yB%d" % k, [128, 1024], F32) for k in range(2)]
            R_yB = [Res("yB%d" % k) for k in range(2)]
            stat = sb(sc, "statc", [128, 32], F32)
            R_stat = Res("statc")

            P.dma("sp", sconv[:, :], I["sconvT"][:, :], writes=[R_cc])
            P.dma("sp", wconv[:, :], I["wconvT"][:, :], writes=[R_cc])
            P.dma("sp", bconv[:, :], I["bconvT"][:, :], writes=[R_cc])
            P.dma("sp", flag[:, :], I["flag"][:, :], writes=[R_cc])
            for k in range(2):
                P.dma("sp", ln3[:, k * 1024:(k + 1) * 1024], I["lnp"][4 + k:5 + k, :].to_broadcast([128, 1024]), writes=[R_ln3])
            k = 0
            for c0 in range(0, NFC * 1024, 2048):
                s = k % 2
                P.dma("sp", wst[s][:, :], I["wdown"][:, c0:c0 + 2048], writes=[R_wst[s]])
                if k % 2 == 0:
                    P.op("act", _call("activation", out=wdb[:, c0:c0 + 2048], in_=wst[s][:, :], func=AF.Copy), reads=[R_wst[s]], writes=[R_wd])
                else:
                    P.op("pool", _call("tensor_copy", out=wdb[:, c0:c0 + 2048], in_=wst[s][:, :]), reads=[R_wst[s]], writes=[R_wd])
                k += 1
            P.dma("sp", h2Tm[:, :].rearrange("p (c q) -> p c q", q=18)[:, :, 0:2], h2TD[16][:, 0:16].rearrange("p (c q) -> p c q", q=2),
                  reads=[R_h2TD[16]], writes=[R_h2Tm], slow=True)
            P.dma("sp", h2Tm[:, :].rearrange("p (c q) -> p c q", q=18)[:, :, 2:18], h2TD[17][:, 0:128].rearrange("p (c q) -> p c q", q=16),
                  reads=[R_h2TD[17]], writes=[R_h2Tm], slow=True)

            checkpoint('phaseC_pre')
            UB = [0, 2, 4]
            GBK = [1, 3, 5]
            MB = 7
            YB = [6, 7]
            wk = [0]

            def ln3_out(pre_banks, qs, h2src, R_h2src, dst_ap, ys, R_ys):
                for n, bank in enumerate(pre_banks):
                    P.op("dve", _call("scalar_tensor_tensor",
                        out=yA[0:qs, n * 512:(n + 1) * 512], in0=h2src[0:qs, n * 512:(n + 1) * 512], scalar=ALPHA, in1=pb[bank][0:qs, :],
                        op0=ALU.mult, op1=ALU.add),
                        reads=[R_h2src, R_pb[bank]], writes=[R_yA])
                for c in range(2):
                    P.op("dve", _call("bn_stats", out=stat[0:qs, c * 6:(c + 1) * 6], in_=yA[0:qs, c * 512:(c + 1) * 512]),
                         reads=[R_yA], writes=[R_stat])
                P.op("dve", _call("bn_aggr", out=stat[0:qs, 12:14], in_=stat[0:qs, 0:12]), reads=[R_stat], writes=[R_stat])
                P.op("dve", _call("tensor_scalar", out=stat[0:qs, 14:15], in0=stat[0:qs, 13:14], scalar1=LN_EPS, scalar2=None, op0=ALU.add),
                     reads=[R_stat], writes=[R_stat])
                P.op("act", _call("activation", out=stat[0:qs, 15:16], in_=stat[0:qs, 14:15], func=AF.Sqrt), reads=[R_stat], writes=[R_stat])
                P.op("dve", _call("reciprocal", out=stat[0:qs, 16:17], in_=stat[0:qs, 15:16]), reads=[R_stat], writes=[R_stat])
                P.op("dve", _call("scalar_tensor_tensor", out=stat[0:qs, 17:18], in0=stat[0:qs, 12:13], scalar=-1.0, in1=stat[0:qs, 16:17],
                                                             op0=ALU.mult, op1=ALU.mult),
                     reads=[R_stat], writes=[R_stat])
                P.op("act", _call("activation", out=ys[0:qs, :], in_=yA[0:qs, :], func=AF.Identity, scale=stat[0:qs, 16:17], bias=stat[0:qs, 17:18]),
                     reads=[R_yA, R_stat], writes=[R_ys])
                P.op("pool", _call("tensor_tensor", out=ys[0:qs, :], in0=ys[0:qs, :], in1=ln3[0:qs, 0:1024], op=ALU.mult),
                     reads=[R_ys, R_ln3], writes=[R_ys])
                P.op("pool", _call("tensor_tensor", out=ys[0:qs, :], in0=ys[0:qs, :], in1=ln3[0:qs, 1024:2048], op=ALU.add),
                     reads=[R_ys, R_ln3], writes=[R_ys])
                P.dma("sp", dst_ap, ys[0:qs, :], reads=[R_ys], defer=True)

            def load_h2Tg(grp):
                gs = grp % 2
                for bi in range(4):
                    blk = grp * 4 + bi
                    P.dma("sp", h2Tg[gs][:, :].rearrange("p (c q) -> p c q", q=512)[:, :, bi * 128:(bi + 1) * 128],
                          h2TD[blk][:, :].rearrange("p (c q) -> p c q", q=128), reads=[R_h2TD[blk]], writes=[R_h2Tg[gs]])

            def c_s1(grp, c):
                s = (grp * NFC + c) % 2
                P.dma("sp", wst[s][:, :], I["wup"][c], writes=[R_wst[s]])
                P.op("pool", _call("tensor_copy", out=wsl[s][:, 0:1152], in_=wst[s][:, 0:1152]), reads=[R_wst[s]], writes=[R_wsl[s]])
                P.op("dve", _call("tensor_copy", out=wsl[s][:, 1152:2048], in_=wst[s][:, 1152:2048]), reads=[R_wst[s]], writes=[R_wslB[s]])

            def c_s2(grp, c):
                s = (grp * NFC + c) % 2
                gs = grp % 2
                mo = (c % 2) * 64
                if grp == 0:
                    for part, oc in ((0, mo), (1, mo + 32)):
                        for kc in range(KC):
                            P.op("pe", _call("matmul", out=pb[MB][:, oc:oc + 18], lhsT=wsl[s][:, kc * 256 + part * 128: kc * 256 + (part + 1) * 128],
                                             rhs=h2Tm[:, kc * 18:(kc + 1) * 18], start=(kc == 0), stop=(kc == KC - 1)),
                                 reads=[R_wsl[s], R_wslB[s], R_h2Tm], writes=[R_pb[MB]])
                k3 = (grp * NFC + c) % 3
                ub, gbk = UB[k3], GBK[k3]
                for part, bank in ((0, ub), (1, gbk)):
                    for kc in range(KC):
                        P.op("pe", _call("matmul", out=pb[bank][:, :], lhsT=wsl[s][:, kc * 256 + part * 128: kc * 256 + (part + 1) * 128],
                                         rhs=h2Tg[gs][:, kc * 512:(kc + 1) * 512], start=(kc == 0), stop=(kc == KC - 1)),
                             reads=[R_wsl[s], R_wslB[s], R_h2Tg[gs]], writes=[R_pb[bank]])

            def c_s3(grp, c):
                hTg, R_hTg = hT2[grp % 2], R_hT2[grp % 2]
                mo = (c % 2) * 64
                if grp == 0:
                    P.op("dve", _call("tensor_scalar", out=carry[:, c * 2:(c + 1) * 2], in0=pb[MB][:, mo + 32:mo + 34], scalar1=flag[:, 0:1],
                                      scalar2=None, op0=ALU.mult),
                         reads=[R_pb[MB], R_cc], writes=[R_carry[c]])
                    P.op("act", _call("activation", out=Gs[:, 0:2], in_=sconv[:, c * 2:(c + 1) * 2], func=AF.Copy), reads=[R_cc], writes=[R_Gs])
                    P.op("act", _call("activation", out=Gs[:, 2:18], in_=pb[MB][:, mo + 34:mo + 50], func=AF.Copy), reads=[R_pb[MB]], writes=[R_Gs])
                    P.op("act", _call("activation", out=t0s[:, :], in_=Gs[:, 2:18], func=AF.Identity, scale=wconv[:, c * 3 + 2:c * 3 + 3],
                                      bias=bconv[:, c:c + 1]),
                         reads=[R_Gs, R_cc], writes=[R_ts])
                    P.op("dve", _call("scalar_tensor_tensor", out=t0s[:, :], in0=Gs[:, 1:17], scalar=wconv[:, c * 3 + 1:c * 3 + 2], in1=t0s[:, :],
                                      op0=ALU.mult, op1=ALU.add),
                         reads=[R_Gs, R_cc, R_ts], writes=[R_ts])
                    P.op("dve", _call("scalar_tensor_tensor", out=t0s[:, :], in0=Gs[:, 0:16], scalar=wconv[:, c * 3:c * 3 + 1], in1=t0s[:, :],
                                      op0=ALU.mult, op1=ALU.add),
                         reads=[R_Gs, R_cc, R_ts], writes=[R_ts])
                    P.op("act", _call("activation", out=ges[:, :], in_=t0s[:, :], func=AF.Gelu_apprx_tanh), reads=[R_ts], writes=[R_ts])
                    P.op("dve", _call("tensor_tensor", out=hTm[:, c * 16:(c + 1) * 16], in0=pb[MB][:, mo + 2:mo + 18], in1=ges[:, :], op=ALU.mult),
                         reads=[R_pb[MB], R_ts], writes=[R_hTm])
                    P.op("act", _call("activation", out=sfc[:, c * 2:(c + 1) * 2], in_=Gs[:, 16:18], func=AF.Copy), reads=[R_Gs], writes=[R_sfc])
                k3 = (grp * NFC + c) % 3
                ub, gbk = UB[k3], GBK[k3]
                G, R_G = Gb[k3], R_Gb[k3]
                t0, R_t = t0b[k3], R_t0[k3]
                ge, R_g = geb[k3], R_ge[k3]
                t1, R_t1 = t1b[k3], R_t1b[k3]
                t2, R_t2 = t2b[k3], R_t2b[k3]
                P.op("act", _call("activation", out=G[:, 0:2], in_=carry[:, c * 2:(c + 1) * 2], func=AF.Copy),
                     reads=[R_carry[c]], writes=[R_G])
                P.op("act", _call("activation", out=G[:, 2:514], in_=pb[gbk][:, :], func=AF.Copy), reads=[R_pb[gbk]], writes=[R_G])
                P.op("act", _call("activation", out=carry[:, c * 2:(c + 1) * 2], in_=G[:, 512:514], func=AF.Copy),
                     reads=[R_G], writes=[R_carry[c]])
                P.op("act", _call("activation", out=t0[:, :], in_=G[:, 2:514], func=AF.Identity,
                                  scale=wconv[:, c * 3 + 2:c * 3 + 3], bias=bconv[:, c:c + 1]),
                     reads=[R_G, R_cc], writes=[R_t])
                P.op("act", _call("activation", out=t1[:, :], in_=G[:, 1:513], func=AF.Identity, scale=wconv[:, c * 3 + 1:c * 3 + 2]),
                     reads=[R_G, R_cc], writes=[R_t1])
                P.op("act", _call("activation", out=t2[:, :], in_=G[:, 0:512], func=AF.Identity, scale=wconv[:, c * 3:c * 3 + 1]),
                     reads=[R_G, R_cc], writes=[R_t2])
                P.op("dve", _call("tensor_tensor", out=t0[:, :], in0=t0[:, :], in1=t1[:, :], op=ALU.add), reads=[R_t, R_t1], writes=[R_t])
                P.op("dve", _call("tensor_tensor", out=t0[:, :], in0=t0[:, :], in1=t2[:, :], op=ALU.add), reads=[R_t, R_t2], writes=[R_t])
                P.op("act", _call("activation", out=ge[:, :], in_=t0[:, :], func=AF.Gelu_apprx_tanh), reads=[R_t], writes=[R_g])
                P.op("dve", _call("tensor_tensor", out=hTg[:, c * 512:(c + 1) * 512], in0=pb[ub][:, :], in1=ge[:, :], op=ALU.mult),
                     reads=[R_pb[ub], R_g], writes=[R_hTg])

            def c_down(grp):
                hTg, R_hTg = hT2[grp % 2], R_hT2[grp % 2]
                if grp == 0:
                    for n, bank in enumerate(YB):
                        for c in range(NFC):
                            P.op("pe", _call("matmul", out=pb[bank][0:16, :], lhsT=hTm[:, c * 16:(c + 1) * 16],
                                             rhs=wdb[:, c * 1024 + n * 512: c * 1024 + (n + 1) * 512], start=(c == 0), stop=(c == NFC - 1)),
                                 reads=[R_hTm, R_wd], writes=[R_pb[bank]])
                    P.dma("sp", h2r[0][0:16, :], h2D[17 * 128: 17 * 128 + 16, :], reads=[R_h2D[17]], writes=[R_h2r[0]])
                    ln3_out(YB, 16, h2r[0], R_h2r[0], O["ys"][:, :], yB[0], R_yB[0])
                    P.dma("sp", O["sfcT"][:, :], sfc[:, :], reads=[R_sfc], defer=True)
                for bi in range(4):
                    blk = grp * 4 + bi
                    hs = blk % 2
                    P.dma("sp", h2r[hs][:, :], h2D[blk * 128:(blk + 1) * 128, :], reads=[R_h2D[blk]], writes=[R_h2r[hs]])
                    for n, bank in enumerate(YB):
                        for c in range(NFC):
                            P.op("pe", _call("matmul", out=pb[bank][:, :], lhsT=hTg[:, c * 512 + bi * 128: c * 512 + (bi + 1) * 128],
                                             rhs=wdb[:, c * 1024 + n * 512: c * 1024 + (n + 1) * 512], start=(c == 0), stop=(c == NFC - 1)),
                                 reads=[R_hTg, R_wd], writes=[R_pb[bank]])
                    ln3_out(YB, 128, h2r[hs], R_h2r[hs], O["y"][blk * 128:(blk + 1) * 128, :], yB[hs], R_yB[hs])

            seq = [(grp, c) for grp in range(4) for c in range(NFC)]
            nseq = len(seq)
            load_h2Tg(0)
            load_h2Tg(1)
            for idx in range(nseq + 2):
                if idx < nseq:
                    c_s1(*seq[idx])
                if 1 <= idx <= nseq:
                    c_s2(*seq[idx - 1])
                if idx >= 2:
                    g3, c3 = seq[idx - 2]
                    c_s3(g3, c3)
                    if c3 == NFC - 1:
                        c_down(g3)
                        if g3 + 2 < 4:
                            load_h2Tg(g3 + 2)
            P.dma("sp", O["fcT"][:, :], carry[:, :], reads=R_carry, defer=True)
            P.finish()
            P.flush(block)
    return nc


def _t5_bucket(rel):
    half, max_exact = 16, 8
    n = np.abs(rel)
    log_ratio = np.log(np.maximum(n, 1).astype(np.float32) / max_exact) / math.log(128 / max_exact)
    large = np.minimum(max_exact + (log_ratio * (half - max_exact)).astype(np.int32), half - 1)
    return np.where(rel < 0, half, 0) + np.where(n < max_exact, n, large)


def _host_inputs(inp):
    f32 = np.float32
    x_prompt = np.asarray(inp["x_prompt"], f32)
    x_sample = np.asarray(inp["x_sample"], f32)
    w_in = np.asarray(inp["w_in"], f32)[0]
    qa, ka, va = w_in[:, 0:512], w_in[:, 512:1024], w_in[:, 1024:1536]
    qb, kb, vb = w_in[:, 1536:2048], w_in[:, 2048:2176], w_in[:, 2176:2304]
    qi, ki, wi = w_in[:, 2304:2816], w_in[:, 2816:2880], w_in[:, 2880:2888]
    qbp = np.concatenate([np.concatenate([qb[:, r * 64:(r + 1) * 64], qb[:, (4 + r) * 64:(5 + r) * 64]], axis=1) for r in range(4)], axis=1)
    winp = np.concatenate([qa, ka, qbp, kb, qi, ki, ki, va, vb, wi], axis=1)
    assert winp.shape[1] == NCOL

    def kc_layout(w):
        n = w.shape[1]
        return np.ascontiguousarray(w.reshape(8, 128, n).transpose(1, 0, 2).reshape(128, 8 * n))

    shared = {}
    shared["win"] = kc_layout(winp)
    shared["wo"] = kc_layout(np.asarray(inp["w_o"], f32)[0])
    shared["wmq"] = kc_layout(np.asarray(inp["w_mq"], f32)[0])
    shared["wmk"] = kc_layout(np.asarray(inp["w_mk"], f32)[0])
    shared["wmv"] = kc_layout(np.asarray(inp["w_mv"], f32)[0])
    wmo = np.asarray(inp["w_mo"], f32)[0]
    shared["wmo"] = np.ascontiguousarray(wmo.reshape(4, 128, 1024).transpose(1, 0, 2).reshape(128, 4096))
    w_up = np.asarray(inp["w_up"], f32)[0]
    wu = w_up[:, :DFF].reshape(8, 128, NFC, 128)
    wg = w_up[:, DFF:].reshape(8, 128, NFC, 128)
    wup = np.stack([wu, wg], axis=3)
    shared["wup"] = np.ascontiguousarray(wup.transpose(2, 1, 0, 3, 4).reshape(NFC, 128, 8 * 256))
    w_down = np.asarray(inp["w_down"], f32)[0]
    shared["wdown"] = np.ascontiguousarray(w_down.reshape(NFC, 128, 1024).transpose(1, 0, 2).reshape(128, NFC * 1024))
    shared["lnp"] = np.ascontiguousarray(np.stack([np.asarray(inp[k], f32)[0] for k in ("ln1_g", "ln1_b", "ln2_g", "ln2_b", "ln3_g", "ln3_b")]))
    w_conv = np.asarray(inp["w_conv"], f32)[0]
    shared["wconvT"] = np.ascontiguousarray(w_conv.reshape(3, NFC, 128).transpose(2, 1, 0).reshape(128, NFC * 3))
    shared["bconvT"] = np.ascontiguousarray(np.asarray(inp["b_conv"], f32)[0].reshape(NFC, 128).T)
    shared["ident"] = np.eye(128, dtype=f32)
    tabA = np.asarray(inp["a_rel_bias"], f32)[0]
    qq = np.arange(128)[:, None]
    kk = np.arange(640)[None, :]
    kpos = kk - 512
    rel = qq - kpos
    cq = qq // 64
    kch = np.floor_divide(kpos, 64)
    allowed = (kch >= cq - 8) & (kch <= cq)
    bias = tabA[np.clip(rel, -64, 64) + 64]
    AB = np.where(allowed[:, :, None], bias, f32(NEGM)).astype(f32)
    shared["AB"] = np.ascontiguousarray(AB.transpose(0, 2, 1).reshape(128, 8 * ABW))
    js = np.arange(16)[:, None]
    ks = np.arange(528)[None, :]
    ABs = tabA[np.clip(512 + js - ks, -64, 64) + 64]
    shared["ABs"] = np.ascontiguousarray(ABs.transpose(0, 2, 1).reshape(16, 8 * 528)).astype(f32)
    t5 = np.asarray(inp["t5_bias"], f32)
    relB = np.arange(128)[:, None] - np.arange(256)[None, :] + 128
    Bn = t5[_t5_bucket(relB)]
    shared["Bn"] = np.ascontiguousarray(Bn.transpose(0, 2, 1).reshape(128, 8 * BNW)).astype(f32)
    relBs = 128 + np.arange(16)[:, None] - np.arange(144)[None, :]
    Bns = t5[_t5_bucket(relBs)]
    shared["Bns"] = np.ascontiguousarray(Bns.transpose(0, 2, 1).reshape(16, 8 * 144)).astype(f32)
    shared["C15"] = np.ascontiguousarray(np.broadcast_to(t5[15][None, :], (128, 8))).astype(f32)
    dm = np.zeros((128, 128), f32)
    dm[0:64, 64:128] = NEGM
    shared["diagmask"] = dm

    mem_prompt = np.asarray(inp["mem_prompt"], f32)
    maps = []
    for c in range(8):
        b, half = c // 2, c % 2
        m = dict(shared)
        xk = np.zeros((4096, 1024), f32)
        if half == 1:
            xk[:] = x_prompt[b]
        else:
            xk[2048:] = x_prompt[b, :2048]
        m["xkT"] = np.ascontiguousarray(xk.reshape(32, 128, 8, 128).transpose(0, 3, 2, 1).reshape(32, 128, 1024))
        xs = x_sample[c]
        m["xsT"] = np.ascontiguousarray(xs.reshape(16, 8, 128).transpose(2, 1, 0).reshape(128, 128))
        xres = np.zeros((NBLK * 128, 1024), f32)
        xres[0:2048] = xk[2048:]
        xres[2048:2050] = xk[2046:2048]
        xres[17 * 128:17 * 128 + 16] = xs
        m["xres"] = xres
        m["memT"] = np.ascontiguousarray(mem_prompt[b].reshape(256, 8, 128).transpose(2, 1, 0).reshape(128, 2048))
        cmk = np.asarray(inp["cache_mem_k"], f32)[0, c]
        m["cmkT"] = np.ascontiguousarray(cmk.transpose(2, 1, 0).reshape(128, 1024))
        m["cmv"] = np.ascontiguousarray(np.asarray(inp["cache_mem_v"], f32)[0, c].reshape(256, 512))
        cak = np.asarray(inp["cache_a_k"], f32)[0, c]
        m["cakT"] = np.ascontiguousarray(cak.reshape(512, 4, 2, 64).transpose(2, 3, 1, 0).reshape(128, 2048))
        m["cav"] = np.ascontiguousarray(np.asarray(inp["cache_a_v"], f32)[0, c].reshape(512, 512))
        cbk = np.asarray(inp["cache_b_k"], f32)[0, c]
        m["cbkT"] = np.ascontiguousarray(cbk.reshape(2048, 128).T)
        m["cbv"] = np.ascontiguousarray(np.asarray(inp["cache_b_v"], f32)[0, c].reshape(2048, 128))
        cbi = np.asarray(inp["cache_b_kidx"], f32)[0, c]
        m["cbiT"] = np.ascontiguousarray(np.concatenate([cbi.T, cbi.T], axis=0))
        sc_ = np.asarray(inp["state_ffn_conv"], f32)[0, c]
        m["sconvT"] = np.ascontiguousarray(sc_.reshape(2, NFC, 128).transpose(2, 1, 0).reshape(128, NFC * 2))
        m["colmask"] = np.full((128, 1), NEGM if half == 0 else 0.0, f32)
        kv = np.ones((128, NT), f32)
        if half == 0:
            kv[:, 0:16] = 0.0
        m["kvalid"] = kv
        m["flag"] = np.full((128, 1), float(half), f32)
        maps.append(m)
    return maps


_NC_CACHE = {}


def _run(inputs, debug=False):
    key = bool(debug)
    if key not in _NC_CACHE:
        _NC_CACHE[key] = build_program(debug=debug)
    nc = _NC_CACHE[key]
    maps = _host_inputs(inputs)
    res = run_bass_kernel_spmd(nc, maps, core_ids=list(range(8)))
    return res.results


def kernel(**inputs):
    R = _run(inputs)
    f32 = np.float32
    y = np.zeros((4, 4096, 1024), f32)
    ys = np.zeros((8, 16, 1024), f32)
    pak = np.zeros((1, 4, 512, 8, 64), f32)
    pav = np.zeros((1, 4, 512, 8, 64), f32)
    pbk = np.zeros((1, 4, 4096, 2, 64), f32)
    pbv = np.zeros((1, 4, 4096, 2, 64), f32)
    pbi = np.zeros((1, 4, 4096, 64), f32)
    pmk = np.zeros((1, 4, 256, 4, 128), f32)
    pmv = np.zeros((1, 4, 256, 4, 128), f32)
    pfc = np.zeros((1, 4, 2, DFF), f32)
    sak = np.zeros((1, 8, 16, 8, 64), f32)
    sav = np.zeros((1, 8, 16, 8, 64), f32)
    sbk = np.zeros((1, 8, 16, 2, 64), f32)
    sbv = np.zeros((1, 8, 16, 2, 64), f32)
    sbi = np.zeros((1, 8, 16, 64), f32)
    sfc = np.zeros((1, 8, 2, DFF), f32)
    for c in range(8):
        b, half = c // 2, c % 2
        r = R[c]
        y[b, half * 2048:(half + 1) * 2048] = np.asarray(r["y"], f32)
        ys[c] = np.asarray(r["ys"], f32)
        if half == 1:
            akT = np.asarray(r["akT"], f32).reshape(2, 64, 4, 512)
            pak[0, b] = akT.transpose(3, 2, 0, 1).reshape(512, 8, 64)
            pav[0, b] = np.asarray(r["av"], f32).reshape(512, 8, 64)
            pbk[0, b] = np.asarray(r["bkT"], f32).T.reshape(4096, 2, 64)
            pbv[0, b] = np.asarray(r["bv"], f32).reshape(4096, 2, 64)
            pbi[0, b] = np.asarray(r["biT"], f32).T
            pmk[0, b] = np.asarray(r["mkT"], f32).reshape(128, 4, 256).transpose(2, 1, 0)
            pmv[0, b] = np.asarray(r["mv"], f32).reshape(256, 4, 128)
            pfc[0, b] = np.asarray(r["fcT"], f32).reshape(128, NFC, 2).transpose(2, 1, 0).reshape(2, DFF)
        sakT = np.asarray(r["sakT"], f32).reshape(2, 64, 4, 16)
        sak[0, c] = sakT.transpose(3, 2, 0, 1).reshape(16, 8, 64)
        sav[0, c] = np.asarray(r["sav"], f32).reshape(16, 8, 64)
        sbk[0, c] = np.asarray(r["sbkT"], f32).T.reshape(16, 2, 64)
        sbv[0, c] = np.asarray(r["sbv"], f32).reshape(16, 2, 64)
        sbi[0, c] = np.asarray(r["sbiT"], f32).T
        sfc[0, c] = np.asarray(r["sfcT"], f32).reshape(128, NFC, 2).transpose(2, 1, 0).reshape(2, DFF)
    return (y, ys, pak, pav, pbk, pbv, pbi, pmk, pmv, pfc, sak, sav, sbk, sbv, sbi, sfc)
```

```python
import math
from contextlib import ExitStack

import numpy as np
import concourse.bass as bass
import concourse.mybir as mybir
from concourse.bass_utils import run_bass_kernel_spmd

F32 = mybir.dt.float32
BF16 = mybir.dt.bfloat16
AF = mybir.ActivationFunctionType
ALU = mybir.AluOpType

D = 1024
KC = 8
NT = 32
NCOL = 2952
C_QA, C_KA, C_QB, C_KB, C_QI, C_KI, C_VA, C_VB, C_WI = 0, 512, 1024, 1536, 1664, 2176, 2304, 2816, 2944
DFF = 2816
NFC = 22
ALPHA = 2.0 ** 0.25
LN_EPS = 1e-5
NEGM = -30000.0
NIT = 17
BIS_W0 = 16.0
ABW = 640
BNW = 256
NBLK = 18


class Res:
    __slots__ = ("lw", "rd", "name", "excl")

    def __init__(self, name="", excl=False):
        self.lw = None
        self.rd = {}
        self.name = name
        self.excl = excl


def _call(name, *args, **kw):
    return lambda e: getattr(e, name)(*args, **kw)


class Prog:
    ENG = ("pe", "act", "dve", "pool", "sp")

    def __init__(self, nc, sems, dma_sems):
        self.nc = nc
        self.streams = {e: [] for e in self.ENG}
        self.sem = sems
        self.cnt = {e: 0 for e in self.ENG}
        self.seen = {e: {} for e in self.ENG}
        self.dsems = dma_sems
        self.dval = [0] * len(dma_sems)
        self.dnext = 0
        self.semh = dict(sems)
        for i, h in enumerate(dma_sems):
            self.semh[("d", i)] = h
        self.ninst = 0
        self.dead = False
        self.deferred = []
        self.defer_lag = 48

    def _deps(self, reads, writes, eng=None):
        d = {}
        for r in reads:
            if r.lw is not None:
                k, v = r.lw
                if d.get(k, 0) < v:
                    d[k] = v
            if r.excl:
                for k, v in r.rd.items():
                    if k != eng and d.get(k, 0) < v:
                        d[k] = v
        for w in writes:
            if w.lw is not None:
                k, v = w.lw
                if d.get(k, 0) < v:
                    d[k] = v
            for k, v in w.rd.items():
                if d.get(k, 0) < v:
                    d[k] = v
        return d

    def _wait(self, eng, deps):
        for k, v in deps.items():
            if k == "pe" and eng == "pe":
                continue
            if self.seen[eng].get(k, 0) >= v:
                continue
            self.seen[eng][k] = v
            h = self.semh[k]
            self.streams[eng].append(lambda e, h=h, v=v: e.wait_ge(h, v))

    def _flush_deferred(self, force=False, reads=(), writes=()):
        if not self.deferred:
            return
        conflict = force
        if not conflict:
            ws = set(id(w) for w in writes)
            rs = set(id(r) for r in reads)
            for d in self.deferred:
                dr = set(id(x) for x in d[3])
                dw = set(id(x) for x in d[4])
                if (ws & dr) or (ws & dw) or (rs & dw):
                    conflict = True
                    break
        if conflict:
            pend, self.deferred = self.deferred, []
            for d in pend:
                self._dma_now(d[0], d[1], d[2], d[3], d[4], d[5])
            return
        while self.deferred and self.ninst - self.deferred[0][6] >= self.defer_lag:
            d = self.deferred.pop(0)
            self._dma_now(d[0], d[1], d[2], d[3], d[4], d[5])

    def op(self, eng, fn, reads=(), writes=()):
        if self.dead:
            return
        self._flush_deferred(False, reads, writes)
        self._wait(eng, self._deps(reads, writes, eng))
        self.cnt[eng] += 1
        n = self.cnt[eng]
        h = self.sem[eng]
        self.streams[eng].append(lambda e, fn=fn, h=h: fn(e).then_inc(h, 1))
        self.ninst += 1
        for r in reads:
            if r.rd.get(eng, 0) < n:
                r.rd[eng] = n
        for w in writes:
            w.lw = (eng, n)
            w.rd = {}

    def dma(self, q, out, in_, reads=(), writes=(), slow=False, defer=False):
        if self.dead:
            return
        if defer:
            self._flush_deferred(False, reads, writes)
            self.deferred.append((q, out, in_, list(reads), list(writes), slow, self.ninst))
            return
        self._flush_deferred(False, reads, writes)
        self._dma_now(q, out, in_, reads, writes, slow)

    def _dma_now(self, q, out, in_, reads=(), writes=(), slow=False):
        deps = self._deps(reads, writes)
        i = self.dnext
        self.dnext = (i + 1) % len(self.dsems)
        k = ("d", i)
        if self.dval[i] > 0 and deps.get(k, 0) < self.dval[i]:
            deps[k] = self.dval[i]
        self._wait(q, deps)
        self.dval[i] += 16
        v = self.dval[i]
        h = self.dsems[i]
        if slow:
            self.streams[q].append(
                lambda e, out=out, in_=in_, h=h: e.dma_start(out=out, in_=in_, allow_slow_non_contiguous=True).then_inc(h, 16))
        else:
            self.streams[q].append(lambda e, out=out, in_=in_, h=h: e.dma_start(out=out, in_=in_).then_inc(h, 16))
        self.ninst += 1
        for r in reads:
            if r.rd.get(k, 0) < v:
                r.rd[k] = v
        for w in writes:
            w.lw = (k, v)
            w.rd = {}

    def barrier(self):
        if self.dead:
            return
        self._flush_deferred(True)
        deps = {e: self.cnt[e] for e in self.ENG if self.cnt[e] > 0}
        for i, v in enumerate(self.dval):
            if v > 0:
                deps[("d", i)] = v
        for e in self.ENG:
            self._wait(e, dict(deps))

    def finish(self):
        self._flush_deferred(True)
        deps = {("d", i): v for i, v in enumerate(self.dval) if v > 0}
        self._wait("sp", deps)

    def flush(self, block):
        self._flush_deferred(True)
        s = self.streams
        self.streams = {e: [] for e in self.ENG}

        def mk(lst):
            def body(e):
                for f in lst:
                    f(e)
            return body

        block.tensor(mk(s["pe"]))
        block.scalar(mk(s["act"]))
        block.vector(mk(s["dve"]))
        block.gpsimd(mk(s["pool"]))
        block.sync(mk(s["sp"]))


def build_program(debug=False, stop_at=None):
    nc = bass.Bass("TRN2", target_bir_lowering=False)

    def din(name, shape, dt=F32):
        return nc.dram_tensor(name, list(shape), dt, kind="ExternalInput").ap()

    def dout(name, shape, dt=F32):
        return nc.dram_tensor(name, list(shape), dt, kind="ExternalOutput").ap()

    def dscr(name, shape, dt):
        return nc.dram_tensor(name, list(shape), dt, kind="Internal").ap()

    I = {}
    I["xkT"] = din("xkT", [NT, 128, 1024])
    I["xsT"] = din("xsT", [128, 8 * 16])
    I["xres"] = din("xres", [NBLK * 128, 1024])
    I["win"] = din("win", [128, KC * NCOL])
    I["wo"] = din("wo", [128, 8 * 1024])
    I["wmq"] = din("wmq", [128, 8 * 512])
    I["wmk"] = din("wmk", [128, 8 * 512])
    I["wmv"] = din("wmv", [128, 8 * 512])
    I["wmo"] = din("wmo", [128, 4 * 1024])
    I["wup"] = din("wup", [NFC, 128, 8 * 256])
    I["wdown"] = din("wdown", [128, NFC * 1024])
    I["lnp"] = din("lnp", [6, 1024])
    I["wconvT"] = din("wconvT", [128, NFC * 3])
    I["bconvT"] = din("bconvT", [128, NFC])
    I["memT"] = din("memT", [128, 8 * 256])
    I["cmkT"] = din("cmkT", [128, 4 * 256])
    I["cmv"] = din("cmv", [256, 512])
    I["cakT"] = din("cakT", [128, 4 * 512])
    I["cav"] = din("cav", [512, 512])
    I["cbkT"] = din("cbkT", [128, 2048])
    I["cbv"] = din("cbv", [2048, 128])
    I["cbiT"] = din("cbiT", [128, 2048])
    I["sconvT"] = din("sconvT", [128, NFC * 2])
    I["ident"] = din("ident", [128, 128])
    I["AB"] = din("AB", [128, 8 * ABW])
    I["ABs"] = din("ABs", [16, 8 * 528])
    I["Bn"] = din("Bn", [128, 8 * BNW])
    I["Bns"] = din("Bns", [16, 8 * 144])
    I["C15"] = din("C15", [128, 8])
    I["colmask"] = din("colmask", [128, 1])
    I["diagmask"] = din("diagmask", [128, 128])
    I["kvalid"] = din("kvalid", [128, NT])
    I["flag"] = din("flag", [128, 1])

    O = {}
    O["y"] = dout("y", [2048, 1024])
    O["ys"] = dout("ys", [16, 1024])
    O["akT"] = dout("akT", [128, 4 * 512])
    O["av"] = dout("av", [512, 512])
    O["bkT"] = dout("bkT", [128, 4096])
    O["bv"] = dout("bv", [4096, 128])
    O["biT"] = dout("biT", [64, 4096])
    O["mkT"] = dout("mkT", [128, 4 * 256])
    O["mv"] = dout("mv", [256, 512])
    O["fcT"] = dout("fcT", [128, NFC * 2])
    O["sakT"] = dout("sakT", [128, 4 * 16])
    O["sav"] = dout("sav", [16, 512])
    O["sbkT"] = dout("sbkT", [128, 16])
    O["sbv"] = dout("sbv", [16, 128])
    O["sbiT"] = dout("sbiT", [64, 16])
    O["sfcT"] = dout("sfcT", [128, NFC * 2])
    if debug:
        O["dbg_mix"] = dout("dbg_mix", [NBLK * 128, 1024], BF16)
        O["dbg_h2"] = dout("dbg_h2", [NBLK * 128, 1024])
        mixD = O["dbg_mix"]
        h2D = O["dbg_h2"]
    else:
        mixD = dscr("mixD", [NBLK * 128, 1024], BF16)
        h2D = dscr("h2D", [NBLK * 128, 1024], F32)
    h2TD = dscr("h2TD", [NBLK, 128, 1024], BF16)
    R_mixD = [Res("mixD%d" % i) for i in range(NBLK)]
    R_h2D = [Res("h2D%d" % i) for i in range(NBLK)]
    R_h2TD = [Res("h2TD%d" % i) for i in range(NBLK)]

    es = ExitStack()
    with es:
        sems = {e: es.enter_context(nc.semaphore("s_" + e)) for e in Prog.ENG}
        dsems = [es.enter_context(nc.semaphore("d%d" % i)) for i in range(32)]
        P = Prog(nc, sems, dsems)
        block = es.enter_context(nc.Block())

        def checkpoint(name):
            if stop_at is not None and name == stop_at and not P.dead:
                P.finish()
                P.flush(block)
                P.dead = True

        pb = [es.enter_context(nc.psum_tensor("pb%d" % i, [128, 512], F32)) for i in range(8)]
        R_pb = [Res("pb%d" % i, excl=True) for i in range(8)]

        class Rot:
            def __init__(self, idxs):
                self.idxs = idxs
                self.i = 0

            def next(self):
                k = self.idxs[self.i % len(self.idxs)]
                self.i += 1
                return k

        def sb(stack, name, shape, dt):
            return stack.enter_context(nc.sbuf_tensor("sb_" + name, list(shape), dt))

        ident_f = sb(es, "ident_f", [128, 128], F32)
        ident = sb(es, "ident", [128, 512], BF16)
        R_ident = Res("ident")
        P.dma("sp", ident_f[:, :], I["ident"][:, :], writes=[R_ident])
        for r in range(4):
            P.op("act", _call("activation", out=ident[:, r * 128:(r + 1) * 128], in_=ident_f[:, :], func=AF.Copy),
                 reads=[R_ident], writes=[R_ident])

        def run_interleaved(gens):
            gens = [[0.0, i, g] for i, g in enumerate(gens)]
            while gens:
                gens.sort(key=lambda x: (x[0], x[1]))
                ent = gens[0]
                try:
                    c = next(ent[2])
                    ent[0] += (c if c else 1.0)
                except StopIteration:
                    gens.remove(ent)

        with ExitStack() as sa:
            winb = sb(sa, "winb", [128, KC * NCOL], BF16)
            R_win = Res("win")
            kbi = sb(sa, "kbi", [128, 2 * 4096], BF16)
            R_kbi = [Res("kbi%d" % r) for r in range(NT)]
            vb_aug = sb(sa, "vb_aug", [128, NT * 2 * 65], BF16)
            R_vb = [Res("vb%d" % r) for r in range(NT)]
            kaT = sb(sa, "kaT", [128, 6 * 512], BF16)
            R_ka = [Res("ka%d" % s) for s in range(6)]
            va_aug = sb(sa, "va_aug", [128, 6 * 8 * 65], BF16)
            R_va = [Res("va%d" % s) for s in range(6)]
            ABb = sb(sa, "ABb", [128, 8 * ABW], BF16)
            R_AB = Res("AB")
            Bnb = sb(sa, "Bnb", [128, 8 * BNW], BF16)
            R_Bn = Res("Bn")
            Mnear = [sb(sa, "Mnear%d" % k, [128, 8 * BNW], BF16) for k in range(2)]
            R_Mnear = [Res("Mnear%d" % k) for k in range(2)]
            score = [sb(sa, "score%d" % k, [128, 4096], F32) for k in range(2)]
            R_score = [Res("score%d" % k) for k in range(2)]
            Mb = [sb(sa, "Mb%d" % k, [128, 4096], BF16) for k in range(2)]
            R_M = [Res("M%d" % k) for k in range(2)]
            relu = [sb(sa, "relu%d" % k, [128, 512], BF16) for k in range(3)]
            R_relu = [Res("relu%d" % k) for k in range(3)]
            xstg2 = [sb(sa, "xstg%d" % k, [128, 1024], F32) for k in range(2)]
            R_xstg2 = [Res("xstg%d" % k) for k in range(2)]
            xstg, R_xstg = xstg2[0], R_xstg2[0]
            xTb = [sb(sa, "xTb%d" % k, [128, 1024], BF16) for k in range(2)]
            R_xT = [Res("xT%d" % k) for k in range(2)]
            qaz = [sb(sa, "qaz%d" % k, [128, 1024], BF16) for k in range(2)]
            qbz = [sb(sa, "qbz%d" % k, [128, 1024], BF16) for k in range(2)]
            qiz = [sb(sa, "qiz%d" % k, [128, 1024], BF16) for k in range(2)]
            R_qa = [Res("qa%d" % k) for k in range(2)]
            R_qb = [Res("qb%d" % k) for k in range(2)]
            R_qi = [Res("qi%d" % k) for k in range(2)]
            coef = [sb(sa, "coef%d" % k, [128, 8], F32) for k in range(2)]
            R_coef = [Res("coef%d" % k) for k in range(2)]
            dg = [sb(sa, "dg%d" % k, [128, 1024], BF16) for k in range(2)]
            R_dg = [Res("dg%d" % k) for k in range(2)]
            PTA = [sb(sa, "PTA%d" % k, [128, 512], BF16) for k in range(3)]
            R_PTA = [Res("PTA%d" % k) for k in range(3)]
            PTB = [sb(sa, "PTB%d" % k, [128, 512], BF16) for k in range(3)]
            R_PTB = [Res("PTB%d" % k) for k in range(3)]
            mixb = [sb(sa, "mixb%d" % k, [128, 1024], BF16) for k in range(2)]
            R_mix = [Res("mix%d" % k) for k in range(2)]
            ostg = [sb(sa, "ostg%d" % k, [128, 256], F32) for k in range(2)]
            R_ostg = [Res("ostg%d" % k) for k in range(2)]
            vbstg = [sb(sa, "vbstg%d" % k, [128, 128], F32) for k in range(2)]
            R_vbstg = [Res("vbstg%d" % k) for k in range(2)]
            astg = sb(sa, "astg", [128, 1024], F32)
            R_astg = Res("astg")
            small = [sb(sa, "small%d" % k, [128, 16], F32) for k in range(2)]
            R_small = [Res("small%d" % k) for k in range(2)]
            recA = [sb(sa, "recA%d" % k, [128, 8], F32) for k in range(2)]
            R_recA = [Res("recA%d" % k) for k in range(2)]
            recB = [sb(sa, "recB%d" % k, [128, 8], F32) for k in range(2)]
            R_recB = [Res("recB%d" % k) for k in range(2)]
            colmask = sb(sa, "colmask", [128, 1], F32)
            diagm = sb(sa, "diagm", [128, 128], F32)
            kvalid = sb(sa, "kvalid", [128, NT], F32)
            c15 = sb(sa, "c15", [128, 8], F32)
            ones8 = sb(sa, "ones8", [128, 8], F32)
            R_cst = Res("cst")

            wrot = Rot([0, 1, 2])

            P.dma("sp", colmask[:, :], I["colmask"][:, :], writes=[R_cst])
            P.dma("sp", diagm[:, :], I["diagmask"][:, :], writes=[R_cst])
            P.dma("sp", kvalid[:, :], I["kvalid"][:, :], writes=[R_cst])
            P.dma("sp", c15[:, :], I["C15"][:, :], writes=[R_cst])
            P.op("pool", _call("memset", ones8[:, :], 1.0), writes=[R_cst])
            for k in range(2):
                P.op("pool", _call("memset", qaz[k][:, :], 0.0), writes=[R_qa[k]])
                P.op("pool", _call("memset", qbz[k][:, :], 0.0), writes=[R_qb[k]])
                P.op("pool", _call("memset", qiz[k][:, :], 0.0), writes=[R_qi[k]])

            HW = NCOL // 2
            for kc in range(KC):
                for hh in range(2):
                    stg, R_stg = score[hh], R_score[hh]
                    P.dma("sp", stg[:, 0:HW], I["win"][:, kc * NCOL + hh * HW: kc * NCOL + (hh + 1) * HW], writes=[R_stg])
                    if hh == 0:
                        P.op("act", _call("activation", out=winb[:, kc * NCOL + hh * HW: kc * NCOL + (hh + 1) * HW], in_=stg[:, 0:HW], func=AF.Copy),
                             reads=[R_stg], writes=[R_win])
                    else:
                        P.op("pool", _call("tensor_copy", out=winb[:, kc * NCOL + hh * HW: kc * NCOL + (hh + 1) * HW], in_=stg[:, 0:HW]),
                             reads=[R_stg], writes=[R_win])
            for hh in range(2):
                w = 4 * ABW
                P.dma("sp", score[hh][:, 0:w], I["AB"][:, hh * w:(hh + 1) * w], writes=[R_score[hh]])
                P.op("act", _call("activation", out=ABb[:, hh * w:(hh + 1) * w], in_=score[hh][:, 0:w], func=AF.Copy),
                     reads=[R_score[hh]], writes=[R_AB])
            P.dma("sp", score[0][:, 0:8 * BNW], I["Bn"][:, :], writes=[R_score[0]])
            for h in range(8):
                P.op("dve", _call("tensor_scalar", out=Bnb[:, h * BNW:(h + 1) * BNW], in0=score[0][:, h * BNW:(h + 1) * BNW],
                                  scalar1=c15[:, h:h + 1], scalar2=None, op0=ALU.subtract),
                     reads=[R_score[0], R_cst], writes=[R_Bn])
            checkpoint('consts')

            def win_cols(kc, c0, n):
                return winb[:, kc * NCOL + c0: kc * NCOL + c0 + n]

            def fm_proj(bank, xT, R_x, N, col0, nchunks, ocol=0):
                for j in range(nchunks):
                    for kc in range(KC):
                        P.op("pe", _call("matmul", out=pb[bank][:, ocol + j * N: ocol + (j + 1) * N], lhsT=win_cols(kc, col0 + j * 128, 128),
                                         rhs=xT[:, kc * N:(kc + 1) * N], start=(kc == 0), stop=(kc == KC - 1)),
                             reads=[R_win, R_x], writes=[R_pb[bank]])

            def tm_proj(bank, xT, R_x, N, col0, ncols, ocol=0):
                for kc in range(KC):
                    P.op("pe", _call("matmul", out=pb[bank][0:N, ocol:ocol + ncols], lhsT=xT[:, kc * N:(kc + 1) * N],
                                     rhs=win_cols(kc, col0, ncols), start=(kc == 0), stop=(kc == KC - 1)),
                         reads=[R_win, R_x], writes=[R_pb[bank]])

            def load_xT(r):
                s = r % 2
                P.dma("sp", xstg2[s][:, :], I["xkT"][r], writes=[R_xstg2[s]])
                P.op("pool", _call("tensor_copy", out=xTb[s][:, :], in_=xstg2[s][:, :]), reads=[R_xstg2[s]], writes=[R_xT[s]])

            def kside(r, full):
                s = r % 2
                xT, R_x = xTb[s], R_xT[s]
                so = r % 2
                bk = wrot.next()
                fm_proj(bk, xT, R_x, 128, C_KB, 1)
                fm_proj(bk, xT, R_x, 128, C_KI, 1, ocol=128)
                P.op("act", _call("activation", out=ostg[so][:, :], in_=pb[bk][:, 0:256], func=AF.Copy), reads=[R_pb[bk]], writes=[R_ostg[so]])
                P.op("pool", _call("tensor_copy", out=kbi[:, :].rearrange("p (a c) -> p a c", a=2)[:, :, r * 128:(r + 1) * 128],
                                   in_=ostg[so][:, :].rearrange("p (a c) -> p a c", a=2)),
                     reads=[R_ostg[so]], writes=[R_kbi[r]])
                P.dma("sp", O["bkT"][:, r * 128:(r + 1) * 128], ostg[so][:, 0:128], reads=[R_ostg[so]], defer=True)
                P.dma("sp", O["biT"][:, r * 128:(r + 1) * 128], ostg[so][0:64, 128:256], reads=[R_ostg[so]], defer=True)
                yield 3.0
                bv_ = wrot.next()
                tm_proj(bv_, xT, R_x, 128, C_VB, 128)
                vbv = vb_aug[:, r * 130:(r + 1) * 130].rearrange("p (g d) -> p g d", d=65)
                P.op("act", _call("activation", out=vbstg[so][:, :], in_=pb[bv_][:, 0:128], func=AF.Copy), reads=[R_pb[bv_]], writes=[R_vbstg[so]])
                P.op("pool", _call("tensor_copy", out=vbv[:, :, 0:64], in_=vbstg[so][:, :].rearrange("p (g d) -> p g d", d=64)),
                     reads=[R_vbstg[so]], writes=[R_vb[r]])
                P.op("pool", _call("tensor_scalar", out=vbv[:, :, 64:65], in0=ones8[:, 0:2].rearrange("p (g o) -> p g o", o=1),
                                   scalar1=kvalid[:, r:r + 1], scalar2=None, op0=ALU.mult),
                     reads=[R_cst], writes=[R_vb[r]])
                P.dma("sp", O["bv"][r * 128:(r + 1) * 128, :], vbstg[so][:, :], reads=[R_vbstg[so]], defer=True)
                yield 3.0
                if not full:
                    return
                slot = r % 6
                ba = wrot.next()
                fm_proj(ba, xT, R_x, 128, C_KA, 4)
                P.op("act", _call("activation", out=kaT[:, slot * 512:(slot + 1) * 512], in_=pb[ba][:, :], func=AF.Copy),
                     reads=[R_pb[ba]], writes=[R_ka[slot]])
                if r >= 28:
                    P.op("dve", _call("tensor_copy", out=astg[:, 0:512], in_=pb[ba][:, :]), reads=[R_pb[ba]], writes=[R_astg])
                    P.dma("sp", O["akT"].rearrange("p (j t) -> p j t", t=512)[:, :, (r - 28) * 128:(r - 27) * 128],
                          astg[:, 0:512].rearrange("p (j t) -> p j t", t=128), reads=[R_astg], defer=True)
                yield 3.0
                bva = wrot.next()
                tm_proj(bva, xT, R_x, 128, C_VA, 512)
                vav = va_aug[:, slot * 520:(slot + 1) * 520].rearrange("p (h d) -> p h d", d=65)
                P.op("act", _call("activation", out=vav[:, :, 0:64], in_=pb[bva][:, :].rearrange("p (h d) -> p h d", d=64), func=AF.Copy),
                     reads=[R_pb[bva]], writes=[R_va[slot]])
                P.op("pool", _call("tensor_scalar", out=vav[:, :, 64:65], in0=ones8[:, :].rearrange("p (h o) -> p h o", o=1),
                                   scalar1=kvalid[:, r:r + 1], scalar2=None, op0=ALU.mult),
                     reads=[R_cst], writes=[R_va[slot]])
                if r >= 28:
                    P.op("dve", _call("tensor_copy", out=astg[:, 512:1024], in_=pb[bva][:, :]), reads=[R_pb[bva]], writes=[R_astg])
                    P.dma("sp", O["av"][(r - 28) * 128:(r - 27) * 128, :], astg[:, 512:1024], reads=[R_astg], defer=True)
                yield 3.0

            def qside(xT, R_x, qs, st):
                b1 = wrot.next()
                fm_proj(b1, xT, R_x, qs, C_QA, 4)
                for hf in range(2):
                    P.op("act", _call("activation",
                                      out=qaz[st][hf * 64:(hf + 1) * 64, 0:8 * qs].rearrange("p (j two q) -> p j two q", two=2, q=qs)[:, :, hf, :],
                                      in_=pb[b1][hf * 64:(hf + 1) * 64, 0:4 * qs].rearrange("p (j q) -> p j q", q=qs), func=AF.Copy, scale=0.125),
                         reads=[R_pb[b1]], writes=[R_qa[st]])
                yield 3.0
                b2 = wrot.next()
                fm_proj(b2, xT, R_x, qs, C_QB, 4)
                for g in range(2):
                    P.op("act", _call("activation", out=qbz[st][g * 64:(g + 1) * 64, g * 4 * qs:(g + 1) * 4 * qs],
                                      in_=pb[b2][g * 64:(g + 1) * 64, 0:4 * qs], func=AF.Copy, scale=0.125),
                         reads=[R_pb[b2]], writes=[R_qb[st]])
                yield 3.0
                b3 = wrot.next()
                fm_proj(b3, xT, R_x, qs, C_QI, 4)
                for hf in range(2):
                    P.op("act", _call("activation",
                                      out=qiz[st][hf * 64:(hf + 1) * 64, 0:8 * qs].rearrange("p (j two q) -> p j two q", two=2, q=qs)[:, :, hf, :],
                                      in_=pb[b3][hf * 64:(hf + 1) * 64, 0:4 * qs].rearrange("p (j q) -> p j q", q=qs), func=AF.Copy),
                         reads=[R_pb[b3]], writes=[R_qi[st]])
                b4 = wrot.next()
                tm_proj(b4, xT, R_x, qs, C_WI, 8)
                P.op("dve", _call("tensor_scalar", out=coef[st][0:qs, :], in0=pb[b4][0:qs, 0:8], scalar1=float(8.0 ** -1.5), scalar2=None, op0=ALU.mult),
                     reads=[R_pb[b4]], writes=[R_coef[st]])
                for h in range(8):
                    P.op("pool", _call("tensor_scalar", out=dg[st][0:qs, h * 128: h * 128 + qs], in0=ident_f[0:qs, 0:qs],
                                       scalar1=coef[st][0:qs, h:h + 1], scalar2=None, op0=ALU.mult),
                         reads=[R_coef[st], R_ident], writes=[R_dg[st]])
                yield 3.0

            def normalize(bank, qs, mixt, R_m, col0, rec, R_rec):
                ov = pb[bank][0:qs, 0:260].rearrange("p (h d) -> p h d", d=65)
                P.op("dve", _call("tensor_scalar", out=rec[0:qs, 0:4].rearrange("p (h o) -> p h o", o=1), in0=ov[:, :, 64:65],
                                  scalar1=1e-30, scalar2=None, op0=ALU.max),
                     reads=[R_pb[bank]], writes=[R_rec])
                P.op("dve", _call("reciprocal", out=rec[0:qs, 0:4], in_=rec[0:qs, 0:4]), reads=[R_rec], writes=[R_rec])
                for hh in range(4):
                    P.op("dve", _call("tensor_scalar", out=mixt[0:qs, col0 + hh * 64: col0 + (hh + 1) * 64],
                                      in0=pb[bank][0:qs, hh * 65: hh * 65 + 64],
                                      scalar1=rec[0:qs, hh:hh + 1], scalar2=None, op0=ALU.mult),
                         reads=[R_pb[bank], R_rec], writes=[R_m])

            def pipe3(items, s1, s2, s3, D, cost=1.0):
                pend = []
                for it in items:
                    s1(it)
                    s2(it)
                    pend.append(it)
                    if len(pend) > D:
                        s3(pend.pop(0))
                    yield cost
                while pend:
                    s3(pend.pop(0))
                    yield cost

            pta_rot = Rot([0, 1, 2])
            relu_rot = Rot([0, 1, 2])
            ptb_rot = Rot([0, 1, 2])
            brot = Rot([3, 7])

            def front_attn(sn, qs, wins, btiles, prompt_masks, abw):
                st = sn % 2
                mixt, R_m = mixb[st], R_mix[st]
                nw = len(wins)

                units = []
                for h in range(8):
                    units.append({"h": h, "t0": 0, "tiles": wins[0:4]})
                    if nw > 4:
                        units.append({"h": h, "t0": 4, "tiles": wins[4:5]})

                def a1(u):
                    h = u["h"]
                    j = h // 2
                    bank = wrot.next()
                    u["bank"] = bank
                    for i, (slot, ts) in enumerate(u["tiles"]):
                        t = u["t0"] + i
                        c0 = i * qs
                        P.op("pe", _call("matmul", out=pb[bank][0:ts, c0:c0 + qs], lhsT=kaT[:, slot * 512 + j * 128: slot * 512 + j * 128 + ts],
                                         rhs=qaz[st][:, h * qs:(h + 1) * qs], start=True, stop=False),
                             reads=[R_ka[slot], R_qa[st]], writes=[R_pb[bank]])
                        P.op("pe", _call("matmul", out=pb[bank][0:ts, c0:c0 + qs], lhsT=ABb[0:qs, h * abw + t * 128: h * abw + t * 128 + ts],
                                         rhs=ident[0:qs, 0:qs], start=False, stop=True),
                             reads=[R_AB, R_ident], writes=[R_pb[bank]])

                def a2(u):
                    k = pta_rot.next()
                    u["pt"], u["R_pt"] = PTA[k], R_PTA[k]
                    bank = u["bank"]
                    tsm = max(ts for (_, ts) in u["tiles"])
                    n = len(u["tiles"])
                    P.op("act", _call("activation", out=u["pt"][0:tsm, 0:n * qs], in_=pb[bank][0:tsm, 0:n * qs], func=AF.Exp),
                         reads=[R_pb[bank]], writes=[u["R_pt"]])

                def a3(u):
                    h = u["h"]
                    last_unit = (u["t0"] + len(u["tiles"]) == nw)
                    for i, (slot, ts) in enumerate(u["tiles"]):
                        t = u["t0"] + i
                        P.op("pe", _call("matmul", out=pb[4][0:qs, (h % 4) * 65:(h % 4) * 65 + 65], lhsT=u["pt"][0:ts, i * qs:(i + 1) * qs],
                                         rhs=va_aug[0:ts, slot * 520 + h * 65: slot * 520 + h * 65 + 65],
                                         start=(h % 4 == 0 and t == 0), stop=(t == nw - 1), skip_group_check=True),
                             reads=[u["R_pt"], R_va[slot]], writes=[R_pb[4]])
                    if last_unit and h % 4 == 3:
                        normalize(4, qs, mixt, R_m, (h // 4) * 256, recA[st], R_recA[st])

                yield from pipe3(units, a1, a2, a3, 2, 0.9)

                L = btiles[-1][1] + btiles[-1][2]
                items = []
                cc = 0
                for c0 in range(0, L, 512):
                    w = min(512, L - c0)
                    rk = [R_kbi[tt[0]] for tt in btiles if tt[1] >= c0 - 127 and tt[1] < c0 + w]
                    for h in range(8):
                        items.append({"c0": c0, "w": w, "h": h, "sc": (5, 4)[cc % 2], "rk": rk})
                    cc += 1

                def i1(it):
                    bank = wrot.next()
                    it["bank"] = bank
                    h, c0, w = it["h"], it["c0"], it["w"]
                    P.op("pe", _call("matmul", out=pb[bank][0:qs, 0:w], lhsT=qiz[st][:, h * qs:(h + 1) * qs],
                                     rhs=kbi[:, 4096 + c0: 4096 + c0 + w], start=True, stop=True),
                         reads=[R_qi[st]] + it["rk"], writes=[R_pb[bank]])

                def i2(it):
                    k = relu_rot.next()
                    it["rl"], it["R_rl"] = relu[k], R_relu[k]
                    w = it["w"]
                    P.op("act", _call("activation", out=it["rl"][0:qs, 0:w], in_=pb[it["bank"]][0:qs, 0:w], func=AF.Relu),
                         reads=[R_pb[it["bank"]]], writes=[it["R_rl"]])

                def i3(it):
                    h, c0, w, sc = it["h"], it["c0"], it["w"], it["sc"]
                    P.op("pe", _call("matmul", out=pb[sc][0:qs, 0:w], lhsT=dg[st][0:qs, h * 128: h * 128 + qs], rhs=it["rl"][0:qs, 0:w],
                                     start=(h == 0), stop=(h == 7)),
                         reads=[R_dg[st], it["R_rl"]], writes=[R_pb[sc]])
                    if h == 7:
                        if prompt_masks and c0 < 2048:
                            wm = min(w, 2048 - c0)
                            P.op("act", _call("activation", out=score[st][0:qs, c0:c0 + wm], in_=pb[sc][0:qs, 0:wm], func=AF.Identity,
                                              bias=colmask[0:qs, 0:1]),
                                 reads=[R_pb[sc], R_cst], writes=[R_score[st]])
                            if wm < w:
                                P.op("act", _call("activation", out=score[st][0:qs, c0 + wm:c0 + w], in_=pb[sc][0:qs, wm:w], func=AF.Copy),
                                     reads=[R_pb[sc]], writes=[R_score[st]])
                        else:
                            P.op("act", _call("activation", out=score[st][0:qs, c0:c0 + w], in_=pb[sc][0:qs, 0:w], func=AF.Copy),
                                 reads=[R_pb[sc]], writes=[R_score[st]])

                yield from pipe3(items, i1, i2, i3, 2, 0.65)
                if prompt_masks:
                    P.op("dve", _call("tensor_tensor", out=score[st][0:qs, L - 128:L], in0=score[st][0:qs, L - 128:L], in1=diagm[0:qs, :], op=ALU.add),
                         reads=[R_score[st], R_cst], writes=[R_score[st]])
                yield

            def back_attn(sn, qs, btiles, blk, bnw):
                st = sn % 2
                mixt, R_m = mixb[st], R_mix[st]
                sm, R_sm = small[st], R_small[st]
                L = btiles[-1][1] + btiles[-1][2]
                P.op("dve", _call("memset", sm[0:qs, 1:2], 0.0), writes=[R_sm])
                for k in range(NIT):
                    wk = BIS_W0 / (2.0 ** k)
                    P.op("dve", _call("tensor_scalar", out=Mb[st][0:qs, 0:L], in0=score[st][0:qs, 0:L], scalar1=sm[0:qs, 1:2], scalar2=None,
                                      op0=ALU.is_ge, op1=ALU.add, accum_out=sm[0:qs, 0:1]),
                         reads=[R_score[st], R_sm], writes=[R_M[st], R_sm])
                    P.op("dve", _call("tensor_scalar", out=sm[0:qs, 2:3], in0=sm[0:qs, 0:1], scalar1=255.5, scalar2=wk,
                                      op0=ALU.is_ge, op1=ALU.mult),
                         reads=[R_sm], writes=[R_sm])
                    P.op("dve", _call("scalar_tensor_tensor", out=sm[0:qs, 1:2], in0=sm[0:qs, 2:3], scalar=-wk / 2.0,
                                      in1=sm[0:qs, 1:2], op0=ALU.add, op1=ALU.add),
                         reads=[R_sm], writes=[R_sm])
                    yield L * 1.08e-3 + 0.5
                wl = BIS_W0 / (2.0 ** (NIT - 1)) / 2.0
                P.op("dve", _call("tensor_scalar", out=sm[0:qs, 3:4], in0=sm[0:qs, 1:2], scalar1=-wl, scalar2=None, op0=ALU.add),
                     reads=[R_sm], writes=[R_sm])
                P.op("dve", _call("tensor_scalar", out=Mb[st][0:qs, 0:L], in0=score[st][0:qs, 0:L], scalar1=sm[0:qs, 3:4], scalar2=NEGM,
                                  op0=ALU.is_lt, op1=ALU.mult),
                     reads=[R_score[st], R_sm], writes=[R_M[st]])
                nearw = btiles[-2][2] + btiles[-1][2]
                for h in range(8):
                    P.op("dve", _call("tensor_tensor", out=Mnear[st][0:qs, h * bnw: h * bnw + nearw], in0=Bnb[0:qs, h * bnw: h * bnw + nearw],
                                      in1=Mb[st][0:qs, L - nearw:L], op=ALU.add),
                         reads=[R_Bn, R_M[st]], writes=[R_Mnear[st]])
                yield
                nb = len(btiles)
                items = [{"g": g, "t": t, "vt": vt, "c0": c0, "ts": ts} for g in range(2) for t, (vt, c0, ts) in enumerate(btiles)]

                def b1(it):
                    g, t, vt, c0, ts = it["g"], it["t"], it["vt"], it["c0"], it["ts"]
                    bank = brot.next()
                    it["bank"] = bank
                    P.op("pe", _call("matmul", out=pb[bank][0:ts, 0:4 * qs], lhsT=kbi[:, c0:c0 + ts],
                                     rhs=qbz[st][:, g * 4 * qs:(g + 1) * 4 * qs], start=True, stop=False),
                         reads=[R_kbi[vt], R_qb[st]], writes=[R_pb[bank]])
                    if t < nb - 2 and qs == 128:
                        P.op("pe", _call("matmul", out=pb[bank][0:ts, 0:512], lhsT=Mb[st][0:qs, c0:c0 + ts], rhs=ident[0:128, 0:512],
                                         start=False, stop=True),
                             reads=[R_M[st], R_ident], writes=[R_pb[bank]])
                    elif t < nb - 2:
                        for r in range(4):
                            P.op("pe", _call("matmul", out=pb[bank][0:ts, r * qs:(r + 1) * qs], lhsT=Mb[st][0:qs, c0:c0 + ts],
                                             rhs=ident[0:qs, 0:qs], start=False, stop=(r == 3)),
                                 reads=[R_M[st], R_ident], writes=[R_pb[bank]])
                    else:
                        tt = t - (nb - 2)
                        for r in range(4):
                            hh = g * 4 + r
                            P.op("pe", _call("matmul", out=pb[bank][0:ts, r * qs:(r + 1) * qs],
                                             lhsT=Mnear[st][0:qs, hh * bnw + tt * 128: hh * bnw + tt * 128 + ts], rhs=ident[0:qs, 0:qs],
                                             start=False, stop=(r == 3)),
                                 reads=[R_Mnear[st], R_ident], writes=[R_pb[bank]])

                def b2(it):
                    k = ptb_rot.next()
                    it["ptb"], it["R_ptb"] = PTB[k], R_PTB[k]
                    ts = it["ts"]
                    P.op("act", _call("activation", out=it["ptb"][0:ts, 0:4 * qs], in_=pb[it["bank"]][0:ts, 0:4 * qs], func=AF.Exp),
                         reads=[R_pb[it["bank"]]], writes=[it["R_ptb"]])

                def b3(it):
                    g, t, vt, ts = it["g"], it["t"], it["vt"], it["ts"]
                    for r in range(4):
                        P.op("pe", _call("matmul", out=pb[6][0:qs, r * 65: r * 65 + 65], lhsT=it["ptb"][0:ts, r * qs:(r + 1) * qs],
                                         rhs=vb_aug[0:ts, (vt * 2 + g) * 65:(vt * 2 + g) * 65 + 65],
                                         start=(t == 0 and r == 0), stop=(t == nb - 1), skip_group_check=True),
                             reads=[it["R_ptb"], R_vb[vt]], writes=[R_pb[6]])
                    if t == nb - 1:
                        normalize(6, qs, mixt, R_m, 512 + g * 256, recB[st], R_recB[st])

                yield from pipe3(items, b1, b2, b3, 1, 0.8)
                P.dma("sp", mixD[blk * 128: blk * 128 + qs, :], mixt[0:qs, :], reads=[R_m], writes=[R_mixD[blk]], defer=True)
                yield

            load_xT(0)
            for r in range(16):
                if r + 1 < 16:
                    load_xT(r + 1)
                for _ in kside(r, full=(r >= 11)):
                    pass
            checkpoint('phase0')

            def prompt_front(sn, T):
                if T >= 16:
                    load_xT(T)
                    yield from kside(T, full=True)
                s = T % 2
                yield from qside(xTb[s], R_xT[s], 128, sn % 2)
                wins = [((T - 4 + t) % 6, 128) for t in range(5)]
                btiles = [(t, t * 128, 128) for t in range(T + 1)]
                yield from front_attn(sn, 128, wins, btiles, True, ABW)

            def prompt_back(sn, T, blk):
                btiles = [(t, t * 128, 128) for t in range(T + 1)]
                yield from back_attn(sn, 128, btiles, blk, BNW)

            steps = [(0, 15, 16)] + [(1 + i, 16 + i, i) for i in range(16)]
            run_interleaved([prompt_front(*steps[0][0:2])])
            for si in range(len(steps)):
                sn, T, blk = steps[si]
                gens = [prompt_back(sn, T, blk)]
                if si + 1 < len(steps):
                    gens.append(prompt_front(*steps[si + 1][0:2]))
                run_interleaved(gens)
            checkpoint('steps')

            SN = len(steps)
            sst = SN % 2
            P.dma("sp", score[0][:, 0:2048], I["cbkT"][:, :], writes=[R_score[0]])
            P.op("act", _call("activation", out=kbi[:, 0:2048], in_=score[0][:, 0:2048], func=AF.Copy),
                 reads=[R_score[0]], writes=R_kbi[0:16])
            P.dma("sp", score[0][:, 2048:4096], I["cbiT"][:, :], writes=[R_score[0]])
            P.op("act", _call("activation", out=kbi[:, 4096:4096 + 2048], in_=score[0][:, 2048:4096], func=AF.Copy),
                 reads=[R_score[0]], writes=R_kbi[0:16])
            P.dma("sp", score[1][:, 0:2048].rearrange("p (t c) -> p t c", c=128), I["cbv"].rearrange("(t p) c -> p t c", p=128), writes=[R_score[1]])
            vball = vb_aug[:, 0:16 * 130].rearrange("p (t d) -> p t d", d=65)
            P.op("act", _call("activation", out=vball[:, :, 0:64], in_=score[1][:, 0:2048].rearrange("p (t d) -> p t d", d=64), func=AF.Copy),
                 reads=[R_score[1]], writes=R_vb[0:17])
            P.op("pool", _call("memset", vb_aug[:, 0:17 * 130].rearrange("p (t d) -> p t d", d=65)[:, :, 64:65], 1.0), writes=R_vb[0:17])
            P.dma("sp", score[0][:, 0:2048], I["cakT"][:, :], writes=[R_score[0]])
            for s4 in range(4):
                P.op("act", _call("activation", out=kaT[:, s4 * 512:(s4 + 1) * 512].rearrange("p (j t) -> p j t", t=128),
                                  in_=score[0][:, 0:2048].rearrange("p (j t) -> p j t", t=512)[:, :, s4 * 128:(s4 + 1) * 128], func=AF.Copy),
                     reads=[R_score[0]], writes=[R_ka[s4]])
            P.dma("sp", score[1][:, 2048:4096].rearrange("p (t c) -> p t c", c=512), I["cav"].rearrange("(t p) c -> p t c", p=128), writes=[R_score[1]])
            vaall = va_aug[:, 0:4 * 520].rearrange("p (t d) -> p t d", d=65)
            P.op("act", _call("activation", out=vaall[:, :, 0:64], in_=score[1][:, 2048:4096].rearrange("p (t d) -> p t d", d=64), func=AF.Copy),
                 reads=[R_score[1]], writes=R_va[0:5])
            P.op("pool", _call("memset", va_aug[:, 0:5 * 520].rearrange("p (t d) -> p t d", d=65)[:, :, 64:65], 1.0), writes=R_va[0:5])
            for hh in range(2):
                w = 4 * 528
                P.dma("sp", score[0][0:16, 0:w], I["ABs"][:, hh * w:(hh + 1) * w], writes=[R_score[0]])
                P.op("act", _call("activation", out=ABb[0:16, hh * w:(hh + 1) * w], in_=score[0][0:16, 0:w], func=AF.Copy),
                     reads=[R_score[0]], writes=[R_AB])
            P.dma("sp", score[1][0:16, 0:8 * 144], I["Bns"][:, :], writes=[R_score[1]])
            for h in range(8):
                P.op("dve", _call("tensor_scalar", out=Bnb[0:16, h * 144:(h + 1) * 144], in0=score[1][0:16, h * 144:(h + 1) * 144],
                                  scalar1=c15[0:16, h:h + 1], scalar2=None, op0=ALU.subtract),
                     reads=[R_score[1], R_cst], writes=[R_Bn])
            P.op("pool", _call("memset", qaz[sst][:, :], 0.0), writes=[R_qa[sst]])
            P.op("pool", _call("memset", qbz[sst][:, :], 0.0), writes=[R_qb[sst]])
            P.op("pool", _call("memset", qiz[sst][:, :], 0.0), writes=[R_qi[sst]])
            P.dma("sp", xstg[:, 0:128], I["xsT"][:, :], writes=[R_xstg])
            P.op("pool", _call("tensor_copy", out=xTb[0][:, 0:128], in_=xstg[:, 0:128]), reads=[R_xstg], writes=[R_xT[0]])
            xs_, R_xs = xTb[0], R_xT[0]
            bk = wrot.next()
            fm_proj(bk, xs_, R_xs, 16, C_KB, 1)
            fm_proj(bk, xs_, R_xs, 16, C_KI, 1, ocol=16)
            P.op("act", _call("activation", out=kbi[:, 2048:2064], in_=pb[bk][:, 0:16], func=AF.Copy), reads=[R_pb[bk]], writes=[R_kbi[16]])
            P.op("act", _call("activation", out=kbi[:, 4096 + 2048:4096 + 2064], in_=pb[bk][:, 16:32], func=AF.Copy), reads=[R_pb[bk]], writes=[R_kbi[16]])
            P.op("dve", _call("tensor_copy", out=ostg[0][:, 0:32], in_=pb[bk][:, 0:32]), reads=[R_pb[bk]], writes=[R_ostg[0]])
            P.dma("sp", O["sbkT"][:, :], ostg[0][:, 0:16], reads=[R_ostg[0]], defer=True)
            P.dma("sp", O["sbiT"][:, :], ostg[0][0:64, 16:32], reads=[R_ostg[0]], defer=True)
            bv_ = wrot.next()
            tm_proj(bv_, xs_, R_xs, 16, C_VB, 128)
            vbv = vb_aug[0:16, 16 * 130:17 * 130].rearrange("p (g d) -> p g d", d=65)
            P.op("act", _call("activation", out=vbv[:, :, 0:64], in_=pb[bv_][0:16, 0:128].rearrange("p (g d) -> p g d", d=64), func=AF.Copy),
                 reads=[R_pb[bv_]], writes=[R_vb[16]])
            P.op("dve", _call("tensor_copy", out=vbstg[0][0:16, :], in_=pb[bv_][0:16, 0:128]), reads=[R_pb[bv_]], writes=[R_vbstg[0]])
            P.dma("sp", O["sbv"][:, :], vbstg[0][0:16, :], reads=[R_vbstg[0]], defer=True)
            ba = wrot.next()
            fm_proj(ba, xs_, R_xs, 16, C_KA, 4)
            P.op("act", _call("activation", out=kaT[:, 4 * 512:5 * 512].rearrange("p (j t) -> p j t", t=128)[:, :, 0:16],
                              in_=pb[ba][:, 0:64].rearrange("p (j t) -> p j t", t=16), func=AF.Copy),
                 reads=[R_pb[ba]], writes=[R_ka[4]])
            P.op("dve", _call("tensor_copy", out=astg[:, 0:64], in_=pb[ba][:, 0:64]), reads=[R_pb[ba]], writes=[R_astg])
            P.dma("sp", O["sakT"][:, :], astg[:, 0:64], reads=[R_astg], defer=True)
            bva = wrot.next()
            tm_proj(bva, xs_, R_xs, 16, C_VA, 512)
            vav = va_aug[0:16, 4 * 520:5 * 520].rearrange("p (h d) -> p h d", d=65)
            P.op("act", _call("activation", out=vav[:, :, 0:64], in_=pb[bva][0:16, :].rearrange("p (h d) -> p h d", d=64), func=AF.Copy),
                 reads=[R_pb[bva]], writes=[R_va[4]])
            P.op("dve", _call("tensor_copy", out=astg[0:16, 512:1024], in_=pb[bva][0:16, :]), reads=[R_pb[bva]], writes=[R_astg])
            P.dma("sp", O["sav"][:, :], astg[0:16, 512:1024], reads=[R_astg], defer=True)
            checkpoint('sample_pre')
            wins = [(0, 128), (1, 128), (2, 128), (3, 128), (4, 16)]
            btiles = [(t, t * 128, 128) for t in range(16)] + [(16, 2048, 16)]

            def sample_all():
                yield from qside(xs_, R_xs, 16, sst)
                yield from front_attn(SN, 16, wins, btiles, False, 528)
                yield from back_attn(SN, 16, btiles, 17, 144)

            run_interleaved([sample_all()])
            checkpoint('phaseA')
            P.flush(block)

        P.barrier()
        with ExitStack() as sbk:
            wob = sb(sbk, "wob", [128, 8 * 1024], BF16)
            wmqb = sb(sbk, "wmqb", [128, 8 * 512], BF16)
            wmob = sb(sbk, "wmob", [128, 4 * 1024], BF16)
            wtmp = sb(sbk, "wtmp", [128, 8 * 512], BF16)
            R_wo, R_wmq, R_wmo, R_wtmp = Res("wo"), Res("wmq"), Res("wmo"), Res("wtmp")
            wst = [sb(sbk, "wst%d" % k, [128, 2048], F32) for k in range(2)]
            R_wst = [Res("wst%d" % k) for k in range(2)]
            lnt = sb(sbk, "lnt", [128, 4 * 1024], F32)
            R_ln = Res("ln")
            memTb = sb(sbk, "memTb", [128, 8 * 256], BF16)
            R_memT = Res("memT")
            mkT = [sb(sbk, "mkT%d" % k, [128, 4 * 256], BF16) for k in range(2)]
            mva = [sb(sbk, "mva%d" % k, [128, 2 * 4 * 129], BF16) for k in range(2)]
            R_mk = [Res("mk%d" % k) for k in range(2)]
            R_mv = [Res("mv%d" % k) for k in range(2)]
            mixl = [sb(sbk, "mixl%d" % k, [128, 1024], BF16) for k in range(4)]
            R_mixl = [Res("mixl%d" % k) for k in range(4)]
            xr = [sb(sbk, "xr%d" % k, [128, 1024], F32) for k in range(4)]
            R_xr = [Res("xr%d" % k) for k in range(4)]
            NB3 = 4
            tT_l = [sb(sbk, "tT%d" % k, [128, 1024], BF16) for k in range(NB3)]
            hA_l = [sb(sbk, "hA%d" % k, [128, 1024], F32) for k in range(NB3)]
            hB_l = [sb(sbk, "hB%d" % k, [128, 1024], F32) for k in range(NB3)]
            h16_l = [sb(sbk, "h16%d" % k, [128, 1024], BF16) for k in range(NB3)]
            qmT_l = [sb(sbk, "qmT%d" % k, [128, 512], BF16) for k in range(NB3)]
            PTm_l = [sb(sbk, "PTm%d" % k, [128, 1024], BF16) for k in range(NB3)]
            o16_l = [sb(sbk, "o16%d" % k, [128, 512], BF16) for k in range(NB3)]
            oT_l = [sb(sbk, "oT%d" % k, [128, 512], BF16) for k in range(NB3)]
            stat_l = [sb(sbk, "stat%d" % k, [128, 32], F32) for k in range(NB3)]
            RB = [{n: Res(n + str(k)) for n in ("tT", "hA", "hB", "h16", "qm", "PTm", "o16", "oT", "stat")} for k in range(NB3)]
            h2T = [sb(sbk, "h2T%d" % k, [128, 1024], BF16) for k in range(4)]
            R_h2T = [Res("h2T%d" % k) for k in range(4)]
            mstg = sb(sbk, "mstg", [128, 1024], F32)
            R_mstg = Res("mstg")
            wrot = Rot([0, 1, 2, 3, 4, 5, 6, 7])

            def load_cast(dst, R_dst, src, ncols, engs=("act", "pool")):
                k = 0
                for c0 in range(0, ncols, 2048):
                    w = min(2048, ncols - c0)
                    s = k % 2
                    P.dma("sp", wst[s][:, 0:w], src[:, c0:c0 + w], writes=[R_wst[s]])
                    eng = engs[k % len(engs)]
                    if eng == "act":
                        P.op("act", _call("activation", out=dst[:, c0:c0 + w], in_=wst[s][:, 0:w], func=AF.Copy),
                             reads=[R_wst[s]], writes=[R_dst])
                    else:
                        P.op(eng, _call("tensor_copy", out=dst[:, c0:c0 + w], in_=wst[s][:, 0:w]),
                             reads=[R_wst[s]], writes=[R_dst])
                    k += 1

            load_cast(wob, R_wo, I["wo"], 8192)
            load_cast(wmqb, R_wmq, I["wmq"], 4096)
            load_cast(wmob, R_wmo, I["wmo"], 4096)
            for k in range(4):
                P.dma("sp", lnt[:, k * 1024:(k + 1) * 1024], I["lnp"][k:k + 1, :].to_broadcast([128, 1024]), writes=[R_ln])
            load_cast(memTb, R_memT, I["memT"], 2048)
            load_cast(wtmp, R_wtmp, I["wmk"], 4096)
            for h in range(4):
                bank = wrot.next()
                for kc in range(KC):
                    P.op("pe", _call("matmul",
                        out=pb[bank][:, 0:256], lhsT=wtmp[:, kc * 512 + h * 128: kc * 512 + (h + 1) * 128],
                        rhs=memTb[:, kc * 256:(kc + 1) * 256], start=(kc == 0), stop=(kc == KC - 1)),
                        reads=[R_wtmp, R_memT], writes=[R_pb[bank]])
                P.op("act", _call("activation", out=mkT[0][:, h * 256:(h + 1) * 256], in_=pb[bank][:, 0:256], func=AF.Copy),
                     reads=[R_pb[bank]], writes=[R_mk[0]])
                P.op("dve", _call("tensor_copy", out=mstg[:, h * 256:(h + 1) * 256], in_=pb[bank][:, 0:256]),
                     reads=[R_pb[bank]], writes=[R_mstg])
            P.dma("sp", O["mkT"][:, :], mstg[:, :], reads=[R_mstg], defer=True)
            load_cast(wtmp, R_wtmp, I["wmv"], 4096)
            for mt in range(2):
                bank = wrot.next()
                for kc in range(KC):
                    P.op("pe", _call("matmul",
                        out=pb[bank][:, 0:512], lhsT=memTb[:, kc * 256 + mt * 128: kc * 256 + (mt + 1) * 128],
                        rhs=wtmp[:, kc * 512:(kc + 1) * 512], start=(kc == 0), stop=(kc == KC - 1)),
                        reads=[R_wtmp, R_memT], writes=[R_pb[bank]])
                mvv = mva[0][:, mt * 516:(mt + 1) * 516].rearrange("p (h d) -> p h d", d=129)
                P.op("act", _call("activation", out=mvv[:, :, 0:128], in_=pb[bank][:, :].rearrange("p (h d) -> p h d", d=128), func=AF.Copy),
                     reads=[R_pb[bank]], writes=[R_mv[0]])
                P.op("dve", _call("tensor_copy", out=mstg[:, mt * 512:(mt + 1) * 512], in_=pb[bank][:, :]),
                     reads=[R_pb[bank]], writes=[R_mstg])
                P.dma("sp", O["mv"][mt * 128:(mt + 1) * 128, :], mstg[:, mt * 512:(mt + 1) * 512], reads=[R_mstg], defer=True)
            for k in range(2):
                P.op("pool", _call("memset", mva[k][:, :].rearrange("p (t d) -> p t d", d=129)[:, :, 128:129], 1.0), writes=[R_mv[k]])
            load_cast(mkT[1], R_mk[1], I["cmkT"], 1024)
            P.dma("sp", wst[0][:, 0:1024].rearrange("p (t c) -> p t c", c=512), I["cmv"].rearrange("(t p) c -> p t c", p=128), writes=[R_wst[0]])
            P.op("act", _call("activation", out=mva[1][:, :].rearrange("p (t d) -> p t d", d=129)[:, :, 0:128],
                                               in_=wst[0][:, 0:1024].rearrange("p (t d) -> p t d", d=128), func=AF.Copy),
                 reads=[R_wst[0]], writes=[R_mv[1]])

            checkpoint('phaseB_pre')
            def transpose_to(src16, R_src, qs, nchunk, dst, R_dst):
                bank = wrot.next()
                pbf = pb[bank][:, :].bitcast(BF16)
                for c in range(nchunk):
                    P.op("pe", _call("transpose", out=pbf[:, c * qs:(c + 1) * qs], in_=src16[0:qs, c * 128:(c + 1) * 128],
                                                                   identity=ident[0:qs, 0:qs]),
                         reads=[R_src, R_ident], writes=[R_pb[bank]])
                P.op("act", _call("activation", out=dst[:, 0:nchunk * qs], in_=pbf[:, 0:nchunk * qs], func=AF.Copy),
                     reads=[R_pb[bank]], writes=[R_dst])

            def layer_norm(hin, R_hin, qs, gcol, hout, R_hout, stat, R_stat):
                for c in range(2):
                    P.op("dve", _call("bn_stats", out=stat[0:qs, c * 6:(c + 1) * 6], in_=hin[0:qs, c * 512:(c + 1) * 512]),
                         reads=[R_hin], writes=[R_stat])
                P.op("dve", _call("bn_aggr", out=stat[0:qs, 12:14], in_=stat[0:qs, 0:12]), reads=[R_stat], writes=[R_stat])
                P.op("dve", _call("tensor_scalar", out=stat[0:qs, 14:15], in0=stat[0:qs, 13:14], scalar1=LN_EPS, scalar2=None, op0=ALU.add),
                     reads=[R_stat], writes=[R_stat])
                P.op("act", _call("activation", out=stat[0:qs, 15:16], in_=stat[0:qs, 14:15], func=AF.Sqrt), reads=[R_stat], writes=[R_stat])
                P.op("dve", _call("reciprocal", out=stat[0:qs, 16:17], in_=stat[0:qs, 15:16]), reads=[R_stat], writes=[R_stat])
                P.op("dve", _call("scalar_tensor_tensor", out=stat[0:qs, 17:18], in0=stat[0:qs, 12:13], scalar=-1.0, in1=stat[0:qs, 16:17],
                                                             op0=ALU.mult, op1=ALU.mult),
                     reads=[R_stat], writes=[R_stat])
                P.op("act", _call("activation", out=hout[0:qs, :], in_=hin[0:qs, :], func=AF.Identity, scale=stat[0:qs, 16:17], bias=stat[0:qs, 17:18]),
                     reads=[R_hin, R_stat], writes=[R_hout])
                P.op("dve", _call("tensor_tensor", out=hout[0:qs, :], in0=hout[0:qs, :], in1=lnt[0:qs, gcol * 1024:(gcol + 1) * 1024], op=ALU.mult),
                     reads=[R_hout, R_ln], writes=[R_hout])
                P.op("dve", _call("tensor_tensor", out=hout[0:qs, :], in0=hout[0:qs, :], in1=lnt[0:qs, (gcol + 1) * 1024:(gcol + 2) * 1024], op=ALU.add),
                     reads=[R_hout, R_ln], writes=[R_hout])

            def phaseB_block(blk, qs, row0, mi, k2):
                s = k2
                tT, hA, hB, h16, qmT, PTm, o16, oT, stat = (tT_l[k2], hA_l[k2], hB_l[k2], h16_l[k2], qmT_l[k2], PTm_l[k2], o16_l[k2],
                                                             oT_l[k2], stat_l[k2])
                R_tT, R_hA, R_hB, R_h16, R_qm, R_PTm, R_o16, R_oT, R_stat = (RB[k2][n] for n in ("tT", "hA", "hB", "h16", "qm", "PTm", "o16", "oT", "stat"))
                P.dma("sp", mixl[s][0:qs, :], mixD[blk * 128 + row0: blk * 128 + row0 + qs, :], reads=[R_mixD[blk]], writes=[R_mixl[s]])
                P.dma("sp", xr[s][0:qs, :], I["xres"][blk * 128: blk * 128 + qs, :], writes=[R_xr[s]])
                transpose_to(mixl[s], R_mixl[s], qs, 8, tT, R_tT)
                yield
                b0, b1 = wrot.next(), wrot.next()
                for n, bank in enumerate((b0, b1)):
                    for kc in range(KC):
                        P.op("pe", _call("matmul",
                            out=pb[bank][0:qs, :], lhsT=tT[:, kc * qs:(kc + 1) * qs], rhs=wob[:, kc * 1024 + n * 512: kc * 1024 + (n + 1) * 512],
                            start=(kc == 0), stop=(kc == KC - 1)),
                            reads=[R_tT, R_wo], writes=[R_pb[bank]])
                    P.op("dve", _call("scalar_tensor_tensor",
                        out=hA[0:qs, n * 512:(n + 1) * 512], in0=xr[s][0:qs, n * 512:(n + 1) * 512], scalar=ALPHA, in1=pb[bank][0:qs, :],
                        op0=ALU.mult, op1=ALU.add),
                        reads=[R_xr[s], R_pb[bank]], writes=[R_hA])
                yield
                layer_norm(hA, R_hA, qs, 0, hB, R_hB, stat, R_stat)
                yield
                P.op("act", _call("activation", out=h16[0:qs, :], in_=hB[0:qs, :], func=AF.Copy), reads=[R_hB], writes=[R_h16])
                transpose_to(h16, R_h16, qs, 8, tT, R_tT)
                yield
                bq = wrot.next()
                for h in range(4):
                    for kc in range(KC):
                        P.op("pe", _call("matmul",
                            out=pb[bq][:, h * qs:(h + 1) * qs], lhsT=wmqb[:, kc * 512 + h * 128: kc * 512 + (h + 1) * 128],
                            rhs=tT[:, kc * qs:(kc + 1) * qs], start=(kc == 0), stop=(kc == KC - 1)),
                            reads=[R_wmq, R_tT], writes=[R_pb[bq]])
                P.op("act", _call("activation", out=qmT[:, 0:4 * qs], in_=pb[bq][:, 0:4 * qs], func=AF.Copy, scale=float(128.0 ** -0.5)),
                     reads=[R_pb[bq]], writes=[R_qm])
                yield
                bs0, bs1 = wrot.next(), wrot.next()
                for h in range(4):
                    for mt in range(2):
                        idx = h * 2 + mt
                        bank = bs0 if idx < 4 else bs1
                        c0 = (idx % 4) * qs
                        P.op("pe", _call("matmul",
                            out=pb[bank][:, c0:c0 + qs], lhsT=mkT[mi][:, h * 256 + mt * 128: h * 256 + (mt + 1) * 128],
                            rhs=qmT[:, h * qs:(h + 1) * qs], start=True, stop=True),
                            reads=[R_mk[mi], R_qm], writes=[R_pb[bank]])
                for k, bank in enumerate((bs0, bs1)):
                    P.op("act", _call("activation", out=PTm[:, k * 4 * qs:(k + 1) * 4 * qs], in_=pb[bank][:, 0:4 * qs], func=AF.Exp),
                         reads=[R_pb[bank]], writes=[R_PTm])
                yield
                bo0, bo1 = wrot.next(), wrot.next()
                for h in range(4):
                    bank = bo0 if h < 2 else bo1
                    for mt in range(2):
                        idx = h * 2 + mt
                        P.op("pe", _call("matmul",
                            out=pb[bank][0:qs, (h % 2) * 129:(h % 2) * 129 + 129], lhsT=PTm[:, idx * qs:(idx + 1) * qs],
                            rhs=mva[mi][:, (mt * 4 + h) * 129:(mt * 4 + h) * 129 + 129],
                            start=(h % 2 == 0 and mt == 0), stop=(mt == 1), skip_group_check=True),
                            reads=[R_PTm, R_mv[mi]], writes=[R_pb[bank]])
                for k, bank in enumerate((bo0, bo1)):
                    ov = pb[bank][0:qs, 0:258].rearrange("p (h d) -> p h d", d=129)
                    P.op("dve", _call("tensor_scalar", out=stat[0:qs, 20 + 2 * k:22 + 2 * k].rearrange("p (h o) -> p h o", o=1),
                                                                      in0=ov[:, :, 128:129], scalar1=1e-30, scalar2=None, op0=ALU.max),
                         reads=[R_pb[bank]], writes=[R_stat])
                    P.op("dve", _call("reciprocal", out=stat[0:qs, 20 + 2 * k:22 + 2 * k], in_=stat[0:qs, 20 + 2 * k:22 + 2 * k]),
                         reads=[R_stat], writes=[R_stat])
                    for hh in range(2):
                        h = k * 2 + hh
                        P.op("dve", _call("tensor_scalar",
                            out=o16[0:qs, h * 128:(h + 1) * 128], in0=pb[bank][0:qs, hh * 129: hh * 129 + 128],
                            scalar1=stat[0:qs, 20 + 2 * k + hh:21 + 2 * k + hh], scalar2=None, op0=ALU.mult),
                            reads=[R_pb[bank], R_stat], writes=[R_o16])
                yield
                transpose_to(o16, R_o16, qs, 4, oT, R_oT)
                yield
                b0, b1 = wrot.next(), wrot.next()
                for n, bank in enumerate((b0, b1)):
                    for c in range(4):
                        P.op("pe", _call("matmul",
                            out=pb[bank][0:qs, :], lhsT=oT[:, c * qs:(c + 1) * qs], rhs=wmob[:, c * 1024 + n * 512: c * 1024 + (n + 1) * 512],
                            start=(c == 0), stop=(c == 3)),
                            reads=[R_oT, R_wmo], writes=[R_pb[bank]])
                    P.op("dve", _call("scalar_tensor_tensor",
                        out=hA[0:qs, n * 512:(n + 1) * 512], in0=hB[0:qs, n * 512:(n + 1) * 512], scalar=ALPHA, in1=pb[bank][0:qs, :],
                        op0=ALU.mult, op1=ALU.add),
                        reads=[R_hB, R_pb[bank]], writes=[R_hA])
                yield
                layer_norm(hA, R_hA, qs, 2, hB, R_hB, stat, R_stat)
                yield
                P.dma("sp", h2D[blk * 128: blk * 128 + qs, :], hB[0:qs, :], reads=[R_hB], writes=[R_h2D[blk]], defer=True)
                P.op("act", _call("activation", out=h16[0:qs, :], in_=hB[0:qs, :], func=AF.Copy), reads=[R_hB], writes=[R_h16])
                transpose_to(h16, R_h16, qs, 8, h2T[s], R_h2T[s])
                P.dma("sp", h2TD[blk][:, 0:8 * qs], h2T[s][:, 0:8 * qs], reads=[R_h2T[s]], writes=[R_h2TD[blk]], defer=True)
                yield

            def run_staggered(gens, lag):
                active = []
                pending = list(gens)
                tick = 0
                while active or pending:
                    if pending and (not active or tick >= lag):
                        active.append(pending.pop(0))
                        tick = 0
                    for g in list(active):
                        try:
                            next(g)
                        except StopIteration:
                            active.remove(g)
                    tick += 1

            blocks = [(16, 2, 126, 0), (17, 16, 0, 1)] + [(i, 128, 0, 0) for i in range(16)]
            run_staggered([phaseB_block(b_, q_, r_, m_, pos % 4) for pos, (b_, q_, r_, m_) in enumerate(blocks)], 3)
            checkpoint('phaseB')
            P.flush(block)

        P.barrier()
        with ExitStack() as sc:
            wdb = sb(sc, "wdb", [128, NFC * 1024], BF16)
            R_wd = Res("wd")
            wst = [sb(sc, "wstc%d" % k, [128, 2048], F32) for k in range(2)]
            R_wst = [Res("wstc%d" % k) for k in range(2)]
            wsl = [sb(sc, "wsl%d" % k, [128, 2048], BF16) for k in range(2)]
            R_wsl = [Res("wsl%d" % k) for k in range(2)]
            R_wslB = [Res("wslB%d" % k) for k in range(2)]
            hT2 = [sb(sc, "hT%d" % k, [128, NFC * 512], BF16) for k in range(2)]
            R_hT2 = [Res("hT%d" % k) for k in range(2)]
            hTm = sb(sc, "hTm", [128, NFC * 16], BF16)
            R_hTm = Res("hTm")
            h2Tg = [sb(sc, "h2Tg%d" % k, [128, 8 * 512], BF16) for k in range(2)]
            R_h2Tg = [Res("h2Tg%d" % k) for k in range(2)]
            h2Tm = sb(sc, "h2Tm", [128, 8 * 18], BF16)
            R_h2Tm = Res("h2Tm")
            Gb = [sb(sc, "Gb%d" % k, [128, 514], F32) for k in range(3)]
            R_Gb = [Res("Gb%d" % k) for k in range(3)]
            Gs = sb(sc, "Gs", [128, 18], F32)
            R_Gs = Res("Gs")
            t0b = [sb(sc, "t0b%d" % k, [128, 512], F32) for k in range(3)]
            R_t0 = [Res("t0%d" % k) for k in range(3)]
            geb = [sb(sc, "geb%d" % k, [128, 512], F32) for k in range(3)]
            R_ge = [Res("ge%d" % k) for k in range(3)]
            t1b = [sb(sc, "t1b%d" % k, [128, 512], F32) for k in range(3)]
            R_t1b = [Res("t1b%d" % k) for k in range(3)]
            t2b = [sb(sc, "t2b%d" % k, [128, 512], F32) for k in range(3)]
            R_t2b = [Res("t2b%d" % k) for k in range(3)]
            t0s = sb(sc, "t0s", [128, 16], F32)
            ges = sb(sc, "ges", [128, 16], F32)
            R_ts = Res("ts")
            carry = sb(sc, "carry", [128, NFC * 2], F32)
            R_carry = [Res("carry%d" % c) for c in range(NFC)]
            sfc = sb(sc, "sfc", [128, NFC * 2], F32)
            R_sfc = Res("sfc")
            sconv = sb(sc, "sconv", [128, NFC * 2], F32)
            wconv = sb(sc, "wconv", [128, NFC * 3], F32)
            bconv = sb(sc, "bconv", [128, NFC], F32)
            flag = sb(sc, "flag", [128, 1], F32)
            R_cc = Res("cc")
            ln3 = sb(sc, "ln3", [128, 2 * 1024], F32)
            R_ln3 = Res("ln3")
            h2r = [sb(sc, "h2r%d" % k, [128, 1024], F32) for k in range(2)]
            R_h2r = [Res("h2r%d" % k) for k in range(2)]
            yA = sb(sc, "yA", [128, 1024], F32)
            R_yA = Res("yA")
            yB = [sb(sc, "yB%d" % k, [128, 1024], F32) for k in range(2)]
            R_yB = [Res("yB%d" % k) for k in range(2)]
            stat = sb(sc, "statc", [128, 32], F32)
            R_stat = Res("statc")

            P.dma("sp", sconv[:, :], I["sconvT"][:, :], writes=[R_cc])
            P.dma("sp", wconv[:, :], I["wconvT"][:, :], writes=[R_cc])
            P.dma("sp", bconv[:, :], I["bconvT"][:, :], writes=[R_cc])
            P.dma("sp", flag[:, :], I["flag"][:, :], writes=[R_cc])
            for k in range(2):
                P.dma("sp", ln3[:, k * 1024:(k + 1) * 1024], I["lnp"][4 + k:5 + k, :].to_broadcast([128, 1024]), writes=[R_ln3])
            k = 0
            for c0 in range(0, NFC * 1024, 2048):
                s = k % 2
                P.dma("sp", wst[s][:, :], I["wdown"][:, c0:c0 + 2048], writes=[R_wst[s]])
                if k % 2 == 0:
                    P.op("act", _call("activation", out=wdb[:, c0:c0 + 2048], in_=wst[s][:, :], func=AF.Copy), reads=[R_wst[s]], writes=[R_wd])
                else:
                    P.op("pool", _call("tensor_copy", out=wdb[:, c0:c0 + 2048], in_=wst[s][:, :]), reads=[R_wst[s]], writes=[R_wd])
                k += 1
            P.dma("sp", h2Tm[:, :].rearrange("p (c q) -> p c q", q=18)[:, :, 0:2], h2TD[16][:, 0:16].rearrange("p (c q) -> p c q", q=2),
                  reads=[R_h2TD[16]], writes=[R_h2Tm], slow=True)
            P.dma("sp", h2Tm[:, :].rearrange("p (c q) -> p c q", q=18)[:, :, 2:18], h2TD[17][:, 0:128].rearrange("p (c q) -> p c q", q=16),
                  reads=[R_h2TD[17]], writes=[R_h2Tm], slow=True)

            checkpoint('phaseC_pre')
            UB = [0, 2, 4]
            GBK = [1, 3, 5]
            MB = 7
            YB = [6, 7]
            wk = [0]

            def ln3_out(pre_banks, qs, h2src, R_h2src, dst_ap, ys, R_ys):
                for n, bank in enumerate(pre_banks):
                    P.op("dve", _call("scalar_tensor_tensor",
                        out=yA[0:qs, n * 512:(n + 1) * 512], in0=h2src[0:qs, n * 512:(n + 1) * 512], scalar=ALPHA, in1=pb[bank][0:qs, :],
                        op0=ALU.mult, op1=ALU.add),
                        reads=[R_h2src, R_pb[bank]], writes=[R_yA])
                for c in range(2):
                    P.op("dve", _call("bn_stats", out=stat[0:qs, c * 6:(c + 1) * 6], in_=yA[0:qs, c * 512:(c + 1) * 512]),
                         reads=[R_yA], writes=[R_stat])
                P.op("dve", _call("bn_aggr", out=stat[0:qs, 12:14], in_=stat[0:qs, 0:12]), reads=[R_stat], writes=[R_stat])
                P.op("dve", _call("tensor_scalar", out=stat[0:qs, 14:15], in0=stat[0:qs, 13:14], scalar1=LN_EPS, scalar2=None, op0=ALU.add),
                     reads=[R_stat], writes=[R_stat])
                P.op("act", _call("activation", out=stat[0:qs, 15:16], in_=stat[0:qs, 14:15], func=AF.Sqrt), reads=[R_stat], writes=[R_stat])
                P.op("dve", _call("reciprocal", out=stat[0:qs, 16:17], in_=stat[0:qs, 15:16]), reads=[R_stat], writes=[R_stat])
                P.op("dve", _call("scalar_tensor_tensor", out=stat[0:qs, 17:18], in0=stat[0:qs, 12:13], scalar=-1.0, in1=stat[0:qs, 16:17],
                                                             op0=ALU.mult, op1=ALU.mult),
                     reads=[R_stat], writes=[R_stat])
                P.op("act", _call("activation", out=ys[0:qs, :], in_=yA[0:qs, :], func=AF.Identity, scale=stat[0:qs, 16:17], bias=stat[0:qs, 17:18]),
                     reads=[R_yA, R_stat], writes=[R_ys])
                P.op("pool", _call("tensor_tensor", out=ys[0:qs, :], in0=ys[0:qs, :], in1=ln3[0:qs, 0:1024], op=ALU.mult),
                     reads=[R_ys, R_ln3], writes=[R_ys])
                P.op("pool", _call("tensor_tensor", out=ys[0:qs, :], in0=ys[0:qs, :], in1=ln3[0:qs, 1024:2048], op=ALU.add),
                     reads=[R_ys, R_ln3], writes=[R_ys])
                P.dma("sp", dst_ap, ys[0:qs, :], reads=[R_ys], defer=True)

            def load_h2Tg(grp):
                gs = grp % 2
                for bi in range(4):
                    blk = grp * 4 + bi
                    P.dma("sp", h2Tg[gs][:, :].rearrange("p (c q) -> p c q", q=512)[:, :, bi * 128:(bi + 1) * 128],
                          h2TD[blk][:, :].rearrange("p (c q) -> p c q", q=128), reads=[R_h2TD[blk]], writes=[R_h2Tg[gs]])

            def c_s1(grp, c):
                s = (grp * NFC + c) % 2
                P.dma("sp", wst[s][:, :], I["wup"][c], writes=[R_wst[s]])
                P.op("dve", _call("tensor_copy", out=wsl[s][:, 0:1024], in_=wst[s][:, 0:1024]), reads=[R_wst[s]], writes=[R_wsl[s]])
                P.op("dve", _call("tensor_copy", out=wsl[s][:, 1024:2048], in_=wst[s][:, 1024:2048]), reads=[R_wst[s]], writes=[R_wslB[s]])

            def c_s2(grp, c):
                s = (grp * NFC + c) % 2
                gs = grp % 2
                mo = (c % 2) * 64
                if grp == 0:
                    for part, oc in ((0, mo), (1, mo + 32)):
                        for kc in range(KC):
                            P.op("pe", _call("matmul", out=pb[MB][:, oc:oc + 18], lhsT=wsl[s][:, kc * 256 + part * 128: kc * 256 + (part + 1) * 128],
                                             rhs=h2Tm[:, kc * 18:(kc + 1) * 18], start=(kc == 0), stop=(kc == KC - 1)),
                                 reads=[R_wsl[s], R_wslB[s], R_h2Tm], writes=[R_pb[MB]])
                k3 = (grp * NFC + c) % 3
                ub, gbk = UB[k3], GBK[k3]
                for part, bank in ((0, ub), (1, gbk)):
                    for kc in range(KC):
                        P.op("pe", _call("matmul", out=pb[bank][:, :], lhsT=wsl[s][:, kc * 256 + part * 128: kc * 256 + (part + 1) * 128],
                                         rhs=h2Tg[gs][:, kc * 512:(kc + 1) * 512], start=(kc == 0), stop=(kc == KC - 1)),
                             reads=[R_wsl[s], R_wslB[s], R_h2Tg[gs]], writes=[R_pb[bank]])

            def c_s3(grp, c):
                hTg, R_hTg = hT2[grp % 2], R_hT2[grp % 2]
                mo = (c % 2) * 64
                if grp == 0:
                    P.op("dve", _call("tensor_scalar", out=carry[:, c * 2:(c + 1) * 2], in0=pb[MB][:, mo + 32:mo + 34], scalar1=flag[:, 0:1],
                                      scalar2=None, op0=ALU.mult),
                         reads=[R_pb[MB], R_cc], writes=[R_carry[c]])
                    P.op("act", _call("activation", out=Gs[:, 0:2], in_=sconv[:, c * 2:(c + 1) * 2], func=AF.Copy), reads=[R_cc], writes=[R_Gs])
                    P.op("act", _call("activation", out=Gs[:, 2:18], in_=pb[MB][:, mo + 34:mo + 50], func=AF.Copy), reads=[R_pb[MB]], writes=[R_Gs])
                    P.op("act", _call("activation", out=t0s[:, :], in_=Gs[:, 2:18], func=AF.Identity, scale=wconv[:, c * 3 + 2:c * 3 + 3],
                                      bias=bconv[:, c:c + 1]),
                         reads=[R_Gs, R_cc], writes=[R_ts])
                    P.op("dve", _call("scalar_tensor_tensor", out=t0s[:, :], in0=Gs[:, 1:17], scalar=wconv[:, c * 3 + 1:c * 3 + 2], in1=t0s[:, :],
                                      op0=ALU.mult, op1=ALU.add),
                         reads=[R_Gs, R_cc, R_ts], writes=[R_ts])
                    P.op("dve", _call("scalar_tensor_tensor", out=t0s[:, :], in0=Gs[:, 0:16], scalar=wconv[:, c * 3:c * 3 + 1], in1=t0s[:, :],
                                      op0=ALU.mult, op1=ALU.add),
                         reads=[R_Gs, R_cc, R_ts], writes=[R_ts])
                    P.op("act", _call("activation", out=ges[:, :], in_=t0s[:, :], func=AF.Gelu_apprx_tanh), reads=[R_ts], writes=[R_ts])
                    P.op("dve", _call("tensor_tensor", out=hTm[:, c * 16:(c + 1) * 16], in0=pb[MB][:, mo + 2:mo + 18], in1=ges[:, :], op=ALU.mult),
                         reads=[R_pb[MB], R_ts], writes=[R_hTm])
                    P.op("act", _call("activation", out=sfc[:, c * 2:(c + 1) * 2], in_=Gs[:, 16:18], func=AF.Copy), reads=[R_Gs], writes=[R_sfc])
                k3 = (grp * NFC + c) % 3
                ub, gbk = UB[k3], GBK[k3]
                G, R_G = Gb[k3], R_Gb[k3]
                t0, R_t = t0b[k3], R_t0[k3]
                ge, R_g = geb[k3], R_ge[k3]
                t1, R_t1 = t1b[k3], R_t1b[k3]
                t2, R_t2 = t2b[k3], R_t2b[k3]
                P.op("act", _call("activation", out=G[:, 0:2], in_=carry[:, c * 2:(c + 1) * 2], func=AF.Copy),
                     reads=[R_carry[c]], writes=[R_G])
                P.op("act", _call("activation", out=G[:, 2:514], in_=pb[gbk][:, :], func=AF.Copy), reads=[R_pb[gbk]], writes=[R_G])
                P.op("act", _call("activation", out=carry[:, c * 2:(c + 1) * 2], in_=G[:, 512:514], func=AF.Copy),
                     reads=[R_G], writes=[R_carry[c]])
                P.op("act", _call("activation", out=t0[:, :], in_=G[:, 2:514], func=AF.Identity,
                                  scale=wconv[:, c * 3 + 2:c * 3 + 3], bias=bconv[:, c:c + 1]),
                     reads=[R_G, R_cc], writes=[R_t])
                P.op("act", _call("activation", out=t1[:, :], in_=G[:, 1:513], func=AF.Identity, scale=wconv[:, c * 3 + 1:c * 3 + 2]),
                     reads=[R_G, R_cc], writes=[R_t1])
                P.op("act", _call("activation", out=t2[:, :], in_=G[:, 0:512], func=AF.Identity, scale=wconv[:, c * 3:c * 3 + 1]),
                     reads=[R_G, R_cc], writes=[R_t2])
                P.op("dve", _call("tensor_tensor", out=t0[:, :], in0=t0[:, :], in1=t1[:, :], op=ALU.add), reads=[R_t, R_t1], writes=[R_t])
                P.op("dve", _call("tensor_tensor", out=t0[:, :], in0=t0[:, :], in1=t2[:, :], op=ALU.add), reads=[R_t, R_t2], writes=[R_t])
                P.op("act", _call("activation", out=ge[:, :], in_=t0[:, :], func=AF.Gelu_apprx_tanh), reads=[R_t], writes=[R_g])
                P.op("dve", _call("tensor_tensor", out=hTg[:, c * 512:(c + 1) * 512], in0=pb[ub][:, :], in1=ge[:, :], op=ALU.mult),
                     reads=[R_pb[ub], R_g], writes=[R_hTg])

            def c_down(grp):
                hTg, R_hTg = hT2[grp % 2], R_hT2[grp % 2]
                if grp == 0:
                    for n, bank in enumerate(YB):
                        for c in range(NFC):
                            P.op("pe", _call("matmul", out=pb[bank][0:16, :], lhsT=hTm[:, c * 16:(c + 1) * 16],
                                             rhs=wdb[:, c * 1024 + n * 512: c * 1024 + (n + 1) * 512], start=(c == 0), stop=(c == NFC - 1)),
                                 reads=[R_hTm, R_wd], writes=[R_pb[bank]])
                    P.dma("sp", h2r[0][0:16, :], h2D[17 * 128: 17 * 128 + 16, :], reads=[R_h2D[17]], writes=[R_h2r[0]])
                    ln3_out(YB, 16, h2r[0], R_h2r[0], O["ys"][:, :], yB[0], R_yB[0])
                    P.dma("sp", O["sfcT"][:, :], sfc[:, :], reads=[R_sfc], defer=True)
                for bi in range(4):
                    blk = grp * 4 + bi
                    hs = blk % 2
                    P.dma("sp", h2r[hs][:, :], h2D[blk * 128:(blk + 1) * 128, :], reads=[R_h2D[blk]], writes=[R_h2r[hs]])
                    for n, bank in enumerate(YB):
                        for c in range(NFC):
                            P.op("pe", _call("matmul", out=pb[bank][:, :], lhsT=hTg[:, c * 512 + bi * 128: c * 512 + (bi + 1) * 128],
                                             rhs=wdb[:, c * 1024 + n * 512: c * 1024 + (n + 1) * 512], start=(c == 0), stop=(c == NFC - 1)),
                                 reads=[R_hTg, R_wd], writes=[R_pb[bank]])
                    ln3_out(YB, 128, h2r[hs], R_h2r[hs], O["y"][blk * 128:(blk + 1) * 128, :], yB[hs], R_yB[hs])

            seq = [(grp, c) for grp in range(4) for c in range(NFC)]
            nseq = len(seq)
            load_h2Tg(0)
            load_h2Tg(1)
            for idx in range(nseq + 2):
                if idx < nseq:
                    c_s1(*seq[idx])
                if 1 <= idx <= nseq:
                    c_s2(*seq[idx - 1])
                if idx >= 2:
                    g3, c3 = seq[idx - 2]
                    c_s3(g3, c3)
                    if c3 == NFC - 1:
                        c_down(g3)
                        if g3 + 2 < 4:
                            load_h2Tg(g3 + 2)
            P.dma("sp", O["fcT"][:, :], carry[:, :], reads=R_carry, defer=True)
            P.finish()
            P.flush(block)
    return nc


def _t5_bucket(rel):
    half, max_exact = 16, 8
    n = np.abs(rel)
    log_ratio = np.log(np.maximum(n, 1).astype(np.float32) / max_exact) / math.log(128 / max_exact)
    large = np.minimum(max_exact + (log_ratio * (half - max_exact)).astype(np.int32), half - 1)
    return np.where(rel < 0, half, 0) + np.where(n < max_exact, n, large)


def _host_inputs(inp):
    f32 = np.float32
    x_prompt = np.asarray(inp["x_prompt"], f32)
    x_sample = np.asarray(inp["x_sample"], f32)
    w_in = np.asarray(inp["w_in"], f32)[0]
    qa, ka, va = w_in[:, 0:512], w_in[:, 512:1024], w_in[:, 1024:1536]
    qb, kb, vb = w_in[:, 1536:2048], w_in[:, 2048:2176], w_in[:, 2176:2304]
    qi, ki, wi = w_in[:, 2304:2816], w_in[:, 2816:2880], w_in[:, 2880:2888]
    qbp = np.concatenate([np.concatenate([qb[:, r * 64:(r + 1) * 64], qb[:, (4 + r) * 64:(5 + r) * 64]], axis=1) for r in range(4)], axis=1)
    winp = np.concatenate([qa, ka, qbp, kb, qi, ki, ki, va, vb, wi], axis=1)
    assert winp.shape[1] == NCOL

    def kc_layout(w):
        n = w.shape[1]
        return np.ascontiguousarray(w.reshape(8, 128, n).transpose(1, 0, 2).reshape(128, 8 * n))

    shared = {}
    shared["win"] = kc_layout(winp)
    shared["wo"] = kc_layout(np.asarray(inp["w_o"], f32)[0])
    shared["wmq"] = kc_layout(np.asarray(inp["w_mq"], f32)[0])
    shared["wmk"] = kc_layout(np.asarray(inp["w_mk"], f32)[0])
    shared["wmv"] = kc_layout(np.asarray(inp["w_mv"], f32)[0])
    wmo = np.asarray(inp["w_mo"], f32)[0]
    shared["wmo"] = np.ascontiguousarray(wmo.reshape(4, 128, 1024).transpose(1, 0, 2).reshape(128, 4096))
    w_up = np.asarray(inp["w_up"], f32)[0]
    wu = w_up[:, :DFF].reshape(8, 128, NFC, 128)
    wg = w_up[:, DFF:].reshape(8, 128, NFC, 128)
    wup = np.stack([wu, wg], axis=3)
    shared["wup"] = np.ascontiguousarray(wup.transpose(2, 1, 0, 3, 4).reshape(NFC, 128, 8 * 256))
    w_down = np.asarray(inp["w_down"], f32)[0]
    shared["wdown"] = np.ascontiguousarray(w_down.reshape(NFC, 128, 1024).transpose(1, 0, 2).reshape(128, NFC * 1024))
    shared["lnp"] = np.ascontiguousarray(np.stack([np.asarray(inp[k], f32)[0] for k in ("ln1_g", "ln1_b", "ln2_g", "ln2_b", "ln3_g", "ln3_b")]))
    w_conv = np.asarray(inp["w_conv"], f32)[0]
    shared["wconvT"] = np.ascontiguousarray(w_conv.reshape(3, NFC, 128).transpose(2, 1, 0).reshape(128, NFC * 3))
    shared["bconvT"] = np.ascontiguousarray(np.asarray(inp["b_conv"], f32)[0].reshape(NFC, 128).T)
    shared["ident"] = np.eye(128, dtype=f32)
    tabA = np.asarray(inp["a_rel_bias"], f32)[0]
    qq = np.arange(128)[:, None]
    kk = np.arange(640)[None, :]
    kpos = kk - 512
    rel = qq - kpos
    cq = qq // 64
    kch = np.floor_divide(kpos, 64)
    allowed = (kch >= cq - 8) & (kch <= cq)
    bias = tabA[np.clip(rel, -64, 64) + 64]
    AB = np.where(allowed[:, :, None], bias, f32(NEGM)).astype(f32)
    shared["AB"] = np.ascontiguousarray(AB.transpose(0, 2, 1).reshape(128, 8 * ABW))
    js = np.arange(16)[:, None]
    ks = np.arange(528)[None, :]
    ABs = tabA[np.clip(512 + js - ks, -64, 64) + 64]
    shared["ABs"] = np.ascontiguousarray(ABs.transpose(0, 2, 1).reshape(16, 8 * 528)).astype(f32)
    t5 = np.asarray(inp["t5_bias"], f32)
    relB = np.arange(128)[:, None] - np.arange(256)[None, :] + 128
    Bn = t5[_t5_bucket(relB)]
    shared["Bn"] = np.ascontiguousarray(Bn.transpose(0, 2, 1).reshape(128, 8 * BNW)).astype(f32)
    relBs = 128 + np.arange(16)[:, None] - np.arange(144)[None, :]
    Bns = t5[_t5_bucket(relBs)]
    shared["Bns"] = np.ascontiguousarray(Bns.transpose(0, 2, 1).reshape(16, 8 * 144)).astype(f32)
    shared["C15"] = np.ascontiguousarray(np.broadcast_to(t5[15][None, :], (128, 8))).astype(f32)
    dm = np.zeros((128, 128), f32)
    dm[0:64, 64:128] = NEGM
    shared["diagmask"] = dm

    mem_prompt = np.asarray(inp["mem_prompt"], f32)
    maps = []
    for c in range(8):
        b, half = c // 2, c % 2
        m = dict(shared)
        xk = np.zeros((4096, 1024), f32)
        if half == 1:
            xk[:] = x_prompt[b]
        else:
            xk[2048:] = x_prompt[b, :2048]
        m["xkT"] = np.ascontiguousarray(xk.reshape(32, 128, 8, 128).transpose(0, 3, 2, 1).reshape(32, 128, 1024))
        xs = x_sample[c]
        m["xsT"] = np.ascontiguousarray(xs.reshape(16, 8, 128).transpose(2, 1, 0).reshape(128, 128))
        xres = np.zeros((NBLK * 128, 1024), f32)
        xres[0:2048] = xk[2048:]
        xres[2048:2050] = xk[2046:2048]
        xres[17 * 128:17 * 128 + 16] = xs
        m["xres"] = xres
        m["memT"] = np.ascontiguousarray(mem_prompt[b].reshape(256, 8, 128).transpose(2, 1, 0).reshape(128, 2048))
        cmk = np.asarray(inp["cache_mem_k"], f32)[0, c]
        m["cmkT"] = np.ascontiguousarray(cmk.transpose(2, 1, 0).reshape(128, 1024))
        m["cmv"] = np.ascontiguousarray(np.asarray(inp["cache_mem_v"], f32)[0, c].reshape(256, 512))
        cak = np.asarray(inp["cache_a_k"], f32)[0, c]
        m["cakT"] = np.ascontiguousarray(cak.reshape(512, 4, 2, 64).transpose(2, 3, 1, 0).reshape(128, 2048))
        m["cav"] = np.ascontiguousarray(np.asarray(inp["cache_a_v"], f32)[0, c].reshape(512, 512))
        cbk = np.asarray(inp["cache_b_k"], f32)[0, c]
        m["cbkT"] = np.ascontiguousarray(cbk.reshape(2048, 128).T)
        m["cbv"] = np.ascontiguousarray(np.asarray(inp["cache_b_v"], f32)[0, c].reshape(2048, 128))
        cbi = np.asarray(inp["cache_b_kidx"], f32)[0, c]
        m["cbiT"] = np.ascontiguousarray(np.concatenate([cbi.T, cbi.T], axis=0))
        sc_ = np.asarray(inp["state_ffn_conv"], f32)[0, c]
        m["sconvT"] = np.ascontiguousarray(sc_.reshape(2, NFC, 128).transpose(2, 1, 0).reshape(128, NFC * 2))
        m["colmask"] = np.full((128, 1), NEGM if half == 0 else 0.0, f32)
        kv = np.ones((128, NT), f32)
        if half == 0:
            kv[:, 0:16] = 0.0
        m["kvalid"] = kv
        m["flag"] = np.full((128, 1), float(half), f32)
        maps.append(m)
    return maps


_NC_CACHE = {}


def _run(inputs, debug=False):
    key = bool(debug)
    if key not in _NC_CACHE:
        _NC_CACHE[key] = build_program(debug=debug)
    nc = _NC_CACHE[key]
    maps = _host_inputs(inputs)
    res = run_bass_kernel_spmd(nc, maps, core_ids=list(range(8)))
    return res.results


def kernel(**inputs):
    R = _run(inputs)
    f32 = np.float32
    y = np.zeros((4, 4096, 1024), f32)
    ys = np.zeros((8, 16, 1024), f32)
    pak = np.zeros((1, 4, 512, 8, 64), f32)
    pav = np.zeros((1, 4, 512, 8, 64), f32)
    pbk = np.zeros((1, 4, 4096, 2, 64), f32)
    pbv = np.zeros((1, 4, 4096, 2, 64), f32)
    pbi = np.zeros((1, 4, 4096, 64), f32)
    pmk = np.zeros((1, 4, 256, 4, 128), f32)
    pmv = np.zeros((1, 4, 256, 4, 128), f32)
    pfc = np.zeros((1, 4, 2, DFF), f32)
    sak = np.zeros((1, 8, 16, 8, 64), f32)
    sav = np.zeros((1, 8, 16, 8, 64), f32)
    sbk = np.zeros((1, 8, 16, 2, 64), f32)
    sbv = np.zeros((1, 8, 16, 2, 64), f32)
    sbi = np.zeros((1, 8, 16, 64), f32)
    sfc = np.zeros((1, 8, 2, DFF), f32)
    for c in range(8):
        b, half = c // 2, c % 2
        r = R[c]
        y[b, half * 2048:(half + 1) * 2048] = np.asarray(r["y"], f32)
        ys[c] = np.asarray(r["ys"], f32)
        if half == 1:
            akT = np.asarray(r["akT"], f32).reshape(2, 64, 4, 512)
            pak[0, b] = akT.transpose(3, 2, 0, 1).reshape(512, 8, 64)
            pav[0, b] = np.asarray(r["av"], f32).reshape(512, 8, 64)
            pbk[0, b] = np.asarray(r["bkT"], f32).T.reshape(4096, 2, 64)
            pbv[0, b] = np.asarray(r["bv"], f32).reshape(4096, 2, 64)
            pbi[0, b] = np.asarray(r["biT"], f32).T
            pmk[0, b] = np.asarray(r["mkT"], f32).reshape(128, 4, 256).transpose(2, 1, 0)
            pmv[0, b] = np.asarray(r["mv"], f32).reshape(256, 4, 128)
            pfc[0, b] = np.asarray(r["fcT"], f32).reshape(128, NFC, 2).transpose(2, 1, 0).reshape(2, DFF)
        sakT = np.asarray(r["sakT"], f32).reshape(2, 64, 4, 16)
        sak[0, c] = sakT.transpose(3, 2, 0, 1).reshape(16, 8, 64)
        sav[0, c] = np.asarray(r["sav"], f32).reshape(16, 8, 64)
        sbk[0, c] = np.asarray(r["sbkT"], f32).T.reshape(16, 2, 64)
        sbv[0, c] = np.asarray(r["sbv"], f32).reshape(16, 2, 64)
        sbi[0, c] = np.asarray(r["sbiT"], f32).T
        sfc[0, c] = np.asarray(r["sfcT"], f32).reshape(128, NFC, 2).transpose(2, 1, 0).reshape(2, DFF)
    return (y, ys, pak, pav, pbk, pbv, pbi, pmk, pmv, pfc, sak, sav, sbk, sbv, sbi, sfc)
```

```python
import math
from contextlib import ExitStack

import numpy as np
import concourse.bass as bass
import concourse.mybir as mybir
from concourse.bass_utils import run_bass_kernel_spmd

F32 = mybir.dt.float32
BF16 = mybir.dt.bfloat16
AF = mybir.ActivationFunctionType
ALU = mybir.AluOpType

D = 1024
KC = 8
NT = 32
NCOL = 2952
C_QA, C_KA, C_QB, C_KB, C_QI, C_KI, C_VA, C_VB, C_WI = 0, 512, 1024, 1536, 1664, 2176, 2304, 2816, 2944
DFF = 2816
NFC = 22
ALPHA = 2.0 ** 0.25
LN_EPS = 1e-5
NEGM = -30000.0
NIT = 17
BIS_W0 = 16.0
ABW = 640
BNW = 256
NBLK = 18


class Res:
    __slots__ = ("lw", "rd", "name", "excl")

    def __init__(self, name="", excl=False):
        self.lw = None
        self.rd = {}
        self.name = name
        self.excl = excl


def _call(name, *args, **kw):
    return lambda e: getattr(e, name)(*args, **kw)


class Prog:
    ENG = ("pe", "act", "dve", "pool", "sp")

    def __init__(self, nc, sems, dma_sems):
        self.nc = nc
        self.streams = {e: [] for e in self.ENG}
        self.sem = sems
        self.cnt = {e: 0 for e in self.ENG}
        self.seen = {e: {} for e in self.ENG}
        self.dsems = dma_sems
        self.dval = [0] * len(dma_sems)
        self.dnext = 0
        self.semh = dict(sems)
        for i, h in enumerate(dma_sems):
            self.semh[("d", i)] = h
        self.ninst = 0
        self.dead = False
        self.deferred = []
        self.defer_lag = 48

    def _deps(self, reads, writes, eng=None):
        d = {}
        for r in reads:
            if r.lw is not None:
                k, v = r.lw
                if d.get(k, 0) < v:
                    d[k] = v
            if r.excl:
                for k, v in r.rd.items():
                    if k != eng and d.get(k, 0) < v:
                        d[k] = v
        for w in writes:
            if w.lw is not None:
                k, v = w.lw
                if d.get(k, 0) < v:
                    d[k] = v
            for k, v in w.rd.items():
                if d.get(k, 0) < v:
                    d[k] = v
        return d

    def _wait(self, eng, deps):
        for k, v in deps.items():
            if k == "pe" and eng == "pe":
                continue
            if self.seen[eng].get(k, 0) >= v:
                continue
            self.seen[eng][k] = v
            h = self.semh[k]
            self.streams[eng].append(lambda e, h=h, v=v: e.wait_ge(h, v))

    def _flush_deferred(self, force=False, reads=(), writes=()):
        if not self.deferred:
            return
        conflict = force
        if not conflict:
            ws = set(id(w) for w in writes)
            rs = set(id(r) for r in reads)
            for d in self.deferred:
                dr = set(id(x) for x in d[3])
                dw = set(id(x) for x in d[4])
                if (ws & dr) or (ws & dw) or (rs & dw):
                    conflict = True
                    break
        if conflict:
            pend, self.deferred = self.deferred, []
            for d in pend:
                self._dma_now(d[0], d[1], d[2], d[3], d[4], d[5])
            return
        while self.deferred and self.ninst - self.deferred[0][6] >= self.defer_lag:
            d = self.deferred.pop(0)
            self._dma_now(d[0], d[1], d[2], d[3], d[4], d[5])

    def op(self, eng, fn, reads=(), writes=()):
        if self.dead:
            return
        self._flush_deferred(False, reads, writes)
        self._wait(eng, self._deps(reads, writes, eng))
        self.cnt[eng] += 1
        n = self.cnt[eng]
        h = self.sem[eng]
        self.streams[eng].append(lambda e, fn=fn, h=h: fn(e).then_inc(h, 1))
        self.ninst += 1
        for r in reads:
            if r.rd.get(eng, 0) < n:
                r.rd[eng] = n
        for w in writes:
            w.lw = (eng, n)
            w.rd = {}

    def dma(self, q, out, in_, reads=(), writes=(), slow=False, defer=False):
        if self.dead:
            return
        if defer:
            self._flush_deferred(False, reads, writes)
            self.deferred.append((q, out, in_, list(reads), list(writes), slow, self.ninst))
            return
        self._flush_deferred(False, reads, writes)
        self._dma_now(q, out, in_, reads, writes, slow)

    def _dma_now(self, q, out, in_, reads=(), writes=(), slow=False):
        deps = self._deps(reads, writes)
        i = self.dnext
        self.dnext = (i + 1) % len(self.dsems)
        k = ("d", i)
        if self.dval[i] > 0 and deps.get(k, 0) < self.dval[i]:
            deps[k] = self.dval[i]
        self._wait(q, deps)
        self.dval[i] += 16
        v = self.dval[i]
        h = self.dsems[i]
        if slow:
            self.streams[q].append(
                lambda e, out=out, in_=in_, h=h: e.dma_start(out=out, in_=in_, allow_slow_non_contiguous=True).then_inc(h, 16))
        else:
            self.streams[q].append(lambda e, out=out, in_=in_, h=h: e.dma_start(out=out, in_=in_).then_inc(h, 16))
        self.ninst += 1
        for r in reads:
            if r.rd.get(k, 0) < v:
                r.rd[k] = v
        for w in writes:
            w.lw = (k, v)
            w.rd = {}

    def barrier(self):
        if self.dead:
            return
        self._flush_deferred(True)
        deps = {e: self.cnt[e] for e in self.ENG if self.cnt[e] > 0}
        for i, v in enumerate(self.dval):
            if v > 0:
                deps[("d", i)] = v
        for e in self.ENG:
            self._wait(e, dict(deps))

    def finish(self):
        self._flush_deferred(True)
        deps = {("d", i): v for i, v in enumerate(self.dval) if v > 0}
        self._wait("sp", deps)

    def flush(self, block):
        self._flush_deferred(True)
        s = self.streams
        self.streams = {e: [] for e in self.ENG}

        def mk(lst):
            def body(e):
                for f in lst:
                    f(e)
            return body

        block.tensor(mk(s["pe"]))
        block.scalar(mk(s["act"]))
        block.vector(mk(s["dve"]))
        block.gpsimd(mk(s["pool"]))
        block.sync(mk(s["sp"]))


def build_program(debug=False, stop_at=None):
    nc = bass.Bass("TRN2", target_bir_lowering=False)

    def din(name, shape, dt=F32):
        return nc.dram_tensor(name, list(shape), dt, kind="ExternalInput").ap()

    def dout(name, shape, dt=F32):
        return nc.dram_tensor(name, list(shape), dt, kind="ExternalOutput").ap()

    def dscr(name, shape, dt):
        return nc.dram_tensor(name, list(shape), dt, kind="Internal").ap()

    I = {}
    I["xkT"] = din("xkT", [NT, 128, 1024])
    I["xsT"] = din("xsT", [128, 8 * 16])
    I["xres"] = din("xres", [NBLK * 128, 1024])
    I["win"] = din("win", [128, KC * NCOL])
    I["wo"] = din("wo", [128, 8 * 1024])
    I["wmq"] = din("wmq", [128, 8 * 512])
    I["wmk"] = din("wmk", [128, 8 * 512])
    I["wmv"] = din("wmv", [128, 8 * 512])
    I["wmo"] = din("wmo", [128, 4 * 1024])
    I["wup"] = din("wup", [NFC, 128, 8 * 256])
    I["wdown"] = din("wdown", [128, NFC * 1024])
    I["lnp"] = din("lnp", [6, 1024])
    I["wconvT"] = din("wconvT", [128, NFC * 3])
    I["bconvT"] = din("bconvT", [128, NFC])
    I["memT"] = din("memT", [128, 8 * 256])
    I["cmkT"] = din("cmkT", [128, 4 * 256])
    I["cmv"] = din("cmv", [256, 512])
    I["cakT"] = din("cakT", [128, 4 * 512])
    I["cav"] = din("cav", [512, 512])
    I["cbkT"] = din("cbkT", [128, 2048])
    I["cbv"] = din("cbv", [2048, 128])
    I["cbiT"] = din("cbiT", [128, 2048])
    I["sconvT"] = din("sconvT", [128, NFC * 2])
    I["ident"] = din("ident", [128, 128])
    I["AB"] = din("AB", [128, 8 * ABW])
    I["ABs"] = din("ABs", [16, 8 * 528])
    I["Bn"] = din("Bn", [128, 8 * BNW])
    I["Bns"] = din("Bns", [16, 8 * 144])
    I["C15"] = din("C15", [128, 8])
    I["colmask"] = din("colmask", [128, 1])
    I["diagmask"] = din("diagmask", [128, 128])
    I["kvalid"] = din("kvalid", [128, NT])
    I["flag"] = din("flag", [128, 1])

    O = {}
    O["y"] = dout("y", [2048, 1024])
    O["ys"] = dout("ys", [16, 1024])
    O["akT"] = dout("akT", [128, 4 * 512])
    O["av"] = dout("av", [512, 512])
    O["bkT"] = dout("bkT", [128, 4096])
    O["bv"] = dout("bv", [4096, 128])
    O["biT"] = dout("biT", [64, 4096])
    O["mkT"] = dout("mkT", [128, 4 * 256])
    O["mv"] = dout("mv", [256, 512])
    O["fcT"] = dout("fcT", [128, NFC * 2])
    O["sakT"] = dout("sakT", [128, 4 * 16])
    O["sav"] = dout("sav", [16, 512])
    O["sbkT"] = dout("sbkT", [128, 16])
    O["sbv"] = dout("sbv", [16, 128])
    O["sbiT"] = dout("sbiT", [64, 16])
    O["sfcT"] = dout("sfcT", [128, NFC * 2])
    if debug:
        O["dbg_mix"] = dout("dbg_mix", [NBLK * 128, 1024], BF16)
        O["dbg_h2"] = dout("dbg_h2", [NBLK * 128, 1024])
        mixD = O["dbg_mix"]
        h2D = O["dbg_h2"]
    else:
        mixD = dscr("mixD", [NBLK * 128, 1024], BF16)
        h2D = dscr("h2D", [NBLK * 128, 1024], F32)
    h2TD = dscr("h2TD", [NBLK, 128, 1024], BF16)
    R_mixD = [Res("mixD%d" % i) for i in range(NBLK)]
    R_h2D = [Res("h2D%d" % i) for i in range(NBLK)]
    R_h2TD = [Res("h2TD%d" % i) for i in range(NBLK)]

    es = ExitStack()
    with es:
        sems = {e: es.enter_context(nc.semaphore("s_" + e)) for e in Prog.ENG}
        dsems = [es.enter_context(nc.semaphore("d%d" % i)) for i in range(32)]
        P = Prog(nc, sems, dsems)
        block = es.enter_context(nc.Block())

        def checkpoint(name):
            if stop_at is not None and name == stop_at and not P.dead:
                P.finish()
                P.flush(block)
                P.dead = True

        pb = [es.enter_context(nc.psum_tensor("pb%d" % i, [128, 512], F32)) for i in range(8)]
        R_pb = [Res("pb%d" % i, excl=True) for i in range(8)]

        class Rot:
            def __init__(self, idxs):
                self.idxs = idxs
                self.i = 0

            def next(self):
                k = self.idxs[self.i % len(self.idxs)]
                self.i += 1
                return k

        def sb(stack, name, shape, dt):
            return stack.enter_context(nc.sbuf_tensor("sb_" + name, list(shape), dt))

        ident_f = sb(es, "ident_f", [128, 128], F32)
        ident = sb(es, "ident", [128, 512], BF16)
        R_ident = Res("ident")
        P.dma("sp", ident_f[:, :], I["ident"][:, :], writes=[R_ident])
        for r in range(4):
            P.op("act", _call("activation", out=ident[:, r * 128:(r + 1) * 128], in_=ident_f[:, :], func=AF.Copy),
                 reads=[R_ident], writes=[R_ident])

        def run_interleaved(gens):
            gens = [[0.0, i, g] for i, g in enumerate(gens)]
            while gens:
                gens.sort(key=lambda x: (x[0], x[1]))
                ent = gens[0]
                try:
                    c = next(ent[2])
                    ent[0] += (c if c else 1.0)
                except StopIteration:
                    gens.remove(ent)

        with ExitStack() as sa:
            winb = sb(sa, "winb", [128, KC * NCOL], BF16)
            R_win = Res("win")
            kbi = sb(sa, "kbi", [128, 2 * 4096], BF16)
            R_kbi = [Res("kbi%d" % r) for r in range(NT)]
            vb_aug = sb(sa, "vb_aug", [128, NT * 2 * 65], BF16)
            R_vb = [Res("vb%d" % r) for r in range(NT)]
            kaT = sb(sa, "kaT", [128, 6 * 512], BF16)
            R_ka = [Res("ka%d" % s) for s in range(6)]
            va_aug = sb(sa, "va_aug", [128, 6 * 8 * 65], BF16)
            R_va = [Res("va%d" % s) for s in range(6)]
            ABb = sb(sa, "ABb", [128, 8 * ABW], BF16)
            R_AB = Res("AB")
            Bnb = sb(sa, "Bnb", [128, 8 * BNW], BF16)
            R_Bn = Res("Bn")
            Mnear = [sb(sa, "Mnear%d" % k, [128, 8 * BNW], BF16) for k in range(2)]
            R_Mnear = [Res("Mnear%d" % k) for k in range(2)]
            score = [sb(sa, "score%d" % k, [128, 4096], F32) for k in range(2)]
            R_score = [Res("score%d" % k) for k in range(2)]
            Mb = [sb(sa, "Mb%d" % k, [128, 4096], BF16) for k in range(2)]
            R_M = [Res("M%d" % k) for k in range(2)]
            relu = [sb(sa, "relu%d" % k, [128, 512], BF16) for k in range(3)]
            R_relu = [Res("relu%d" % k) for k in range(3)]
            xstg2 = [sb(sa, "xstg%d" % k, [128, 1024], F32) for k in range(2)]
            R_xstg2 = [Res("xstg%d" % k) for k in range(2)]
            xstg, R_xstg = xstg2[0], R_xstg2[0]
            xTb = [sb(sa, "xTb%d" % k, [128, 1024], BF16) for k in range(2)]
            R_xT = [Res("xT%d" % k) for k in range(2)]
            qaz = [sb(sa, "qaz%d" % k, [128, 1024], BF16) for k in range(2)]
            qbz = [sb(sa, "qbz%d" % k, [128, 1024], BF16) for k in range(3)]
            qiz = [sb(sa, "qiz%d" % k, [128, 1024], BF16) for k in range(2)]
            R_qa = [Res("qa%d" % k) for k in range(2)]
            R_qb = [Res("qb%d" % k) for k in range(3)]
            R_qi = [Res("qi%d" % k) for k in range(2)]
            coef = [sb(sa, "coef%d" % k, [128, 8], F32) for k in range(2)]
            R_coef = [Res("coef%d" % k) for k in range(2)]
            dg = [sb(sa, "dg%d" % k, [128, 1024], BF16) for k in range(2)]
            R_dg = [Res("dg%d" % k) for k in range(2)]
            PTA = [sb(sa, "PTA%d" % k, [128, 512], BF16) for k in range(3)]
            R_PTA = [Res("PTA%d" % k) for k in range(3)]
            PTB = [sb(sa, "PTB%d" % k, [128, 512], BF16) for k in range(3)]
            R_PTB = [Res("PTB%d" % k) for k in range(3)]
            mixb = [sb(sa, "mixb%d" % k, [128, 1024], BF16) for k in range(3)]
            R_mix = [Res("mix%d" % k) for k in range(3)]
            ostg = [sb(sa, "ostg%d" % k, [128, 256], F32) for k in range(2)]
            R_ostg = [Res("ostg%d" % k) for k in range(2)]
            vbstg = [sb(sa, "vbstg%d" % k, [128, 128], F32) for k in range(2)]
            R_vbstg = [Res("vbstg%d" % k) for k in range(2)]
            astg = sb(sa, "astg", [128, 1024], F32)
            R_astg = Res("astg")
            small = [sb(sa, "small%d" % k, [128, 16], F32) for k in range(2)]
            R_small = [Res("small%d" % k) for k in range(2)]
            recA = [sb(sa, "recA%d" % k, [128, 8], F32) for k in range(2)]
            R_recA = [Res("recA%d" % k) for k in range(2)]
            recB = [sb(sa, "recB%d" % k, [128, 8], F32) for k in range(2)]
            R_recB = [Res("recB%d" % k) for k in range(2)]
            colmask = sb(sa, "colmask", [128, 1], F32)
            diagm = sb(sa, "diagm", [128, 128], F32)
            kvalid = sb(sa, "kvalid", [128, NT], F32)
            c15 = sb(sa, "c15", [128, 8], F32)
            ones8 = sb(sa, "ones8", [128, 8], F32)
            R_cst = Res("cst")

            wrot = Rot([0, 1, 2])

            P.dma("sp", colmask[:, :], I["colmask"][:, :], writes=[R_cst])
            P.dma("sp", diagm[:, :], I["diagmask"][:, :], writes=[R_cst])
            P.dma("sp", kvalid[:, :], I["kvalid"][:, :], writes=[R_cst])
            P.dma("sp", c15[:, :], I["C15"][:, :], writes=[R_cst])
            P.op("pool", _call("memset", ones8[:, :], 1.0), writes=[R_cst])
            for k in range(2):
                P.op("pool", _call("memset", qaz[k][:, :], 0.0), writes=[R_qa[k]])
                P.op("pool", _call("memset", qiz[k][:, :], 0.0), writes=[R_qi[k]])
            for k in range(3):
                P.op("pool", _call("memset", qbz[k][:, :], 0.0), writes=[R_qb[k]])

            HW = NCOL // 2
            for kc in range(KC):
                for hh in range(2):
                    stg, R_stg = score[hh], R_score[hh]
                    P.dma("sp", stg[:, 0:HW], I["win"][:, kc * NCOL + hh * HW: kc * NCOL + (hh + 1) * HW], writes=[R_stg])
                    if hh == 0:
                        P.op("act", _call("activation", out=winb[:, kc * NCOL + hh * HW: kc * NCOL + (hh + 1) * HW], in_=stg[:, 0:HW], func=AF.Copy),
                             reads=[R_stg], writes=[R_win])
                    else:
                        P.op("pool", _call("tensor_copy", out=winb[:, kc * NCOL + hh * HW: kc * NCOL + (hh + 1) * HW], in_=stg[:, 0:HW]),
                             reads=[R_stg], writes=[R_win])
            for hh in range(2):
                w = 4 * ABW
                P.dma("sp", score[hh][:, 0:w], I["AB"][:, hh * w:(hh + 1) * w], writes=[R_score[hh]])
                P.op("act", _call("activation", out=ABb[:, hh * w:(hh + 1) * w], in_=score[hh][:, 0:w], func=AF.Copy),
                     reads=[R_score[hh]], writes=[R_AB])
            P.dma("sp", score[0][:, 0:8 * BNW], I["Bn"][:, :], writes=[R_score[0]])
            for h in range(8):
                P.op("dve", _call("tensor_scalar", out=Bnb[:, h * BNW:(h + 1) * BNW], in0=score[0][:, h * BNW:(h + 1) * BNW],
                                  scalar1=c15[:, h:h + 1], scalar2=None, op0=ALU.subtract),
                     reads=[R_score[0], R_cst], writes=[R_Bn])
            checkpoint('consts')

            def win_cols(kc, c0, n):
                return winb[:, kc * NCOL + c0: kc * NCOL + c0 + n]

            def fm_proj(bank, xT, R_x, N, col0, nchunks, ocol=0):
                for j in range(nchunks):
                    for kc in range(KC):
                        P.op("pe", _call("matmul", out=pb[bank][:, ocol + j * N: ocol + (j + 1) * N], lhsT=win_cols(kc, col0 + j * 128, 128),
                                         rhs=xT[:, kc * N:(kc + 1) * N], start=(kc == 0), stop=(kc == KC - 1)),
                             reads=[R_win, R_x], writes=[R_pb[bank]])

            def tm_proj(bank, xT, R_x, N, col0, ncols, ocol=0):
                for kc in range(KC):
                    P.op("pe", _call("matmul", out=pb[bank][0:N, ocol:ocol + ncols], lhsT=xT[:, kc * N:(kc + 1) * N],
                                     rhs=win_cols(kc, col0, ncols), start=(kc == 0), stop=(kc == KC - 1)),
                         reads=[R_win, R_x], writes=[R_pb[bank]])

            def load_xT(r):
                s = r % 2
                P.dma("sp", xstg2[s][:, :], I["xkT"][r], writes=[R_xstg2[s]])
                P.op("pool", _call("tensor_copy", out=xTb[s][:, :], in_=xstg2[s][:, :]), reads=[R_xstg2[s]], writes=[R_xT[s]])

            def kside(r, full):
                s = r % 2
                xT, R_x = xTb[s], R_xT[s]
                so = r % 2
                bk = wrot.next()
                fm_proj(bk, xT, R_x, 128, C_KB, 1)
                fm_proj(bk, xT, R_x, 128, C_KI, 1, ocol=128)
                P.op("act", _call("activation", out=ostg[so][:, :], in_=pb[bk][:, 0:256], func=AF.Copy), reads=[R_pb[bk]], writes=[R_ostg[so]])
                P.op("pool", _call("tensor_copy", out=kbi[:, :].rearrange("p (a c) -> p a c", a=2)[:, :, r * 128:(r + 1) * 128],
                                   in_=ostg[so][:, :].rearrange("p (a c) -> p a c", a=2)),
                     reads=[R_ostg[so]], writes=[R_kbi[r]])
                P.dma("sp", O["bkT"][:, r * 128:(r + 1) * 128], ostg[so][:, 0:128], reads=[R_ostg[so]], defer=True)
                P.dma("sp", O["biT"][:, r * 128:(r + 1) * 128], ostg[so][0:64, 128:256], reads=[R_ostg[so]], defer=True)
                yield 3.0
                bv_ = wrot.next()
                tm_proj(bv_, xT, R_x, 128, C_VB, 128)
                vbv = vb_aug[:, r * 130:(r + 1) * 130].rearrange("p (g d) -> p g d", d=65)
                P.op("act", _call("activation", out=vbstg[so][:, :], in_=pb[bv_][:, 0:128], func=AF.Copy), reads=[R_pb[bv_]], writes=[R_vbstg[so]])
                P.op("pool", _call("tensor_copy", out=vbv[:, :, 0:64], in_=vbstg[so][:, :].rearrange("p (g d) -> p g d", d=64)),
                     reads=[R_vbstg[so]], writes=[R_vb[r]])
                P.op("pool", _call("tensor_scalar", out=vbv[:, :, 64:65], in0=ones8[:, 0:2].rearrange("p (g o) -> p g o", o=1),
                                   scalar1=kvalid[:, r:r + 1], scalar2=None, op0=ALU.mult),
                     reads=[R_cst], writes=[R_vb[r]])
                P.dma("sp", O["bv"][r * 128:(r + 1) * 128, :], vbstg[so][:, :], reads=[R_vbstg[so]], defer=True)
                yield 3.0
                if not full:
                    return
                slot = r % 6
                ba = wrot.next()
                fm_proj(ba, xT, R_x, 128, C_KA, 4)
                P.op("act", _call("activation", out=kaT[:, slot * 512:(slot + 1) * 512], in_=pb[ba][:, :], func=AF.Copy),
                     reads=[R_pb[ba]], writes=[R_ka[slot]])
                if r >= 28:
                    P.op("dve", _call("tensor_copy", out=astg[:, 0:512], in_=pb[ba][:, :]), reads=[R_pb[ba]], writes=[R_astg])
                    P.dma("sp", O["akT"].rearrange("p (j t) -> p j t", t=512)[:, :, (r - 28) * 128:(r - 27) * 128],
                          astg[:, 0:512].rearrange("p (j t) -> p j t", t=128), reads=[R_astg], defer=True)
                yield 3.0
                bva = wrot.next()
                tm_proj(bva, xT, R_x, 128, C_VA, 512)
                vav = va_aug[:, slot * 520:(slot + 1) * 520].rearrange("p (h d) -> p h d", d=65)
                P.op("act", _call("activation", out=vav[:, :, 0:64], in_=pb[bva][:, :].rearrange("p (h d) -> p h d", d=64), func=AF.Copy),
                     reads=[R_pb[bva]], writes=[R_va[slot]])
                P.op("pool", _call("tensor_scalar", out=vav[:, :, 64:65], in0=ones8[:, :].rearrange("p (h o) -> p h o", o=1),
                                   scalar1=kvalid[:, r:r + 1], scalar2=None, op0=ALU.mult),
                     reads=[R_cst], writes=[R_va[slot]])
                if r >= 28:
                    P.op("dve", _call("tensor_copy", out=astg[:, 512:1024], in_=pb[bva][:, :]), reads=[R_pb[bva]], writes=[R_astg])
                    P.dma("sp", O["av"][(r - 28) * 128:(r - 27) * 128, :], astg[:, 512:1024], reads=[R_astg], defer=True)
                yield 3.0

            def qside(xT, R_x, qs, st, st3):
                b1 = wrot.next()
                fm_proj(b1, xT, R_x, qs, C_QA, 4)
                for hf in range(2):
                    P.op("act", _call("activation",
                                      out=qaz[st][hf * 64:(hf + 1) * 64, 0:8 * qs].rearrange("p (j two q) -> p j two q", two=2, q=qs)[:, :, hf, :],
                                      in_=pb[b1][hf * 64:(hf + 1) * 64, 0:4 * qs].rearrange("p (j q) -> p j q", q=qs), func=AF.Copy, scale=0.125),
                         reads=[R_pb[b1]], writes=[R_qa[st]])
                yield 3.0
                b2 = wrot.next()
                fm_proj(b2, xT, R_x, qs, C_QB, 4)
                for g in range(2):
                    P.op("act", _call("activation", out=qbz[st3][g * 64:(g + 1) * 64, g * 4 * qs:(g + 1) * 4 * qs],
                                      in_=pb[b2][g * 64:(g + 1) * 64, 0:4 * qs], func=AF.Copy, scale=0.125),
                         reads=[R_pb[b2]], writes=[R_qb[st3]])
                yield 3.0
                b3 = wrot.next()
                fm_proj(b3, xT, R_x, qs, C_QI, 4)
                for hf in range(2):
                    P.op("act", _call("activation",
                                      out=qiz[st][hf * 64:(hf + 1) * 64, 0:8 * qs].rearrange("p (j two q) -> p j two q", two=2, q=qs)[:, :, hf, :],
                                      in_=pb[b3][hf * 64:(hf + 1) * 64, 0:4 * qs].rearrange("p (j q) -> p j q", q=qs), func=AF.Copy),
                         reads=[R_pb[b3]], writes=[R_qi[st]])
                b4 = wrot.next()
                tm_proj(b4, xT, R_x, qs, C_WI, 8)
                P.op("dve", _call("tensor_scalar", out=coef[st][0:qs, :], in0=pb[b4][0:qs, 0:8], scalar1=float(8.0 ** -1.5), scalar2=None, op0=ALU.mult),
                     reads=[R_pb[b4]], writes=[R_coef[st]])
                for h in range(8):
                    P.op("pool", _call("tensor_scalar", out=dg[st][0:qs, h * 128: h * 128 + qs], in0=ident_f[0:qs, 0:qs],
                                       scalar1=coef[st][0:qs, h:h + 1], scalar2=None, op0=ALU.mult),
                         reads=[R_coef[st], R_ident], writes=[R_dg[st]])
                yield 3.0

            def normalize(bank, qs, mixt, R_m, col0, rec, R_rec):
                ov = pb[bank][0:qs, 0:260].rearrange("p (h d) -> p h d", d=65)
                P.op("dve", _call("tensor_scalar", out=rec[0:qs, 0:4].rearrange("p (h o) -> p h o", o=1), in0=ov[:, :, 64:65],
                                  scalar1=1e-30, scalar2=None, op0=ALU.max),
                     reads=[R_pb[bank]], writes=[R_rec])
                P.op("dve", _call("reciprocal", out=rec[0:qs, 0:4], in_=rec[0:qs, 0:4]), reads=[R_rec], writes=[R_rec])
                for hh in range(4):
                    P.op("dve", _call("tensor_scalar", out=mixt[0:qs, col0 + hh * 64: col0 + (hh + 1) * 64],
                                      in0=pb[bank][0:qs, hh * 65: hh * 65 + 64],
                                      scalar1=rec[0:qs, hh:hh + 1], scalar2=None, op0=ALU.mult),
                         reads=[R_pb[bank], R_rec], writes=[R_m])

            def pipe3(items, s1, s2, s3, D, cost=1.0):
                pend = []
                for it in items:
                    s1(it)
                    s2(it)
                    pend.append(it)
                    if len(pend) > D:
                        s3(pend.pop(0))
                    yield cost
                while pend:
                    s3(pend.pop(0))
                    yield cost

            pta_rot = Rot([0, 1, 2])
            relu_rot = Rot([0, 1, 2])
            ptb_rot = Rot([0, 1, 2])
            brot = Rot([3, 7])

            def front_attn(sn, qs, wins, btiles, prompt_masks, abw):
                st = sn % 2
                mixt, R_m = mixb[sn % 3], R_mix[sn % 3]
                nw = len(wins)

                units = []
                for h in range(8):
                    units.append({"h": h, "t0": 0, "tiles": wins[0:4]})
                    if nw > 4:
                        units.append({"h": h, "t0": 4, "tiles": wins[4:5]})

                def a1(u):
                    h = u["h"]
                    j = h // 2
                    bank = wrot.next()
                    u["bank"] = bank
                    for i, (slot, ts) in enumerate(u["tiles"]):
                        t = u["t0"] + i
                        c0 = i * qs
                        P.op("pe", _call("matmul", out=pb[bank][0:ts, c0:c0 + qs], lhsT=kaT[:, slot * 512 + j * 128: slot * 512 + j * 128 + ts],
                                         rhs=qaz[st][:, h * qs:(h + 1) * qs], start=True, stop=False),
                             reads=[R_ka[slot], R_qa[st]], writes=[R_pb[bank]])
                        P.op("pe", _call("matmul", out=pb[bank][0:ts, c0:c0 + qs], lhsT=ABb[0:qs, h * abw + t * 128: h * abw + t * 128 + ts],
                                         rhs=ident[0:qs, 0:qs], start=False, stop=True),
                             reads=[R_AB, R_ident], writes=[R_pb[bank]])

                def a2(u):
                    k = pta_rot.next()
                    u["pt"], u["R_pt"] = PTA[k], R_PTA[k]
                    bank = u["bank"]
                    tsm = max(ts for (_, ts) in u["tiles"])
                    n = len(u["tiles"])
                    P.op("act", _call("activation", out=u["pt"][0:tsm, 0:n * qs], in_=pb[bank][0:tsm, 0:n * qs], func=AF.Exp),
                         reads=[R_pb[bank]], writes=[u["R_pt"]])

                def a3(u):
                    h = u["h"]
                    last_unit = (u["t0"] + len(u["tiles"]) == nw)
                    for i, (slot, ts) in enumerate(u["tiles"]):
                        t = u["t0"] + i
                        P.op("pe", _call("matmul", out=pb[4][0:qs, (h % 4) * 65:(h % 4) * 65 + 65], lhsT=u["pt"][0:ts, i * qs:(i + 1) * qs],
                                         rhs=va_aug[0:ts, slot * 520 + h * 65: slot * 520 + h * 65 + 65],
                                         start=(h % 4 == 0 and t == 0), stop=(t == nw - 1), skip_group_check=True),
                             reads=[u["R_pt"], R_va[slot]], writes=[R_pb[4]])
                    if last_unit and h % 4 == 3:
                        normalize(4, qs, mixt, R_m, (h // 4) * 256, recA[st], R_recA[st])

                yield from pipe3(units, a1, a2, a3, 2, 0.9)

                L = btiles[-1][1] + btiles[-1][2]
                items = []
                cc = 0
                for c0 in range(0, L, 512):
                    w = min(512, L - c0)
                    rk = [R_kbi[tt[0]] for tt in btiles if tt[1] >= c0 - 127 and tt[1] < c0 + w]
                    for h in range(8):
                        items.append({"c0": c0, "w": w, "h": h, "sc": (5, 4)[cc % 2], "rk": rk})
                    cc += 1

                def i1(it):
                    bank = wrot.next()
                    it["bank"] = bank
                    h, c0, w = it["h"], it["c0"], it["w"]
                    P.op("pe", _call("matmul", out=pb[bank][0:qs, 0:w], lhsT=qiz[st][:, h * qs:(h + 1) * qs],
                                     rhs=kbi[:, 4096 + c0: 4096 + c0 + w], start=True, stop=True),
                         reads=[R_qi[st]] + it["rk"], writes=[R_pb[bank]])

                def i2(it):
                    k = relu_rot.next()
                    it["rl"], it["R_rl"] = relu[k], R_relu[k]
                    w = it["w"]
                    P.op("act", _call("activation", out=it["rl"][0:qs, 0:w], in_=pb[it["bank"]][0:qs, 0:w], func=AF.Relu),
                         reads=[R_pb[it["bank"]]], writes=[it["R_rl"]])

                def i3(it):
                    h, c0, w, sc = it["h"], it["c0"], it["w"], it["sc"]
                    P.op("pe", _call("matmul", out=pb[sc][0:qs, 0:w], lhsT=dg[st][0:qs, h * 128: h * 128 + qs], rhs=it["rl"][0:qs, 0:w],
                                     start=(h == 0), stop=(h == 7)),
                         reads=[R_dg[st], it["R_rl"]], writes=[R_pb[sc]])
                    if h == 7:
                        if prompt_masks and c0 < 2048:
                            wm = min(w, 2048 - c0)
                            P.op("act", _call("activation", out=score[st][0:qs, c0:c0 + wm], in_=pb[sc][0:qs, 0:wm], func=AF.Identity,
                                              bias=colmask[0:qs, 0:1]),
                                 reads=[R_pb[sc], R_cst], writes=[R_score[st]])
                            if wm < w:
                                P.op("act", _call("activation", out=score[st][0:qs, c0 + wm:c0 + w], in_=pb[sc][0:qs, wm:w], func=AF.Copy),
                                     reads=[R_pb[sc]], writes=[R_score[st]])
                        else:
                            P.op("act", _call("activation", out=score[st][0:qs, c0:c0 + w], in_=pb[sc][0:qs, 0:w], func=AF.Copy),
                                 reads=[R_pb[sc]], writes=[R_score[st]])

                yield from pipe3(items, i1, i2, i3, 2, 0.65)
                if prompt_masks:
                    P.op("dve", _call("tensor_tensor", out=score[st][0:qs, L - 128:L], in0=score[st][0:qs, L - 128:L], in1=diagm[0:qs, :], op=ALU.add),
                         reads=[R_score[st], R_cst], writes=[R_score[st]])
                yield

            def bis_gen(sn, qs, btiles, bnw):
                st = sn % 2
                sm, R_sm = small[st], R_small[st]
                L = btiles[-1][1] + btiles[-1][2]
                P.op("dve", _call("memset", sm[0:qs, 1:2], 0.0), writes=[R_sm])
                for k in range(NIT):
                    wk = BIS_W0 / (2.0 ** k)
                    P.op("dve", _call("tensor_scalar", out=Mb[st][0:qs, 0:L], in0=score[st][0:qs, 0:L], scalar1=sm[0:qs, 1:2], scalar2=None,
                                      op0=ALU.is_ge, op1=ALU.add, accum_out=sm[0:qs, 0:1]),
                         reads=[R_score[st], R_sm], writes=[R_M[st], R_sm])
                    P.op("dve", _call("tensor_scalar", out=sm[0:qs, 2:3], in0=sm[0:qs, 0:1], scalar1=255.5, scalar2=wk,
                                      op0=ALU.is_ge, op1=ALU.mult),
                         reads=[R_sm], writes=[R_sm])
                    P.op("dve", _call("scalar_tensor_tensor", out=sm[0:qs, 1:2], in0=sm[0:qs, 2:3], scalar=-wk / 2.0,
                                      in1=sm[0:qs, 1:2], op0=ALU.add, op1=ALU.add),
                         reads=[R_sm], writes=[R_sm])
                    yield L * 1.08e-3 + 0.5
                wl = BIS_W0 / (2.0 ** (NIT - 1)) / 2.0
                P.op("dve", _call("tensor_scalar", out=sm[0:qs, 3:4], in0=sm[0:qs, 1:2], scalar1=-wl, scalar2=None, op0=ALU.add),
                     reads=[R_sm], writes=[R_sm])
                P.op("dve", _call("tensor_scalar", out=Mb[st][0:qs, 0:L], in0=score[st][0:qs, 0:L], scalar1=sm[0:qs, 3:4], scalar2=NEGM,
                                  op0=ALU.is_lt, op1=ALU.mult),
                     reads=[R_score[st], R_sm], writes=[R_M[st]])
                nearw = btiles[-2][2] + btiles[-1][2]
                for h in range(8):
                    P.op("dve", _call("tensor_tensor", out=Mnear[st][0:qs, h * bnw: h * bnw + nearw], in0=Bnb[0:qs, h * bnw: h * bnw + nearw],
                                      in1=Mb[st][0:qs, L - nearw:L], op=ALU.add),
                         reads=[R_Bn, R_M[st]], writes=[R_Mnear[st]])
                yield
            def battn_gen(sn, qs, btiles, blk, bnw):
                st = sn % 2
                st3 = sn % 3
                mixt, R_m = mixb[st3], R_mix[st3]
                nb = len(btiles)
                items = [{"g": g, "t": t, "vt": vt, "c0": c0, "ts": ts} for g in range(2) for t, (vt, c0, ts) in enumerate(btiles)]

                def b1(it):
                    g, t, vt, c0, ts = it["g"], it["t"], it["vt"], it["c0"], it["ts"]
                    bank = brot.next()
                    it["bank"] = bank
                    P.op("pe", _call("matmul", out=pb[bank][0:ts, 0:4 * qs], lhsT=kbi[:, c0:c0 + ts],
                                     rhs=qbz[st3][:, g * 4 * qs:(g + 1) * 4 * qs], start=True, stop=False),
                         reads=[R_kbi[vt], R_qb[st3]], writes=[R_pb[bank]])
                    if t < nb - 2 and qs == 128:
                        P.op("pe", _call("matmul", out=pb[bank][0:ts, 0:512], lhsT=Mb[st][0:qs, c0:c0 + ts], rhs=ident[0:128, 0:512],
                                         start=False, stop=True),
                             reads=[R_M[st], R_ident], writes=[R_pb[bank]])
                    elif t < nb - 2:
                        for r in range(4):
                            P.op("pe", _call("matmul", out=pb[bank][0:ts, r * qs:(r + 1) * qs], lhsT=Mb[st][0:qs, c0:c0 + ts],
                                             rhs=ident[0:qs, 0:qs], start=False, stop=(r == 3)),
                                 reads=[R_M[st], R_ident], writes=[R_pb[bank]])
                    else:
                        tt = t - (nb - 2)
                        for r in range(4):
                            hh = g * 4 + r
                            P.op("pe", _call("matmul", out=pb[bank][0:ts, r * qs:(r + 1) * qs],
                                             lhsT=Mnear[st][0:qs, hh * bnw + tt * 128: hh * bnw + tt * 128 + ts], rhs=ident[0:qs, 0:qs],
                                             start=False, stop=(r == 3)),
                                 reads=[R_Mnear[st], R_ident], writes=[R_pb[bank]])

                def b2(it):
                    k = ptb_rot.next()
                    it["ptb"], it["R_ptb"] = PTB[k], R_PTB[k]
                    ts = it["ts"]
                    P.op("act", _call("activation", out=it["ptb"][0:ts, 0:4 * qs], in_=pb[it["bank"]][0:ts, 0:4 * qs], func=AF.Exp),
                         reads=[R_pb[it["bank"]]], writes=[it["R_ptb"]])

                def b3(it):
                    g, t, vt, ts = it["g"], it["t"], it["vt"], it["ts"]
                    for r in range(4):
                        P.op("pe", _call("matmul", out=pb[6][0:qs, r * 65: r * 65 + 65], lhsT=it["ptb"][0:ts, r * qs:(r + 1) * qs],
                                         rhs=vb_aug[0:ts, (vt * 2 + g) * 65:(vt * 2 + g) * 65 + 65],
                                         start=(t == 0 and r == 0), stop=(t == nb - 1), skip_group_check=True),
                             reads=[it["R_ptb"], R_vb[vt]], writes=[R_pb[6]])
                    if t == nb - 1:
                        normalize(6, qs, mixt, R_m, 512 + g * 256, recB[st], R_recB[st])

                yield from pipe3(items, b1, b2, b3, 1, 0.8)
                P.dma("sp", mixD[blk * 128: blk * 128 + qs, :], mixt[0:qs, :], reads=[R_m], writes=[R_mixD[blk]], defer=True)
                yield

            load_xT(0)
            for r in range(16):
                if r + 1 < 16:
                    load_xT(r + 1)
                for _ in kside(r, full=(r >= 11)):
                    pass
            checkpoint('phase0')

            def prompt_front(sn, T):
                if T >= 16:
                    load_xT(T)
                    yield from kside(T, full=True)
                s = T % 2
                yield from qside(xTb[s], R_xT[s], 128, sn % 2, sn % 3)
                wins = [((T - 4 + t) % 6, 128) for t in range(5)]
                btiles = [(t, t * 128, 128) for t in range(T + 1)]
                yield from front_attn(sn, 128, wins, btiles, True, ABW)

            def prompt_bis(sn, T):
                btiles = [(t, t * 128, 128) for t in range(T + 1)]
                yield from bis_gen(sn, 128, btiles, BNW)

            def prompt_battn(sn, T, blk):
                btiles = [(t, t * 128, 128) for t in range(T + 1)]
                yield from battn_gen(sn, 128, btiles, blk, BNW)

            steps = [(0, 15, 16)] + [(1 + i, 16 + i, i) for i in range(16)]
            ns = len(steps)
            for tick in range(ns + 2):
                gens = []
                if 0 <= tick - 2 < ns:
                    gens.append(prompt_battn(*steps[tick - 2]))
                if 0 <= tick - 1 < ns:
                    gens.append(prompt_bis(*steps[tick - 1][0:2]))
                if tick < ns:
                    gens.append(prompt_front(*steps[tick][0:2]))
                run_interleaved(gens)
            checkpoint('steps')

            SN = len(steps)
            sst = SN % 2
            P.dma("sp", score[0][:, 0:2048], I["cbkT"][:, :], writes=[R_score[0]])
            P.op("act", _call("activation", out=kbi[:, 0:2048], in_=score[0][:, 0:2048], func=AF.Copy),
                 reads=[R_score[0]], writes=R_kbi[0:16])
            P.dma("sp", score[0][:, 2048:4096], I["cbiT"][:, :], writes=[R_score[0]])
            P.op("act", _call("activation", out=kbi[:, 4096:4096 + 2048], in_=score[0][:, 2048:4096], func=AF.Copy),
                 reads=[R_score[0]], writes=R_kbi[0:16])
            P.dma("sp", score[1][:, 0:2048].rearrange("p (t c) -> p t c", c=128), I["cbv"].rearrange("(t p) c -> p t c", p=128), writes=[R_score[1]])
            vball = vb_aug[:, 0:16 * 130].rearrange("p (t d) -> p t d", d=65)
            P.op("act", _call("activation", out=vball[:, :, 0:64], in_=score[1][:, 0:2048].rearrange("p (t d) -> p t d", d=64), func=AF.Copy),
                 reads=[R_score[1]], writes=R_vb[0:17])
            P.op("pool", _call("memset", vb_aug[:, 0:17 * 130].rearrange("p (t d) -> p t d", d=65)[:, :, 64:65], 1.0), writes=R_vb[0:17])
            P.dma("sp", score[0][:, 0:2048], I["cakT"][:, :], writes=[R_score[0]])
            for s4 in range(4):
                P.op("act", _call("activation", out=kaT[:, s4 * 512:(s4 + 1) * 512].rearrange("p (j t) -> p j t", t=128),
                                  in_=score[0][:, 0:2048].rearrange("p (j t) -> p j t", t=512)[:, :, s4 * 128:(s4 + 1) * 128], func=AF.Copy),
                     reads=[R_score[0]], writes=[R_ka[s4]])
            P.dma("sp", score[1][:, 2048:4096].rearrange("p (t c) -> p t c", c=512), I["cav"].rearrange("(t p) c -> p t c", p=128), writes=[R_score[1]])
            vaall = va_aug[:, 0:4 * 520].rearrange("p (t d) -> p t d", d=65)
            P.op("act", _call("activation", out=vaall[:, :, 0:64], in_=score[1][:, 2048:4096].rearrange("p (t d) -> p t d", d=64), func=AF.Copy),
                 reads=[R_score[1]], writes=R_va[0:5])
            P.op("pool", _call("memset", va_aug[:, 0:5 * 520].rearrange("p (t d) -> p t d", d=65)[:, :, 64:65], 1.0), writes=R_va[0:5])
            for hh in range(2):
                w = 4 * 528
                P.dma("sp", score[0][0:16, 0:w], I["ABs"][:, hh * w:(hh + 1) * w], writes=[R_score[0]])
                P.op("act", _call("activation", out=ABb[0:16, hh * w:(hh + 1) * w], in_=score[0][0:16, 0:w], func=AF.Copy),
                     reads=[R_score[0]], writes=[R_AB])
            P.dma("sp", score[1][0:16, 0:8 * 144], I["Bns"][:, :], writes=[R_score[1]])
            for h in range(8):
                P.op("dve", _call("tensor_scalar", out=Bnb[0:16, h * 144:(h + 1) * 144], in0=score[1][0:16, h * 144:(h + 1) * 144],
                                  scalar1=c15[0:16, h:h + 1], scalar2=None, op0=ALU.subtract),
                     reads=[R_score[1], R_cst], writes=[R_Bn])
            P.op("pool", _call("memset", qaz[sst][:, :], 0.0), writes=[R_qa[sst]])
            P.op("pool", _call("memset", qbz[SN % 3][:, :], 0.0), writes=[R_qb[SN % 3]])
            P.op("pool", _call("memset", qiz[sst][:, :], 0.0), writes=[R_qi[sst]])
            P.dma("sp", xstg[:, 0:128], I["xsT"][:, :], writes=[R_xstg])
            P.op("pool", _call("tensor_copy", out=xTb[0][:, 0:128], in_=xstg[:, 0:128]), reads=[R_xstg], writes=[R_xT[0]])
            xs_, R_xs = xTb[0], R_xT[0]
            bk = wrot.next()
            fm_proj(bk, xs_, R_xs, 16, C_KB, 1)
            fm_proj(bk, xs_, R_xs, 16, C_KI, 1, ocol=16)
            P.op("act", _call("activation", out=kbi[:, 2048:2064], in_=pb[bk][:, 0:16], func=AF.Copy), reads=[R_pb[bk]], writes=[R_kbi[16]])
            P.op("act", _call("activation", out=kbi[:, 4096 + 2048:4096 + 2064], in_=pb[bk][:, 16:32], func=AF.Copy), reads=[R_pb[bk]], writes=[R_kbi[16]])
            P.op("dve", _call("tensor_copy", out=ostg[0][:, 0:32], in_=pb[bk][:, 0:32]), reads=[R_pb[bk]], writes=[R_ostg[0]])
            P.dma("sp", O["sbkT"][:, :], ostg[0][:, 0:16], reads=[R_ostg[0]], defer=True)
            P.dma("sp", O["sbiT"][:, :], ostg[0][0:64, 16:32], reads=[R_ostg[0]], defer=True)
            bv_ = wrot.next()
            tm_proj(bv_, xs_, R_xs, 16, C_VB, 128)
            vbv = vb_aug[0:16, 16 * 130:17 * 130].rearrange("p (g d) -> p g d", d=65)
            P.op("act", _call("activation", out=vbv[:, :, 0:64], in_=pb[bv_][0:16, 0:128].rearrange("p (g d) -> p g d", d=64), func=AF.Copy),
                 reads=[R_pb[bv_]], writes=[R_vb[16]])
            P.op("dve", _call("tensor_copy", out=vbstg[0][0:16, :], in_=pb[bv_][0:16, 0:128]), reads=[R_pb[bv_]], writes=[R_vbstg[0]])
            P.dma("sp", O["sbv"][:, :], vbstg[0][0:16, :], reads=[R_vbstg[0]], defer=True)
            ba = wrot.next()
            fm_proj(ba, xs_, R_xs, 16, C_KA, 4)
            P.op("act", _call("activation", out=kaT[:, 4 * 512:5 * 512].rearrange("p (j t) -> p j t", t=128)[:, :, 0:16],
                              in_=pb[ba][:, 0:64].rearrange("p (j t) -> p j t", t=16), func=AF.Copy),
                 reads=[R_pb[ba]], writes=[R_ka[4]])
            P.op("dve", _call("tensor_copy", out=astg[:, 0:64], in_=pb[ba][:, 0:64]), reads=[R_pb[ba]], writes=[R_astg])
            P.dma("sp", O["sakT"][:, :], astg[:, 0:64], reads=[R_astg], defer=True)
            bva = wrot.next()
            tm_proj(bva, xs_, R_xs, 16, C_VA, 512)
            vav = va_aug[0:16, 4 * 520:5 * 520].rearrange("p (h d) -> p h d", d=65)
            P.op("act", _call("activation", out=vav[:, :, 0:64], in_=pb[bva][0:16, :].rearrange("p (h d) -> p h d", d=64), func=AF.Copy),
                 reads=[R_pb[bva]], writes=[R_va[4]])
            P.op("dve", _call("tensor_copy", out=astg[0:16, 512:1024], in_=pb[bva][0:16, :]), reads=[R_pb[bva]], writes=[R_astg])
            P.dma("sp", O["sav"][:, :], astg[0:16, 512:1024], reads=[R_astg], defer=True)
            checkpoint('sample_pre')
            wins = [(0, 128), (1, 128), (2, 128), (3, 128), (4, 16)]
            btiles = [(t, t * 128, 128) for t in range(16)] + [(16, 2048, 16)]

            def sample_all():
                yield from qside(xs_, R_xs, 16, sst, SN % 3)
                yield from front_attn(SN, 16, wins, btiles, False, 528)
                yield from bis_gen(SN, 16, btiles, 144)
                yield from battn_gen(SN, 16, btiles, 17, 144)

            run_interleaved([sample_all()])
            checkpoint('phaseA')
            P.flush(block)

        P.barrier()
        with ExitStack() as sbk:
            wob = sb(sbk, "wob", [128, 8 * 1024], BF16)
            wmqb = sb(sbk, "wmqb", [128, 8 * 512], BF16)
            wmob = sb(sbk, "wmob", [128, 4 * 1024], BF16)
            wtmp = sb(sbk, "wtmp", [128, 8 * 512], BF16)
            R_wo, R_wmq, R_wmo, R_wtmp = Res("wo"), Res("wmq"), Res("wmo"), Res("wtmp")
            wst = [sb(sbk, "wst%d" % k, [128, 2048], F32) for k in range(2)]
            R_wst = [Res("wst%d" % k) for k in range(2)]
            lnt = sb(sbk, "lnt", [128, 4 * 1024], F32)
            R_ln = Res("ln")
            memTb = sb(sbk, "memTb", [128, 8 * 256], BF16)
            R_memT = Res("memT")
            mkT = [sb(sbk, "mkT%d" % k, [128, 4 * 256], BF16) for k in range(2)]
            mva = [sb(sbk, "mva%d" % k, [128, 2 * 4 * 129], BF16) for k in range(2)]
            R_mk = [Res("mk%d" % k) for k in range(2)]
            R_mv = [Res("mv%d" % k) for k in range(2)]
            mixl = [sb(sbk, "mixl%d" % k, [128, 1024], BF16) for k in range(4)]
            R_mixl = [Res("mixl%d" % k) for k in range(4)]
            xr = [sb(sbk, "xr%d" % k, [128, 1024], F32) for k in range(4)]
            R_xr = [Res("xr%d" % k) for k in range(4)]
            NB3 = 4
            tT_l = [sb(sbk, "tT%d" % k, [128, 1024], BF16) for k in range(NB3)]
            hA_l = [sb(sbk, "hA%d" % k, [128, 1024], F32) for k in range(NB3)]
            hB_l = [sb(sbk, "hB%d" % k, [128, 1024], F32) for k in range(NB3)]
            h16_l = [sb(sbk, "h16%d" % k, [128, 1024], BF16) for k in range(NB3)]
            qmT_l = [sb(sbk, "qmT%d" % k, [128, 512], BF16) for k in range(NB3)]
            PTm_l = [sb(sbk, "PTm%d" % k, [128, 1024], BF16) for k in range(NB3)]
            o16_l = [sb(sbk, "o16%d" % k, [128, 512], BF16) for k in range(NB3)]
            oT_l = [sb(sbk, "oT%d" % k, [128, 512], BF16) for k in range(NB3)]
            stat_l = [sb(sbk, "stat%d" % k, [128, 32], F32) for k in range(NB3)]
            RB = [{n: Res(n + str(k)) for n in ("tT", "hA", "hB", "h16", "qm", "PTm", "o16", "oT", "stat")} for k in range(NB3)]
            h2T = [sb(sbk, "h2T%d" % k, [128, 1024], BF16) for k in range(4)]
            R_h2T = [Res("h2T%d" % k) for k in range(4)]
            mstg = sb(sbk, "mstg", [128, 1024], F32)
            R_mstg = Res("mstg")
            wrot = Rot([0, 1, 2, 3, 4, 5, 6, 7])

            def load_cast(dst, R_dst, src, ncols, engs=("act", "pool")):
                k = 0
                for c0 in range(0, ncols, 2048):
                    w = min(2048, ncols - c0)
                    s = k % 2
                    P.dma("sp", wst[s][:, 0:w], src[:, c0:c0 + w], writes=[R_wst[s]])
                    eng = engs[k % len(engs)]
                    if eng == "act":
                        P.op("act", _call("activation", out=dst[:, c0:c0 + w], in_=wst[s][:, 0:w], func=AF.Copy),
                             reads=[R_wst[s]], writes=[R_dst])
                    else:
                        P.op(eng, _call("tensor_copy", out=dst[:, c0:c0 + w], in_=wst[s][:, 0:w]),
                             reads=[R_wst[s]], writes=[R_dst])
                    k += 1

            load_cast(wob, R_wo, I["wo"], 8192)
            load_cast(wmqb, R_wmq, I["wmq"], 4096)
            load_cast(wmob, R_wmo, I["wmo"], 4096)
            for k in range(4):
                P.dma("sp", lnt[:, k * 1024:(k + 1) * 1024], I["lnp"][k:k + 1, :].to_broadcast([128, 1024]), writes=[R_ln])
            load_cast(memTb, R_memT, I["memT"], 2048)
            load_cast(wtmp, R_wtmp, I["wmk"], 4096)
            for h in range(4):
                bank = wrot.next()
                for kc in range(KC):
                    P.op("pe", _call("matmul",
                        out=pb[bank][:, 0:256], lhsT=wtmp[:, kc * 512 + h * 128: kc * 512 + (h + 1) * 128],
                        rhs=memTb[:, kc * 256:(kc + 1) * 256], start=(kc == 0), stop=(kc == KC - 1)),
                        reads=[R_wtmp, R_memT], writes=[R_pb[bank]])
                P.op("act", _call("activation", out=mkT[0][:, h * 256:(h + 1) * 256], in_=pb[bank][:, 0:256], func=AF.Copy),
                     reads=[R_pb[bank]], writes=[R_mk[0]])
                P.op("dve", _call("tensor_copy", out=mstg[:, h * 256:(h + 1) * 256], in_=pb[bank][:, 0:256]),
                     reads=[R_pb[bank]], writes=[R_mstg])
            P.dma("sp", O["mkT"][:, :], mstg[:, :], reads=[R_mstg], defer=True)
            load_cast(wtmp, R_wtmp, I["wmv"], 4096)
            for mt in range(2):
                bank = wrot.next()
                for kc in range(KC):
                    P.op("pe", _call("matmul",
                        out=pb[bank][:, 0:512], lhsT=memTb[:, kc * 256 + mt * 128: kc * 256 + (mt + 1) * 128],
                        rhs=wtmp[:, kc * 512:(kc + 1) * 512], start=(kc == 0), stop=(kc == KC - 1)),
                        reads=[R_wtmp, R_memT], writes=[R_pb[bank]])
                mvv = mva[0][:, mt * 516:(mt + 1) * 516].rearrange("p (h d) -> p h d", d=129)
                P.op("act", _call("activation", out=mvv[:, :, 0:128], in_=pb[bank][:, :].rearrange("p (h d) -> p h d", d=128), func=AF.Copy),
                     reads=[R_pb[bank]], writes=[R_mv[0]])
                P.op("dve", _call("tensor_copy", out=mstg[:, mt * 512:(mt + 1) * 512], in_=pb[bank][:, :]),
                     reads=[R_pb[bank]], writes=[R_mstg])
                P.dma("sp", O["mv"][mt * 128:(mt + 1) * 128, :], mstg[:, mt * 512:(mt + 1) * 512], reads=[R_mstg], defer=True)
            for k in range(2):
                P.op("pool", _call("memset", mva[k][:, :].rearrange("p (t d) -> p t d", d=129)[:, :, 128:129], 1.0), writes=[R_mv[k]])
            load_cast(mkT[1], R_mk[1], I["cmkT"], 1024)
            P.dma("sp", wst[0][:, 0:1024].rearrange("p (t c) -> p t c", c=512), I["cmv"].rearrange("(t p) c -> p t c", p=128), writes=[R_wst[0]])
            P.op("act", _call("activation", out=mva[1][:, :].rearrange("p (t d) -> p t d", d=129)[:, :, 0:128],
                                               in_=wst[0][:, 0:1024].rearrange("p (t d) -> p t d", d=128), func=AF.Copy),
                 reads=[R_wst[0]], writes=[R_mv[1]])

            checkpoint('phaseB_pre')
            def transpose_to(src16, R_src, qs, nchunk, dst, R_dst):
                bank = wrot.next()
                pbf = pb[bank][:, :].bitcast(BF16)
                for c in range(nchunk):
                    P.op("pe", _call("transpose", out=pbf[:, c * qs:(c + 1) * qs], in_=src16[0:qs, c * 128:(c + 1) * 128],
                                                                   identity=ident[0:qs, 0:qs]),
                         reads=[R_src, R_ident], writes=[R_pb[bank]])
                P.op("act", _call("activation", out=dst[:, 0:nchunk * qs], in_=pbf[:, 0:nchunk * qs], func=AF.Copy),
                     reads=[R_pb[bank]], writes=[R_dst])

            def layer_norm(hin, R_hin, qs, gcol, hout, R_hout, stat, R_stat):
                for c in range(2):
                    P.op("dve", _call("bn_stats", out=stat[0:qs, c * 6:(c + 1) * 6], in_=hin[0:qs, c * 512:(c + 1) * 512]),
                         reads=[R_hin], writes=[R_stat])
                P.op("dve", _call("bn_aggr", out=stat[0:qs, 12:14], in_=stat[0:qs, 0:12]), reads=[R_stat], writes=[R_stat])
                P.op("dve", _call("tensor_scalar", out=stat[0:qs, 14:15], in0=stat[0:qs, 13:14], scalar1=LN_EPS, scalar2=None, op0=ALU.add),
                     reads=[R_stat], writes=[R_stat])
                P.op("act", _call("activation", out=stat[0:qs, 15:16], in_=stat[0:qs, 14:15], func=AF.Sqrt), reads=[R_stat], writes=[R_stat])
                P.op("dve", _call("reciprocal", out=stat[0:qs, 16:17], in_=stat[0:qs, 15:16]), reads=[R_stat], writes=[R_stat])
                P.op("dve", _call("scalar_tensor_tensor", out=stat[0:qs, 17:18], in0=stat[0:qs, 12:13], scalar=-1.0, in1=stat[0:qs, 16:17],
                                                             op0=ALU.mult, op1=ALU.mult),
                     reads=[R_stat], writes=[R_stat])
                P.op("act", _call("activation", out=hout[0:qs, :], in_=hin[0:qs, :], func=AF.Identity, scale=stat[0:qs, 16:17], bias=stat[0:qs, 17:18]),
                     reads=[R_hin, R_stat], writes=[R_hout])
                P.op("dve", _call("tensor_tensor", out=hout[0:qs, :], in0=hout[0:qs, :], in1=lnt[0:qs, gcol * 1024:(gcol + 1) * 1024], op=ALU.mult),
                     reads=[R_hout, R_ln], writes=[R_hout])
                P.op("dve", _call("tensor_tensor", out=hout[0:qs, :], in0=hout[0:qs, :], in1=lnt[0:qs, (gcol + 1) * 1024:(gcol + 2) * 1024], op=ALU.add),
                     reads=[R_hout, R_ln], writes=[R_hout])

            def phaseB_block(blk, qs, row0, mi, k2):
                s = k2
                tT, hA, hB, h16, qmT, PTm, o16, oT, stat = (tT_l[k2], hA_l[k2], hB_l[k2], h16_l[k2], qmT_l[k2], PTm_l[k2], o16_l[k2],
                                                             oT_l[k2], stat_l[k2])
                R_tT, R_hA, R_hB, R_h16, R_qm, R_PTm, R_o16, R_oT, R_stat = (RB[k2][n] for n in ("tT", "hA", "hB", "h16", "qm", "PTm", "o16", "oT", "stat"))
                P.dma("sp", mixl[s][0:qs, :], mixD[blk * 128 + row0: blk * 128 + row0 + qs, :], reads=[R_mixD[blk]], writes=[R_mixl[s]])
                P.dma("sp", xr[s][0:qs, :], I["xres"][blk * 128: blk * 128 + qs, :], writes=[R_xr[s]])
                transpose_to(mixl[s], R_mixl[s], qs, 8, tT, R_tT)
                yield
                b0, b1 = wrot.next(), wrot.next()
                for n, bank in enumerate((b0, b1)):
                    for kc in range(KC):
                        P.op("pe", _call("matmul",
                            out=pb[bank][0:qs, :], lhsT=tT[:, kc * qs:(kc + 1) * qs], rhs=wob[:, kc * 1024 + n * 512: kc * 1024 + (n + 1) * 512],
                            start=(kc == 0), stop=(kc == KC - 1)),
                            reads=[R_tT, R_wo], writes=[R_pb[bank]])
                    P.op("dve", _call("scalar_tensor_tensor",
                        out=hA[0:qs, n * 512:(n + 1) * 512], in0=xr[s][0:qs, n * 512:(n + 1) * 512], scalar=ALPHA, in1=pb[bank][0:qs, :],
                        op0=ALU.mult, op1=ALU.add),
                        reads=[R_xr[s], R_pb[bank]], writes=[R_hA])
                yield
                layer_norm(hA, R_hA, qs, 0, hB, R_hB, stat, R_stat)
                yield
                P.op("act", _call("activation", out=h16[0:qs, :], in_=hB[0:qs, :], func=AF.Copy), reads=[R_hB], writes=[R_h16])
                transpose_to(h16, R_h16, qs, 8, tT, R_tT)
                yield
                bq = wrot.next()
                for h in range(4):
                    for kc in range(KC):
                        P.op("pe", _call("matmul",
                            out=pb[bq][:, h * qs:(h + 1) * qs], lhsT=wmqb[:, kc * 512 + h * 128: kc * 512 + (h + 1) * 128],
                            rhs=tT[:, kc * qs:(kc + 1) * qs], start=(kc == 0), stop=(kc == KC - 1)),
                            reads=[R_wmq, R_tT], writes=[R_pb[bq]])
                P.op("act", _call("activation", out=qmT[:, 0:4 * qs], in_=pb[bq][:, 0:4 * qs], func=AF.Copy, scale=float(128.0 ** -0.5)),
                     reads=[R_pb[bq]], writes=[R_qm])
                yield
                bs0, bs1 = wrot.next(), wrot.next()
                for h in range(4):
                    for mt in range(2):
                        idx = h * 2 + mt
                        bank = bs0 if idx < 4 else bs1
                        c0 = (idx % 4) * qs
                        P.op("pe", _call("matmul",
                            out=pb[bank][:, c0:c0 + qs], lhsT=mkT[mi][:, h * 256 + mt * 128: h * 256 + (mt + 1) * 128],
                            rhs=qmT[:, h * qs:(h + 1) * qs], start=True, stop=True),
                            reads=[R_mk[mi], R_qm], writes=[R_pb[bank]])
                for k, bank in enumerate((bs0, bs1)):
                    P.op("act", _call("activation", out=PTm[:, k * 4 * qs:(k + 1) * 4 * qs], in_=pb[bank][:, 0:4 * qs], func=AF.Exp),
                         reads=[R_pb[bank]], writes=[R_PTm])
                yield
                bo0, bo1 = wrot.next(), wrot.next()
                for h in range(4):
                    bank = bo0 if h < 2 else bo1
                    for mt in range(2):
                        idx = h * 2 + mt
                        P.op("pe", _call("matmul",
                            out=pb[bank][0:qs, (h % 2) * 129:(h % 2) * 129 + 129], lhsT=PTm[:, idx * qs:(idx + 1) * qs],
                            rhs=mva[mi][:, (mt * 4 + h) * 129:(mt * 4 + h) * 129 + 129],
                            start=(h % 2 == 0 and mt == 0), stop=(mt == 1), skip_group_check=True),
                            reads=[R_PTm, R_mv[mi]], writes=[R_pb[bank]])
                for k, bank in enumerate((bo0, bo1)):
                    ov = pb[bank][0:qs, 0:258].rearrange("p (h d) -> p h d", d=129)
                    P.op("dve", _call("tensor_scalar", out=stat[0:qs, 20 + 2 * k:22 + 2 * k].rearrange("p (h o) -> p h o", o=1),
                                                                      in0=ov[:, :, 128:129], scalar1=1e-30, scalar2=None, op0=ALU.max),
                         reads=[R_pb[bank]], writes=[R_stat])
                    P.op("dve", _call("reciprocal", out=stat[0:qs, 20 + 2 * k:22 + 2 * k], in_=stat[0:qs, 20 + 2 * k:22 + 2 * k]),
                         reads=[R_stat], writes=[R_stat])
                    for hh in range(2):
                        h = k * 2 + hh
                        P.op("dve", _call("tensor_scalar",
                            out=o16[0:qs, h * 128:(h + 1) * 128], in0=pb[bank][0:qs, hh * 129: hh * 129 + 128],
                            scalar1=stat[0:qs, 20 + 2 * k + hh:21 + 2 * k + hh], scalar2=None, op0=ALU.mult),
                            reads=[R_pb[bank], R_stat], writes=[R_o16])
                yield
                transpose_to(o16, R_o16, qs, 4, oT, R_oT)
                yield
                b0, b1 = wrot.next(), wrot.next()
                for n, bank in enumerate((b0, b1)):
                    for c in range(4):
                        P.op("pe", _call("matmul",
                            out=pb[bank][0:qs, :], lhsT=oT[:, c * qs:(c + 1) * qs], rhs=wmob[:, c * 1024 + n * 512: c * 1024 + (n + 1) * 512],
                            start=(c == 0), stop=(c == 3)),
                            reads=[R_oT, R_wmo], writes=[R_pb[bank]])
                    P.op("dve", _call("scalar_tensor_tensor",
                        out=hA[0:qs, n * 512:(n + 1) * 512], in0=hB[0:qs, n * 512:(n + 1) * 512], scalar=ALPHA, in1=pb[bank][0:qs, :],
                        op0=ALU.mult, op1=ALU.add),
                        reads=[R_hB, R_pb[bank]], writes=[R_hA])
                yield
                layer_norm(hA, R_hA, qs, 2, hB, R_hB, stat, R_stat)
                yield
                P.dma("sp", h2D[blk * 128: blk * 128 + qs, :], hB[0:qs, :], reads=[R_hB], writes=[R_h2D[blk]], defer=True)
                P.op("act", _call("activation", out=h16[0:qs, :], in_=hB[0:qs, :], func=AF.Copy), reads=[R_hB], writes=[R_h16])
                transpose_to(h16, R_h16, qs, 8, h2T[s], R_h2T[s])
                P.dma("sp", h2TD[blk][:, 0:8 * qs], h2T[s][:, 0:8 * qs], reads=[R_h2T[s]], writes=[R_h2TD[blk]], defer=True)
                yield

            def run_staggered(gens, lag):
                active = []
                pending = list(gens)
                tick = 0
                while active or pending:
                    if pending and (not active or tick >= lag):
                        active.append(pending.pop(0))
                        tick = 0
                    for g in list(active):
                        try:
                            next(g)
                        except StopIteration:
                            active.remove(g)
                    tick += 1

            blocks = [(16, 2, 126, 0), (17, 16, 0, 1)] + [(i, 128, 0, 0) for i in range(16)]
            run_staggered([phaseB_block(b_, q_, r_, m_, pos % 4) for pos, (b_, q_, r_, m_) in enumerate(blocks)], 3)
            checkpoint('phaseB')
            P.flush(block)

        P.barrier()
        with ExitStack() as sc:
            wdb = sb(sc, "wdb", [128, NFC * 1024], BF16)
            R_wd = Res("wd")
            wst = [sb(sc, "wstc%d" % k, [128, 2048], F32) for k in range(2)]
            R_wst = [Res("wstc%d" % k) for k in range(2)]
            wsl = [sb(sc, "wsl%d" % k, [128, 2048], BF16) for k in range(2)]
            R_wsl = [Res("wsl%d" % k) for k in range(2)]
            R_wslB = [Res("wslB%d" % k) for k in range(2)]
            hT2 = [sb(sc, "hT%d" % k, [128, NFC * 512], BF16) for k in range(2)]
            R_hT2 = [Res("hT%d" % k) for k in range(2)]
            hTm = sb(sc, "hTm", [128, NFC * 16], BF16)
            R_hTm = Res("hTm")
            h2Tg = [sb(sc, "h2Tg%d" % k, [128, 8 * 512], BF16) for k in range(2)]
            R_h2Tg = [Res("h2Tg%d" % k) for k in range(2)]
            h2Tm = sb(sc, "h2Tm", [128, 8 * 18], BF16)
            R_h2Tm = Res("h2Tm")
            Gb = [sb(sc, "Gb%d" % k, [128, 514], F32) for k in range(3)]
            R_Gb = [Res("Gb%d" % k) for k in range(3)]
            Gs = sb(sc, "Gs", [128, 18], F32)
            R_Gs = Res("Gs")
            t0b = [sb(sc, "t0b%d" % k, [128, 512], F32) for k in range(3)]
            R_t0 = [Res("t0%d" % k) for k in range(3)]
            geb = [sb(sc, "geb%d" % k, [128, 512], F32) for k in range(3)]
            R_ge = [Res("ge%d" % k) for k in range(3)]
            t1b = [sb(sc, "t1b%d" % k, [128, 512], F32) for k in range(3)]
            R_t1b = [Res("t1b%d" % k) for k in range(3)]
            t2b = [sb(sc, "t2b%d" % k, [128, 512], F32) for k in range(3)]
            R_t2b = [Res("t2b%d" % k) for k in range(3)]
            t0s = sb(sc, "t0s", [128, 16], F32)
            ges = sb(sc, "ges", [128, 16], F32)
            R_ts = Res("ts")
            carry = sb(sc, "carry", [128, NFC * 2], F32)
            R_carry = [Res("carry%d" % c) for c in range(NFC)]
            sfc = sb(sc, "sfc", [128, NFC * 2], F32)
            R_sfc = Res("sfc")
            sconv = sb(sc, "sconv", [128, NFC * 2], F32)
            wconv = sb(sc, "wconv", [128, NFC * 3], F32)
            bconv = sb(sc, "bconv", [128, NFC], F32)
            flag = sb(sc, "flag", [128, 1], F32)
            R_cc = Res("cc")
            ln3 = sb(sc, "ln3", [128, 2 * 1024], F32)
            R_ln3 = Res("ln3")
            h2r = [sb(sc, "h2r%d" % k, [128, 1024], F32) for k in range(2)]
            R_h2r = [Res("h2r%d" % k) for k in range(2)]
            yA = sb(sc, "yA", [128, 1024], F32)
            R_yA = Res("yA")
            yB = [sb(sc, "yB%d" % k, [128, 1024], F32) for k in range(2)]
            R_yB = [Res("yB%d" % k) for k in range(2)]
            stat = sb(sc, "statc", [128, 32], F32)
            R_stat = Res("statc")

            P.dma("sp", sconv[:, :], I["sconvT"][:, :], writes=[R_cc])
            P.dma("sp", wconv[:, :], I["wconvT"][:, :], writes=[R_cc])
            P.dma("sp", bconv[:, :], I["bconvT"][:, :], writes=[R_cc])
            P.dma("sp", flag[:, :], I["flag"][:, :], writes=[R_cc])
            for k in range(2):
                P.dma("sp", ln3[:, k * 1024:(k + 1) * 1024], I["lnp"][4 + k:5 + k, :].to_broadcast([128, 1024]), writes=[R_ln3])
            k = 0
            for c0 in range(0, NFC * 1024, 2048):
                s = k % 2
                P.dma("sp", wst[s][:, :], I["wdown"][:, c0:c0 + 2048], writes=[R_wst[s]])
                if k % 2 == 0:
                    P.op("act", _call("activation", out=wdb[:, c0:c0 + 2048], in_=wst[s][:, :], func=AF.Copy), reads=[R_wst[s]], writes=[R_wd])
                else:
                    P.op("pool", _call("tensor_copy", out=wdb[:, c0:c0 + 2048], in_=wst[s][:, :]), reads=[R_wst[s]], writes=[R_wd])
                k += 1
            P.dma("sp", h2Tm[:, :].rearrange("p (c q) -> p c q", q=18)[:, :, 0:2], h2TD[16][:, 0:16].rearrange("p (c q) -> p c q", q=2),
                  reads=[R_h2TD[16]], writes=[R_h2Tm], slow=True)
            P.dma("sp", h2Tm[:, :].rearrange("p (c q) -> p c q", q=18)[:, :, 2:18], h2TD[17][:, 0:128].rearrange("p (c q) -> p c q", q=16),
                  reads=[R_h2TD[17]], writes=[R_h2Tm], slow=True)

            checkpoint('phaseC_pre')
            UB = [0, 2, 4]
            GBK = [1, 3, 5]
            MB = 7
            YB = [6, 7]
            wk = [0]

            def ln3_out(pre_banks, qs, h2src, R_h2src, dst_ap, ys, R_ys):
                for n, bank in enumerate(pre_banks):
                    P.op("dve", _call("scalar_tensor_tensor",
                        out=yA[0:qs, n * 512:(n + 1) * 512], in0=h2src[0:qs, n * 512:(n + 1) * 512], scalar=ALPHA, in1=pb[bank][0:qs, :],
                        op0=ALU.mult, op1=ALU.add),
                        reads=[R_h2src, R_pb[bank]], writes=[R_yA])
                for c in range(2):
                    P.op("dve", _call("bn_stats", out=stat[0:qs, c * 6:(c + 1) * 6], in_=yA[0:qs, c * 512:(c + 1) * 512]),
                         reads=[R_yA], writes=[R_stat])
                P.op("dve", _call("bn_aggr", out=stat[0:qs, 12:14], in_=stat[0:qs, 0:12]), reads=[R_stat], writes=[R_stat])
                P.op("dve", _call("tensor_scalar", out=stat[0:qs, 14:15], in0=stat[0:qs, 13:14], scalar1=LN_EPS, scalar2=None, op0=ALU.add),
                     reads=[R_stat], writes=[R_stat])
                P.op("act", _call("activation", out=stat[0:qs, 15:16], in_=stat[0:qs, 14:15], func=AF.Sqrt), reads=[R_stat], writes=[R_stat])
                P.op("dve", _call("reciprocal", out=stat[0:qs, 16:17], in_=stat[0:qs, 15:16]), reads=[R_stat], writes=[R_stat])
                P.op("dve", _call("scalar_tensor_tensor", out=stat[0:qs, 17:18], in0=stat[0:qs, 12:13], scalar=-1.0, in1=stat[0:qs, 16:17],
                                                             op0=ALU.mult, op1=ALU.mult),
                     reads=[R_stat], writes=[R_stat])
                P.op("act", _call("activation", out=ys[0:qs, :], in_=yA[0:qs, :], func=AF.Identity, scale=stat[0:qs, 16:17], bias=stat[0:qs, 17:18]),
                     reads=[R_yA, R_stat], writes=[R_ys])
                P.op("pool", _call("tensor_tensor", out=ys[0:qs, :], in0=ys[0:qs, :], in1=ln3[0:qs, 0:1024], op=ALU.mult),
                     reads=[R_ys, R_ln3], writes=[R_ys])
                P.op("pool", _call("tensor_tensor", out=ys[0:qs, :], in0=ys[0:qs, :], in1=ln3[0:qs, 1024:2048], op=ALU.add),
                     reads=[R_ys, R_ln3], writes=[R_ys])
                P.dma("sp", dst_ap, ys[0:qs, :], reads=[R_ys], defer=True)

            def load_h2Tg(grp):
                gs = grp % 2
                for bi in range(4):
                    blk = grp * 4 + bi
                    P.dma("sp", h2Tg[gs][:, :].rearrange("p (c q) -> p c q", q=512)[:, :, bi * 128:(bi + 1) * 128],
                          h2TD[blk][:, :].rearrange("p (c q) -> p c q", q=128), reads=[R_h2TD[blk]], writes=[R_h2Tg[gs]])

            def c_s1(grp, c):
                s = (grp * NFC + c) % 2
                P.dma("sp", wst[s][:, :], I["wup"][c], writes=[R_wst[s]])
                P.op("dve", _call("tensor_copy", out=wsl[s][:, 0:1024], in_=wst[s][:, 0:1024]), reads=[R_wst[s]], writes=[R_wsl[s]])
                P.op("dve", _call("tensor_copy", out=wsl[s][:, 1024:2048], in_=wst[s][:, 1024:2048]), reads=[R_wst[s]], writes=[R_wslB[s]])

            def c_s2(grp, c):
                s = (grp * NFC + c) % 2
                gs = grp % 2
                mo = (c % 2) * 64
                if grp == 0:
                    for part, oc in ((0, mo), (1, mo + 32)):
                        for kc in range(KC):
                            P.op("pe", _call("matmul", out=pb[MB][:, oc:oc + 18], lhsT=wsl[s][:, kc * 256 + part * 128: kc * 256 + (part + 1) * 128],
                                             rhs=h2Tm[:, kc * 18:(kc + 1) * 18], start=(kc == 0), stop=(kc == KC - 1)),
                                 reads=[R_wsl[s], R_wslB[s], R_h2Tm], writes=[R_pb[MB]])
                k3 = (grp * NFC + c) % 3
                ub, gbk = UB[k3], GBK[k3]
                for part, bank in ((0, ub), (1, gbk)):
                    for kc in range(KC):
                        P.op("pe", _call("matmul", out=pb[bank][:, :], lhsT=wsl[s][:, kc * 256 + part * 128: kc * 256 + (part + 1) * 128],
                                         rhs=h2Tg[gs][:, kc * 512:(kc + 1) * 512], start=(kc == 0), stop=(kc == KC - 1)),
                             reads=[R_wsl[s], R_wslB[s], R_h2Tg[gs]], writes=[R_pb[bank]])

            def c_s3(grp, c):
                hTg, R_hTg = hT2[grp % 2], R_hT2[grp % 2]
                mo = (c % 2) * 64
                if grp == 0:
                    P.op("dve", _call("tensor_scalar", out=carry[:, c * 2:(c + 1) * 2], in0=pb[MB][:, mo + 32:mo + 34], scalar1=flag[:, 0:1],
                                      scalar2=None, op0=ALU.mult),
                         reads=[R_pb[MB], R_cc], writes=[R_carry[c]])
                    P.op("act", _call("activation", out=Gs[:, 0:2], in_=sconv[:, c * 2:(c + 1) * 2], func=AF.Copy), reads=[R_cc], writes=[R_Gs])
                    P.op("act", _call("activation", out=Gs[:, 2:18], in_=pb[MB][:, mo + 34:mo + 50], func=AF.Copy), reads=[R_pb[MB]], writes=[R_Gs])
                    P.op("act", _call("activation", out=t0s[:, :], in_=Gs[:, 2:18], func=AF.Identity, scale=wconv[:, c * 3 + 2:c * 3 + 3],
                                      bias=bconv[:, c:c + 1]),
                         reads=[R_Gs, R_cc], writes=[R_ts])
                    P.op("dve", _call("scalar_tensor_tensor", out=t0s[:, :], in0=Gs[:, 1:17], scalar=wconv[:, c * 3 + 1:c * 3 + 2], in1=t0s[:, :],
                                      op0=ALU.mult, op1=ALU.add),
                         reads=[R_Gs, R_cc, R_ts], writes=[R_ts])
                    P.op("dve", _call("scalar_tensor_tensor", out=t0s[:, :], in0=Gs[:, 0:16], scalar=wconv[:, c * 3:c * 3 + 1], in1=t0s[:, :],
                                      op0=ALU.mult, op1=ALU.add),
                         reads=[R_Gs, R_cc, R_ts], writes=[R_ts])
                    P.op("act", _call("activation", out=ges[:, :], in_=t0s[:, :], func=AF.Gelu_apprx_tanh), reads=[R_ts], writes=[R_ts])
                    P.op("dve", _call("tensor_tensor", out=hTm[:, c * 16:(c + 1) * 16], in0=pb[MB][:, mo + 2:mo + 18], in1=ges[:, :], op=ALU.mult),
                         reads=[R_pb[MB], R_ts], writes=[R_hTm])
                    P.op("act", _call("activation", out=sfc[:, c * 2:(c + 1) * 2], in_=Gs[:, 16:18], func=AF.Copy), reads=[R_Gs], writes=[R_sfc])
                k3 = (grp * NFC + c) % 3
                ub, gbk = UB[k3], GBK[k3]
                G, R_G = Gb[k3], R_Gb[k3]
                t0, R_t = t0b[k3], R_t0[k3]
                ge, R_g = geb[k3], R_ge[k3]
                t1, R_t1 = t1b[k3], R_t1b[k3]
                t2, R_t2 = t2b[k3], R_t2b[k3]
                P.op("act", _call("activation", out=G[:, 0:2], in_=carry[:, c * 2:(c + 1) * 2], func=AF.Copy),
                     reads=[R_carry[c]], writes=[R_G])
                P.op("act", _call("activation", out=G[:, 2:514], in_=pb[gbk][:, :], func=AF.Copy), reads=[R_pb[gbk]], writes=[R_G])
                P.op("act", _call("activation", out=carry[:, c * 2:(c + 1) * 2], in_=G[:, 512:514], func=AF.Copy),
                     reads=[R_G], writes=[R_carry[c]])
                P.op("act", _call("activation", out=t0[:, :], in_=G[:, 2:514], func=AF.Identity,
                                  scale=wconv[:, c * 3 + 2:c * 3 + 3], bias=bconv[:, c:c + 1]),
                     reads=[R_G, R_cc], writes=[R_t])
                P.op("act", _call("activation", out=t1[:, :], in_=G[:, 1:513], func=AF.Identity, scale=wconv[:, c * 3 + 1:c * 3 + 2]),
                     reads=[R_G, R_cc], writes=[R_t1])
                P.op("act", _call("activation", out=t2[:, :], in_=G[:, 0:512], func=AF.Identity, scale=wconv[:, c * 3:c * 3 + 1]),
                     reads=[R_G, R_cc], writes=[R_t2])
                P.op("dve", _call("tensor_tensor", out=t0[:, :], in0=t0[:, :], in1=t1[:, :], op=ALU.add), reads=[R_t, R_t1], writes=[R_t])
                P.op("dve", _call("tensor_tensor", out=t0[:, :], in0=t0[:, :], in1=t2[:, :], op=ALU.add), reads=[R_t, R_t2], writes=[R_t])
                P.op("act", _call("activation", out=ge[:, :], in_=t0[:, :], func=AF.Gelu_apprx_tanh), reads=[R_t], writes=[R_g])
                P.op("dve", _call("tensor_tensor", out=hTg[:, c * 512:(c + 1) * 512], in0=pb[ub][:, :], in1=ge[:, :], op=ALU.mult),
                     reads=[R_pb[ub], R_g], writes=[R_hTg])

            def c_down(grp):
                hTg, R_hTg = hT2[grp % 2], R_hT2[grp % 2]
                if grp == 0:
                    for n, bank in enumerate(YB):
                        for c in range(NFC):
                            P.op("pe", _call("matmul", out=pb[bank][0:16, :], lhsT=hTm[:, c * 16:(c + 1) * 16],
                                             rhs=wdb[:, c * 1024 + n * 512: c * 1024 + (n + 1) * 512], start=(c == 0), stop=(c == NFC - 1)),
                                 reads=[R_hTm, R_wd], writes=[R_pb[bank]])
                    P.dma("sp", h2r[0][0:16, :], h2D[17 * 128: 17 * 128 + 16, :], reads=[R_h2D[17]], writes=[R_h2r[0]])
                    ln3_out(YB, 16, h2r[0], R_h2r[0], O["ys"][:, :], yB[0], R_yB[0])
                    P.dma("sp", O["sfcT"][:, :], sfc[:, :], reads=[R_sfc], defer=True)
                for bi in range(4):
                    blk = grp * 4 + bi
                    hs = blk % 2
                    P.dma("sp", h2r[hs][:, :], h2D[blk * 128:(blk + 1) * 128, :], reads=[R_h2D[blk]], writes=[R_h2r[hs]])
                    for n, bank in enumerate(YB):
                        for c in range(NFC):
                            P.op("pe", _call("matmul", out=pb[bank][:, :], lhsT=hTg[:, c * 512 + bi * 128: c * 512 + (bi + 1) * 128],
                                             rhs=wdb[:, c * 1024 + n * 512: c * 1024 + (n + 1) * 512], start=(c == 0), stop=(c == NFC - 1)),
                                 reads=[R_hTg, R_wd], writes=[R_pb[bank]])
                    ln3_out(YB, 128, h2r[hs], R_h2r[hs], O["y"][blk * 128:(blk + 1) * 128, :], yB[hs], R_yB[hs])

            seq = [(grp, c) for grp in range(4) for c in range(NFC)]
            nseq = len(seq)
            load_h2Tg(0)
            load_h2Tg(1)
            for idx in range(nseq + 2):
                if idx < nseq:
                    c_s1(*seq[idx])
                if 1 <= idx <= nseq:
                    c_s2(*seq[idx - 1])
                if idx >= 2:
                    g3, c3 = seq[idx - 2]
                    c_s3(g3, c3)
                    if c3 == NFC - 1:
                        c_down(g3)
                        if g3 + 2 < 4:
                            load_h2Tg(g3 + 2)
            P.dma("sp", O["fcT"][:, :], carry[:, :], reads=R_carry, defer=True)
            P.finish()
            P.flush(block)
    return nc


def _t5_bucket(rel):
    half, max_exact = 16, 8
    n = np.abs(rel)
    log_ratio = np.log(np.maximum(n, 1).astype(np.float32) / max_exact) / math.log(128 / max_exact)
    large = np.minimum(max_exact + (log_ratio * (half - max_exact)).astype(np.int32), half - 1)
    return np.where(rel < 0, half, 0) + np.where(n < max_exact, n, large)


def _host_inputs(inp):
    f32 = np.float32
    x_prompt = np.asarray(inp["x_prompt"], f32)
    x_sample = np.asarray(inp["x_sample"], f32)
    w_in = np.asarray(inp["w_in"], f32)[0]
    qa, ka, va = w_in[:, 0:512], w_in[:, 512:1024], w_in[:, 1024:1536]
    qb, kb, vb = w_in[:, 1536:2048], w_in[:, 2048:2176], w_in[:, 2176:2304]
    qi, ki, wi = w_in[:, 2304:2816], w_in[:, 2816:2880], w_in[:, 2880:2888]
    qbp = np.concatenate([np.concatenate([qb[:, r * 64:(r + 1) * 64], qb[:, (4 + r) * 64:(5 + r) * 64]], axis=1) for r in range(4)], axis=1)
    winp = np.concatenate([qa, ka, qbp, kb, qi, ki, ki, va, vb, wi], axis=1)
    assert winp.shape[1] == NCOL

    def kc_layout(w):
        n = w.shape[1]
        return np.ascontiguousarray(w.reshape(8, 128, n).transpose(1, 0, 2).reshape(128, 8 * n))

    shared = {}
    shared["win"] = kc_layout(winp)
    shared["wo"] = kc_layout(np.asarray(inp["w_o"], f32)[0])
    shared["wmq"] = kc_layout(np.asarray(inp["w_mq"], f32)[0])
    shared["wmk"] = kc_layout(np.asarray(inp["w_mk"], f32)[0])
    shared["wmv"] = kc_layout(np.asarray(inp["w_mv"], f32)[0])
    wmo = np.asarray(inp["w_mo"], f32)[0]
    shared["wmo"] = np.ascontiguousarray(wmo.reshape(4, 128, 1024).transpose(1, 0, 2).reshape(128, 4096))
    w_up = np.asarray(inp["w_up"], f32)[0]
    wu = w_up[:, :DFF].reshape(8, 128, NFC, 128)
    wg = w_up[:, DFF:].reshape(8, 128, NFC, 128)
    wup = np.stack([wu, wg], axis=3)
    shared["wup"] = np.ascontiguousarray(wup.transpose(2, 1, 0, 3, 4).reshape(NFC, 128, 8 * 256))
    w_down = np.asarray(inp["w_down"], f32)[0]
    shared["wdown"] = np.ascontiguousarray(w_down.reshape(NFC, 128, 1024).transpose(1, 0, 2).reshape(128, NFC * 1024))
    shared["lnp"] = np.ascontiguousarray(np.stack([np.asarray(inp[k], f32)[0] for k in ("ln1_g", "ln1_b", "ln2_g", "ln2_b", "ln3_g", "ln3_b")]))
    w_conv = np.asarray(inp["w_conv"], f32)[0]
    shared["wconvT"] = np.ascontiguousarray(w_conv.reshape(3, NFC, 128).transpose(2, 1, 0).reshape(128, NFC * 3))
    shared["bconvT"] = np.ascontiguousarray(np.asarray(inp["b_conv"], f32)[0].reshape(NFC, 128).T)
    shared["ident"] = np.eye(128, dtype=f32)
    tabA = np.asarray(inp["a_rel_bias"], f32)[0]
    qq = np.arange(128)[:, None]
    kk = np.arange(640)[None, :]
    kpos = kk - 512
    rel = qq - kpos
    cq = qq // 64
    kch = np.floor_divide(kpos, 64)
    allowed = (kch >= cq - 8) & (kch <= cq)
    bias = tabA[np.clip(rel, -64, 64) + 64]
    AB = np.where(allowed[:, :, None], bias, f32(NEGM)).astype(f32)
    shared["AB"] = np.ascontiguousarray(AB.transpose(0, 2, 1).reshape(128, 8 * ABW))
    js = np.arange(16)[:, None]
    ks = np.arange(528)[None, :]
    ABs = tabA[np.clip(512 + js - ks, -64, 64) + 64]
    shared["ABs"] = np.ascontiguousarray(ABs.transpose(0, 2, 1).reshape(16, 8 * 528)).astype(f32)
    t5 = np.asarray(inp["t5_bias"], f32)
    relB = np.arange(128)[:, None] - np.arange(256)[None, :] + 128
    Bn = t5[_t5_bucket(relB)]
    shared["Bn"] = np.ascontiguousarray(Bn.transpose(0, 2, 1).reshape(128, 8 * BNW)).astype(f32)
    relBs = 128 + np.arange(16)[:, None] - np.arange(144)[None, :]
    Bns = t5[_t5_bucket(relBs)]
    shared["Bns"] = np.ascontiguousarray(Bns.transpose(0, 2, 1).reshape(16, 8 * 144)).astype(f32)
    shared["C15"] = np.ascontiguousarray(np.broadcast_to(t5[15][None, :], (128, 8))).astype(f32)
    dm = np.zeros((128, 128), f32)
    dm[0:64, 64:128] = NEGM
    shared["diagmask"] = dm

    mem_prompt = np.asarray(inp["mem_prompt"], f32)
    maps = []
    for c in range(8):
        b, half = c // 2, c % 2
        m = dict(shared)
        xk = np.zeros((4096, 1024), f32)
        if half == 1:
            xk[:] = x_prompt[b]
        else:
            xk[2048:] = x_prompt[b, :2048]
        m["xkT"] = np.ascontiguousarray(xk.reshape(32, 128, 8, 128).transpose(0, 3, 2, 1).reshape(32, 128, 1024))
        xs = x_sample[c]
        m["xsT"] = np.ascontiguousarray(xs.reshape(16, 8, 128).transpose(2, 1, 0).reshape(128, 128))
        xres = np.zeros((NBLK * 128, 1024), f32)
        xres[0:2048] = xk[2048:]
        xres[2048:2050] = xk[2046:2048]
        xres[17 * 128:17 * 128 + 16] = xs
        m["xres"] = xres
        m["memT"] = np.ascontiguousarray(mem_prompt[b].reshape(256, 8, 128).transpose(2, 1, 0).reshape(128, 2048))
        cmk = np.asarray(inp["cache_mem_k"], f32)[0, c]
        m["cmkT"] = np.ascontiguousarray(cmk.transpose(2, 1, 0).reshape(128, 1024))
        m["cmv"] = np.ascontiguousarray(np.asarray(inp["cache_mem_v"], f32)[0, c].reshape(256, 512))
        cak = np.asarray(inp["cache_a_k"], f32)[0, c]
        m["cakT"] = np.ascontiguousarray(cak.reshape(512, 4, 2, 64).transpose(2, 3, 1, 0).reshape(128, 2048))
        m["cav"] = np.ascontiguousarray(np.asarray(inp["cache_a_v"], f32)[0, c].reshape(512, 512))
        cbk = np.asarray(inp["cache_b_k"], f32)[0, c]
        m["cbkT"] = np.ascontiguousarray(cbk.reshape(2048, 128).T)
        m["cbv"] = np.ascontiguousarray(np.asarray(inp["cache_b_v"], f32)[0, c].reshape(2048, 128))
        cbi = np.asarray(inp["cache_b_kidx"], f32)[0, c]
        m["cbiT"] = np.ascontiguousarray(np.concatenate([cbi.T, cbi.T], axis=0))
        sc_ = np.asarray(inp["state_ffn_conv"], f32)[0, c]
        m["sconvT"] = np.ascontiguousarray(sc_.reshape(2, NFC, 128).transpose(2, 1, 0).reshape(128, NFC * 2))
        m["colmask"] = np.full((128, 1), NEGM if half == 0 else 0.0, f32)
        kv = np.ones((128, NT), f32)
        if half == 0:
            kv[:, 0:16] = 0.0
        m["kvalid"] = kv
        m["flag"] = np.full((128, 1), float(half), f32)
        maps.append(m)
    return maps


_NC_CACHE = {}


def _run(inputs, debug=False):
    key = bool(debug)
    if key not in _NC_CACHE:
        _NC_CACHE[key] = build_program(debug=debug)
    nc = _NC_CACHE[key]
    maps = _host_inputs(inputs)
    res = run_bass_kernel_spmd(nc, maps, core_ids=list(range(8)))
    return res.results


def kernel(**inputs):
    R = _run(inputs)
    f32 = np.float32
    y = np.zeros((4, 4096, 1024), f32)
    ys = np.zeros((8, 16, 1024), f32)
    pak = np.zeros((1, 4, 512, 8, 64), f32)
    pav = np.zeros((1, 4, 512, 8, 64), f32)
    pbk = np.zeros((1, 4, 4096, 2, 64), f32)
    pbv = np.zeros((1, 4, 4096, 2, 64), f32)
    pbi = np.zeros((1, 4, 4096, 64), f32)
    pmk = np.zeros((1, 4, 256, 4, 128), f32)
    pmv = np.zeros((1, 4, 256, 4, 128), f32)
    pfc = np.zeros((1, 4, 2, DFF), f32)
    sak = np.zeros((1, 8, 16, 8, 64), f32)
    sav = np.zeros((1, 8, 16, 8, 64), f32)
    sbk = np.zeros((1, 8, 16, 2, 64), f32)
    sbv = np.zeros((1, 8, 16, 2, 64), f32)
    sbi = np.zeros((1, 8, 16, 64), f32)
    sfc = np.zeros((1, 8, 2, DFF), f32)
    for c in range(8):
        b, half = c // 2, c % 2
        r = R[c]
        y[b, half * 2048:(half + 1) * 2048] = np.asarray(r["y"], f32)
        ys[c] = np.asarray(r["ys"], f32)
        if half == 1:
            akT = np.asarray(r["akT"], f32).reshape(2, 64, 4, 512)
            pak[0, b] = akT.transpose(3, 2, 0, 1).reshape(512, 8, 64)
            pav[0, b] = np.asarray(r["av"], f32).reshape(512, 8, 64)
            pbk[0, b] = np.asarray(r["bkT"], f32).T.reshape(4096, 2, 64)
            pbv[0, b] = np.asarray(r["bv"], f32).reshape(4096, 2, 64)
            pbi[0, b] = np.asarray(r["biT"], f32).T
            pmk[0, b] = np.asarray(r["mkT"], f32).reshape(128, 4, 256).transpose(2, 1, 0)
            pmv[0, b] = np.asarray(r["mv"], f32).reshape(256, 4, 128)
            pfc[0, b] = np.asarray(r["fcT"], f32).reshape(128, NFC, 2).transpose(2, 1, 0).reshape(2, DFF)
        sakT = np.asarray(r["sakT"], f32).reshape(2, 64, 4, 16)
        sak[0, c] = sakT.transpose(3, 2, 0, 1).reshape(16, 8, 64)
        sav[0, c] = np.asarray(r["sav"], f32).reshape(16, 8, 64)
        sbk[0, c] = np.asarray(r["sbkT"], f32).T.reshape(16, 2, 64)
        sbv[0, c] = np.asarray(r["sbv"], f32).reshape(16, 2, 64)
        sbi[0, c] = np.asarray(r["sbiT"], f32).T
        sfc[0, c] = np.asarray(r["sfcT"], f32).reshape(128, NFC, 2).transpose(2, 1, 0).reshape(2, DFF)
    return (y, ys, pak, pav, pbk, pbv, pbi, pmk, pmv, pfc, sak, sav, sbk, sbv, sbi, sfc)
```

```python
import math
from contextlib import ExitStack

import numpy as np
import concourse.bass as bass
import concourse.mybir as mybir
from concourse.bass_utils import run_bass_kernel_spmd

F32 = mybir.dt.float32
BF16 = mybir.dt.bfloat16
AF = mybir.ActivationFunctionType
ALU = mybir.AluOpType

D = 1024
KC = 8
NT = 32
NCOL = 2952
C_QA, C_KA, C_QB, C_KB, C_QI, C_KI, C_VA, C_VB, C_WI = 0, 512, 1024, 1536, 1664, 2176, 2304, 2816, 2944
DFF = 2816
NFC = 22
ALPHA = 2.0 ** 0.25
LN_EPS = 1e-5
NEGM = -30000.0
NIT = 17
BIS_W0 = 16.0
ABW = 640
BNW = 256
NBLK = 18


class Res:
    __slots__ = ("lw", "rd", "name", "excl")

    def __init__(self, name="", excl=False):
        self.lw = None
        self.rd = {}
        self.name = name
        self.excl = excl


def _call(name, *args, **kw):
    return lambda e: getattr(e, name)(*args, **kw)


class Prog:
    ENG = ("pe", "act", "dve", "pool", "sp")

    def __init__(self, nc, sems, dma_sems):
        self.nc = nc
        self.streams = {e: [] for e in self.ENG}
        self.sem = sems
        self.cnt = {e: 0 for e in self.ENG}
        self.seen = {e: {} for e in self.ENG}
        self.dsems = dma_sems
        self.dval = [0] * len(dma_sems)
        self.dnext = 0
        self.semh = dict(sems)
        for i, h in enumerate(dma_sems):
            self.semh[("d", i)] = h
        self.ninst = 0
        self.dead = False
        self.deferred = []
        self.defer_lag = 48

    def _deps(self, reads, writes, eng=None):
        d = {}
        for r in reads:
            if r.lw is not None:
                k, v = r.lw
                if d.get(k, 0) < v:
                    d[k] = v
            if r.excl:
                for k, v in r.rd.items():
                    if k != eng and d.get(k, 0) < v:
                        d[k] = v
        for w in writes:
            if w.lw is not None:
                k, v = w.lw
                if d.get(k, 0) < v:
                    d[k] = v
            for k, v in w.rd.items():
                if d.get(k, 0) < v:
                    d[k] = v
        return d

    def _wait(self, eng, deps):
        for k, v in deps.items():
            if k == "pe" and eng == "pe":
                continue
            if self.seen[eng].get(k, 0) >= v:
                continue
            self.seen[eng][k] = v
            h = self.semh[k]
            self.streams[eng].append(lambda e, h=h, v=v: e.wait_ge(h, v))

    def _flush_deferred(self, force=False, reads=(), writes=()):
        if not self.deferred:
            return
        conflict = force
        if not conflict:
            ws = set(id(w) for w in writes)
            rs = set(id(r) for r in reads)
            for d in self.deferred:
                dr = set(id(x) for x in d[3])
                dw = set(id(x) for x in d[4])
                if (ws & dr) or (ws & dw) or (rs & dw):
                    conflict = True
                    break
        if conflict:
            pend, self.deferred = self.deferred, []
            for d in pend:
                self._dma_now(d[0], d[1], d[2], d[3], d[4], d[5])
            return
        while self.deferred and self.ninst - self.deferred[0][6] >= self.defer_lag:
            d = self.deferred.pop(0)
            self._dma_now(d[0], d[1], d[2], d[3], d[4], d[5])

    def op(self, eng, fn, reads=(), writes=()):
        if self.dead:
            return
        self._flush_deferred(False, reads, writes)
        self._wait(eng, self._deps(reads, writes, eng))
        self.cnt[eng] += 1
        n = self.cnt[eng]
        h = self.sem[eng]
        self.streams[eng].append(lambda e, fn=fn, h=h: fn(e).then_inc(h, 1))
        self.ninst += 1
        for r in reads:
            if r.rd.get(eng, 0) < n:
                r.rd[eng] = n
        for w in writes:
            w.lw = (eng, n)
            w.rd = {}

    def dma(self, q, out, in_, reads=(), writes=(), slow=False, defer=False):
        if self.dead:
            return
        if defer:
            self._flush_deferred(False, reads, writes)
            self.deferred.append((q, out, in_, list(reads), list(writes), slow, self.ninst))
            return
        self._flush_deferred(False, reads, writes)
        self._dma_now(q, out, in_, reads, writes, slow)

    def _dma_now(self, q, out, in_, reads=(), writes=(), slow=False):
        deps = self._deps(reads, writes)
        i = self.dnext
        self.dnext = (i + 1) % len(self.dsems)
        k = ("d", i)
        if self.dval[i] > 0 and deps.get(k, 0) < self.dval[i]:
            deps[k] = self.dval[i]
        self._wait(q, deps)
        self.dval[i] += 16
        v = self.dval[i]
        h = self.dsems[i]
        if slow:
            self.streams[q].append(
                lambda e, out=out, in_=in_, h=h: e.dma_start(out=out, in_=in_, allow_slow_non_contiguous=True).then_inc(h, 16))
        else:
            self.streams[q].append(lambda e, out=out, in_=in_, h=h: e.dma_start(out=out, in_=in_).then_inc(h, 16))
        self.ninst += 1
        for r in reads:
            if r.rd.get(k, 0) < v:
                r.rd[k] = v
        for w in writes:
            w.lw = (k, v)
            w.rd = {}

    def barrier(self):
        if self.dead:
            return
        self._flush_deferred(True)
        deps = {e: self.cnt[e] for e in self.ENG if self.cnt[e] > 0}
        for i, v in enumerate(self.dval):
            if v > 0:
                deps[("d", i)] = v
        for e in self.ENG:
            self._wait(e, dict(deps))

    def finish(self):
        self._flush_deferred(True)
        deps = {("d", i): v for i, v in enumerate(self.dval) if v > 0}
        self._wait("sp", deps)

    def flush(self, block):
        self._flush_deferred(True)
        s = self.streams
        self.streams = {e: [] for e in self.ENG}

        def mk(lst):
            def body(e):
                for f in lst:
                    f(e)
            return body

        block.tensor(mk(s["pe"]))
        block.scalar(mk(s["act"]))
        block.vector(mk(s["dve"]))
        block.gpsimd(mk(s["pool"]))
        block.sync(mk(s["sp"]))


def build_program(debug=False, stop_at=None):
    nc = bass.Bass("TRN2", target_bir_lowering=False)

    def din(name, shape, dt=F32):
        return nc.dram_tensor(name, list(shape), dt, kind="ExternalInput").ap()

    def dout(name, shape, dt=F32):
        return nc.dram_tensor(name, list(shape), dt, kind="ExternalOutput").ap()

    def dscr(name, shape, dt):
        return nc.dram_tensor(name, list(shape), dt, kind="Internal").ap()

    I = {}
    I["xkT"] = din("xkT", [NT, 128, 1024])
    I["xsT"] = din("xsT", [128, 8 * 16])
    I["xres"] = din("xres", [NBLK * 128, 1024])
    I["win"] = din("win", [128, KC * NCOL])
    I["wo"] = din("wo", [128, 8 * 1024])
    I["wmq"] = din("wmq", [128, 8 * 512])
    I["wmk"] = din("wmk", [128, 8 * 512])
    I["wmv"] = din("wmv", [128, 8 * 512])
    I["wmo"] = din("wmo", [128, 4 * 1024])
    I["wup"] = din("wup", [NFC, 128, 8 * 256])
    I["wdown"] = din("wdown", [128, NFC * 1024])
    I["lnp"] = din("lnp", [6, 1024])
    I["wconvT"] = din("wconvT", [128, NFC * 3])
    I["bconvT"] = din("bconvT", [128, NFC])
    I["memT"] = din("memT", [128, 8 * 256])
    I["cmkT"] = din("cmkT", [128, 4 * 256])
    I["cmv"] = din("cmv", [256, 512])
    I["cakT"] = din("cakT", [128, 4 * 512])
    I["cav"] = din("cav", [512, 512])
    I["cbkT"] = din("cbkT", [128, 2048])
    I["cbv"] = din("cbv", [2048, 128])
    I["cbiT"] = din("cbiT", [128, 2048])
    I["sconvT"] = din("sconvT", [128, NFC * 2])
    I["ident"] = din("ident", [128, 128])
    I["AB"] = din("AB", [128, 8 * ABW])
    I["ABs"] = din("ABs", [16, 8 * 528])
    I["Bn"] = din("Bn", [128, 8 * BNW])
    I["Bns"] = din("Bns", [16, 8 * 144])
    I["C15"] = din("C15", [128, 8])
    I["colmask"] = din("colmask", [128, 1])
    I["diagmask"] = din("diagmask", [128, 128])
    I["kvalid"] = din("kvalid", [128, NT])
    I["flag"] = din("flag", [128, 1])

    O = {}
    O["y"] = dout("y", [2048, 1024])
    O["ys"] = dout("ys", [16, 1024])
    O["akT"] = dout("akT", [128, 4 * 512])
    O["av"] = dout("av", [512, 512])
    O["bkT"] = dout("bkT", [128, 4096])
    O["bv"] = dout("bv", [4096, 128])
    O["biT"] = dout("biT", [64, 4096])
    O["mkT"] = dout("mkT", [128, 4 * 256])
    O["mv"] = dout("mv", [256, 512])
    O["fcT"] = dout("fcT", [128, NFC * 2])
    O["sakT"] = dout("sakT", [128, 4 * 16])
    O["sav"] = dout("sav", [16, 512])
    O["sbkT"] = dout("sbkT", [128, 16])
    O["sbv"] = dout("sbv", [16, 128])
    O["sbiT"] = dout("sbiT", [64, 16])
    O["sfcT"] = dout("sfcT", [128, NFC * 2])
    if debug:
        O["dbg_mix"] = dout("dbg_mix", [NBLK * 128, 1024], BF16)
        O["dbg_h2"] = dout("dbg_h2", [NBLK * 128, 1024])
        mixD = O["dbg_mix"]
        h2D = O["dbg_h2"]
    else:
        mixD = dscr("mixD", [NBLK * 128, 1024], BF16)
        h2D = dscr("h2D", [NBLK * 128, 1024], F32)
    h2TD = dscr("h2TD", [NBLK, 128, 1024], BF16)
    R_mixD = [Res("mixD%d" % i) for i in range(NBLK)]
    R_h2D = [Res("h2D%d" % i) for i in range(NBLK)]
    R_h2TD = [Res("h2TD%d" % i) for i in range(NBLK)]

    es = ExitStack()
    with es:
        sems = {e: es.enter_context(nc.semaphore("s_" + e)) for e in Prog.ENG}
        dsems = [es.enter_context(nc.semaphore("d%d" % i)) for i in range(32)]
        P = Prog(nc, sems, dsems)
        block = es.enter_context(nc.Block())

        def checkpoint(name):
            if stop_at is not None and name == stop_at and not P.dead:
                P.finish()
                P.flush(block)
                P.dead = True

        pb = [es.enter_context(nc.psum_tensor("pb%d" % i, [128, 512], F32)) for i in range(8)]
        R_pb = [Res("pb%d" % i, excl=True) for i in range(8)]

        class Rot:
            def __init__(self, idxs):
                self.idxs = idxs
                self.i = 0

            def next(self):
                k = self.idxs[self.i % len(self.idxs)]
                self.i += 1
                return k

        def sb(stack, name, shape, dt):
            return stack.enter_context(nc.sbuf_tensor("sb_" + name, list(shape), dt))

        ident_f = sb(es, "ident_f", [128, 128], F32)
        ident = sb(es, "ident", [128, 512], BF16)
        R_ident = Res("ident")
        P.dma("sp", ident_f[:, :], I["ident"][:, :], writes=[R_ident])
        for r in range(4):
            P.op("act", _call("activation", out=ident[:, r * 128:(r + 1) * 128], in_=ident_f[:, :], func=AF.Copy),
                 reads=[R_ident], writes=[R_ident])

        def run_interleaved(gens):
            gens = [[0.0, i, g] for i, g in enumerate(gens)]
            while gens:
                gens.sort(key=lambda x: (x[0], x[1]))
                ent = gens[0]
                try:
                    c = next(ent[2])
                    ent[0] += (c if c else 1.0)
                except StopIteration:
                    gens.remove(ent)

        with ExitStack() as sa:
            winb = sb(sa, "winb", [128, KC * NCOL], BF16)
            R_win = Res("win")
            kbi = sb(sa, "kbi", [128, 2 * 4096], BF16)
            R_kbi = [Res("kbi%d" % r) for r in range(NT)]
            R_ki = [Res("ki%d" % r) for r in range(NT)]
            vb_aug = sb(sa, "vb_aug", [128, NT * 2 * 65], BF16)
            R_vb = [Res("vb%d" % r) for r in range(NT)]
            kaT = sb(sa, "kaT", [128, 6 * 512], BF16)
            R_ka = [Res("ka%d" % s) for s in range(6)]
            va_aug = sb(sa, "va_aug", [128, 6 * 8 * 65], BF16)
            R_va = [Res("va%d" % s) for s in range(6)]
            ABb = sb(sa, "ABb", [128, 8 * ABW], BF16)
            R_AB = Res("AB")
            Bnb = sb(sa, "Bnb", [128, 8 * BNW], BF16)
            R_Bn = Res("Bn")
            Mnear = [sb(sa, "Mnear%d" % k, [128, 8 * BNW], BF16) for k in range(2)]
            R_Mnear = [Res("Mnear%d" % k) for k in range(2)]
            score = [sb(sa, "score%d" % k, [128, 4096], F32) for k in range(2)]
            R_score = [Res("score%d" % k) for k in range(2)]
            Mb = [sb(sa, "Mb%d" % k, [128, 4096], BF16) for k in range(2)]
            R_M = [Res("M%d" % k) for k in range(2)]
            relu = [sb(sa, "relu%d" % k, [128, 512], BF16) for k in range(3)]
            R_relu = [Res("relu%d" % k) for k in range(3)]
            xstg2 = [sb(sa, "xstg%d" % k, [128, 1024], F32) for k in range(2)]
            R_xstg2 = [Res("xstg%d" % k) for k in range(2)]
            xstg, R_xstg = xstg2[0], R_xstg2[0]
            xTb = [sb(sa, "xTb%d" % k, [128, 1024], BF16) for k in range(2)]
            R_xT = [Res("xT%d" % k) for k in range(2)]
            qaz = [sb(sa, "qaz%d" % k, [128, 1024], BF16) for k in range(2)]
            qbz = [sb(sa, "qbz%d" % k, [128, 1024], BF16) for k in range(3)]
            qiz = [sb(sa, "qiz%d" % k, [128, 1024], BF16) for k in range(2)]
            R_qa = [Res("qa%d" % k) for k in range(2)]
            R_qb = [Res("qb%d" % k) for k in range(3)]
            R_qi = [Res("qi%d" % k) for k in range(2)]
            coef = [sb(sa, "coef%d" % k, [128, 8], F32) for k in range(2)]
            R_coef = [Res("coef%d" % k) for k in range(2)]
            dg = [sb(sa, "dg%d" % k, [128, 1024], BF16) for k in range(2)]
            R_dg = [Res("dg%d" % k) for k in range(2)]
            PTA = [sb(sa, "PTA%d" % k, [128, 512], BF16) for k in range(3)]
            R_PTA = [Res("PTA%d" % k) for k in range(3)]
            PTB = [sb(sa, "PTB%d" % k, [128, 512], BF16) for k in range(3)]
            R_PTB = [Res("PTB%d" % k) for k in range(3)]
            mixb = [sb(sa, "mixb%d" % k, [128, 1024], BF16) for k in range(3)]
            R_mix = [Res("mix%d" % k) for k in range(3)]
            ostg = [sb(sa, "ostg%d" % k, [128, 256], F32) for k in range(2)]
            R_ostg = [Res("ostg%d" % k) for k in range(2)]
            vbstg = [sb(sa, "vbstg%d" % k, [128, 128], F32) for k in range(2)]
            R_vbstg = [Res("vbstg%d" % k) for k in range(2)]
            astg = sb(sa, "astg", [128, 1024], F32)
            R_astg = Res("astg")
            small = [sb(sa, "small%d" % k, [128, 16], F32) for k in range(2)]
            R_small = [Res("small%d" % k) for k in range(2)]
            recA = [sb(sa, "recA%d" % k, [128, 8], F32) for k in range(2)]
            R_recA = [Res("recA%d" % k) for k in range(2)]
            recB = [sb(sa, "recB%d" % k, [128, 8], F32) for k in range(2)]
            R_recB = [Res("recB%d" % k) for k in range(2)]
            colmask = sb(sa, "colmask", [128, 1], F32)
            diagm = sb(sa, "diagm", [128, 128], F32)
            kvalid = sb(sa, "kvalid", [128, NT], F32)
            c15 = sb(sa, "c15", [128, 8], F32)
            ones8 = sb(sa, "ones8", [128, 8], F32)
            R_cst = Res("cst")

            wrot = Rot([0, 1, 2])

            P.dma("sp", colmask[:, :], I["colmask"][:, :], writes=[R_cst])
            P.dma("sp", diagm[:, :], I["diagmask"][:, :], writes=[R_cst])
            P.dma("sp", kvalid[:, :], I["kvalid"][:, :], writes=[R_cst])
            P.dma("sp", c15[:, :], I["C15"][:, :], writes=[R_cst])
            P.op("pool", _call("memset", ones8[:, :], 1.0), writes=[R_cst])
            for k in range(2):
                P.op("pool", _call("memset", qaz[k][:, :], 0.0), writes=[R_qa[k]])
                P.op("pool", _call("memset", qiz[k][:, :], 0.0), writes=[R_qi[k]])
            for k in range(3):
                P.op("pool", _call("memset", qbz[k][:, :], 0.0), writes=[R_qb[k]])

            HW = NCOL // 2
            for kc in range(KC):
                for hh in range(2):
                    stg, R_stg = score[hh], R_score[hh]
                    P.dma("sp", stg[:, 0:HW], I["win"][:, kc * NCOL + hh * HW: kc * NCOL + (hh + 1) * HW], writes=[R_stg])
                    if hh == 0:
                        P.op("act", _call("activation", out=winb[:, kc * NCOL + hh * HW: kc * NCOL + (hh + 1) * HW], in_=stg[:, 0:HW], func=AF.Copy),
                             reads=[R_stg], writes=[R_win])
                    else:
                        P.op("pool", _call("tensor_copy", out=winb[:, kc * NCOL + hh * HW: kc * NCOL + (hh + 1) * HW], in_=stg[:, 0:HW]),
                             reads=[R_stg], writes=[R_win])
            for hh in range(2):
                w = 4 * ABW
                P.dma("sp", score[hh][:, 0:w], I["AB"][:, hh * w:(hh + 1) * w], writes=[R_score[hh]])
                P.op("act", _call("activation", out=ABb[:, hh * w:(hh + 1) * w], in_=score[hh][:, 0:w], func=AF.Copy),
                     reads=[R_score[hh]], writes=[R_AB])
            P.dma("sp", score[0][:, 0:8 * BNW], I["Bn"][:, :], writes=[R_score[0]])
            for h in range(8):
                P.op("dve", _call("tensor_scalar", out=Bnb[:, h * BNW:(h + 1) * BNW], in0=score[0][:, h * BNW:(h + 1) * BNW],
                                  scalar1=c15[:, h:h + 1], scalar2=None, op0=ALU.subtract),
                     reads=[R_score[0], R_cst], writes=[R_Bn])
            checkpoint('consts')

            def win_cols(kc, c0, n):
                return winb[:, kc * NCOL + c0: kc * NCOL + c0 + n]

            def fm_proj(bank, xT, R_x, N, col0, nchunks, ocol=0):
                for j in range(nchunks):
                    for kc in range(KC):
                        P.op("pe", _call("matmul", out=pb[bank][:, ocol + j * N: ocol + (j + 1) * N], lhsT=win_cols(kc, col0 + j * 128, 128),
                                         rhs=xT[:, kc * N:(kc + 1) * N], start=(kc == 0), stop=(kc == KC - 1)),
                             reads=[R_win, R_x], writes=[R_pb[bank]])

            def tm_proj(bank, xT, R_x, N, col0, ncols, ocol=0):
                for kc in range(KC):
                    P.op("pe", _call("matmul", out=pb[bank][0:N, ocol:ocol + ncols], lhsT=xT[:, kc * N:(kc + 1) * N],
                                     rhs=win_cols(kc, col0, ncols), start=(kc == 0), stop=(kc == KC - 1)),
                         reads=[R_win, R_x], writes=[R_pb[bank]])

            def load_xT(r):
                s = r % 2
                P.dma("sp", xstg2[s][:, :], I["xkT"][r], writes=[R_xstg2[s]])
                P.op("pool", _call("tensor_copy", out=xTb[s][:, :], in_=xstg2[s][:, :]), reads=[R_xstg2[s]], writes=[R_xT[s]])

            def kside(r, full):
                s = r % 2
                xT, R_x = xTb[s], R_xT[s]
                so = r % 2
                bk = wrot.next()
                fm_proj(bk, xT, R_x, 128, C_KB, 1)
                fm_proj(bk, xT, R_x, 128, C_KI, 1, ocol=128)
                P.op("act", _call("activation", out=ostg[so][:, :], in_=pb[bk][:, 0:256], func=AF.Copy), reads=[R_pb[bk]], writes=[R_ostg[so]])
                P.op("pool", _call("tensor_copy", out=kbi[:, :].rearrange("p (a c) -> p a c", a=2)[:, :, r * 128:(r + 1) * 128],
                                   in_=ostg[so][:, :].rearrange("p (a c) -> p a c", a=2)),
                     reads=[R_ostg[so]], writes=[R_kbi[r], R_ki[r]])
                P.dma("sp", O["bkT"][:, r * 128:(r + 1) * 128], ostg[so][:, 0:128], reads=[R_ostg[so]], defer=True)
                P.dma("sp", O["biT"][:, r * 128:(r + 1) * 128], ostg[so][0:64, 128:256], reads=[R_ostg[so]], defer=True)
                yield 3.0
                bv_ = wrot.next()
                tm_proj(bv_, xT, R_x, 128, C_VB, 128)
                vbv = vb_aug[:, r * 130:(r + 1) * 130].rearrange("p (g d) -> p g d", d=65)
                P.op("act", _call("activation", out=vbstg[so][:, :], in_=pb[bv_][:, 0:128], func=AF.Copy), reads=[R_pb[bv_]], writes=[R_vbstg[so]])
                P.op("pool", _call("tensor_copy", out=vbv[:, :, 0:64], in_=vbstg[so][:, :].rearrange("p (g d) -> p g d", d=64)),
                     reads=[R_vbstg[so]], writes=[R_vb[r]])
                P.op("pool", _call("tensor_scalar", out=vbv[:, :, 64:65], in0=ones8[:, 0:2].rearrange("p (g o) -> p g o", o=1),
                                   scalar1=kvalid[:, r:r + 1], scalar2=None, op0=ALU.mult),
                     reads=[R_cst], writes=[R_vb[r]])
                P.dma("sp", O["bv"][r * 128:(r + 1) * 128, :], vbstg[so][:, :], reads=[R_vbstg[so]], defer=True)
                yield 3.0
                if not full:
                    return
                slot = r % 6
                ba = wrot.next()
                fm_proj(ba, xT, R_x, 128, C_KA, 4)
                P.op("act", _call("activation", out=kaT[:, slot * 512:(slot + 1) * 512], in_=pb[ba][:, :], func=AF.Copy),
                     reads=[R_pb[ba]], writes=[R_ka[slot]])
                if r >= 28:
                    P.op("dve", _call("tensor_copy", out=astg[:, 0:512], in_=pb[ba][:, :]), reads=[R_pb[ba]], writes=[R_astg])
                    P.dma("sp", O["akT"].rearrange("p (j t) -> p j t", t=512)[:, :, (r - 28) * 128:(r - 27) * 128],
                          astg[:, 0:512].rearrange("p (j t) -> p j t", t=128), reads=[R_astg], defer=True)
                yield 3.0
                bva = wrot.next()
                tm_proj(bva, xT, R_x, 128, C_VA, 512)
                vav = va_aug[:, slot * 520:(slot + 1) * 520].rearrange("p (h d) -> p h d", d=65)
                P.op("act", _call("activation", out=vav[:, :, 0:64], in_=pb[bva][:, :].rearrange("p (h d) -> p h d", d=64), func=AF.Copy),
                     reads=[R_pb[bva]], writes=[R_va[slot]])
                P.op("pool", _call("tensor_scalar", out=vav[:, :, 64:65], in0=ones8[:, :].rearrange("p (h o) -> p h o", o=1),
                                   scalar1=kvalid[:, r:r + 1], scalar2=None, op0=ALU.mult),
                     reads=[R_cst], writes=[R_va[slot]])
                if r >= 28:
                    P.op("dve", _call("tensor_copy", out=astg[:, 512:1024], in_=pb[bva][:, :]), reads=[R_pb[bva]], writes=[R_astg])
                    P.dma("sp", O["av"][(r - 28) * 128:(r - 27) * 128, :], astg[:, 512:1024], reads=[R_astg], defer=True)
                yield 3.0

            def qside(xT, R_x, qs, st, st3):
                b1 = wrot.next()
                fm_proj(b1, xT, R_x, qs, C_QA, 4)
                for hf in range(2):
                    P.op("act", _call("activation",
                                      out=qaz[st][hf * 64:(hf + 1) * 64, 0:8 * qs].rearrange("p (j two q) -> p j two q", two=2, q=qs)[:, :, hf, :],
                                      in_=pb[b1][hf * 64:(hf + 1) * 64, 0:4 * qs].rearrange("p (j q) -> p j q", q=qs), func=AF.Copy, scale=0.125),
                         reads=[R_pb[b1]], writes=[R_qa[st]])
                yield 3.0
                b2 = wrot.next()
                fm_proj(b2, xT, R_x, qs, C_QB, 4)
                for g in range(2):
                    P.op("act", _call("activation", out=qbz[st3][g * 64:(g + 1) * 64, g * 4 * qs:(g + 1) * 4 * qs],
                                      in_=pb[b2][g * 64:(g + 1) * 64, 0:4 * qs], func=AF.Copy, scale=0.125),
                         reads=[R_pb[b2]], writes=[R_qb[st3]])
                yield 3.0
                b3 = wrot.next()
                fm_proj(b3, xT, R_x, qs, C_QI, 4)
                for hf in range(2):
                    P.op("act", _call("activation",
                                      out=qiz[st][hf * 64:(hf + 1) * 64, 0:8 * qs].rearrange("p (j two q) -> p j two q", two=2, q=qs)[:, :, hf, :],
                                      in_=pb[b3][hf * 64:(hf + 1) * 64, 0:4 * qs].rearrange("p (j q) -> p j q", q=qs), func=AF.Copy),
                         reads=[R_pb[b3]], writes=[R_qi[st]])
                b4 = wrot.next()
                tm_proj(b4, xT, R_x, qs, C_WI, 8)
                P.op("dve", _call("tensor_scalar", out=coef[st][0:qs, :], in0=pb[b4][0:qs, 0:8], scalar1=float(8.0 ** -1.5), scalar2=None, op0=ALU.mult),
                     reads=[R_pb[b4]], writes=[R_coef[st]])
                for h in range(8):
                    P.op("pool", _call("tensor_scalar", out=dg[st][0:qs, h * 128: h * 128 + qs], in0=ident_f[0:qs, 0:qs],
                                       scalar1=coef[st][0:qs, h:h + 1], scalar2=None, op0=ALU.mult),
                         reads=[R_coef[st], R_ident], writes=[R_dg[st]])
                yield 3.0

            def normalize(bank, qs, mixt, R_m, col0, rec, R_rec):
                ov = pb[bank][0:qs, 0:260].rearrange("p (h d) -> p h d", d=65)
                P.op("dve", _call("tensor_scalar", out=rec[0:qs, 0:4].rearrange("p (h o) -> p h o", o=1), in0=ov[:, :, 64:65],
                                  scalar1=1e-30, scalar2=None, op0=ALU.max),
                     reads=[R_pb[bank]], writes=[R_rec])
                P.op("dve", _call("reciprocal", out=rec[0:qs, 0:4], in_=rec[0:qs, 0:4]), reads=[R_rec], writes=[R_rec])
                for hh in range(4):
                    P.op("dve", _call("tensor_scalar", out=mixt[0:qs, col0 + hh * 64: col0 + (hh + 1) * 64],
                                      in0=pb[bank][0:qs, hh * 65: hh * 65 + 64],
                                      scalar1=rec[0:qs, hh:hh + 1], scalar2=None, op0=ALU.mult),
                         reads=[R_pb[bank], R_rec], writes=[R_m])

            def pipe3(items, s1, s2, s3, D, cost=1.0):
                pend = []
                for it in items:
                    s1(it)
                    s2(it)
                    pend.append(it)
                    if len(pend) > D:
                        s3(pend.pop(0))
                    yield cost
                while pend:
                    s3(pend.pop(0))
                    yield cost

            pta_rot = Rot([0, 1, 2])
            relu_rot = Rot([0, 1, 2])
            ptb_rot = Rot([0, 1, 2])
            brot = Rot([3, 7])

            def front_attn(sn, qs, wins, btiles, prompt_masks, abw):
                st = sn % 2
                mixt, R_m = mixb[sn % 3], R_mix[sn % 3]
                nw = len(wins)

                units = []
                for h in range(8):
                    units.append({"h": h, "t0": 0, "tiles": wins[0:4]})
                    if nw > 4:
                        units.append({"h": h, "t0": 4, "tiles": wins[4:5]})

                def a1(u):
                    h = u["h"]
                    j = h // 2
                    bank = wrot.next()
                    u["bank"] = bank
                    for i, (slot, ts) in enumerate(u["tiles"]):
                        t = u["t0"] + i
                        c0 = i * qs
                        P.op("pe", _call("matmul", out=pb[bank][0:ts, c0:c0 + qs], lhsT=kaT[:, slot * 512 + j * 128: slot * 512 + j * 128 + ts],
                                         rhs=qaz[st][:, h * qs:(h + 1) * qs], start=True, stop=False),
                             reads=[R_ka[slot], R_qa[st]], writes=[R_pb[bank]])
                        P.op("pe", _call("matmul", out=pb[bank][0:ts, c0:c0 + qs], lhsT=ABb[0:qs, h * abw + t * 128: h * abw + t * 128 + ts],
                                         rhs=ident[0:qs, 0:qs], start=False, stop=True),
                             reads=[R_AB, R_ident], writes=[R_pb[bank]])

                def a2(u):
                    k = pta_rot.next()
                    u["pt"], u["R_pt"] = PTA[k], R_PTA[k]
                    bank = u["bank"]
                    tsm = max(ts for (_, ts) in u["tiles"])
                    n = len(u["tiles"])
                    P.op("act", _call("activation", out=u["pt"][0:tsm, 0:n * qs], in_=pb[bank][0:tsm, 0:n * qs], func=AF.Exp),
                         reads=[R_pb[bank]], writes=[u["R_pt"]])

                def a3(u):
                    h = u["h"]
                    last_unit = (u["t0"] + len(u["tiles"]) == nw)
                    for i, (slot, ts) in enumerate(u["tiles"]):
                        t = u["t0"] + i
                        P.op("pe", _call("matmul", out=pb[4][0:qs, (h % 4) * 65:(h % 4) * 65 + 65], lhsT=u["pt"][0:ts, i * qs:(i + 1) * qs],
                                         rhs=va_aug[0:ts, slot * 520 + h * 65: slot * 520 + h * 65 + 65],
                                         start=(h % 4 == 0 and t == 0), stop=(t == nw - 1), skip_group_check=True),
                             reads=[u["R_pt"], R_va[slot]], writes=[R_pb[4]])
                    if last_unit and h % 4 == 3:
                        normalize(4, qs, mixt, R_m, (h // 4) * 256, recA[st], R_recA[st])

                yield from pipe3(units, a1, a2, a3, 2, 0.9)

                L = btiles[-1][1] + btiles[-1][2]
                items = []
                cc = 0
                for c0 in range(0, L, 512):
                    w = min(512, L - c0)
                    rk = [R_ki[tt[0]] for tt in btiles if tt[1] >= c0 - 127 and tt[1] < c0 + w]
                    for h in range(8):
                        items.append({"c0": c0, "w": w, "h": h, "sc": (5, 4)[cc % 2], "rk": rk})
                    cc += 1

                def i1(it):
                    bank = wrot.next()
                    it["bank"] = bank
                    h, c0, w = it["h"], it["c0"], it["w"]
                    P.op("pe", _call("matmul", out=pb[bank][0:qs, 0:w], lhsT=qiz[st][:, h * qs:(h + 1) * qs],
                                     rhs=kbi[:, 4096 + c0: 4096 + c0 + w], start=True, stop=True),
                         reads=[R_qi[st]] + it["rk"], writes=[R_pb[bank]])

                def i2(it):
                    k = relu_rot.next()
                    it["rl"], it["R_rl"] = relu[k], R_relu[k]
                    w = it["w"]
                    P.op("act", _call("activation", out=it["rl"][0:qs, 0:w], in_=pb[it["bank"]][0:qs, 0:w], func=AF.Relu),
                         reads=[R_pb[it["bank"]]], writes=[it["R_rl"]])

                def i3(it):
                    h, c0, w, sc = it["h"], it["c0"], it["w"], it["sc"]
                    P.op("pe", _call("matmul", out=pb[sc][0:qs, 0:w], lhsT=dg[st][0:qs, h * 128: h * 128 + qs], rhs=it["rl"][0:qs, 0:w],
                                     start=(h == 0), stop=(h == 7)),
                         reads=[R_dg[st], it["R_rl"]], writes=[R_pb[sc]])
                    if h == 7:
                        if prompt_masks and c0 < 2048:
                            wm = min(w, 2048 - c0)
                            P.op("act", _call("activation", out=score[st][0:qs, c0:c0 + wm], in_=pb[sc][0:qs, 0:wm], func=AF.Identity,
                                              bias=colmask[0:qs, 0:1]),
                                 reads=[R_pb[sc], R_cst], writes=[R_score[st]])
                            if wm < w:
                                P.op("act", _call("activation", out=score[st][0:qs, c0 + wm:c0 + w], in_=pb[sc][0:qs, wm:w], func=AF.Copy),
                                     reads=[R_pb[sc]], writes=[R_score[st]])
                        else:
                            P.op("act", _call("activation", out=score[st][0:qs, c0:c0 + w], in_=pb[sc][0:qs, 0:w], func=AF.Copy),
                                 reads=[R_pb[sc]], writes=[R_score[st]])

                yield from pipe3(items, i1, i2, i3, 2, 0.65)
                if prompt_masks:
                    P.op("dve", _call("tensor_tensor", out=score[st][0:qs, L - 128:L], in0=score[st][0:qs, L - 128:L], in1=diagm[0:qs, :], op=ALU.add),
                         reads=[R_score[st], R_cst], writes=[R_score[st]])
                yield

            def bis_gen(sn, qs, btiles, bnw):
                st = sn % 2
                sm, R_sm = small[st], R_small[st]
                L = btiles[-1][1] + btiles[-1][2]
                P.op("dve", _call("memset", sm[0:qs, 1:2], 0.0), writes=[R_sm])
                for k in range(NIT):
                    wk = BIS_W0 / (2.0 ** k)
                    P.op("dve", _call("tensor_scalar", out=Mb[st][0:qs, 0:L], in0=score[st][0:qs, 0:L], scalar1=sm[0:qs, 1:2], scalar2=None,
                                      op0=ALU.is_ge, op1=ALU.add, accum_out=sm[0:qs, 0:1]),
                         reads=[R_score[st], R_sm], writes=[R_M[st], R_sm])
                    P.op("dve", _call("tensor_scalar", out=sm[0:qs, 2:3], in0=sm[0:qs, 0:1], scalar1=255.5, scalar2=wk,
                                      op0=ALU.is_ge, op1=ALU.mult),
                         reads=[R_sm], writes=[R_sm])
                    P.op("dve", _call("scalar_tensor_tensor", out=sm[0:qs, 1:2], in0=sm[0:qs, 2:3], scalar=-wk / 2.0,
                                      in1=sm[0:qs, 1:2], op0=ALU.add, op1=ALU.add),
                         reads=[R_sm], writes=[R_sm])
                    yield L * 1.08e-3 + 0.5
                wl = BIS_W0 / (2.0 ** (NIT - 1)) / 2.0
                P.op("dve", _call("tensor_scalar", out=sm[0:qs, 3:4], in0=sm[0:qs, 1:2], scalar1=-wl, scalar2=None, op0=ALU.add),
                     reads=[R_sm], writes=[R_sm])
                P.op("dve", _call("tensor_scalar", out=Mb[st][0:qs, 0:L], in0=score[st][0:qs, 0:L], scalar1=sm[0:qs, 3:4], scalar2=NEGM,
                                  op0=ALU.is_lt, op1=ALU.mult),
                     reads=[R_score[st], R_sm], writes=[R_M[st]])
                nearw = btiles[-2][2] + btiles[-1][2]
                for h in range(8):
                    P.op("dve", _call("tensor_tensor", out=Mnear[st][0:qs, h * bnw: h * bnw + nearw], in0=Bnb[0:qs, h * bnw: h * bnw + nearw],
                                      in1=Mb[st][0:qs, L - nearw:L], op=ALU.add),
                         reads=[R_Bn, R_M[st]], writes=[R_Mnear[st]])
                yield
            def battn_gen(sn, qs, btiles, blk, bnw):
                st = sn % 2
                st3 = sn % 3
                mixt, R_m = mixb[st3], R_mix[st3]
                nb = len(btiles)
                items = [{"g": g, "t": t, "vt": vt, "c0": c0, "ts": ts} for g in range(2) for t, (vt, c0, ts) in enumerate(btiles)]

                def b1(it):
                    g, t, vt, c0, ts = it["g"], it["t"], it["vt"], it["c0"], it["ts"]
                    bank = brot.next()
                    it["bank"] = bank
                    P.op("pe", _call("matmul", out=pb[bank][0:ts, 0:4 * qs], lhsT=kbi[:, c0:c0 + ts],
                                     rhs=qbz[st3][:, g * 4 * qs:(g + 1) * 4 * qs], start=True, stop=False),
                         reads=[R_kbi[vt], R_qb[st3]], writes=[R_pb[bank]])
                    if t < nb - 2 and qs == 128:
                        P.op("pe", _call("matmul", out=pb[bank][0:ts, 0:512], lhsT=Mb[st][0:qs, c0:c0 + ts], rhs=ident[0:128, 0:512],
                                         start=False, stop=True),
                             reads=[R_M[st], R_ident], writes=[R_pb[bank]])
                    elif t < nb - 2:
                        for r in range(4):
                            P.op("pe", _call("matmul", out=pb[bank][0:ts, r * qs:(r + 1) * qs], lhsT=Mb[st][0:qs, c0:c0 + ts],
                                             rhs=ident[0:qs, 0:qs], start=False, stop=(r == 3)),
                                 reads=[R_M[st], R_ident], writes=[R_pb[bank]])
                    else:
                        tt = t - (nb - 2)
                        for r in range(4):
                            hh = g * 4 + r
                            P.op("pe", _call("matmul", out=pb[bank][0:ts, r * qs:(r + 1) * qs],
                                             lhsT=Mnear[st][0:qs, hh * bnw + tt * 128: hh * bnw + tt * 128 + ts], rhs=ident[0:qs, 0:qs],
                                             start=False, stop=(r == 3)),
                                 reads=[R_Mnear[st], R_ident], writes=[R_pb[bank]])

                def b2(it):
                    k = ptb_rot.next()
                    it["ptb"], it["R_ptb"] = PTB[k], R_PTB[k]
                    ts = it["ts"]
                    P.op("act", _call("activation", out=it["ptb"][0:ts, 0:4 * qs], in_=pb[it["bank"]][0:ts, 0:4 * qs], func=AF.Exp),
                         reads=[R_pb[it["bank"]]], writes=[it["R_ptb"]])

                def b3(it):
                    g, t, vt, ts = it["g"], it["t"], it["vt"], it["ts"]
                    for r in range(4):
                        P.op("pe", _call("matmul", out=pb[6][0:qs, r * 65: r * 65 + 65], lhsT=it["ptb"][0:ts, r * qs:(r + 1) * qs],
                                         rhs=vb_aug[0:ts, (vt * 2 + g) * 65:(vt * 2 + g) * 65 + 65],
                                         start=(t == 0 and r == 0), stop=(t == nb - 1), skip_group_check=True),
                             reads=[it["R_ptb"], R_vb[vt]], writes=[R_pb[6]])
                    if t == nb - 1:
                        normalize(6, qs, mixt, R_m, 512 + g * 256, recB[st], R_recB[st])

                yield from pipe3(items, b1, b2, b3, 1, 0.8)
                P.dma("sp", mixD[blk * 128: blk * 128 + qs, :], mixt[0:qs, :], reads=[R_m], writes=[R_mixD[blk]], defer=True)
                yield

            load_xT(0)
            for r in range(16):
                if r + 1 < 16:
                    load_xT(r + 1)
                for _ in kside(r, full=(r >= 11)):
                    pass
            checkpoint('phase0')

            def prompt_front(sn, T):
                if T >= 16:
                    load_xT(T)
                    yield from kside(T, full=True)
                s = T % 2
                yield from qside(xTb[s], R_xT[s], 128, sn % 2, sn % 3)
                wins = [((T - 4 + t) % 6, 128) for t in range(5)]
                btiles = [(t, t * 128, 128) for t in range(T + 1)]
                yield from front_attn(sn, 128, wins, btiles, True, ABW)

            def prompt_bis(sn, T):
                btiles = [(t, t * 128, 128) for t in range(T + 1)]
                yield from bis_gen(sn, 128, btiles, BNW)

            def prompt_battn(sn, T, blk):
                btiles = [(t, t * 128, 128) for t in range(T + 1)]
                yield from battn_gen(sn, 128, btiles, blk, BNW)

            steps = [(0, 15, 16)] + [(1 + i, 16 + i, i) for i in range(16)]
            ns = len(steps)
            SN = ns
            sst = SN % 2
            s_wins = [(0, 128), (1, 128), (2, 128), (3, 128), (4, 16)]
            s_btiles = [(t, t * 128, 128) for t in range(16)] + [(16, 2048, 16)]
            xs_, R_xs = xTb[0], R_xT[0]

            def sample_front():
                stg, R_stg = score[sst], R_score[sst]
                P.dma("sp", stg[:, 0:2048], I["cbiT"][:, :], writes=[R_stg])
                P.op("act", _call("activation", out=kbi[:, 4096:4096 + 2048], in_=stg[:, 0:2048], func=AF.Copy),
                     reads=[R_stg], writes=R_ki[0:16])
                P.dma("sp", stg[:, 2048:4096], I["cakT"][:, :], writes=[R_stg])
                for s4 in range(4):
                    P.op("act", _call("activation", out=kaT[:, s4 * 512:(s4 + 1) * 512].rearrange("p (j t) -> p j t", t=128),
                                      in_=stg[:, 2048:4096].rearrange("p (j t) -> p j t", t=512)[:, :, s4 * 128:(s4 + 1) * 128], func=AF.Copy),
                         reads=[R_stg], writes=[R_ka[s4]])
                yield 3.0
                P.dma("sp", stg[:, 0:2048].rearrange("p (t c) -> p t c", c=512), I["cav"].rearrange("(t p) c -> p t c", p=128), writes=[R_stg])
                vaall = va_aug[:, 0:4 * 520].rearrange("p (t d) -> p t d", d=65)
                P.op("act", _call("activation", out=vaall[:, :, 0:64], in_=stg[:, 0:2048].rearrange("p (t d) -> p t d", d=64), func=AF.Copy),
                     reads=[R_stg], writes=R_va[0:5])
                P.op("pool", _call("memset", va_aug[:, 0:5 * 520].rearrange("p (t d) -> p t d", d=65)[:, :, 64:65], 1.0), writes=R_va[0:5])
                for hh in range(2):
                    w = 4 * 528
                    P.dma("sp", stg[0:16, 0:w], I["ABs"][:, hh * w:(hh + 1) * w], writes=[R_stg])
                    P.op("act", _call("activation", out=ABb[0:16, hh * w:(hh + 1) * w], in_=stg[0:16, 0:w], func=AF.Copy),
                         reads=[R_stg], writes=[R_AB])
                P.op("pool", _call("memset", qaz[sst][:, :], 0.0), writes=[R_qa[sst]])
                P.op("pool", _call("memset", qbz[SN % 3][:, :], 0.0), writes=[R_qb[SN % 3]])
                P.op("pool", _call("memset", qiz[sst][:, :], 0.0), writes=[R_qi[sst]])
                P.dma("sp", xstg[:, 0:128], I["xsT"][:, :], writes=[R_xstg])
                P.op("pool", _call("tensor_copy", out=xTb[0][:, 0:128], in_=xstg[:, 0:128]), reads=[R_xstg], writes=[R_xT[0]])
                yield 3.0
                bk = wrot.next()
                fm_proj(bk, xs_, R_xs, 16, C_KI, 1)
                P.op("act", _call("activation", out=kbi[:, 4096 + 2048:4096 + 2064], in_=pb[bk][:, 0:16], func=AF.Copy), reads=[R_pb[bk]], writes=[R_ki[16]])
                P.op("dve", _call("tensor_copy", out=ostg[0][:, 16:32], in_=pb[bk][:, 0:16]), reads=[R_pb[bk]], writes=[R_ostg[0]])
                P.dma("sp", O["sbiT"][:, :], ostg[0][0:64, 16:32], reads=[R_ostg[0]], defer=True)
                ba = wrot.next()
                fm_proj(ba, xs_, R_xs, 16, C_KA, 4)
                P.op("act", _call("activation", out=kaT[:, 4 * 512:5 * 512].rearrange("p (j t) -> p j t", t=128)[:, :, 0:16],
                                  in_=pb[ba][:, 0:64].rearrange("p (j t) -> p j t", t=16), func=AF.Copy),
                     reads=[R_pb[ba]], writes=[R_ka[4]])
                P.op("dve", _call("tensor_copy", out=astg[:, 0:64], in_=pb[ba][:, 0:64]), reads=[R_pb[ba]], writes=[R_astg])
                P.dma("sp", O["sakT"][:, :], astg[:, 0:64], reads=[R_astg], defer=True)
                bva = wrot.next()
                tm_proj(bva, xs_, R_xs, 16, C_VA, 512)
                vav = va_aug[0:16, 4 * 520:5 * 520].rearrange("p (h d) -> p h d", d=65)
                P.op("act", _call("activation", out=vav[:, :, 0:64], in_=pb[bva][0:16, :].rearrange("p (h d) -> p h d", d=64), func=AF.Copy),
                     reads=[R_pb[bva]], writes=[R_va[4]])
                P.op("dve", _call("tensor_copy", out=astg[0:16, 512:1024], in_=pb[bva][0:16, :]), reads=[R_pb[bva]], writes=[R_astg])
                P.dma("sp", O["sav"][:, :], astg[0:16, 512:1024], reads=[R_astg], defer=True)
                yield 3.0
                yield from qside(xs_, R_xs, 16, sst, SN % 3)
                yield from front_attn(SN, 16, s_wins, s_btiles, False, 528)

            def sample_bis():
                stg, R_stg = score[1 - sst], R_score[1 - sst]
                P.dma("sp", stg[0:16, 0:8 * 144], I["Bns"][:, :], writes=[R_stg])
                for h in range(8):
                    P.op("dve", _call("tensor_scalar", out=Bnb[0:16, h * 144:(h + 1) * 144], in0=stg[0:16, h * 144:(h + 1) * 144],
                                      scalar1=c15[0:16, h:h + 1], scalar2=None, op0=ALU.subtract),
                         reads=[R_stg, R_cst], writes=[R_Bn])
                yield 1.0
                yield from bis_gen(SN, 16, s_btiles, 144)

            def sample_battn():
                stg, R_stg = score[1 - sst], R_score[1 - sst]
                P.dma("sp", stg[:, 0:2048], I["cbkT"][:, :], writes=[R_stg])
                P.op("act", _call("activation", out=kbi[:, 0:2048], in_=stg[:, 0:2048], func=AF.Copy),
                     reads=[R_stg], writes=R_kbi[0:16])
                P.dma("sp", stg[:, 2048:4096].rearrange("p (t c) -> p t c", c=128), I["cbv"].rearrange("(t p) c -> p t c", p=128), writes=[R_stg])
                vball = vb_aug[:, 0:16 * 130].rearrange("p (t d) -> p t d", d=65)
                P.op("act", _call("activation", out=vball[:, :, 0:64], in_=stg[:, 2048:4096].rearrange("p (t d) -> p t d", d=64), func=AF.Copy),
                     reads=[R_stg], writes=R_vb[0:17])
                P.op("pool", _call("memset", vb_aug[:, 0:17 * 130].rearrange("p (t d) -> p t d", d=65)[:, :, 64:65], 1.0), writes=R_vb[0:17])
                bk = wrot.next()
                fm_proj(bk, xs_, R_xs, 16, C_KB, 1)
                P.op("act", _call("activation", out=kbi[:, 2048:2064], in_=pb[bk][:, 0:16], func=AF.Copy), reads=[R_pb[bk]], writes=[R_kbi[16]])
                P.op("dve", _call("tensor_copy", out=ostg[1][:, 0:16], in_=pb[bk][:, 0:16]), reads=[R_pb[bk]], writes=[R_ostg[1]])
                P.dma("sp", O["sbkT"][:, :], ostg[1][:, 0:16], reads=[R_ostg[1]], defer=True)
                bv_ = wrot.next()
                tm_proj(bv_, xs_, R_xs, 16, C_VB, 128)
                vbv = vb_aug[0:16, 16 * 130:17 * 130].rearrange("p (g d) -> p g d", d=65)
                P.op("act", _call("activation", out=vbv[:, :, 0:64], in_=pb[bv_][0:16, 0:128].rearrange("p (g d) -> p g d", d=64), func=AF.Copy),
                     reads=[R_pb[bv_]], writes=[R_vb[16]])
                P.op("dve", _call("tensor_copy", out=vbstg[0][0:16, :], in_=pb[bv_][0:16, 0:128]), reads=[R_pb[bv_]], writes=[R_vbstg[0]])
                P.dma("sp", O["sbv"][:, :], vbstg[0][0:16, :], reads=[R_vbstg[0]], defer=True)
                yield 3.0
                yield from battn_gen(SN, 16, s_btiles, 17, 144)

            for tick in range(ns + 3):
                gens = []
                if 0 <= tick - 2 < ns:
                    gens.append(prompt_battn(*steps[tick - 2]))
                elif tick - 2 == ns:
                    gens.append(sample_battn())
                if 0 <= tick - 1 < ns:
                    gens.append(prompt_bis(*steps[tick - 1][0:2]))
                elif tick - 1 == ns:
                    gens.append(sample_bis())
                if tick < ns:
                    gens.append(prompt_front(*steps[tick][0:2]))
                elif tick == ns:
                    gens.append(sample_front())
                run_interleaved(gens)
            checkpoint('steps')
            checkpoint('phaseA')
            P.flush(block)

        P.barrier()
        with ExitStack() as sbk:
            wob = sb(sbk, "wob", [128, 8 * 1024], BF16)
            wmqb = sb(sbk, "wmqb", [128, 8 * 512], BF16)
            wmob = sb(sbk, "wmob", [128, 4 * 1024], BF16)
            wtmp = sb(sbk, "wtmp", [128, 8 * 512], BF16)
            R_wo, R_wmq, R_wmo, R_wtmp = Res("wo"), Res("wmq"), Res("wmo"), Res("wtmp")
            wst = [sb(sbk, "wst%d" % k, [128, 2048], F32) for k in range(2)]
            R_wst = [Res("wst%d" % k) for k in range(2)]
            lnt = sb(sbk, "lnt", [128, 4 * 1024], F32)
            R_ln = Res("ln")
            memTb = sb(sbk, "memTb", [128, 8 * 256], BF16)
            R_memT = Res("memT")
            mkT = [sb(sbk, "mkT%d" % k, [128, 4 * 256], BF16) for k in range(2)]
            mva = [sb(sbk, "mva%d" % k, [128, 2 * 4 * 129], BF16) for k in range(2)]
            R_mk = [Res("mk%d" % k) for k in range(2)]
            R_mv = [Res("mv%d" % k) for k in range(2)]
            mixl = [sb(sbk, "mixl%d" % k, [128, 1024], BF16) for k in range(4)]
            R_mixl = [Res("mixl%d" % k) for k in range(4)]
            xr = [sb(sbk, "xr%d" % k, [128, 1024], F32) for k in range(4)]
            R_xr = [Res("xr%d" % k) for k in range(4)]
            NB3 = 4
            tT_l = [sb(sbk, "tT%d" % k, [128, 1024], BF16) for k in range(NB3)]
            hA_l = [sb(sbk, "hA%d" % k, [128, 1024], F32) for k in range(NB3)]
            hB_l = [sb(sbk, "hB%d" % k, [128, 1024], F32) for k in range(NB3)]
            h16_l = [sb(sbk, "h16%d" % k, [128, 1024], BF16) for k in range(NB3)]
            qmT_l = [sb(sbk, "qmT%d" % k, [128, 512], BF16) for k in range(NB3)]
            PTm_l = [sb(sbk, "PTm%d" % k, [128, 1024], BF16) for k in range(NB3)]
            o16_l = [sb(sbk, "o16%d" % k, [128, 512], BF16) for k in range(NB3)]
            oT_l = [sb(sbk, "oT%d" % k, [128, 512], BF16) for k in range(NB3)]
            stat_l = [sb(sbk, "stat%d" % k, [128, 32], F32) for k in range(NB3)]
            RB = [{n: Res(n + str(k)) for n in ("tT", "hA", "hB", "h16", "qm", "PTm", "o16", "oT", "stat")} for k in range(NB3)]
            h2T = [sb(sbk, "h2T%d" % k, [128, 1024], BF16) for k in range(4)]
            R_h2T = [Res("h2T%d" % k) for k in range(4)]
            mstg = sb(sbk, "mstg", [128, 1024], F32)
            R_mstg = Res("mstg")
            wrot = Rot([0, 1, 2, 3, 4, 5, 6, 7])

            def load_cast(dst, R_dst, src, ncols, engs=("act", "pool")):
                k = 0
                for c0 in range(0, ncols, 2048):
                    w = min(2048, ncols - c0)
                    s = k % 2
                    P.dma("sp", wst[s][:, 0:w], src[:, c0:c0 + w], writes=[R_wst[s]])
                    eng = engs[k % len(engs)]
                    if eng == "act":
                        P.op("act", _call("activation", out=dst[:, c0:c0 + w], in_=wst[s][:, 0:w], func=AF.Copy),
                             reads=[R_wst[s]], writes=[R_dst])
                    else:
                        P.op(eng, _call("tensor_copy", out=dst[:, c0:c0 + w], in_=wst[s][:, 0:w]),
                             reads=[R_wst[s]], writes=[R_dst])
                    k += 1

            load_cast(wob, R_wo, I["wo"], 8192)
            load_cast(wmqb, R_wmq, I["wmq"], 4096)
            load_cast(wmob, R_wmo, I["wmo"], 4096)
            for k in range(4):
                P.dma("sp", lnt[:, k * 1024:(k + 1) * 1024], I["lnp"][k:k + 1, :].to_broadcast([128, 1024]), writes=[R_ln])
            load_cast(memTb, R_memT, I["memT"], 2048)
            load_cast(wtmp, R_wtmp, I["wmk"], 4096)
            for h in range(4):
                bank = wrot.next()
                for kc in range(KC):
                    P.op("pe", _call("matmul",
                        out=pb[bank][:, 0:256], lhsT=wtmp[:, kc * 512 + h * 128: kc * 512 + (h + 1) * 128],
                        rhs=memTb[:, kc * 256:(kc + 1) * 256], start=(kc == 0), stop=(kc == KC - 1)),
                        reads=[R_wtmp, R_memT], writes=[R_pb[bank]])
                P.op("act", _call("activation", out=mkT[0][:, h * 256:(h + 1) * 256], in_=pb[bank][:, 0:256], func=AF.Copy),
                     reads=[R_pb[bank]], writes=[R_mk[0]])
                P.op("dve", _call("tensor_copy", out=mstg[:, h * 256:(h + 1) * 256], in_=pb[bank][:, 0:256]),
                     reads=[R_pb[bank]], writes=[R_mstg])
            P.dma("sp", O["mkT"][:, :], mstg[:, :], reads=[R_mstg], defer=True)
            load_cast(wtmp, R_wtmp, I["wmv"], 4096)
            for mt in range(2):
                bank = wrot.next()
                for kc in range(KC):
                    P.op("pe", _call("matmul",
                        out=pb[bank][:, 0:512], lhsT=memTb[:, kc * 256 + mt * 128: kc * 256 + (mt + 1) * 128],
                        rhs=wtmp[:, kc * 512:(kc + 1) * 512], start=(kc == 0), stop=(kc == KC - 1)),
                        reads=[R_wtmp, R_memT], writes=[R_pb[bank]])
                mvv = mva[0][:, mt * 516:(mt + 1) * 516].rearrange("p (h d) -> p h d", d=129)
                P.op("act", _call("activation", out=mvv[:, :, 0:128], in_=pb[bank][:, :].rearrange("p (h d) -> p h d", d=128), func=AF.Copy),
                     reads=[R_pb[bank]], writes=[R_mv[0]])
                P.op("dve", _call("tensor_copy", out=mstg[:, mt * 512:(mt + 1) * 512], in_=pb[bank][:, :]),
                     reads=[R_pb[bank]], writes=[R_mstg])
                P.dma("sp", O["mv"][mt * 128:(mt + 1) * 128, :], mstg[:, mt * 512:(mt + 1) * 512], reads=[R_mstg], defer=True)
            for k in range(2):
                P.op("pool", _call("memset", mva[k][:, :].rearrange("p (t d) -> p t d", d=129)[:, :, 128:129], 1.0), writes=[R_mv[k]])
            load_cast(mkT[1], R_mk[1], I["cmkT"], 1024)
            P.dma("sp", wst[0][:, 0:1024].rearrange("p (t c) -> p t c", c=512), I["cmv"].rearrange("(t p) c -> p t c", p=128), writes=[R_wst[0]])
            P.op("act", _call("activation", out=mva[1][:, :].rearrange("p (t d) -> p t d", d=129)[:, :, 0:128],
                                               in_=wst[0][:, 0:1024].rearrange("p (t d) -> p t d", d=128), func=AF.Copy),
                 reads=[R_wst[0]], writes=[R_mv[1]])

            checkpoint('phaseB_pre')
            def transpose_to(src16, R_src, qs, nchunk, dst, R_dst):
                bank = wrot.next()
                pbf = pb[bank][:, :].bitcast(BF16)
                for c in range(nchunk):
                    P.op("pe", _call("transpose", out=pbf[:, c * qs:(c + 1) * qs], in_=src16[0:qs, c * 128:(c + 1) * 128],
                                                                   identity=ident[0:qs, 0:qs]),
                         reads=[R_src, R_ident], writes=[R_pb[bank]])
                P.op("act", _call("activation", out=dst[:, 0:nchunk * qs], in_=pbf[:, 0:nchunk * qs], func=AF.Copy),
                     reads=[R_pb[bank]], writes=[R_dst])

            def layer_norm(hin, R_hin, qs, gcol, hout, R_hout, stat, R_stat):
                for c in range(2):
                    P.op("dve", _call("bn_stats", out=stat[0:qs, c * 6:(c + 1) * 6], in_=hin[0:qs, c * 512:(c + 1) * 512]),
                         reads=[R_hin], writes=[R_stat])
                P.op("dve", _call("bn_aggr", out=stat[0:qs, 12:14], in_=stat[0:qs, 0:12]), reads=[R_stat], writes=[R_stat])
                P.op("dve", _call("tensor_scalar", out=stat[0:qs, 14:15], in0=stat[0:qs, 13:14], scalar1=LN_EPS, scalar2=None, op0=ALU.add),
                     reads=[R_stat], writes=[R_stat])
                P.op("act", _call("activation", out=stat[0:qs, 15:16], in_=stat[0:qs, 14:15], func=AF.Sqrt), reads=[R_stat], writes=[R_stat])
                P.op("dve", _call("reciprocal", out=stat[0:qs, 16:17], in_=stat[0:qs, 15:16]), reads=[R_stat], writes=[R_stat])
                P.op("dve", _call("scalar_tensor_tensor", out=stat[0:qs, 17:18], in0=stat[0:qs, 12:13], scalar=-1.0, in1=stat[0:qs, 16:17],
                                                             op0=ALU.mult, op1=ALU.mult),
                     reads=[R_stat], writes=[R_stat])
                P.op("act", _call("activation", out=hout[0:qs, :], in_=hin[0:qs, :], func=AF.Identity, scale=stat[0:qs, 16:17], bias=stat[0:qs, 17:18]),
                     reads=[R_hin, R_stat], writes=[R_hout])
                P.op("dve", _call("tensor_tensor", out=hout[0:qs, :], in0=hout[0:qs, :], in1=lnt[0:qs, gcol * 1024:(gcol + 1) * 1024], op=ALU.mult),
                     reads=[R_hout, R_ln], writes=[R_hout])
                P.op("dve", _call("tensor_tensor", out=hout[0:qs, :], in0=hout[0:qs, :], in1=lnt[0:qs, (gcol + 1) * 1024:(gcol + 2) * 1024], op=ALU.add),
                     reads=[R_hout, R_ln], writes=[R_hout])

            def phaseB_block(blk, qs, row0, mi, k2):
                s = k2
                tT, hA, hB, h16, qmT, PTm, o16, oT, stat = (tT_l[k2], hA_l[k2], hB_l[k2], h16_l[k2], qmT_l[k2], PTm_l[k2], o16_l[k2],
                                                             oT_l[k2], stat_l[k2])
                R_tT, R_hA, R_hB, R_h16, R_qm, R_PTm, R_o16, R_oT, R_stat = (RB[k2][n] for n in ("tT", "hA", "hB", "h16", "qm", "PTm", "o16", "oT", "stat"))
                P.dma("sp", mixl[s][0:qs, :], mixD[blk * 128 + row0: blk * 128 + row0 + qs, :], reads=[R_mixD[blk]], writes=[R_mixl[s]])
                P.dma("sp", xr[s][0:qs, :], I["xres"][blk * 128: blk * 128 + qs, :], writes=[R_xr[s]])
                transpose_to(mixl[s], R_mixl[s], qs, 8, tT, R_tT)
                yield
                b0, b1 = wrot.next(), wrot.next()
                for n, bank in enumerate((b0, b1)):
                    for kc in range(KC):
                        P.op("pe", _call("matmul",
                            out=pb[bank][0:qs, :], lhsT=tT[:, kc * qs:(kc + 1) * qs], rhs=wob[:, kc * 1024 + n * 512: kc * 1024 + (n + 1) * 512],
                            start=(kc == 0), stop=(kc == KC - 1)),
                            reads=[R_tT, R_wo], writes=[R_pb[bank]])
                    P.op("dve", _call("scalar_tensor_tensor",
                        out=hA[0:qs, n * 512:(n + 1) * 512], in0=xr[s][0:qs, n * 512:(n + 1) * 512], scalar=ALPHA, in1=pb[bank][0:qs, :],
                        op0=ALU.mult, op1=ALU.add),
                        reads=[R_xr[s], R_pb[bank]], writes=[R_hA])
                yield
                layer_norm(hA, R_hA, qs, 0, hB, R_hB, stat, R_stat)
                yield
                P.op("act", _call("activation", out=h16[0:qs, :], in_=hB[0:qs, :], func=AF.Copy), reads=[R_hB], writes=[R_h16])
                transpose_to(h16, R_h16, qs, 8, tT, R_tT)
                yield
                bq = wrot.next()
                for h in range(4):
                    for kc in range(KC):
                        P.op("pe", _call("matmul",
                            out=pb[bq][:, h * qs:(h + 1) * qs], lhsT=wmqb[:, kc * 512 + h * 128: kc * 512 + (h + 1) * 128],
                            rhs=tT[:, kc * qs:(kc + 1) * qs], start=(kc == 0), stop=(kc == KC - 1)),
                            reads=[R_wmq, R_tT], writes=[R_pb[bq]])
                P.op("act", _call("activation", out=qmT[:, 0:4 * qs], in_=pb[bq][:, 0:4 * qs], func=AF.Copy, scale=float(128.0 ** -0.5)),
                     reads=[R_pb[bq]], writes=[R_qm])
                yield
                bs0, bs1 = wrot.next(), wrot.next()
                for h in range(4):
                    for mt in range(2):
                        idx = h * 2 + mt
                        bank = bs0 if idx < 4 else bs1
                        c0 = (idx % 4) * qs
                        P.op("pe", _call("matmul",
                            out=pb[bank][:, c0:c0 + qs], lhsT=mkT[mi][:, h * 256 + mt * 128: h * 256 + (mt + 1) * 128],
                            rhs=qmT[:, h * qs:(h + 1) * qs], start=True, stop=True),
                            reads=[R_mk[mi], R_qm], writes=[R_pb[bank]])
                for k, bank in enumerate((bs0, bs1)):
                    P.op("act", _call("activation", out=PTm[:, k * 4 * qs:(k + 1) * 4 * qs], in_=pb[bank][:, 0:4 * qs], func=AF.Exp),
                         reads=[R_pb[bank]], writes=[R_PTm])
                yield
                bo0, bo1 = wrot.next(), wrot.next()
                for h in range(4):
                    bank = bo0 if h < 2 else bo1
                    for mt in range(2):
                        idx = h * 2 + mt
                        P.op("pe", _call("matmul",
                            out=pb[bank][0:qs, (h % 2) * 129:(h % 2) * 129 + 129], lhsT=PTm[:, idx * qs:(idx + 1) * qs],
                            rhs=mva[mi][:, (mt * 4 + h) * 129:(mt * 4 + h) * 129 + 129],
                            start=(h % 2 == 0 and mt == 0), stop=(mt == 1), skip_group_check=True),
                            reads=[R_PTm, R_mv[mi]], writes=[R_pb[bank]])
                for k, bank in enumerate((bo0, bo1)):
                    ov = pb[bank][0:qs, 0:258].rearrange("p (h d) -> p h d", d=129)
                    P.op("dve", _call("tensor_scalar", out=stat[0:qs, 20 + 2 * k:22 + 2 * k].rearrange("p (h o) -> p h o", o=1),
                                                                      in0=ov[:, :, 128:129], scalar1=1e-30, scalar2=None, op0=ALU.max),
                         reads=[R_pb[bank]], writes=[R_stat])
                    P.op("dve", _call("reciprocal", out=stat[0:qs, 20 + 2 * k:22 + 2 * k], in_=stat[0:qs, 20 + 2 * k:22 + 2 * k]),
                         reads=[R_stat], writes=[R_stat])
                    for hh in range(2):
                        h = k * 2 + hh
                        P.op("dve", _call("tensor_scalar",
                            out=o16[0:qs, h * 128:(h + 1) * 128], in0=pb[bank][0:qs, hh * 129: hh * 129 + 128],
                            scalar1=stat[0:qs, 20 + 2 * k + hh:21 + 2 * k + hh], scalar2=None, op0=ALU.mult),
                            reads=[R_pb[bank], R_stat], writes=[R_o16])
                yield
                transpose_to(o16, R_o16, qs, 4, oT, R_oT)
                yield
                b0, b1 = wrot.next(), wrot.next()
                for n, bank in enumerate((b0, b1)):
                    for c in range(4):
                        P.op("pe", _call("matmul",
                            out=pb[bank][0:qs, :], lhsT=oT[:, c * qs:(c + 1) * qs], rhs=wmob[:, c * 1024 + n * 512: c * 1024 + (n + 1) * 512],
                            start=(c == 0), stop=(c == 3)),
                            reads=[R_oT, R_wmo], writes=[R_pb[bank]])
                    P.op("dve", _call("scalar_tensor_tensor",
                        out=hA[0:qs, n * 512:(n + 1) * 512], in0=hB[0:qs, n * 512:(n + 1) * 512], scalar=ALPHA, in1=pb[bank][0:qs, :],
                        op0=ALU.mult, op1=ALU.add),
                        reads=[R_hB, R_pb[bank]], writes=[R_hA])
                yield
                layer_norm(hA, R_hA, qs, 2, hB, R_hB, stat, R_stat)
                yield
                P.dma("sp", h2D[blk * 128: blk * 128 + qs, :], hB[0:qs, :], reads=[R_hB], writes=[R_h2D[blk]], defer=True)
                P.op("act", _call("activation", out=h16[0:qs, :], in_=hB[0:qs, :], func=AF.Copy), reads=[R_hB], writes=[R_h16])
                transpose_to(h16, R_h16, qs, 8, h2T[s], R_h2T[s])
                P.dma("sp", h2TD[blk][:, 0:8 * qs], h2T[s][:, 0:8 * qs], reads=[R_h2T[s]], writes=[R_h2TD[blk]], defer=True)
                yield

            def run_staggered(gens, lag):
                active = []
                pending = list(gens)
                tick = 0
                while active or pending:
                    if pending and (not active or tick >= lag):
                        active.append(pending.pop(0))
                        tick = 0
                    for g in list(active):
                        try:
                            next(g)
                        except StopIteration:
                            active.remove(g)
                    tick += 1

            blocks = [(16, 2, 126, 0), (17, 16, 0, 1)] + [(i, 128, 0, 0) for i in range(16)]
            run_staggered([phaseB_block(b_, q_, r_, m_, pos % 4) for pos, (b_, q_, r_, m_) in enumerate(blocks)], 3)
            checkpoint('phaseB')
            P.flush(block)

        P.barrier()
        with ExitStack() as sc:
            wdb = sb(sc, "wdb", [128, NFC * 1024], BF16)
            R_wd = Res("wd")
            wst = [sb(sc, "wstc%d" % k, [128, 2048], F32) for k in range(2)]
            R_wst = [Res("wstc%d" % k) for k in range(2)]
            wsl = [sb(sc, "wsl%d" % k, [128, 2048], BF16) for k in range(2)]
            R_wsl = [Res("wsl%d" % k) for k in range(2)]
            R_wslB = [Res("wslB%d" % k) for k in range(2)]
            hT2 = [sb(sc, "hT%d" % k, [128, NFC * 512], BF16) for k in range(2)]
            R_hT2 = [Res("hT%d" % k) for k in range(2)]
            hTm = sb(sc, "hTm", [128, NFC * 16], BF16)
            R_hTm = Res("hTm")
            h2Tg = [sb(sc, "h2Tg%d" % k, [128, 8 * 512], BF16) for k in range(2)]
            R_h2Tg = [Res("h2Tg%d" % k) for k in range(2)]
            h2Tm = sb(sc, "h2Tm", [128, 8 * 18], BF16)
            R_h2Tm = Res("h2Tm")
            Gb = [sb(sc, "Gb%d" % k, [128, 514], F32) for k in range(3)]
            R_Gb = [Res("Gb%d" % k) for k in range(3)]
            Gs = sb(sc, "Gs", [128, 18], F32)
            R_Gs = Res("Gs")
            t0b = [sb(sc, "t0b%d" % k, [128, 512], F32) for k in range(3)]
            R_t0 = [Res("t0%d" % k) for k in range(3)]
            geb = [sb(sc, "geb%d" % k, [128, 512], F32) for k in range(3)]
            R_ge = [Res("ge%d" % k) for k in range(3)]
            t1b = [sb(sc, "t1b%d" % k, [128, 512], F32) for k in range(3)]
            R_t1b = [Res("t1b%d" % k) for k in range(3)]
            t2b = [sb(sc, "t2b%d" % k, [128, 512], F32) for k in range(3)]
            R_t2b = [Res("t2b%d" % k) for k in range(3)]
            t0s = sb(sc, "t0s", [128, 16], F32)
            ges = sb(sc, "ges", [128, 16], F32)
            R_ts = Res("ts")
            carry = sb(sc, "carry", [128, NFC * 2], F32)
            R_carry = [Res("carry%d" % c) for c in range(NFC)]
            sfc = sb(sc, "sfc", [128, NFC * 2], F32)
            R_sfc = Res("sfc")
            sconv = sb(sc, "sconv", [128, NFC * 2], F32)
            wconv = sb(sc, "wconv", [128, NFC * 3], F32)
            bconv = sb(sc, "bconv", [128, NFC], F32)
            flag = sb(sc, "flag", [128, 1], F32)
            R_cc = Res("cc")
            ln3 = sb(sc, "ln3", [128, 2 * 1024], F32)
            R_ln3 = Res("ln3")
            h2r = [sb(sc, "h2r%d" % k, [128, 1024], F32) for k in range(2)]
            R_h2r = [Res("h2r%d" % k) for k in range(2)]
            yA = sb(sc, "yA", [128, 1024], F32)
            R_yA = Res("yA")
            yB = [sb(sc, "yB%d" % k, [128, 1024], F32) for k in range(2)]
            R_yB = [Res("yB%d" % k) for k in range(2)]
            stat = sb(sc, "statc", [128, 32], F32)
            R_stat = Res("statc")

            P.dma("sp", sconv[:, :], I["sconvT"][:, :], writes=[R_cc])
            P.dma("sp", wconv[:, :], I["wconvT"][:, :], writes=[R_cc])
            P.dma("sp", bconv[:, :], I["bconvT"][:, :], writes=[R_cc])
            P.dma("sp", flag[:, :], I["flag"][:, :], writes=[R_cc])
            for k in range(2):
                P.dma("sp", ln3[:, k * 1024:(k + 1) * 1024], I["lnp"][4 + k:5 + k, :].to_broadcast([128, 1024]), writes=[R_ln3])
            k = 0
            for c0 in range(0, NFC * 1024, 2048):
                s = k % 2
                P.dma("sp", wst[s][:, :], I["wdown"][:, c0:c0 + 2048], writes=[R_wst[s]])
                if k % 2 == 0:
                    P.op("act", _call("activation", out=wdb[:, c0:c0 + 2048], in_=wst[s][:, :], func=AF.Copy), reads=[R_wst[s]], writes=[R_wd])
                else:
                    P.op("pool", _call("tensor_copy", out=wdb[:, c0:c0 + 2048], in_=wst[s][:, :]), reads=[R_wst[s]], writes=[R_wd])
                k += 1
            P.dma("sp", h2Tm[:, :].rearrange("p (c q) -> p c q", q=18)[:, :, 0:2], h2TD[16][:, 0:16].rearrange("p (c q) -> p c q", q=2),
                  reads=[R_h2TD[16]], writes=[R_h2Tm], slow=True)
            P.dma("sp", h2Tm[:, :].rearrange("p (c q) -> p c q", q=18)[:, :, 2:18], h2TD[17][:, 0:128].rearrange("p (c q) -> p c q", q=16),
                  reads=[R_h2TD[17]], writes=[R_h2Tm], slow=True)

            checkpoint('phaseC_pre')
            UB = [0, 2, 4]
            GBK = [1, 3, 5]
            MB = 7
            YB = [6, 7]
            wk = [0]

            def ln3_out(pre_banks, qs, h2src, R_h2src, dst_ap, ys, R_ys):
                for n, bank in enumerate(pre_banks):
                    P.op("dve", _call("scalar_tensor_tensor",
                        out=yA[0:qs, n * 512:(n + 1) * 512], in0=h2src[0:qs, n * 512:(n + 1) * 512], scalar=ALPHA, in1=pb[bank][0:qs, :],
                        op0=ALU.mult, op1=ALU.add),
                        reads=[R_h2src, R_pb[bank]], writes=[R_yA])
                for c in range(2):
                    P.op("dve", _call("bn_stats", out=stat[0:qs, c * 6:(c + 1) * 6], in_=yA[0:qs, c * 512:(c + 1) * 512]),
                         reads=[R_yA], writes=[R_stat])
                P.op("dve", _call("bn_aggr", out=stat[0:qs, 12:14], in_=stat[0:qs, 0:12]), reads=[R_stat], writes=[R_stat])
                P.op("dve", _call("tensor_scalar", out=stat[0:qs, 14:15], in0=stat[0:qs, 13:14], scalar1=LN_EPS, scalar2=None, op0=ALU.add),
                     reads=[R_stat], writes=[R_stat])
                P.op("act", _call("activation", out=stat[0:qs, 15:16], in_=stat[0:qs, 14:15], func=AF.Sqrt), reads=[R_stat], writes=[R_stat])
                P.op("dve", _call("reciprocal", out=stat[0:qs, 16:17], in_=stat[0:qs, 15:16]), reads=[R_stat], writes=[R_stat])
                P.op("dve", _call("scalar_tensor_tensor", out=stat[0:qs, 17:18], in0=stat[0:qs, 12:13], scalar=-1.0, in1=stat[0:qs, 16:17],
                                                             op0=ALU.mult, op1=ALU.mult),
                     reads=[R_stat], writes=[R_stat])
                P.op("act", _call("activation", out=ys[0:qs, :], in_=yA[0:qs, :], func=AF.Identity, scale=stat[0:qs, 16:17], bias=stat[0:qs, 17:18]),
                     reads=[R_yA, R_stat], writes=[R_ys])
                P.op("pool", _call("tensor_tensor", out=ys[0:qs, :], in0=ys[0:qs, :], in1=ln3[0:qs, 0:1024], op=ALU.mult),
                     reads=[R_ys, R_ln3], writes=[R_ys])
                P.op("pool", _call("tensor_tensor", out=ys[0:qs, :], in0=ys[0:qs, :], in1=ln3[0:qs, 1024:2048], op=ALU.add),
                     reads=[R_ys, R_ln3], writes=[R_ys])
                P.dma("sp", dst_ap, ys[0:qs, :], reads=[R_ys], defer=True)

            def load_h2Tg(grp):
                gs = grp % 2
                for bi in range(4):
                    blk = grp * 4 + bi
                    P.dma("sp", h2Tg[gs][:, :].rearrange("p (c q) -> p c q", q=512)[:, :, bi * 128:(bi + 1) * 128],
                          h2TD[blk][:, :].rearrange("p (c q) -> p c q", q=128), reads=[R_h2TD[blk]], writes=[R_h2Tg[gs]])

            def c_s1(grp, c):
                s = (grp * NFC + c) % 2
                P.dma("sp", wst[s][:, :], I["wup"][c], writes=[R_wst[s]])
                P.op("dve", _call("tensor_copy", out=wsl[s][:, 0:1024], in_=wst[s][:, 0:1024]), reads=[R_wst[s]], writes=[R_wsl[s]])
                P.op("dve", _call("tensor_copy", out=wsl[s][:, 1024:2048], in_=wst[s][:, 1024:2048]), reads=[R_wst[s]], writes=[R_wslB[s]])

            def c_s2(grp, c):
                s = (grp * NFC + c) % 2
                gs = grp % 2
                mo = (c % 2) * 64
                if grp == 0:
                    for part, oc in ((0, mo), (1, mo + 32)):
                        for kc in range(KC):
                            P.op("pe", _call("matmul", out=pb[MB][:, oc:oc + 18], lhsT=wsl[s][:, kc * 256 + part * 128: kc * 256 + (part + 1) * 128],
                                             rhs=h2Tm[:, kc * 18:(kc + 1) * 18], start=(kc == 0), stop=(kc == KC - 1)),
                                 reads=[R_wsl[s], R_wslB[s], R_h2Tm], writes=[R_pb[MB]])
                k3 = (grp * NFC + c) % 3
                ub, gbk = UB[k3], GBK[k3]
                for part, bank in ((0, ub), (1, gbk)):
                    for kc in range(KC):
                        P.op("pe", _call("matmul", out=pb[bank][:, :], lhsT=wsl[s][:, kc * 256 + part * 128: kc * 256 + (part + 1) * 128],
                                         rhs=h2Tg[gs][:, kc * 512:(kc + 1) * 512], start=(kc == 0), stop=(kc == KC - 1)),
                             reads=[R_wsl[s], R_wslB[s], R_h2Tg[gs]], writes=[R_pb[bank]])

            def c_s3(grp, c):
                hTg, R_hTg = hT2[grp % 2], R_hT2[grp % 2]
                mo = (c % 2) * 64
                if grp == 0:
                    P.op("dve", _call("tensor_scalar", out=carry[:, c * 2:(c + 1) * 2], in0=pb[MB][:, mo + 32:mo + 34], scalar1=flag[:, 0:1],
                                      scalar2=None, op0=ALU.mult),
                         reads=[R_pb[MB], R_cc], writes=[R_carry[c]])
                    P.op("act", _call("activation", out=Gs[:, 0:2], in_=sconv[:, c * 2:(c + 1) * 2], func=AF.Copy), reads=[R_cc], writes=[R_Gs])
                    P.op("act", _call("activation", out=Gs[:, 2:18], in_=pb[MB][:, mo + 34:mo + 50], func=AF.Copy), reads=[R_pb[MB]], writes=[R_Gs])
                    P.op("act", _call("activation", out=t0s[:, :], in_=Gs[:, 2:18], func=AF.Identity, scale=wconv[:, c * 3 + 2:c * 3 + 3],
                                      bias=bconv[:, c:c + 1]),
                         reads=[R_Gs, R_cc], writes=[R_ts])
                    P.op("dve", _call("scalar_tensor_tensor", out=t0s[:, :], in0=Gs[:, 1:17], scalar=wconv[:, c * 3 + 1:c * 3 + 2], in1=t0s[:, :],
                                      op0=ALU.mult, op1=ALU.add),
                         reads=[R_Gs, R_cc, R_ts], writes=[R_ts])
                    P.op("dve", _call("scalar_tensor_tensor", out=t0s[:, :], in0=Gs[:, 0:16], scalar=wconv[:, c * 3:c * 3 + 1], in1=t0s[:, :],
                                      op0=ALU.mult, op1=ALU.add),
                         reads=[R_Gs, R_cc, R_ts], writes=[R_ts])
                    P.op("act", _call("activation", out=ges[:, :], in_=t0s[:, :], func=AF.Gelu_apprx_tanh), reads=[R_ts], writes=[R_ts])
                    P.op("dve", _call("tensor_tensor", out=hTm[:, c * 16:(c + 1) * 16], in0=pb[MB][:, mo + 2:mo + 18], in1=ges[:, :], op=ALU.mult),
                         reads=[R_pb[MB], R_ts], writes=[R_hTm])
                    P.op("act", _call("activation", out=sfc[:, c * 2:(c + 1) * 2], in_=Gs[:, 16:18], func=AF.Copy), reads=[R_Gs], writes=[R_sfc])
                k3 = (grp * NFC + c) % 3
                ub, gbk = UB[k3], GBK[k3]
                G, R_G = Gb[k3], R_Gb[k3]
                t0, R_t = t0b[k3], R_t0[k3]
                ge, R_g = geb[k3], R_ge[k3]
                t1, R_t1 = t1b[k3], R_t1b[k3]
                t2, R_t2 = t2b[k3], R_t2b[k3]
                P.op("act", _call("activation", out=G[:, 0:2], in_=carry[:, c * 2:(c + 1) * 2], func=AF.Copy),
                     reads=[R_carry[c]], writes=[R_G])
                P.op("act", _call("activation", out=G[:, 2:514], in_=pb[gbk][:, :], func=AF.Copy), reads=[R_pb[gbk]], writes=[R_G])
                P.op("act", _call("activation", out=carry[:, c * 2:(c + 1) * 2], in_=G[:, 512:514], func=AF.Copy),
                     reads=[R_G], writes=[R_carry[c]])
                P.op("act", _call("activation", out=t0[:, :], in_=G[:, 2:514], func=AF.Identity,
                                  scale=wconv[:, c * 3 + 2:c * 3 + 3], bias=bconv[:, c:c + 1]),
                     reads=[R_G, R_cc], writes=[R_t])
                P.op("act", _call("activation", out=t1[:, :], in_=G[:, 1:513], func=AF.Identity, scale=wconv[:, c * 3 + 1:c * 3 + 2]),
                     reads=[R_G, R_cc], writes=[R_t1])
                P.op("act", _call("activation", out=t2[:, :], in_=G[:, 0:512], func=AF.Identity, scale=wconv[:, c * 3:c * 3 + 1]),
                     reads=[R_G, R_cc], writes=[R_t2])
                P.op("dve", _call("tensor_tensor", out=t0[:, :], in0=t0[:, :], in1=t1[:, :], op=ALU.add), reads=[R_t, R_t1], writes=[R_t])
                P.op("dve", _call("tensor_tensor", out=t0[:, :], in0=t0[:, :], in1=t2[:, :], op=ALU.add), reads=[R_t, R_t2], writes=[R_t])
                P.op("act", _call("activation", out=ge[:, :], in_=t0[:, :], func=AF.Gelu_apprx_tanh), reads=[R_t], writes=[R_g])
                P.op("dve", _call("tensor_tensor", out=hTg[:, c * 512:(c + 1) * 512], in0=pb[ub][:, :], in1=ge[:, :], op=ALU.mult),
                     reads=[R_pb[ub], R_g], writes=[R_hTg])

            def c_down(grp):
                hTg, R_hTg = hT2[grp % 2], R_hT2[grp % 2]
                if grp == 0:
                    for n, bank in enumerate(YB):
                        for c in range(NFC):
                            P.op("pe", _call("matmul", out=pb[bank][0:16, :], lhsT=hTm[:, c * 16:(c + 1) * 16],
                                             rhs=wdb[:, c * 1024 + n * 512: c * 1024 + (n + 1) * 512], start=(c == 0), stop=(c == NFC - 1)),
                                 reads=[R_hTm, R_wd], writes=[R_pb[bank]])
                    P.dma("sp", h2r[0][0:16, :], h2D[17 * 128: 17 * 128 + 16, :], reads=[R_h2D[17]], writes=[R_h2r[0]])
                    ln3_out(YB, 16, h2r[0], R_h2r[0], O["ys"][:, :], yB[0], R_yB[0])
                    P.dma("sp", O["sfcT"][:, :], sfc[:, :], reads=[R_sfc], defer=True)
                for bi in range(4):
                    blk = grp * 4 + bi
                    hs = blk % 2
                    P.dma("sp", h2r[hs][:, :], h2D[blk * 128:(blk + 1) * 128, :], reads=[R_h2D[blk]], writes=[R_h2r[hs]])
                    for n, bank in enumerate(YB):
                        for c in range(NFC):
                            P.op("pe", _call("matmul", out=pb[bank][:, :], lhsT=hTg[:, c * 512 + bi * 128: c * 512 + (bi + 1) * 128],
                                             rhs=wdb[:, c * 1024 + n * 512: c * 1024 + (n + 1) * 512], start=(c == 0), stop=(c == NFC - 1)),
                                 reads=[R_hTg, R_wd], writes=[R_pb[bank]])
                    ln3_out(YB, 128, h2r[hs], R_h2r[hs], O["y"][blk * 128:(blk + 1) * 128, :], yB[hs], R_yB[hs])

            seq = [(grp, c) for grp in range(4) for c in range(NFC)]
            nseq = len(seq)
            load_h2Tg(0)
            load_h2Tg(1)
            for idx in range(nseq + 2):
                if idx < nseq:
                    c_s1(*seq[idx])
                if 1 <= idx <= nseq:
                    c_s2(*seq[idx - 1])
                if idx >= 2:
                    g3, c3 = seq[idx - 2]
                    c_s3(g3, c3)
                    if c3 == NFC - 1:
                        c_down(g3)
                        if g3 + 2 < 4:
                            load_h2Tg(g3 + 2)
            P.dma("sp", O["fcT"][:, :], carry[:, :], reads=R_carry, defer=True)
            P.finish()
            P.flush(block)
    return nc


def _t5_bucket(rel):
    half, max_exact = 16, 8
    n = np.abs(rel)
    log_ratio = np.log(np.maximum(n, 1).astype(np.float32) / max_exact) / math.log(128 / max_exact)
    large = np.minimum(max_exact + (log_ratio * (half - max_exact)).astype(np.int32), half - 1)
    return np.where(rel < 0, half, 0) + np.where(n < max_exact, n, large)


def _host_inputs(inp):
    f32 = np.float32
    x_prompt = np.asarray(inp["x_prompt"], f32)
    x_sample = np.asarray(inp["x_sample"], f32)
    w_in = np.asarray(inp["w_in"], f32)[0]
    qa, ka, va = w_in[:, 0:512], w_in[:, 512:1024], w_in[:, 1024:1536]
    qb, kb, vb = w_in[:, 1536:2048], w_in[:, 2048:2176], w_in[:, 2176:2304]
    qi, ki, wi = w_in[:, 2304:2816], w_in[:, 2816:2880], w_in[:, 2880:2888]
    qbp = np.concatenate([np.concatenate([qb[:, r * 64:(r + 1) * 64], qb[:, (4 + r) * 64:(5 + r) * 64]], axis=1) for r in range(4)], axis=1)
    winp = np.concatenate([qa, ka, qbp, kb, qi, ki, ki, va, vb, wi], axis=1)
    assert winp.shape[1] == NCOL

    def kc_layout(w):
        n = w.shape[1]
        return np.ascontiguousarray(w.reshape(8, 128, n).transpose(1, 0, 2).reshape(128, 8 * n))

    shared = {}
    shared["win"] = kc_layout(winp)
    shared["wo"] = kc_layout(np.asarray(inp["w_o"], f32)[0])
    shared["wmq"] = kc_layout(np.asarray(inp["w_mq"], f32)[0])
    shared["wmk"] = kc_layout(np.asarray(inp["w_mk"], f32)[0])
    shared["wmv"] = kc_layout(np.asarray(inp["w_mv"], f32)[0])
    wmo = np.asarray(inp["w_mo"], f32)[0]
    shared["wmo"] = np.ascontiguousarray(wmo.reshape(4, 128, 1024).transpose(1, 0, 2).reshape(128, 4096))
    w_up = np.asarray(inp["w_up"], f32)[0]
    wu = w_up[:, :DFF].reshape(8, 128, NFC, 128)
    wg = w_up[:, DFF:].reshape(8, 128, NFC, 128)
    wup = np.stack([wu, wg], axis=3)
    shared["wup"] = np.ascontiguousarray(wup.transpose(2, 1, 0, 3, 4).reshape(NFC, 128, 8 * 256))
    w_down = np.asarray(inp["w_down"], f32)[0]
    shared["wdown"] = np.ascontiguousarray(w_down.reshape(NFC, 128, 1024).transpose(1, 0, 2).reshape(128, NFC * 1024))
    shared["lnp"] = np.ascontiguousarray(np.stack([np.asarray(inp[k], f32)[0] for k in ("ln1_g", "ln1_b", "ln2_g", "ln2_b", "ln3_g", "ln3_b")]))
    w_conv = np.asarray(inp["w_conv"], f32)[0]
    shared["wconvT"] = np.ascontiguousarray(w_conv.reshape(3, NFC, 128).transpose(2, 1, 0).reshape(128, NFC * 3))
    shared["bconvT"] = np.ascontiguousarray(np.asarray(inp["b_conv"], f32)[0].reshape(NFC, 128).T)
    shared["ident"] = np.eye(128, dtype=f32)
    tabA = np.asarray(inp["a_rel_bias"], f32)[0]
    qq = np.arange(128)[:, None]
    kk = np.arange(640)[None, :]
    kpos = kk - 512
    rel = qq - kpos
    cq = qq // 64
    kch = np.floor_divide(kpos, 64)
    allowed = (kch >= cq - 8) & (kch <= cq)
    bias = tabA[np.clip(rel, -64, 64) + 64]
    AB = np.where(allowed[:, :, None], bias, f32(NEGM)).astype(f32)
    shared["AB"] = np.ascontiguousarray(AB.transpose(0, 2, 1).reshape(128, 8 * ABW))
    js = np.arange(16)[:, None]
    ks = np.arange(528)[None, :]
    ABs = tabA[np.clip(512 + js - ks, -64, 64) + 64]
    shared["ABs"] = np.ascontiguousarray(ABs.transpose(0, 2, 1).reshape(16, 8 * 528)).astype(f32)
    t5 = np.asarray(inp["t5_bias"], f32)
    relB = np.arange(128)[:, None] - np.arange(256)[None, :] + 128
    Bn = t5[_t5_bucket(relB)]
    shared["Bn"] = np.ascontiguousarray(Bn.transpose(0, 2, 1).reshape(128, 8 * BNW)).astype(f32)
    relBs = 128 + np.arange(16)[:, None] - np.arange(144)[None, :]
    Bns = t5[_t5_bucket(relBs)]
    shared["Bns"] = np.ascontiguousarray(Bns.transpose(0, 2, 1).reshape(16, 8 * 144)).astype(f32)
    shared["C15"] = np.ascontiguousarray(np.broadcast_to(t5[15][None, :], (128, 8))).astype(f32)
    dm = np.zeros((128, 128), f32)
    dm[0:64, 64:128] = NEGM
    shared["diagmask"] = dm

    mem_prompt = np.asarray(inp["mem_prompt"], f32)
    maps = []
    for c in range(8):
        b, half = c // 2, c % 2
        m = dict(shared)
        xk = np.zeros((4096, 1024), f32)
        if half == 1:
            xk[:] = x_prompt[b]
        else:
            xk[2048:] = x_prompt[b, :2048]
        m["xkT"] = np.ascontiguousarray(xk.reshape(32, 128, 8, 128).transpose(0, 3, 2, 1).reshape(32, 128, 1024))
        xs = x_sample[c]
        m["xsT"] = np.ascontiguousarray(xs.reshape(16, 8, 128).transpose(2, 1, 0).reshape(128, 128))
        xres = np.zeros((NBLK * 128, 1024), f32)
        xres[0:2048] = xk[2048:]
        xres[2048:2050] = xk[2046:2048]
        xres[17 * 128:17 * 128 + 16] = xs
        m["xres"] = xres
        m["memT"] = np.ascontiguousarray(mem_prompt[b].reshape(256, 8, 128).transpose(2, 1, 0).reshape(128, 2048))
        cmk = np.asarray(inp["cache_mem_k"], f32)[0, c]
        m["cmkT"] = np.ascontiguousarray(cmk.transpose(2, 1, 0).reshape(128, 1024))
        m["cmv"] = np.ascontiguousarray(np.asarray(inp["cache_mem_v"], f32)[0, c].reshape(256, 512))
        cak = np.asarray(inp["cache_a_k"], f32)[0, c]
        m["cakT"] = np.ascontiguousarray(cak.reshape(512, 4, 2, 64).transpose(2, 3, 1, 0).reshape(128, 2048))
        m["cav"] = np.ascontiguousarray(np.asarray(inp["cache_a_v"], f32)[0, c].reshape(512, 512))
        cbk = np.asarray(inp["cache_b_k"], f32)[0, c]
        m["cbkT"] = np.ascontiguousarray(cbk.reshape(2048, 128).T)
        m["cbv"] = np.ascontiguousarray(np.asarray(inp["cache_b_v"], f32)[0, c].reshape(2048, 128))
        cbi = np.asarray(inp["cache_b_kidx"], f32)[0, c]
        m["cbiT"] = np.ascontiguousarray(np.concatenate([cbi.T, cbi.T], axis=0))
        sc_ = np.asarray(inp["state_ffn_conv"], f32)[0, c]
        m["sconvT"] = np.ascontiguousarray(sc_.reshape(2, NFC, 128).transpose(2, 1, 0).reshape(128, NFC * 2))
        m["colmask"] = np.full((128, 1), NEGM if half == 0 else 0.0, f32)
        kv = np.ones((128, NT), f32)
        if half == 0:
            kv[:, 0:16] = 0.0
        m["kvalid"] = kv
        m["flag"] = np.full((128, 1), float(half), f32)
        maps.append(m)
    return maps


_NC_CACHE = {}


def _run(inputs, debug=False):
    key = bool(debug)
    if key not in _NC_CACHE:
        _NC_CACHE[key] = build_program(debug=debug)
    nc = _NC_CACHE[key]
    maps = _host_inputs(inputs)
    res = run_bass_kernel_spmd(nc, maps, core_ids=list(range(8)))
    return res.results


def kernel(**inputs):
    R = _run(inputs)
    f32 = np.float32
    y = np.zeros((4, 4096, 1024), f32)
    ys = np.zeros((8, 16, 1024), f32)
    pak = np.zeros((1, 4, 512, 8, 64), f32)
    pav = np.zeros((1, 4, 512, 8, 64), f32)
    pbk = np.zeros((1, 4, 4096, 2, 64), f32)
    pbv = np.zeros((1, 4, 4096, 2, 64), f32)
    pbi = np.zeros((1, 4, 4096, 64), f32)
    pmk = np.zeros((1, 4, 256, 4, 128), f32)
    pmv = np.zeros((1, 4, 256, 4, 128), f32)
    pfc = np.zeros((1, 4, 2, DFF), f32)
    sak = np.zeros((1, 8, 16, 8, 64), f32)
    sav = np.zeros((1, 8, 16, 8, 64), f32)
    sbk = np.zeros((1, 8, 16, 2, 64), f32)
    sbv = np.zeros((1, 8, 16, 2, 64), f32)
    sbi = np.zeros((1, 8, 16, 64), f32)
    sfc = np.zeros((1, 8, 2, DFF), f32)
    for c in range(8):
        b, half = c // 2, c % 2
        r = R[c]
        y[b, half * 2048:(half + 1) * 2048] = np.asarray(r["y"], f32)
        ys[c] = np.asarray(r["ys"], f32)
        if half == 1:
            akT = np.asarray(r["akT"], f32).reshape(2, 64, 4, 512)
            pak[0, b] = akT.transpose(3, 2, 0, 1).reshape(512, 8, 64)
            pav[0, b] = np.asarray(r["av"], f32).reshape(512, 8, 64)
            pbk[0, b] = np.asarray(r["bkT"], f32).T.reshape(4096, 2, 64)
            pbv[0, b] = np.asarray(r["bv"], f32).reshape(4096, 2, 64)
            pbi[0, b] = np.asarray(r["biT"], f32).T
            pmk[0, b] = np.asarray(r["mkT"], f32).reshape(128, 4, 256).transpose(2, 1, 0)
            pmv[0, b] = np.asarray(r["mv"], f32).reshape(256, 4, 128)
            pfc[0, b] = np.asarray(r["fcT"], f32).reshape(128, NFC, 2).transpose(2, 1, 0).reshape(2, DFF)
        sakT = np.asarray(r["sakT"], f32).reshape(2, 64, 4, 16)
        sak[0, c] = sakT.transpose(3, 2, 0, 1).reshape(16, 8, 64)
        sav[0, c] = np.asarray(r["sav"], f32).reshape(16, 8, 64)
        sbk[0, c] = np.asarray(r["sbkT"], f32).T.reshape(16, 2, 64)
        sbv[0, c] = np.asarray(r["sbv"], f32).reshape(16, 2, 64)
        sbi[0, c] = np.asarray(r["sbiT"], f32).T
        sfc[0, c] = np.asarray(r["sfcT"], f32).reshape(128, NFC, 2).transpose(2, 1, 0).reshape(2, DFF)
    return (y, ys, pak, pav, pbk, pbv, pbi, pmk, pmv, pfc, sak, sav, sbk, sbv, sbi, sfc)
```

```python
import math
from contextlib import ExitStack

import numpy as np
import concourse.bass as bass
import concourse.mybir as mybir
from concourse.bass_utils import run_bass_kernel_spmd

F32 = mybir.dt.float32
BF16 = mybir.dt.bfloat16
AF = mybir.ActivationFunctionType
ALU = mybir.AluOpType

D = 1024
KC = 8
NT = 32
NCOL = 2952
C_QA, C_KA, C_QB, C_KB, C_QI, C_KI, C_VA, C_VB, C_WI = 0, 512, 1024, 1536, 1664, 2176, 2304, 2816, 2944
DFF = 2816
NFC = 22
ALPHA = 2.0 ** 0.25
LN_EPS = 1e-5
NEGM = -30000.0
NIT = 16
BIS_W0 = 16.0
ABW = 640
BNW = 256
NBLK = 18


class Res:
    __slots__ = ("lw", "rd", "name", "excl")

    def __init__(self, name="", excl=False):
        self.lw = None
        self.rd = {}
        self.name = name
        self.excl = excl


def _call(name, *args, **kw):
    return lambda e: getattr(e, name)(*args, **kw)


class Prog:
    ENG = ("pe", "act", "dve", "pool", "sp")

    def __init__(self, nc, sems, dma_sems):
        self.nc = nc
        self.streams = {e: [] for e in self.ENG}
        self.sem = sems
        self.cnt = {e: 0 for e in self.ENG}
        self.seen = {e: {} for e in self.ENG}
        self.dsems = dma_sems
        self.dval = [0] * len(dma_sems)
        self.dnext = 0
        self.semh = dict(sems)
        for i, h in enumerate(dma_sems):
            self.semh[("d", i)] = h
        self.ninst = 0
        self.dead = False
        self.deferred = []
        self.defer_lag = 48

    def _deps(self, reads, writes, eng=None):
        d = {}
        for r in reads:
            if r.lw is not None:
                k, v = r.lw
                if d.get(k, 0) < v:
                    d[k] = v
            if r.excl:
                for k, v in r.rd.items():
                    if k != eng and d.get(k, 0) < v:
                        d[k] = v
        for w in writes:
            if w.lw is not None:
                k, v = w.lw
                if d.get(k, 0) < v:
                    d[k] = v
            for k, v in w.rd.items():
                if d.get(k, 0) < v:
                    d[k] = v
        return d

    def _wait(self, eng, deps):
        for k, v in deps.items():
            if k == "pe" and eng == "pe":
                continue
            if self.seen[eng].get(k, 0) >= v:
                continue
            self.seen[eng][k] = v
            h = self.semh[k]
            self.streams[eng].append(lambda e, h=h, v=v: e.wait_ge(h, v))

    def _flush_deferred(self, force=False, reads=(), writes=()):
        if not self.deferred:
            return
        conflict = force
        if not conflict:
            ws = set(id(w) for w in writes)
            rs = set(id(r) for r in reads)
            for d in self.deferred:
                dr = set(id(x) for x in d[3])
                dw = set(id(x) for x in d[4])
                if (ws & dr) or (ws & dw) or (rs & dw):
                    conflict = True
                    break
        if conflict:
            pend, self.deferred = self.deferred, []
            for d in pend:
                self._dma_now(d[0], d[1], d[2], d[3], d[4], d[5])
            return
        while self.deferred and self.ninst - self.deferred[0][6] >= self.defer_lag:
            d = self.deferred.pop(0)
            self._dma_now(d[0], d[1], d[2], d[3], d[4], d[5])

    def op(self, eng, fn, reads=(), writes=()):
        if self.dead:
            return
        self._flush_deferred(False, reads, writes)
        self._wait(eng, self._deps(reads, writes, eng))
        self.cnt[eng] += 1
        n = self.cnt[eng]
        h = self.sem[eng]
        self.streams[eng].append(lambda e, fn=fn, h=h: fn(e).then_inc(h, 1))
        self.ninst += 1
        for r in reads:
            if r.rd.get(eng, 0) < n:
                r.rd[eng] = n
        for w in writes:
            w.lw = (eng, n)
            w.rd = {}

    def dma(self, q, out, in_, reads=(), writes=(), slow=False, defer=False):
        if self.dead:
            return
        if defer:
            self._flush_deferred(False, reads, writes)
            self.deferred.append((q, out, in_, list(reads), list(writes), slow, self.ninst))
            return
        self._flush_deferred(False, reads, writes)
        self._dma_now(q, out, in_, reads, writes, slow)

    def _dma_now(self, q, out, in_, reads=(), writes=(), slow=False):
        deps = self._deps(reads, writes)
        i = self.dnext
        self.dnext = (i + 1) % len(self.dsems)
        k = ("d", i)
        if self.dval[i] > 0 and deps.get(k, 0) < self.dval[i]:
            deps[k] = self.dval[i]
        self._wait(q, deps)
        self.dval[i] += 16
        v = self.dval[i]
        h = self.dsems[i]
        if slow:
            self.streams[q].append(
                lambda e, out=out, in_=in_, h=h: e.dma_start(out=out, in_=in_, allow_slow_non_contiguous=True).then_inc(h, 16))
        else:
            self.streams[q].append(lambda e, out=out, in_=in_, h=h: e.dma_start(out=out, in_=in_).then_inc(h, 16))
        self.ninst += 1
        for r in reads:
            if r.rd.get(k, 0) < v:
                r.rd[k] = v
        for w in writes:
            w.lw = (k, v)
            w.rd = {}

    def barrier(self):
        if self.dead:
            return
        self._flush_deferred(True)
        deps = {e: self.cnt[e] for e in self.ENG if self.cnt[e] > 0}
        for i, v in enumerate(self.dval):
            if v > 0:
                deps[("d", i)] = v
        for e in self.ENG:
            self._wait(e, dict(deps))

    def finish(self):
        self._flush_deferred(True)
        deps = {("d", i): v for i, v in enumerate(self.dval) if v > 0}
        self._wait("sp", deps)

    def flush(self, block):
        self._flush_deferred(True)
        s = self.streams
        self.streams = {e: [] for e in self.ENG}

        def mk(lst):
            def body(e):
                for f in lst:
                    f(e)
            return body

        block.tensor(mk(s["pe"]))
        block.scalar(mk(s["act"]))
        block.vector(mk(s["dve"]))
        block.gpsimd(mk(s["pool"]))
        block.sync(mk(s["sp"]))


def build_program(debug=False, stop_at=None):
    nc = bass.Bass("TRN2", target_bir_lowering=False)

    def din(name, shape, dt=F32):
        return nc.dram_tensor(name, list(shape), dt, kind="ExternalInput").ap()

    def dout(name, shape, dt=F32):
        return nc.dram_tensor(name, list(shape), dt, kind="ExternalOutput").ap()

    def dscr(name, shape, dt):
        return nc.dram_tensor(name, list(shape), dt, kind="Internal").ap()

    I = {}
    I["xkT"] = din("xkT", [NT, 128, 1024])
    I["xsT"] = din("xsT", [128, 8 * 16])
    I["xres"] = din("xres", [NBLK * 128, 1024])
    I["win"] = din("win", [128, KC * NCOL])
    I["wo"] = din("wo", [128, 8 * 1024])
    I["wmq"] = din("wmq", [128, 8 * 512])
    I["wmk"] = din("wmk", [128, 8 * 512])
    I["wmv"] = din("wmv", [128, 8 * 512])
    I["wmo"] = din("wmo", [128, 4 * 1024])
    I["wup"] = din("wup", [NFC, 128, 8 * 256])
    I["wdown"] = din("wdown", [128, NFC * 1024])
    I["lnp"] = din("lnp", [6, 1024])
    I["wconvT"] = din("wconvT", [128, NFC * 3])
    I["bconvT"] = din("bconvT", [128, NFC])
    I["memT"] = din("memT", [128, 8 * 256])
    I["cmkT"] = din("cmkT", [128, 4 * 256])
    I["cmv"] = din("cmv", [256, 512])
    I["cakT"] = din("cakT", [128, 4 * 512])
    I["cav"] = din("cav", [512, 512])
    I["cbkT"] = din("cbkT", [128, 2048])
    I["cbv"] = din("cbv", [2048, 128])
    I["cbiT"] = din("cbiT", [128, 2048])
    I["sconvT"] = din("sconvT", [128, NFC * 2])
    I["ident"] = din("ident", [128, 128])
    I["AB"] = din("AB", [128, 8 * ABW])
    I["ABs"] = din("ABs", [16, 8 * 528])
    I["Bn"] = din("Bn", [128, 8 * BNW])
    I["Bns"] = din("Bns", [16, 8 * 144])
    I["C15"] = din("C15", [128, 8])
    I["colmask"] = din("colmask", [128, 1])
    I["diagmask"] = din("diagmask", [128, 128])
    I["kvalid"] = din("kvalid", [128, NT])
    I["flag"] = din("flag", [128, 1])

    O = {}
    O["y"] = dout("y", [2048, 1024])
    O["ys"] = dout("ys", [16, 1024])
    O["akT"] = dout("akT", [128, 4 * 512])
    O["av"] = dout("av", [512, 512])
    O["bkT"] = dout("bkT", [128, 4096])
    O["bv"] = dout("bv", [4096, 128])
    O["biT"] = dout("biT", [64, 4096])
    O["mkT"] = dout("mkT", [128, 4 * 256])
    O["mv"] = dout("mv", [256, 512])
    O["fcT"] = dout("fcT", [128, NFC * 2])
    O["sakT"] = dout("sakT", [128, 4 * 16])
    O["sav"] = dout("sav", [16, 512])
    O["sbkT"] = dout("sbkT", [128, 16])
    O["sbv"] = dout("sbv", [16, 128])
    O["sbiT"] = dout("sbiT", [64, 16])
    O["sfcT"] = dout("sfcT", [128, NFC * 2])
    if debug:
        O["dbg_mix"] = dout("dbg_mix", [NBLK * 128, 1024], BF16)
        O["dbg_h2"] = dout("dbg_h2", [NBLK * 128, 1024])
        mixD = O["dbg_mix"]
        h2D = O["dbg_h2"]
    else:
        mixD = dscr("mixD", [NBLK * 128, 1024], BF16)
        h2D = dscr("h2D", [NBLK * 128, 1024], F32)
    h2TD = dscr("h2TD", [NBLK, 128, 1024], BF16)
    R_mixD = [Res("mixD%d" % i) for i in range(NBLK)]
    R_h2D = [Res("h2D%d" % i) for i in range(NBLK)]
    R_h2TD = [Res("h2TD%d" % i) for i in range(NBLK)]

    es = ExitStack()
    with es:
        sems = {e: es.enter_context(nc.semaphore("s_" + e)) for e in Prog.ENG}
        dsems = [es.enter_context(nc.semaphore("d%d" % i)) for i in range(32)]
        P = Prog(nc, sems, dsems)
        block = es.enter_context(nc.Block())

        def checkpoint(name):
            if stop_at is not None and name == stop_at and not P.dead:
                P.finish()
                P.flush(block)
                P.dead = True

        pb = [es.enter_context(nc.psum_tensor("pb%d" % i, [128, 512], F32)) for i in range(8)]
        R_pb = [Res("pb%d" % i, excl=True) for i in range(8)]

        class Rot:
            def __init__(self, idxs):
                self.idxs = idxs
                self.i = 0

            def next(self):
                k = self.idxs[self.i % len(self.idxs)]
                self.i += 1
                return k

        def sb(stack, name, shape, dt):
            return stack.enter_context(nc.sbuf_tensor("sb_" + name, list(shape), dt))

        ident_f = sb(es, "ident_f", [128, 128], F32)
        ident = sb(es, "ident", [128, 512], BF16)
        R_ident = Res("ident")
        P.dma("sp", ident_f[:, :], I["ident"][:, :], writes=[R_ident])
        for r in range(4):
            P.op("act", _call("activation", out=ident[:, r * 128:(r + 1) * 128], in_=ident_f[:, :], func=AF.Copy),
                 reads=[R_ident], writes=[R_ident])

        def run_interleaved(gens):
            gens = [[0.0, i, g] for i, g in enumerate(gens)]
            while gens:
                gens.sort(key=lambda x: (x[0], x[1]))
                ent = gens[0]
                try:
                    c = next(ent[2])
                    ent[0] += (c if c else 1.0)
                except StopIteration:
                    gens.remove(ent)

        with ExitStack() as sa:
            winb = sb(sa, "winb", [128, KC * NCOL], BF16)
            R_win = Res("win")
            kbi = sb(sa, "kbi", [128, 2 * 4096], BF16)
            R_kbi = [Res("kbi%d" % r) for r in range(NT)]
            R_ki = [Res("ki%d" % r) for r in range(NT)]
            vb_aug = sb(sa, "vb_aug", [128, NT * 2 * 65], BF16)
            R_vb = [Res("vb%d" % r) for r in range(NT)]
            kaT = sb(sa, "kaT", [128, 6 * 512], BF16)
            R_ka = [Res("ka%d" % s) for s in range(6)]
            va_aug = sb(sa, "va_aug", [128, 6 * 8 * 65], BF16)
            R_va = [Res("va%d" % s) for s in range(6)]
            ABb = sb(sa, "ABb", [128, 8 * ABW], BF16)
            R_AB = Res("AB")
            Bnb = sb(sa, "Bnb", [128, 8 * BNW], BF16)
            R_Bn = Res("Bn")
            Mnear = [sb(sa, "Mnear%d" % k, [128, 8 * BNW], BF16) for k in range(2)]
            R_Mnear = [Res("Mnear%d" % k) for k in range(2)]
            score = [sb(sa, "score%d" % k, [128, 4096], F32) for k in range(2)]
            R_score = [Res("score%d" % k) for k in range(2)]
            Mb = [sb(sa, "Mb%d" % k, [128, 4096], BF16) for k in range(2)]
            R_M = [Res("M%d" % k) for k in range(2)]
            relu = [sb(sa, "relu%d" % k, [128, 512], BF16) for k in range(3)]
            R_relu = [Res("relu%d" % k) for k in range(3)]
            xstg2 = [sb(sa, "xstg%d" % k, [128, 1024], F32) for k in range(2)]
            R_xstg2 = [Res("xstg%d" % k) for k in range(2)]
            xstg, R_xstg = xstg2[0], R_xstg2[0]
            xTb = [sb(sa, "xTb%d" % k, [128, 1024], BF16) for k in range(2)]
            R_xT = [Res("xT%d" % k) for k in range(2)]
            qaz = [sb(sa, "qaz%d" % k, [128, 1024], BF16) for k in range(2)]
            qbz = [sb(sa, "qbz%d" % k, [128, 1024], BF16) for k in range(3)]
            qiz = [sb(sa, "qiz%d" % k, [128, 1024], BF16) for k in range(2)]
            R_qa = [Res("qa%d" % k) for k in range(2)]
            R_qb = [Res("qb%d" % k) for k in range(3)]
            R_qi = [Res("qi%d" % k) for k in range(2)]
            coef = [sb(sa, "coef%d" % k, [128, 8], F32) for k in range(2)]
            R_coef = [Res("coef%d" % k) for k in range(2)]
            dg = [sb(sa, "dg%d" % k, [128, 1024], BF16) for k in range(2)]
            R_dg = [Res("dg%d" % k) for k in range(2)]
            PTA = [sb(sa, "PTA%d" % k, [128, 512], BF16) for k in range(3)]
            R_PTA = [Res("PTA%d" % k) for k in range(3)]
            PTB = [sb(sa, "PTB%d" % k, [128, 512], BF16) for k in range(3)]
            R_PTB = [Res("PTB%d" % k) for k in range(3)]
            mixb = [sb(sa, "mixb%d" % k, [128, 1024], BF16) for k in range(3)]
            R_mix = [Res("mix%d" % k) for k in range(3)]
            ostg = [sb(sa, "ostg%d" % k, [128, 256], F32) for k in range(2)]
            R_ostg = [Res("ostg%d" % k) for k in range(2)]
            vbstg = [sb(sa, "vbstg%d" % k, [128, 128], F32) for k in range(2)]
            R_vbstg = [Res("vbstg%d" % k) for k in range(2)]
            astg = sb(sa, "astg", [128, 1024], F32)
            R_astg = Res("astg")
            small = [sb(sa, "small%d" % k, [128, 16], F32) for k in range(2)]
            R_small = [Res("small%d" % k) for k in range(2)]
            recA = [sb(sa, "recA%d" % k, [128, 8], F32) for k in range(2)]
            R_recA = [Res("recA%d" % k) for k in range(2)]
            recB = [sb(sa, "recB%d" % k, [128, 8], F32) for k in range(2)]
            R_recB = [Res("recB%d" % k) for k in range(2)]
            colmask = sb(sa, "colmask", [128, 1], F32)
            diagm = sb(sa, "diagm", [128, 128], F32)
            kvalid = sb(sa, "kvalid", [128, NT], F32)
            c15 = sb(sa, "c15", [128, 8], F32)
            ones8 = sb(sa, "ones8", [128, 8], F32)
            R_cst = Res("cst")

            wrot = Rot([0, 1, 2])

            P.dma("sp", colmask[:, :], I["colmask"][:, :], writes=[R_cst])
            P.dma("sp", diagm[:, :], I["diagmask"][:, :], writes=[R_cst])
            P.dma("sp", kvalid[:, :], I["kvalid"][:, :], writes=[R_cst])
            P.dma("sp", c15[:, :], I["C15"][:, :], writes=[R_cst])
            P.op("pool", _call("memset", ones8[:, :], 1.0), writes=[R_cst])
            for k in range(2):
                P.op("pool", _call("memset", qaz[k][:, :], 0.0), writes=[R_qa[k]])
                P.op("pool", _call("memset", qiz[k][:, :], 0.0), writes=[R_qi[k]])
            for k in range(3):
                P.op("pool", _call("memset", qbz[k][:, :], 0.0), writes=[R_qb[k]])

            HW = NCOL // 2
            for kc in range(KC):
                for hh in range(2):
                    stg, R_stg = score[hh], R_score[hh]
                    P.dma("sp", stg[:, 0:HW], I["win"][:, kc * NCOL + hh * HW: kc * NCOL + (hh + 1) * HW], writes=[R_stg])
                    if hh == 0:
                        P.op("act", _call("activation", out=winb[:, kc * NCOL + hh * HW: kc * NCOL + (hh + 1) * HW], in_=stg[:, 0:HW], func=AF.Copy),
                             reads=[R_stg], writes=[R_win])
                    else:
                        P.op("dve", _call("tensor_copy", out=winb[:, kc * NCOL + hh * HW: kc * NCOL + (hh + 1) * HW], in_=stg[:, 0:HW]),
                             reads=[R_stg], writes=[R_win])
            for hh in range(2):
                w = 4 * ABW
                P.dma("sp", score[hh][:, 0:w], I["AB"][:, hh * w:(hh + 1) * w], writes=[R_score[hh]])
                P.op("act", _call("activation", out=ABb[:, hh * w:(hh + 1) * w], in_=score[hh][:, 0:w], func=AF.Copy),
                     reads=[R_score[hh]], writes=[R_AB])
            P.dma("sp", score[0][:, 0:8 * BNW], I["Bn"][:, :], writes=[R_score[0]])
            for h in range(8):
                P.op("dve", _call("tensor_scalar", out=Bnb[:, h * BNW:(h + 1) * BNW], in0=score[0][:, h * BNW:(h + 1) * BNW],
                                  scalar1=c15[:, h:h + 1], scalar2=None, op0=ALU.subtract),
                     reads=[R_score[0], R_cst], writes=[R_Bn])
            checkpoint('consts')

            def win_cols(kc, c0, n):
                return winb[:, kc * NCOL + c0: kc * NCOL + c0 + n]

            def fm_proj(bank, xT, R_x, N, col0, nchunks, ocol=0):
                for j in range(nchunks):
                    for kc in range(KC):
                        P.op("pe", _call("matmul", out=pb[bank][:, ocol + j * N: ocol + (j + 1) * N], lhsT=win_cols(kc, col0 + j * 128, 128),
                                         rhs=xT[:, kc * N:(kc + 1) * N], start=(kc == 0), stop=(kc == KC - 1)),
                             reads=[R_win, R_x], writes=[R_pb[bank]])

            def tm_proj(bank, xT, R_x, N, col0, ncols, ocol=0):
                for kc in range(KC):
                    P.op("pe", _call("matmul", out=pb[bank][0:N, ocol:ocol + ncols], lhsT=xT[:, kc * N:(kc + 1) * N],
                                     rhs=win_cols(kc, col0, ncols), start=(kc == 0), stop=(kc == KC - 1)),
                         reads=[R_win, R_x], writes=[R_pb[bank]])

            def load_xT(r, eng="pool"):
                s = r % 2
                P.dma("sp", xstg2[s][:, :], I["xkT"][r], writes=[R_xstg2[s]])
                P.op(eng, _call("tensor_copy", out=xTb[s][:, :], in_=xstg2[s][:, :]), reads=[R_xstg2[s]], writes=[R_xT[s]])

            def kside(r, full):
                s = r % 2
                xT, R_x = xTb[s], R_xT[s]
                so = r % 2
                bk = wrot.next()
                fm_proj(bk, xT, R_x, 128, C_KB, 1)
                fm_proj(bk, xT, R_x, 128, C_KI, 1, ocol=128)
                P.op("act", _call("activation", out=ostg[so][:, :], in_=pb[bk][:, 0:256], func=AF.Copy), reads=[R_pb[bk]], writes=[R_ostg[so]])
                P.op("pool", _call("tensor_copy", out=kbi[:, :].rearrange("p (a c) -> p a c", a=2)[:, :, r * 128:(r + 1) * 128],
                                   in_=ostg[so][:, :].rearrange("p (a c) -> p a c", a=2)),
                     reads=[R_ostg[so]], writes=[R_kbi[r], R_ki[r]])
                P.dma("sp", O["bkT"][:, r * 128:(r + 1) * 128], ostg[so][:, 0:128], reads=[R_ostg[so]], defer=True)
                P.dma("sp", O["biT"][:, r * 128:(r + 1) * 128], ostg[so][0:64, 128:256], reads=[R_ostg[so]], defer=True)
                yield 3.0
                bv_ = wrot.next()
                tm_proj(bv_, xT, R_x, 128, C_VB, 128)
                vbv = vb_aug[:, r * 130:(r + 1) * 130].rearrange("p (g d) -> p g d", d=65)
                P.op("act", _call("activation", out=vbstg[so][:, :], in_=pb[bv_][:, 0:128], func=AF.Copy), reads=[R_pb[bv_]], writes=[R_vbstg[so]])
                P.op("pool", _call("tensor_copy", out=vbv[:, :, 0:64], in_=vbstg[so][:, :].rearrange("p (g d) -> p g d", d=64)),
                     reads=[R_vbstg[so]], writes=[R_vb[r]])
                P.op("pool", _call("tensor_scalar", out=vbv[:, :, 64:65], in0=ones8[:, 0:2].rearrange("p (g o) -> p g o", o=1),
                                   scalar1=kvalid[:, r:r + 1], scalar2=None, op0=ALU.mult),
                     reads=[R_cst], writes=[R_vb[r]])
                P.dma("sp", O["bv"][r * 128:(r + 1) * 128, :], vbstg[so][:, :], reads=[R_vbstg[so]], defer=True)
                yield 3.0
                if not full:
                    return
                slot = r % 6
                ba = wrot.next()
                fm_proj(ba, xT, R_x, 128, C_KA, 4)
                P.op("act", _call("activation", out=kaT[:, slot * 512:(slot + 1) * 512], in_=pb[ba][:, :], func=AF.Copy),
                     reads=[R_pb[ba]], writes=[R_ka[slot]])
                if r >= 28:
                    P.op("dve", _call("tensor_copy", out=astg[:, 0:512], in_=pb[ba][:, :]), reads=[R_pb[ba]], writes=[R_astg])
                    P.dma("sp", O["akT"].rearrange("p (j t) -> p j t", t=512)[:, :, (r - 28) * 128:(r - 27) * 128],
                          astg[:, 0:512].rearrange("p (j t) -> p j t", t=128), reads=[R_astg], defer=True)
                yield 3.0
                bva = wrot.next()
                tm_proj(bva, xT, R_x, 128, C_VA, 512)
                vav = va_aug[:, slot * 520:(slot + 1) * 520].rearrange("p (h d) -> p h d", d=65)
                P.op("act", _call("activation", out=vav[:, :, 0:64], in_=pb[bva][:, :].rearrange("p (h d) -> p h d", d=64), func=AF.Copy),
                     reads=[R_pb[bva]], writes=[R_va[slot]])
                P.op("pool", _call("tensor_scalar", out=vav[:, :, 64:65], in0=ones8[:, :].rearrange("p (h o) -> p h o", o=1),
                                   scalar1=kvalid[:, r:r + 1], scalar2=None, op0=ALU.mult),
                     reads=[R_cst], writes=[R_va[slot]])
                if r >= 28:
                    P.op("dve", _call("tensor_copy", out=astg[:, 512:1024], in_=pb[bva][:, :]), reads=[R_pb[bva]], writes=[R_astg])
                    P.dma("sp", O["av"][(r - 28) * 128:(r - 27) * 128, :], astg[:, 512:1024], reads=[R_astg], defer=True)
                yield 3.0

            def qside(xT, R_x, qs, st, st3):
                b1 = wrot.next()
                fm_proj(b1, xT, R_x, qs, C_QA, 4)
                for hf in range(2):
                    P.op("act", _call("activation",
                                      out=qaz[st][hf * 64:(hf + 1) * 64, 0:8 * qs].rearrange("p (j two q) -> p j two q", two=2, q=qs)[:, :, hf, :],
                                      in_=pb[b1][hf * 64:(hf + 1) * 64, 0:4 * qs].rearrange("p (j q) -> p j q", q=qs), func=AF.Copy, scale=0.125),
                         reads=[R_pb[b1]], writes=[R_qa[st]])
                yield 3.0
                b2 = wrot.next()
                fm_proj(b2, xT, R_x, qs, C_QB, 4)
                for g in range(2):
                    P.op("act", _call("activation", out=qbz[st3][g * 64:(g + 1) * 64, g * 4 * qs:(g + 1) * 4 * qs],
                                      in_=pb[b2][g * 64:(g + 1) * 64, 0:4 * qs], func=AF.Copy, scale=0.125),
                         reads=[R_pb[b2]], writes=[R_qb[st3]])
                yield 3.0
                b3 = wrot.next()
                fm_proj(b3, xT, R_x, qs, C_QI, 4)
                for hf in range(2):
                    P.op("act", _call("activation",
                                      out=qiz[st][hf * 64:(hf + 1) * 64, 0:8 * qs].rearrange("p (j two q) -> p j two q", two=2, q=qs)[:, :, hf, :],
                                      in_=pb[b3][hf * 64:(hf + 1) * 64, 0:4 * qs].rearrange("p (j q) -> p j q", q=qs), func=AF.Copy),
                         reads=[R_pb[b3]], writes=[R_qi[st]])
                b4 = wrot.next()
                tm_proj(b4, xT, R_x, qs, C_WI, 8)
                P.op("dve", _call("tensor_scalar", out=coef[st][0:qs, :], in0=pb[b4][0:qs, 0:8], scalar1=float(8.0 ** -1.5), scalar2=None, op0=ALU.mult),
                     reads=[R_pb[b4]], writes=[R_coef[st]])
                for h in range(8):
                    P.op("pool", _call("tensor_scalar", out=dg[st][0:qs, h * 128: h * 128 + qs], in0=ident_f[0:qs, 0:qs],
                                       scalar1=coef[st][0:qs, h:h + 1], scalar2=None, op0=ALU.mult),
                         reads=[R_coef[st], R_ident], writes=[R_dg[st]])
                yield 3.0

            def normalize(bank, qs, mixt, R_m, col0, rec, R_rec):
                ov = pb[bank][0:qs, 0:260].rearrange("p (h d) -> p h d", d=65)
                P.op("dve", _call("tensor_scalar", out=rec[0:qs, 0:4].rearrange("p (h o) -> p h o", o=1), in0=ov[:, :, 64:65],
                                  scalar1=1e-30, scalar2=None, op0=ALU.max),
                     reads=[R_pb[bank]], writes=[R_rec])
                P.op("dve", _call("reciprocal", out=rec[0:qs, 0:4], in_=rec[0:qs, 0:4]), reads=[R_rec], writes=[R_rec])
                for hh in range(4):
                    P.op("dve", _call("tensor_scalar", out=mixt[0:qs, col0 + hh * 64: col0 + (hh + 1) * 64],
                                      in0=pb[bank][0:qs, hh * 65: hh * 65 + 64],
                                      scalar1=rec[0:qs, hh:hh + 1], scalar2=None, op0=ALU.mult),
                         reads=[R_pb[bank], R_rec], writes=[R_m])

            def pipe3(items, s1, s2, s3, D, cost=1.0):
                pend = []
                for it in items:
                    s1(it)
                    s2(it)
                    pend.append(it)
                    if len(pend) > D:
                        s3(pend.pop(0))
                    yield cost
                while pend:
                    s3(pend.pop(0))
                    yield cost

            pta_rot = Rot([0, 1, 2])
            relu_rot = Rot([0, 1, 2])
            ptb_rot = Rot([0, 1, 2])
            brot = Rot([3, 7])

            def front_attn(sn, qs, wins, btiles, prompt_masks, abw):
                st = sn % 2
                mixt, R_m = mixb[sn % 3], R_mix[sn % 3]
                nw = len(wins)

                units = []
                for h in range(8):
                    units.append({"h": h, "t0": 0, "tiles": wins[0:4]})
                    if nw > 4:
                        units.append({"h": h, "t0": 4, "tiles": wins[4:5]})

                def a1(u):
                    h = u["h"]
                    j = h // 2
                    bank = wrot.next()
                    u["bank"] = bank
                    for i, (slot, ts) in enumerate(u["tiles"]):
                        t = u["t0"] + i
                        c0 = i * qs
                        P.op("pe", _call("matmul", out=pb[bank][0:ts, c0:c0 + qs], lhsT=kaT[:, slot * 512 + j * 128: slot * 512 + j * 128 + ts],
                                         rhs=qaz[st][:, h * qs:(h + 1) * qs], start=True, stop=False),
                             reads=[R_ka[slot], R_qa[st]], writes=[R_pb[bank]])
                        P.op("pe", _call("matmul", out=pb[bank][0:ts, c0:c0 + qs], lhsT=ABb[0:qs, h * abw + t * 128: h * abw + t * 128 + ts],
                                         rhs=ident[0:qs, 0:qs], start=False, stop=True),
                             reads=[R_AB, R_ident], writes=[R_pb[bank]])

                def a2(u):
                    k = pta_rot.next()
                    u["pt"], u["R_pt"] = PTA[k], R_PTA[k]
                    bank = u["bank"]
                    tsm = max(ts for (_, ts) in u["tiles"])
                    n = len(u["tiles"])
                    P.op("act", _call("activation", out=u["pt"][0:tsm, 0:n * qs], in_=pb[bank][0:tsm, 0:n * qs], func=AF.Exp),
                         reads=[R_pb[bank]], writes=[u["R_pt"]])

                def a3(u):
                    h = u["h"]
                    last_unit = (u["t0"] + len(u["tiles"]) == nw)
                    for i, (slot, ts) in enumerate(u["tiles"]):
                        t = u["t0"] + i
                        P.op("pe", _call("matmul", out=pb[4][0:qs, (h % 4) * 65:(h % 4) * 65 + 65], lhsT=u["pt"][0:ts, i * qs:(i + 1) * qs],
                                         rhs=va_aug[0:ts, slot * 520 + h * 65: slot * 520 + h * 65 + 65],
                                         start=(h % 4 == 0 and t == 0), stop=(t == nw - 1), skip_group_check=True),
                             reads=[u["R_pt"], R_va[slot]], writes=[R_pb[4]])
                    if last_unit and h % 4 == 3:
                        normalize(4, qs, mixt, R_m, (h // 4) * 256, recA[st], R_recA[st])

                yield from pipe3(units, a1, a2, a3, 2, 0.9)

                L = btiles[-1][1] + btiles[-1][2]
                items = []
                cc = 0
                for c0 in range(0, L, 512):
                    w = min(512, L - c0)
                    rk = [R_ki[tt[0]] for tt in btiles if tt[1] >= c0 - 127 and tt[1] < c0 + w]
                    for h in range(8):
                        items.append({"c0": c0, "w": w, "h": h, "sc": (5, 4)[cc % 2], "rk": rk})
                    cc += 1

                def i1(it):
                    bank = wrot.next()
                    it["bank"] = bank
                    h, c0, w = it["h"], it["c0"], it["w"]
                    P.op("pe", _call("matmul", out=pb[bank][0:qs, 0:w], lhsT=qiz[st][:, h * qs:(h + 1) * qs],
                                     rhs=kbi[:, 4096 + c0: 4096 + c0 + w], start=True, stop=True),
                         reads=[R_qi[st]] + it["rk"], writes=[R_pb[bank]])

                def i2(it):
                    k = relu_rot.next()
                    it["rl"], it["R_rl"] = relu[k], R_relu[k]
                    w = it["w"]
                    P.op("act", _call("activation", out=it["rl"][0:qs, 0:w], in_=pb[it["bank"]][0:qs, 0:w], func=AF.Relu),
                         reads=[R_pb[it["bank"]]], writes=[it["R_rl"]])

                def i3(it):
                    h, c0, w, sc = it["h"], it["c0"], it["w"], it["sc"]
                    P.op("pe", _call("matmul", out=pb[sc][0:qs, 0:w], lhsT=dg[st][0:qs, h * 128: h * 128 + qs], rhs=it["rl"][0:qs, 0:w],
                                     start=(h == 0), stop=(h == 7)),
                         reads=[R_dg[st], it["R_rl"]], writes=[R_pb[sc]])
                    if h == 7:
                        if prompt_masks and c0 < 2048:
                            wm = min(w, 2048 - c0)
                            P.op("act", _call("activation", out=score[st][0:qs, c0:c0 + wm], in_=pb[sc][0:qs, 0:wm], func=AF.Identity,
                                              bias=colmask[0:qs, 0:1]),
                                 reads=[R_pb[sc], R_cst], writes=[R_score[st]])
                            if wm < w:
                                P.op("act", _call("activation", out=score[st][0:qs, c0 + wm:c0 + w], in_=pb[sc][0:qs, wm:w], func=AF.Copy),
                                     reads=[R_pb[sc]], writes=[R_score[st]])
                        else:
                            P.op("act", _call("activation", out=score[st][0:qs, c0:c0 + w], in_=pb[sc][0:qs, 0:w], func=AF.Copy),
                                 reads=[R_pb[sc]], writes=[R_score[st]])

                yield from pipe3(items, i1, i2, i3, 2, 0.65)
                if prompt_masks:
                    P.op("dve", _call("tensor_tensor", out=score[st][0:qs, L - 128:L], in0=score[st][0:qs, L - 128:L], in1=diagm[0:qs, :], op=ALU.add),
                         reads=[R_score[st], R_cst], writes=[R_score[st]])
                yield

            def bis_gen(sn, qs, btiles, bnw):
                st = sn % 2
                sm, R_sm = small[st], R_small[st]
                L = btiles[-1][1] + btiles[-1][2]
                P.op("dve", _call("memset", sm[0:qs, 1:2], 0.0), writes=[R_sm])
                for k in range(NIT):
                    wk = BIS_W0 / (2.0 ** k)
                    P.op("dve", _call("tensor_scalar", out=Mb[st][0:qs, 0:L], in0=score[st][0:qs, 0:L], scalar1=sm[0:qs, 1:2], scalar2=None,
                                      op0=ALU.is_ge, op1=ALU.add, accum_out=sm[0:qs, 0:1]),
                         reads=[R_score[st], R_sm], writes=[R_M[st], R_sm])
                    P.op("dve", _call("tensor_scalar", out=sm[0:qs, 2:3], in0=sm[0:qs, 0:1], scalar1=255.5, scalar2=wk,
                                      op0=ALU.is_ge, op1=ALU.mult),
                         reads=[R_sm], writes=[R_sm])
                    P.op("dve", _call("scalar_tensor_tensor", out=sm[0:qs, 1:2], in0=sm[0:qs, 2:3], scalar=-wk / 2.0,
                                      in1=sm[0:qs, 1:2], op0=ALU.add, op1=ALU.add),
                         reads=[R_sm], writes=[R_sm])
                    yield L * 1.08e-3 + 0.5
                wl = BIS_W0 / (2.0 ** (NIT - 1)) / 2.0
                P.op("dve", _call("tensor_scalar", out=sm[0:qs, 3:4], in0=sm[0:qs, 1:2], scalar1=-wl, scalar2=None, op0=ALU.add),
                     reads=[R_sm], writes=[R_sm])
                P.op("dve", _call("tensor_scalar", out=Mb[st][0:qs, 0:L], in0=score[st][0:qs, 0:L], scalar1=sm[0:qs, 3:4], scalar2=NEGM,
                                  op0=ALU.is_lt, op1=ALU.mult),
                     reads=[R_score[st], R_sm], writes=[R_M[st]])
                nearw = btiles[-2][2] + btiles[-1][2]
                for h in range(8):
                    P.op("dve", _call("tensor_tensor", out=Mnear[st][0:qs, h * bnw: h * bnw + nearw], in0=Bnb[0:qs, h * bnw: h * bnw + nearw],
                                      in1=Mb[st][0:qs, L - nearw:L], op=ALU.add),
                         reads=[R_Bn, R_M[st]], writes=[R_Mnear[st]])
                yield
            def battn_gen(sn, qs, btiles, blk, bnw):
                st = sn % 2
                st3 = sn % 3
                mixt, R_m = mixb[st3], R_mix[st3]
                nb = len(btiles)
                items = [{"g": g, "t": t, "vt": vt, "c0": c0, "ts": ts} for g in range(2) for t, (vt, c0, ts) in enumerate(btiles)]

                def b1(it):
                    g, t, vt, c0, ts = it["g"], it["t"], it["vt"], it["c0"], it["ts"]
                    bank = brot.next()
                    it["bank"] = bank
                    P.op("pe", _call("matmul", out=pb[bank][0:ts, 0:4 * qs], lhsT=kbi[:, c0:c0 + ts],
                                     rhs=qbz[st3][:, g * 4 * qs:(g + 1) * 4 * qs], start=True, stop=False),
                         reads=[R_kbi[vt], R_qb[st3]], writes=[R_pb[bank]])
                    if t < nb - 2 and qs == 128:
                        P.op("pe", _call("matmul", out=pb[bank][0:ts, 0:512], lhsT=Mb[st][0:qs, c0:c0 + ts], rhs=ident[0:128, 0:512],
                                         start=False, stop=True),
                             reads=[R_M[st], R_ident], writes=[R_pb[bank]])
                    elif t < nb - 2:
                        for r in range(4):
                            P.op("pe", _call("matmul", out=pb[bank][0:ts, r * qs:(r + 1) * qs], lhsT=Mb[st][0:qs, c0:c0 + ts],
                                             rhs=ident[0:qs, 0:qs], start=False, stop=(r == 3)),
                                 reads=[R_M[st], R_ident], writes=[R_pb[bank]])
                    else:
                        tt = t - (nb - 2)
                        for r in range(4):
                            hh = g * 4 + r
                            P.op("pe", _call("matmul", out=pb[bank][0:ts, r * qs:(r + 1) * qs],
                                             lhsT=Mnear[st][0:qs, hh * bnw + tt * 128: hh * bnw + tt * 128 + ts], rhs=ident[0:qs, 0:qs],
                                             start=False, stop=(r == 3)),
                                 reads=[R_Mnear[st], R_ident], writes=[R_pb[bank]])

                def b2(it):
                    k = ptb_rot.next()
                    it["ptb"], it["R_ptb"] = PTB[k], R_PTB[k]
                    ts = it["ts"]
                    P.op("act", _call("activation", out=it["ptb"][0:ts, 0:4 * qs], in_=pb[it["bank"]][0:ts, 0:4 * qs], func=AF.Exp),
                         reads=[R_pb[it["bank"]]], writes=[it["R_ptb"]])

                def b3(it):
                    g, t, vt, ts = it["g"], it["t"], it["vt"], it["ts"]
                    for r in range(4):
                        P.op("pe", _call("matmul", out=pb[6][0:qs, r * 65: r * 65 + 65], lhsT=it["ptb"][0:ts, r * qs:(r + 1) * qs],
                                         rhs=vb_aug[0:ts, (vt * 2 + g) * 65:(vt * 2 + g) * 65 + 65],
                                         start=(t == 0 and r == 0), stop=(t == nb - 1), skip_group_check=True),
                             reads=[it["R_ptb"], R_vb[vt]], writes=[R_pb[6]])
                    if t == nb - 1:
                        normalize(6, qs, mixt, R_m, 512 + g * 256, recB[st], R_recB[st])

                yield from pipe3(items, b1, b2, b3, 1, 0.8)
                P.dma("sp", mixD[blk * 128: blk * 128 + qs, :], mixt[0:qs, :], reads=[R_m], writes=[R_mixD[blk]], defer=True)
                yield

            load_xT(0, "dve")
            for r in range(16):
                if r + 1 < 16:
                    load_xT(r + 1, "dve")
                for _ in kside(r, full=(r >= 11)):
                    pass
            checkpoint('phase0')

            def prompt_front(sn, T):
                if T >= 16:
                    load_xT(T)
                    yield from kside(T, full=True)
                s = T % 2
                yield from qside(xTb[s], R_xT[s], 128, sn % 2, sn % 3)
                wins = [((T - 4 + t) % 6, 128) for t in range(5)]
                btiles = [(t, t * 128, 128) for t in range(T + 1)]
                yield from front_attn(sn, 128, wins, btiles, True, ABW)

            def prompt_bis(sn, T):
                btiles = [(t, t * 128, 128) for t in range(T + 1)]
                yield from bis_gen(sn, 128, btiles, BNW)

            def prompt_battn(sn, T, blk):
                btiles = [(t, t * 128, 128) for t in range(T + 1)]
                yield from battn_gen(sn, 128, btiles, blk, BNW)

            steps = [(0, 15, 16)] + [(1 + i, 16 + i, i) for i in range(16)]
            ns = len(steps)
            SN = ns
            sst = SN % 2
            s_wins = [(0, 128), (1, 128), (2, 128), (3, 128), (4, 16)]
            s_btiles = [(t, t * 128, 128) for t in range(16)] + [(16, 2048, 16)]
            xs_, R_xs = xTb[0], R_xT[0]

            def sample_front():
                stg, R_stg = score[sst], R_score[sst]
                P.dma("sp", stg[:, 0:2048], I["cbiT"][:, :], writes=[R_stg])
                P.op("act", _call("activation", out=kbi[:, 4096:4096 + 2048], in_=stg[:, 0:2048], func=AF.Copy),
                     reads=[R_stg], writes=R_ki[0:16])
                P.dma("sp", stg[:, 2048:4096], I["cakT"][:, :], writes=[R_stg])
                for s4 in range(4):
                    P.op("act", _call("activation", out=kaT[:, s4 * 512:(s4 + 1) * 512].rearrange("p (j t) -> p j t", t=128),
                                      in_=stg[:, 2048:4096].rearrange("p (j t) -> p j t", t=512)[:, :, s4 * 128:(s4 + 1) * 128], func=AF.Copy),
                         reads=[R_stg], writes=[R_ka[s4]])
                yield 3.0
                P.dma("sp", stg[:, 0:2048].rearrange("p (t c) -> p t c", c=512), I["cav"].rearrange("(t p) c -> p t c", p=128), writes=[R_stg])
                vaall = va_aug[:, 0:4 * 520].rearrange("p (t d) -> p t d", d=65)
                P.op("act", _call("activation", out=vaall[:, :, 0:64], in_=stg[:, 0:2048].rearrange("p (t d) -> p t d", d=64), func=AF.Copy),
                     reads=[R_stg], writes=R_va[0:5])
                P.op("pool", _call("memset", va_aug[:, 0:5 * 520].rearrange("p (t d) -> p t d", d=65)[:, :, 64:65], 1.0), writes=R_va[0:5])
                for hh in range(2):
                    w = 4 * 528
                    P.dma("sp", stg[0:16, 0:w], I["ABs"][:, hh * w:(hh + 1) * w], writes=[R_stg])
                    P.op("act", _call("activation", out=ABb[0:16, hh * w:(hh + 1) * w], in_=stg[0:16, 0:w], func=AF.Copy),
                         reads=[R_stg], writes=[R_AB])
                P.op("pool", _call("memset", qaz[sst][:, :], 0.0), writes=[R_qa[sst]])
                P.op("pool", _call("memset", qbz[SN % 3][:, :], 0.0), writes=[R_qb[SN % 3]])
                P.op("pool", _call("memset", qiz[sst][:, :], 0.0), writes=[R_qi[sst]])
                P.dma("sp", xstg[:, 0:128], I["xsT"][:, :], writes=[R_xstg])
                P.op("pool", _call("tensor_copy", out=xTb[0][:, 0:128], in_=xstg[:, 0:128]), reads=[R_xstg], writes=[R_xT[0]])
                yield 3.0
                bk = wrot.next()
                fm_proj(bk, xs_, R_xs, 16, C_KI, 1)
                P.op("act", _call("activation", out=kbi[:, 4096 + 2048:4096 + 2064], in_=pb[bk][:, 0:16], func=AF.Copy), reads=[R_pb[bk]], writes=[R_ki[16]])
                P.op("dve", _call("tensor_copy", out=ostg[0][:, 16:32], in_=pb[bk][:, 0:16]), reads=[R_pb[bk]], writes=[R_ostg[0]])
                P.dma("sp", O["sbiT"][:, :], ostg[0][0:64, 16:32], reads=[R_ostg[0]], defer=True)
                ba = wrot.next()
                fm_proj(ba, xs_, R_xs, 16, C_KA, 4)
                P.op("act", _call("activation", out=kaT[:, 4 * 512:5 * 512].rearrange("p (j t) -> p j t", t=128)[:, :, 0:16],
                                  in_=pb[ba][:, 0:64].rearrange("p (j t) -> p j t", t=16), func=AF.Copy),
                     reads=[R_pb[ba]], writes=[R_ka[4]])
                P.op("dve", _call("tensor_copy", out=astg[:, 0:64], in_=pb[ba][:, 0:64]), reads=[R_pb[ba]], writes=[R_astg])
                P.dma("sp", O["sakT"][:, :], astg[:, 0:64], reads=[R_astg], defer=True)
                bva = wrot.next()
                tm_proj(bva, xs_, R_xs, 16, C_VA, 512)
                vav = va_aug[0:16, 4 * 520:5 * 520].rearrange("p (h d) -> p h d", d=65)
                P.op("act", _call("activation", out=vav[:, :, 0:64], in_=pb[bva][0:16, :].rearrange("p (h d) -> p h d", d=64), func=AF.Copy),
                     reads=[R_pb[bva]], writes=[R_va[4]])
                P.op("dve", _call("tensor_copy", out=astg[0:16, 512:1024], in_=pb[bva][0:16, :]), reads=[R_pb[bva]], writes=[R_astg])
                P.dma("sp", O["sav"][:, :], astg[0:16, 512:1024], reads=[R_astg], defer=True)
                yield 3.0
                yield from qside(xs_, R_xs, 16, sst, SN % 3)
                yield from front_attn(SN, 16, s_wins, s_btiles, False, 528)

            def sample_bis():
                stg, R_stg = score[1 - sst], R_score[1 - sst]
                P.dma("sp", stg[0:16, 0:8 * 144], I["Bns"][:, :], writes=[R_stg])
                for h in range(8):
                    P.op("dve", _call("tensor_scalar", out=Bnb[0:16, h * 144:(h + 1) * 144], in0=stg[0:16, h * 144:(h + 1) * 144],
                                      scalar1=c15[0:16, h:h + 1], scalar2=None, op0=ALU.subtract),
                         reads=[R_stg, R_cst], writes=[R_Bn])
                yield 1.0
                yield from bis_gen(SN, 16, s_btiles, 144)

            def sample_battn():
                stg, R_stg = score[1 - sst], R_score[1 - sst]
                P.dma("sp", stg[:, 0:2048], I["cbkT"][:, :], writes=[R_stg])
                P.op("act", _call("activation", out=kbi[:, 0:2048], in_=stg[:, 0:2048], func=AF.Copy),
                     reads=[R_stg], writes=R_kbi[0:16])
                P.dma("sp", stg[:, 2048:4096].rearrange("p (t c) -> p t c", c=128), I["cbv"].rearrange("(t p) c -> p t c", p=128), writes=[R_stg])
                vball = vb_aug[:, 0:16 * 130].rearrange("p (t d) -> p t d", d=65)
                P.op("act", _call("activation", out=vball[:, :, 0:64], in_=stg[:, 2048:4096].rearrange("p (t d) -> p t d", d=64), func=AF.Copy),
                     reads=[R_stg], writes=R_vb[0:17])
                P.op("pool", _call("memset", vb_aug[:, 0:17 * 130].rearrange("p (t d) -> p t d", d=65)[:, :, 64:65], 1.0), writes=R_vb[0:17])
                bk = wrot.next()
                fm_proj(bk, xs_, R_xs, 16, C_KB, 1)
                P.op("act", _call("activation", out=kbi[:, 2048:2064], in_=pb[bk][:, 0:16], func=AF.Copy), reads=[R_pb[bk]], writes=[R_kbi[16]])
                P.op("dve", _call("tensor_copy", out=ostg[1][:, 0:16], in_=pb[bk][:, 0:16]), reads=[R_pb[bk]], writes=[R_ostg[1]])
                P.dma("sp", O["sbkT"][:, :], ostg[1][:, 0:16], reads=[R_ostg[1]], defer=True)
                bv_ = wrot.next()
                tm_proj(bv_, xs_, R_xs, 16, C_VB, 128)
                vbv = vb_aug[0:16, 16 * 130:17 * 130].rearrange("p (g d) -> p g d", d=65)
                P.op("act", _call("activation", out=vbv[:, :, 0:64], in_=pb[bv_][0:16, 0:128].rearrange("p (g d) -> p g d", d=64), func=AF.Copy),
                     reads=[R_pb[bv_]], writes=[R_vb[16]])
                P.op("dve", _call("tensor_copy", out=vbstg[0][0:16, :], in_=pb[bv_][0:16, 0:128]), reads=[R_pb[bv_]], writes=[R_vbstg[0]])
                P.dma("sp", O["sbv"][:, :], vbstg[0][0:16, :], reads=[R_vbstg[0]], defer=True)
                yield 3.0
                yield from battn_gen(SN, 16, s_btiles, 17, 144)

            for tick in range(ns + 3):
                gens = []
                if 0 <= tick - 2 < ns:
                    gens.append(prompt_battn(*steps[tick - 2]))
                elif tick - 2 == ns:
                    gens.append(sample_battn())
                if 0 <= tick - 1 < ns:
                    gens.append(prompt_bis(*steps[tick - 1][0:2]))
                elif tick - 1 == ns:
                    gens.append(sample_bis())
                if tick < ns:
                    gens.append(prompt_front(*steps[tick][0:2]))
                elif tick == ns:
                    gens.append(sample_front())
                run_interleaved(gens)
            checkpoint('steps')
            checkpoint('phaseA')
            P.flush(block)

        P.barrier()
        with ExitStack() as sbk:
            wob = sb(sbk, "wob", [128, 8 * 1024], BF16)
            wmqb = sb(sbk, "wmqb", [128, 8 * 512], BF16)
            wmob = sb(sbk, "wmob", [128, 4 * 1024], BF16)
            wtmp = sb(sbk, "wtmp", [128, 8 * 512], BF16)
            R_wo, R_wmq, R_wmo, R_wtmp = Res("wo"), Res("wmq"), Res("wmo"), Res("wtmp")
            wst = [sb(sbk, "wst%d" % k, [128, 2048], F32) for k in range(2)]
            R_wst = [Res("wst%d" % k) for k in range(2)]
            lnt = sb(sbk, "lnt", [128, 4 * 1024], F32)
            R_ln = Res("ln")
            memTb = sb(sbk, "memTb", [128, 8 * 256], BF16)
            R_memT = Res("memT")
            mkT = [sb(sbk, "mkT%d" % k, [128, 4 * 256], BF16) for k in range(2)]
            mva = [sb(sbk, "mva%d" % k, [128, 2 * 4 * 129], BF16) for k in range(2)]
            R_mk = [Res("mk%d" % k) for k in range(2)]
            R_mv = [Res("mv%d" % k) for k in range(2)]
            mixl = [sb(sbk, "mixl%d" % k, [128, 1024], BF16) for k in range(4)]
            R_mixl = [Res("mixl%d" % k) for k in range(4)]
            xr = [sb(sbk, "xr%d" % k, [128, 1024], F32) for k in range(4)]
            R_xr = [Res("xr%d" % k) for k in range(4)]
            NB3 = 4
            tT_l = [sb(sbk, "tT%d" % k, [128, 1024], BF16) for k in range(NB3)]
            hA_l = [sb(sbk, "hA%d" % k, [128, 1024], F32) for k in range(NB3)]
            hB_l = [sb(sbk, "hB%d" % k, [128, 1024], F32) for k in range(NB3)]
            h16_l = [sb(sbk, "h16%d" % k, [128, 1024], BF16) for k in range(NB3)]
            qmT_l = [sb(sbk, "qmT%d" % k, [128, 512], BF16) for k in range(NB3)]
            PTm_l = [sb(sbk, "PTm%d" % k, [128, 1024], BF16) for k in range(NB3)]
            o16_l = [sb(sbk, "o16%d" % k, [128, 512], BF16) for k in range(NB3)]
            oT_l = [sb(sbk, "oT%d" % k, [128, 512], BF16) for k in range(NB3)]
            stat_l = [sb(sbk, "stat%d" % k, [128, 32], F32) for k in range(NB3)]
            RB = [{n: Res(n + str(k)) for n in ("tT", "hA", "hB", "h16", "qm", "PTm", "o16", "oT", "stat")} for k in range(NB3)]
            h2T = [sb(sbk, "h2T%d" % k, [128, 1024], BF16) for k in range(4)]
            R_h2T = [Res("h2T%d" % k) for k in range(4)]
            mstg = sb(sbk, "mstg", [128, 1024], F32)
            R_mstg = Res("mstg")
            wrot = Rot([0, 1, 2, 3, 4, 5, 6, 7])

            def load_cast(dst, R_dst, src, ncols, engs=("act", "pool")):
                k = 0
                for c0 in range(0, ncols, 2048):
                    w = min(2048, ncols - c0)
                    s = k % 2
                    P.dma("sp", wst[s][:, 0:w], src[:, c0:c0 + w], writes=[R_wst[s]])
                    eng = engs[k % len(engs)]
                    if eng == "act":
                        P.op("act", _call("activation", out=dst[:, c0:c0 + w], in_=wst[s][:, 0:w], func=AF.Copy),
                             reads=[R_wst[s]], writes=[R_dst])
                    else:
                        P.op(eng, _call("tensor_copy", out=dst[:, c0:c0 + w], in_=wst[s][:, 0:w]),
                             reads=[R_wst[s]], writes=[R_dst])
                    k += 1

            load_cast(wob, R_wo, I["wo"], 8192)
            load_cast(wmqb, R_wmq, I["wmq"], 4096)
            load_cast(wmob, R_wmo, I["wmo"], 4096)
            for k in range(4):
                P.dma("sp", lnt[:, k * 1024:(k + 1) * 1024], I["lnp"][k:k + 1, :].to_broadcast([128, 1024]), writes=[R_ln])
            load_cast(memTb, R_memT, I["memT"], 2048)
            load_cast(wtmp, R_wtmp, I["wmk"], 4096)
            for h in range(4):
                bank = wrot.next()
                for kc in range(KC):
                    P.op("pe", _call("matmul",
                        out=pb[bank][:, 0:256], lhsT=wtmp[:, kc * 512 + h * 128: kc * 512 + (h + 1) * 128],
                        rhs=memTb[:, kc * 256:(kc + 1) * 256], start=(kc == 0), stop=(kc == KC - 1)),
                        reads=[R_wtmp, R_memT], writes=[R_pb[bank]])
                P.op("act", _call("activation", out=mkT[0][:, h * 256:(h + 1) * 256], in_=pb[bank][:, 0:256], func=AF.Copy),
                     reads=[R_pb[bank]], writes=[R_mk[0]])
                P.op("dve", _call("tensor_copy", out=mstg[:, h * 256:(h + 1) * 256], in_=pb[bank][:, 0:256]),
                     reads=[R_pb[bank]], writes=[R_mstg])
            P.dma("sp", O["mkT"][:, :], mstg[:, :], reads=[R_mstg], defer=True)
            load_cast(wtmp, R_wtmp, I["wmv"], 4096)
            for mt in range(2):
                bank = wrot.next()
                for kc in range(KC):
                    P.op("pe", _call("matmul",
                        out=pb[bank][:, 0:512], lhsT=memTb[:, kc * 256 + mt * 128: kc * 256 + (mt + 1) * 128],
                        rhs=wtmp[:, kc * 512:(kc + 1) * 512], start=(kc == 0), stop=(kc == KC - 1)),
                        reads=[R_wtmp, R_memT], writes=[R_pb[bank]])
                mvv = mva[0][:, mt * 516:(mt + 1) * 516].rearrange("p (h d) -> p h d", d=129)
                P.op("act", _call("activation", out=mvv[:, :, 0:128], in_=pb[bank][:, :].rearrange("p (h d) -> p h d", d=128), func=AF.Copy),
                     reads=[R_pb[bank]], writes=[R_mv[0]])
                P.op("dve", _call("tensor_copy", out=mstg[:, mt * 512:(mt + 1) * 512], in_=pb[bank][:, :]),
                     reads=[R_pb[bank]], writes=[R_mstg])
                P.dma("sp", O["mv"][mt * 128:(mt + 1) * 128, :], mstg[:, mt * 512:(mt + 1) * 512], reads=[R_mstg], defer=True)
            for k in range(2):
                P.op("pool", _call("memset", mva[k][:, :].rearrange("p (t d) -> p t d", d=129)[:, :, 128:129], 1.0), writes=[R_mv[k]])
            load_cast(mkT[1], R_mk[1], I["cmkT"], 1024)
            P.dma("sp", wst[0][:, 0:1024].rearrange("p (t c) -> p t c", c=512), I["cmv"].rearrange("(t p) c -> p t c", p=128), writes=[R_wst[0]])
            P.op("act", _call("activation", out=mva[1][:, :].rearrange("p (t d) -> p t d", d=129)[:, :, 0:128],
                                               in_=wst[0][:, 0:1024].rearrange("p (t d) -> p t d", d=128), func=AF.Copy),
                 reads=[R_wst[0]], writes=[R_mv[1]])

            checkpoint('phaseB_pre')
            def transpose_to(src16, R_src, qs, nchunk, dst, R_dst):
                bank = wrot.next()
                pbf = pb[bank][:, :].bitcast(BF16)
                for c in range(nchunk):
                    P.op("pe", _call("transpose", out=pbf[:, c * qs:(c + 1) * qs], in_=src16[0:qs, c * 128:(c + 1) * 128],
                                                                   identity=ident[0:qs, 0:qs]),
                         reads=[R_src, R_ident], writes=[R_pb[bank]])
                P.op("act", _call("activation", out=dst[:, 0:nchunk * qs], in_=pbf[:, 0:nchunk * qs], func=AF.Copy),
                     reads=[R_pb[bank]], writes=[R_dst])

            def layer_norm(hin, R_hin, qs, gcol, hout, R_hout, stat, R_stat):
                for c in range(2):
                    P.op("dve", _call("bn_stats", out=stat[0:qs, c * 6:(c + 1) * 6], in_=hin[0:qs, c * 512:(c + 1) * 512]),
                         reads=[R_hin], writes=[R_stat])
                P.op("dve", _call("bn_aggr", out=stat[0:qs, 12:14], in_=stat[0:qs, 0:12]), reads=[R_stat], writes=[R_stat])
                P.op("dve", _call("tensor_scalar", out=stat[0:qs, 14:15], in0=stat[0:qs, 13:14], scalar1=LN_EPS, scalar2=None, op0=ALU.add),
                     reads=[R_stat], writes=[R_stat])
                P.op("act", _call("activation", out=stat[0:qs, 15:16], in_=stat[0:qs, 14:15], func=AF.Sqrt), reads=[R_stat], writes=[R_stat])
                P.op("dve", _call("reciprocal", out=stat[0:qs, 16:17], in_=stat[0:qs, 15:16]), reads=[R_stat], writes=[R_stat])
                P.op("dve", _call("scalar_tensor_tensor", out=stat[0:qs, 17:18], in0=stat[0:qs, 12:13], scalar=-1.0, in1=stat[0:qs, 16:17],
                                                             op0=ALU.mult, op1=ALU.mult),
                     reads=[R_stat], writes=[R_stat])
                P.op("act", _call("activation", out=hout[0:qs, :], in_=hin[0:qs, :], func=AF.Identity, scale=stat[0:qs, 16:17], bias=stat[0:qs, 17:18]),
                     reads=[R_hin, R_stat], writes=[R_hout])
                P.op("dve", _call("tensor_tensor", out=hout[0:qs, :], in0=hout[0:qs, :], in1=lnt[0:qs, gcol * 1024:(gcol + 1) * 1024], op=ALU.mult),
                     reads=[R_hout, R_ln], writes=[R_hout])
                P.op("dve", _call("tensor_tensor", out=hout[0:qs, :], in0=hout[0:qs, :], in1=lnt[0:qs, (gcol + 1) * 1024:(gcol + 2) * 1024], op=ALU.add),
                     reads=[R_hout, R_ln], writes=[R_hout])

            def phaseB_block(blk, qs, row0, mi, k2):
                s = k2
                tT, hA, hB, h16, qmT, PTm, o16, oT, stat = (tT_l[k2], hA_l[k2], hB_l[k2], h16_l[k2], qmT_l[k2], PTm_l[k2], o16_l[k2],
                                                             oT_l[k2], stat_l[k2])
                R_tT, R_hA, R_hB, R_h16, R_qm, R_PTm, R_o16, R_oT, R_stat = (RB[k2][n] for n in ("tT", "hA", "hB", "h16", "qm", "PTm", "o16", "oT", "stat"))
                P.dma("sp", mixl[s][0:qs, :], mixD[blk * 128 + row0: blk * 128 + row0 + qs, :], reads=[R_mixD[blk]], writes=[R_mixl[s]])
                P.dma("sp", xr[s][0:qs, :], I["xres"][blk * 128: blk * 128 + qs, :], writes=[R_xr[s]])
                transpose_to(mixl[s], R_mixl[s], qs, 8, tT, R_tT)
                yield
                b0, b1 = wrot.next(), wrot.next()
                for n, bank in enumerate((b0, b1)):
                    for kc in range(KC):
                        P.op("pe", _call("matmul",
                            out=pb[bank][0:qs, :], lhsT=tT[:, kc * qs:(kc + 1) * qs], rhs=wob[:, kc * 1024 + n * 512: kc * 1024 + (n + 1) * 512],
                            start=(kc == 0), stop=(kc == KC - 1)),
                            reads=[R_tT, R_wo], writes=[R_pb[bank]])
                    P.op("dve", _call("scalar_tensor_tensor",
                        out=hA[0:qs, n * 512:(n + 1) * 512], in0=xr[s][0:qs, n * 512:(n + 1) * 512], scalar=ALPHA, in1=pb[bank][0:qs, :],
                        op0=ALU.mult, op1=ALU.add),
                        reads=[R_xr[s], R_pb[bank]], writes=[R_hA])
                yield
                layer_norm(hA, R_hA, qs, 0, hB, R_hB, stat, R_stat)
                yield
                P.op("act", _call("activation", out=h16[0:qs, :], in_=hB[0:qs, :], func=AF.Copy), reads=[R_hB], writes=[R_h16])
                transpose_to(h16, R_h16, qs, 8, tT, R_tT)
                yield
                bq = wrot.next()
                for h in range(4):
                    for kc in range(KC):
                        P.op("pe", _call("matmul",
                            out=pb[bq][:, h * qs:(h + 1) * qs], lhsT=wmqb[:, kc * 512 + h * 128: kc * 512 + (h + 1) * 128],
                            rhs=tT[:, kc * qs:(kc + 1) * qs], start=(kc == 0), stop=(kc == KC - 1)),
                            reads=[R_wmq, R_tT], writes=[R_pb[bq]])
                P.op("act", _call("activation", out=qmT[:, 0:4 * qs], in_=pb[bq][:, 0:4 * qs], func=AF.Copy, scale=float(128.0 ** -0.5)),
                     reads=[R_pb[bq]], writes=[R_qm])
                yield
                bs0, bs1 = wrot.next(), wrot.next()
                for h in range(4):
                    for mt in range(2):
                        idx = h * 2 + mt
                        bank = bs0 if idx < 4 else bs1
                        c0 = (idx % 4) * qs
                        P.op("pe", _call("matmul",
                            out=pb[bank][:, c0:c0 + qs], lhsT=mkT[mi][:, h * 256 + mt * 128: h * 256 + (mt + 1) * 128],
                            rhs=qmT[:, h * qs:(h + 1) * qs], start=True, stop=True),
                            reads=[R_mk[mi], R_qm], writes=[R_pb[bank]])
                for k, bank in enumerate((bs0, bs1)):
                    P.op("act", _call("activation", out=PTm[:, k * 4 * qs:(k + 1) * 4 * qs], in_=pb[bank][:, 0:4 * qs], func=AF.Exp),
                         reads=[R_pb[bank]], writes=[R_PTm])
                yield
                bo0, bo1 = wrot.next(), wrot.next()
                for h in range(4):
                    bank = bo0 if h < 2 else bo1
                    for mt in range(2):
                        idx = h * 2 + mt
                        P.op("pe", _call("matmul",
                            out=pb[bank][0:qs, (h % 2) * 129:(h % 2) * 129 + 129], lhsT=PTm[:, idx * qs:(idx + 1) * qs],
                            rhs=mva[mi][:, (mt * 4 + h) * 129:(mt * 4 + h) * 129 + 129],
                            start=(h % 2 == 0 and mt == 0), stop=(mt == 1), skip_group_check=True),
                            reads=[R_PTm, R_mv[mi]], writes=[R_pb[bank]])
                for k, bank in enumerate((bo0, bo1)):
                    ov = pb[bank][0:qs, 0:258].rearrange("p (h d) -> p h d", d=129)
                    P.op("dve", _call("tensor_scalar", out=stat[0:qs, 20 + 2 * k:22 + 2 * k].rearrange("p (h o) -> p h o", o=1),
                                                                      in0=ov[:, :, 128:129], scalar1=1e-30, scalar2=None, op0=ALU.max),
                         reads=[R_pb[bank]], writes=[R_stat])
                    P.op("dve", _call("reciprocal", out=stat[0:qs, 20 + 2 * k:22 + 2 * k], in_=stat[0:qs, 20 + 2 * k:22 + 2 * k]),
                         reads=[R_stat], writes=[R_stat])
                    for hh in range(2):
                        h = k * 2 + hh
                        P.op("dve", _call("tensor_scalar",
                            out=o16[0:qs, h * 128:(h + 1) * 128], in0=pb[bank][0:qs, hh * 129: hh * 129 + 128],
                            scalar1=stat[0:qs, 20 + 2 * k + hh:21 + 2 * k + hh], scalar2=None, op0=ALU.mult),
                            reads=[R_pb[bank], R_stat], writes=[R_o16])
                yield
                transpose_to(o16, R_o16, qs, 4, oT, R_oT)
                yield
                b0, b1 = wrot.next(), wrot.next()
                for n, bank in enumerate((b0, b1)):
                    for c in range(4):
                        P.op("pe", _call("matmul",
                            out=pb[bank][0:qs, :], lhsT=oT[:, c * qs:(c + 1) * qs], rhs=wmob[:, c * 1024 + n * 512: c * 1024 + (n + 1) * 512],
                            start=(c == 0), stop=(c == 3)),
                            reads=[R_oT, R_wmo], writes=[R_pb[bank]])
                    P.op("dve", _call("scalar_tensor_tensor",
                        out=hA[0:qs, n * 512:(n + 1) * 512], in0=hB[0:qs, n * 512:(n + 1) * 512], scalar=ALPHA, in1=pb[bank][0:qs, :],
                        op0=ALU.mult, op1=ALU.add),
                        reads=[R_hB, R_pb[bank]], writes=[R_hA])
                yield
                layer_norm(hA, R_hA, qs, 2, hB, R_hB, stat, R_stat)
                yield
                P.dma("sp", h2D[blk * 128: blk * 128 + qs, :], hB[0:qs, :], reads=[R_hB], writes=[R_h2D[blk]], defer=True)
                P.op("act", _call("activation", out=h16[0:qs, :], in_=hB[0:qs, :], func=AF.Copy), reads=[R_hB], writes=[R_h16])
                transpose_to(h16, R_h16, qs, 8, h2T[s], R_h2T[s])
                P.dma("sp", h2TD[blk][:, 0:8 * qs], h2T[s][:, 0:8 * qs], reads=[R_h2T[s]], writes=[R_h2TD[blk]], defer=True)
                yield

            def run_staggered(gens, lag):
                active = []
                pending = list(gens)
                tick = 0
                while active or pending:
                    if pending and (not active or tick >= lag):
                        active.append(pending.pop(0))
                        tick = 0
                    for g in list(active):
                        try:
                            next(g)
                        except StopIteration:
                            active.remove(g)
                    tick += 1

            blocks = [(16, 2, 126, 0), (17, 16, 0, 1)] + [(i, 128, 0, 0) for i in range(16)]
            run_staggered([phaseB_block(b_, q_, r_, m_, pos % 4) for pos, (b_, q_, r_, m_) in enumerate(blocks)], 3)
            checkpoint('phaseB')
            P.flush(block)

        P.barrier()
        with ExitStack() as sc:
            wdb = sb(sc, "wdb", [128, NFC * 1024], BF16)
            R_wd = Res("wd")
            wst = [sb(sc, "wstc%d" % k, [128, 2048], F32) for k in range(2)]
            R_wst = [Res("wstc%d" % k) for k in range(2)]
            wsl = [sb(sc, "wsl%d" % k, [128, 2048], BF16) for k in range(2)]
            R_wsl = [Res("wsl%d" % k) for k in range(2)]
            R_wslB = [Res("wslB%d" % k) for k in range(2)]
            hT2 = [sb(sc, "hT%d" % k, [128, NFC * 512], BF16) for k in range(2)]
            R_hT2 = [Res("hT%d" % k) for k in range(2)]
            hTm = sb(sc, "hTm", [128, NFC * 16], BF16)
            R_hTm = Res("hTm")
            h2Tg = [sb(sc, "h2Tg%d" % k, [128, 8 * 512], BF16) for k in range(2)]
            R_h2Tg = [Res("h2Tg%d" % k) for k in range(2)]
            h2Tm = sb(sc, "h2Tm", [128, 8 * 18], BF16)
            R_h2Tm = Res("h2Tm")
            Gb = [sb(sc, "Gb%d" % k, [128, 514], F32) for k in range(3)]
            R_Gb = [Res("Gb%d" % k) for k in range(3)]
            Gs = sb(sc, "Gs", [128, 18], F32)
            R_Gs = Res("Gs")
            t0b = [sb(sc, "t0b%d" % k, [128, 512], F32) for k in range(3)]
            R_t0 = [Res("t0%d" % k) for k in range(3)]
            geb = [sb(sc, "geb%d" % k, [128, 512], F32) for k in range(3)]
            R_ge = [Res("ge%d" % k) for k in range(3)]
            t1b = [sb(sc, "t1b%d" % k, [128, 512], F32) for k in range(3)]
            R_t1b = [Res("t1b%d" % k) for k in range(3)]
            t2b = [sb(sc, "t2b%d" % k, [128, 512], F32) for k in range(3)]
            R_t2b = [Res("t2b%d" % k) for k in range(3)]
            t0s = sb(sc, "t0s", [128, 16], F32)
            ges = sb(sc, "ges", [128, 16], F32)
            R_ts = Res("ts")
            carry = sb(sc, "carry", [128, NFC * 2], F32)
            R_carry = [Res("carry%d" % c) for c in range(NFC)]
            sfc = sb(sc, "sfc", [128, NFC * 2], F32)
            R_sfc = Res("sfc")
            sconv = sb(sc, "sconv", [128, NFC * 2], F32)
            wconv = sb(sc, "wconv", [128, NFC * 3], F32)
            bconv = sb(sc, "bconv", [128, NFC], F32)
            flag = sb(sc, "flag", [128, 1], F32)
            R_cc = Res("cc")
            ln3 = sb(sc, "ln3", [128, 2 * 1024], F32)
            R_ln3 = Res("ln3")
            h2r = [sb(sc, "h2r%d" % k, [128, 1024], F32) for k in range(2)]
            R_h2r = [Res("h2r%d" % k) for k in range(2)]
            yA = sb(sc, "yA", [128, 1024], F32)
            R_yA = Res("yA")
            yB = [sb(sc, "yB%d" % k, [128, 1024], F32) for k in range(2)]
            R_yB = [Res("yB%d" % k) for k in range(2)]
            stat = sb(sc, "statc", [128, 32], F32)
            R_stat = Res("statc")

            P.dma("sp", sconv[:, :], I["sconvT"][:, :], writes=[R_cc])
            P.dma("sp", wconv[:, :], I["wconvT"][:, :], writes=[R_cc])
            P.dma("sp", bconv[:, :], I["bconvT"][:, :], writes=[R_cc])
            P.dma("sp", flag[:, :], I["flag"][:, :], writes=[R_cc])
            for k in range(2):
                P.dma("sp", ln3[:, k * 1024:(k + 1) * 1024], I["lnp"][4 + k:5 + k, :].to_broadcast([128, 1024]), writes=[R_ln3])
            def wdown_piece(j):
                kq = j % 2
                P.dma("sp", yB[kq][:, :], I["wdown"][:, j * 1024:(j + 1) * 1024], writes=[R_yB[kq]])
                P.op("act", _call("activation", out=wdb[:, j * 1024:(j + 1) * 1024], in_=yB[kq][:, :], func=AF.Copy), reads=[R_yB[kq]], writes=[R_wd])

            P.dma("sp", h2Tm[:, :].rearrange("p (c q) -> p c q", q=18)[:, :, 0:2], h2TD[16][:, 0:16].rearrange("p (c q) -> p c q", q=2),
                  reads=[R_h2TD[16]], writes=[R_h2Tm], slow=True)
            P.dma("sp", h2Tm[:, :].rearrange("p (c q) -> p c q", q=18)[:, :, 2:18], h2TD[17][:, 0:128].rearrange("p (c q) -> p c q", q=16),
                  reads=[R_h2TD[17]], writes=[R_h2Tm], slow=True)

            checkpoint('phaseC_pre')
            UB = [0, 2, 4]
            GBK = [1, 3, 5]
            MB = 7
            YB = [6, 7]
            wk = [0]

            def ln3_out(pre_banks, qs, h2src, R_h2src, dst_ap, ys, R_ys):
                for n, bank in enumerate(pre_banks):
                    P.op("dve", _call("scalar_tensor_tensor",
                        out=yA[0:qs, n * 512:(n + 1) * 512], in0=h2src[0:qs, n * 512:(n + 1) * 512], scalar=ALPHA, in1=pb[bank][0:qs, :],
                        op0=ALU.mult, op1=ALU.add),
                        reads=[R_h2src, R_pb[bank]], writes=[R_yA])
                for c in range(2):
                    P.op("dve", _call("bn_stats", out=stat[0:qs, c * 6:(c + 1) * 6], in_=yA[0:qs, c * 512:(c + 1) * 512]),
                         reads=[R_yA], writes=[R_stat])
                P.op("dve", _call("bn_aggr", out=stat[0:qs, 12:14], in_=stat[0:qs, 0:12]), reads=[R_stat], writes=[R_stat])
                P.op("dve", _call("tensor_scalar", out=stat[0:qs, 14:15], in0=stat[0:qs, 13:14], scalar1=LN_EPS, scalar2=None, op0=ALU.add),
                     reads=[R_stat], writes=[R_stat])
                P.op("act", _call("activation", out=stat[0:qs, 15:16], in_=stat[0:qs, 14:15], func=AF.Sqrt), reads=[R_stat], writes=[R_stat])
                P.op("dve", _call("reciprocal", out=stat[0:qs, 16:17], in_=stat[0:qs, 15:16]), reads=[R_stat], writes=[R_stat])
                P.op("dve", _call("scalar_tensor_tensor", out=stat[0:qs, 17:18], in0=stat[0:qs, 12:13], scalar=-1.0, in1=stat[0:qs, 16:17],
                                                             op0=ALU.mult, op1=ALU.mult),
                     reads=[R_stat], writes=[R_stat])
                P.op("act", _call("activation", out=ys[0:qs, :], in_=yA[0:qs, :], func=AF.Identity, scale=stat[0:qs, 16:17], bias=stat[0:qs, 17:18]),
                     reads=[R_yA, R_stat], writes=[R_ys])
                P.op("pool", _call("tensor_tensor", out=ys[0:qs, :], in0=ys[0:qs, :], in1=ln3[0:qs, 0:1024], op=ALU.mult),
                     reads=[R_ys, R_ln3], writes=[R_ys])
                P.op("pool", _call("tensor_tensor", out=ys[0:qs, :], in0=ys[0:qs, :], in1=ln3[0:qs, 1024:2048], op=ALU.add),
                     reads=[R_ys, R_ln3], writes=[R_ys])
                P.dma("sp", dst_ap, ys[0:qs, :], reads=[R_ys], defer=True)

            def load_h2Tg(grp):
                gs = grp % 2
                for bi in range(4):
                    blk = grp * 4 + bi
                    P.dma("sp", h2Tg[gs][:, :].rearrange("p (c q) -> p c q", q=512)[:, :, bi * 128:(bi + 1) * 128],
                          h2TD[blk][:, :].rearrange("p (c q) -> p c q", q=128), reads=[R_h2TD[blk]], writes=[R_h2Tg[gs]])

            def c_s1(grp, c):
                s = (grp * NFC + c) % 2
                P.dma("sp", wst[s][:, :], I["wup"][c], writes=[R_wst[s]])
                P.op("dve", _call("tensor_copy", out=wsl[s][:, 0:1024], in_=wst[s][:, 0:1024]), reads=[R_wst[s]], writes=[R_wsl[s]])
                P.op("dve", _call("tensor_copy", out=wsl[s][:, 1024:2048], in_=wst[s][:, 1024:2048]), reads=[R_wst[s]], writes=[R_wslB[s]])

            def c_s2(grp, c):
                s = (grp * NFC + c) % 2
                gs = grp % 2
                mo = (c % 2) * 64
                if grp == 0:
                    for part, oc in ((0, mo), (1, mo + 32)):
                        for kc in range(KC):
                            P.op("pe", _call("matmul", out=pb[MB][:, oc:oc + 18], lhsT=wsl[s][:, kc * 256 + part * 128: kc * 256 + (part + 1) * 128],
                                             rhs=h2Tm[:, kc * 18:(kc + 1) * 18], start=(kc == 0), stop=(kc == KC - 1)),
                                 reads=[R_wsl[s], R_wslB[s], R_h2Tm], writes=[R_pb[MB]])
                k3 = (grp * NFC + c) % 3
                ub, gbk = UB[k3], GBK[k3]
                for part, bank in ((0, ub), (1, gbk)):
                    for kc in range(KC):
                        P.op("pe", _call("matmul", out=pb[bank][:, :], lhsT=wsl[s][:, kc * 256 + part * 128: kc * 256 + (part + 1) * 128],
                                         rhs=h2Tg[gs][:, kc * 512:(kc + 1) * 512], start=(kc == 0), stop=(kc == KC - 1)),
                             reads=[R_wsl[s], R_wslB[s], R_h2Tg[gs]], writes=[R_pb[bank]])

            def c_s3(grp, c):
                hTg, R_hTg = hT2[grp % 2], R_hT2[grp % 2]
                mo = (c % 2) * 64
                if grp == 0:
                    P.op("dve", _call("tensor_scalar", out=carry[:, c * 2:(c + 1) * 2], in0=pb[MB][:, mo + 32:mo + 34], scalar1=flag[:, 0:1],
                                      scalar2=None, op0=ALU.mult),
                         reads=[R_pb[MB], R_cc], writes=[R_carry[c]])
                    P.op("act", _call("activation", out=Gs[:, 0:2], in_=sconv[:, c * 2:(c + 1) * 2], func=AF.Copy), reads=[R_cc], writes=[R_Gs])
                    P.op("act", _call("activation", out=Gs[:, 2:18], in_=pb[MB][:, mo + 34:mo + 50], func=AF.Copy), reads=[R_pb[MB]], writes=[R_Gs])
                    P.op("act", _call("activation", out=t0s[:, :], in_=Gs[:, 2:18], func=AF.Identity, scale=wconv[:, c * 3 + 2:c * 3 + 3],
                                      bias=bconv[:, c:c + 1]),
                         reads=[R_Gs, R_cc], writes=[R_ts])
                    P.op("dve", _call("scalar_tensor_tensor", out=t0s[:, :], in0=Gs[:, 1:17], scalar=wconv[:, c * 3 + 1:c * 3 + 2], in1=t0s[:, :],
                                      op0=ALU.mult, op1=ALU.add),
                         reads=[R_Gs, R_cc, R_ts], writes=[R_ts])
                    P.op("dve", _call("scalar_tensor_tensor", out=t0s[:, :], in0=Gs[:, 0:16], scalar=wconv[:, c * 3:c * 3 + 1], in1=t0s[:, :],
                                      op0=ALU.mult, op1=ALU.add),
                         reads=[R_Gs, R_cc, R_ts], writes=[R_ts])
                    P.op("act", _call("activation", out=ges[:, :], in_=t0s[:, :], func=AF.Gelu_apprx_tanh), reads=[R_ts], writes=[R_ts])
                    P.op("dve", _call("tensor_tensor", out=hTm[:, c * 16:(c + 1) * 16], in0=pb[MB][:, mo + 2:mo + 18], in1=ges[:, :], op=ALU.mult),
                         reads=[R_pb[MB], R_ts], writes=[R_hTm])
                    P.op("act", _call("activation", out=sfc[:, c * 2:(c + 1) * 2], in_=Gs[:, 16:18], func=AF.Copy), reads=[R_Gs], writes=[R_sfc])
                k3 = (grp * NFC + c) % 3
                ub, gbk = UB[k3], GBK[k3]
                G, R_G = Gb[k3], R_Gb[k3]
                t0, R_t = t0b[k3], R_t0[k3]
                ge, R_g = geb[k3], R_ge[k3]
                t1, R_t1 = t1b[k3], R_t1b[k3]
                t2, R_t2 = t2b[k3], R_t2b[k3]
                P.op("act", _call("activation", out=G[:, 0:2], in_=carry[:, c * 2:(c + 1) * 2], func=AF.Copy),
                     reads=[R_carry[c]], writes=[R_G])
                P.op("act", _call("activation", out=G[:, 2:514], in_=pb[gbk][:, :], func=AF.Copy), reads=[R_pb[gbk]], writes=[R_G])
                P.op("act", _call("activation", out=carry[:, c * 2:(c + 1) * 2], in_=G[:, 512:514], func=AF.Copy),
                     reads=[R_G], writes=[R_carry[c]])
                P.op("act", _call("activation", out=t0[:, :], in_=G[:, 2:514], func=AF.Identity,
                                  scale=wconv[:, c * 3 + 2:c * 3 + 3], bias=bconv[:, c:c + 1]),
                     reads=[R_G, R_cc], writes=[R_t])
                P.op("act", _call("activation", out=t1[:, :], in_=G[:, 1:513], func=AF.Identity, scale=wconv[:, c * 3 + 1:c * 3 + 2]),
                     reads=[R_G, R_cc], writes=[R_t1])
                P.op("act", _call("activation", out=t2[:, :], in_=G[:, 0:512], func=AF.Identity, scale=wconv[:, c * 3:c * 3 + 1]),
                     reads=[R_G, R_cc], writes=[R_t2])
                P.op("dve", _call("tensor_tensor", out=t0[:, :], in0=t0[:, :], in1=t1[:, :], op=ALU.add), reads=[R_t, R_t1], writes=[R_t])
                P.op("dve", _call("tensor_tensor", out=t0[:, :], in0=t0[:, :], in1=t2[:, :], op=ALU.add), reads=[R_t, R_t2], writes=[R_t])
                P.op("act", _call("activation", out=ge[:, :], in_=t0[:, :], func=AF.Gelu_apprx_tanh), reads=[R_t], writes=[R_g])
                P.op("dve", _call("tensor_tensor", out=hTg[:, c * 512:(c + 1) * 512], in0=pb[ub][:, :], in1=ge[:, :], op=ALU.mult),
                     reads=[R_pb[ub], R_g], writes=[R_hTg])

            def c_down(grp):
                hTg, R_hTg = hT2[grp % 2], R_hT2[grp % 2]
                if grp == 0:
                    for n, bank in enumerate(YB):
                        for c in range(NFC):
                            P.op("pe", _call("matmul", out=pb[bank][0:16, :], lhsT=hTm[:, c * 16:(c + 1) * 16],
                                             rhs=wdb[:, c * 1024 + n * 512: c * 1024 + (n + 1) * 512], start=(c == 0), stop=(c == NFC - 1)),
                                 reads=[R_hTm, R_wd], writes=[R_pb[bank]])
                    P.dma("sp", h2r[0][0:16, :], h2D[17 * 128: 17 * 128 + 16, :], reads=[R_h2D[17]], writes=[R_h2r[0]])
                    ln3_out(YB, 16, h2r[0], R_h2r[0], O["ys"][:, :], yB[0], R_yB[0])
                    P.dma("sp", O["sfcT"][:, :], sfc[:, :], reads=[R_sfc], defer=True)
                for bi in range(4):
                    blk = grp * 4 + bi
                    hs = blk % 2
                    P.dma("sp", h2r[hs][:, :], h2D[blk * 128:(blk + 1) * 128, :], reads=[R_h2D[blk]], writes=[R_h2r[hs]])
                    for n, bank in enumerate(YB):
                        for c in range(NFC):
                            P.op("pe", _call("matmul", out=pb[bank][:, :], lhsT=hTg[:, c * 512 + bi * 128: c * 512 + (bi + 1) * 128],
                                             rhs=wdb[:, c * 1024 + n * 512: c * 1024 + (n + 1) * 512], start=(c == 0), stop=(c == NFC - 1)),
                                 reads=[R_hTg, R_wd], writes=[R_pb[bank]])
                    ln3_out(YB, 128, h2r[hs], R_h2r[hs], O["y"][blk * 128:(blk + 1) * 128, :], yB[hs], R_yB[hs])

            seq = [(grp, c) for grp in range(4) for c in range(NFC)]
            nseq = len(seq)
            load_h2Tg(0)
            load_h2Tg(1)
            for idx in range(nseq + 2):
                if 1 <= idx <= NFC:
                    wdown_piece(idx - 1)
                if idx < nseq:
                    c_s1(*seq[idx])
                if 1 <= idx <= nseq:
                    c_s2(*seq[idx - 1])
                if idx >= 2:
                    g3, c3 = seq[idx - 2]
                    c_s3(g3, c3)
                    if c3 == NFC - 1:
                        c_down(g3)
                        if g3 + 2 < 4:
                            load_h2Tg(g3 + 2)
            P.dma("sp", O["fcT"][:, :], carry[:, :], reads=R_carry, defer=True)
            P.finish()
            P.flush(block)
    return nc


def _t5_bucket(rel):
    half, max_exact = 16, 8
    n = np.abs(rel)
    log_ratio = np.log(np.maximum(n, 1).astype(np.float32) / max_exact) / math.log(128 / max_exact)
    large = np.minimum(max_exact + (log_ratio * (half - max_exact)).astype(np.int32), half - 1)
    return np.where(rel < 0, half, 0) + np.where(n < max_exact, n, large)


def _host_inputs(inp):
    f32 = np.float32
    x_prompt = np.asarray(inp["x_prompt"], f32)
    x_sample = np.asarray(inp["x_sample"], f32)
    w_in = np.asarray(inp["w_in"], f32)[0]
    qa, ka, va = w_in[:, 0:512], w_in[:, 512:1024], w_in[:, 1024:1536]
    qb, kb, vb = w_in[:, 1536:2048], w_in[:, 2048:2176], w_in[:, 2176:2304]
    qi, ki, wi = w_in[:, 2304:2816], w_in[:, 2816:2880], w_in[:, 2880:2888]
    qbp = np.concatenate([np.concatenate([qb[:, r * 64:(r + 1) * 64], qb[:, (4 + r) * 64:(5 + r) * 64]], axis=1) for r in range(4)], axis=1)
    winp = np.concatenate([qa, ka, qbp, kb, qi, ki, ki, va, vb, wi], axis=1)
    assert winp.shape[1] == NCOL

    def kc_layout(w):
        n = w.shape[1]
        return np.ascontiguousarray(w.reshape(8, 128, n).transpose(1, 0, 2).reshape(128, 8 * n))

    shared = {}
    shared["win"] = kc_layout(winp)
    shared["wo"] = kc_layout(np.asarray(inp["w_o"], f32)[0])
    shared["wmq"] = kc_layout(np.asarray(inp["w_mq"], f32)[0])
    shared["wmk"] = kc_layout(np.asarray(inp["w_mk"], f32)[0])
    shared["wmv"] = kc_layout(np.asarray(inp["w_mv"], f32)[0])
    wmo = np.asarray(inp["w_mo"], f32)[0]
    shared["wmo"] = np.ascontiguousarray(wmo.reshape(4, 128, 1024).transpose(1, 0, 2).reshape(128, 4096))
    w_up = np.asarray(inp["w_up"], f32)[0]
    wu = w_up[:, :DFF].reshape(8, 128, NFC, 128)
    wg = w_up[:, DFF:].reshape(8, 128, NFC, 128)
    wup = np.stack([wu, wg], axis=3)
    shared["wup"] = np.ascontiguousarray(wup.transpose(2, 1, 0, 3, 4).reshape(NFC, 128, 8 * 256))
    w_down = np.asarray(inp["w_down"], f32)[0]
    shared["wdown"] = np.ascontiguousarray(w_down.reshape(NFC, 128, 1024).transpose(1, 0, 2).reshape(128, NFC * 1024))
    shared["lnp"] = np.ascontiguousarray(np.stack([np.asarray(inp[k], f32)[0] for k in ("ln1_g", "ln1_b", "ln2_g", "ln2_b", "ln3_g", "ln3_b")]))
    w_conv = np.asarray(inp["w_conv"], f32)[0]
    shared["wconvT"] = np.ascontiguousarray(w_conv.reshape(3, NFC, 128).transpose(2, 1, 0).reshape(128, NFC * 3))
    shared["bconvT"] = np.ascontiguousarray(np.asarray(inp["b_conv"], f32)[0].reshape(NFC, 128).T)
    shared["ident"] = np.eye(128, dtype=f32)
    tabA = np.asarray(inp["a_rel_bias"], f32)[0]
    qq = np.arange(128)[:, None]
    kk = np.arange(640)[None, :]
    kpos = kk - 512
    rel = qq - kpos
    cq = qq // 64
    kch = np.floor_divide(kpos, 64)
    allowed = (kch >= cq - 8) & (kch <= cq)
    bias = tabA[np.clip(rel, -64, 64) + 64]
    AB = np.where(allowed[:, :, None], bias, f32(NEGM)).astype(f32)
    shared["AB"] = np.ascontiguousarray(AB.transpose(0, 2, 1).reshape(128, 8 * ABW))
    js = np.arange(16)[:, None]
    ks = np.arange(528)[None, :]
    ABs = tabA[np.clip(512 + js - ks, -64, 64) + 64]
    shared["ABs"] = np.ascontiguousarray(ABs.transpose(0, 2, 1).reshape(16, 8 * 528)).astype(f32)
    t5 = np.asarray(inp["t5_bias"], f32)
    relB = np.arange(128)[:, None] - np.arange(256)[None, :] + 128
    Bn = t5[_t5_bucket(relB)]
    shared["Bn"] = np.ascontiguousarray(Bn.transpose(0, 2, 1).reshape(128, 8 * BNW)).astype(f32)
    relBs = 128 + np.arange(16)[:, None] - np.arange(144)[None, :]
    Bns = t5[_t5_bucket(relBs)]
    shared["Bns"] = np.ascontiguousarray(Bns.transpose(0, 2, 1).reshape(16, 8 * 144)).astype(f32)
    shared["C15"] = np.ascontiguousarray(np.broadcast_to(t5[15][None, :], (128, 8))).astype(f32)
    dm = np.zeros((128, 128), f32)
    dm[0:64, 64:128] = NEGM
    shared["diagmask"] = dm

    mem_prompt = np.asarray(inp["mem_prompt"], f32)
    maps = []
    for c in range(8):
        b, half = c // 2, c % 2
        m = dict(shared)
        xk = np.zeros((4096, 1024), f32)
        if half == 1:
            xk[:] = x_prompt[b]
        else:
            xk[2048:] = x_prompt[b, :2048]
        m["xkT"] = np.ascontiguousarray(xk.reshape(32, 128, 8, 128).transpose(0, 3, 2, 1).reshape(32, 128, 1024))
        xs = x_sample[c]
        m["xsT"] = np.ascontiguousarray(xs.reshape(16, 8, 128).transpose(2, 1, 0).reshape(128, 128))
        xres = np.zeros((NBLK * 128, 1024), f32)
        xres[0:2048] = xk[2048:]
        xres[2048:2050] = xk[2046:2048]
        xres[17 * 128:17 * 128 + 16] = xs
        m["xres"] = xres
        m["memT"] = np.ascontiguousarray(mem_prompt[b].reshape(256, 8, 128).transpose(2, 1, 0).reshape(128, 2048))
        cmk = np.asarray(inp["cache_mem_k"], f32)[0, c]
        m["cmkT"] = np.ascontiguousarray(cmk.transpose(2, 1, 0).reshape(128, 1024))
        m["cmv"] = np.ascontiguousarray(np.asarray(inp["cache_mem_v"], f32)[0, c].reshape(256, 512))
        cak = np.asarray(inp["cache_a_k"], f32)[0, c]
        m["cakT"] = np.ascontiguousarray(cak.reshape(512, 4, 2, 64).transpose(2, 3, 1, 0).reshape(128, 2048))
        m["cav"] = np.ascontiguousarray(np.asarray(inp["cache_a_v"], f32)[0, c].reshape(512, 512))
        cbk = np.asarray(inp["cache_b_k"], f32)[0, c]
        m["cbkT"] = np.ascontiguousarray(cbk.reshape(2048, 128).T)
        m["cbv"] = np.ascontiguousarray(np.asarray(inp["cache_b_v"], f32)[0, c].reshape(2048, 128))
        cbi = np.asarray(inp["cache_b_kidx"], f32)[0, c]
        m["cbiT"] = np.ascontiguousarray(np.concatenate([cbi.T, cbi.T], axis=0))
        sc_ = np.asarray(inp["state_ffn_conv"], f32)[0, c]
        m["sconvT"] = np.ascontiguousarray(sc_.reshape(2, NFC, 128).transpose(2, 1, 0).reshape(128, NFC * 2))
        m["colmask"] = np.full((128, 1), NEGM if half == 0 else 0.0, f32)
        kv = np.ones((128, NT), f32)
        if half == 0:
            kv[:, 0:16] = 0.0
        m["kvalid"] = kv
        m["flag"] = np.full((128, 1), float(half), f32)
        maps.append(m)
    return maps


_NC_CACHE = {}


def _run(inputs, debug=False):
    key = bool(debug)
    if key not in _NC_CACHE:
        _NC_CACHE[key] = build_program(debug=debug)
    nc = _NC_CACHE[key]
    maps = _host_inputs(inputs)
    res = run_bass_kernel_spmd(nc, maps, core_ids=list(range(8)))
    return res.results


def kernel(**inputs):
    R = _run(inputs)
    f32 = np.float32
    y = np.zeros((4, 4096, 1024), f32)
    ys = np.zeros((8, 16, 1024), f32)
    pak = np.zeros((1, 4, 512, 8, 64), f32)
    pav = np.zeros((1, 4, 512, 8, 64), f32)
    pbk = np.zeros((1, 4, 4096, 2, 64), f32)
    pbv = np.zeros((1, 4, 4096, 2, 64), f32)
    pbi = np.zeros((1, 4, 4096, 64), f32)
    pmk = np.zeros((1, 4, 256, 4, 128), f32)
    pmv = np.zeros((1, 4, 256, 4, 128), f32)
    pfc = np.zeros((1, 4, 2, DFF), f32)
    sak = np.zeros((1, 8, 16, 8, 64), f32)
    sav = np.zeros((1, 8, 16, 8, 64), f32)
    sbk = np.zeros((1, 8, 16, 2, 64), f32)
    sbv = np.zeros((1, 8, 16, 2, 64), f32)
    sbi = np.zeros((1, 8, 16, 64), f32)
    sfc = np.zeros((1, 8, 2, DFF), f32)
    for c in range(8):
        b, half = c // 2, c % 2
        r = R[c]
        y[b, half * 2048:(half + 1) * 2048] = np.asarray(r["y"], f32)
        ys[c] = np.asarray(r["ys"], f32)
        if half == 1:
            akT = np.asarray(r["akT"], f32).reshape(2, 64, 4, 512)
            pak[0, b] = akT.transpose(3, 2, 0, 1).reshape(512, 8, 64)
            pav[0, b] = np.asarray(r["av"], f32).reshape(512, 8, 64)
            pbk[0, b] = np.asarray(r["bkT"], f32).T.reshape(4096, 2, 64)
            pbv[0, b] = np.asarray(r["bv"], f32).reshape(4096, 2, 64)
            pbi[0, b] = np.asarray(r["biT"], f32).T
            pmk[0, b] = np.asarray(r["mkT"], f32).reshape(128, 4, 256).transpose(2, 1, 0)
            pmv[0, b] = np.asarray(r["mv"], f32).reshape(256, 4, 128)
            pfc[0, b] = np.asarray(r["fcT"], f32).reshape(128, NFC, 2).transpose(2, 1, 0).reshape(2, DFF)
        sakT = np.asarray(r["sakT"], f32).reshape(2, 64, 4, 16)
        sak[0, c] = sakT.transpose(3, 2, 0, 1).reshape(16, 8, 64)
        sav[0, c] = np.asarray(r["sav"], f32).reshape(16, 8, 64)
        sbk[0, c] = np.asarray(r["sbkT"], f32).T.reshape(16, 2, 64)
        sbv[0, c] = np.asarray(r["sbv"], f32).reshape(16, 2, 64)
        sbi[0, c] = np.asarray(r["sbiT"], f32).T
        sfc[0, c] = np.asarray(r["sfcT"], f32).reshape(128, NFC, 2).transpose(2, 1, 0).reshape(2, DFF)
    return (y, ys, pak, pav, pbk, pbv, pbi, pmk, pmv, pfc, sak, sav, sbk, sbv, sbi, sfc)
```

```python
import math
from contextlib import ExitStack

import numpy as np
import concourse.bass as bass
import concourse.mybir as mybir
from concourse.bass_utils import run_bass_kernel_spmd

F32 = mybir.dt.float32
BF16 = mybir.dt.bfloat16
AF = mybir.ActivationFunctionType
ALU = mybir.AluOpType

D = 1024
KC = 8
NT = 32
NCOL = 2952
C_QA, C_KA, C_QB, C_KB, C_QI, C_KI, C_VA, C_VB, C_WI = 0, 512, 1024, 1536, 1664, 2176, 2304, 2816, 2944
DFF = 2816
NFC = 22
ALPHA = 2.0 ** 0.25
LN_EPS = 1e-5
NEGM = -30000.0
NIT = 16
BIS_W0 = 16.0
ABW = 640
BNW = 256
NBLK = 18


class Res:
    __slots__ = ("lw", "rd", "name", "excl")

    def __init__(self, name="", excl=False):
        self.lw = None
        self.rd = {}
        self.name = name
        self.excl = excl


def _call(name, *args, **kw):
    return lambda e: getattr(e, name)(*args, **kw)


class Prog:
    ENG = ("pe", "act", "dve", "pool", "sp")

    def __init__(self, nc, sems, dma_sems):
        self.nc = nc
        self.streams = {e: [] for e in self.ENG}
        self.sem = sems
        self.cnt = {e: 0 for e in self.ENG}
        self.seen = {e: {} for e in self.ENG}
        self.dsems = dma_sems
        self.dval = [0] * len(dma_sems)
        self.dnext = 0
        self.semh = dict(sems)
        for i, h in enumerate(dma_sems):
            self.semh[("d", i)] = h
        self.ninst = 0
        self.dead = False
        self.deferred = []
        self.defer_lag = 48

    def _deps(self, reads, writes, eng=None):
        d = {}
        for r in reads:
            if r.lw is not None:
                k, v = r.lw
                if d.get(k, 0) < v:
                    d[k] = v
            if r.excl:
                for k, v in r.rd.items():
                    if k != eng and d.get(k, 0) < v:
                        d[k] = v
        for w in writes:
            if w.lw is not None:
                k, v = w.lw
                if d.get(k, 0) < v:
                    d[k] = v
            for k, v in w.rd.items():
                if d.get(k, 0) < v:
                    d[k] = v
        return d

    def _wait(self, eng, deps):
        for k, v in deps.items():
            if k == "pe" and eng == "pe":
                continue
            if self.seen[eng].get(k, 0) >= v:
                continue
            self.seen[eng][k] = v
            h = self.semh[k]
            self.streams[eng].append(lambda e, h=h, v=v: e.wait_ge(h, v))

    def _flush_deferred(self, force=False, reads=(), writes=()):
        if not self.deferred:
            return
        conflict = force
        if not conflict:
            ws = set(id(w) for w in writes)
            rs = set(id(r) for r in reads)
            for d in self.deferred:
                dr = set(id(x) for x in d[3])
                dw = set(id(x) for x in d[4])
                if (ws & dr) or (ws & dw) or (rs & dw):
                    conflict = True
                    break
        if conflict:
            pend, self.deferred = self.deferred, []
            for d in pend:
                self._dma_now(d[0], d[1], d[2], d[3], d[4], d[5])
            return
        while self.deferred and self.ninst - self.deferred[0][6] >= self.defer_lag:
            d = self.deferred.pop(0)
            self._dma_now(d[0], d[1], d[2], d[3], d[4], d[5])

    def op(self, eng, fn, reads=(), writes=()):
        if self.dead:
            return
        self._flush_deferred(False, reads, writes)
        self._wait(eng, self._deps(reads, writes, eng))
        self.cnt[eng] += 1
        n = self.cnt[eng]
        h = self.sem[eng]
        self.streams[eng].append(lambda e, fn=fn, h=h: fn(e).then_inc(h, 1))
        self.ninst += 1
        for r in reads:
            if r.rd.get(eng, 0) < n:
                r.rd[eng] = n
        for w in writes:
            w.lw = (eng, n)
            w.rd = {}

    def dma(self, q, out, in_, reads=(), writes=(), slow=False, defer=False):
        if self.dead:
            return
        if defer:
            self._flush_deferred(False, reads, writes)
            self.deferred.append((q, out, in_, list(reads), list(writes), slow, self.ninst))
            return
        self._flush_deferred(False, reads, writes)
        self._dma_now(q, out, in_, reads, writes, slow)

    def _dma_now(self, q, out, in_, reads=(), writes=(), slow=False):
        deps = self._deps(reads, writes)
        i = self.dnext
        self.dnext = (i + 1) % len(self.dsems)
        k = ("d", i)
        if self.dval[i] > 0 and deps.get(k, 0) < self.dval[i]:
            deps[k] = self.dval[i]
        self._wait(q, deps)
        self.dval[i] += 16
        v = self.dval[i]
        h = self.dsems[i]
        if slow:
            self.streams[q].append(
                lambda e, out=out, in_=in_, h=h: e.dma_start(out=out, in_=in_, allow_slow_non_contiguous=True).then_inc(h, 16))
        else:
            self.streams[q].append(lambda e, out=out, in_=in_, h=h: e.dma_start(out=out, in_=in_).then_inc(h, 16))
        self.ninst += 1
        for r in reads:
            if r.rd.get(k, 0) < v:
                r.rd[k] = v
        for w in writes:
            w.lw = (k, v)
            w.rd = {}

    def barrier(self):
        if self.dead:
            return
        self._flush_deferred(True)
        deps = {e: self.cnt[e] for e in self.ENG if self.cnt[e] > 0}
        for i, v in enumerate(self.dval):
            if v > 0:
                deps[("d", i)] = v
        for e in self.ENG:
            self._wait(e, dict(deps))

    def finish(self):
        self._flush_deferred(True)
        deps = {("d", i): v for i, v in enumerate(self.dval) if v > 0}
        self._wait("sp", deps)

    def flush(self, block):
        self._flush_deferred(True)
        s = self.streams
        self.streams = {e: [] for e in self.ENG}

        def mk(lst):
            def body(e):
                for f in lst:
                    f(e)
            return body

        block.tensor(mk(s["pe"]))
        block.scalar(mk(s["act"]))
        block.vector(mk(s["dve"]))
        block.gpsimd(mk(s["pool"]))
        block.sync(mk(s["sp"]))


def build_program(debug=False, stop_at=None):
    nc = bass.Bass("TRN2", target_bir_lowering=False)

    def din(name, shape, dt=F32):
        return nc.dram_tensor(name, list(shape), dt, kind="ExternalInput").ap()

    def dout(name, shape, dt=F32):
        return nc.dram_tensor(name, list(shape), dt, kind="ExternalOutput").ap()

    def dscr(name, shape, dt):
        return nc.dram_tensor(name, list(shape), dt, kind="Internal").ap()

    I = {}
    I["xkT"] = din("xkT", [NT, 128, 1024])
    I["xsT"] = din("xsT", [128, 8 * 16])
    I["xres"] = din("xres", [NBLK * 128, 1024])
    I["win"] = din("win", [128, KC * NCOL])
    I["wo"] = din("wo", [128, 8 * 1024])
    I["wmq"] = din("wmq", [128, 8 * 512])
    I["wmk"] = din("wmk", [128, 8 * 512])
    I["wmv"] = din("wmv", [128, 8 * 512])
    I["wmo"] = din("wmo", [128, 4 * 1024])
    I["wup"] = din("wup", [NFC, 128, 8 * 256])
    I["wdown"] = din("wdown", [128, NFC * 1024])
    I["lnp"] = din("lnp", [6, 1024])
    I["wconvT"] = din("wconvT", [128, NFC * 3])
    I["bconvT"] = din("bconvT", [128, NFC])
    I["memT"] = din("memT", [128, 8 * 256])
    I["cmkT"] = din("cmkT", [128, 4 * 256])
    I["cmv"] = din("cmv", [256, 512])
    I["cakT"] = din("cakT", [128, 4 * 512])
    I["cav"] = din("cav", [512, 512])
    I["cbkT"] = din("cbkT", [128, 2048])
    I["cbv"] = din("cbv", [2048, 128])
    I["cbiT"] = din("cbiT", [128, 2048])
    I["sconvT"] = din("sconvT", [128, NFC * 2])
    I["ident"] = din("ident", [128, 128])
    I["AB"] = din("AB", [128, 8 * ABW])
    I["ABs"] = din("ABs", [16, 8 * 528])
    I["Bn"] = din("Bn", [128, 8 * BNW])
    I["Bns"] = din("Bns", [16, 8 * 144])
    I["C15"] = din("C15", [128, 8])
    I["colmask"] = din("colmask", [128, 1])
    I["diagmask"] = din("diagmask", [128, 128])
    I["kvalid"] = din("kvalid", [128, NT])
    I["flag"] = din("flag", [128, 1])

    O = {}
    O["y"] = dout("y", [2048, 1024])
    O["ys"] = dout("ys", [16, 1024])
    O["akT"] = dout("akT", [128, 4 * 512])
    O["av"] = dout("av", [512, 512])
    O["bkT"] = dout("bkT", [128, 4096])
    O["bv"] = dout("bv", [4096, 128])
    O["biT"] = dout("biT", [64, 4096])
    O["mkT"] = dout("mkT", [128, 4 * 256])
    O["mv"] = dout("mv", [256, 512])
    O["fcT"] = dout("fcT", [128, NFC * 2])
    O["sakT"] = dout("sakT", [128, 4 * 16])
    O["sav"] = dout("sav", [16, 512])
    O["sbkT"] = dout("sbkT", [128, 16])
    O["sbv"] = dout("sbv", [16, 128])
    O["sbiT"] = dout("sbiT", [64, 16])
    O["sfcT"] = dout("sfcT", [128, NFC * 2])
    if debug:
        O["dbg_mix"] = dout("dbg_mix", [NBLK * 128, 1024], BF16)
        O["dbg_h2"] = dout("dbg_h2", [NBLK * 128, 1024])
        mixD = O["dbg_mix"]
        h2D = O["dbg_h2"]
    else:
        mixD = dscr("mixD", [NBLK * 128, 1024], BF16)
        h2D = dscr("h2D", [NBLK * 128, 1024], F32)
    h2TD = dscr("h2TD", [NBLK, 128, 1024], BF16)
    R_mixD = [Res("mixD%d" % i) for i in range(NBLK)]
    R_h2D = [Res("h2D%d" % i) for i in range(NBLK)]
    R_h2TD = [Res("h2TD%d" % i) for i in range(NBLK)]

    es = ExitStack()
    with es:
        sems = {e: es.enter_context(nc.semaphore("s_" + e)) for e in Prog.ENG}
        dsems = [es.enter_context(nc.semaphore("d%d" % i)) for i in range(32)]
        P = Prog(nc, sems, dsems)
        block = es.enter_context(nc.Block())

        def checkpoint(name):
            if stop_at is not None and name == stop_at and not P.dead:
                P.finish()
                P.flush(block)
                P.dead = True

        pb = [es.enter_context(nc.psum_tensor("pb%d" % i, [128, 512], F32)) for i in range(8)]
        R_pb = [Res("pb%d" % i, excl=True) for i in range(8)]

        class Rot:
            def __init__(self, idxs):
                self.idxs = idxs
                self.i = 0

            def next(self):
                k = self.idxs[self.i % len(self.idxs)]
                self.i += 1
                return k

        def sb(stack, name, shape, dt):
            return stack.enter_context(nc.sbuf_tensor("sb_" + name, list(shape), dt))

        ident_f = sb(es, "ident_f", [128, 128], F32)
        ident = sb(es, "ident", [128, 512], BF16)
        R_ident = Res("ident")
        P.dma("sp", ident_f[:, :], I["ident"][:, :], writes=[R_ident])
        for r in range(4):
            P.op("act", _call("activation", out=ident[:, r * 128:(r + 1) * 128], in_=ident_f[:, :], func=AF.Copy),
                 reads=[R_ident], writes=[R_ident])

        def run_interleaved(gens):
            gens = [[0.0, i, g] for i, g in enumerate(gens)]
            while gens:
                gens.sort(key=lambda x: (x[0], x[1]))
                ent = gens[0]
                try:
                    c = next(ent[2])
                    ent[0] += (c if c else 1.0)
                except StopIteration:
                    gens.remove(ent)

        with ExitStack() as sa:
            winb = sb(sa, "winb", [128, KC * NCOL], BF16)
            R_win = Res("win")
            kbi = sb(sa, "kbi", [128, 2 * 4096], BF16)
            R_kbi = [Res("kbi%d" % r) for r in range(NT)]
            R_ki = [Res("ki%d" % r) for r in range(NT)]
            vb_aug = sb(sa, "vb_aug", [128, NT * 2 * 65], BF16)
            R_vb = [Res("vb%d" % r) for r in range(NT)]
            kaT = sb(sa, "kaT", [128, 6 * 512], BF16)
            R_ka = [Res("ka%d" % s) for s in range(6)]
            va_aug = sb(sa, "va_aug", [128, 6 * 8 * 65], BF16)
            R_va = [Res("va%d" % s) for s in range(6)]
            ABb = sb(sa, "ABb", [128, 8 * ABW], BF16)
            R_AB = Res("AB")
            Bnb = sb(sa, "Bnb", [128, 8 * BNW], BF16)
            R_Bn = Res("Bn")
            Mnear = [sb(sa, "Mnear%d" % k, [128, 8 * BNW], BF16) for k in range(2)]
            R_Mnear = [Res("Mnear%d" % k) for k in range(2)]
            score = [sb(sa, "score%d" % k, [128, 4096], F32) for k in range(2)]
            R_score = [Res("score%d" % k) for k in range(2)]
            Mb = [sb(sa, "Mb%d" % k, [128, 4096], BF16) for k in range(2)]
            R_M = [Res("M%d" % k) for k in range(2)]
            relu = [sb(sa, "relu%d" % k, [128, 512], BF16) for k in range(3)]
            R_relu = [Res("relu%d" % k) for k in range(3)]
            xstg2 = [sb(sa, "xstg%d" % k, [128, 1024], F32) for k in range(2)]
            R_xstg2 = [Res("xstg%d" % k) for k in range(2)]
            xstg, R_xstg = xstg2[0], R_xstg2[0]
            xTb = [sb(sa, "xTb%d" % k, [128, 1024], BF16) for k in range(2)]
            R_xT = [Res("xT%d" % k) for k in range(2)]
            qaz = [sb(sa, "qaz%d" % k, [128, 1024], BF16) for k in range(2)]
            qbz = [sb(sa, "qbz%d" % k, [128, 1024], BF16) for k in range(3)]
            qiz = [sb(sa, "qiz%d" % k, [128, 1024], BF16) for k in range(2)]
            R_qa = [Res("qa%d" % k) for k in range(2)]
            R_qb = [Res("qb%d" % k) for k in range(3)]
            R_qi = [Res("qi%d" % k) for k in range(2)]
            coef = [sb(sa, "coef%d" % k, [128, 8], F32) for k in range(2)]
            R_coef = [Res("coef%d" % k) for k in range(2)]
            dg = [sb(sa, "dg%d" % k, [128, 1024], BF16) for k in range(2)]
            R_dg = [Res("dg%d" % k) for k in range(2)]
            PTA = [sb(sa, "PTA%d" % k, [128, 512], BF16) for k in range(3)]
            R_PTA = [Res("PTA%d" % k) for k in range(3)]
            PTB = [sb(sa, "PTB%d" % k, [128, 512], BF16) for k in range(3)]
            R_PTB = [Res("PTB%d" % k) for k in range(3)]
            mixb = [sb(sa, "mixb%d" % k, [128, 1024], BF16) for k in range(3)]
            R_mix = [Res("mix%d" % k) for k in range(3)]
            ostg = [sb(sa, "ostg%d" % k, [128, 256], F32) for k in range(2)]
            R_ostg = [Res("ostg%d" % k) for k in range(2)]
            vbstg = [sb(sa, "vbstg%d" % k, [128, 128], F32) for k in range(2)]
            R_vbstg = [Res("vbstg%d" % k) for k in range(2)]
            astg = sb(sa, "astg", [128, 1024], F32)
            R_astg = Res("astg")
            small = [sb(sa, "small%d" % k, [128, 16], F32) for k in range(2)]
            R_small = [Res("small%d" % k) for k in range(2)]
            recA = [sb(sa, "recA%d" % k, [128, 8], F32) for k in range(2)]
            R_recA = [Res("recA%d" % k) for k in range(2)]
            recB = [sb(sa, "recB%d" % k, [128, 8], F32) for k in range(2)]
            R_recB = [Res("recB%d" % k) for k in range(2)]
            colmask = sb(sa, "colmask", [128, 1], F32)
            diagm = sb(sa, "diagm", [128, 128], F32)
            kvalid = sb(sa, "kvalid", [128, NT], F32)
            c15 = sb(sa, "c15", [128, 8], F32)
            ones8 = sb(sa, "ones8", [128, 8], F32)
            R_cst = Res("cst")

            wrot = Rot([0, 1, 2])

            P.dma("sp", colmask[:, :], I["colmask"][:, :], writes=[R_cst])
            P.dma("sp", diagm[:, :], I["diagmask"][:, :], writes=[R_cst])
            P.dma("sp", kvalid[:, :], I["kvalid"][:, :], writes=[R_cst])
            P.dma("sp", c15[:, :], I["C15"][:, :], writes=[R_cst])
            P.op("pool", _call("memset", ones8[:, :], 1.0), writes=[R_cst])
            for k in range(2):
                P.op("pool", _call("memset", qaz[k][:, :], 0.0), writes=[R_qa[k]])
                P.op("pool", _call("memset", qiz[k][:, :], 0.0), writes=[R_qi[k]])
            for k in range(3):
                P.op("pool", _call("memset", qbz[k][:, :], 0.0), writes=[R_qb[k]])

            HW = NCOL // 2
            for kc in range(KC):
                for hh in range(2):
                    stg, R_stg = score[hh], R_score[hh]
                    P.dma("sp", stg[:, 0:HW], I["win"][:, kc * NCOL + hh * HW: kc * NCOL + (hh + 1) * HW], writes=[R_stg])
                    if hh == 0:
                        P.op("act", _call("activation", out=winb[:, kc * NCOL + hh * HW: kc * NCOL + (hh + 1) * HW], in_=stg[:, 0:HW], func=AF.Copy),
                             reads=[R_stg], writes=[R_win])
                    else:
                        P.op("dve", _call("tensor_copy", out=winb[:, kc * NCOL + hh * HW: kc * NCOL + (hh + 1) * HW], in_=stg[:, 0:HW]),
                             reads=[R_stg], writes=[R_win])
            for hh in range(2):
                w = 4 * ABW
                P.dma("sp", score[hh][:, 0:w], I["AB"][:, hh * w:(hh + 1) * w], writes=[R_score[hh]])
                P.op("act", _call("activation", out=ABb[:, hh * w:(hh + 1) * w], in_=score[hh][:, 0:w], func=AF.Copy),
                     reads=[R_score[hh]], writes=[R_AB])
            P.dma("sp", score[0][:, 0:8 * BNW], I["Bn"][:, :], writes=[R_score[0]])
            for h in range(8):
                P.op("dve", _call("tensor_scalar", out=Bnb[:, h * BNW:(h + 1) * BNW], in0=score[0][:, h * BNW:(h + 1) * BNW],
                                  scalar1=c15[:, h:h + 1], scalar2=None, op0=ALU.subtract),
                     reads=[R_score[0], R_cst], writes=[R_Bn])
            checkpoint('consts')

            def win_cols(kc, c0, n):
                return winb[:, kc * NCOL + c0: kc * NCOL + c0 + n]

            def fm_proj(bank, xT, R_x, N, col0, nchunks, ocol=0):
                for j in range(nchunks):
                    for kc in range(KC):
                        P.op("pe", _call("matmul", out=pb[bank][:, ocol + j * N: ocol + (j + 1) * N], lhsT=win_cols(kc, col0 + j * 128, 128),
                                         rhs=xT[:, kc * N:(kc + 1) * N], start=(kc == 0), stop=(kc == KC - 1)),
                             reads=[R_win, R_x], writes=[R_pb[bank]])

            def tm_proj(bank, xT, R_x, N, col0, ncols, ocol=0):
                for kc in range(KC):
                    P.op("pe", _call("matmul", out=pb[bank][0:N, ocol:ocol + ncols], lhsT=xT[:, kc * N:(kc + 1) * N],
                                     rhs=win_cols(kc, col0, ncols), start=(kc == 0), stop=(kc == KC - 1)),
                         reads=[R_win, R_x], writes=[R_pb[bank]])

            def load_xT(r, eng="pool"):
                s = r % 2
                P.dma("sp", xstg2[s][:, :], I["xkT"][r], writes=[R_xstg2[s]])
                P.op(eng, _call("tensor_copy", out=xTb[s][:, :], in_=xstg2[s][:, :]), reads=[R_xstg2[s]], writes=[R_xT[s]])

            def kside(r, full):
                s = r % 2
                xT, R_x = xTb[s], R_xT[s]
                so = r % 2
                bk = wrot.next()
                fm_proj(bk, xT, R_x, 128, C_KB, 1)
                fm_proj(bk, xT, R_x, 128, C_KI, 1, ocol=128)
                P.op("act", _call("activation", out=ostg[so][:, :], in_=pb[bk][:, 0:256], func=AF.Copy), reads=[R_pb[bk]], writes=[R_ostg[so]])
                P.op("pool", _call("tensor_copy", out=kbi[:, :].rearrange("p (a c) -> p a c", a=2)[:, :, r * 128:(r + 1) * 128],
                                   in_=ostg[so][:, :].rearrange("p (a c) -> p a c", a=2)),
                     reads=[R_ostg[so]], writes=[R_kbi[r], R_ki[r]])
                P.dma("sp", O["bkT"][:, r * 128:(r + 1) * 128], ostg[so][:, 0:128], reads=[R_ostg[so]], defer=True)
                P.dma("sp", O["biT"][:, r * 128:(r + 1) * 128], ostg[so][0:64, 128:256], reads=[R_ostg[so]], defer=True)
                yield 3.0
                bv_ = wrot.next()
                tm_proj(bv_, xT, R_x, 128, C_VB, 128)
                vbv = vb_aug[:, r * 130:(r + 1) * 130].rearrange("p (g d) -> p g d", d=65)
                P.op("act", _call("activation", out=vbstg[so][:, :], in_=pb[bv_][:, 0:128], func=AF.Copy), reads=[R_pb[bv_]], writes=[R_vbstg[so]])
                P.op("pool", _call("tensor_copy", out=vbv[:, :, 0:64], in_=vbstg[so][:, :].rearrange("p (g d) -> p g d", d=64)),
                     reads=[R_vbstg[so]], writes=[R_vb[r]])
                P.op("pool", _call("tensor_scalar", out=vbv[:, :, 64:65], in0=ones8[:, 0:2].rearrange("p (g o) -> p g o", o=1),
                                   scalar1=kvalid[:, r:r + 1], scalar2=None, op0=ALU.mult),
                     reads=[R_cst], writes=[R_vb[r]])
                P.dma("sp", O["bv"][r * 128:(r + 1) * 128, :], vbstg[so][:, :], reads=[R_vbstg[so]], defer=True)
                yield 3.0
                if not full:
                    return
                slot = r % 6
                ba = wrot.next()
                fm_proj(ba, xT, R_x, 128, C_KA, 4)
                P.op("act", _call("activation", out=kaT[:, slot * 512:(slot + 1) * 512], in_=pb[ba][:, :], func=AF.Copy),
                     reads=[R_pb[ba]], writes=[R_ka[slot]])
                if r >= 28:
                    P.op("dve", _call("tensor_copy", out=astg[:, 0:512], in_=pb[ba][:, :]), reads=[R_pb[ba]], writes=[R_astg])
                    P.dma("sp", O["akT"].rearrange("p (j t) -> p j t", t=512)[:, :, (r - 28) * 128:(r - 27) * 128],
                          astg[:, 0:512].rearrange("p (j t) -> p j t", t=128), reads=[R_astg], defer=True)
                yield 3.0
                bva = wrot.next()
                tm_proj(bva, xT, R_x, 128, C_VA, 512)
                vav = va_aug[:, slot * 520:(slot + 1) * 520].rearrange("p (h d) -> p h d", d=65)
                P.op("act", _call("activation", out=vav[:, :, 0:64], in_=pb[bva][:, :].rearrange("p (h d) -> p h d", d=64), func=AF.Copy),
                     reads=[R_pb[bva]], writes=[R_va[slot]])
                P.op("pool", _call("tensor_scalar", out=vav[:, :, 64:65], in0=ones8[:, :].rearrange("p (h o) -> p h o", o=1),
                                   scalar1=kvalid[:, r:r + 1], scalar2=None, op0=ALU.mult),
                     reads=[R_cst], writes=[R_va[slot]])
                if r >= 28:
                    P.op("dve", _call("tensor_copy", out=astg[:, 512:1024], in_=pb[bva][:, :]), reads=[R_pb[bva]], writes=[R_astg])
                    P.dma("sp", O["av"][(r - 28) * 128:(r - 27) * 128, :], astg[:, 512:1024], reads=[R_astg], defer=True)
                yield 3.0

            def qside(xT, R_x, qs, st, st3):
                b1 = wrot.next()
                fm_proj(b1, xT, R_x, qs, C_QA, 4)
                for hf in range(2):
                    P.op("act", _call("activation",
                                      out=qaz[st][hf * 64:(hf + 1) * 64, 0:8 * qs].rearrange("p (j two q) -> p j two q", two=2, q=qs)[:, :, hf, :],
                                      in_=pb[b1][hf * 64:(hf + 1) * 64, 0:4 * qs].rearrange("p (j q) -> p j q", q=qs), func=AF.Copy, scale=0.125),
                         reads=[R_pb[b1]], writes=[R_qa[st]])
                yield 3.0
                b2 = wrot.next()
                fm_proj(b2, xT, R_x, qs, C_QB, 4)
                for g in range(2):
                    P.op("act", _call("activation", out=qbz[st3][g * 64:(g + 1) * 64, g * 4 * qs:(g + 1) * 4 * qs],
                                      in_=pb[b2][g * 64:(g + 1) * 64, 0:4 * qs], func=AF.Copy, scale=0.125),
                         reads=[R_pb[b2]], writes=[R_qb[st3]])
                yield 3.0
                b3 = wrot.next()
                fm_proj(b3, xT, R_x, qs, C_QI, 4)
                for hf in range(2):
                    P.op("act", _call("activation",
                                      out=qiz[st][hf * 64:(hf + 1) * 64, 0:8 * qs].rearrange("p (j two q) -> p j two q", two=2, q=qs)[:, :, hf, :],
                                      in_=pb[b3][hf * 64:(hf + 1) * 64, 0:4 * qs].rearrange("p (j q) -> p j q", q=qs), func=AF.Copy),
                         reads=[R_pb[b3]], writes=[R_qi[st]])
                b4 = wrot.next()
                tm_proj(b4, xT, R_x, qs, C_WI, 8)
                P.op("dve", _call("tensor_scalar", out=coef[st][0:qs, :], in0=pb[b4][0:qs, 0:8], scalar1=float(8.0 ** -1.5), scalar2=None, op0=ALU.mult),
                     reads=[R_pb[b4]], writes=[R_coef[st]])
                for h in range(8):
                    P.op("pool", _call("tensor_scalar", out=dg[st][0:qs, h * 128: h * 128 + qs], in0=ident_f[0:qs, 0:qs],
                                       scalar1=coef[st][0:qs, h:h + 1], scalar2=None, op0=ALU.mult),
                         reads=[R_coef[st], R_ident], writes=[R_dg[st]])
                yield 3.0

            def normalize(bank, qs, mixt, R_m, col0, rec, R_rec):
                ov = pb[bank][0:qs, 0:260].rearrange("p (h d) -> p h d", d=65)
                P.op("dve", _call("tensor_scalar", out=rec[0:qs, 0:4].rearrange("p (h o) -> p h o", o=1), in0=ov[:, :, 64:65],
                                  scalar1=1e-30, scalar2=None, op0=ALU.max),
                     reads=[R_pb[bank]], writes=[R_rec])
                P.op("dve", _call("reciprocal", out=rec[0:qs, 0:4], in_=rec[0:qs, 0:4]), reads=[R_rec], writes=[R_rec])
                for hh in range(4):
                    P.op("dve", _call("tensor_scalar", out=mixt[0:qs, col0 + hh * 64: col0 + (hh + 1) * 64],
                                      in0=pb[bank][0:qs, hh * 65: hh * 65 + 64],
                                      scalar1=rec[0:qs, hh:hh + 1], scalar2=None, op0=ALU.mult),
                         reads=[R_pb[bank], R_rec], writes=[R_m])

            def pipe3(items, s1, s2, s3, D, cost=1.0):
                pend = []
                for it in items:
                    s1(it)
                    s2(it)
                    pend.append(it)
                    if len(pend) > D:
                        s3(pend.pop(0))
                    yield cost
                while pend:
                    s3(pend.pop(0))
                    yield cost

            pta_rot = Rot([0, 1, 2])
            relu_rot = Rot([0, 1, 2])
            ptb_rot = Rot([0, 1, 2])
            brot = Rot([3, 7])

            def front_attn(sn, qs, wins, btiles, prompt_masks, abw):
                st = sn % 2
                mixt, R_m = mixb[sn % 3], R_mix[sn % 3]
                nw = len(wins)

                units = []
                for h in range(8):
                    units.append({"h": h, "t0": 0, "tiles": wins[0:4]})
                    if nw > 4:
                        units.append({"h": h, "t0": 4, "tiles": wins[4:5]})

                def a1(u):
                    h = u["h"]
                    j = h // 2
                    bank = wrot.next()
                    u["bank"] = bank
                    for i, (slot, ts) in enumerate(u["tiles"]):
                        t = u["t0"] + i
                        c0 = i * qs
                        P.op("pe", _call("matmul", out=pb[bank][0:ts, c0:c0 + qs], lhsT=kaT[:, slot * 512 + j * 128: slot * 512 + j * 128 + ts],
                                         rhs=qaz[st][:, h * qs:(h + 1) * qs], start=True, stop=False),
                             reads=[R_ka[slot], R_qa[st]], writes=[R_pb[bank]])
                        P.op("pe", _call("matmul", out=pb[bank][0:ts, c0:c0 + qs], lhsT=ABb[0:qs, h * abw + t * 128: h * abw + t * 128 + ts],
                                         rhs=ident[0:qs, 0:qs], start=False, stop=True),
                             reads=[R_AB, R_ident], writes=[R_pb[bank]])

                def a2(u):
                    k = pta_rot.next()
                    u["pt"], u["R_pt"] = PTA[k], R_PTA[k]
                    bank = u["bank"]
                    tsm = max(ts for (_, ts) in u["tiles"])
                    n = len(u["tiles"])
                    P.op("act", _call("activation", out=u["pt"][0:tsm, 0:n * qs], in_=pb[bank][0:tsm, 0:n * qs], func=AF.Exp),
                         reads=[R_pb[bank]], writes=[u["R_pt"]])

                def a3(u):
                    h = u["h"]
                    last_unit = (u["t0"] + len(u["tiles"]) == nw)
                    for i, (slot, ts) in enumerate(u["tiles"]):
                        t = u["t0"] + i
                        P.op("pe", _call("matmul", out=pb[4][0:qs, (h % 4) * 65:(h % 4) * 65 + 65], lhsT=u["pt"][0:ts, i * qs:(i + 1) * qs],
                                         rhs=va_aug[0:ts, slot * 520 + h * 65: slot * 520 + h * 65 + 65],
                                         start=(h % 4 == 0 and t == 0), stop=(t == nw - 1), skip_group_check=True),
                             reads=[u["R_pt"], R_va[slot]], writes=[R_pb[4]])
                    if last_unit and h % 4 == 3:
                        normalize(4, qs, mixt, R_m, (h // 4) * 256, recA[st], R_recA[st])

                yield from pipe3(units, a1, a2, a3, 2, 0.9)

                L = btiles[-1][1] + btiles[-1][2]
                items = []
                cc = 0
                for c0 in range(0, L, 512):
                    w = min(512, L - c0)
                    rk = [R_ki[tt[0]] for tt in btiles if tt[1] >= c0 - 127 and tt[1] < c0 + w]
                    for h in range(8):
                        items.append({"c0": c0, "w": w, "h": h, "sc": (5, 4)[cc % 2], "rk": rk})
                    cc += 1

                def i1(it):
                    bank = wrot.next()
                    it["bank"] = bank
                    h, c0, w = it["h"], it["c0"], it["w"]
                    P.op("pe", _call("matmul", out=pb[bank][0:qs, 0:w], lhsT=qiz[st][:, h * qs:(h + 1) * qs],
                                     rhs=kbi[:, 4096 + c0: 4096 + c0 + w], start=True, stop=True),
                         reads=[R_qi[st]] + it["rk"], writes=[R_pb[bank]])

                def i2(it):
                    k = relu_rot.next()
                    it["rl"], it["R_rl"] = relu[k], R_relu[k]
                    w = it["w"]
                    P.op("act", _call("activation", out=it["rl"][0:qs, 0:w], in_=pb[it["bank"]][0:qs, 0:w], func=AF.Relu),
                         reads=[R_pb[it["bank"]]], writes=[it["R_rl"]])

                def i3(it):
                    h, c0, w, sc = it["h"], it["c0"], it["w"], it["sc"]
                    P.op("pe", _call("matmul", out=pb[sc][0:qs, 0:w], lhsT=dg[st][0:qs, h * 128: h * 128 + qs], rhs=it["rl"][0:qs, 0:w],
                                     start=(h == 0), stop=(h == 7)),
                         reads=[R_dg[st], it["R_rl"]], writes=[R_pb[sc]])
                    if h == 7:
                        if prompt_masks and c0 < 2048:
                            wm = min(w, 2048 - c0)
                            P.op("act", _call("activation", out=score[st][0:qs, c0:c0 + wm], in_=pb[sc][0:qs, 0:wm], func=AF.Identity,
                                              bias=colmask[0:qs, 0:1]),
                                 reads=[R_pb[sc], R_cst], writes=[R_score[st]])
                            if wm < w:
                                P.op("act", _call("activation", out=score[st][0:qs, c0 + wm:c0 + w], in_=pb[sc][0:qs, wm:w], func=AF.Copy),
                                     reads=[R_pb[sc]], writes=[R_score[st]])
                        else:
                            P.op("act", _call("activation", out=score[st][0:qs, c0:c0 + w], in_=pb[sc][0:qs, 0:w], func=AF.Copy),
                                 reads=[R_pb[sc]], writes=[R_score[st]])

                yield from pipe3(items, i1, i2, i3, 2, 0.65)
                if prompt_masks:
                    P.op("dve", _call("tensor_tensor", out=score[st][0:qs, L - 128:L], in0=score[st][0:qs, L - 128:L], in1=diagm[0:qs, :], op=ALU.add),
                         reads=[R_score[st], R_cst], writes=[R_score[st]])
                yield

            def bis_gen(sn, qs, btiles, bnw):
                st = sn % 2
                sm, R_sm = small[st], R_small[st]
                L = btiles[-1][1] + btiles[-1][2]
                P.op("dve", _call("memset", sm[0:qs, 1:2], 0.0), writes=[R_sm])
                for k in range(NIT):
                    wk = BIS_W0 / (2.0 ** k)
                    P.op("dve", _call("tensor_scalar", out=Mb[st][0:qs, 0:L], in0=score[st][0:qs, 0:L], scalar1=sm[0:qs, 1:2], scalar2=None,
                                      op0=ALU.is_ge, op1=ALU.add, accum_out=sm[0:qs, 0:1]),
                         reads=[R_score[st], R_sm], writes=[R_M[st], R_sm])
                    P.op("dve", _call("tensor_scalar", out=sm[0:qs, 2:3], in0=sm[0:qs, 0:1], scalar1=255.5, scalar2=wk,
                                      op0=ALU.is_ge, op1=ALU.mult),
                         reads=[R_sm], writes=[R_sm])
                    P.op("dve", _call("scalar_tensor_tensor", out=sm[0:qs, 1:2], in0=sm[0:qs, 2:3], scalar=-wk / 2.0,
                                      in1=sm[0:qs, 1:2], op0=ALU.add, op1=ALU.add),
                         reads=[R_sm], writes=[R_sm])
                    yield L * 1.08e-3 + 0.5
                wl = BIS_W0 / (2.0 ** (NIT - 1)) / 2.0
                P.op("dve", _call("tensor_scalar", out=sm[0:qs, 3:4], in0=sm[0:qs, 1:2], scalar1=-wl, scalar2=None, op0=ALU.add),
                     reads=[R_sm], writes=[R_sm])
                P.op("dve", _call("tensor_scalar", out=Mb[st][0:qs, 0:L], in0=score[st][0:qs, 0:L], scalar1=sm[0:qs, 3:4], scalar2=NEGM,
                                  op0=ALU.is_lt, op1=ALU.mult),
                     reads=[R_score[st], R_sm], writes=[R_M[st]])
                nearw = btiles[-2][2] + btiles[-1][2]
                for h in range(8):
                    P.op("dve", _call("tensor_tensor", out=Mnear[st][0:qs, h * bnw: h * bnw + nearw], in0=Bnb[0:qs, h * bnw: h * bnw + nearw],
                                      in1=Mb[st][0:qs, L - nearw:L], op=ALU.add),
                         reads=[R_Bn, R_M[st]], writes=[R_Mnear[st]])
                yield
            def battn_gen(sn, qs, btiles, blk, bnw):
                st = sn % 2
                st3 = sn % 3
                mixt, R_m = mixb[st3], R_mix[st3]
                nb = len(btiles)
                items = [{"g": g, "t": t, "vt": vt, "c0": c0, "ts": ts} for g in range(2) for t, (vt, c0, ts) in enumerate(btiles)]

                def b1(it):
                    g, t, vt, c0, ts = it["g"], it["t"], it["vt"], it["c0"], it["ts"]
                    bank = brot.next()
                    it["bank"] = bank
                    P.op("pe", _call("matmul", out=pb[bank][0:ts, 0:4 * qs], lhsT=kbi[:, c0:c0 + ts],
                                     rhs=qbz[st3][:, g * 4 * qs:(g + 1) * 4 * qs], start=True, stop=False),
                         reads=[R_kbi[vt], R_qb[st3]], writes=[R_pb[bank]])
                    if t < nb - 2 and qs == 128:
                        P.op("pe", _call("matmul", out=pb[bank][0:ts, 0:512], lhsT=Mb[st][0:qs, c0:c0 + ts], rhs=ident[0:128, 0:512],
                                         start=False, stop=True),
                             reads=[R_M[st], R_ident], writes=[R_pb[bank]])
                    elif t < nb - 2:
                        for r in range(4):
                            P.op("pe", _call("matmul", out=pb[bank][0:ts, r * qs:(r + 1) * qs], lhsT=Mb[st][0:qs, c0:c0 + ts],
                                             rhs=ident[0:qs, 0:qs], start=False, stop=(r == 3)),
                                 reads=[R_M[st], R_ident], writes=[R_pb[bank]])
                    else:
                        tt = t - (nb - 2)
                        for r in range(4):
                            hh = g * 4 + r
                            P.op("pe", _call("matmul", out=pb[bank][0:ts, r * qs:(r + 1) * qs],
                                             lhsT=Mnear[st][0:qs, hh * bnw + tt * 128: hh * bnw + tt * 128 + ts], rhs=ident[0:qs, 0:qs],
                                             start=False, stop=(r == 3)),
                                 reads=[R_Mnear[st], R_ident], writes=[R_pb[bank]])

                def b2(it):
                    k = ptb_rot.next()
                    it["ptb"], it["R_ptb"] = PTB[k], R_PTB[k]
                    ts = it["ts"]
                    P.op("act", _call("activation", out=it["ptb"][0:ts, 0:4 * qs], in_=pb[it["bank"]][0:ts, 0:4 * qs], func=AF.Exp),
                         reads=[R_pb[it["bank"]]], writes=[it["R_ptb"]])

                def b3(it):
                    g, t, vt, ts = it["g"], it["t"], it["vt"], it["ts"]
                    for r in range(4):
                        P.op("pe", _call("matmul", out=pb[6][0:qs, r * 65: r * 65 + 65], lhsT=it["ptb"][0:ts, r * qs:(r + 1) * qs],
                                         rhs=vb_aug[0:ts, (vt * 2 + g) * 65:(vt * 2 + g) * 65 + 65],
                                         start=(t == 0 and r == 0), stop=(t == nb - 1), skip_group_check=True),
                             reads=[it["R_ptb"], R_vb[vt]], writes=[R_pb[6]])
                    if t == nb - 1:
                        normalize(6, qs, mixt, R_m, 512 + g * 256, recB[st], R_recB[st])

                yield from pipe3(items, b1, b2, b3, 1, 0.8)
                P.dma("sp", mixD[blk * 128: blk * 128 + qs, :], mixt[0:qs, :], reads=[R_m], writes=[R_mixD[blk]], defer=True)
                yield

            load_xT(0, "dve")
            for r in range(16):
                if r + 1 < 16:
                    load_xT(r + 1, "dve")
                for _ in kside(r, full=(r >= 11)):
                    pass
            checkpoint('phase0')

            def prompt_front(sn, T):
                if T >= 16:
                    load_xT(T)
                    yield from kside(T, full=True)
                s = T % 2
                yield from qside(xTb[s], R_xT[s], 128, sn % 2, sn % 3)
                wins = [((T - 4 + t) % 6, 128) for t in range(5)]
                btiles = [(t, t * 128, 128) for t in range(T + 1)]
                yield from front_attn(sn, 128, wins, btiles, True, ABW)

            def prompt_bis(sn, T):
                btiles = [(t, t * 128, 128) for t in range(T + 1)]
                yield from bis_gen(sn, 128, btiles, BNW)

            def prompt_battn(sn, T, blk):
                btiles = [(t, t * 128, 128) for t in range(T + 1)]
                yield from battn_gen(sn, 128, btiles, blk, BNW)

            steps = [(0, 15, 16)] + [(1 + i, 16 + i, i) for i in range(16)]
            ns = len(steps)
            SN = ns
            sst = SN % 2
            s_wins = [(0, 128), (1, 128), (2, 128), (3, 128), (4, 16)]
            s_btiles = [(t, t * 128, 128) for t in range(16)] + [(16, 2048, 16)]
            xs_, R_xs = xTb[0], R_xT[0]

            def sample_front():
                stg, R_stg = score[sst], R_score[sst]
                P.dma("sp", stg[:, 0:2048], I["cbiT"][:, :], writes=[R_stg])
                P.op("act", _call("activation", out=kbi[:, 4096:4096 + 2048], in_=stg[:, 0:2048], func=AF.Copy),
                     reads=[R_stg], writes=R_ki[0:16])
                P.dma("sp", stg[:, 2048:4096], I["cakT"][:, :], writes=[R_stg])
                for s4 in range(4):
                    P.op("act", _call("activation", out=kaT[:, s4 * 512:(s4 + 1) * 512].rearrange("p (j t) -> p j t", t=128),
                                      in_=stg[:, 2048:4096].rearrange("p (j t) -> p j t", t=512)[:, :, s4 * 128:(s4 + 1) * 128], func=AF.Copy),
                         reads=[R_stg], writes=[R_ka[s4]])
                yield 3.0
                P.dma("sp", stg[:, 0:2048].rearrange("p (t c) -> p t c", c=512), I["cav"].rearrange("(t p) c -> p t c", p=128), writes=[R_stg])
                vaall = va_aug[:, 0:4 * 520].rearrange("p (t d) -> p t d", d=65)
                P.op("act", _call("activation", out=vaall[:, :, 0:64], in_=stg[:, 0:2048].rearrange("p (t d) -> p t d", d=64), func=AF.Copy),
                     reads=[R_stg], writes=R_va[0:5])
                P.op("pool", _call("memset", va_aug[:, 0:5 * 520].rearrange("p (t d) -> p t d", d=65)[:, :, 64:65], 1.0), writes=R_va[0:5])
                for hh in range(2):
                    w = 4 * 528
                    P.dma("sp", stg[0:16, 0:w], I["ABs"][:, hh * w:(hh + 1) * w], writes=[R_stg])
                    P.op("act", _call("activation", out=ABb[0:16, hh * w:(hh + 1) * w], in_=stg[0:16, 0:w], func=AF.Copy),
                         reads=[R_stg], writes=[R_AB])
                P.op("pool", _call("memset", qaz[sst][:, :], 0.0), writes=[R_qa[sst]])
                P.op("pool", _call("memset", qbz[SN % 3][:, :], 0.0), writes=[R_qb[SN % 3]])
                P.op("pool", _call("memset", qiz[sst][:, :], 0.0), writes=[R_qi[sst]])
                P.dma("sp", xstg[:, 0:128], I["xsT"][:, :], writes=[R_xstg])
                P.op("pool", _call("tensor_copy", out=xTb[0][:, 0:128], in_=xstg[:, 0:128]), reads=[R_xstg], writes=[R_xT[0]])
                yield 3.0
                bk = wrot.next()
                fm_proj(bk, xs_, R_xs, 16, C_KI, 1)
                P.op("act", _call("activation", out=kbi[:, 4096 + 2048:4096 + 2064], in_=pb[bk][:, 0:16], func=AF.Copy), reads=[R_pb[bk]], writes=[R_ki[16]])
                P.op("dve", _call("tensor_copy", out=ostg[0][:, 16:32], in_=pb[bk][:, 0:16]), reads=[R_pb[bk]], writes=[R_ostg[0]])
                P.dma("sp", O["sbiT"][:, :], ostg[0][0:64, 16:32], reads=[R_ostg[0]], defer=True)
                ba = wrot.next()
                fm_proj(ba, xs_, R_xs, 16, C_KA, 4)
                P.op("act", _call("activation", out=kaT[:, 4 * 512:5 * 512].rearrange("p (j t) -> p j t", t=128)[:, :, 0:16],
                                  in_=pb[ba][:, 0:64].rearrange("p (j t) -> p j t", t=16), func=AF.Copy),
                     reads=[R_pb[ba]], writes=[R_ka[4]])
                P.op("dve", _call("tensor_copy", out=astg[:, 0:64], in_=pb[ba][:, 0:64]), reads=[R_pb[ba]], writes=[R_astg])
                P.dma("sp", O["sakT"][:, :], astg[:, 0:64], reads=[R_astg], defer=True)
                bva = wrot.next()
                tm_proj(bva, xs_, R_xs, 16, C_VA, 512)
                vav = va_aug[0:16, 4 * 520:5 * 520].rearrange("p (h d) -> p h d", d=65)
                P.op("act", _call("activation", out=vav[:, :, 0:64], in_=pb[bva][0:16, :].rearrange("p (h d) -> p h d", d=64), func=AF.Copy),
                     reads=[R_pb[bva]], writes=[R_va[4]])
                P.op("dve", _call("tensor_copy", out=astg[0:16, 512:1024], in_=pb[bva][0:16, :]), reads=[R_pb[bva]], writes=[R_astg])
                P.dma("sp", O["sav"][:, :], astg[0:16, 512:1024], reads=[R_astg], defer=True)
                yield 3.0
                yield from qside(xs_, R_xs, 16, sst, SN % 3)
                yield from front_attn(SN, 16, s_wins, s_btiles, False, 528)

            def sample_bis():
                stg, R_stg = score[1 - sst], R_score[1 - sst]
                P.dma("sp", stg[0:16, 0:8 * 144], I["Bns"][:, :], writes=[R_stg])
                for h in range(8):
                    P.op("dve", _call("tensor_scalar", out=Bnb[0:16, h * 144:(h + 1) * 144], in0=stg[0:16, h * 144:(h + 1) * 144],
                                      scalar1=c15[0:16, h:h + 1], scalar2=None, op0=ALU.subtract),
                         reads=[R_stg, R_cst], writes=[R_Bn])
                yield 1.0
                yield from bis_gen(SN, 16, s_btiles, 144)

            def sample_battn():
                stg, R_stg = score[1 - sst], R_score[1 - sst]
                P.dma("sp", stg[:, 0:2048], I["cbkT"][:, :], writes=[R_stg])
                P.op("act", _call("activation", out=kbi[:, 0:2048], in_=stg[:, 0:2048], func=AF.Copy),
                     reads=[R_stg], writes=R_kbi[0:16])
                P.dma("sp", stg[:, 2048:4096].rearrange("p (t c) -> p t c", c=128), I["cbv"].rearrange("(t p) c -> p t c", p=128), writes=[R_stg])
                vball = vb_aug[:, 0:16 * 130].rearrange("p (t d) -> p t d", d=65)
                P.op("act", _call("activation", out=vball[:, :, 0:64], in_=stg[:, 2048:4096].rearrange("p (t d) -> p t d", d=64), func=AF.Copy),
                     reads=[R_stg], writes=R_vb[0:17])
                P.op("pool", _call("memset", vb_aug[:, 0:17 * 130].rearrange("p (t d) -> p t d", d=65)[:, :, 64:65], 1.0), writes=R_vb[0:17])
                bk = wrot.next()
                fm_proj(bk, xs_, R_xs, 16, C_KB, 1)
                P.op("act", _call("activation", out=kbi[:, 2048:2064], in_=pb[bk][:, 0:16], func=AF.Copy), reads=[R_pb[bk]], writes=[R_kbi[16]])
                P.op("dve", _call("tensor_copy", out=ostg[1][:, 0:16], in_=pb[bk][:, 0:16]), reads=[R_pb[bk]], writes=[R_ostg[1]])
                P.dma("sp", O["sbkT"][:, :], ostg[1][:, 0:16], reads=[R_ostg[1]], defer=True)
                bv_ = wrot.next()
                tm_proj(bv_, xs_, R_xs, 16, C_VB, 128)
                vbv = vb_aug[0:16, 16 * 130:17 * 130].rearrange("p (g d) -> p g d", d=65)
                P.op("act", _call("activation", out=vbv[:, :, 0:64], in_=pb[bv_][0:16, 0:128].rearrange("p (g d) -> p g d", d=64), func=AF.Copy),
                     reads=[R_pb[bv_]], writes=[R_vb[16]])
                P.op("dve", _call("tensor_copy", out=vbstg[0][0:16, :], in_=pb[bv_][0:16, 0:128]), reads=[R_pb[bv_]], writes=[R_vbstg[0]])
                P.dma("sp", O["sbv"][:, :], vbstg[0][0:16, :], reads=[R_vbstg[0]], defer=True)
                yield 3.0
                yield from battn_gen(SN, 16, s_btiles, 17, 144)

            for tick in range(ns + 3):
                gens = []
                if 0 <= tick - 2 < ns:
                    gens.append(prompt_battn(*steps[tick - 2]))
                elif tick - 2 == ns:
                    gens.append(sample_battn())
                if 0 <= tick - 1 < ns:
                    gens.append(prompt_bis(*steps[tick - 1][0:2]))
                elif tick - 1 == ns:
                    gens.append(sample_bis())
                if tick < ns:
                    gens.append(prompt_front(*steps[tick][0:2]))
                elif tick == ns:
                    gens.append(sample_front())
                run_interleaved(gens)
            checkpoint('steps')
            checkpoint('phaseA')
            P.flush(block)

        P.barrier()
        with ExitStack() as sbk:
            wob = sb(sbk, "wob", [128, 8 * 1024], BF16)
            wmqb = sb(sbk, "wmqb", [128, 8 * 512], BF16)
            wmob = sb(sbk, "wmob", [128, 4 * 1024], BF16)
            wtmp = sb(sbk, "wtmp", [128, 8 * 512], BF16)
            R_wo, R_wmq, R_wmo, R_wtmp = Res("wo"), Res("wmq"), Res("wmo"), Res("wtmp")
            wst = [sb(sbk, "wst%d" % k, [128, 2048], F32) for k in range(2)]
            R_wst = [Res("wst%d" % k) for k in range(2)]
            lnt = sb(sbk, "lnt", [128, 4 * 1024], F32)
            R_ln = Res("ln")
            memTb = sb(sbk, "memTb", [128, 8 * 256], BF16)
            R_memT = Res("memT")
            mkT = [sb(sbk, "mkT%d" % k, [128, 4 * 256], BF16) for k in range(2)]
            mva = [sb(sbk, "mva%d" % k, [128, 2 * 4 * 129], BF16) for k in range(2)]
            R_mk = [Res("mk%d" % k) for k in range(2)]
            R_mv = [Res("mv%d" % k) for k in range(2)]
            mixl = [sb(sbk, "mixl%d" % k, [128, 1024], BF16) for k in range(4)]
            R_mixl = [Res("mixl%d" % k) for k in range(4)]
            xr = [sb(sbk, "xr%d" % k, [128, 1024], F32) for k in range(4)]
            R_xr = [Res("xr%d" % k) for k in range(4)]
            NB3 = 4
            tT_l = [sb(sbk, "tT%d" % k, [128, 1024], BF16) for k in range(NB3)]
            hA_l = [sb(sbk, "hA%d" % k, [128, 1024], F32) for k in range(NB3)]
            hB_l = [sb(sbk, "hB%d" % k, [128, 1024], F32) for k in range(NB3)]
            h16_l = [sb(sbk, "h16%d" % k, [128, 1024], BF16) for k in range(NB3)]
            qmT_l = [sb(sbk, "qmT%d" % k, [128, 512], BF16) for k in range(NB3)]
            PTm_l = [sb(sbk, "PTm%d" % k, [128, 1024], BF16) for k in range(NB3)]
            o16_l = [sb(sbk, "o16%d" % k, [128, 512], BF16) for k in range(NB3)]
            oT_l = [sb(sbk, "oT%d" % k, [128, 512], BF16) for k in range(NB3)]
            stat_l = [sb(sbk, "stat%d" % k, [128, 32], F32) for k in range(NB3)]
            RB = [{n: Res(n + str(k)) for n in ("tT", "hA", "hB", "h16", "qm", "PTm", "o16", "oT", "stat")} for k in range(NB3)]
            h2T = [sb(sbk, "h2T%d" % k, [128, 1024], BF16) for k in range(4)]
            R_h2T = [Res("h2T%d" % k) for k in range(4)]
            mstg = sb(sbk, "mstg", [128, 1024], F32)
            R_mstg = Res("mstg")
            wrot = Rot([0, 1, 2, 3, 4, 5, 6, 7])

            def load_cast(dst, R_dst, src, ncols, engs=("act", "pool")):
                k = 0
                for c0 in range(0, ncols, 2048):
                    w = min(2048, ncols - c0)
                    s = k % 2
                    P.dma("sp", wst[s][:, 0:w], src[:, c0:c0 + w], writes=[R_wst[s]])
                    eng = engs[k % len(engs)]
                    if eng == "act":
                        P.op("act", _call("activation", out=dst[:, c0:c0 + w], in_=wst[s][:, 0:w], func=AF.Copy),
                             reads=[R_wst[s]], writes=[R_dst])
                    else:
                        P.op(eng, _call("tensor_copy", out=dst[:, c0:c0 + w], in_=wst[s][:, 0:w]),
                             reads=[R_wst[s]], writes=[R_dst])
                    k += 1

            load_cast(wob, R_wo, I["wo"], 8192)
            load_cast(wmqb, R_wmq, I["wmq"], 4096)
            load_cast(wmob, R_wmo, I["wmo"], 4096)
            for k in range(4):
                P.dma("sp", lnt[:, k * 1024:(k + 1) * 1024], I["lnp"][k:k + 1, :].to_broadcast([128, 1024]), writes=[R_ln])
            load_cast(memTb, R_memT, I["memT"], 2048)
            load_cast(wtmp, R_wtmp, I["wmk"], 4096)
            for h in range(4):
                bank = wrot.next()
                for kc in range(KC):
                    P.op("pe", _call("matmul",
                        out=pb[bank][:, 0:256], lhsT=wtmp[:, kc * 512 + h * 128: kc * 512 + (h + 1) * 128],
                        rhs=memTb[:, kc * 256:(kc + 1) * 256], start=(kc == 0), stop=(kc == KC - 1)),
                        reads=[R_wtmp, R_memT], writes=[R_pb[bank]])
                P.op("act", _call("activation", out=mkT[0][:, h * 256:(h + 1) * 256], in_=pb[bank][:, 0:256], func=AF.Copy),
                     reads=[R_pb[bank]], writes=[R_mk[0]])
                P.op("dve", _call("tensor_copy", out=mstg[:, h * 256:(h + 1) * 256], in_=pb[bank][:, 0:256]),
                     reads=[R_pb[bank]], writes=[R_mstg])
            P.dma("sp", O["mkT"][:, :], mstg[:, :], reads=[R_mstg], defer=True)
            load_cast(wtmp, R_wtmp, I["wmv"], 4096)
            for mt in range(2):
                bank = wrot.next()
                for kc in range(KC):
                    P.op("pe", _call("matmul",
                        out=pb[bank][:, 0:512], lhsT=memTb[:, kc * 256 + mt * 128: kc * 256 + (mt + 1) * 128],
                        rhs=wtmp[:, kc * 512:(kc + 1) * 512], start=(kc == 0), stop=(kc == KC - 1)),
                        reads=[R_wtmp, R_memT], writes=[R_pb[bank]])
                mvv = mva[0][:, mt * 516:(mt + 1) * 516].rearrange("p (h d) -> p h d", d=129)
                P.op("act", _call("activation", out=mvv[:, :, 0:128], in_=pb[bank][:, :].rearrange("p (h d) -> p h d", d=128), func=AF.Copy),
                     reads=[R_pb[bank]], writes=[R_mv[0]])
                P.op("dve", _call("tensor_copy", out=mstg[:, mt * 512:(mt + 1) * 512], in_=pb[bank][:, :]),
                     reads=[R_pb[bank]], writes=[R_mstg])
                P.dma("sp", O["mv"][mt * 128:(mt + 1) * 128, :], mstg[:, mt * 512:(mt + 1) * 512], reads=[R_mstg], defer=True)
            for k in range(2):
                P.op("pool", _call("memset", mva[k][:, :].rearrange("p (t d) -> p t d", d=129)[:, :, 128:129], 1.0), writes=[R_mv[k]])
            load_cast(mkT[1], R_mk[1], I["cmkT"], 1024)
            P.dma("sp", wst[0][:, 0:1024].rearrange("p (t c) -> p t c", c=512), I["cmv"].rearrange("(t p) c -> p t c", p=128), writes=[R_wst[0]])
            P.op("act", _call("activation", out=mva[1][:, :].rearrange("p (t d) -> p t d", d=129)[:, :, 0:128],
                                               in_=wst[0][:, 0:1024].rearrange("p (t d) -> p t d", d=128), func=AF.Copy),
                 reads=[R_wst[0]], writes=[R_mv[1]])

            checkpoint('phaseB_pre')
            def transpose_to(src16, R_src, qs, nchunk, dst, R_dst):
                bank = wrot.next()
                pbf = pb[bank][:, :].bitcast(BF16)
                for c in range(nchunk):
                    P.op("pe", _call("transpose", out=pbf[:, c * qs:(c + 1) * qs], in_=src16[0:qs, c * 128:(c + 1) * 128],
                                                                   identity=ident[0:qs, 0:qs]),
                         reads=[R_src, R_ident], writes=[R_pb[bank]])
                P.op("act", _call("activation", out=dst[:, 0:nchunk * qs], in_=pbf[:, 0:nchunk * qs], func=AF.Copy),
                     reads=[R_pb[bank]], writes=[R_dst])

            def layer_norm(hin, R_hin, qs, gcol, hout, R_hout, stat, R_stat):
                for c in range(2):
                    P.op("dve", _call("bn_stats", out=stat[0:qs, c * 6:(c + 1) * 6], in_=hin[0:qs, c * 512:(c + 1) * 512]),
                         reads=[R_hin], writes=[R_stat])
                P.op("dve", _call("bn_aggr", out=stat[0:qs, 12:14], in_=stat[0:qs, 0:12]), reads=[R_stat], writes=[R_stat])
                P.op("dve", _call("tensor_scalar", out=stat[0:qs, 14:15], in0=stat[0:qs, 13:14], scalar1=LN_EPS, scalar2=None, op0=ALU.add),
                     reads=[R_stat], writes=[R_stat])
                P.op("act", _call("activation", out=stat[0:qs, 15:16], in_=stat[0:qs, 14:15], func=AF.Sqrt), reads=[R_stat], writes=[R_stat])
                P.op("dve", _call("reciprocal", out=stat[0:qs, 16:17], in_=stat[0:qs, 15:16]), reads=[R_stat], writes=[R_stat])
                P.op("dve", _call("scalar_tensor_tensor", out=stat[0:qs, 17:18], in0=stat[0:qs, 12:13], scalar=-1.0, in1=stat[0:qs, 16:17],
                                                             op0=ALU.mult, op1=ALU.mult),
                     reads=[R_stat], writes=[R_stat])
                P.op("act", _call("activation", out=hout[0:qs, :], in_=hin[0:qs, :], func=AF.Identity, scale=stat[0:qs, 16:17], bias=stat[0:qs, 17:18]),
                     reads=[R_hin, R_stat], writes=[R_hout])
                P.op("dve", _call("tensor_tensor", out=hout[0:qs, :], in0=hout[0:qs, :], in1=lnt[0:qs, gcol * 1024:(gcol + 1) * 1024], op=ALU.mult),
                     reads=[R_hout, R_ln], writes=[R_hout])
                P.op("dve", _call("tensor_tensor", out=hout[0:qs, :], in0=hout[0:qs, :], in1=lnt[0:qs, (gcol + 1) * 1024:(gcol + 2) * 1024], op=ALU.add),
                     reads=[R_hout, R_ln], writes=[R_hout])

            def phaseB_block(blk, qs, row0, mi, k2):
                s = k2
                tT, hA, hB, h16, qmT, PTm, o16, oT, stat = (tT_l[k2], hA_l[k2], hB_l[k2], h16_l[k2], qmT_l[k2], PTm_l[k2], o16_l[k2],
                                                             oT_l[k2], stat_l[k2])
                R_tT, R_hA, R_hB, R_h16, R_qm, R_PTm, R_o16, R_oT, R_stat = (RB[k2][n] for n in ("tT", "hA", "hB", "h16", "qm", "PTm", "o16", "oT", "stat"))
                P.dma("sp", mixl[s][0:qs, :], mixD[blk * 128 + row0: blk * 128 + row0 + qs, :], reads=[R_mixD[blk]], writes=[R_mixl[s]])
                P.dma("sp", xr[s][0:qs, :], I["xres"][blk * 128: blk * 128 + qs, :], writes=[R_xr[s]])
                transpose_to(mixl[s], R_mixl[s], qs, 8, tT, R_tT)
                yield
                b0, b1 = wrot.next(), wrot.next()
                for n, bank in enumerate((b0, b1)):
                    for kc in range(KC):
                        P.op("pe", _call("matmul",
                            out=pb[bank][0:qs, :], lhsT=tT[:, kc * qs:(kc + 1) * qs], rhs=wob[:, kc * 1024 + n * 512: kc * 1024 + (n + 1) * 512],
                            start=(kc == 0), stop=(kc == KC - 1)),
                            reads=[R_tT, R_wo], writes=[R_pb[bank]])
                    P.op("dve", _call("scalar_tensor_tensor",
                        out=hA[0:qs, n * 512:(n + 1) * 512], in0=xr[s][0:qs, n * 512:(n + 1) * 512], scalar=ALPHA, in1=pb[bank][0:qs, :],
                        op0=ALU.mult, op1=ALU.add),
                        reads=[R_xr[s], R_pb[bank]], writes=[R_hA])
                yield
                layer_norm(hA, R_hA, qs, 0, hB, R_hB, stat, R_stat)
                yield
                P.op("act", _call("activation", out=h16[0:qs, :], in_=hB[0:qs, :], func=AF.Copy), reads=[R_hB], writes=[R_h16])
                transpose_to(h16, R_h16, qs, 8, tT, R_tT)
                yield
                bq = wrot.next()
                for h in range(4):
                    for kc in range(KC):
                        P.op("pe", _call("matmul",
                            out=pb[bq][:, h * qs:(h + 1) * qs], lhsT=wmqb[:, kc * 512 + h * 128: kc * 512 + (h + 1) * 128],
                            rhs=tT[:, kc * qs:(kc + 1) * qs], start=(kc == 0), stop=(kc == KC - 1)),
                            reads=[R_wmq, R_tT], writes=[R_pb[bq]])
                P.op("act", _call("activation", out=qmT[:, 0:4 * qs], in_=pb[bq][:, 0:4 * qs], func=AF.Copy, scale=float(128.0 ** -0.5)),
                     reads=[R_pb[bq]], writes=[R_qm])
                yield
                bs0, bs1 = wrot.next(), wrot.next()
                for h in range(4):
                    for mt in range(2):
                        idx = h * 2 + mt
                        bank = bs0 if idx < 4 else bs1
                        c0 = (idx % 4) * qs
                        P.op("pe", _call("matmul",
                            out=pb[bank][:, c0:c0 + qs], lhsT=mkT[mi][:, h * 256 + mt * 128: h * 256 + (mt + 1) * 128],
                            rhs=qmT[:, h * qs:(h + 1) * qs], start=True, stop=True),
                            reads=[R_mk[mi], R_qm], writes=[R_pb[bank]])
                for k, bank in enumerate((bs0, bs1)):
                    P.op("act", _call("activation", out=PTm[:, k * 4 * qs:(k + 1) * 4 * qs], in_=pb[bank][:, 0:4 * qs], func=AF.Exp),
                         reads=[R_pb[bank]], writes=[R_PTm])
                yield
                bo0, bo1 = wrot.next(), wrot.next()
                for h in range(4):
                    bank = bo0 if h < 2 else bo1
                    for mt in range(2):
                        idx = h * 2 + mt
                        P.op("pe", _call("matmul",
                            out=pb[bank][0:qs, (h % 2) * 129:(h % 2) * 129 + 129], lhsT=PTm[:, idx * qs:(idx + 1) * qs],
                            rhs=mva[mi][:, (mt * 4 + h) * 129:(mt * 4 + h) * 129 + 129],
                            start=(h % 2 == 0 and mt == 0), stop=(mt == 1), skip_group_check=True),
                            reads=[R_PTm, R_mv[mi]], writes=[R_pb[bank]])
                for k, bank in enumerate((bo0, bo1)):
                    ov = pb[bank][0:qs, 0:258].rearrange("p (h d) -> p h d", d=129)
                    P.op("dve", _call("tensor_scalar", out=stat[0:qs, 20 + 2 * k:22 + 2 * k].rearrange("p (h o) -> p h o", o=1),
                                                                      in0=ov[:, :, 128:129], scalar1=1e-30, scalar2=None, op0=ALU.max),
                         reads=[R_pb[bank]], writes=[R_stat])
                    P.op("dve", _call("reciprocal", out=stat[0:qs, 20 + 2 * k:22 + 2 * k], in_=stat[0:qs, 20 + 2 * k:22 + 2 * k]),
                         reads=[R_stat], writes=[R_stat])
                    for hh in range(2):
                        h = k * 2 + hh
                        P.op("dve", _call("tensor_scalar",
                            out=o16[0:qs, h * 128:(h + 1) * 128], in0=pb[bank][0:qs, hh * 129: hh * 129 + 128],
                            scalar1=stat[0:qs, 20 + 2 * k + hh:21 + 2 * k + hh], scalar2=None, op0=ALU.mult),
                            reads=[R_pb[bank], R_stat], writes=[R_o16])
                yield
                transpose_to(o16, R_o16, qs, 4, oT, R_oT)
                yield
                b0, b1 = wrot.next(), wrot.next()
                for n, bank in enumerate((b0, b1)):
                    for c in range(4):
                        P.op("pe", _call("matmul",
                            out=pb[bank][0:qs, :], lhsT=oT[:, c * qs:(c + 1) * qs], rhs=wmob[:, c * 1024 + n * 512: c * 1024 + (n + 1) * 512],
                            start=(c == 0), stop=(c == 3)),
                            reads=[R_oT, R_wmo], writes=[R_pb[bank]])
                    P.op("dve", _call("scalar_tensor_tensor",
                        out=hA[0:qs, n * 512:(n + 1) * 512], in0=hB[0:qs, n * 512:(n + 1) * 512], scalar=ALPHA, in1=pb[bank][0:qs, :],
                        op0=ALU.mult, op1=ALU.add),
                        reads=[R_hB, R_pb[bank]], writes=[R_hA])
                yield
                layer_norm(hA, R_hA, qs, 2, hB, R_hB, stat, R_stat)
                yield
                P.dma("sp", h2D[blk * 128: blk * 128 + qs, :], hB[0:qs, :], reads=[R_hB], writes=[R_h2D[blk]], defer=True)
                P.op("act", _call("activation", out=h16[0:qs, :], in_=hB[0:qs, :], func=AF.Copy), reads=[R_hB], writes=[R_h16])
                transpose_to(h16, R_h16, qs, 8, h2T[s], R_h2T[s])
                P.dma("sp", h2TD[blk][:, 0:8 * qs], h2T[s][:, 0:8 * qs], reads=[R_h2T[s]], writes=[R_h2TD[blk]], defer=True)
                yield

            def run_staggered(gens, lag):
                active = []
                pending = list(gens)
                tick = 0
                while active or pending:
                    if pending and (not active or tick >= lag):
                        active.append(pending.pop(0))
                        tick = 0
                    for g in list(active):
                        try:
                            next(g)
                        except StopIteration:
                            active.remove(g)
                    tick += 1

            blocks = [(16, 2, 126, 0), (17, 16, 0, 1)] + [(i, 128, 0, 0) for i in range(16)]
            run_staggered([phaseB_block(b_, q_, r_, m_, pos % 4) for pos, (b_, q_, r_, m_) in enumerate(blocks)], 3)
            checkpoint('phaseB')
            P.flush(block)

        P.barrier()
        with ExitStack() as sc:
            wdb = sb(sc, "wdb", [128, NFC * 1024], BF16)
            R_wd = Res("wd")
            wst = [sb(sc, "wstc%d" % k, [128, 2048], F32) for k in range(2)]
            R_wst = [Res("wstc%d" % k) for k in range(2)]
            wsl = [sb(sc, "wsl%d" % k, [128, 2048], BF16) for k in range(2)]
            R_wsl = [Res("wsl%d" % k) for k in range(2)]
            R_wslB = [Res("wslB%d" % k) for k in range(2)]
            hT2 = [sb(sc, "hT%d" % k, [128, NFC * 512], BF16) for k in range(2)]
            R_hT2 = [Res("hT%d" % k) for k in range(2)]
            hTm = sb(sc, "hTm", [128, NFC * 16], BF16)
            R_hTm = Res("hTm")
            h2Tg = [sb(sc, "h2Tg%d" % k, [128, 8 * 512], BF16) for k in range(2)]
            R_h2Tg = [Res("h2Tg%d" % k) for k in range(2)]
            h2Tm = sb(sc, "h2Tm", [128, 8 * 18], BF16)
            R_h2Tm = Res("h2Tm")
            Gb = [sb(sc, "Gb%d" % k, [128, 532], F32) for k in range(3)]
            R_Gb = [Res("Gb%d" % k) for k in range(3)]
            Gs = sb(sc, "Gs", [128, 18], F32)
            R_Gs = Res("Gs")
            t0b = [sb(sc, "t0b%d" % k, [128, 530], F32) for k in range(3)]
            R_t0 = [Res("t0%d" % k) for k in range(3)]
            geb = [sb(sc, "geb%d" % k, [128, 530], F32) for k in range(3)]
            R_ge = [Res("ge%d" % k) for k in range(3)]
            t1b = [sb(sc, "t1b%d" % k, [128, 530], F32) for k in range(3)]
            R_t1b = [Res("t1b%d" % k) for k in range(3)]
            t2b = [sb(sc, "t2b%d" % k, [128, 530], F32) for k in range(3)]
            R_t2b = [Res("t2b%d" % k) for k in range(3)]
            t0s = sb(sc, "t0s", [128, 16], F32)
            ges = sb(sc, "ges", [128, 16], F32)
            R_ts = Res("ts")
            carry = sb(sc, "carry", [128, NFC * 2], F32)
            R_carry = [Res("carry%d" % c) for c in range(NFC)]
            sfc = sb(sc, "sfc", [128, NFC * 2], F32)
            R_sfc = Res("sfc")
            sconv = sb(sc, "sconv", [128, NFC * 2], F32)
            wconv = sb(sc, "wconv", [128, NFC * 3], F32)
            bconv = sb(sc, "bconv", [128, NFC], F32)
            flag = sb(sc, "flag", [128, 1], F32)
            R_cc = Res("cc")
            ln3 = sb(sc, "ln3", [128, 2 * 1024], F32)
            R_ln3 = Res("ln3")
            h2r = [sb(sc, "h2r%d" % k, [128, 1024], F32) for k in range(2)]
            R_h2r = [Res("h2r%d" % k) for k in range(2)]
            yA = sb(sc, "yA", [128, 1024], F32)
            R_yA = Res("yA")
            yB = [sb(sc, "yB%d" % k, [128, 1024], F32) for k in range(2)]
            R_yB = [Res("yB%d" % k) for k in range(2)]
            stat = sb(sc, "statc", [128, 32], F32)
            R_stat = Res("statc")

            P.dma("sp", sconv[:, :], I["sconvT"][:, :], writes=[R_cc])
            P.dma("sp", wconv[:, :], I["wconvT"][:, :], writes=[R_cc])
            P.dma("sp", bconv[:, :], I["bconvT"][:, :], writes=[R_cc])
            P.dma("sp", flag[:, :], I["flag"][:, :], writes=[R_cc])
            for k in range(2):
                P.dma("sp", ln3[:, k * 1024:(k + 1) * 1024], I["lnp"][4 + k:5 + k, :].to_broadcast([128, 1024]), writes=[R_ln3])
            def wdown_piece(j):
                kq = j % 2
                P.dma("sp", yB[kq][:, :], I["wdown"][:, j * 1024:(j + 1) * 1024], writes=[R_yB[kq]])
                P.op("act", _call("activation", out=wdb[:, j * 1024:(j + 1) * 1024], in_=yB[kq][:, :], func=AF.Copy), reads=[R_yB[kq]], writes=[R_wd])

            P.dma("sp", h2Tm[:, :].rearrange("p (c q) -> p c q", q=18)[:, :, 0:2], h2TD[16][:, 0:16].rearrange("p (c q) -> p c q", q=2),
                  reads=[R_h2TD[16]], writes=[R_h2Tm], slow=True)
            P.dma("sp", h2Tm[:, :].rearrange("p (c q) -> p c q", q=18)[:, :, 2:18], h2TD[17][:, 0:128].rearrange("p (c q) -> p c q", q=16),
                  reads=[R_h2TD[17]], writes=[R_h2Tm], slow=True)

            checkpoint('phaseC_pre')
            UB = [0, 2, 4]
            GBK = [1, 3, 5]
            MB = 7
            YB = [6, 7]
            wk = [0]

            def ln3_out(pre_banks, qs, h2src, R_h2src, dst_ap, ys, R_ys):
                for n, bank in enumerate(pre_banks):
                    P.op("dve", _call("scalar_tensor_tensor",
                        out=yA[0:qs, n * 512:(n + 1) * 512], in0=h2src[0:qs, n * 512:(n + 1) * 512], scalar=ALPHA, in1=pb[bank][0:qs, :],
                        op0=ALU.mult, op1=ALU.add),
                        reads=[R_h2src, R_pb[bank]], writes=[R_yA])
                for c in range(2):
                    P.op("dve", _call("bn_stats", out=stat[0:qs, c * 6:(c + 1) * 6], in_=yA[0:qs, c * 512:(c + 1) * 512]),
                         reads=[R_yA], writes=[R_stat])
                P.op("dve", _call("bn_aggr", out=stat[0:qs, 12:14], in_=stat[0:qs, 0:12]), reads=[R_stat], writes=[R_stat])
                P.op("dve", _call("tensor_scalar", out=stat[0:qs, 14:15], in0=stat[0:qs, 13:14], scalar1=LN_EPS, scalar2=None, op0=ALU.add),
                     reads=[R_stat], writes=[R_stat])
                P.op("act", _call("activation", out=stat[0:qs, 15:16], in_=stat[0:qs, 14:15], func=AF.Sqrt), reads=[R_stat], writes=[R_stat])
                P.op("dve", _call("reciprocal", out=stat[0:qs, 16:17], in_=stat[0:qs, 15:16]), reads=[R_stat], writes=[R_stat])
                P.op("dve", _call("scalar_tensor_tensor", out=stat[0:qs, 17:18], in0=stat[0:qs, 12:13], scalar=-1.0, in1=stat[0:qs, 16:17],
                                                             op0=ALU.mult, op1=ALU.mult),
                     reads=[R_stat], writes=[R_stat])
                P.op("act", _call("activation", out=ys[0:qs, :], in_=yA[0:qs, :], func=AF.Identity, scale=stat[0:qs, 16:17], bias=stat[0:qs, 17:18]),
                     reads=[R_yA, R_stat], writes=[R_ys])
                P.op("pool", _call("tensor_tensor", out=ys[0:qs, :], in0=ys[0:qs, :], in1=ln3[0:qs, 0:1024], op=ALU.mult),
                     reads=[R_ys, R_ln3], writes=[R_ys])
                P.op("pool", _call("tensor_tensor", out=ys[0:qs, :], in0=ys[0:qs, :], in1=ln3[0:qs, 1024:2048], op=ALU.add),
                     reads=[R_ys, R_ln3], writes=[R_ys])
                P.dma("sp", dst_ap, ys[0:qs, :], reads=[R_ys], defer=True)

            def load_h2Tg(grp):
                gs = grp % 2
                for bi in range(4):
                    blk = grp * 4 + bi
                    P.dma("sp", h2Tg[gs][:, :].rearrange("p (c q) -> p c q", q=512)[:, :, bi * 128:(bi + 1) * 128],
                          h2TD[blk][:, :].rearrange("p (c q) -> p c q", q=128), reads=[R_h2TD[blk]], writes=[R_h2Tg[gs]])

            def c_s1(grp, c):
                s = (grp * NFC + c) % 2
                P.dma("sp", wst[s][:, :], I["wup"][c], writes=[R_wst[s]])
                P.op("dve", _call("tensor_copy", out=wsl[s][:, 0:1024], in_=wst[s][:, 0:1024]), reads=[R_wst[s]], writes=[R_wsl[s]])
                P.op("dve", _call("tensor_copy", out=wsl[s][:, 1024:2048], in_=wst[s][:, 1024:2048]), reads=[R_wst[s]], writes=[R_wslB[s]])

            def c_s2(grp, c):
                s = (grp * NFC + c) % 2
                gs = grp % 2
                mo = (c % 2) * 64
                if grp == 0:
                    for part, oc in ((0, mo), (1, mo + 32)):
                        for kc in range(KC):
                            P.op("pe", _call("matmul", out=pb[MB][:, oc:oc + 18], lhsT=wsl[s][:, kc * 256 + part * 128: kc * 256 + (part + 1) * 128],
                                             rhs=h2Tm[:, kc * 18:(kc + 1) * 18], start=(kc == 0), stop=(kc == KC - 1)),
                                 reads=[R_wsl[s], R_wslB[s], R_h2Tm], writes=[R_pb[MB]])
                k3 = (grp * NFC + c) % 3
                ub, gbk = UB[k3], GBK[k3]
                for part, bank in ((0, ub), (1, gbk)):
                    for kc in range(KC):
                        P.op("pe", _call("matmul", out=pb[bank][:, :], lhsT=wsl[s][:, kc * 256 + part * 128: kc * 256 + (part + 1) * 128],
                                         rhs=h2Tg[gs][:, kc * 512:(kc + 1) * 512], start=(kc == 0), stop=(kc == KC - 1)),
                             reads=[R_wsl[s], R_wslB[s], R_h2Tg[gs]], writes=[R_pb[bank]])

            def c_s3(grp, c):
                hTg, R_hTg = hT2[grp % 2], R_hT2[grp % 2]
                mo = (c % 2) * 64
                k3 = (grp * NFC + c) % 3
                ub, gbk = UB[k3], GBK[k3]
                G, R_G = Gb[k3], R_Gb[k3]
                t0, R_t = t0b[k3], R_t0[k3]
                ge, R_g = geb[k3], R_ge[k3]
                t1, R_t1 = t1b[k3], R_t1b[k3]
                t2, R_t2 = t2b[k3], R_t2b[k3]
                W = 530 if grp == 0 else 512
                if grp == 0:
                    P.op("dve", _call("tensor_scalar", out=carry[:, c * 2:(c + 1) * 2], in0=pb[MB][:, mo + 32:mo + 34], scalar1=flag[:, 0:1],
                                      scalar2=None, op0=ALU.mult),
                         reads=[R_pb[MB], R_cc], writes=[R_carry[c]])
                P.op("act", _call("activation", out=G[:, 0:2], in_=carry[:, c * 2:(c + 1) * 2], func=AF.Copy),
                     reads=[R_carry[c]], writes=[R_G])
                P.op("act", _call("activation", out=G[:, 2:514], in_=pb[gbk][:, :], func=AF.Copy), reads=[R_pb[gbk]], writes=[R_G])
                if grp == 0:
                    P.op("act", _call("activation", out=G[:, 514:516], in_=sconv[:, c * 2:(c + 1) * 2], func=AF.Copy), reads=[R_cc], writes=[R_G])
                    P.op("act", _call("activation", out=G[:, 516:532], in_=pb[MB][:, mo + 34:mo + 50], func=AF.Copy), reads=[R_pb[MB]], writes=[R_G])
                    P.op("act", _call("activation", out=sfc[:, c * 2:(c + 1) * 2], in_=G[:, 530:532], func=AF.Copy), reads=[R_G], writes=[R_sfc])
                P.op("act", _call("activation", out=carry[:, c * 2:(c + 1) * 2], in_=G[:, 512:514], func=AF.Copy),
                     reads=[R_G], writes=[R_carry[c]])
                P.op("act", _call("activation", out=t0[:, 0:W], in_=G[:, 2:2 + W], func=AF.Identity,
                                  scale=wconv[:, c * 3 + 2:c * 3 + 3], bias=bconv[:, c:c + 1]),
                     reads=[R_G, R_cc], writes=[R_t])
                P.op("act", _call("activation", out=t1[:, 0:W], in_=G[:, 1:1 + W], func=AF.Identity, scale=wconv[:, c * 3 + 1:c * 3 + 2]),
                     reads=[R_G, R_cc], writes=[R_t1])
                P.op("act", _call("activation", out=t2[:, 0:W], in_=G[:, 0:W], func=AF.Identity, scale=wconv[:, c * 3:c * 3 + 1]),
                     reads=[R_G, R_cc], writes=[R_t2])
                P.op("dve", _call("tensor_tensor", out=t0[:, 0:W], in0=t0[:, 0:W], in1=t1[:, 0:W], op=ALU.add), reads=[R_t, R_t1], writes=[R_t])
                P.op("dve", _call("tensor_tensor", out=t0[:, 0:W], in0=t0[:, 0:W], in1=t2[:, 0:W], op=ALU.add), reads=[R_t, R_t2], writes=[R_t])
                P.op("act", _call("activation", out=ge[:, 0:W], in_=t0[:, 0:W], func=AF.Gelu_apprx_tanh), reads=[R_t], writes=[R_g])
                P.op("dve", _call("tensor_tensor", out=hTg[:, c * 512:(c + 1) * 512], in0=pb[ub][:, :], in1=ge[:, 0:512], op=ALU.mult),
                     reads=[R_pb[ub], R_g], writes=[R_hTg])
                if grp == 0:
                    P.op("dve", _call("tensor_tensor", out=hTm[:, c * 16:(c + 1) * 16], in0=pb[MB][:, mo + 2:mo + 18], in1=ge[:, 514:530], op=ALU.mult),
                         reads=[R_pb[MB], R_g], writes=[R_hTm])

            def c_down(grp):
                hTg, R_hTg = hT2[grp % 2], R_hT2[grp % 2]
                if grp == 0:
                    for n, bank in enumerate(YB):
                        for c in range(NFC):
                            P.op("pe", _call("matmul", out=pb[bank][0:16, :], lhsT=hTm[:, c * 16:(c + 1) * 16],
                                             rhs=wdb[:, c * 1024 + n * 512: c * 1024 + (n + 1) * 512], start=(c == 0), stop=(c == NFC - 1)),
                                 reads=[R_hTm, R_wd], writes=[R_pb[bank]])
                    P.dma("sp", h2r[0][0:16, :], h2D[17 * 128: 17 * 128 + 16, :], reads=[R_h2D[17]], writes=[R_h2r[0]])
                    ln3_out(YB, 16, h2r[0], R_h2r[0], O["ys"][:, :], yB[0], R_yB[0])
                    P.dma("sp", O["sfcT"][:, :], sfc[:, :], reads=[R_sfc], defer=True)
                for bi in range(4):
                    blk = grp * 4 + bi
                    hs = blk % 2
                    P.dma("sp", h2r[hs][:, :], h2D[blk * 128:(blk + 1) * 128, :], reads=[R_h2D[blk]], writes=[R_h2r[hs]])
                    for n, bank in enumerate(YB):
                        for c in range(NFC):
                            P.op("pe", _call("matmul", out=pb[bank][:, :], lhsT=hTg[:, c * 512 + bi * 128: c * 512 + (bi + 1) * 128],
                                             rhs=wdb[:, c * 1024 + n * 512: c * 1024 + (n + 1) * 512], start=(c == 0), stop=(c == NFC - 1)),
                                 reads=[R_hTg, R_wd], writes=[R_pb[bank]])
                    ln3_out(YB, 128, h2r[hs], R_h2r[hs], O["y"][blk * 128:(blk + 1) * 128, :], yB[hs], R_yB[hs])

            seq = [(grp, c) for grp in range(4) for c in range(NFC)]
            nseq = len(seq)
            load_h2Tg(0)
            load_h2Tg(1)
            for idx in range(nseq + 2):
                if 1 <= idx <= NFC:
                    wdown_piece(idx - 1)
                if idx < nseq:
                    c_s1(*seq[idx])
                if 1 <= idx <= nseq:
                    c_s2(*seq[idx - 1])
                if idx >= 2:
                    g3, c3 = seq[idx - 2]
                    c_s3(g3, c3)
                    if c3 == NFC - 1:
                        c_down(g3)
                        if g3 + 2 < 4:
                            load_h2Tg(g3 + 2)
            P.dma("sp", O["fcT"][:, :], carry[:, :], reads=R_carry, defer=True)
            P.finish()
            P.flush(block)
    return nc


def _t5_bucket(rel):
    half, max_exact = 16, 8
    n = np.abs(rel)
    log_ratio = np.log(np.maximum(n, 1).astype(np.float32) / max_exact) / math.log(128 / max_exact)
    large = np.minimum(max_exact + (log_ratio * (half - max_exact)).astype(np.int32), half - 1)
    return np.where(rel < 0, half, 0) + np.where(n < max_exact, n, large)


def _host_inputs(inp):
    f32 = np.float32
    x_prompt = np.asarray(inp["x_prompt"], f32)
    x_sample = np.asarray(inp["x_sample"], f32)
    w_in = np.asarray(inp["w_in"], f32)[0]
    qa, ka, va = w_in[:, 0:512], w_in[:, 512:1024], w_in[:, 1024:1536]
    qb, kb, vb = w_in[:, 1536:2048], w_in[:, 2048:2176], w_in[:, 2176:2304]
    qi, ki, wi = w_in[:, 2304:2816], w_in[:, 2816:2880], w_in[:, 2880:2888]
    qbp = np.concatenate([np.concatenate([qb[:, r * 64:(r + 1) * 64], qb[:, (4 + r) * 64:(5 + r) * 64]], axis=1) for r in range(4)], axis=1)
    winp = np.concatenate([qa, ka, qbp, kb, qi, ki, ki, va, vb, wi], axis=1)
    assert winp.shape[1] == NCOL

    def kc_layout(w):
        n = w.shape[1]
        return np.ascontiguousarray(w.reshape(8, 128, n).transpose(1, 0, 2).reshape(128, 8 * n))

    shared = {}
    shared["win"] = kc_layout(winp)
    shared["wo"] = kc_layout(np.asarray(inp["w_o"], f32)[0])
    shared["wmq"] = kc_layout(np.asarray(inp["w_mq"], f32)[0])
    shared["wmk"] = kc_layout(np.asarray(inp["w_mk"], f32)[0])
    shared["wmv"] = kc_layout(np.asarray(inp["w_mv"], f32)[0])
    wmo = np.asarray(inp["w_mo"], f32)[0]
    shared["wmo"] = np.ascontiguousarray(wmo.reshape(4, 128, 1024).transpose(1, 0, 2).reshape(128, 4096))
    w_up = np.asarray(inp["w_up"], f32)[0]
    wu = w_up[:, :DFF].reshape(8, 128, NFC, 128)
    wg = w_up[:, DFF:].reshape(8, 128, NFC, 128)
    wup = np.stack([wu, wg], axis=3)
    shared["wup"] = np.ascontiguousarray(wup.transpose(2, 1, 0, 3, 4).reshape(NFC, 128, 8 * 256))
    w_down = np.asarray(inp["w_down"], f32)[0]
    shared["wdown"] = np.ascontiguousarray(w_down.reshape(NFC, 128, 1024).transpose(1, 0, 2).reshape(128, NFC * 1024))
    shared["lnp"] = np.ascontiguousarray(np.stack([np.asarray(inp[k], f32)[0] for k in ("ln1_g", "ln1_b", "ln2_g", "ln2_b", "ln3_g", "ln3_b")]))
    w_conv = np.asarray(inp["w_conv"], f32)[0]
    shared["wconvT"] = np.ascontiguousarray(w_conv.reshape(3, NFC, 128).transpose(2, 1, 0).reshape(128, NFC * 3))
    shared["bconvT"] = np.ascontiguousarray(np.asarray(inp["b_conv"], f32)[0].reshape(NFC, 128).T)
    shared["ident"] = np.eye(128, dtype=f32)
    tabA = np.asarray(inp["a_rel_bias"], f32)[0]
    qq = np.arange(128)[:, None]
    kk = np.arange(640)[None, :]
    kpos = kk - 512
    rel = qq - kpos
    cq = qq // 64
    kch = np.floor_divide(kpos, 64)
    allowed = (kch >= cq - 8) & (kch <= cq)
    bias = tabA[np.clip(rel, -64, 64) + 64]
    AB = np.where(allowed[:, :, None], bias, f32(NEGM)).astype(f32)
    shared["AB"] = np.ascontiguousarray(AB.transpose(0, 2, 1).reshape(128, 8 * ABW))
    js = np.arange(16)[:, None]
    ks = np.arange(528)[None, :]
    ABs = tabA[np.clip(512 + js - ks, -64, 64) + 64]
    shared["ABs"] = np.ascontiguousarray(ABs.transpose(0, 2, 1).reshape(16, 8 * 528)).astype(f32)
    t5 = np.asarray(inp["t5_bias"], f32)
    relB = np.arange(128)[:, None] - np.arange(256)[None, :] + 128
    Bn = t5[_t5_bucket(relB)]
    shared["Bn"] = np.ascontiguousarray(Bn.transpose(0, 2, 1).reshape(128, 8 * BNW)).astype(f32)
    relBs = 128 + np.arange(16)[:, None] - np.arange(144)[None, :]
    Bns = t5[_t5_bucket(relBs)]
    shared["Bns"] = np.ascontiguousarray(Bns.transpose(0, 2, 1).reshape(16, 8 * 144)).astype(f32)
    shared["C15"] = np.ascontiguousarray(np.broadcast_to(t5[15][None, :], (128, 8))).astype(f32)
    dm = np.zeros((128, 128), f32)
    dm[0:64, 64:128] = NEGM
    shared["diagmask"] = dm

    mem_prompt = np.asarray(inp["mem_prompt"], f32)
    maps = []
    for c in range(8):
        b, half = c // 2, c % 2
        m = dict(shared)
        xk = np.zeros((4096, 1024), f32)
        if half == 1:
            xk[:] = x_prompt[b]
        else:
            xk[2048:] = x_prompt[b, :2048]
        m["xkT"] = np.ascontiguousarray(xk.reshape(32, 128, 8, 128).transpose(0, 3, 2, 1).reshape(32, 128, 1024))
        xs = x_sample[c]
        m["xsT"] = np.ascontiguousarray(xs.reshape(16, 8, 128).transpose(2, 1, 0).reshape(128, 128))
        xres = np.zeros((NBLK * 128, 1024), f32)
        xres[0:2048] = xk[2048:]
        xres[2048:2050] = xk[2046:2048]
        xres[17 * 128:17 * 128 + 16] = xs
        m["xres"] = xres
        m["memT"] = np.ascontiguousarray(mem_prompt[b].reshape(256, 8, 128).transpose(2, 1, 0).reshape(128, 2048))
        cmk = np.asarray(inp["cache_mem_k"], f32)[0, c]
        m["cmkT"] = np.ascontiguousarray(cmk.transpose(2, 1, 0).reshape(128, 1024))
        m["cmv"] = np.ascontiguousarray(np.asarray(inp["cache_mem_v"], f32)[0, c].reshape(256, 512))
        cak = np.asarray(inp["cache_a_k"], f32)[0, c]
        m["cakT"] = np.ascontiguousarray(cak.reshape(512, 4, 2, 64).transpose(2, 3, 1, 0).reshape(128, 2048))
        m["cav"] = np.ascontiguousarray(np.asarray(inp["cache_a_v"], f32)[0, c].reshape(512, 512))
        cbk = np.asarray(inp["cache_b_k"], f32)[0, c]
        m["cbkT"] = np.ascontiguousarray(cbk.reshape(2048, 128).T)
        m["cbv"] = np.ascontiguousarray(np.asarray(inp["cache_b_v"], f32)[0, c].reshape(2048, 128))
        cbi = np.asarray(inp["cache_b_kidx"], f32)[0, c]
        m["cbiT"] = np.ascontiguousarray(np.concatenate([cbi.T, cbi.T], axis=0))
        sc_ = np.asarray(inp["state_ffn_conv"], f32)[0, c]
        m["sconvT"] = np.ascontiguousarray(sc_.reshape(2, NFC, 128).transpose(2, 1, 0).reshape(128, NFC * 2))
        m["colmask"] = np.full((128, 1), NEGM if half == 0 else 0.0, f32)
        kv = np.ones((128, NT), f32)
        if half == 0:
            kv[:, 0:16] = 0.0
        m["kvalid"] = kv
        m["flag"] = np.full((128, 1), float(half), f32)
        maps.append(m)
    return maps


_NC_CACHE = {}


def _run(inputs, debug=False):
    key = bool(debug)
    if key not in _NC_CACHE:
        _NC_CACHE[key] = build_program(debug=debug)
    nc = _NC_CACHE[key]
    maps = _host_inputs(inputs)
    res = run_bass_kernel_spmd(nc, maps, core_ids=list(range(8)))
    return res.results


def kernel(**inputs):
    R = _run(inputs)
    f32 = np.float32
    y = np.zeros((4, 4096, 1024), f32)
    ys = np.zeros((8, 16, 1024), f32)
    pak = np.zeros((1, 4, 512, 8, 64), f32)
    pav = np.zeros((1, 4, 512, 8, 64), f32)
    pbk = np.zeros((1, 4, 4096, 2, 64), f32)
    pbv = np.zeros((1, 4, 4096, 2, 64), f32)
    pbi = np.zeros((1, 4, 4096, 64), f32)
    pmk = np.zeros((1, 4, 256, 4, 128), f32)
    pmv = np.zeros((1, 4, 256, 4, 128), f32)
    pfc = np.zeros((1, 4, 2, DFF), f32)
    sak = np.zeros((1, 8, 16, 8, 64), f32)
    sav = np.zeros((1, 8, 16, 8, 64), f32)
    sbk = np.zeros((1, 8, 16, 2, 64), f32)
    sbv = np.zeros((1, 8, 16, 2, 64), f32)
    sbi = np.zeros((1, 8, 16, 64), f32)
    sfc = np.zeros((1, 8, 2, DFF), f32)
    for c in range(8):
        b, half = c // 2, c % 2
        r = R[c]
        y[b, half * 2048:(half + 1) * 2048] = np.asarray(r["y"], f32)
        ys[c] = np.asarray(r["ys"], f32)
        if half == 1:
            akT = np.asarray(r["akT"], f32).reshape(2, 64, 4, 512)
            pak[0, b] = akT.transpose(3, 2, 0, 1).reshape(512, 8, 64)
            pav[0, b] = np.asarray(r["av"], f32).reshape(512, 8, 64)
            pbk[0, b] = np.asarray(r["bkT"], f32).T.reshape(4096, 2, 64)
            pbv[0, b] = np.asarray(r["bv"], f32).reshape(4096, 2, 64)
            pbi[0, b] = np.asarray(r["biT"], f32).T
            pmk[0, b] = np.asarray(r["mkT"], f32).reshape(128, 4, 256).transpose(2, 1, 0)
            pmv[0, b] = np.asarray(r["mv"], f32).reshape(256, 4, 128)
            pfc[0, b] = np.asarray(r["fcT"], f32).reshape(128, NFC, 2).transpose(2, 1, 0).reshape(2, DFF)
        sakT = np.asarray(r["sakT"], f32).reshape(2, 64, 4, 16)
        sak[0, c] = sakT.transpose(3, 2, 0, 1).reshape(16, 8, 64)
        sav[0, c] = np.asarray(r["sav"], f32).reshape(16, 8, 64)
        sbk[0, c] = np.asarray(r["sbkT"], f32).T.reshape(16, 2, 64)
        sbv[0, c] = np.asarray(r["sbv"], f32).reshape(16, 2, 64)
        sbi[0, c] = np.asarray(r["sbiT"], f32).T
        sfc[0, c] = np.asarray(r["sfcT"], f32).reshape(128, NFC, 2).transpose(2, 1, 0).reshape(2, DFF)
    return (y, ys, pak, pav, pbk, pbv, pbi, pmk, pmv, pfc, sak, sav, sbk, sbv, sbi, sfc)
```

```python
import math
from contextlib import ExitStack

import numpy as np
import concourse.bass as bass
import concourse.mybir as mybir
from concourse.bass_utils import run_bass_kernel_spmd

F32 = mybir.dt.float32
BF16 = mybir.dt.bfloat16
AF = mybir.ActivationFunctionType
ALU = mybir.AluOpType

D = 1024
KC = 8
NT = 32
NCOL = 2952
C_QA, C_KA, C_QB, C_KB, C_QI, C_KI, C_VA, C_VB, C_WI = 0, 512, 1024, 1536, 1664, 2176, 2304, 2816, 2944
DFF = 2816
NFC = 22
ALPHA = 2.0 ** 0.25
LN_EPS = 1e-5
NEGM = -30000.0
NIT = 16
BIS_W0 = 16.0
ABW = 640
BNW = 256
NBLK = 18


class Res:
    __slots__ = ("lw", "rd", "name", "excl")

    def __init__(self, name="", excl=False):
        self.lw = None
        self.rd = {}
        self.name = name
        self.excl = excl


def _call(name, *args, **kw):
    return lambda e: getattr(e, name)(*args, **kw)


class Prog:
    ENG = ("pe", "act", "dve", "pool", "sp")

    def __init__(self, nc, sems, dma_sems):
        self.nc = nc
        self.streams = {e: [] for e in self.ENG}
        self.sem = sems
        self.cnt = {e: 0 for e in self.ENG}
        self.seen = {e: {} for e in self.ENG}
        self.dsems = dma_sems
        self.dval = [0] * len(dma_sems)
        self.dnext = 0
        self.semh = dict(sems)
        for i, h in enumerate(dma_sems):
            self.semh[("d", i)] = h
        self.ninst = 0
        self.dead = False
        self.deferred = []
        self.defer_lag = 48

    def _deps(self, reads, writes, eng=None):
        d = {}
        for r in reads:
            if r.lw is not None:
                k, v = r.lw
                if d.get(k, 0) < v:
                    d[k] = v
            if r.excl:
                for k, v in r.rd.items():
                    if k != eng and d.get(k, 0) < v:
                        d[k] = v
        for w in writes:
            if w.lw is not None:
                k, v = w.lw
                if d.get(k, 0) < v:
                    d[k] = v
            for k, v in w.rd.items():
                if d.get(k, 0) < v:
                    d[k] = v
        return d

    def _wait(self, eng, deps):
        for k, v in deps.items():
            if k == "pe" and eng == "pe":
                continue
            if self.seen[eng].get(k, 0) >= v:
                continue
            self.seen[eng][k] = v
            h = self.semh[k]
            self.streams[eng].append(lambda e, h=h, v=v: e.wait_ge(h, v))

    def _flush_deferred(self, force=False, reads=(), writes=()):
        if not self.deferred:
            return
        conflict = force
        if not conflict:
            ws = set(id(w) for w in writes)
            rs = set(id(r) for r in reads)
            for d in self.deferred:
                dr = set(id(x) for x in d[3])
                dw = set(id(x) for x in d[4])
                if (ws & dr) or (ws & dw) or (rs & dw):
                    conflict = True
                    break
        if conflict:
            pend, self.deferred = self.deferred, []
            for d in pend:
                self._dma_now(d[0], d[1], d[2], d[3], d[4], d[5])
            return
        while self.deferred and self.ninst - self.deferred[0][6] >= self.defer_lag:
            d = self.deferred.pop(0)
            self._dma_now(d[0], d[1], d[2], d[3], d[4], d[5])

    def op(self, eng, fn, reads=(), writes=()):
        if self.dead:
            return
        self._flush_deferred(False, reads, writes)
        self._wait(eng, self._deps(reads, writes, eng))
        self.cnt[eng] += 1
        n = self.cnt[eng]
        h = self.sem[eng]
        self.streams[eng].append(lambda e, fn=fn, h=h: fn(e).then_inc(h, 1))
        self.ninst += 1
        for r in reads:
            if r.rd.get(eng, 0) < n:
                r.rd[eng] = n
        for w in writes:
            w.lw = (eng, n)
            w.rd = {}

    def dma(self, q, out, in_, reads=(), writes=(), slow=False, defer=False):
        if self.dead:
            return
        if defer:
            self._flush_deferred(False, reads, writes)
            self.deferred.append((q, out, in_, list(reads), list(writes), slow, self.ninst))
            return
        self._flush_deferred(False, reads, writes)
        self._dma_now(q, out, in_, reads, writes, slow)

    def _dma_now(self, q, out, in_, reads=(), writes=(), slow=False):
        deps = self._deps(reads, writes)
        i = self.dnext
        self.dnext = (i + 1) % len(self.dsems)
        k = ("d", i)
        if self.dval[i] > 0 and deps.get(k, 0) < self.dval[i]:
            deps[k] = self.dval[i]
        self._wait(q, deps)
        self.dval[i] += 16
        v = self.dval[i]
        h = self.dsems[i]
        if slow:
            self.streams[q].append(
                lambda e, out=out, in_=in_, h=h: e.dma_start(out=out, in_=in_, allow_slow_non_contiguous=True).then_inc(h, 16))
        else:
            self.streams[q].append(lambda e, out=out, in_=in_, h=h: e.dma_start(out=out, in_=in_).then_inc(h, 16))
        self.ninst += 1
        for r in reads:
            if r.rd.get(k, 0) < v:
                r.rd[k] = v
        for w in writes:
            w.lw = (k, v)
            w.rd = {}

    def barrier(self):
        if self.dead:
            return
        self._flush_deferred(True)
        deps = {e: self.cnt[e] for e in self.ENG if self.cnt[e] > 0}
        for i, v in enumerate(self.dval):
            if v > 0:
                deps[("d", i)] = v
        for e in self.ENG:
            self._wait(e, dict(deps))

    def finish(self):
        self._flush_deferred(True)
        deps = {("d", i): v for i, v in enumerate(self.dval) if v > 0}
        self._wait("sp", deps)

    def flush(self, block):
        self._flush_deferred(True)
        s = self.streams
        self.streams = {e: [] for e in self.ENG}

        def mk(lst):
            def body(e):
                for f in lst:
                    f(e)
            return body

        block.tensor(mk(s["pe"]))
        block.scalar(mk(s["act"]))
        block.vector(mk(s["dve"]))
        block.gpsimd(mk(s["pool"]))
        block.sync(mk(s["sp"]))


def build_program(debug=False, stop_at=None):
    nc = bass.Bass("TRN2", target_bir_lowering=False)

    def din(name, shape, dt=F32):
        return nc.dram_tensor(name, list(shape), dt, kind="ExternalInput").ap()

    def dout(name, shape, dt=F32):
        return nc.dram_tensor(name, list(shape), dt, kind="ExternalOutput").ap()

    def dscr(name, shape, dt):
        return nc.dram_tensor(name, list(shape), dt, kind="Internal").ap()

    I = {}
    I["xkT"] = din("xkT", [NT, 128, 1024])
    I["xsT"] = din("xsT", [128, 8 * 16])
    I["xres"] = din("xres", [NBLK * 128, 1024])
    I["win"] = din("win", [128, KC * NCOL])
    I["wo"] = din("wo", [128, 8 * 1024])
    I["wmq"] = din("wmq", [128, 8 * 512])
    I["wmk"] = din("wmk", [128, 8 * 512])
    I["wmv"] = din("wmv", [128, 8 * 512])
    I["wmo"] = din("wmo", [128, 4 * 1024])
    I["wup"] = din("wup", [NFC, 128, 8 * 256])
    I["wdown"] = din("wdown", [128, NFC * 1024])
    I["lnp"] = din("lnp", [6, 1024])
    I["wconvT"] = din("wconvT", [128, NFC * 3])
    I["bconvT"] = din("bconvT", [128, NFC])
    I["memT"] = din("memT", [128, 8 * 256])
    I["cmkT"] = din("cmkT", [128, 4 * 256])
    I["cmv"] = din("cmv", [256, 512])
    I["cakT"] = din("cakT", [128, 4 * 512])
    I["cav"] = din("cav", [512, 512])
    I["cbkT"] = din("cbkT", [128, 2048])
    I["cbv"] = din("cbv", [2048, 128])
    I["cbiT"] = din("cbiT", [128, 2048])
    I["sconvT"] = din("sconvT", [128, NFC * 2])
    I["ident"] = din("ident", [128, 128])
    I["AB"] = din("AB", [128, 8 * ABW])
    I["ABs"] = din("ABs", [16, 8 * 528])
    I["Bn"] = din("Bn", [128, 8 * BNW])
    I["Bns"] = din("Bns", [16, 8 * 144])
    I["C15"] = din("C15", [128, 8])
    I["colmask"] = din("colmask", [128, 1])
    I["diagmask"] = din("diagmask", [128, 128])
    I["kvalid"] = din("kvalid", [128, NT])
    I["flag"] = din("flag", [128, 1])

    O = {}
    O["y"] = dout("y", [2048, 1024])
    O["ys"] = dout("ys", [16, 1024])
    O["akT"] = dout("akT", [128, 4 * 512])
    O["av"] = dout("av", [512, 512])
    O["bkT"] = dout("bkT", [128, 4096])
    O["bv"] = dout("bv", [4096, 128])
    O["biT"] = dout("biT", [64, 4096])
    O["mkT"] = dout("mkT", [128, 4 * 256])
    O["mv"] = dout("mv", [256, 512])
    O["fcT"] = dout("fcT", [128, NFC * 2])
    O["sakT"] = dout("sakT", [128, 4 * 16])
    O["sav"] = dout("sav", [16, 512])
    O["sbkT"] = dout("sbkT", [128, 16])
    O["sbv"] = dout("sbv", [16, 128])
    O["sbiT"] = dout("sbiT", [64, 16])
    O["sfcT"] = dout("sfcT", [128, NFC * 2])
    if debug:
        O["dbg_mix"] = dout("dbg_mix", [NBLK * 128, 1024], BF16)
        O["dbg_h2"] = dout("dbg_h2", [NBLK * 128, 1024])
        mixD = O["dbg_mix"]
        h2D = O["dbg_h2"]
    else:
        mixD = dscr("mixD", [NBLK * 128, 1024], BF16)
        h2D = dscr("h2D", [NBLK * 128, 1024], F32)
    h2TD = dscr("h2TD", [NBLK, 128, 1024], BF16)
    R_mixD = [Res("mixD%d" % i) for i in range(NBLK)]
    R_h2D = [Res("h2D%d" % i) for i in range(NBLK)]
    R_h2TD = [Res("h2TD%d" % i) for i in range(NBLK)]

    es = ExitStack()
    with es:
        sems = {e: es.enter_context(nc.semaphore("s_" + e)) for e in Prog.ENG}
        dsems = [es.enter_context(nc.semaphore("d%d" % i)) for i in range(32)]
        P = Prog(nc, sems, dsems)
        block = es.enter_context(nc.Block())

        def checkpoint(name):
            if stop_at is not None and name == stop_at and not P.dead:
                P.finish()
                P.flush(block)
                P.dead = True

        pb = [es.enter_context(nc.psum_tensor("pb%d" % i, [128, 512], F32)) for i in range(8)]
        R_pb = [Res("pb%d" % i, excl=True) for i in range(8)]

        class Rot:
            def __init__(self, idxs):
                self.idxs = idxs
                self.i = 0

            def next(self):
                k = self.idxs[self.i % len(self.idxs)]
                self.i += 1
                return k

        def sb(stack, name, shape, dt):
            return stack.enter_context(nc.sbuf_tensor("sb_" + name, list(shape), dt))

        ident_f = sb(es, "ident_f", [128, 128], F32)
        ident = sb(es, "ident", [128, 512], BF16)
        R_ident = Res("ident")
        P.dma("sp", ident_f[:, :], I["ident"][:, :], writes=[R_ident])
        for r in range(4):
            P.op("act", _call("activation", out=ident[:, r * 128:(r + 1) * 128], in_=ident_f[:, :], func=AF.Copy),
                 reads=[R_ident], writes=[R_ident])

        def run_interleaved(gens):
            gens = [[0.0, i, g] for i, g in enumerate(gens)]
            while gens:
                gens.sort(key=lambda x: (x[0], x[1]))
                ent = gens[0]
                try:
                    c = next(ent[2])
                    ent[0] += (c if c else 1.0)
                except StopIteration:
                    gens.remove(ent)

        with ExitStack() as sa:
            winb = sb(sa, "winb", [128, KC * NCOL], BF16)
            R_win = Res("win")
            kbi = sb(sa, "kbi", [128, 2 * 4096], BF16)
            R_kbi = [Res("kbi%d" % r) for r in range(NT)]
            R_ki = [Res("ki%d" % r) for r in range(NT)]
            vb_aug = sb(sa, "vb_aug", [128, NT * 2 * 65], BF16)
            R_vb = [Res("vb%d" % r) for r in range(NT)]
            kaT = sb(sa, "kaT", [128, 6 * 512], BF16)
            R_ka = [Res("ka%d" % s) for s in range(6)]
            va_aug = sb(sa, "va_aug", [128, 6 * 8 * 65], BF16)
            R_va = [Res("va%d" % s) for s in range(6)]
            ABb = sb(sa, "ABb", [128, 8 * ABW], BF16)
            R_AB = Res("AB")
            Bnb = sb(sa, "Bnb", [128, 8 * BNW], BF16)
            R_Bn = Res("Bn")
            Mnear = [sb(sa, "Mnear%d" % k, [128, 8 * BNW], BF16) for k in range(2)]
            R_Mnear = [Res("Mnear%d" % k) for k in range(2)]
            score = [sb(sa, "score%d" % k, [128, 4096], F32) for k in range(2)]
            R_score = [Res("score%d" % k) for k in range(2)]
            Mb = [sb(sa, "Mb%d" % k, [128, 4096], BF16) for k in range(2)]
            R_M = [Res("M%d" % k) for k in range(2)]
            relu = [sb(sa, "relu%d" % k, [128, 512], BF16) for k in range(3)]
            R_relu = [Res("relu%d" % k) for k in range(3)]
            xstg2 = [sb(sa, "xstg%d" % k, [128, 1024], F32) for k in range(2)]
            R_xstg2 = [Res("xstg%d" % k) for k in range(2)]
            xstg, R_xstg = xstg2[0], R_xstg2[0]
            xTb = [sb(sa, "xTb%d" % k, [128, 1024], BF16) for k in range(2)]
            R_xT = [Res("xT%d" % k) for k in range(2)]
            qaz = [sb(sa, "qaz%d" % k, [128, 1024], BF16) for k in range(2)]
            qbz = [sb(sa, "qbz%d" % k, [128, 1024], BF16) for k in range(3)]
            qiz = [sb(sa, "qiz%d" % k, [128, 1024], BF16) for k in range(2)]
            R_qa = [Res("qa%d" % k) for k in range(2)]
            R_qb = [Res("qb%d" % k) for k in range(3)]
            R_qi = [Res("qi%d" % k) for k in range(2)]
            coef = [sb(sa, "coef%d" % k, [128, 8], F32) for k in range(2)]
            R_coef = [Res("coef%d" % k) for k in range(2)]
            dg = [sb(sa, "dg%d" % k, [128, 1024], BF16) for k in range(2)]
            R_dg = [Res("dg%d" % k) for k in range(2)]
            PTA = [sb(sa, "PTA%d" % k, [128, 512], BF16) for k in range(3)]
            R_PTA = [Res("PTA%d" % k) for k in range(3)]
            PTB = [sb(sa, "PTB%d" % k, [128, 512], BF16) for k in range(3)]
            R_PTB = [Res("PTB%d" % k) for k in range(3)]
            mixb = [sb(sa, "mixb%d" % k, [128, 1024], BF16) for k in range(3)]
            R_mix = [Res("mix%d" % k) for k in range(3)]
            ostg = [sb(sa, "ostg%d" % k, [128, 256], F32) for k in range(2)]
            R_ostg = [Res("ostg%d" % k) for k in range(2)]
            vbstg = [sb(sa, "vbstg%d" % k, [128, 128], F32) for k in range(2)]
            R_vbstg = [Res("vbstg%d" % k) for k in range(2)]
            astg = sb(sa, "astg", [128, 1024], F32)
            R_astg = Res("astg")
            small = [sb(sa, "small%d" % k, [128, 16], F32) for k in range(2)]
            R_small = [Res("small%d" % k) for k in range(2)]
            recA = [sb(sa, "recA%d" % k, [128, 8], F32) for k in range(2)]
            R_recA = [Res("recA%d" % k) for k in range(2)]
            recB = [sb(sa, "recB%d" % k, [128, 8], F32) for k in range(2)]
            R_recB = [Res("recB%d" % k) for k in range(2)]
            colmask = sb(sa, "colmask", [128, 1], F32)
            diagm = sb(sa, "diagm", [128, 128], F32)
            kvalid = sb(sa, "kvalid", [128, NT], F32)
            c15 = sb(sa, "c15", [128, 8], F32)
            ones8 = sb(sa, "ones8", [128, 8], F32)
            R_cst = Res("cst")

            wrot = Rot([0, 1, 2])

            P.dma("sp", colmask[:, :], I["colmask"][:, :], writes=[R_cst])
            P.dma("sp", diagm[:, :], I["diagmask"][:, :], writes=[R_cst])
            P.dma("sp", kvalid[:, :], I["kvalid"][:, :], writes=[R_cst])
            P.dma("sp", c15[:, :], I["C15"][:, :], writes=[R_cst])
            P.op("pool", _call("memset", ones8[:, :], 1.0), writes=[R_cst])
            for k in range(2):
                P.op("pool", _call("memset", qaz[k][:, :], 0.0), writes=[R_qa[k]])
                P.op("pool", _call("memset", qiz[k][:, :], 0.0), writes=[R_qi[k]])
            for k in range(3):
                P.op("pool", _call("memset", qbz[k][:, :], 0.0), writes=[R_qb[k]])

            HW = NCOL // 2
            for kc in range(KC):
                for hh in range(2):
                    stg, R_stg = score[hh], R_score[hh]
                    P.dma("sp", stg[:, 0:HW], I["win"][:, kc * NCOL + hh * HW: kc * NCOL + (hh + 1) * HW], writes=[R_stg])
                    if hh == 0:
                        P.op("act", _call("activation", out=winb[:, kc * NCOL + hh * HW: kc * NCOL + (hh + 1) * HW], in_=stg[:, 0:HW], func=AF.Copy),
                             reads=[R_stg], writes=[R_win])
                    else:
                        P.op("dve", _call("tensor_copy", out=winb[:, kc * NCOL + hh * HW: kc * NCOL + (hh + 1) * HW], in_=stg[:, 0:HW]),
                             reads=[R_stg], writes=[R_win])
            for hh in range(2):
                w = 4 * ABW
                P.dma("sp", score[hh][:, 0:w], I["AB"][:, hh * w:(hh + 1) * w], writes=[R_score[hh]])
                P.op("act", _call("activation", out=ABb[:, hh * w:(hh + 1) * w], in_=score[hh][:, 0:w], func=AF.Copy),
                     reads=[R_score[hh]], writes=[R_AB])
            P.dma("sp", score[0][:, 0:8 * BNW], I["Bn"][:, :], writes=[R_score[0]])
            for h in range(8):
                P.op("dve", _call("tensor_scalar", out=Bnb[:, h * BNW:(h + 1) * BNW], in0=score[0][:, h * BNW:(h + 1) * BNW],
                                  scalar1=c15[:, h:h + 1], scalar2=None, op0=ALU.subtract),
                     reads=[R_score[0], R_cst], writes=[R_Bn])
            checkpoint('consts')

            def win_cols(kc, c0, n):
                return winb[:, kc * NCOL + c0: kc * NCOL + c0 + n]

            def fm_proj(bank, xT, R_x, N, col0, nchunks, ocol=0):
                for j in range(nchunks):
                    for kc in range(KC):
                        P.op("pe", _call("matmul", out=pb[bank][:, ocol + j * N: ocol + (j + 1) * N], lhsT=win_cols(kc, col0 + j * 128, 128),
                                         rhs=xT[:, kc * N:(kc + 1) * N], start=(kc == 0), stop=(kc == KC - 1)),
                             reads=[R_win, R_x], writes=[R_pb[bank]])

            def tm_proj(bank, xT, R_x, N, col0, ncols, ocol=0):
                for kc in range(KC):
                    P.op("pe", _call("matmul", out=pb[bank][0:N, ocol:ocol + ncols], lhsT=xT[:, kc * N:(kc + 1) * N],
                                     rhs=win_cols(kc, col0, ncols), start=(kc == 0), stop=(kc == KC - 1)),
                         reads=[R_win, R_x], writes=[R_pb[bank]])

            def load_xT(r, eng="pool"):
                s = r % 2
                P.dma("sp", xstg2[s][:, :], I["xkT"][r], writes=[R_xstg2[s]])
                P.op(eng, _call("tensor_copy", out=xTb[s][:, :], in_=xstg2[s][:, :]), reads=[R_xstg2[s]], writes=[R_xT[s]])

            def kside(r, full):
                s = r % 2
                xT, R_x = xTb[s], R_xT[s]
                so = r % 2
                bk = wrot.next()
                fm_proj(bk, xT, R_x, 128, C_KB, 1)
                fm_proj(bk, xT, R_x, 128, C_KI, 1, ocol=128)
                P.op("act", _call("activation", out=ostg[so][:, :], in_=pb[bk][:, 0:256], func=AF.Copy), reads=[R_pb[bk]], writes=[R_ostg[so]])
                P.op("pool", _call("tensor_copy", out=kbi[:, :].rearrange("p (a c) -> p a c", a=2)[:, :, r * 128:(r + 1) * 128],
                                   in_=ostg[so][:, :].rearrange("p (a c) -> p a c", a=2)),
                     reads=[R_ostg[so]], writes=[R_kbi[r], R_ki[r]])
                P.dma("sp", O["bkT"][:, r * 128:(r + 1) * 128], ostg[so][:, 0:128], reads=[R_ostg[so]], defer=True)
                P.dma("sp", O["biT"][:, r * 128:(r + 1) * 128], ostg[so][0:64, 128:256], reads=[R_ostg[so]], defer=True)
                yield 3.0
                bv_ = wrot.next()
                tm_proj(bv_, xT, R_x, 128, C_VB, 128)
                vbv = vb_aug[:, r * 130:(r + 1) * 130].rearrange("p (g d) -> p g d", d=65)
                P.op("act", _call("activation", out=vbstg[so][:, :], in_=pb[bv_][:, 0:128], func=AF.Copy), reads=[R_pb[bv_]], writes=[R_vbstg[so]])
                P.op("pool", _call("tensor_copy", out=vbv[:, :, 0:64], in_=vbstg[so][:, :].rearrange("p (g d) -> p g d", d=64)),
                     reads=[R_vbstg[so]], writes=[R_vb[r]])
                P.op("pool", _call("tensor_scalar", out=vbv[:, :, 64:65], in0=ones8[:, 0:2].rearrange("p (g o) -> p g o", o=1),
                                   scalar1=kvalid[:, r:r + 1], scalar2=None, op0=ALU.mult),
                     reads=[R_cst], writes=[R_vb[r]])
                P.dma("sp", O["bv"][r * 128:(r + 1) * 128, :], vbstg[so][:, :], reads=[R_vbstg[so]], defer=True)
                yield 3.0
                if not full:
                    return
                slot = r % 6
                ba = wrot.next()
                fm_proj(ba, xT, R_x, 128, C_KA, 4)
                P.op("act", _call("activation", out=kaT[:, slot * 512:(slot + 1) * 512], in_=pb[ba][:, :], func=AF.Copy),
                     reads=[R_pb[ba]], writes=[R_ka[slot]])
                if r >= 28:
                    P.op("dve", _call("tensor_copy", out=astg[:, 0:512], in_=pb[ba][:, :]), reads=[R_pb[ba]], writes=[R_astg])
                    P.dma("sp", O["akT"].rearrange("p (j t) -> p j t", t=512)[:, :, (r - 28) * 128:(r - 27) * 128],
                          astg[:, 0:512].rearrange("p (j t) -> p j t", t=128), reads=[R_astg], defer=True)
                yield 3.0
                bva = wrot.next()
                tm_proj(bva, xT, R_x, 128, C_VA, 512)
                vav = va_aug[:, slot * 520:(slot + 1) * 520].rearrange("p (h d) -> p h d", d=65)
                P.op("act", _call("activation", out=vav[:, :, 0:64], in_=pb[bva][:, :].rearrange("p (h d) -> p h d", d=64), func=AF.Copy),
                     reads=[R_pb[bva]], writes=[R_va[slot]])
                P.op("pool", _call("tensor_scalar", out=vav[:, :, 64:65], in0=ones8[:, :].rearrange("p (h o) -> p h o", o=1),
                                   scalar1=kvalid[:, r:r + 1], scalar2=None, op0=ALU.mult),
                     reads=[R_cst], writes=[R_va[slot]])
                if r >= 28:
                    P.op("dve", _call("tensor_copy", out=astg[:, 512:1024], in_=pb[bva][:, :]), reads=[R_pb[bva]], writes=[R_astg])
                    P.dma("sp", O["av"][(r - 28) * 128:(r - 27) * 128, :], astg[:, 512:1024], reads=[R_astg], defer=True)
                yield 3.0

            def qside(xT, R_x, qs, st, st3):
                b1 = wrot.next()
                fm_proj(b1, xT, R_x, qs, C_QA, 4)
                for hf in range(2):
                    P.op("act", _call("activation",
                                      out=qaz[st][hf * 64:(hf + 1) * 64, 0:8 * qs].rearrange("p (j two q) -> p j two q", two=2, q=qs)[:, :, hf, :],
                                      in_=pb[b1][hf * 64:(hf + 1) * 64, 0:4 * qs].rearrange("p (j q) -> p j q", q=qs), func=AF.Copy, scale=0.125),
                         reads=[R_pb[b1]], writes=[R_qa[st]])
                yield 3.0
                b2 = wrot.next()
                fm_proj(b2, xT, R_x, qs, C_QB, 4)
                for g in range(2):
                    P.op("act", _call("activation", out=qbz[st3][g * 64:(g + 1) * 64, g * 4 * qs:(g + 1) * 4 * qs],
                                      in_=pb[b2][g * 64:(g + 1) * 64, 0:4 * qs], func=AF.Copy, scale=0.125),
                         reads=[R_pb[b2]], writes=[R_qb[st3]])
                yield 3.0
                b3 = wrot.next()
                fm_proj(b3, xT, R_x, qs, C_QI, 4)
                for hf in range(2):
                    P.op("act", _call("activation",
                                      out=qiz[st][hf * 64:(hf + 1) * 64, 0:8 * qs].rearrange("p (j two q) -> p j two q", two=2, q=qs)[:, :, hf, :],
                                      in_=pb[b3][hf * 64:(hf + 1) * 64, 0:4 * qs].rearrange("p (j q) -> p j q", q=qs), func=AF.Copy),
                         reads=[R_pb[b3]], writes=[R_qi[st]])
                b4 = wrot.next()
                tm_proj(b4, xT, R_x, qs, C_WI, 8)
                P.op("dve", _call("tensor_scalar", out=coef[st][0:qs, :], in0=pb[b4][0:qs, 0:8], scalar1=float(8.0 ** -1.5), scalar2=None, op0=ALU.mult),
                     reads=[R_pb[b4]], writes=[R_coef[st]])
                for h in range(8):
                    P.op("pool", _call("tensor_scalar", out=dg[st][0:qs, h * 128: h * 128 + qs], in0=ident_f[0:qs, 0:qs],
                                       scalar1=coef[st][0:qs, h:h + 1], scalar2=None, op0=ALU.mult),
                         reads=[R_coef[st], R_ident], writes=[R_dg[st]])
                yield 3.0

            def normalize(bank, qs, mixt, R_m, col0, rec, R_rec):
                ov = pb[bank][0:qs, 0:260].rearrange("p (h d) -> p h d", d=65)
                P.op("dve", _call("tensor_scalar", out=rec[0:qs, 0:4].rearrange("p (h o) -> p h o", o=1), in0=ov[:, :, 64:65],
                                  scalar1=1e-30, scalar2=None, op0=ALU.max),
                     reads=[R_pb[bank]], writes=[R_rec])
                P.op("dve", _call("reciprocal", out=rec[0:qs, 0:4], in_=rec[0:qs, 0:4]), reads=[R_rec], writes=[R_rec])
                for hh in range(4):
                    P.op("dve", _call("tensor_scalar", out=mixt[0:qs, col0 + hh * 64: col0 + (hh + 1) * 64],
                                      in0=pb[bank][0:qs, hh * 65: hh * 65 + 64],
                                      scalar1=rec[0:qs, hh:hh + 1], scalar2=None, op0=ALU.mult),
                         reads=[R_pb[bank], R_rec], writes=[R_m])

            def pipe3(items, s1, s2, s3, D, cost=1.0):
                pend = []
                for it in items:
                    s1(it)
                    s2(it)
                    pend.append(it)
                    if len(pend) > D:
                        s3(pend.pop(0))
                    yield cost
                while pend:
                    s3(pend.pop(0))
                    yield cost

            pta_rot = Rot([0, 1, 2])
            relu_rot = Rot([0, 1, 2])
            ptb_rot = Rot([0, 1, 2])
            brot = Rot([3, 7])

            def front_attn(sn, qs, wins, btiles, prompt_masks, abw):
                st = sn % 2
                mixt, R_m = mixb[sn % 3], R_mix[sn % 3]
                nw = len(wins)

                units = []
                for h in range(8):
                    units.append({"h": h, "t0": 0, "tiles": wins[0:4]})
                    if nw > 4:
                        units.append({"h": h, "t0": 4, "tiles": wins[4:5]})

                def a1(u):
                    h = u["h"]
                    j = h // 2
                    bank = wrot.next()
                    u["bank"] = bank
                    for i, (slot, ts) in enumerate(u["tiles"]):
                        t = u["t0"] + i
                        c0 = i * qs
                        P.op("pe", _call("matmul", out=pb[bank][0:ts, c0:c0 + qs], lhsT=kaT[:, slot * 512 + j * 128: slot * 512 + j * 128 + ts],
                                         rhs=qaz[st][:, h * qs:(h + 1) * qs], start=True, stop=False),
                             reads=[R_ka[slot], R_qa[st]], writes=[R_pb[bank]])
                        P.op("pe", _call("matmul", out=pb[bank][0:ts, c0:c0 + qs], lhsT=ABb[0:qs, h * abw + t * 128: h * abw + t * 128 + ts],
                                         rhs=ident[0:qs, 0:qs], start=False, stop=True),
                             reads=[R_AB, R_ident], writes=[R_pb[bank]])

                def a2(u):
                    k = pta_rot.next()
                    u["pt"], u["R_pt"] = PTA[k], R_PTA[k]
                    bank = u["bank"]
                    tsm = max(ts for (_, ts) in u["tiles"])
                    n = len(u["tiles"])
                    P.op("act", _call("activation", out=u["pt"][0:tsm, 0:n * qs], in_=pb[bank][0:tsm, 0:n * qs], func=AF.Exp),
                         reads=[R_pb[bank]], writes=[u["R_pt"]])

                def a3(u):
                    h = u["h"]
                    last_unit = (u["t0"] + len(u["tiles"]) == nw)
                    for i, (slot, ts) in enumerate(u["tiles"]):
                        t = u["t0"] + i
                        P.op("pe", _call("matmul", out=pb[4][0:qs, (h % 4) * 65:(h % 4) * 65 + 65], lhsT=u["pt"][0:ts, i * qs:(i + 1) * qs],
                                         rhs=va_aug[0:ts, slot * 520 + h * 65: slot * 520 + h * 65 + 65],
                                         start=(h % 4 == 0 and t == 0), stop=(t == nw - 1), skip_group_check=True),
                             reads=[u["R_pt"], R_va[slot]], writes=[R_pb[4]])
                    if last_unit and h % 4 == 3:
                        normalize(4, qs, mixt, R_m, (h // 4) * 256, recA[st], R_recA[st])

                yield from pipe3(units, a1, a2, a3, 2, 0.9)

                L = btiles[-1][1] + btiles[-1][2]
                items = []
                cc = 0
                for c0 in range(0, L, 512):
                    w = min(512, L - c0)
                    rk = [R_ki[tt[0]] for tt in btiles if tt[1] >= c0 - 127 and tt[1] < c0 + w]
                    for h in range(8):
                        items.append({"c0": c0, "w": w, "h": h, "sc": (5, 4)[cc % 2], "rk": rk})
                    cc += 1

                def i1(it):
                    bank = wrot.next()
                    it["bank"] = bank
                    h, c0, w = it["h"], it["c0"], it["w"]
                    P.op("pe", _call("matmul", out=pb[bank][0:qs, 0:w], lhsT=qiz[st][:, h * qs:(h + 1) * qs],
                                     rhs=kbi[:, 4096 + c0: 4096 + c0 + w], start=True, stop=True),
                         reads=[R_qi[st]] + it["rk"], writes=[R_pb[bank]])

                def i2(it):
                    k = relu_rot.next()
                    it["rl"], it["R_rl"] = relu[k], R_relu[k]
                    w = it["w"]
                    P.op("act", _call("activation", out=it["rl"][0:qs, 0:w], in_=pb[it["bank"]][0:qs, 0:w], func=AF.Relu),
                         reads=[R_pb[it["bank"]]], writes=[it["R_rl"]])

                def i3(it):
                    h, c0, w, sc = it["h"], it["c0"], it["w"], it["sc"]
                    P.op("pe", _call("matmul", out=pb[sc][0:qs, 0:w], lhsT=dg[st][0:qs, h * 128: h * 128 + qs], rhs=it["rl"][0:qs, 0:w],
                                     start=(h == 0), stop=(h == 7)),
                         reads=[R_dg[st], it["R_rl"]], writes=[R_pb[sc]])
                    if h == 7:
                        if prompt_masks and c0 < 2048:
                            wm = min(w, 2048 - c0)
                            P.op("act", _call("activation", out=score[st][0:qs, c0:c0 + wm], in_=pb[sc][0:qs, 0:wm], func=AF.Identity,
                                              bias=colmask[0:qs, 0:1]),
                                 reads=[R_pb[sc], R_cst], writes=[R_score[st]])
                            if wm < w:
                                P.op("act", _call("activation", out=score[st][0:qs, c0 + wm:c0 + w], in_=pb[sc][0:qs, wm:w], func=AF.Copy),
                                     reads=[R_pb[sc]], writes=[R_score[st]])
                        else:
                            P.op("act", _call("activation", out=score[st][0:qs, c0:c0 + w], in_=pb[sc][0:qs, 0:w], func=AF.Copy),
                                 reads=[R_pb[sc]], writes=[R_score[st]])

                yield from pipe3(items, i1, i2, i3, 2, 0.65)
                if prompt_masks:
                    P.op("dve", _call("tensor_tensor", out=score[st][0:qs, L - 128:L], in0=score[st][0:qs, L - 128:L], in1=diagm[0:qs, :], op=ALU.add),
                         reads=[R_score[st], R_cst], writes=[R_score[st]])
                yield

            def bis_gen(sn, qs, btiles, bnw):
                st = sn % 2
                sm, R_sm = small[st], R_small[st]
                L = btiles[-1][1] + btiles[-1][2]
                P.op("dve", _call("memset", sm[0:qs, 1:2], 0.0), writes=[R_sm])
                for k in range(NIT):
                    wk = BIS_W0 / (2.0 ** k)
                    P.op("dve", _call("tensor_scalar", out=Mb[st][0:qs, 0:L], in0=score[st][0:qs, 0:L], scalar1=sm[0:qs, 1:2], scalar2=None,
                                      op0=ALU.is_ge, op1=ALU.add, accum_out=sm[0:qs, 0:1]),
                         reads=[R_score[st], R_sm], writes=[R_M[st], R_sm])
                    P.op("dve", _call("tensor_scalar", out=sm[0:qs, 2:3], in0=sm[0:qs, 0:1], scalar1=255.5, scalar2=wk,
                                      op0=ALU.is_ge, op1=ALU.mult),
                         reads=[R_sm], writes=[R_sm])
                    P.op("dve", _call("scalar_tensor_tensor", out=sm[0:qs, 1:2], in0=sm[0:qs, 2:3], scalar=-wk / 2.0,
                                      in1=sm[0:qs, 1:2], op0=ALU.add, op1=ALU.add),
                         reads=[R_sm], writes=[R_sm])
                    yield L * 1.08e-3 + 0.5
                wl = BIS_W0 / (2.0 ** (NIT - 1)) / 2.0
                P.op("dve", _call("tensor_scalar", out=sm[0:qs, 3:4], in0=sm[0:qs, 1:2], scalar1=-wl, scalar2=None, op0=ALU.add),
                     reads=[R_sm], writes=[R_sm])
                P.op("dve", _call("tensor_scalar", out=Mb[st][0:qs, 0:L], in0=score[st][0:qs, 0:L], scalar1=sm[0:qs, 3:4], scalar2=NEGM,
                                  op0=ALU.is_lt, op1=ALU.mult),
                     reads=[R_score[st], R_sm], writes=[R_M[st]])
                nearw = btiles[-2][2] + btiles[-1][2]
                for h in range(8):
                    P.op("dve", _call("tensor_tensor", out=Mnear[st][0:qs, h * bnw: h * bnw + nearw], in0=Bnb[0:qs, h * bnw: h * bnw + nearw],
                                      in1=Mb[st][0:qs, L - nearw:L], op=ALU.add),
                         reads=[R_Bn, R_M[st]], writes=[R_Mnear[st]])
                yield
            def battn_gen(sn, qs, btiles, blk, bnw):
                st = sn % 2
                st3 = sn % 3
                mixt, R_m = mixb[st3], R_mix[st3]
                nb = len(btiles)
                items = [{"g": g, "t": t, "vt": vt, "c0": c0, "ts": ts} for g in range(2) for t, (vt, c0, ts) in enumerate(btiles)]

                def b1(it):
                    g, t, vt, c0, ts = it["g"], it["t"], it["vt"], it["c0"], it["ts"]
                    bank = brot.next()
                    it["bank"] = bank
                    P.op("pe", _call("matmul", out=pb[bank][0:ts, 0:4 * qs], lhsT=kbi[:, c0:c0 + ts],
                                     rhs=qbz[st3][:, g * 4 * qs:(g + 1) * 4 * qs], start=True, stop=False),
                         reads=[R_kbi[vt], R_qb[st3]], writes=[R_pb[bank]])
                    if t < nb - 2 and qs == 128:
                        P.op("pe", _call("matmul", out=pb[bank][0:ts, 0:512], lhsT=Mb[st][0:qs, c0:c0 + ts], rhs=ident[0:128, 0:512],
                                         start=False, stop=True),
                             reads=[R_M[st], R_ident], writes=[R_pb[bank]])
                    elif t < nb - 2:
                        for r in range(4):
                            P.op("pe", _call("matmul", out=pb[bank][0:ts, r * qs:(r + 1) * qs], lhsT=Mb[st][0:qs, c0:c0 + ts],
                                             rhs=ident[0:qs, 0:qs], start=False, stop=(r == 3)),
                                 reads=[R_M[st], R_ident], writes=[R_pb[bank]])
                    else:
                        tt = t - (nb - 2)
                        for r in range(4):
                            hh = g * 4 + r
                            P.op("pe", _call("matmul", out=pb[bank][0:ts, r * qs:(r + 1) * qs],
                                             lhsT=Mnear[st][0:qs, hh * bnw + tt * 128: hh * bnw + tt * 128 + ts], rhs=ident[0:qs, 0:qs],
                                             start=False, stop=(r == 3)),
                                 reads=[R_Mnear[st], R_ident], writes=[R_pb[bank]])

                def b2(it):
                    k = ptb_rot.next()
                    it["ptb"], it["R_ptb"] = PTB[k], R_PTB[k]
                    ts = it["ts"]
                    P.op("act", _call("activation", out=it["ptb"][0:ts, 0:4 * qs], in_=pb[it["bank"]][0:ts, 0:4 * qs], func=AF.Exp),
                         reads=[R_pb[it["bank"]]], writes=[it["R_ptb"]])

                def b3(it):
                    g, t, vt, ts = it["g"], it["t"], it["vt"], it["ts"]
                    for r in range(4):
                        P.op("pe", _call("matmul", out=pb[6][0:qs, r * 65: r * 65 + 65], lhsT=it["ptb"][0:ts, r * qs:(r + 1) * qs],
                                         rhs=vb_aug[0:ts, (vt * 2 + g) * 65:(vt * 2 + g) * 65 + 65],
                                         start=(t == 0 and r == 0), stop=(t == nb - 1), skip_group_check=True),
                             reads=[it["R_ptb"], R_vb[vt]], writes=[R_pb[6]])
                    if t == nb - 1:
                        normalize(6, qs, mixt, R_m, 512 + g * 256, recB[st], R_recB[st])

                yield from pipe3(items, b1, b2, b3, 1, 0.8)
                P.dma("sp", mixD[blk * 128: blk * 128 + qs, :], mixt[0:qs, :], reads=[R_m], writes=[R_mixD[blk]], defer=True)
                yield

            load_xT(0, "dve")
            for r in range(16):
                if r + 1 < 16:
                    load_xT(r + 1, "dve")
                for _ in kside(r, full=(r >= 11)):
                    pass
            checkpoint('phase0')

            def prompt_front(sn, T):
                if T >= 16:
                    load_xT(T)
                    yield from kside(T, full=True)
                s = T % 2
                yield from qside(xTb[s], R_xT[s], 128, sn % 2, sn % 3)
                wins = [((T - 4 + t) % 6, 128) for t in range(5)]
                btiles = [(t, t * 128, 128) for t in range(T + 1)]
                yield from front_attn(sn, 128, wins, btiles, True, ABW)

            def prompt_bis(sn, T):
                btiles = [(t, t * 128, 128) for t in range(T + 1)]
                yield from bis_gen(sn, 128, btiles, BNW)

            def prompt_battn(sn, T, blk):
                btiles = [(t, t * 128, 128) for t in range(T + 1)]
                yield from battn_gen(sn, 128, btiles, blk, BNW)

            steps = [(0, 15, 16)] + [(1 + i, 16 + i, i) for i in range(16)]
            ns = len(steps)
            SN = ns
            sst = SN % 2
            s_wins = [(0, 128), (1, 128), (2, 128), (3, 128), (4, 16)]
            s_btiles = [(t, t * 128, 128) for t in range(16)] + [(16, 2048, 16)]
            xs_, R_xs = xTb[0], R_xT[0]

            def sample_front():
                stg, R_stg = score[sst], R_score[sst]
                P.dma("sp", stg[:, 0:2048], I["cbiT"][:, :], writes=[R_stg])
                P.op("act", _call("activation", out=kbi[:, 4096:4096 + 2048], in_=stg[:, 0:2048], func=AF.Copy),
                     reads=[R_stg], writes=R_ki[0:16])
                P.dma("sp", stg[:, 2048:4096], I["cakT"][:, :], writes=[R_stg])
                for s4 in range(4):
                    P.op("act", _call("activation", out=kaT[:, s4 * 512:(s4 + 1) * 512].rearrange("p (j t) -> p j t", t=128),
                                      in_=stg[:, 2048:4096].rearrange("p (j t) -> p j t", t=512)[:, :, s4 * 128:(s4 + 1) * 128], func=AF.Copy),
                         reads=[R_stg], writes=[R_ka[s4]])
                yield 3.0
                P.dma("sp", stg[:, 0:2048].rearrange("p (t c) -> p t c", c=512), I["cav"].rearrange("(t p) c -> p t c", p=128), writes=[R_stg])
                vaall = va_aug[:, 0:4 * 520].rearrange("p (t d) -> p t d", d=65)
                P.op("act", _call("activation", out=vaall[:, :, 0:64], in_=stg[:, 0:2048].rearrange("p (t d) -> p t d", d=64), func=AF.Copy),
                     reads=[R_stg], writes=R_va[0:5])
                P.op("pool", _call("memset", va_aug[:, 0:5 * 520].rearrange("p (t d) -> p t d", d=65)[:, :, 64:65], 1.0), writes=R_va[0:5])
                for hh in range(2):
                    w = 4 * 528
                    P.dma("sp", stg[0:16, 0:w], I["ABs"][:, hh * w:(hh + 1) * w], writes=[R_stg])
                    P.op("act", _call("activation", out=ABb[0:16, hh * w:(hh + 1) * w], in_=stg[0:16, 0:w], func=AF.Copy),
                         reads=[R_stg], writes=[R_AB])
                P.op("pool", _call("memset", qaz[sst][:, :], 0.0), writes=[R_qa[sst]])
                P.op("pool", _call("memset", qbz[SN % 3][:, :], 0.0), writes=[R_qb[SN % 3]])
                P.op("pool", _call("memset", qiz[sst][:, :], 0.0), writes=[R_qi[sst]])
                P.dma("sp", xstg[:, 0:128], I["xsT"][:, :], writes=[R_xstg])
                P.op("pool", _call("tensor_copy", out=xTb[0][:, 0:128], in_=xstg[:, 0:128]), reads=[R_xstg], writes=[R_xT[0]])
                yield 3.0
                bk = wrot.next()
                fm_proj(bk, xs_, R_xs, 16, C_KI, 1)
                P.op("act", _call("activation", out=kbi[:, 4096 + 2048:4096 + 2064], in_=pb[bk][:, 0:16], func=AF.Copy), reads=[R_pb[bk]], writes=[R_ki[16]])
                P.op("dve", _call("tensor_copy", out=ostg[0][:, 16:32], in_=pb[bk][:, 0:16]), reads=[R_pb[bk]], writes=[R_ostg[0]])
                P.dma("sp", O["sbiT"][:, :], ostg[0][0:64, 16:32], reads=[R_ostg[0]], defer=True)
                ba = wrot.next()
                fm_proj(ba, xs_, R_xs, 16, C_KA, 4)
                P.op("act", _call("activation", out=kaT[:, 4 * 512:5 * 512].rearrange("p (j t) -> p j t", t=128)[:, :, 0:16],
                                  in_=pb[ba][:, 0:64].rearrange("p (j t) -> p j t", t=16), func=AF.Copy),
                     reads=[R_pb[ba]], writes=[R_ka[4]])
                P.op("dve", _call("tensor_copy", out=astg[:, 0:64], in_=pb[ba][:, 0:64]), reads=[R_pb[ba]], writes=[R_astg])
                P.dma("sp", O["sakT"][:, :], astg[:, 0:64], reads=[R_astg], defer=True)
                bva = wrot.next()
                tm_proj(bva, xs_, R_xs, 16, C_VA, 512)
                vav = va_aug[0:16, 4 * 520:5 * 520].rearrange("p (h d) -> p h d", d=65)
                P.op("act", _call("activation", out=vav[:, :, 0:64], in_=pb[bva][0:16, :].rearrange("p (h d) -> p h d", d=64), func=AF.Copy),
                     reads=[R_pb[bva]], writes=[R_va[4]])
                P.op("dve", _call("tensor_copy", out=astg[0:16, 512:1024], in_=pb[bva][0:16, :]), reads=[R_pb[bva]], writes=[R_astg])
                P.dma("sp", O["sav"][:, :], astg[0:16, 512:1024], reads=[R_astg], defer=True)
                yield 3.0
                yield from qside(xs_, R_xs, 16, sst, SN % 3)
                yield from front_attn(SN, 16, s_wins, s_btiles, False, 528)

            def sample_bis():
                stg, R_stg = score[1 - sst], R_score[1 - sst]
                P.dma("sp", stg[0:16, 0:8 * 144], I["Bns"][:, :], writes=[R_stg])
                for h in range(8):
                    P.op("dve", _call("tensor_scalar", out=Bnb[0:16, h * 144:(h + 1) * 144], in0=stg[0:16, h * 144:(h + 1) * 144],
                                      scalar1=c15[0:16, h:h + 1], scalar2=None, op0=ALU.subtract),
                         reads=[R_stg, R_cst], writes=[R_Bn])
                yield 1.0
                yield from bis_gen(SN, 16, s_btiles, 144)

            def sample_battn():
                stg, R_stg = score[1 - sst], R_score[1 - sst]
                P.dma("sp", stg[:, 0:2048], I["cbkT"][:, :], writes=[R_stg])
                P.op("act", _call("activation", out=kbi[:, 0:2048], in_=stg[:, 0:2048], func=AF.Copy),
                     reads=[R_stg], writes=R_kbi[0:16])
                P.dma("sp", stg[:, 2048:4096].rearrange("p (t c) -> p t c", c=128), I["cbv"].rearrange("(t p) c -> p t c", p=128), writes=[R_stg])
                vball = vb_aug[:, 0:16 * 130].rearrange("p (t d) -> p t d", d=65)
                P.op("act", _call("activation", out=vball[:, :, 0:64], in_=stg[:, 2048:4096].rearrange("p (t d) -> p t d", d=64), func=AF.Copy),
                     reads=[R_stg], writes=R_vb[0:17])
                P.op("pool", _call("memset", vb_aug[:, 0:17 * 130].rearrange("p (t d) -> p t d", d=65)[:, :, 64:65], 1.0), writes=R_vb[0:17])
                bk = wrot.next()
                fm_proj(bk, xs_, R_xs, 16, C_KB, 1)
                P.op("act", _call("activation", out=kbi[:, 2048:2064], in_=pb[bk][:, 0:16], func=AF.Copy), reads=[R_pb[bk]], writes=[R_kbi[16]])
                P.op("dve", _call("tensor_copy", out=ostg[1][:, 0:16], in_=pb[bk][:, 0:16]), reads=[R_pb[bk]], writes=[R_ostg[1]])
                P.dma("sp", O["sbkT"][:, :], ostg[1][:, 0:16], reads=[R_ostg[1]], defer=True)
                bv_ = wrot.next()
                tm_proj(bv_, xs_, R_xs, 16, C_VB, 128)
                vbv = vb_aug[0:16, 16 * 130:17 * 130].rearrange("p (g d) -> p g d", d=65)
                P.op("act", _call("activation", out=vbv[:, :, 0:64], in_=pb[bv_][0:16, 0:128].rearrange("p (g d) -> p g d", d=64), func=AF.Copy),
                     reads=[R_pb[bv_]], writes=[R_vb[16]])
                P.op("dve", _call("tensor_copy", out=vbstg[0][0:16, :], in_=pb[bv_][0:16, 0:128]), reads=[R_pb[bv_]], writes=[R_vbstg[0]])
                P.dma("sp", O["sbv"][:, :], vbstg[0][0:16, :], reads=[R_vbstg[0]], defer=True)
                yield 3.0
                yield from battn_gen(SN, 16, s_btiles, 17, 144)

            for tick in range(ns + 3):
                gens = []
                if 0 <= tick - 2 < ns:
                    gens.append(prompt_battn(*steps[tick - 2]))
                elif tick - 2 == ns:
                    gens.append(sample_battn())
                if 0 <= tick - 1 < ns:
                    gens.append(prompt_bis(*steps[tick - 1][0:2]))
                elif tick - 1 == ns:
                    gens.append(sample_bis())
                if tick < ns:
                    gens.append(prompt_front(*steps[tick][0:2]))
                elif tick == ns:
                    gens.append(sample_front())
                run_interleaved(gens)
            checkpoint('steps')
            checkpoint('phaseA')
            P.flush(block)

        P.barrier()
        with ExitStack() as sbk:
            wob = sb(sbk, "wob", [128, 8 * 1024], BF16)
            wmqb = sb(sbk, "wmqb", [128, 8 * 512], BF16)
            wmob = sb(sbk, "wmob", [128, 4 * 1024], BF16)
            wtmp = sb(sbk, "wtmp", [128, 8 * 512], BF16)
            R_wo, R_wmq, R_wmo, R_wtmp = Res("wo"), Res("wmq"), Res("wmo"), Res("wtmp")
            wst = [sb(sbk, "wst%d" % k, [128, 2048], F32) for k in range(2)]
            R_wst = [Res("wst%d" % k) for k in range(2)]
            lnt = sb(sbk, "lnt", [128, 4 * 1024], F32)
            R_ln = Res("ln")
            memTb = sb(sbk, "memTb", [128, 8 * 256], BF16)
            R_memT = Res("memT")
            mkT = [sb(sbk, "mkT%d" % k, [128, 4 * 256], BF16) for k in range(2)]
            mva = [sb(sbk, "mva%d" % k, [128, 2 * 4 * 129], BF16) for k in range(2)]
            R_mk = [Res("mk%d" % k) for k in range(2)]
            R_mv = [Res("mv%d" % k) for k in range(2)]
            mixl = [sb(sbk, "mixl%d" % k, [128, 1024], BF16) for k in range(4)]
            R_mixl = [Res("mixl%d" % k) for k in range(4)]
            xr = [sb(sbk, "xr%d" % k, [128, 1024], F32) for k in range(4)]
            R_xr = [Res("xr%d" % k) for k in range(4)]
            NB3 = 4
            tT_l = [sb(sbk, "tT%d" % k, [128, 1024], BF16) for k in range(NB3)]
            hA_l = [sb(sbk, "hA%d" % k, [128, 1024], F32) for k in range(NB3)]
            hB_l = [sb(sbk, "hB%d" % k, [128, 1024], F32) for k in range(NB3)]
            h16_l = [sb(sbk, "h16%d" % k, [128, 1024], BF16) for k in range(NB3)]
            qmT_l = [sb(sbk, "qmT%d" % k, [128, 512], BF16) for k in range(NB3)]
            PTm_l = [sb(sbk, "PTm%d" % k, [128, 1024], BF16) for k in range(NB3)]
            o16_l = [sb(sbk, "o16%d" % k, [128, 512], BF16) for k in range(NB3)]
            oT_l = [sb(sbk, "oT%d" % k, [128, 512], BF16) for k in range(NB3)]
            stat_l = [sb(sbk, "stat%d" % k, [128, 32], F32) for k in range(NB3)]
            RB = [{n: Res(n + str(k)) for n in ("tT", "hA", "hB", "h16", "qm", "PTm", "o16", "oT", "stat")} for k in range(NB3)]
            h2T = [sb(sbk, "h2T%d" % k, [128, 1024], BF16) for k in range(4)]
            R_h2T = [Res("h2T%d" % k) for k in range(4)]
            mstg = sb(sbk, "mstg", [128, 1024], F32)
            R_mstg = Res("mstg")
            wrot = Rot([0, 1, 2, 3, 4, 5, 6, 7])

            def load_cast(dst, R_dst, src, ncols, engs=("act", "pool")):
                k = 0
                for c0 in range(0, ncols, 2048):
                    w = min(2048, ncols - c0)
                    s = k % 2
                    P.dma("sp", wst[s][:, 0:w], src[:, c0:c0 + w], writes=[R_wst[s]])
                    eng = engs[k % len(engs)]
                    if eng == "act":
                        P.op("act", _call("activation", out=dst[:, c0:c0 + w], in_=wst[s][:, 0:w], func=AF.Copy),
                             reads=[R_wst[s]], writes=[R_dst])
                    else:
                        P.op(eng, _call("tensor_copy", out=dst[:, c0:c0 + w], in_=wst[s][:, 0:w]),
                             reads=[R_wst[s]], writes=[R_dst])
                    k += 1

            load_cast(wob, R_wo, I["wo"], 8192)
            load_cast(wmqb, R_wmq, I["wmq"], 4096)
            load_cast(wmob, R_wmo, I["wmo"], 4096)
            for k in range(4):
                P.dma("sp", lnt[:, k * 1024:(k + 1) * 1024], I["lnp"][k:k + 1, :].to_broadcast([128, 1024]), writes=[R_ln])
            load_cast(memTb, R_memT, I["memT"], 2048)
            load_cast(wtmp, R_wtmp, I["wmk"], 4096)
            for h in range(4):
                bank = wrot.next()
                for kc in range(KC):
                    P.op("pe", _call("matmul",
                        out=pb[bank][:, 0:256], lhsT=wtmp[:, kc * 512 + h * 128: kc * 512 + (h + 1) * 128],
                        rhs=memTb[:, kc * 256:(kc + 1) * 256], start=(kc == 0), stop=(kc == KC - 1)),
                        reads=[R_wtmp, R_memT], writes=[R_pb[bank]])
                P.op("act", _call("activation", out=mkT[0][:, h * 256:(h + 1) * 256], in_=pb[bank][:, 0:256], func=AF.Copy),
                     reads=[R_pb[bank]], writes=[R_mk[0]])
                P.op("dve", _call("tensor_copy", out=mstg[:, h * 256:(h + 1) * 256], in_=pb[bank][:, 0:256]),
                     reads=[R_pb[bank]], writes=[R_mstg])
            P.dma("sp", O["mkT"][:, :], mstg[:, :], reads=[R_mstg], defer=True)
            load_cast(wtmp, R_wtmp, I["wmv"], 4096)
            for mt in range(2):
                bank = wrot.next()
                for kc in range(KC):
                    P.op("pe", _call("matmul",
                        out=pb[bank][:, 0:512], lhsT=memTb[:, kc * 256 + mt * 128: kc * 256 + (mt + 1) * 128],
                        rhs=wtmp[:, kc * 512:(kc + 1) * 512], start=(kc == 0), stop=(kc == KC - 1)),
                        reads=[R_wtmp, R_memT], writes=[R_pb[bank]])
                mvv = mva[0][:, mt * 516:(mt + 1) * 516].rearrange("p (h d) -> p h d", d=129)
                P.op("act", _call("activation", out=mvv[:, :, 0:128], in_=pb[bank][:, :].rearrange("p (h d) -> p h d", d=128), func=AF.Copy),
                     reads=[R_pb[bank]], writes=[R_mv[0]])
                P.op("dve", _call("tensor_copy", out=mstg[:, mt * 512:(mt + 1) * 512], in_=pb[bank][:, :]),
                     reads=[R_pb[bank]], writes=[R_mstg])
                P.dma("sp", O["mv"][mt * 128:(mt + 1) * 128, :], mstg[:, mt * 512:(mt + 1) * 512], reads=[R_mstg], defer=True)
            for k in range(2):
                P.op("pool", _call("memset", mva[k][:, :].rearrange("p (t d) -> p t d", d=129)[:, :, 128:129], 1.0), writes=[R_mv[k]])
            load_cast(mkT[1], R_mk[1], I["cmkT"], 1024)
            P.dma("sp", wst[0][:, 0:1024].rearrange("p (t c) -> p t c", c=512), I["cmv"].rearrange("(t p) c -> p t c", p=128), writes=[R_wst[0]])
            P.op("act", _call("activation", out=mva[1][:, :].rearrange("p (t d) -> p t d", d=129)[:, :, 0:128],
                                               in_=wst[0][:, 0:1024].rearrange("p (t d) -> p t d", d=128), func=AF.Copy),
                 reads=[R_wst[0]], writes=[R_mv[1]])

            checkpoint('phaseB_pre')
            def transpose_to(src16, R_src, qs, nchunk, dst, R_dst):
                bank = wrot.next()
                pbf = pb[bank][:, :].bitcast(BF16)
                for c in range(nchunk):
                    P.op("pe", _call("transpose", out=pbf[:, c * qs:(c + 1) * qs], in_=src16[0:qs, c * 128:(c + 1) * 128],
                                                                   identity=ident[0:qs, 0:qs]),
                         reads=[R_src, R_ident], writes=[R_pb[bank]])
                P.op("act", _call("activation", out=dst[:, 0:nchunk * qs], in_=pbf[:, 0:nchunk * qs], func=AF.Copy),
                     reads=[R_pb[bank]], writes=[R_dst])

            def layer_norm(hin, R_hin, qs, gcol, hout, R_hout, stat, R_stat):
                for c in range(2):
                    P.op("dve", _call("bn_stats", out=stat[0:qs, c * 6:(c + 1) * 6], in_=hin[0:qs, c * 512:(c + 1) * 512]),
                         reads=[R_hin], writes=[R_stat])
                P.op("dve", _call("bn_aggr", out=stat[0:qs, 12:14], in_=stat[0:qs, 0:12]), reads=[R_stat], writes=[R_stat])
                P.op("dve", _call("tensor_scalar", out=stat[0:qs, 14:15], in0=stat[0:qs, 13:14], scalar1=LN_EPS, scalar2=None, op0=ALU.add),
                     reads=[R_stat], writes=[R_stat])
                P.op("act", _call("activation", out=stat[0:qs, 15:16], in_=stat[0:qs, 14:15], func=AF.Sqrt), reads=[R_stat], writes=[R_stat])
                P.op("dve", _call("reciprocal", out=stat[0:qs, 16:17], in_=stat[0:qs, 15:16]), reads=[R_stat], writes=[R_stat])
                P.op("dve", _call("scalar_tensor_tensor", out=stat[0:qs, 17:18], in0=stat[0:qs, 12:13], scalar=-1.0, in1=stat[0:qs, 16:17],
                                                             op0=ALU.mult, op1=ALU.mult),
                     reads=[R_stat], writes=[R_stat])
                P.op("act", _call("activation", out=hout[0:qs, :], in_=hin[0:qs, :], func=AF.Identity, scale=stat[0:qs, 16:17], bias=stat[0:qs, 17:18]),
                     reads=[R_hin, R_stat], writes=[R_hout])
                P.op("dve", _call("tensor_tensor", out=hout[0:qs, :], in0=hout[0:qs, :], in1=lnt[0:qs, gcol * 1024:(gcol + 1) * 1024], op=ALU.mult),
                     reads=[R_hout, R_ln], writes=[R_hout])
                P.op("dve", _call("tensor_tensor", out=hout[0:qs, :], in0=hout[0:qs, :], in1=lnt[0:qs, (gcol + 1) * 1024:(gcol + 2) * 1024], op=ALU.add),
                     reads=[R_hout, R_ln], writes=[R_hout])

            def phaseB_block(blk, qs, row0, mi, k2):
                s = k2
                tT, hA, hB, h16, qmT, PTm, o16, oT, stat = (tT_l[k2], hA_l[k2], hB_l[k2], h16_l[k2], qmT_l[k2], PTm_l[k2], o16_l[k2],
                                                             oT_l[k2], stat_l[k2])
                R_tT, R_hA, R_hB, R_h16, R_qm, R_PTm, R_o16, R_oT, R_stat = (RB[k2][n] for n in ("tT", "hA", "hB", "h16", "qm", "PTm", "o16", "oT", "stat"))
                P.dma("sp", mixl[s][0:qs, :], mixD[blk * 128 + row0: blk * 128 + row0 + qs, :], reads=[R_mixD[blk]], writes=[R_mixl[s]])
                P.dma("sp", xr[s][0:qs, :], I["xres"][blk * 128: blk * 128 + qs, :], writes=[R_xr[s]])
                transpose_to(mixl[s], R_mixl[s], qs, 8, tT, R_tT)
                yield
                b0, b1 = wrot.next(), wrot.next()
                for n, bank in enumerate((b0, b1)):
                    for kc in range(KC):
                        P.op("pe", _call("matmul",
                            out=pb[bank][0:qs, :], lhsT=tT[:, kc * qs:(kc + 1) * qs], rhs=wob[:, kc * 1024 + n * 512: kc * 1024 + (n + 1) * 512],
                            start=(kc == 0), stop=(kc == KC - 1)),
                            reads=[R_tT, R_wo], writes=[R_pb[bank]])
                    P.op("dve", _call("scalar_tensor_tensor",
                        out=hA[0:qs, n * 512:(n + 1) * 512], in0=xr[s][0:qs, n * 512:(n + 1) * 512], scalar=ALPHA, in1=pb[bank][0:qs, :],
                        op0=ALU.mult, op1=ALU.add),
                        reads=[R_xr[s], R_pb[bank]], writes=[R_hA])
                yield
                layer_norm(hA, R_hA, qs, 0, hB, R_hB, stat, R_stat)
                yield
                P.op("act", _call("activation", out=h16[0:qs, :], in_=hB[0:qs, :], func=AF.Copy), reads=[R_hB], writes=[R_h16])
                transpose_to(h16, R_h16, qs, 8, tT, R_tT)
                yield
                bq = wrot.next()
                for h in range(4):
                    for kc in range(KC):
                        P.op("pe", _call("matmul",
                            out=pb[bq][:, h * qs:(h + 1) * qs], lhsT=wmqb[:, kc * 512 + h * 128: kc * 512 + (h + 1) * 128],
                            rhs=tT[:, kc * qs:(kc + 1) * qs], start=(kc == 0), stop=(kc == KC - 1)),
                            reads=[R_wmq, R_tT], writes=[R_pb[bq]])
                P.op("act", _call("activation", out=qmT[:, 0:4 * qs], in_=pb[bq][:, 0:4 * qs], func=AF.Copy, scale=float(128.0 ** -0.5)),
                     reads=[R_pb[bq]], writes=[R_qm])
                yield
                bs0, bs1 = wrot.next(), wrot.next()
                for h in range(4):
                    for mt in range(2):
                        idx = h * 2 + mt
                        bank = bs0 if idx < 4 else bs1
                        c0 = (idx % 4) * qs
                        P.op("pe", _call("matmul",
                            out=pb[bank][:, c0:c0 + qs], lhsT=mkT[mi][:, h * 256 + mt * 128: h * 256 + (mt + 1) * 128],
                            rhs=qmT[:, h * qs:(h + 1) * qs], start=True, stop=True),
                            reads=[R_mk[mi], R_qm], writes=[R_pb[bank]])
                for k, bank in enumerate((bs0, bs1)):
                    P.op("act", _call("activation", out=PTm[:, k * 4 * qs:(k + 1) * 4 * qs], in_=pb[bank][:, 0:4 * qs], func=AF.Exp),
                         reads=[R_pb[bank]], writes=[R_PTm])
                yield
                bo0, bo1 = wrot.next(), wrot.next()
                for h in range(4):
                    bank = bo0 if h < 2 else bo1
                    for mt in range(2):
                        idx = h * 2 + mt
                        P.op("pe", _call("matmul",
                            out=pb[bank][0:qs, (h % 2) * 129:(h % 2) * 129 + 129], lhsT=PTm[:, idx * qs:(idx + 1) * qs],
                            rhs=mva[mi][:, (mt * 4 + h) * 129:(mt * 4 + h) * 129 + 129],
                            start=(h % 2 == 0 and mt == 0), stop=(mt == 1), skip_group_check=True),
                            reads=[R_PTm, R_mv[mi]], writes=[R_pb[bank]])
                for k, bank in enumerate((bo0, bo1)):
                    ov = pb[bank][0:qs, 0:258].rearrange("p (h d) -> p h d", d=129)
                    P.op("dve", _call("tensor_scalar", out=stat[0:qs, 20 + 2 * k:22 + 2 * k].rearrange("p (h o) -> p h o", o=1),
                                                                      in0=ov[:, :, 128:129], scalar1=1e-30, scalar2=None, op0=ALU.max),
                         reads=[R_pb[bank]], writes=[R_stat])
                    P.op("dve", _call("reciprocal", out=stat[0:qs, 20 + 2 * k:22 + 2 * k], in_=stat[0:qs, 20 + 2 * k:22 + 2 * k]),
                         reads=[R_stat], writes=[R_stat])
                    for hh in range(2):
                        h = k * 2 + hh
                        P.op("dve", _call("tensor_scalar",
                            out=o16[0:qs, h * 128:(h + 1) * 128], in0=pb[bank][0:qs, hh * 129: hh * 129 + 128],
                            scalar1=stat[0:qs, 20 + 2 * k + hh:21 + 2 * k + hh], scalar2=None, op0=ALU.mult),
                            reads=[R_pb[bank], R_stat], writes=[R_o16])
                yield
                transpose_to(o16, R_o16, qs, 4, oT, R_oT)
                yield
                b0, b1 = wrot.next(), wrot.next()
                for n, bank in enumerate((b0, b1)):
                    for c in range(4):
                        P.op("pe", _call("matmul",
                            out=pb[bank][0:qs, :], lhsT=oT[:, c * qs:(c + 1) * qs], rhs=wmob[:, c * 1024 + n * 512: c * 1024 + (n + 1) * 512],
                            start=(c == 0), stop=(c == 3)),
                            reads=[R_oT, R_wmo], writes=[R_pb[bank]])
                    P.op("dve", _call("scalar_tensor_tensor",
                        out=hA[0:qs, n * 512:(n + 1) * 512], in0=hB[0:qs, n * 512:(n + 1) * 512], scalar=ALPHA, in1=pb[bank][0:qs, :],
                        op0=ALU.mult, op1=ALU.add),
                        reads=[R_hB, R_pb[bank]], writes=[R_hA])
                yield
                layer_norm(hA, R_hA, qs, 2, hB, R_hB, stat, R_stat)
                yield
                P.dma("sp", h2D[blk * 128: blk * 128 + qs, :], hB[0:qs, :], reads=[R_hB], writes=[R_h2D[blk]], defer=True)
                P.op("act", _call("activation", out=h16[0:qs, :], in_=hB[0:qs, :], func=AF.Copy), reads=[R_hB], writes=[R_h16])
                transpose_to(h16, R_h16, qs, 8, h2T[s], R_h2T[s])
                P.dma("sp", h2TD[blk][:, 0:8 * qs], h2T[s][:, 0:8 * qs], reads=[R_h2T[s]], writes=[R_h2TD[blk]], defer=True)
                yield

            def run_staggered(gens, lag):
                active = []
                pending = list(gens)
                tick = 0
                while active or pending:
                    if pending and (not active or tick >= lag):
                        active.append(pending.pop(0))
                        tick = 0
                    for g in list(active):
                        try:
                            next(g)
                        except StopIteration:
                            active.remove(g)
                    tick += 1

            blocks = [(16, 2, 126, 0), (17, 16, 0, 1)] + [(i, 128, 0, 0) for i in range(16)]
            run_staggered([phaseB_block(b_, q_, r_, m_, pos % 4) for pos, (b_, q_, r_, m_) in enumerate(blocks)], 3)
            checkpoint('phaseB')
            P.flush(block)

        P.barrier()
        with ExitStack() as sc:
            wdb = sb(sc, "wdb", [128, NFC * 1024], BF16)
            R_wd = Res("wd")
            wst = [sb(sc, "wstc%d" % k, [128, 2048], F32) for k in range(2)]
            R_wst = [Res("wstc%d" % k) for k in range(2)]
            wsl = [sb(sc, "wsl%d" % k, [128, 2048], BF16) for k in range(2)]
            R_wsl = [Res("wsl%d" % k) for k in range(2)]
            R_wslB = [Res("wslB%d" % k) for k in range(2)]
            hT2 = [sb(sc, "hT%d" % k, [128, NFC * 512], BF16) for k in range(2)]
            R_hT2 = [Res("hT%d" % k) for k in range(2)]
            hTm = sb(sc, "hTm", [128, NFC * 16], BF16)
            R_hTm = Res("hTm")
            h2Tg = [sb(sc, "h2Tg%d" % k, [128, 8 * 512], BF16) for k in range(2)]
            R_h2Tg = [Res("h2Tg%d" % k) for k in range(2)]
            h2Tm = sb(sc, "h2Tm", [128, 8 * 18], BF16)
            R_h2Tm = Res("h2Tm")
            Gb = [sb(sc, "Gb%d" % k, [128, 532], F32) for k in range(3)]
            R_Gb = [Res("Gb%d" % k) for k in range(3)]
            Gs = sb(sc, "Gs", [128, 18], F32)
            R_Gs = Res("Gs")
            t0b = [sb(sc, "t0b%d" % k, [128, 530], F32) for k in range(3)]
            R_t0 = [Res("t0%d" % k) for k in range(3)]
            geb = [sb(sc, "geb%d" % k, [128, 530], F32) for k in range(3)]
            R_ge = [Res("ge%d" % k) for k in range(3)]
            t1b = [sb(sc, "t1b%d" % k, [128, 530], F32) for k in range(3)]
            R_t1b = [Res("t1b%d" % k) for k in range(3)]
            t2b = [sb(sc, "t2b%d" % k, [128, 530], F32) for k in range(3)]
            R_t2b = [Res("t2b%d" % k) for k in range(3)]
            t0s = sb(sc, "t0s", [128, 16], F32)
            ges = sb(sc, "ges", [128, 16], F32)
            R_ts = Res("ts")
            carry = sb(sc, "carry", [128, NFC * 2], F32)
            R_carry = [Res("carry%d" % c) for c in range(NFC)]
            sfc = sb(sc, "sfc", [128, NFC * 2], F32)
            R_sfc = Res("sfc")
            sconv = sb(sc, "sconv", [128, NFC * 2], F32)
            wconv = sb(sc, "wconv", [128, NFC * 3], F32)
            bconv = sb(sc, "bconv", [128, NFC], F32)
            flag = sb(sc, "flag", [128, 1], F32)
            R_cc = Res("cc")
            ln3 = sb(sc, "ln3", [128, 2 * 1024], F32)
            R_ln3 = Res("ln3")
            h2r = [sb(sc, "h2r%d" % k, [128, 1024], F32) for k in range(2)]
            R_h2r = [Res("h2r%d" % k) for k in range(2)]
            yA = sb(sc, "yA", [128, 1024], F32)
            R_yA = Res("yA")
            yB = [sb(sc, "yB%d" % k, [128, 1024], F32) for k in range(2)]
            R_yB = [Res("yB%d" % k) for k in range(2)]
            stat = sb(sc, "statc", [128, 32], F32)
            R_stat = Res("statc")

            P.dma("sp", sconv[:, :], I["sconvT"][:, :], writes=[R_cc])
            P.dma("sp", wconv[:, :], I["wconvT"][:, :], writes=[R_cc])
            P.dma("sp", bconv[:, :], I["bconvT"][:, :], writes=[R_cc])
            P.dma("sp", flag[:, :], I["flag"][:, :], writes=[R_cc])
            for k in range(2):
                P.dma("sp", ln3[:, k * 1024:(k + 1) * 1024], I["lnp"][4 + k:5 + k, :].to_broadcast([128, 1024]), writes=[R_ln3])
            def wdown_piece(j):
                kq = j % 2
                P.dma("sp", yB[kq][:, :], I["wdown"][:, j * 1024:(j + 1) * 1024], writes=[R_yB[kq]])
                P.op("dve", _call("tensor_copy", out=wdb[:, j * 1024:(j + 1) * 1024], in_=yB[kq][:, :]), reads=[R_yB[kq]], writes=[R_wd])

            P.dma("sp", h2Tm[:, :].rearrange("p (c q) -> p c q", q=18)[:, :, 0:2], h2TD[16][:, 0:16].rearrange("p (c q) -> p c q", q=2),
                  reads=[R_h2TD[16]], writes=[R_h2Tm], slow=True)
            P.dma("sp", h2Tm[:, :].rearrange("p (c q) -> p c q", q=18)[:, :, 2:18], h2TD[17][:, 0:128].rearrange("p (c q) -> p c q", q=16),
                  reads=[R_h2TD[17]], writes=[R_h2Tm], slow=True)

            checkpoint('phaseC_pre')
            UB = [0, 2, 4]
            GBK = [1, 3, 5]
            MB = 7
            YB = [6, 7]
            wk = [0]

            def ln3_out(pre_banks, qs, h2src, R_h2src, dst_ap, ys, R_ys):
                for n, bank in enumerate(pre_banks):
                    P.op("dve", _call("scalar_tensor_tensor",
                        out=yA[0:qs, n * 512:(n + 1) * 512], in0=h2src[0:qs, n * 512:(n + 1) * 512], scalar=ALPHA, in1=pb[bank][0:qs, :],
                        op0=ALU.mult, op1=ALU.add),
                        reads=[R_h2src, R_pb[bank]], writes=[R_yA])
                for c in range(2):
                    P.op("dve", _call("bn_stats", out=stat[0:qs, c * 6:(c + 1) * 6], in_=yA[0:qs, c * 512:(c + 1) * 512]),
                         reads=[R_yA], writes=[R_stat])
                P.op("dve", _call("bn_aggr", out=stat[0:qs, 12:14], in_=stat[0:qs, 0:12]), reads=[R_stat], writes=[R_stat])
                P.op("dve", _call("tensor_scalar", out=stat[0:qs, 14:15], in0=stat[0:qs, 13:14], scalar1=LN_EPS, scalar2=None, op0=ALU.add),
                     reads=[R_stat], writes=[R_stat])
                P.op("act", _call("activation", out=stat[0:qs, 15:16], in_=stat[0:qs, 14:15], func=AF.Sqrt), reads=[R_stat], writes=[R_stat])
                P.op("dve", _call("reciprocal", out=stat[0:qs, 16:17], in_=stat[0:qs, 15:16]), reads=[R_stat], writes=[R_stat])
                P.op("dve", _call("scalar_tensor_tensor", out=stat[0:qs, 17:18], in0=stat[0:qs, 12:13], scalar=-1.0, in1=stat[0:qs, 16:17],
                                                             op0=ALU.mult, op1=ALU.mult),
                     reads=[R_stat], writes=[R_stat])
                P.op("act", _call("activation", out=ys[0:qs, :], in_=yA[0:qs, :], func=AF.Identity, scale=stat[0:qs, 16:17], bias=stat[0:qs, 17:18]),
                     reads=[R_yA, R_stat], writes=[R_ys])
                P.op("pool", _call("tensor_tensor", out=ys[0:qs, :], in0=ys[0:qs, :], in1=ln3[0:qs, 0:1024], op=ALU.mult),
                     reads=[R_ys, R_ln3], writes=[R_ys])
                P.op("pool", _call("tensor_tensor", out=ys[0:qs, :], in0=ys[0:qs, :], in1=ln3[0:qs, 1024:2048], op=ALU.add),
                     reads=[R_ys, R_ln3], writes=[R_ys])
                P.dma("sp", dst_ap, ys[0:qs, :], reads=[R_ys], defer=True)

            def load_h2Tg(grp):
                gs = grp % 2
                for bi in range(4):
                    blk = grp * 4 + bi
                    P.dma("sp", h2Tg[gs][:, :].rearrange("p (c q) -> p c q", q=512)[:, :, bi * 128:(bi + 1) * 128],
                          h2TD[blk][:, :].rearrange("p (c q) -> p c q", q=128), reads=[R_h2TD[blk]], writes=[R_h2Tg[gs]])

            def c_s1(grp, c):
                s = (grp * NFC + c) % 2
                P.dma("sp", wst[s][:, :], I["wup"][c], writes=[R_wst[s]])
                P.op("dve", _call("tensor_copy", out=wsl[s][:, 0:1024], in_=wst[s][:, 0:1024]), reads=[R_wst[s]], writes=[R_wsl[s]])
                P.op("dve", _call("tensor_copy", out=wsl[s][:, 1024:2048], in_=wst[s][:, 1024:2048]), reads=[R_wst[s]], writes=[R_wslB[s]])

            def c_s2(grp, c):
                s = (grp * NFC + c) % 2
                gs = grp % 2
                mo = (c % 3) * 64
                if grp == 0:
                    for part, oc in ((0, mo), (1, mo + 32)):
                        for kc in range(KC):
                            P.op("pe", _call("matmul", out=pb[MB][:, oc:oc + 18], lhsT=wsl[s][:, kc * 256 + part * 128: kc * 256 + (part + 1) * 128],
                                             rhs=h2Tm[:, kc * 18:(kc + 1) * 18], start=(kc == 0), stop=(kc == KC - 1)),
                                 reads=[R_wsl[s], R_wslB[s], R_h2Tm], writes=[R_pb[MB]])
                k3 = (grp * NFC + c) % 3
                ub, gbk = UB[k3], GBK[k3]
                for part, bank in ((0, ub), (1, gbk)):
                    for kc in range(KC):
                        P.op("pe", _call("matmul", out=pb[bank][:, :], lhsT=wsl[s][:, kc * 256 + part * 128: kc * 256 + (part + 1) * 128],
                                         rhs=h2Tg[gs][:, kc * 512:(kc + 1) * 512], start=(kc == 0), stop=(kc == KC - 1)),
                             reads=[R_wsl[s], R_wslB[s], R_h2Tg[gs]], writes=[R_pb[bank]])

            def c_s3(grp, c):
                hTg, R_hTg = hT2[grp % 2], R_hT2[grp % 2]
                mo = (c % 3) * 64
                k3 = (grp * NFC + c) % 3
                ub, gbk = UB[k3], GBK[k3]
                G, R_G = Gb[k3], R_Gb[k3]
                t0, R_t = t0b[k3], R_t0[k3]
                ge, R_g = geb[k3], R_ge[k3]
                t1, R_t1 = t1b[k3], R_t1b[k3]
                t2, R_t2 = t2b[k3], R_t2b[k3]
                W = 530 if grp == 0 else 512
                if grp == 0:
                    P.op("dve", _call("tensor_scalar", out=carry[:, c * 2:(c + 1) * 2], in0=pb[MB][:, mo + 32:mo + 34], scalar1=flag[:, 0:1],
                                      scalar2=None, op0=ALU.mult),
                         reads=[R_pb[MB], R_cc], writes=[R_carry[c]])
                P.op("act", _call("activation", out=G[:, 0:2], in_=carry[:, c * 2:(c + 1) * 2], func=AF.Copy),
                     reads=[R_carry[c]], writes=[R_G])
                P.op("act", _call("activation", out=G[:, 2:514], in_=pb[gbk][:, :], func=AF.Copy), reads=[R_pb[gbk]], writes=[R_G])
                if grp == 0:
                    P.op("act", _call("activation", out=G[:, 514:516], in_=sconv[:, c * 2:(c + 1) * 2], func=AF.Copy), reads=[R_cc], writes=[R_G])
                    P.op("act", _call("activation", out=G[:, 516:532], in_=pb[MB][:, mo + 34:mo + 50], func=AF.Copy), reads=[R_pb[MB]], writes=[R_G])
                    P.op("act", _call("activation", out=sfc[:, c * 2:(c + 1) * 2], in_=G[:, 530:532], func=AF.Copy), reads=[R_G], writes=[R_sfc])
                P.op("act", _call("activation", out=carry[:, c * 2:(c + 1) * 2], in_=G[:, 512:514], func=AF.Copy),
                     reads=[R_G], writes=[R_carry[c]])
                P.op("act", _call("activation", out=t0[:, 0:W], in_=G[:, 2:2 + W], func=AF.Identity,
                                  scale=wconv[:, c * 3 + 2:c * 3 + 3], bias=bconv[:, c:c + 1]),
                     reads=[R_G, R_cc], writes=[R_t])
                P.op("act", _call("activation", out=t1[:, 0:W], in_=G[:, 1:1 + W], func=AF.Identity, scale=wconv[:, c * 3 + 1:c * 3 + 2]),
                     reads=[R_G, R_cc], writes=[R_t1])
                P.op("act", _call("activation", out=t2[:, 0:W], in_=G[:, 0:W], func=AF.Identity, scale=wconv[:, c * 3:c * 3 + 1]),
                     reads=[R_G, R_cc], writes=[R_t2])
                P.op("dve", _call("tensor_tensor", out=t0[:, 0:W], in0=t0[:, 0:W], in1=t1[:, 0:W], op=ALU.add), reads=[R_t, R_t1], writes=[R_t])
                P.op("dve", _call("tensor_tensor", out=t0[:, 0:W], in0=t0[:, 0:W], in1=t2[:, 0:W], op=ALU.add), reads=[R_t, R_t2], writes=[R_t])

            def c_s3b(grp, c):
                hTg, R_hTg = hT2[grp % 2], R_hT2[grp % 2]
                mo = (c % 3) * 64
                k3 = (grp * NFC + c) % 3
                ub = UB[k3]
                t0, R_t = t0b[k3], R_t0[k3]
                ge, R_g = geb[k3], R_ge[k3]
                W = 530 if grp == 0 else 512
                P.op("act", _call("activation", out=ge[:, 0:W], in_=t0[:, 0:W], func=AF.Gelu_apprx_tanh), reads=[R_t], writes=[R_g])
                P.op("dve", _call("tensor_tensor", out=hTg[:, c * 512:(c + 1) * 512], in0=pb[ub][:, :], in1=ge[:, 0:512], op=ALU.mult),
                     reads=[R_pb[ub], R_g], writes=[R_hTg])
                if grp == 0:
                    P.op("dve", _call("tensor_tensor", out=hTm[:, c * 16:(c + 1) * 16], in0=pb[MB][:, mo + 2:mo + 18], in1=ge[:, 514:530], op=ALU.mult),
                         reads=[R_pb[MB], R_g], writes=[R_hTm])

            def c_down(grp):
                hTg, R_hTg = hT2[grp % 2], R_hT2[grp % 2]
                if grp == 0:
                    for n, bank in enumerate(YB):
                        for c in range(NFC):
                            P.op("pe", _call("matmul", out=pb[bank][0:16, :], lhsT=hTm[:, c * 16:(c + 1) * 16],
                                             rhs=wdb[:, c * 1024 + n * 512: c * 1024 + (n + 1) * 512], start=(c == 0), stop=(c == NFC - 1)),
                                 reads=[R_hTm, R_wd], writes=[R_pb[bank]])
                    P.dma("sp", h2r[0][0:16, :], h2D[17 * 128: 17 * 128 + 16, :], reads=[R_h2D[17]], writes=[R_h2r[0]])
                    ln3_out(YB, 16, h2r[0], R_h2r[0], O["ys"][:, :], yB[0], R_yB[0])
                    P.dma("sp", O["sfcT"][:, :], sfc[:, :], reads=[R_sfc], defer=True)
                for bi in range(4):
                    blk = grp * 4 + bi
                    hs = blk % 2
                    P.dma("sp", h2r[hs][:, :], h2D[blk * 128:(blk + 1) * 128, :], reads=[R_h2D[blk]], writes=[R_h2r[hs]])
                    for n, bank in enumerate(YB):
                        for c in range(NFC):
                            P.op("pe", _call("matmul", out=pb[bank][:, :], lhsT=hTg[:, c * 512 + bi * 128: c * 512 + (bi + 1) * 128],
                                             rhs=wdb[:, c * 1024 + n * 512: c * 1024 + (n + 1) * 512], start=(c == 0), stop=(c == NFC - 1)),
                                 reads=[R_hTg, R_wd], writes=[R_pb[bank]])
                    ln3_out(YB, 128, h2r[hs], R_h2r[hs], O["y"][blk * 128:(blk + 1) * 128, :], yB[hs], R_yB[hs])

            seq = [(grp, c) for grp in range(4) for c in range(NFC)]
            nseq = len(seq)
            load_h2Tg(0)
            load_h2Tg(1)
            for idx in range(nseq + 3):
                if 1 <= idx <= NFC:
                    wdown_piece(idx - 1)
                if idx < nseq:
                    c_s1(*seq[idx])
                if 1 <= idx <= nseq:
                    c_s2(*seq[idx - 1])
                if 3 <= idx:
                    g4, c4 = seq[idx - 3]
                    c_s3b(g4, c4)
                    if c4 == NFC - 1:
                        c_down(g4)
                        if g4 + 2 < 4:
                            load_h2Tg(g4 + 2)
                if 2 <= idx <= nseq + 1:
                    c_s3(*seq[idx - 2])
            P.dma("sp", O["fcT"][:, :], carry[:, :], reads=R_carry, defer=True)
            P.finish()
            P.flush(block)
    return nc


def _t5_bucket(rel):
    half, max_exact = 16, 8
    n = np.abs(rel)
    log_ratio = np.log(np.maximum(n, 1).astype(np.float32) / max_exact) / math.log(128 / max_exact)
    large = np.minimum(max_exact + (log_ratio * (half - max_exact)).astype(np.int32), half - 1)
    return np.where(rel < 0, half, 0) + np.where(n < max_exact, n, large)


def _host_inputs(inp):
    f32 = np.float32
    x_prompt = np.asarray(inp["x_prompt"], f32)
    x_sample = np.asarray(inp["x_sample"], f32)
    w_in = np.asarray(inp["w_in"], f32)[0]
    qa, ka, va = w_in[:, 0:512], w_in[:, 512:1024], w_in[:, 1024:1536]
    qb, kb, vb = w_in[:, 1536:2048], w_in[:, 2048:2176], w_in[:, 2176:2304]
    qi, ki, wi = w_in[:, 2304:2816], w_in[:, 2816:2880], w_in[:, 2880:2888]
    qbp = np.concatenate([np.concatenate([qb[:, r * 64:(r + 1) * 64], qb[:, (4 + r) * 64:(5 + r) * 64]], axis=1) for r in range(4)], axis=1)
    winp = np.concatenate([qa, ka, qbp, kb, qi, ki, ki, va, vb, wi], axis=1)
    assert winp.shape[1] == NCOL

    def kc_layout(w):
        n = w.shape[1]
        return np.ascontiguousarray(w.reshape(8, 128, n).transpose(1, 0, 2).reshape(128, 8 * n))

    shared = {}
    shared["win"] = kc_layout(winp)
    shared["wo"] = kc_layout(np.asarray(inp["w_o"], f32)[0])
    shared["wmq"] = kc_layout(np.asarray(inp["w_mq"], f32)[0])
    shared["wmk"] = kc_layout(np.asarray(inp["w_mk"], f32)[0])
    shared["wmv"] = kc_layout(np.asarray(inp["w_mv"], f32)[0])
    wmo = np.asarray(inp["w_mo"], f32)[0]
    shared["wmo"] = np.ascontiguousarray(wmo.reshape(4, 128, 1024).transpose(1, 0, 2).reshape(128, 4096))
    w_up = np.asarray(inp["w_up"], f32)[0]
    wu = w_up[:, :DFF].reshape(8, 128, NFC, 128)
    wg = w_up[:, DFF:].reshape(8, 128, NFC, 128)
    wup = np.stack([wu, wg], axis=3)
    shared["wup"] = np.ascontiguousarray(wup.transpose(2, 1, 0, 3, 4).reshape(NFC, 128, 8 * 256))
    w_down = np.asarray(inp["w_down"], f32)[0]
    shared["wdown"] = np.ascontiguousarray(w_down.reshape(NFC, 128, 1024).transpose(1, 0, 2).reshape(128, NFC * 1024))
    shared["lnp"] = np.ascontiguousarray(np.stack([np.asarray(inp[k], f32)[0] for k in ("ln1_g", "ln1_b", "ln2_g", "ln2_b", "ln3_g", "ln3_b")]))
    w_conv = np.asarray(inp["w_conv"], f32)[0]
    shared["wconvT"] = np.ascontiguousarray(w_conv.reshape(3, NFC, 128).transpose(2, 1, 0).reshape(128, NFC * 3))
    shared["bconvT"] = np.ascontiguousarray(np.asarray(inp["b_conv"], f32)[0].reshape(NFC, 128).T)
    shared["ident"] = np.eye(128, dtype=f32)
    tabA = np.asarray(inp["a_rel_bias"], f32)[0]
    qq = np.arange(128)[:, None]
    kk = np.arange(640)[None, :]
    kpos = kk - 512
    rel = qq - kpos
    cq = qq // 64
    kch = np.floor_divide(kpos, 64)
    allowed = (kch >= cq - 8) & (kch <= cq)
    bias = tabA[np.clip(rel, -64, 64) + 64]
    AB = np.where(allowed[:, :, None], bias, f32(NEGM)).astype(f32)
    shared["AB"] = np.ascontiguousarray(AB.transpose(0, 2, 1).reshape(128, 8 * ABW))
    js = np.arange(16)[:, None]
    ks = np.arange(528)[None, :]
    ABs = tabA[np.clip(512 + js - ks, -64, 64) + 64]
    shared["ABs"] = np.ascontiguousarray(ABs.transpose(0, 2, 1).reshape(16, 8 * 528)).astype(f32)
    t5 = np.asarray(inp["t5_bias"], f32)
    relB = np.arange(128)[:, None] - np.arange(256)[None, :] + 128
    Bn = t5[_t5_bucket(relB)]
    shared["Bn"] = np.ascontiguousarray(Bn.transpose(0, 2, 1).reshape(128, 8 * BNW)).astype(f32)
    relBs = 128 + np.arange(16)[:, None] - np.arange(144)[None, :]
    Bns = t5[_t5_bucket(relBs)]
    shared["Bns"] = np.ascontiguousarray(Bns.transpose(0, 2, 1).reshape(16, 8 * 144)).astype(f32)
    shared["C15"] = np.ascontiguousarray(np.broadcast_to(t5[15][None, :], (128, 8))).astype(f32)
    dm = np.zeros((128, 128), f32)
    dm[0:64, 64:128] = NEGM
    shared["diagmask"] = dm

    mem_prompt = np.asarray(inp["mem_prompt"], f32)
    maps = []
    for c in range(8):
        b, half = c // 2, c % 2
        m = dict(shared)
        xk = np.zeros((4096, 1024), f32)
        if half == 1:
            xk[:] = x_prompt[b]
        else:
            xk[2048:] = x_prompt[b, :2048]
        m["xkT"] = np.ascontiguousarray(xk.reshape(32, 128, 8, 128).transpose(0, 3, 2, 1).reshape(32, 128, 1024))
        xs = x_sample[c]
        m["xsT"] = np.ascontiguousarray(xs.reshape(16, 8, 128).transpose(2, 1, 0).reshape(128, 128))
        xres = np.zeros((NBLK * 128, 1024), f32)
        xres[0:2048] = xk[2048:]
        xres[2048:2050] = xk[2046:2048]
        xres[17 * 128:17 * 128 + 16] = xs
        m["xres"] = xres
        m["memT"] = np.ascontiguousarray(mem_prompt[b].reshape(256, 8, 128).transpose(2, 1, 0).reshape(128, 2048))
        cmk = np.asarray(inp["cache_mem_k"], f32)[0, c]
        m["cmkT"] = np.ascontiguousarray(cmk.transpose(2, 1, 0).reshape(128, 1024))
        m["cmv"] = np.ascontiguousarray(np.asarray(inp["cache_mem_v"], f32)[0, c].reshape(256, 512))
        cak = np.asarray(inp["cache_a_k"], f32)[0, c]
        m["cakT"] = np.ascontiguousarray(cak.reshape(512, 4, 2, 64).transpose(2, 3, 1, 0).reshape(128, 2048))
        m["cav"] = np.ascontiguousarray(np.asarray(inp["cache_a_v"], f32)[0, c].reshape(512, 512))
        cbk = np.asarray(inp["cache_b_k"], f32)[0, c]
        m["cbkT"] = np.ascontiguousarray(cbk.reshape(2048, 128).T)
        m["cbv"] = np.ascontiguousarray(np.asarray(inp["cache_b_v"], f32)[0, c].reshape(2048, 128))
        cbi = np.asarray(inp["cache_b_kidx"], f32)[0, c]
        m["cbiT"] = np.ascontiguousarray(np.concatenate([cbi.T, cbi.T], axis=0))
        sc_ = np.asarray(inp["state_ffn_conv"], f32)[0, c]
        m["sconvT"] = np.ascontiguousarray(sc_.reshape(2, NFC, 128).transpose(2, 1, 0).reshape(128, NFC * 2))
        m["colmask"] = np.full((128, 1), NEGM if half == 0 else 0.0, f32)
        kv = np.ones((128, NT), f32)
        if half == 0:
            kv[:, 0:16] = 0.0
        m["kvalid"] = kv
        m["flag"] = np.full((128, 1), float(half), f32)
        maps.append(m)
    return maps


_NC_CACHE = {}


def _run(inputs, debug=False):
    key = bool(debug)
    if key not in _NC_CACHE:
        _NC_CACHE[key] = build_program(debug=debug)
    nc = _NC_CACHE[key]
    maps = _host_inputs(inputs)
    res = run_bass_kernel_spmd(nc, maps, core_ids=list(range(8)))
    return res.results


def kernel(**inputs):
    R = _run(inputs)
    f32 = np.float32
    y = np.zeros((4, 4096, 1024), f32)
    ys = np.zeros((8, 16, 1024), f32)
    pak = np.zeros((1, 4, 512, 8, 64), f32)
    pav = np.zeros((1, 4, 512, 8, 64), f32)
    pbk = np.zeros((1, 4, 4096, 2, 64), f32)
    pbv = np.zeros((1, 4, 4096, 2, 64), f32)
    pbi = np.zeros((1, 4, 4096, 64), f32)
    pmk = np.zeros((1, 4, 256, 4, 128), f32)
    pmv = np.zeros((1, 4, 256, 4, 128), f32)
    pfc = np.zeros((1, 4, 2, DFF), f32)
    sak = np.zeros((1, 8, 16, 8, 64), f32)
    sav = np.zeros((1, 8, 16, 8, 64), f32)
    sbk = np.zeros((1, 8, 16, 2, 64), f32)
    sbv = np.zeros((1, 8, 16, 2, 64), f32)
    sbi = np.zeros((1, 8, 16, 64), f32)
    sfc = np.zeros((1, 8, 2, DFF), f32)
    for c in range(8):
        b, half = c // 2, c % 2
        r = R[c]
        y[b, half * 2048:(half + 1) * 2048] = np.asarray(r["y"], f32)
        ys[c] = np.asarray(r["ys"], f32)
        if half == 1:
            akT = np.asarray(r["akT"], f32).reshape(2, 64, 4, 512)
            pak[0, b] = akT.transpose(3, 2, 0, 1).reshape(512, 8, 64)
            pav[0, b] = np.asarray(r["av"], f32).reshape(512, 8, 64)
            pbk[0, b] = np.asarray(r["bkT"], f32).T.reshape(4096, 2, 64)
            pbv[0, b] = np.asarray(r["bv"], f32).reshape(4096, 2, 64)
            pbi[0, b] = np.asarray(r["biT"], f32).T
            pmk[0, b] = np.asarray(r["mkT"], f32).reshape(128, 4, 256).transpose(2, 1, 0)
            pmv[0, b] = np.asarray(r["mv"], f32).reshape(256, 4, 128)
            pfc[0, b] = np.asarray(r["fcT"], f32).reshape(128, NFC, 2).transpose(2, 1, 0).reshape(2, DFF)
        sakT = np.asarray(r["sakT"], f32).reshape(2, 64, 4, 16)
        sak[0, c] = sakT.transpose(3, 2, 0, 1).reshape(16, 8, 64)
        sav[0, c] = np.asarray(r["sav"], f32).reshape(16, 8, 64)
        sbk[0, c] = np.asarray(r["sbkT"], f32).T.reshape(16, 2, 64)
        sbv[0, c] = np.asarray(r["sbv"], f32).reshape(16, 2, 64)
        sbi[0, c] = np.asarray(r["sbiT"], f32).T
        sfc[0, c] = np.asarray(r["sfcT"], f32).reshape(128, NFC, 2).transpose(2, 1, 0).reshape(2, DFF)
    return (y, ys, pak, pav, pbk, pbv, pbi, pmk, pmv, pfc, sak, sav, sbk, sbv, sbi, sfc)
```

```python
import math
from contextlib import ExitStack

import numpy as np
import concourse.bass as bass
import concourse.mybir as mybir
from concourse.bass_utils import run_bass_kernel_spmd

F32 = mybir.dt.float32
BF16 = mybir.dt.bfloat16
AF = mybir.ActivationFunctionType
ALU = mybir.AluOpType

D = 1024
KC = 8
NT = 32
NCOL = 2952
C_QA, C_KA, C_QB, C_KB, C_QI, C_KI, C_VA, C_VB, C_WI = 0, 512, 1024, 1536, 1664, 2176, 2304, 2816, 2944
DFF = 2816
NFC = 22
ALPHA = 2.0 ** 0.25
LN_EPS = 1e-5
NEGM = -30000.0
NIT = 16
BIS_W0 = 16.0
ABW = 640
BNW = 256
NBLK = 18


class Res:
    __slots__ = ("lw", "rd", "name", "excl")

    def __init__(self, name="", excl=False):
        self.lw = None
        self.rd = {}
        self.name = name
        self.excl = excl


def _call(name, *args, **kw):
    return lambda e: getattr(e, name)(*args, **kw)


class Prog:
    ENG = ("pe", "act", "dve", "pool", "sp")

    def __init__(self, nc, sems, dma_sems):
        self.nc = nc
        self.streams = {e: [] for e in self.ENG}
        self.sem = sems
        self.cnt = {e: 0 for e in self.ENG}
        self.seen = {e: {} for e in self.ENG}
        self.dsems = dma_sems
        self.dval = [0] * len(dma_sems)
        self.dnext = 0
        self.semh = dict(sems)
        for i, h in enumerate(dma_sems):
            self.semh[("d", i)] = h
        self.ninst = 0
        self.dead = False
        self.deferred = []
        self.defer_lag = 48

    def _deps(self, reads, writes, eng=None):
        d = {}
        for r in reads:
            if r.lw is not None:
                k, v = r.lw
                if d.get(k, 0) < v:
                    d[k] = v
            if r.excl:
                for k, v in r.rd.items():
                    if k != eng and d.get(k, 0) < v:
                        d[k] = v
        for w in writes:
            if w.lw is not None:
                k, v = w.lw
                if d.get(k, 0) < v:
                    d[k] = v
            for k, v in w.rd.items():
                if d.get(k, 0) < v:
                    d[k] = v
        return d

    def _wait(self, eng, deps):
        for k, v in deps.items():
            if k == "pe" and eng == "pe":
                continue
            if self.seen[eng].get(k, 0) >= v:
                continue
            self.seen[eng][k] = v
            h = self.semh[k]
            self.streams[eng].append(lambda e, h=h, v=v: e.wait_ge(h, v))

    def _flush_deferred(self, force=False, reads=(), writes=()):
        if not self.deferred:
            return
        conflict = force
        if not conflict:
            ws = set(id(w) for w in writes)
            rs = set(id(r) for r in reads)
            for d in self.deferred:
                dr = set(id(x) for x in d[3])
                dw = set(id(x) for x in d[4])
                if (ws & dr) or (ws & dw) or (rs & dw):
                    conflict = True
                    break
        if conflict:
            pend, self.deferred = self.deferred, []
            for d in pend:
                self._dma_now(d[0], d[1], d[2], d[3], d[4], d[5])
            return
        while self.deferred and self.ninst - self.deferred[0][6] >= self.defer_lag:
            d = self.deferred.pop(0)
            self._dma_now(d[0], d[1], d[2], d[3], d[4], d[5])

    def op(self, eng, fn, reads=(), writes=()):
        if self.dead:
            return
        self._flush_deferred(False, reads, writes)
        self._wait(eng, self._deps(reads, writes, eng))
        self.cnt[eng] += 1
        n = self.cnt[eng]
        h = self.sem[eng]
        self.streams[eng].append(lambda e, fn=fn, h=h: fn(e).then_inc(h, 1))
        self.ninst += 1
        for r in reads:
            if r.rd.get(eng, 0) < n:
                r.rd[eng] = n
        for w in writes:
            w.lw = (eng, n)
            w.rd = {}

    def dma(self, q, out, in_, reads=(), writes=(), slow=False, defer=False):
        if self.dead:
            return
        if defer:
            self._flush_deferred(False, reads, writes)
            self.deferred.append((q, out, in_, list(reads), list(writes), slow, self.ninst))
            return
        self._flush_deferred(False, reads, writes)
        self._dma_now(q, out, in_, reads, writes, slow)

    def _dma_now(self, q, out, in_, reads=(), writes=(), slow=False):
        deps = self._deps(reads, writes)
        i = self.dnext
        self.dnext = (i + 1) % len(self.dsems)
        k = ("d", i)
        if self.dval[i] > 0 and deps.get(k, 0) < self.dval[i]:
            deps[k] = self.dval[i]
        self._wait(q, deps)
        self.dval[i] += 16
        v = self.dval[i]
        h = self.dsems[i]
        if slow:
            self.streams[q].append(
                lambda e, out=out, in_=in_, h=h: e.dma_start(out=out, in_=in_, allow_slow_non_contiguous=True).then_inc(h, 16))
        else:
            self.streams[q].append(lambda e, out=out, in_=in_, h=h: e.dma_start(out=out, in_=in_).then_inc(h, 16))
        self.ninst += 1
        for r in reads:
            if r.rd.get(k, 0) < v:
                r.rd[k] = v
        for w in writes:
            w.lw = (k, v)
            w.rd = {}

    def barrier(self):
        if self.dead:
            return
        self._flush_deferred(True)
        deps = {e: self.cnt[e] for e in self.ENG if self.cnt[e] > 0}
        for i, v in enumerate(self.dval):
            if v > 0:
                deps[("d", i)] = v
        for e in self.ENG:
            self._wait(e, dict(deps))

    def finish(self):
        self._flush_deferred(True)
        deps = {("d", i): v for i, v in enumerate(self.dval) if v > 0}
        self._wait("sp", deps)

    def flush(self, block):
        self._flush_deferred(True)
        s = self.streams
        self.streams = {e: [] for e in self.ENG}

        def mk(lst):
            def body(e):
                for f in lst:
                    f(e)
            return body

        block.tensor(mk(s["pe"]))
        block.scalar(mk(s["act"]))
        block.vector(mk(s["dve"]))
        block.gpsimd(mk(s["pool"]))
        block.sync(mk(s["sp"]))


def build_program(debug=False, stop_at=None):
    nc = bass.Bass("TRN2", target_bir_lowering=False)

    def din(name, shape, dt=F32):
        return nc.dram_tensor(name, list(shape), dt, kind="ExternalInput").ap()

    def dout(name, shape, dt=F32):
        return nc.dram_tensor(name, list(shape), dt, kind="ExternalOutput").ap()

    def dscr(name, shape, dt):
        return nc.dram_tensor(name, list(shape), dt, kind="Internal").ap()

    I = {}
    I["xkT"] = din("xkT", [NT, 128, 1024])
    I["xsT"] = din("xsT", [128, 8 * 16])
    I["xres"] = din("xres", [NBLK * 128, 1024])
    I["win"] = din("win", [128, KC * NCOL])
    I["wo"] = din("wo", [128, 8 * 1024])
    I["wmq"] = din("wmq", [128, 8 * 512])
    I["wmk"] = din("wmk", [128, 8 * 512])
    I["wmv"] = din("wmv", [128, 8 * 512])
    I["wmo"] = din("wmo", [128, 4 * 1024])
    I["wup"] = din("wup", [NFC, 128, 8 * 256])
    I["wdown"] = din("wdown", [128, NFC * 1024])
    I["lnp"] = din("lnp", [6, 1024])
    I["wconvT"] = din("wconvT", [128, NFC * 3])
    I["bconvT"] = din("bconvT", [128, NFC])
    I["memT"] = din("memT", [128, 8 * 256])
    I["cmkT"] = din("cmkT", [128, 4 * 256])
    I["cmv"] = din("cmv", [256, 512])
    I["cakT"] = din("cakT", [128, 4 * 512])
    I["cav"] = din("cav", [512, 512])
    I["cbkT"] = din("cbkT", [128, 2048])
    I["cbv"] = din("cbv", [2048, 128])
    I["cbiT"] = din("cbiT", [128, 2048])
    I["sconvT"] = din("sconvT", [128, NFC * 2])
    I["ident"] = din("ident", [128, 128])
    I["AB"] = din("AB", [128, 8 * ABW])
    I["ABs"] = din("ABs", [16, 8 * 528])
    I["Bn"] = din("Bn", [128, 8 * BNW])
    I["Bns"] = din("Bns", [16, 8 * 144])
    I["C15"] = din("C15", [128, 8])
    I["colmask"] = din("colmask", [128, 1])
    I["diagmask"] = din("diagmask", [128, 128])
    I["kvalid"] = din("kvalid", [128, NT])
    I["flag"] = din("flag", [128, 1])

    O = {}
    O["y"] = dout("y", [2048, 1024])
    O["ys"] = dout("ys", [16, 1024])
    O["akT"] = dout("akT", [128, 4 * 512])
    O["av"] = dout("av", [512, 512])
    O["bkT"] = dout("bkT", [128, 4096])
    O["bv"] = dout("bv", [4096, 128])
    O["biT"] = dout("biT", [64, 4096])
    O["mkT"] = dout("mkT", [128, 4 * 256])
    O["mv"] = dout("mv", [256, 512])
    O["fcT"] = dout("fcT", [128, NFC * 2])
    O["sakT"] = dout("sakT", [128, 4 * 16])
    O["sav"] = dout("sav", [16, 512])
    O["sbkT"] = dout("sbkT", [128, 16])
    O["sbv"] = dout("sbv", [16, 128])
    O["sbiT"] = dout("sbiT", [64, 16])
    O["sfcT"] = dout("sfcT", [128, NFC * 2])
    if debug:
        O["dbg_mix"] = dout("dbg_mix", [NBLK * 128, 1024], BF16)
        O["dbg_h2"] = dout("dbg_h2", [NBLK * 128, 1024])
        mixD = O["dbg_mix"]
        h2D = O["dbg_h2"]
    else:
        mixD = dscr("mixD", [NBLK * 128, 1024], BF16)
        h2D = dscr("h2D", [NBLK * 128, 1024], F32)
    h2TD = dscr("h2TD", [NBLK, 128, 1024], BF16)
    R_mixD = [Res("mixD%d" % i) for i in range(NBLK)]
    R_h2D = [Res("h2D%d" % i) for i in range(NBLK)]
    R_h2TD = [Res("h2TD%d" % i) for i in range(NBLK)]

    es = ExitStack()
    with es:
        sems = {e: es.enter_context(nc.semaphore("s_" + e)) for e in Prog.ENG}
        dsems = [es.enter_context(nc.semaphore("d%d" % i)) for i in range(32)]
        P = Prog(nc, sems, dsems)
        block = es.enter_context(nc.Block())

        def checkpoint(name):
            if stop_at is not None and name == stop_at and not P.dead:
                P.finish()
                P.flush(block)
                P.dead = True

        pb = [es.enter_context(nc.psum_tensor("pb%d" % i, [128, 512], F32)) for i in range(8)]
        R_pb = [Res("pb%d" % i, excl=True) for i in range(8)]

        class Rot:
            def __init__(self, idxs):
                self.idxs = idxs
                self.i = 0

            def next(self):
                k = self.idxs[self.i % len(self.idxs)]
                self.i += 1
                return k

        def sb(stack, name, shape, dt):
            return stack.enter_context(nc.sbuf_tensor("sb_" + name, list(shape), dt))

        ident_f = sb(es, "ident_f", [128, 128], F32)
        ident = sb(es, "ident", [128, 512], BF16)
        R_ident = Res("ident")
        P.dma("sp", ident_f[:, :], I["ident"][:, :], writes=[R_ident])
        for r in range(4):
            P.op("act", _call("activation", out=ident[:, r * 128:(r + 1) * 128], in_=ident_f[:, :], func=AF.Copy),
                 reads=[R_ident], writes=[R_ident])

        def run_interleaved(gens):
            gens = [[0.0, i, g] for i, g in enumerate(gens)]
            while gens:
                gens.sort(key=lambda x: (x[0], x[1]))
                ent = gens[0]
                try:
                    c = next(ent[2])
                    ent[0] += (c if c else 1.0)
                except StopIteration:
                    gens.remove(ent)

        with ExitStack() as sa:
            winb = sb(sa, "winb", [128, KC * NCOL], BF16)
            R_win = Res("win")
            kbi = sb(sa, "kbi", [128, 2 * 4096], BF16)
            R_kbi = [Res("kbi%d" % r) for r in range(NT)]
            R_ki = [Res("ki%d" % r) for r in range(NT)]
            vb_aug = sb(sa, "vb_aug", [128, NT * 2 * 65], BF16)
            R_vb = [Res("vb%d" % r) for r in range(NT)]
            kaT = sb(sa, "kaT", [128, 6 * 512], BF16)
            R_ka = [Res("ka%d" % s) for s in range(6)]
            va_aug = sb(sa, "va_aug", [128, 6 * 8 * 65], BF16)
            R_va = [Res("va%d" % s) for s in range(6)]
            ABb = sb(sa, "ABb", [128, 8 * ABW], BF16)
            R_AB = Res("AB")
            Bnb = sb(sa, "Bnb", [128, 8 * BNW], BF16)
            R_Bn = Res("Bn")
            Mnear = [sb(sa, "Mnear%d" % k, [128, 8 * BNW], BF16) for k in range(2)]
            R_Mnear = [Res("Mnear%d" % k) for k in range(2)]
            score = [sb(sa, "score%d" % k, [128, 4096], F32) for k in range(2)]
            R_score = [Res("score%d" % k) for k in range(2)]
            Mb = [sb(sa, "Mb%d" % k, [128, 4096], BF16) for k in range(2)]
            R_M = [Res("M%d" % k) for k in range(2)]
            relu = [sb(sa, "relu%d" % k, [128, 512], BF16) for k in range(3)]
            R_relu = [Res("relu%d" % k) for k in range(3)]
            xstg2 = [sb(sa, "xstg%d" % k, [128, 1024], F32) for k in range(2)]
            R_xstg2 = [Res("xstg%d" % k) for k in range(2)]
            xstg, R_xstg = xstg2[0], R_xstg2[0]
            xTb = [sb(sa, "xTb%d" % k, [128, 1024], BF16) for k in range(2)]
            R_xT = [Res("xT%d" % k) for k in range(2)]
            qaz = [sb(sa, "qaz%d" % k, [128, 1024], BF16) for k in range(2)]
            qbz = [sb(sa, "qbz%d" % k, [128, 1024], BF16) for k in range(3)]
            qiz = [sb(sa, "qiz%d" % k, [128, 1024], BF16) for k in range(2)]
            R_qa = [Res("qa%d" % k) for k in range(2)]
            R_qb = [Res("qb%d" % k) for k in range(3)]
            R_qi = [Res("qi%d" % k) for k in range(2)]
            coef = [sb(sa, "coef%d" % k, [128, 8], F32) for k in range(2)]
            R_coef = [Res("coef%d" % k) for k in range(2)]
            dg = [sb(sa, "dg%d" % k, [128, 1024], BF16) for k in range(2)]
            R_dg = [Res("dg%d" % k) for k in range(2)]
            PTA = [sb(sa, "PTA%d" % k, [128, 512], BF16) for k in range(3)]
            R_PTA = [Res("PTA%d" % k) for k in range(3)]
            PTB = [sb(sa, "PTB%d" % k, [128, 512], BF16) for k in range(3)]
            R_PTB = [Res("PTB%d" % k) for k in range(3)]
            mixb = [sb(sa, "mixb%d" % k, [128, 1024], BF16) for k in range(3)]
            R_mix = [Res("mix%d" % k) for k in range(3)]
            ostg = [sb(sa, "ostg%d" % k, [128, 256], F32) for k in range(2)]
            R_ostg = [Res("ostg%d" % k) for k in range(2)]
            vbstg = [sb(sa, "vbstg%d" % k, [128, 128], F32) for k in range(2)]
            R_vbstg = [Res("vbstg%d" % k) for k in range(2)]
            astg = sb(sa, "astg", [128, 1024], F32)
            R_astg = Res("astg")
            small = [sb(sa, "small%d" % k, [128, 16], F32) for k in range(2)]
            R_small = [Res("small%d" % k) for k in range(2)]
            recA = [sb(sa, "recA%d" % k, [128, 8], F32) for k in range(2)]
            R_recA = [Res("recA%d" % k) for k in range(2)]
            recB = [sb(sa, "recB%d" % k, [128, 8], F32) for k in range(2)]
            R_recB = [Res("recB%d" % k) for k in range(2)]
            colmask = sb(sa, "colmask", [128, 1], F32)
            diagm = sb(sa, "diagm", [128, 128], F32)
            kvalid = sb(sa, "kvalid", [128, NT], F32)
            c15 = sb(sa, "c15", [128, 8], F32)
            ones8 = sb(sa, "ones8", [128, 8], F32)
            R_cst = Res("cst")

            wrot = Rot([0, 1, 2])

            P.dma("sp", colmask[:, :], I["colmask"][:, :], writes=[R_cst])
            P.dma("sp", diagm[:, :], I["diagmask"][:, :], writes=[R_cst])
            P.dma("sp", kvalid[:, :], I["kvalid"][:, :], writes=[R_cst])
            P.dma("sp", c15[:, :], I["C15"][:, :], writes=[R_cst])
            P.op("pool", _call("memset", ones8[:, :], 1.0), writes=[R_cst])
            for k in range(2):
                P.op("pool", _call("memset", qaz[k][:, :], 0.0), writes=[R_qa[k]])
                P.op("pool", _call("memset", qiz[k][:, :], 0.0), writes=[R_qi[k]])
            for k in range(3):
                P.op("pool", _call("memset", qbz[k][:, :], 0.0), writes=[R_qb[k]])

            HW = NCOL // 2
            R_slot = [Res("wslot%d" % q) for q in range(4)]
            for kc in range(KC):
                for hh in range(2):
                    q = (kc * 2 + hh) % 4
                    stg = score[q // 2][:, (q % 2) * HW:(q % 2 + 1) * HW]
                    P.dma("sp", stg, I["win"][:, kc * NCOL + hh * HW: kc * NCOL + (hh + 1) * HW], writes=[R_slot[q]])
                    if hh == 0:
                        P.op("act", _call("activation", out=winb[:, kc * NCOL + hh * HW: kc * NCOL + (hh + 1) * HW], in_=stg, func=AF.Copy),
                             reads=[R_slot[q]], writes=[R_win])
                    else:
                        P.op("dve", _call("tensor_copy", out=winb[:, kc * NCOL + hh * HW: kc * NCOL + (hh + 1) * HW], in_=stg),
                             reads=[R_slot[q]], writes=[R_win])
            for hh in range(2):
                w = 4 * ABW
                P.dma("sp", score[hh][:, 0:w], I["AB"][:, hh * w:(hh + 1) * w], writes=[R_score[hh], R_slot[2 * hh], R_slot[2 * hh + 1]])
                P.op("act", _call("activation", out=ABb[:, hh * w:(hh + 1) * w], in_=score[hh][:, 0:w], func=AF.Copy),
                     reads=[R_score[hh]], writes=[R_AB])
            P.dma("sp", score[0][:, 0:8 * BNW], I["Bn"][:, :], writes=[R_score[0]])
            for h in range(8):
                P.op("dve", _call("tensor_scalar", out=Bnb[:, h * BNW:(h + 1) * BNW], in0=score[0][:, h * BNW:(h + 1) * BNW],
                                  scalar1=c15[:, h:h + 1], scalar2=None, op0=ALU.subtract),
                     reads=[R_score[0], R_cst], writes=[R_Bn])
            checkpoint('consts')

            def win_cols(kc, c0, n):
                return winb[:, kc * NCOL + c0: kc * NCOL + c0 + n]

            def fm_proj(bank, xT, R_x, N, col0, nchunks, ocol=0):
                for j in range(nchunks):
                    for kc in range(KC):
                        P.op("pe", _call("matmul", out=pb[bank][:, ocol + j * N: ocol + (j + 1) * N], lhsT=win_cols(kc, col0 + j * 128, 128),
                                         rhs=xT[:, kc * N:(kc + 1) * N], start=(kc == 0), stop=(kc == KC - 1)),
                             reads=[R_win, R_x], writes=[R_pb[bank]])

            def tm_proj(bank, xT, R_x, N, col0, ncols, ocol=0):
                for kc in range(KC):
                    P.op("pe", _call("matmul", out=pb[bank][0:N, ocol:ocol + ncols], lhsT=xT[:, kc * N:(kc + 1) * N],
                                     rhs=win_cols(kc, col0, ncols), start=(kc == 0), stop=(kc == KC - 1)),
                         reads=[R_win, R_x], writes=[R_pb[bank]])

            def load_xT(r, eng="pool"):
                s = r % 2
                P.dma("sp", xstg2[s][:, :], I["xkT"][r], writes=[R_xstg2[s]])
                P.op(eng, _call("tensor_copy", out=xTb[s][:, :], in_=xstg2[s][:, :]), reads=[R_xstg2[s]], writes=[R_xT[s]])

            def kside(r, full):
                s = r % 2
                xT, R_x = xTb[s], R_xT[s]
                so = r % 2
                bk = wrot.next()
                fm_proj(bk, xT, R_x, 128, C_KB, 1)
                fm_proj(bk, xT, R_x, 128, C_KI, 1, ocol=128)
                P.op("act", _call("activation", out=ostg[so][:, :], in_=pb[bk][:, 0:256], func=AF.Copy), reads=[R_pb[bk]], writes=[R_ostg[so]])
                P.op("pool", _call("tensor_copy", out=kbi[:, :].rearrange("p (a c) -> p a c", a=2)[:, :, r * 128:(r + 1) * 128],
                                   in_=ostg[so][:, :].rearrange("p (a c) -> p a c", a=2)),
                     reads=[R_ostg[so]], writes=[R_kbi[r], R_ki[r]])
                P.dma("sp", O["bkT"][:, r * 128:(r + 1) * 128], ostg[so][:, 0:128], reads=[R_ostg[so]], defer=True)
                P.dma("sp", O["biT"][:, r * 128:(r + 1) * 128], ostg[so][0:64, 128:256], reads=[R_ostg[so]], defer=True)
                yield 3.0
                bv_ = wrot.next()
                tm_proj(bv_, xT, R_x, 128, C_VB, 128)
                vbv = vb_aug[:, r * 130:(r + 1) * 130].rearrange("p (g d) -> p g d", d=65)
                P.op("act", _call("activation", out=vbstg[so][:, :], in_=pb[bv_][:, 0:128], func=AF.Copy), reads=[R_pb[bv_]], writes=[R_vbstg[so]])
                P.op("pool", _call("tensor_copy", out=vbv[:, :, 0:64], in_=vbstg[so][:, :].rearrange("p (g d) -> p g d", d=64)),
                     reads=[R_vbstg[so]], writes=[R_vb[r]])
                P.op("pool", _call("tensor_scalar", out=vbv[:, :, 64:65], in0=ones8[:, 0:2].rearrange("p (g o) -> p g o", o=1),
                                   scalar1=kvalid[:, r:r + 1], scalar2=None, op0=ALU.mult),
                     reads=[R_cst], writes=[R_vb[r]])
                P.dma("sp", O["bv"][r * 128:(r + 1) * 128, :], vbstg[so][:, :], reads=[R_vbstg[so]], defer=True)
                yield 3.0
                if not full:
                    return
                slot = r % 6
                ba = wrot.next()
                fm_proj(ba, xT, R_x, 128, C_KA, 4)
                P.op("act", _call("activation", out=kaT[:, slot * 512:(slot + 1) * 512], in_=pb[ba][:, :], func=AF.Copy),
                     reads=[R_pb[ba]], writes=[R_ka[slot]])
                if r >= 28:
                    P.op("dve", _call("tensor_copy", out=astg[:, 0:512], in_=pb[ba][:, :]), reads=[R_pb[ba]], writes=[R_astg])
                    P.dma("sp", O["akT"].rearrange("p (j t) -> p j t", t=512)[:, :, (r - 28) * 128:(r - 27) * 128],
                          astg[:, 0:512].rearrange("p (j t) -> p j t", t=128), reads=[R_astg], defer=True)
                yield 3.0
                bva = wrot.next()
                tm_proj(bva, xT, R_x, 128, C_VA, 512)
                vav = va_aug[:, slot * 520:(slot + 1) * 520].rearrange("p (h d) -> p h d", d=65)
                P.op("act", _call("activation", out=vav[:, :, 0:64], in_=pb[bva][:, :].rearrange("p (h d) -> p h d", d=64), func=AF.Copy),
                     reads=[R_pb[bva]], writes=[R_va[slot]])
                P.op("pool", _call("tensor_scalar", out=vav[:, :, 64:65], in0=ones8[:, :].rearrange("p (h o) -> p h o", o=1),
                                   scalar1=kvalid[:, r:r + 1], scalar2=None, op0=ALU.mult),
                     reads=[R_cst], writes=[R_va[slot]])
                if r >= 28:
                    P.op("dve", _call("tensor_copy", out=astg[:, 512:1024], in_=pb[bva][:, :]), reads=[R_pb[bva]], writes=[R_astg])
                    P.dma("sp", O["av"][(r - 28) * 128:(r - 27) * 128, :], astg[:, 512:1024], reads=[R_astg], defer=True)
                yield 3.0

            def qside(xT, R_x, qs, st, st3):
                b1 = wrot.next()
                fm_proj(b1, xT, R_x, qs, C_QA, 4)
                for hf in range(2):
                    P.op("act", _call("activation",
                                      out=qaz[st][hf * 64:(hf + 1) * 64, 0:8 * qs].rearrange("p (j two q) -> p j two q", two=2, q=qs)[:, :, hf, :],
                                      in_=pb[b1][hf * 64:(hf + 1) * 64, 0:4 * qs].rearrange("p (j q) -> p j q", q=qs), func=AF.Copy, scale=0.125),
                         reads=[R_pb[b1]], writes=[R_qa[st]])
                yield 3.0
                b2 = wrot.next()
                fm_proj(b2, xT, R_x, qs, C_QB, 4)
                for g in range(2):
                    P.op("act", _call("activation", out=qbz[st3][g * 64:(g + 1) * 64, g * 4 * qs:(g + 1) * 4 * qs],
                                      in_=pb[b2][g * 64:(g + 1) * 64, 0:4 * qs], func=AF.Copy, scale=0.125),
                         reads=[R_pb[b2]], writes=[R_qb[st3]])
                yield 3.0
                b3 = wrot.next()
                fm_proj(b3, xT, R_x, qs, C_QI, 4)
                for hf in range(2):
                    P.op("act", _call("activation",
                                      out=qiz[st][hf * 64:(hf + 1) * 64, 0:8 * qs].rearrange("p (j two q) -> p j two q", two=2, q=qs)[:, :, hf, :],
                                      in_=pb[b3][hf * 64:(hf + 1) * 64, 0:4 * qs].rearrange("p (j q) -> p j q", q=qs), func=AF.Copy),
                         reads=[R_pb[b3]], writes=[R_qi[st]])
                b4 = wrot.next()
                tm_proj(b4, xT, R_x, qs, C_WI, 8)
                P.op("dve", _call("tensor_scalar", out=coef[st][0:qs, :], in0=pb[b4][0:qs, 0:8], scalar1=float(8.0 ** -1.5), scalar2=None, op0=ALU.mult),
                     reads=[R_pb[b4]], writes=[R_coef[st]])
                for h in range(8):
                    P.op("pool", _call("tensor_scalar", out=dg[st][0:qs, h * 128: h * 128 + qs], in0=ident_f[0:qs, 0:qs],
                                       scalar1=coef[st][0:qs, h:h + 1], scalar2=None, op0=ALU.mult),
                         reads=[R_coef[st], R_ident], writes=[R_dg[st]])
                yield 3.0

            def normalize(bank, qs, mixt, R_m, col0, rec, R_rec):
                ov = pb[bank][0:qs, 0:260].rearrange("p (h d) -> p h d", d=65)
                P.op("dve", _call("tensor_scalar", out=rec[0:qs, 0:4].rearrange("p (h o) -> p h o", o=1), in0=ov[:, :, 64:65],
                                  scalar1=1e-30, scalar2=None, op0=ALU.max),
                     reads=[R_pb[bank]], writes=[R_rec])
                P.op("dve", _call("reciprocal", out=rec[0:qs, 0:4], in_=rec[0:qs, 0:4]), reads=[R_rec], writes=[R_rec])
                for hh in range(4):
                    P.op("dve", _call("tensor_scalar", out=mixt[0:qs, col0 + hh * 64: col0 + (hh + 1) * 64],
                                      in0=pb[bank][0:qs, hh * 65: hh * 65 + 64],
                                      scalar1=rec[0:qs, hh:hh + 1], scalar2=None, op0=ALU.mult),
                         reads=[R_pb[bank], R_rec], writes=[R_m])

            def pipe3(items, s1, s2, s3, D, cost=1.0):
                pend = []
                for it in items:
                    s1(it)
                    s2(it)
                    pend.append(it)
                    if len(pend) > D:
                        s3(pend.pop(0))
                    yield cost
                while pend:
                    s3(pend.pop(0))
                    yield cost

            pta_rot = Rot([0, 1, 2])
            relu_rot = Rot([0, 1, 2])
            ptb_rot = Rot([0, 1, 2])
            brot = Rot([3, 7])

            def front_attn(sn, qs, wins, btiles, prompt_masks, abw):
                st = sn % 2
                mixt, R_m = mixb[sn % 3], R_mix[sn % 3]
                nw = len(wins)

                units = []
                for h in range(8):
                    units.append({"h": h, "t0": 0, "tiles": wins[0:4]})
                    if nw > 4:
                        units.append({"h": h, "t0": 4, "tiles": wins[4:5]})

                def a1(u):
                    h = u["h"]
                    j = h // 2
                    bank = wrot.next()
                    u["bank"] = bank
                    for i, (slot, ts) in enumerate(u["tiles"]):
                        t = u["t0"] + i
                        c0 = i * qs
                        P.op("pe", _call("matmul", out=pb[bank][0:ts, c0:c0 + qs], lhsT=kaT[:, slot * 512 + j * 128: slot * 512 + j * 128 + ts],
                                         rhs=qaz[st][:, h * qs:(h + 1) * qs], start=True, stop=False),
                             reads=[R_ka[slot], R_qa[st]], writes=[R_pb[bank]])
                        P.op("pe", _call("matmul", out=pb[bank][0:ts, c0:c0 + qs], lhsT=ABb[0:qs, h * abw + t * 128: h * abw + t * 128 + ts],
                                         rhs=ident[0:qs, 0:qs], start=False, stop=True),
                             reads=[R_AB, R_ident], writes=[R_pb[bank]])

                def a2(u):
                    k = pta_rot.next()
                    u["pt"], u["R_pt"] = PTA[k], R_PTA[k]
                    bank = u["bank"]
                    tsm = max(ts for (_, ts) in u["tiles"])
                    n = len(u["tiles"])
                    P.op("act", _call("activation", out=u["pt"][0:tsm, 0:n * qs], in_=pb[bank][0:tsm, 0:n * qs], func=AF.Exp),
                         reads=[R_pb[bank]], writes=[u["R_pt"]])

                def a3(u):
                    h = u["h"]
                    last_unit = (u["t0"] + len(u["tiles"]) == nw)
                    for i, (slot, ts) in enumerate(u["tiles"]):
                        t = u["t0"] + i
                        P.op("pe", _call("matmul", out=pb[4][0:qs, (h % 4) * 65:(h % 4) * 65 + 65], lhsT=u["pt"][0:ts, i * qs:(i + 1) * qs],
                                         rhs=va_aug[0:ts, slot * 520 + h * 65: slot * 520 + h * 65 + 65],
                                         start=(h % 4 == 0 and t == 0), stop=(t == nw - 1), skip_group_check=True),
                             reads=[u["R_pt"], R_va[slot]], writes=[R_pb[4]])
                    if last_unit and h % 4 == 3:
                        normalize(4, qs, mixt, R_m, (h // 4) * 256, recA[st], R_recA[st])

                yield from pipe3(units, a1, a2, a3, 2, 0.9)

                L = btiles[-1][1] + btiles[-1][2]
                items = []
                cc = 0
                for c0 in range(0, L, 512):
                    w = min(512, L - c0)
                    rk = [R_ki[tt[0]] for tt in btiles if tt[1] >= c0 - 127 and tt[1] < c0 + w]
                    for h in range(8):
                        items.append({"c0": c0, "w": w, "h": h, "sc": (5, 4)[cc % 2], "rk": rk})
                    cc += 1

                def i1(it):
                    bank = wrot.next()
                    it["bank"] = bank
                    h, c0, w = it["h"], it["c0"], it["w"]
                    P.op("pe", _call("matmul", out=pb[bank][0:qs, 0:w], lhsT=qiz[st][:, h * qs:(h + 1) * qs],
                                     rhs=kbi[:, 4096 + c0: 4096 + c0 + w], start=True, stop=True),
                         reads=[R_qi[st]] + it["rk"], writes=[R_pb[bank]])

                def i2(it):
                    k = relu_rot.next()
                    it["rl"], it["R_rl"] = relu[k], R_relu[k]
                    w = it["w"]
                    P.op("act", _call("activation", out=it["rl"][0:qs, 0:w], in_=pb[it["bank"]][0:qs, 0:w], func=AF.Relu),
                         reads=[R_pb[it["bank"]]], writes=[it["R_rl"]])

                def i3(it):
                    h, c0, w, sc = it["h"], it["c0"], it["w"], it["sc"]
                    P.op("pe", _call("matmul", out=pb[sc][0:qs, 0:w], lhsT=dg[st][0:qs, h * 128: h * 128 + qs], rhs=it["rl"][0:qs, 0:w],
                                     start=(h == 0), stop=(h == 7)),
                         reads=[R_dg[st], it["R_rl"]], writes=[R_pb[sc]])
                    if h == 7:
                        if prompt_masks and c0 < 2048:
                            wm = min(w, 2048 - c0)
                            P.op("act", _call("activation", out=score[st][0:qs, c0:c0 + wm], in_=pb[sc][0:qs, 0:wm], func=AF.Identity,
                                              bias=colmask[0:qs, 0:1]),
                                 reads=[R_pb[sc], R_cst], writes=[R_score[st]])
                            if wm < w:
                                P.op("act", _call("activation", out=score[st][0:qs, c0 + wm:c0 + w], in_=pb[sc][0:qs, wm:w], func=AF.Copy),
                                     reads=[R_pb[sc]], writes=[R_score[st]])
                        else:
                            P.op("act", _call("activation", out=score[st][0:qs, c0:c0 + w], in_=pb[sc][0:qs, 0:w], func=AF.Copy),
                                 reads=[R_pb[sc]], writes=[R_score[st]])

                yield from pipe3(items, i1, i2, i3, 2, 0.65)
                if prompt_masks:
                    P.op("dve", _call("tensor_tensor", out=score[st][0:qs, L - 128:L], in0=score[st][0:qs, L - 128:L], in1=diagm[0:qs, :], op=ALU.add),
                         reads=[R_score[st], R_cst], writes=[R_score[st]])
                yield

            def bis_gen(sn, qs, btiles, bnw):
                st = sn % 2
                sm, R_sm = small[st], R_small[st]
                L = btiles[-1][1] + btiles[-1][2]
                P.op("dve", _call("memset", sm[0:qs, 1:2], 0.0), writes=[R_sm])
                for k in range(NIT):
                    wk = BIS_W0 / (2.0 ** k)
                    P.op("dve", _call("tensor_scalar", out=Mb[st][0:qs, 0:L], in0=score[st][0:qs, 0:L], scalar1=sm[0:qs, 1:2], scalar2=None,
                                      op0=ALU.is_ge, op1=ALU.add, accum_out=sm[0:qs, 0:1]),
                         reads=[R_score[st], R_sm], writes=[R_M[st], R_sm])
                    P.op("dve", _call("tensor_scalar", out=sm[0:qs, 2:3], in0=sm[0:qs, 0:1], scalar1=255.5, scalar2=wk,
                                      op0=ALU.is_ge, op1=ALU.mult),
                         reads=[R_sm], writes=[R_sm])
                    P.op("dve", _call("scalar_tensor_tensor", out=sm[0:qs, 1:2], in0=sm[0:qs, 2:3], scalar=-wk / 2.0,
                                      in1=sm[0:qs, 1:2], op0=ALU.add, op1=ALU.add),
                         reads=[R_sm], writes=[R_sm])
                    yield L * 1.08e-3 + 0.5
                wl = BIS_W0 / (2.0 ** (NIT - 1)) / 2.0
                P.op("dve", _call("tensor_scalar", out=sm[0:qs, 3:4], in0=sm[0:qs, 1:2], scalar1=-wl, scalar2=None, op0=ALU.add),
                     reads=[R_sm], writes=[R_sm])
                P.op("dve", _call("tensor_scalar", out=Mb[st][0:qs, 0:L], in0=score[st][0:qs, 0:L], scalar1=sm[0:qs, 3:4], scalar2=NEGM,
                                  op0=ALU.is_lt, op1=ALU.mult),
                     reads=[R_score[st], R_sm], writes=[R_M[st]])
                nearw = btiles[-2][2] + btiles[-1][2]
                for h in range(8):
                    P.op("dve", _call("tensor_tensor", out=Mnear[st][0:qs, h * bnw: h * bnw + nearw], in0=Bnb[0:qs, h * bnw: h * bnw + nearw],
                                      in1=Mb[st][0:qs, L - nearw:L], op=ALU.add),
                         reads=[R_Bn, R_M[st]], writes=[R_Mnear[st]])
                yield
            def battn_gen(sn, qs, btiles, blk, bnw):
                st = sn % 2
                st3 = sn % 3
                mixt, R_m = mixb[st3], R_mix[st3]
                nb = len(btiles)
                items = [{"g": g, "t": t, "vt": vt, "c0": c0, "ts": ts} for g in range(2) for t, (vt, c0, ts) in enumerate(btiles)]

                def b1(it):
                    g, t, vt, c0, ts = it["g"], it["t"], it["vt"], it["c0"], it["ts"]
                    bank = brot.next()
                    it["bank"] = bank
                    P.op("pe", _call("matmul", out=pb[bank][0:ts, 0:4 * qs], lhsT=kbi[:, c0:c0 + ts],
                                     rhs=qbz[st3][:, g * 4 * qs:(g + 1) * 4 * qs], start=True, stop=False),
                         reads=[R_kbi[vt], R_qb[st3]], writes=[R_pb[bank]])
                    if t < nb - 2 and qs == 128:
                        P.op("pe", _call("matmul", out=pb[bank][0:ts, 0:512], lhsT=Mb[st][0:qs, c0:c0 + ts], rhs=ident[0:128, 0:512],
                                         start=False, stop=True),
                             reads=[R_M[st], R_ident], writes=[R_pb[bank]])
                    elif t < nb - 2:
                        for r in range(4):
                            P.op("pe", _call("matmul", out=pb[bank][0:ts, r * qs:(r + 1) * qs], lhsT=Mb[st][0:qs, c0:c0 + ts],
                                             rhs=ident[0:qs, 0:qs], start=False, stop=(r == 3)),
                                 reads=[R_M[st], R_ident], writes=[R_pb[bank]])
                    else:
                        tt = t - (nb - 2)
                        for r in range(4):
                            hh = g * 4 + r
                            P.op("pe", _call("matmul", out=pb[bank][0:ts, r * qs:(r + 1) * qs],
                                             lhsT=Mnear[st][0:qs, hh * bnw + tt * 128: hh * bnw + tt * 128 + ts], rhs=ident[0:qs, 0:qs],
                                             start=False, stop=(r == 3)),
                                 reads=[R_Mnear[st], R_ident], writes=[R_pb[bank]])

                def b2(it):
                    k = ptb_rot.next()
                    it["ptb"], it["R_ptb"] = PTB[k], R_PTB[k]
                    ts = it["ts"]
                    P.op("act", _call("activation", out=it["ptb"][0:ts, 0:4 * qs], in_=pb[it["bank"]][0:ts, 0:4 * qs], func=AF.Exp),
                         reads=[R_pb[it["bank"]]], writes=[it["R_ptb"]])

                def b3(it):
                    g, t, vt, ts = it["g"], it["t"], it["vt"], it["ts"]
                    for r in range(4):
                        P.op("pe", _call("matmul", out=pb[6][0:qs, r * 65: r * 65 + 65], lhsT=it["ptb"][0:ts, r * qs:(r + 1) * qs],
                                         rhs=vb_aug[0:ts, (vt * 2 + g) * 65:(vt * 2 + g) * 65 + 65],
                                         start=(t == 0 and r == 0), stop=(t == nb - 1), skip_group_check=True),
                             reads=[it["R_ptb"], R_vb[vt]], writes=[R_pb[6]])
                    if t == nb - 1:
                        normalize(6, qs, mixt, R_m, 512 + g * 256, recB[st], R_recB[st])

                yield from pipe3(items, b1, b2, b3, 1, 0.8)
                P.dma("sp", mixD[blk * 128: blk * 128 + qs, :], mixt[0:qs, :], reads=[R_m], writes=[R_mixD[blk]], defer=True)
                yield

            load_xT(0, "dve")
            for r in range(16):
                if r + 1 < 16:
                    load_xT(r + 1, "dve")
                for _ in kside(r, full=(r >= 11)):
                    pass
            checkpoint('phase0')

            def prompt_front(sn, T):
                if T >= 16:
                    load_xT(T)
                    yield from kside(T, full=True)
                s = T % 2
                yield from qside(xTb[s], R_xT[s], 128, sn % 2, sn % 3)
                wins = [((T - 4 + t) % 6, 128) for t in range(5)]
                btiles = [(t, t * 128, 128) for t in range(T + 1)]
                yield from front_attn(sn, 128, wins, btiles, True, ABW)

            def prompt_bis(sn, T):
                btiles = [(t, t * 128, 128) for t in range(T + 1)]
                yield from bis_gen(sn, 128, btiles, BNW)

            def prompt_battn(sn, T, blk):
                btiles = [(t, t * 128, 128) for t in range(T + 1)]
                yield from battn_gen(sn, 128, btiles, blk, BNW)

            steps = [(0, 15, 16)] + [(1 + i, 16 + i, i) for i in range(16)]
            ns = len(steps)
            SN = ns
            sst = SN % 2
            s_wins = [(0, 128), (1, 128), (2, 128), (3, 128), (4, 16)]
            s_btiles = [(t, t * 128, 128) for t in range(16)] + [(16, 2048, 16)]
            xs_, R_xs = xTb[0], R_xT[0]

            def sample_front():
                stg, R_stg = score[sst], R_score[sst]
                P.dma("sp", stg[:, 0:2048], I["cbiT"][:, :], writes=[R_stg])
                P.op("act", _call("activation", out=kbi[:, 4096:4096 + 2048], in_=stg[:, 0:2048], func=AF.Copy),
                     reads=[R_stg], writes=R_ki[0:16])
                P.dma("sp", stg[:, 2048:4096], I["cakT"][:, :], writes=[R_stg])
                for s4 in range(4):
                    P.op("act", _call("activation", out=kaT[:, s4 * 512:(s4 + 1) * 512].rearrange("p (j t) -> p j t", t=128),
                                      in_=stg[:, 2048:4096].rearrange("p (j t) -> p j t", t=512)[:, :, s4 * 128:(s4 + 1) * 128], func=AF.Copy),
                         reads=[R_stg], writes=[R_ka[s4]])
                yield 3.0
                P.dma("sp", stg[:, 0:2048].rearrange("p (t c) -> p t c", c=512), I["cav"].rearrange("(t p) c -> p t c", p=128), writes=[R_stg])
                vaall = va_aug[:, 0:4 * 520].rearrange("p (t d) -> p t d", d=65)
                P.op("act", _call("activation", out=vaall[:, :, 0:64], in_=stg[:, 0:2048].rearrange("p (t d) -> p t d", d=64), func=AF.Copy),
                     reads=[R_stg], writes=R_va[0:5])
                P.op("pool", _call("memset", va_aug[:, 0:5 * 520].rearrange("p (t d) -> p t d", d=65)[:, :, 64:65], 1.0), writes=R_va[0:5])
                for hh in range(2):
                    w = 4 * 528
                    P.dma("sp", stg[0:16, 0:w], I["ABs"][:, hh * w:(hh + 1) * w], writes=[R_stg])
                    P.op("act", _call("activation", out=ABb[0:16, hh * w:(hh + 1) * w], in_=stg[0:16, 0:w], func=AF.Copy),
                         reads=[R_stg], writes=[R_AB])
                P.op("pool", _call("memset", qaz[sst][:, :], 0.0), writes=[R_qa[sst]])
                P.op("pool", _call("memset", qbz[SN % 3][:, :], 0.0), writes=[R_qb[SN % 3]])
                P.op("pool", _call("memset", qiz[sst][:, :], 0.0), writes=[R_qi[sst]])
                P.dma("sp", xstg[:, 0:128], I["xsT"][:, :], writes=[R_xstg])
                P.op("pool", _call("tensor_copy", out=xTb[0][:, 0:128], in_=xstg[:, 0:128]), reads=[R_xstg], writes=[R_xT[0]])
                yield 3.0
                bk = wrot.next()
                fm_proj(bk, xs_, R_xs, 16, C_KI, 1)
                P.op("act", _call("activation", out=kbi[:, 4096 + 2048:4096 + 2064], in_=pb[bk][:, 0:16], func=AF.Copy), reads=[R_pb[bk]], writes=[R_ki[16]])
                P.op("dve", _call("tensor_copy", out=ostg[0][:, 16:32], in_=pb[bk][:, 0:16]), reads=[R_pb[bk]], writes=[R_ostg[0]])
                P.dma("sp", O["sbiT"][:, :], ostg[0][0:64, 16:32], reads=[R_ostg[0]], defer=True)
                ba = wrot.next()
                fm_proj(ba, xs_, R_xs, 16, C_KA, 4)
                P.op("act", _call("activation", out=kaT[:, 4 * 512:5 * 512].rearrange("p (j t) -> p j t", t=128)[:, :, 0:16],
                                  in_=pb[ba][:, 0:64].rearrange("p (j t) -> p j t", t=16), func=AF.Copy),
                     reads=[R_pb[ba]], writes=[R_ka[4]])
                P.op("dve", _call("tensor_copy", out=astg[:, 0:64], in_=pb[ba][:, 0:64]), reads=[R_pb[ba]], writes=[R_astg])
                P.dma("sp", O["sakT"][:, :], astg[:, 0:64], reads=[R_astg], defer=True)
                bva = wrot.next()
                tm_proj(bva, xs_, R_xs, 16, C_VA, 512)
                vav = va_aug[0:16, 4 * 520:5 * 520].rearrange("p (h d) -> p h d", d=65)
                P.op("act", _call("activation", out=vav[:, :, 0:64], in_=pb[bva][0:16, :].rearrange("p (h d) -> p h d", d=64), func=AF.Copy),
                     reads=[R_pb[bva]], writes=[R_va[4]])
                P.op("dve", _call("tensor_copy", out=astg[0:16, 512:1024], in_=pb[bva][0:16, :]), reads=[R_pb[bva]], writes=[R_astg])
                P.dma("sp", O["sav"][:, :], astg[0:16, 512:1024], reads=[R_astg], defer=True)
                yield 3.0
                yield from qside(xs_, R_xs, 16, sst, SN % 3)
                yield from front_attn(SN, 16, s_wins, s_btiles, False, 528)

            def sample_bis():
                stg, R_stg = score[1 - sst], R_score[1 - sst]
                P.dma("sp", stg[0:16, 0:8 * 144], I["Bns"][:, :], writes=[R_stg])
                for h in range(8):
                    P.op("dve", _call("tensor_scalar", out=Bnb[0:16, h * 144:(h + 1) * 144], in0=stg[0:16, h * 144:(h + 1) * 144],
                                      scalar1=c15[0:16, h:h + 1], scalar2=None, op0=ALU.subtract),
                         reads=[R_stg, R_cst], writes=[R_Bn])
                yield 1.0
                yield from bis_gen(SN, 16, s_btiles, 144)

            def sample_battn():
                stg, R_stg = score[1 - sst], R_score[1 - sst]
                P.dma("sp", stg[:, 0:2048], I["cbkT"][:, :], writes=[R_stg])
                P.op("act", _call("activation", out=kbi[:, 0:2048], in_=stg[:, 0:2048], func=AF.Copy),
                     reads=[R_stg], writes=R_kbi[0:16])
                P.dma("sp", stg[:, 2048:4096].rearrange("p (t c) -> p t c", c=128), I["cbv"].rearrange("(t p) c -> p t c", p=128), writes=[R_stg])
                vball = vb_aug[:, 0:16 * 130].rearrange("p (t d) -> p t d", d=65)
                P.op("act", _call("activation", out=vball[:, :, 0:64], in_=stg[:, 2048:4096].rearrange("p (t d) -> p t d", d=64), func=AF.Copy),
                     reads=[R_stg], writes=R_vb[0:17])
                P.op("pool", _call("memset", vb_aug[:, 0:17 * 130].rearrange("p (t d) -> p t d", d=65)[:, :, 64:65], 1.0), writes=R_vb[0:17])
                bk = wrot.next()
                fm_proj(bk, xs_, R_xs, 16, C_KB, 1)
                P.op("act", _call("activation", out=kbi[:, 2048:2064], in_=pb[bk][:, 0:16], func=AF.Copy), reads=[R_pb[bk]], writes=[R_kbi[16]])
                P.op("dve", _call("tensor_copy", out=ostg[1][:, 0:16], in_=pb[bk][:, 0:16]), reads=[R_pb[bk]], writes=[R_ostg[1]])
                P.dma("sp", O["sbkT"][:, :], ostg[1][:, 0:16], reads=[R_ostg[1]], defer=True)
                bv_ = wrot.next()
                tm_proj(bv_, xs_, R_xs, 16, C_VB, 128)
                vbv = vb_aug[0:16, 16 * 130:17 * 130].rearrange("p (g d) -> p g d", d=65)
                P.op("act", _call("activation", out=vbv[:, :, 0:64], in_=pb[bv_][0:16, 0:128].rearrange("p (g d) -> p g d", d=64), func=AF.Copy),
                     reads=[R_pb[bv_]], writes=[R_vb[16]])
                P.op("dve", _call("tensor_copy", out=vbstg[0][0:16, :], in_=pb[bv_][0:16, 0:128]), reads=[R_pb[bv_]], writes=[R_vbstg[0]])
                P.dma("sp", O["sbv"][:, :], vbstg[0][0:16, :], reads=[R_vbstg[0]], defer=True)
                yield 3.0
                yield from battn_gen(SN, 16, s_btiles, 17, 144)

            for tick in range(ns + 3):
                gens = []
                if 0 <= tick - 2 < ns:
                    gens.append(prompt_battn(*steps[tick - 2]))
                elif tick - 2 == ns:
                    gens.append(sample_battn())
                if 0 <= tick - 1 < ns:
                    gens.append(prompt_bis(*steps[tick - 1][0:2]))
                elif tick - 1 == ns:
                    gens.append(sample_bis())
                if tick < ns:
                    gens.append(prompt_front(*steps[tick][0:2]))
                elif tick == ns:
                    gens.append(sample_front())
                run_interleaved(gens)
            checkpoint('steps')
            checkpoint('phaseA')
            P.flush(block)

        P.barrier()
        with ExitStack() as sbk:
            wob = sb(sbk, "wob", [128, 8 * 1024], BF16)
            wmqb = sb(sbk, "wmqb", [128, 8 * 512], BF16)
            wmob = sb(sbk, "wmob", [128, 4 * 1024], BF16)
            wtmp = sb(sbk, "wtmp", [128, 8 * 512], BF16)
            R_wo, R_wmq, R_wmo, R_wtmp = Res("wo"), Res("wmq"), Res("wmo"), Res("wtmp")
            wst = [sb(sbk, "wst%d" % k, [128, 2048], F32) for k in range(2)]
            R_wst = [Res("wst%d" % k) for k in range(2)]
            lnt = sb(sbk, "lnt", [128, 4 * 1024], F32)
            R_ln = Res("ln")
            memTb = sb(sbk, "memTb", [128, 8 * 256], BF16)
            R_memT = Res("memT")
            mkT = [sb(sbk, "mkT%d" % k, [128, 4 * 256], BF16) for k in range(2)]
            mva = [sb(sbk, "mva%d" % k, [128, 2 * 4 * 129], BF16) for k in range(2)]
            R_mk = [Res("mk%d" % k) for k in range(2)]
            R_mv = [Res("mv%d" % k) for k in range(2)]
            mixl = [sb(sbk, "mixl%d" % k, [128, 1024], BF16) for k in range(4)]
            R_mixl = [Res("mixl%d" % k) for k in range(4)]
            xr = [sb(sbk, "xr%d" % k, [128, 1024], F32) for k in range(4)]
            R_xr = [Res("xr%d" % k) for k in range(4)]
            NB3 = 4
            tT_l = [sb(sbk, "tT%d" % k, [128, 1024], BF16) for k in range(NB3)]
            hA_l = [sb(sbk, "hA%d" % k, [128, 1024], F32) for k in range(NB3)]
            hB_l = [sb(sbk, "hB%d" % k, [128, 1024], F32) for k in range(NB3)]
            h16_l = [sb(sbk, "h16%d" % k, [128, 1024], BF16) for k in range(NB3)]
            qmT_l = [sb(sbk, "qmT%d" % k, [128, 512], BF16) for k in range(NB3)]
            PTm_l = [sb(sbk, "PTm%d" % k, [128, 1024], BF16) for k in range(NB3)]
            o16_l = [sb(sbk, "o16%d" % k, [128, 512], BF16) for k in range(NB3)]
            oT_l = [sb(sbk, "oT%d" % k, [128, 512], BF16) for k in range(NB3)]
            stat_l = [sb(sbk, "stat%d" % k, [128, 32], F32) for k in range(NB3)]
            RB = [{n: Res(n + str(k)) for n in ("tT", "hA", "hB", "h16", "qm", "PTm", "o16", "oT", "stat")} for k in range(NB3)]
            h2T = [sb(sbk, "h2T%d" % k, [128, 1024], BF16) for k in range(4)]
            R_h2T = [Res("h2T%d" % k) for k in range(4)]
            mstg = sb(sbk, "mstg", [128, 1024], F32)
            R_mstg = Res("mstg")
            wrot = Rot([0, 1, 2, 3, 4, 5, 6, 7])

            def load_cast(dst, R_dst, src, ncols, engs=("act", "pool")):
                k = 0
                for c0 in range(0, ncols, 2048):
                    w = min(2048, ncols - c0)
                    s = k % 2
                    P.dma("sp", wst[s][:, 0:w], src[:, c0:c0 + w], writes=[R_wst[s]])
                    eng = engs[k % len(engs)]
                    if eng == "act":
                        P.op("act", _call("activation", out=dst[:, c0:c0 + w], in_=wst[s][:, 0:w], func=AF.Copy),
                             reads=[R_wst[s]], writes=[R_dst])
                    else:
                        P.op(eng, _call("tensor_copy", out=dst[:, c0:c0 + w], in_=wst[s][:, 0:w]),
                             reads=[R_wst[s]], writes=[R_dst])
                    k += 1

            load_cast(wob, R_wo, I["wo"], 8192)
            load_cast(wmqb, R_wmq, I["wmq"], 4096)
            load_cast(wmob, R_wmo, I["wmo"], 4096)
            for k in range(4):
                P.dma("sp", lnt[:, k * 1024:(k + 1) * 1024], I["lnp"][k:k + 1, :].to_broadcast([128, 1024]), writes=[R_ln])
            load_cast(memTb, R_memT, I["memT"], 2048)
            load_cast(wtmp, R_wtmp, I["wmk"], 4096)
            for h in range(4):
                bank = wrot.next()
                for kc in range(KC):
                    P.op("pe", _call("matmul",
                        out=pb[bank][:, 0:256], lhsT=wtmp[:, kc * 512 + h * 128: kc * 512 + (h + 1) * 128],
                        rhs=memTb[:, kc * 256:(kc + 1) * 256], start=(kc == 0), stop=(kc == KC - 1)),
                        reads=[R_wtmp, R_memT], writes=[R_pb[bank]])
                P.op("act", _call("activation", out=mkT[0][:, h * 256:(h + 1) * 256], in_=pb[bank][:, 0:256], func=AF.Copy),
                     reads=[R_pb[bank]], writes=[R_mk[0]])
                P.op("dve", _call("tensor_copy", out=mstg[:, h * 256:(h + 1) * 256], in_=pb[bank][:, 0:256]),
                     reads=[R_pb[bank]], writes=[R_mstg])
            P.dma("sp", O["mkT"][:, :], mstg[:, :], reads=[R_mstg], defer=True)
            load_cast(wtmp, R_wtmp, I["wmv"], 4096)
            for mt in range(2):
                bank = wrot.next()
                for kc in range(KC):
                    P.op("pe", _call("matmul",
                        out=pb[bank][:, 0:512], lhsT=memTb[:, kc * 256 + mt * 128: kc * 256 + (mt + 1) * 128],
                        rhs=wtmp[:, kc * 512:(kc + 1) * 512], start=(kc == 0), stop=(kc == KC - 1)),
                        reads=[R_wtmp, R_memT], writes=[R_pb[bank]])
                mvv = mva[0][:, mt * 516:(mt + 1) * 516].rearrange("p (h d) -> p h d", d=129)
                P.op("act", _call("activation", out=mvv[:, :, 0:128], in_=pb[bank][:, :].rearrange("p (h d) -> p h d", d=128), func=AF.Copy),
                     reads=[R_pb[bank]], writes=[R_mv[0]])
                P.op("dve", _call("tensor_copy", out=mstg[:, mt * 512:(mt + 1) * 512], in_=pb[bank][:, :]),
                     reads=[R_pb[bank]], writes=[R_mstg])
                P.dma("sp", O["mv"][mt * 128:(mt + 1) * 128, :], mstg[:, mt * 512:(mt + 1) * 512], reads=[R_mstg], defer=True)
            for k in range(2):
                P.op("pool", _call("memset", mva[k][:, :].rearrange("p (t d) -> p t d", d=129)[:, :, 128:129], 1.0), writes=[R_mv[k]])
            load_cast(mkT[1], R_mk[1], I["cmkT"], 1024)
            P.dma("sp", wst[0][:, 0:1024].rearrange("p (t c) -> p t c", c=512), I["cmv"].rearrange("(t p) c -> p t c", p=128), writes=[R_wst[0]])
            P.op("act", _call("activation", out=mva[1][:, :].rearrange("p (t d) -> p t d", d=129)[:, :, 0:128],
                                               in_=wst[0][:, 0:1024].rearrange("p (t d) -> p t d", d=128), func=AF.Copy),
                 reads=[R_wst[0]], writes=[R_mv[1]])

            checkpoint('phaseB_pre')
            def transpose_to(src16, R_src, qs, nchunk, dst, R_dst):
                bank = wrot.next()
                pbf = pb[bank][:, :].bitcast(BF16)
                for c in range(nchunk):
                    P.op("pe", _call("transpose", out=pbf[:, c * qs:(c + 1) * qs], in_=src16[0:qs, c * 128:(c + 1) * 128],
                                                                   identity=ident[0:qs, 0:qs]),
                         reads=[R_src, R_ident], writes=[R_pb[bank]])
                P.op("act", _call("activation", out=dst[:, 0:nchunk * qs], in_=pbf[:, 0:nchunk * qs], func=AF.Copy),
                     reads=[R_pb[bank]], writes=[R_dst])

            def layer_norm(hin, R_hin, qs, gcol, hout, R_hout, stat, R_stat):
                for c in range(2):
                    P.op("dve", _call("bn_stats", out=stat[0:qs, c * 6:(c + 1) * 6], in_=hin[0:qs, c * 512:(c + 1) * 512]),
                         reads=[R_hin], writes=[R_stat])
                P.op("dve", _call("bn_aggr", out=stat[0:qs, 12:14], in_=stat[0:qs, 0:12]), reads=[R_stat], writes=[R_stat])
                P.op("dve", _call("tensor_scalar", out=stat[0:qs, 14:15], in0=stat[0:qs, 13:14], scalar1=LN_EPS, scalar2=None, op0=ALU.add),
                     reads=[R_stat], writes=[R_stat])
                P.op("act", _call("activation", out=stat[0:qs, 15:16], in_=stat[0:qs, 14:15], func=AF.Sqrt), reads=[R_stat], writes=[R_stat])
                P.op("dve", _call("reciprocal", out=stat[0:qs, 16:17], in_=stat[0:qs, 15:16]), reads=[R_stat], writes=[R_stat])
                P.op("dve", _call("scalar_tensor_tensor", out=stat[0:qs, 17:18], in0=stat[0:qs, 12:13], scalar=-1.0, in1=stat[0:qs, 16:17],
                                                             op0=ALU.mult, op1=ALU.mult),
                     reads=[R_stat], writes=[R_stat])
                P.op("act", _call("activation", out=hout[0:qs, :], in_=hin[0:qs, :], func=AF.Identity, scale=stat[0:qs, 16:17], bias=stat[0:qs, 17:18]),
                     reads=[R_hin, R_stat], writes=[R_hout])
                P.op("dve", _call("tensor_tensor", out=hout[0:qs, :], in0=hout[0:qs, :], in1=lnt[0:qs, gcol * 1024:(gcol + 1) * 1024], op=ALU.mult),
                     reads=[R_hout, R_ln], writes=[R_hout])
                P.op("dve", _call("tensor_tensor", out=hout[0:qs, :], in0=hout[0:qs, :], in1=lnt[0:qs, (gcol + 1) * 1024:(gcol + 2) * 1024], op=ALU.add),
                     reads=[R_hout, R_ln], writes=[R_hout])

            def phaseB_block(blk, qs, row0, mi, k2):
                s = k2
                tT, hA, hB, h16, qmT, PTm, o16, oT, stat = (tT_l[k2], hA_l[k2], hB_l[k2], h16_l[k2], qmT_l[k2], PTm_l[k2], o16_l[k2],
                                                             oT_l[k2], stat_l[k2])
                R_tT, R_hA, R_hB, R_h16, R_qm, R_PTm, R_o16, R_oT, R_stat = (RB[k2][n] for n in ("tT", "hA", "hB", "h16", "qm", "PTm", "o16", "oT", "stat"))
                P.dma("sp", mixl[s][0:qs, :], mixD[blk * 128 + row0: blk * 128 + row0 + qs, :], reads=[R_mixD[blk]], writes=[R_mixl[s]])
                P.dma("sp", xr[s][0:qs, :], I["xres"][blk * 128: blk * 128 + qs, :], writes=[R_xr[s]])
                transpose_to(mixl[s], R_mixl[s], qs, 8, tT, R_tT)
                yield
                b0, b1 = wrot.next(), wrot.next()
                for n, bank in enumerate((b0, b1)):
                    for kc in range(KC):
                        P.op("pe", _call("matmul",
                            out=pb[bank][0:qs, :], lhsT=tT[:, kc * qs:(kc + 1) * qs], rhs=wob[:, kc * 1024 + n * 512: kc * 1024 + (n + 1) * 512],
                            start=(kc == 0), stop=(kc == KC - 1)),
                            reads=[R_tT, R_wo], writes=[R_pb[bank]])
                    P.op("dve", _call("scalar_tensor_tensor",
                        out=hA[0:qs, n * 512:(n + 1) * 512], in0=xr[s][0:qs, n * 512:(n + 1) * 512], scalar=ALPHA, in1=pb[bank][0:qs, :],
                        op0=ALU.mult, op1=ALU.add),
                        reads=[R_xr[s], R_pb[bank]], writes=[R_hA])
                yield
                layer_norm(hA, R_hA, qs, 0, hB, R_hB, stat, R_stat)
                yield
                P.op("act", _call("activation", out=h16[0:qs, :], in_=hB[0:qs, :], func=AF.Copy), reads=[R_hB], writes=[R_h16])
                transpose_to(h16, R_h16, qs, 8, tT, R_tT)
                yield
                bq = wrot.next()
                for h in range(4):
                    for kc in range(KC):
                        P.op("pe", _call("matmul",
                            out=pb[bq][:, h * qs:(h + 1) * qs], lhsT=wmqb[:, kc * 512 + h * 128: kc * 512 + (h + 1) * 128],
                            rhs=tT[:, kc * qs:(kc + 1) * qs], start=(kc == 0), stop=(kc == KC - 1)),
                            reads=[R_wmq, R_tT], writes=[R_pb[bq]])
                P.op("act", _call("activation", out=qmT[:, 0:4 * qs], in_=pb[bq][:, 0:4 * qs], func=AF.Copy, scale=float(128.0 ** -0.5)),
                     reads=[R_pb[bq]], writes=[R_qm])
                yield
                bs0, bs1 = wrot.next(), wrot.next()
                for h in range(4):
                    for mt in range(2):
                        idx = h * 2 + mt
                        bank = bs0 if idx < 4 else bs1
                        c0 = (idx % 4) * qs
                        P.op("pe", _call("matmul",
                            out=pb[bank][:, c0:c0 + qs], lhsT=mkT[mi][:, h * 256 + mt * 128: h * 256 + (mt + 1) * 128],
                            rhs=qmT[:, h * qs:(h + 1) * qs], start=True, stop=True),
                            reads=[R_mk[mi], R_qm], writes=[R_pb[bank]])
                for k, bank in enumerate((bs0, bs1)):
                    P.op("act", _call("activation", out=PTm[:, k * 4 * qs:(k + 1) * 4 * qs], in_=pb[bank][:, 0:4 * qs], func=AF.Exp),
                         reads=[R_pb[bank]], writes=[R_PTm])
                yield
                bo0, bo1 = wrot.next(), wrot.next()
                for h in range(4):
                    bank = bo0 if h < 2 else bo1
                    for mt in range(2):
                        idx = h * 2 + mt
                        P.op("pe", _call("matmul",
                            out=pb[bank][0:qs, (h % 2) * 129:(h % 2) * 129 + 129], lhsT=PTm[:, idx * qs:(idx + 1) * qs],
                            rhs=mva[mi][:, (mt * 4 + h) * 129:(mt * 4 + h) * 129 + 129],
                            start=(h % 2 == 0 and mt == 0), stop=(mt == 1), skip_group_check=True),
                            reads=[R_PTm, R_mv[mi]], writes=[R_pb[bank]])
                for k, bank in enumerate((bo0, bo1)):
                    ov = pb[bank][0:qs, 0:258].rearrange("p (h d) -> p h d", d=129)
                    P.op("dve", _call("tensor_scalar", out=stat[0:qs, 20 + 2 * k:22 + 2 * k].rearrange("p (h o) -> p h o", o=1),
                                                                      in0=ov[:, :, 128:129], scalar1=1e-30, scalar2=None, op0=ALU.max),
                         reads=[R_pb[bank]], writes=[R_stat])
                    P.op("dve", _call("reciprocal", out=stat[0:qs, 20 + 2 * k:22 + 2 * k], in_=stat[0:qs, 20 + 2 * k:22 + 2 * k]),
                         reads=[R_stat], writes=[R_stat])
                    for hh in range(2):
                        h = k * 2 + hh
                        P.op("dve", _call("tensor_scalar",
                            out=o16[0:qs, h * 128:(h + 1) * 128], in0=pb[bank][0:qs, hh * 129: hh * 129 + 128],
                            scalar1=stat[0:qs, 20 + 2 * k + hh:21 + 2 * k + hh], scalar2=None, op0=ALU.mult),
                            reads=[R_pb[bank], R_stat], writes=[R_o16])
                yield
                transpose_to(o16, R_o16, qs, 4, oT, R_oT)
                yield
                b0, b1 = wrot.next(), wrot.next()
                for n, bank in enumerate((b0, b1)):
                    for c in range(4):
                        P.op("pe", _call("matmul",
                            out=pb[bank][0:qs, :], lhsT=oT[:, c * qs:(c + 1) * qs], rhs=wmob[:, c * 1024 + n * 512: c * 1024 + (n + 1) * 512],
                            start=(c == 0), stop=(c == 3)),
                            reads=[R_oT, R_wmo], writes=[R_pb[bank]])
                    P.op("dve", _call("scalar_tensor_tensor",
                        out=hA[0:qs, n * 512:(n + 1) * 512], in0=hB[0:qs, n * 512:(n + 1) * 512], scalar=ALPHA, in1=pb[bank][0:qs, :],
                        op0=ALU.mult, op1=ALU.add),
                        reads=[R_hB, R_pb[bank]], writes=[R_hA])
                yield
                layer_norm(hA, R_hA, qs, 2, hB, R_hB, stat, R_stat)
                yield
                P.dma("sp", h2D[blk * 128: blk * 128 + qs, :], hB[0:qs, :], reads=[R_hB], writes=[R_h2D[blk]], defer=True)
                P.op("act", _call("activation", out=h16[0:qs, :], in_=hB[0:qs, :], func=AF.Copy), reads=[R_hB], writes=[R_h16])
                transpose_to(h16, R_h16, qs, 8, h2T[s], R_h2T[s])
                P.dma("sp", h2TD[blk][:, 0:8 * qs], h2T[s][:, 0:8 * qs], reads=[R_h2T[s]], writes=[R_h2TD[blk]], defer=True)
                yield

            def run_staggered(gens, lag):
                active = []
                pending = list(gens)
                tick = 0
                while active or pending:
                    if pending and (not active or tick >= lag):
                        active.append(pending.pop(0))
                        tick = 0
                    for g in list(active):
                        try:
                            next(g)
                        except StopIteration:
                            active.remove(g)
                    tick += 1

            blocks = [(16, 2, 126, 0), (17, 16, 0, 1)] + [(i, 128, 0, 0) for i in range(16)]
            run_staggered([phaseB_block(b_, q_, r_, m_, pos % 4) for pos, (b_, q_, r_, m_) in enumerate(blocks)], 3)
            checkpoint('phaseB')
            P.flush(block)

        P.barrier()
        with ExitStack() as sc:
            wdb = sb(sc, "wdb", [128, NFC * 1024], BF16)
            R_wd = Res("wd")
            wst = [sb(sc, "wstc%d" % k, [128, 2048], F32) for k in range(2)]
            R_wst = [Res("wstc%d" % k) for k in range(2)]
            wsl = [sb(sc, "wsl%d" % k, [128, 2048], BF16) for k in range(2)]
            R_wsl = [Res("wsl%d" % k) for k in range(2)]
            R_wslB = [Res("wslB%d" % k) for k in range(2)]
            hT2 = [sb(sc, "hT%d" % k, [128, NFC * 512], BF16) for k in range(2)]
            R_hT2 = [Res("hT%d" % k) for k in range(2)]
            hTm = sb(sc, "hTm", [128, NFC * 16], BF16)
            R_hTm = Res("hTm")
            h2Tg = [sb(sc, "h2Tg%d" % k, [128, 8 * 512], BF16) for k in range(2)]
            R_h2Tg = [Res("h2Tg%d" % k) for k in range(2)]
            h2Tm = sb(sc, "h2Tm", [128, 8 * 18], BF16)
            R_h2Tm = Res("h2Tm")
            Gb = [sb(sc, "Gb%d" % k, [128, 532], F32) for k in range(3)]
            R_Gb = [Res("Gb%d" % k) for k in range(3)]
            Gs = sb(sc, "Gs", [128, 18], F32)
            R_Gs = Res("Gs")
            t0b = [sb(sc, "t0b%d" % k, [128, 530], F32) for k in range(3)]
            R_t0 = [Res("t0%d" % k) for k in range(3)]
            geb = [sb(sc, "geb%d" % k, [128, 530], F32) for k in range(3)]
            R_ge = [Res("ge%d" % k) for k in range(3)]
            t1b = [sb(sc, "t1b%d" % k, [128, 530], F32) for k in range(3)]
            R_t1b = [Res("t1b%d" % k) for k in range(3)]
            t2b = [sb(sc, "t2b%d" % k, [128, 530], F32) for k in range(3)]
            R_t2b = [Res("t2b%d" % k) for k in range(3)]
            t0s = sb(sc, "t0s", [128, 16], F32)
            ges = sb(sc, "ges", [128, 16], F32)
            R_ts = Res("ts")
            carry = sb(sc, "carry", [128, NFC * 2], F32)
            R_carry = [Res("carry%d" % c) for c in range(NFC)]
            sfc = sb(sc, "sfc", [128, NFC * 2], F32)
            R_sfc = Res("sfc")
            sconv = sb(sc, "sconv", [128, NFC * 2], F32)
            wconv = sb(sc, "wconv", [128, NFC * 3], F32)
            bconv = sb(sc, "bconv", [128, NFC], F32)
            flag = sb(sc, "flag", [128, 1], F32)
            R_cc = Res("cc")
            ln3 = sb(sc, "ln3", [128, 2 * 1024], F32)
            R_ln3 = Res("ln3")
            h2r = [sb(sc, "h2r%d" % k, [128, 1024], F32) for k in range(2)]
            R_h2r = [Res("h2r%d" % k) for k in range(2)]
            yA = sb(sc, "yA", [128, 1024], F32)
            R_yA = Res("yA")
            yB = [sb(sc, "yB%d" % k, [128, 1024], F32) for k in range(2)]
            R_yB = [Res("yB%d" % k) for k in range(2)]
            stat = sb(sc, "statc", [128, 32], F32)
            R_stat = Res("statc")

            P.dma("sp", sconv[:, :], I["sconvT"][:, :], writes=[R_cc])
            P.dma("sp", wconv[:, :], I["wconvT"][:, :], writes=[R_cc])
            P.dma("sp", bconv[:, :], I["bconvT"][:, :], writes=[R_cc])
            P.dma("sp", flag[:, :], I["flag"][:, :], writes=[R_cc])
            for k in range(2):
                P.dma("sp", ln3[:, k * 1024:(k + 1) * 1024], I["lnp"][4 + k:5 + k, :].to_broadcast([128, 1024]), writes=[R_ln3])
            def wdown_piece(j):
                kq = j % 2
                P.dma("sp", yB[kq][:, :], I["wdown"][:, j * 1024:(j + 1) * 1024], writes=[R_yB[kq]])
                P.op("dve", _call("tensor_copy", out=wdb[:, j * 1024:(j + 1) * 1024], in_=yB[kq][:, :]), reads=[R_yB[kq]], writes=[R_wd])

            P.dma("sp", h2Tm[:, :].rearrange("p (c q) -> p c q", q=18)[:, :, 0:2], h2TD[16][:, 0:16].rearrange("p (c q) -> p c q", q=2),
                  reads=[R_h2TD[16]], writes=[R_h2Tm], slow=True)
            P.dma("sp", h2Tm[:, :].rearrange("p (c q) -> p c q", q=18)[:, :, 2:18], h2TD[17][:, 0:128].rearrange("p (c q) -> p c q", q=16),
                  reads=[R_h2TD[17]], writes=[R_h2Tm], slow=True)

            checkpoint('phaseC_pre')
            UB = [0, 2, 4]
            GBK = [1, 3, 5]
            MB = 7
            YB = [6, 7]
            wk = [0]

            def ln3_out(pre_banks, qs, h2src, R_h2src, dst_ap, ys, R_ys):
                for n, bank in enumerate(pre_banks):
                    P.op("dve", _call("scalar_tensor_tensor",
                        out=yA[0:qs, n * 512:(n + 1) * 512], in0=h2src[0:qs, n * 512:(n + 1) * 512], scalar=ALPHA, in1=pb[bank][0:qs, :],
                        op0=ALU.mult, op1=ALU.add),
                        reads=[R_h2src, R_pb[bank]], writes=[R_yA])
                for c in range(2):
                    P.op("dve", _call("bn_stats", out=stat[0:qs, c * 6:(c + 1) * 6], in_=yA[0:qs, c * 512:(c + 1) * 512]),
                         reads=[R_yA], writes=[R_stat])
                P.op("dve", _call("bn_aggr", out=stat[0:qs, 12:14], in_=stat[0:qs, 0:12]), reads=[R_stat], writes=[R_stat])
                P.op("dve", _call("tensor_scalar", out=stat[0:qs, 14:15], in0=stat[0:qs, 13:14], scalar1=LN_EPS, scalar2=None, op0=ALU.add),
                     reads=[R_stat], writes=[R_stat])
                P.op("act", _call("activation", out=stat[0:qs, 15:16], in_=stat[0:qs, 14:15], func=AF.Sqrt), reads=[R_stat], writes=[R_stat])
                P.op("dve", _call("reciprocal", out=stat[0:qs, 16:17], in_=stat[0:qs, 15:16]), reads=[R_stat], writes=[R_stat])
                P.op("dve", _call("scalar_tensor_tensor", out=stat[0:qs, 17:18], in0=stat[0:qs, 12:13], scalar=-1.0, in1=stat[0:qs, 16:17],
                                                             op0=ALU.mult, op1=ALU.mult),
                     reads=[R_stat], writes=[R_stat])
                P.op("act", _call("activation", out=ys[0:qs, :], in_=yA[0:qs, :], func=AF.Identity, scale=stat[0:qs, 16:17], bias=stat[0:qs, 17:18]),
                     reads=[R_yA, R_stat], writes=[R_ys])
                P.op("pool", _call("tensor_tensor", out=ys[0:qs, :], in0=ys[0:qs, :], in1=ln3[0:qs, 0:1024], op=ALU.mult),
                     reads=[R_ys, R_ln3], writes=[R_ys])
                P.op("pool", _call("tensor_tensor", out=ys[0:qs, :], in0=ys[0:qs, :], in1=ln3[0:qs, 1024:2048], op=ALU.add),
                     reads=[R_ys, R_ln3], writes=[R_ys])
                P.dma("sp", dst_ap, ys[0:qs, :], reads=[R_ys], defer=True)

            def load_h2Tg(grp):
                gs = grp % 2
                for bi in range(4):
                    blk = grp * 4 + bi
                    P.dma("sp", h2Tg[gs][:, :].rearrange("p (c q) -> p c q", q=512)[:, :, bi * 128:(bi + 1) * 128],
                          h2TD[blk][:, :].rearrange("p (c q) -> p c q", q=128), reads=[R_h2TD[blk]], writes=[R_h2Tg[gs]])

            def c_s1(grp, c):
                s = (grp * NFC + c) % 2
                P.dma("sp", wst[s][:, :], I["wup"][c], writes=[R_wst[s]])
                P.op("dve", _call("tensor_copy", out=wsl[s][:, 0:1024], in_=wst[s][:, 0:1024]), reads=[R_wst[s]], writes=[R_wsl[s]])
                P.op("dve", _call("tensor_copy", out=wsl[s][:, 1024:2048], in_=wst[s][:, 1024:2048]), reads=[R_wst[s]], writes=[R_wslB[s]])

            def c_s2(grp, c):
                s = (grp * NFC + c) % 2
                gs = grp % 2
                mo = (c % 3) * 64
                if grp == 0:
                    for part, oc in ((0, mo), (1, mo + 32)):
                        for kc in range(KC):
                            P.op("pe", _call("matmul", out=pb[MB][:, oc:oc + 18], lhsT=wsl[s][:, kc * 256 + part * 128: kc * 256 + (part + 1) * 128],
                                             rhs=h2Tm[:, kc * 18:(kc + 1) * 18], start=(kc == 0), stop=(kc == KC - 1)),
                                 reads=[R_wsl[s], R_wslB[s], R_h2Tm], writes=[R_pb[MB]])
                k3 = (grp * NFC + c) % 3
                ub, gbk = UB[k3], GBK[k3]
                for part, bank in ((0, ub), (1, gbk)):
                    for kc in range(KC):
                        P.op("pe", _call("matmul", out=pb[bank][:, :], lhsT=wsl[s][:, kc * 256 + part * 128: kc * 256 + (part + 1) * 128],
                                         rhs=h2Tg[gs][:, kc * 512:(kc + 1) * 512], start=(kc == 0), stop=(kc == KC - 1)),
                             reads=[R_wsl[s], R_wslB[s], R_h2Tg[gs]], writes=[R_pb[bank]])

            def c_s3(grp, c):
                hTg, R_hTg = hT2[grp % 2], R_hT2[grp % 2]
                mo = (c % 3) * 64
                k3 = (grp * NFC + c) % 3
                ub, gbk = UB[k3], GBK[k3]
                G, R_G = Gb[k3], R_Gb[k3]
                t0, R_t = t0b[k3], R_t0[k3]
                ge, R_g = geb[k3], R_ge[k3]
                t1, R_t1 = t1b[k3], R_t1b[k3]
                t2, R_t2 = t2b[k3], R_t2b[k3]
                W = 530 if grp == 0 else 512
                if grp == 0:
                    P.op("dve", _call("tensor_scalar", out=carry[:, c * 2:(c + 1) * 2], in0=pb[MB][:, mo + 32:mo + 34], scalar1=flag[:, 0:1],
                                      scalar2=None, op0=ALU.mult),
                         reads=[R_pb[MB], R_cc], writes=[R_carry[c]])
                P.op("act", _call("activation", out=G[:, 0:2], in_=carry[:, c * 2:(c + 1) * 2], func=AF.Copy),
                     reads=[R_carry[c]], writes=[R_G])
                P.op("act", _call("activation", out=G[:, 2:514], in_=pb[gbk][:, :], func=AF.Copy), reads=[R_pb[gbk]], writes=[R_G])
                if grp == 0:
                    P.op("act", _call("activation", out=G[:, 514:516], in_=sconv[:, c * 2:(c + 1) * 2], func=AF.Copy), reads=[R_cc], writes=[R_G])
                    P.op("act", _call("activation", out=G[:, 516:532], in_=pb[MB][:, mo + 34:mo + 50], func=AF.Copy), reads=[R_pb[MB]], writes=[R_G])
                    P.op("act", _call("activation", out=sfc[:, c * 2:(c + 1) * 2], in_=G[:, 530:532], func=AF.Copy), reads=[R_G], writes=[R_sfc])
                P.op("act", _call("activation", out=carry[:, c * 2:(c + 1) * 2], in_=G[:, 512:514], func=AF.Copy),
                     reads=[R_G], writes=[R_carry[c]])
                P.op("act", _call("activation", out=t0[:, 0:W], in_=G[:, 2:2 + W], func=AF.Identity,
                                  scale=wconv[:, c * 3 + 2:c * 3 + 3], bias=bconv[:, c:c + 1]),
                     reads=[R_G, R_cc], writes=[R_t])
                P.op("act", _call("activation", out=t1[:, 0:W], in_=G[:, 1:1 + W], func=AF.Identity, scale=wconv[:, c * 3 + 1:c * 3 + 2]),
                     reads=[R_G, R_cc], writes=[R_t1])
                P.op("act", _call("activation", out=t2[:, 0:W], in_=G[:, 0:W], func=AF.Identity, scale=wconv[:, c * 3:c * 3 + 1]),
                     reads=[R_G, R_cc], writes=[R_t2])
                P.op("dve", _call("tensor_tensor", out=t0[:, 0:W], in0=t0[:, 0:W], in1=t1[:, 0:W], op=ALU.add), reads=[R_t, R_t1], writes=[R_t])
                P.op("dve", _call("tensor_tensor", out=t0[:, 0:W], in0=t0[:, 0:W], in1=t2[:, 0:W], op=ALU.add), reads=[R_t, R_t2], writes=[R_t])

            def c_s3b(grp, c):
                hTg, R_hTg = hT2[grp % 2], R_hT2[grp % 2]
                mo = (c % 3) * 64
                k3 = (grp * NFC + c) % 3
                ub = UB[k3]
                t0, R_t = t0b[k3], R_t0[k3]
                ge, R_g = geb[k3], R_ge[k3]
                W = 530 if grp == 0 else 512
                P.op("act", _call("activation", out=ge[:, 0:W], in_=t0[:, 0:W], func=AF.Gelu_apprx_tanh), reads=[R_t], writes=[R_g])
                P.op("dve", _call("tensor_tensor", out=hTg[:, c * 512:(c + 1) * 512], in0=pb[ub][:, :], in1=ge[:, 0:512], op=ALU.mult),
                     reads=[R_pb[ub], R_g], writes=[R_hTg])
                if grp == 0:
                    P.op("dve", _call("tensor_tensor", out=hTm[:, c * 16:(c + 1) * 16], in0=pb[MB][:, mo + 2:mo + 18], in1=ge[:, 514:530], op=ALU.mult),
                         reads=[R_pb[MB], R_g], writes=[R_hTm])

            def c_down(grp):
                hTg, R_hTg = hT2[grp % 2], R_hT2[grp % 2]
                if grp == 0:
                    for n, bank in enumerate(YB):
                        for c in range(NFC):
                            P.op("pe", _call("matmul", out=pb[bank][0:16, :], lhsT=hTm[:, c * 16:(c + 1) * 16],
                                             rhs=wdb[:, c * 1024 + n * 512: c * 1024 + (n + 1) * 512], start=(c == 0), stop=(c == NFC - 1)),
                                 reads=[R_hTm, R_wd], writes=[R_pb[bank]])
                    P.dma("sp", h2r[0][0:16, :], h2D[17 * 128: 17 * 128 + 16, :], reads=[R_h2D[17]], writes=[R_h2r[0]])
                    ln3_out(YB, 16, h2r[0], R_h2r[0], O["ys"][:, :], yB[0], R_yB[0])
                    P.dma("sp", O["sfcT"][:, :], sfc[:, :], reads=[R_sfc], defer=True)
                for bi in range(4):
                    blk = grp * 4 + bi
                    hs = blk % 2
                    P.dma("sp", h2r[hs][:, :], h2D[blk * 128:(blk + 1) * 128, :], reads=[R_h2D[blk]], writes=[R_h2r[hs]])
                    for n, bank in enumerate(YB):
                        for c in range(NFC):
                            P.op("pe", _call("matmul", out=pb[bank][:, :], lhsT=hTg[:, c * 512 + bi * 128: c * 512 + (bi + 1) * 128],
                                             rhs=wdb[:, c * 1024 + n * 512: c * 1024 + (n + 1) * 512], start=(c == 0), stop=(c == NFC - 1)),
                                 reads=[R_hTg, R_wd], writes=[R_pb[bank]])
                    ln3_out(YB, 128, h2r[hs], R_h2r[hs], O["y"][blk * 128:(blk + 1) * 128, :], yB[hs], R_yB[hs])

            seq = [(grp, c) for grp in range(4) for c in range(NFC)]
            nseq = len(seq)
            load_h2Tg(0)
            load_h2Tg(1)
            for idx in range(nseq + 3):
                if 1 <= idx <= NFC:
                    wdown_piece(idx - 1)
                if idx < nseq:
                    c_s1(*seq[idx])
                if 1 <= idx <= nseq:
                    c_s2(*seq[idx - 1])
                if 3 <= idx:
                    g4, c4 = seq[idx - 3]
                    c_s3b(g4, c4)
                    if c4 == NFC - 1:
                        c_down(g4)
                        if g4 + 2 < 4:
                            load_h2Tg(g4 + 2)
                if 2 <= idx <= nseq + 1:
                    c_s3(*seq[idx - 2])
            P.dma("sp", O["fcT"][:, :], carry[:, :], reads=R_carry, defer=True)
            P.finish()
            P.flush(block)
    return nc


def _t5_bucket(rel):
    half, max_exact = 16, 8
    n = np.abs(rel)
    log_ratio = np.log(np.maximum(n, 1).astype(np.float32) / max_exact) / math.log(128 / max_exact)
    large = np.minimum(max_exact + (log_ratio * (half - max_exact)).astype(np.int32), half - 1)
    return np.where(rel < 0, half, 0) + np.where(n < max_exact, n, large)


def _host_inputs(inp):
    f32 = np.float32
    x_prompt = np.asarray(inp["x_prompt"], f32)
    x_sample = np.asarray(inp["x_sample"], f32)
    w_in = np.asarray(inp["w_in"], f32)[0]
    qa, ka, va = w_in[:, 0:512], w_in[:, 512:1024], w_in[:, 1024:1536]
    qb, kb, vb = w_in[:, 1536:2048], w_in[:, 2048:2176], w_in[:, 2176:2304]
    qi, ki, wi = w_in[:, 2304:2816], w_in[:, 2816:2880], w_in[:, 2880:2888]
    qbp = np.concatenate([np.concatenate([qb[:, r * 64:(r + 1) * 64], qb[:, (4 + r) * 64:(5 + r) * 64]], axis=1) for r in range(4)], axis=1)
    winp = np.concatenate([qa, ka, qbp, kb, qi, ki, ki, va, vb, wi], axis=1)
    assert winp.shape[1] == NCOL

    def kc_layout(w):
        n = w.shape[1]
        return np.ascontiguousarray(w.reshape(8, 128, n).transpose(1, 0, 2).reshape(128, 8 * n))

    shared = {}
    shared["win"] = kc_layout(winp)
    shared["wo"] = kc_layout(np.asarray(inp["w_o"], f32)[0])
    shared["wmq"] = kc_layout(np.asarray(inp["w_mq"], f32)[0])
    shared["wmk"] = kc_layout(np.asarray(inp["w_mk"], f32)[0])
    shared["wmv"] = kc_layout(np.asarray(inp["w_mv"], f32)[0])
    wmo = np.asarray(inp["w_mo"], f32)[0]
    shared["wmo"] = np.ascontiguousarray(wmo.reshape(4, 128, 1024).transpose(1, 0, 2).reshape(128, 4096))
    w_up = np.asarray(inp["w_up"], f32)[0]
    wu = w_up[:, :DFF].reshape(8, 128, NFC, 128)
    wg = w_up[:, DFF:].reshape(8, 128, NFC, 128)
    wup = np.stack([wu, wg], axis=3)
    shared["wup"] = np.ascontiguousarray(wup.transpose(2, 1, 0, 3, 4).reshape(NFC, 128, 8 * 256))
    w_down = np.asarray(inp["w_down"], f32)[0]
    shared["wdown"] = np.ascontiguousarray(w_down.reshape(NFC, 128, 1024).transpose(1, 0, 2).reshape(128, NFC * 1024))
    shared["lnp"] = np.ascontiguousarray(np.stack([np.asarray(inp[k], f32)[0] for k in ("ln1_g", "ln1_b", "ln2_g", "ln2_b", "ln3_g", "ln3_b")]))
    w_conv = np.asarray(inp["w_conv"], f32)[0]
    shared["wconvT"] = np.ascontiguousarray(w_conv.reshape(3, NFC, 128).transpose(2, 1, 0).reshape(128, NFC * 3))
    shared["bconvT"] = np.ascontiguousarray(np.asarray(inp["b_conv"], f32)[0].reshape(NFC, 128).T)
    shared["ident"] = np.eye(128, dtype=f32)
    tabA = np.asarray(inp["a_rel_bias"], f32)[0]
    qq = np.arange(128)[:, None]
    kk = np.arange(640)[None, :]
    kpos = kk - 512
    rel = qq - kpos
    cq = qq // 64
    kch = np.floor_divide(kpos, 64)
    allowed = (kch >= cq - 8) & (kch <= cq)
    bias = tabA[np.clip(rel, -64, 64) + 64]
    AB = np.where(allowed[:, :, None], bias, f32(NEGM)).astype(f32)
    shared["AB"] = np.ascontiguousarray(AB.transpose(0, 2, 1).reshape(128, 8 * ABW))
    js = np.arange(16)[:, None]
    ks = np.arange(528)[None, :]
    ABs = tabA[np.clip(512 + js - ks, -64, 64) + 64]
    shared["ABs"] = np.ascontiguousarray(ABs.transpose(0, 2, 1).reshape(16, 8 * 528)).astype(f32)
    t5 = np.asarray(inp["t5_bias"], f32)
    relB = np.arange(128)[:, None] - np.arange(256)[None, :] + 128
    Bn = t5[_t5_bucket(relB)]
    shared["Bn"] = np.ascontiguousarray(Bn.transpose(0, 2, 1).reshape(128, 8 * BNW)).astype(f32)
    relBs = 128 + np.arange(16)[:, None] - np.arange(144)[None, :]
    Bns = t5[_t5_bucket(relBs)]
    shared["Bns"] = np.ascontiguousarray(Bns.transpose(0, 2, 1).reshape(16, 8 * 144)).astype(f32)
    shared["C15"] = np.ascontiguousarray(np.broadcast_to(t5[15][None, :], (128, 8))).astype(f32)
    dm = np.zeros((128, 128), f32)
    dm[0:64, 64:128] = NEGM
    shared["diagmask"] = dm

    mem_prompt = np.asarray(inp["mem_prompt"], f32)
    maps = []
    for c in range(8):
        b, half = c // 2, c % 2
        m = dict(shared)
        xk = np.zeros((4096, 1024), f32)
        if half == 1:
            xk[:] = x_prompt[b]
        else:
            xk[2048:] = x_prompt[b, :2048]
        m["xkT"] = np.ascontiguousarray(xk.reshape(32, 128, 8, 128).transpose(0, 3, 2, 1).reshape(32, 128, 1024))
        xs = x_sample[c]
        m["xsT"] = np.ascontiguousarray(xs.reshape(16, 8, 128).transpose(2, 1, 0).reshape(128, 128))
        xres = np.zeros((NBLK * 128, 1024), f32)
        xres[0:2048] = xk[2048:]
        xres[2048:2050] = xk[2046:2048]
        xres[17 * 128:17 * 128 + 16] = xs
        m["xres"] = xres
        m["memT"] = np.ascontiguousarray(mem_prompt[b].reshape(256, 8, 128).transpose(2, 1, 0).reshape(128, 2048))
        cmk = np.asarray(inp["cache_mem_k"], f32)[0, c]
        m["cmkT"] = np.ascontiguousarray(cmk.transpose(2, 1, 0).reshape(128, 1024))
        m["cmv"] = np.ascontiguousarray(np.asarray(inp["cache_mem_v"], f32)[0, c].reshape(256, 512))
        cak = np.asarray(inp["cache_a_k"], f32)[0, c]
        m["cakT"] = np.ascontiguousarray(cak.reshape(512, 4, 2, 64).transpose(2, 3, 1, 0).reshape(128, 2048))
        m["cav"] = np.ascontiguousarray(np.asarray(inp["cache_a_v"], f32)[0, c].reshape(512, 512))
        cbk = np.asarray(inp["cache_b_k"], f32)[0, c]
        m["cbkT"] = np.ascontiguousarray(cbk.reshape(2048, 128).T)
        m["cbv"] = np.ascontiguousarray(np.asarray(inp["cache_b_v"], f32)[0, c].reshape(2048, 128))
        cbi = np.asarray(inp["cache_b_kidx"], f32)[0, c]
        m["cbiT"] = np.ascontiguousarray(np.concatenate([cbi.T, cbi.T], axis=0))
        sc_ = np.asarray(inp["state_ffn_conv"], f32)[0, c]
        m["sconvT"] = np.ascontiguousarray(sc_.reshape(2, NFC, 128).transpose(2, 1, 0).reshape(128, NFC * 2))
        m["colmask"] = np.full((128, 1), NEGM if half == 0 else 0.0, f32)
        kv = np.ones((128, NT), f32)
        if half == 0:
            kv[:, 0:16] = 0.0
        m["kvalid"] = kv
        m["flag"] = np.full((128, 1), float(half), f32)
        maps.append(m)
    return maps


_NC_CACHE = {}


def _run(inputs, debug=False):
    key = bool(debug)
    if key not in _NC_CACHE:
        _NC_CACHE[key] = build_program(debug=debug)
    nc = _NC_CACHE[key]
    maps = _host_inputs(inputs)
    res = run_bass_kernel_spmd(nc, maps, core_ids=list(range(8)))
    return res.results


def kernel(**inputs):
    R = _run(inputs)
    f32 = np.float32
    y = np.zeros((4, 4096, 1024), f32)
    ys = np.zeros((8, 16, 1024), f32)
    pak = np.zeros((1, 4, 512, 8, 64), f32)
    pav = np.zeros((1, 4, 512, 8, 64), f32)
    pbk = np.zeros((1, 4, 4096, 2, 64), f32)
    pbv = np.zeros((1, 4, 4096, 2, 64), f32)
    pbi = np.zeros((1, 4, 4096, 64), f32)
    pmk = np.zeros((1, 4, 256, 4, 128), f32)
    pmv = np.zeros((1, 4, 256, 4, 128), f32)
    pfc = np.zeros((1, 4, 2, DFF), f32)
    sak = np.zeros((1, 8, 16, 8, 64), f32)
    sav = np.zeros((1, 8, 16, 8, 64), f32)
    sbk = np.zeros((1, 8, 16, 2, 64), f32)
    sbv = np.zeros((1, 8, 16, 2, 64), f32)
    sbi = np.zeros((1, 8, 16, 64), f32)
    sfc = np.zeros((1, 8, 2, DFF), f32)
    for c in range(8):
        b, half = c // 2, c % 2
        r = R[c]
        y[b, half * 2048:(half + 1) * 2048] = np.asarray(r["y"], f32)
        ys[c] = np.asarray(r["ys"], f32)
        if half == 1:
            akT = np.asarray(r["akT"], f32).reshape(2, 64, 4, 512)
            pak[0, b] = akT.transpose(3, 2, 0, 1).reshape(512, 8, 64)
            pav[0, b] = np.asarray(r["av"], f32).reshape(512, 8, 64)
            pbk[0, b] = np.asarray(r["bkT"], f32).T.reshape(4096, 2, 64)
            pbv[0, b] = np.asarray(r["bv"], f32).reshape(4096, 2, 64)
            pbi[0, b] = np.asarray(r["biT"], f32).T
            pmk[0, b] = np.asarray(r["mkT"], f32).reshape(128, 4, 256).transpose(2, 1, 0)
            pmv[0, b] = np.asarray(r["mv"], f32).reshape(256, 4, 128)
            pfc[0, b] = np.asarray(r["fcT"], f32).reshape(128, NFC, 2).transpose(2, 1, 0).reshape(2, DFF)
        sakT = np.asarray(r["sakT"], f32).reshape(2, 64, 4, 16)
        sak[0, c] = sakT.transpose(3, 2, 0, 1).reshape(16, 8, 64)
        sav[0, c] = np.asarray(r["sav"], f32).reshape(16, 8, 64)
        sbk[0, c] = np.asarray(r["sbkT"], f32).T.reshape(16, 2, 64)
        sbv[0, c] = np.asarray(r["sbv"], f32).reshape(16, 2, 64)
        sbi[0, c] = np.asarray(r["sbiT"], f32).T
        sfc[0, c] = np.asarray(r["sfcT"], f32).reshape(128, NFC, 2).transpose(2, 1, 0).reshape(2, DFF)
    return (y, ys, pak, pav, pbk, pbv, pbi, pmk, pmv, pfc, sak, sav, sbk, sbv, sbi, sfc)
```

```python
import math
from contextlib import ExitStack

import numpy as np
import concourse.bass as bass
import concourse.mybir as mybir
from concourse.bass_utils import run_bass_kernel_spmd

F32 = mybir.dt.float32
BF16 = mybir.dt.bfloat16
AF = mybir.ActivationFunctionType
ALU = mybir.AluOpType

D = 1024
KC = 8
NT = 32
NCOL = 2952
C_QA, C_KA, C_QB, C_KB, C_QI, C_KI, C_VA, C_VB, C_WI = 0, 512, 1024, 1536, 1664, 2176, 2304, 2816, 2944
DFF = 2816
NFC = 22
ALPHA = 2.0 ** 0.25
LN_EPS = 1e-5
NEGM = -30000.0
NIT = 16
BIS_W0 = 16.0
ABW = 640
BNW = 256
NBLK = 18


class Res:
    __slots__ = ("lw", "rd", "name", "excl")

    def __init__(self, name="", excl=False):
        self.lw = None
        self.rd = {}
        self.name = name
        self.excl = excl


def _call(name, *args, **kw):
    return lambda e: getattr(e, name)(*args, **kw)


class Prog:
    ENG = ("pe", "act", "dve", "pool", "sp")

    def __init__(self, nc, sems, dma_sems):
        self.nc = nc
        self.streams = {e: [] for e in self.ENG}
        self.sem = sems
        self.cnt = {e: 0 for e in self.ENG}
        self.seen = {e: {} for e in self.ENG}
        self.dsems = dma_sems
        self.dval = [0] * len(dma_sems)
        self.dnext = 0
        self.semh = dict(sems)
        for i, h in enumerate(dma_sems):
            self.semh[("d", i)] = h
        self.ninst = 0
        self.dead = False
        self.deferred = []
        self.defer_lag = 48

    def _deps(self, reads, writes, eng=None):
        d = {}
        for r in reads:
            if r.lw is not None:
                k, v = r.lw
                if d.get(k, 0) < v:
                    d[k] = v
            if r.excl:
                for k, v in r.rd.items():
                    if k != eng and d.get(k, 0) < v:
                        d[k] = v
        for w in writes:
            if w.lw is not None:
                k, v = w.lw
                if d.get(k, 0) < v:
                    d[k] = v
            for k, v in w.rd.items():
                if d.get(k, 0) < v:
                    d[k] = v
        return d

    def _wait(self, eng, deps):
        for k, v in deps.items():
            if k == "pe" and eng == "pe":
                continue
            if self.seen[eng].get(k, 0) >= v:
                continue
            self.seen[eng][k] = v
            h = self.semh[k]
            self.streams[eng].append(lambda e, h=h, v=v: e.wait_ge(h, v))

    def _flush_deferred(self, force=False, reads=(), writes=()):
        if not self.deferred:
            return
        conflict = force
        if not conflict:
            ws = set(id(w) for w in writes)
            rs = set(id(r) for r in reads)
            for d in self.deferred:
                dr = set(id(x) for x in d[3])
                dw = set(id(x) for x in d[4])
                if (ws & dr) or (ws & dw) or (rs & dw):
                    conflict = True
                    break
        if conflict:
            pend, self.deferred = self.deferred, []
            for d in pend:
                self._dma_now(d[0], d[1], d[2], d[3], d[4], d[5])
            return
        while self.deferred and self.ninst - self.deferred[0][6] >= self.defer_lag:
            d = self.deferred.pop(0)
            self._dma_now(d[0], d[1], d[2], d[3], d[4], d[5])

    def op(self, eng, fn, reads=(), writes=()):
        if self.dead:
            return
        self._flush_deferred(False, reads, writes)
        self._wait(eng, self._deps(reads, writes, eng))
        self.cnt[eng] += 1
        n = self.cnt[eng]
        h = self.sem[eng]
        self.streams[eng].append(lambda e, fn=fn, h=h: fn(e).then_inc(h, 1))
        self.ninst += 1
        for r in reads:
            if r.rd.get(eng, 0) < n:
                r.rd[eng] = n
        for w in writes:
            w.lw = (eng, n)
            w.rd = {}

    def dma(self, q, out, in_, reads=(), writes=(), slow=False, defer=False):
        if self.dead:
            return
        if defer:
            self._flush_deferred(False, reads, writes)
            self.deferred.append((q, out, in_, list(reads), list(writes), slow, self.ninst))
            return
        self._flush_deferred(False, reads, writes)
        self._dma_now(q, out, in_, reads, writes, slow)

    def _dma_now(self, q, out, in_, reads=(), writes=(), slow=False):
        deps = self._deps(reads, writes)
        i = self.dnext
        self.dnext = (i + 1) % len(self.dsems)
        k = ("d", i)
        if self.dval[i] > 0 and deps.get(k, 0) < self.dval[i]:
            deps[k] = self.dval[i]
        self._wait(q, deps)
        self.dval[i] += 16
        v = self.dval[i]
        h = self.dsems[i]
        if slow:
            self.streams[q].append(
                lambda e, out=out, in_=in_, h=h: e.dma_start(out=out, in_=in_, allow_slow_non_contiguous=True).then_inc(h, 16))
        else:
            self.streams[q].append(lambda e, out=out, in_=in_, h=h: e.dma_start(out=out, in_=in_).then_inc(h, 16))
        self.ninst += 1
        for r in reads:
            if r.rd.get(k, 0) < v:
                r.rd[k] = v
        for w in writes:
            w.lw = (k, v)
            w.rd = {}

    def barrier(self):
        if self.dead:
            return
        self._flush_deferred(True)
        deps = {e: self.cnt[e] for e in self.ENG if self.cnt[e] > 0}
        for i, v in enumerate(self.dval):
            if v > 0:
                deps[("d", i)] = v
        for e in self.ENG:
            self._wait(e, dict(deps))

    def finish(self):
        self._flush_deferred(True)
        deps = {("d", i): v for i, v in enumerate(self.dval) if v > 0}
        self._wait("sp", deps)

    def flush(self, block):
        self._flush_deferred(True)
        s = self.streams
        self.streams = {e: [] for e in self.ENG}

        def mk(lst):
            def body(e):
                for f in lst:
                    f(e)
            return body

        block.tensor(mk(s["pe"]))
        block.scalar(mk(s["act"]))
        block.vector(mk(s["dve"]))
        block.gpsimd(mk(s["pool"]))
        block.sync(mk(s["sp"]))


def build_program(debug=False, stop_at=None):
    nc = bass.Bass("TRN2", target_bir_lowering=False)

    def din(name, shape, dt=F32):
        return nc.dram_tensor(name, list(shape), dt, kind="ExternalInput").ap()

    def dout(name, shape, dt=F32):
        return nc.dram_tensor(name, list(shape), dt, kind="ExternalOutput").ap()

    def dscr(name, shape, dt):
        return nc.dram_tensor(name, list(shape), dt, kind="Internal").ap()

    I = {}
    I["xkT"] = din("xkT", [NT, 128, 1024])
    I["xsT"] = din("xsT", [128, 8 * 16])
    I["xres"] = din("xres", [NBLK * 128, 1024])
    I["win"] = din("win", [128, KC * NCOL])
    I["wo"] = din("wo", [128, 8 * 1024])
    I["wmq"] = din("wmq", [128, 8 * 512])
    I["wmk"] = din("wmk", [128, 8 * 512])
    I["wmv"] = din("wmv", [128, 8 * 512])
    I["wmo"] = din("wmo", [128, 4 * 1024])
    I["wup"] = din("wup", [NFC, 128, 8 * 256])
    I["wdown"] = din("wdown", [128, NFC * 1024])
    I["lnp"] = din("lnp", [6, 1024])
    I["wconvT"] = din("wconvT", [128, NFC * 3])
    I["bconvT"] = din("bconvT", [128, NFC])
    I["memT"] = din("memT", [128, 8 * 256])
    I["cmkT"] = din("cmkT", [128, 4 * 256])
    I["cmv"] = din("cmv", [256, 512])
    I["cakT"] = din("cakT", [128, 4 * 512])
    I["cav"] = din("cav", [512, 512])
    I["cbkT"] = din("cbkT", [128, 2048])
    I["cbv"] = din("cbv", [2048, 128])
    I["cbiT"] = din("cbiT", [128, 2048])
    I["sconvT"] = din("sconvT", [128, NFC * 2])
    I["ident"] = din("ident", [128, 128])
    I["AB"] = din("AB", [128, 8 * ABW])
    I["ABs"] = din("ABs", [16, 8 * 528])
    I["Bn"] = din("Bn", [128, 8 * BNW])
    I["Bns"] = din("Bns", [16, 8 * 144])
    I["C15"] = din("C15", [128, 8])
    I["colmask"] = din("colmask", [128, 1])
    I["diagmask"] = din("diagmask", [128, 128])
    I["kvalid"] = din("kvalid", [128, NT])
    I["flag"] = din("flag", [128, 1])

    O = {}
    O["y"] = dout("y", [2048, 1024])
    O["ys"] = dout("ys", [16, 1024])
    O["akT"] = dout("akT", [128, 4 * 512])
    O["av"] = dout("av", [512, 512])
    O["bkT"] = dout("bkT", [128, 4096])
    O["bv"] = dout("bv", [4096, 128])
    O["biT"] = dout("biT", [64, 4096])
    O["mkT"] = dout("mkT", [128, 4 * 256])
    O["mv"] = dout("mv", [256, 512])
    O["fcT"] = dout("fcT", [128, NFC * 2])
    O["sakT"] = dout("sakT", [128, 4 * 16])
    O["sav"] = dout("sav", [16, 512])
    O["sbkT"] = dout("sbkT", [128, 16])
    O["sbv"] = dout("sbv", [16, 128])
    O["sbiT"] = dout("sbiT", [64, 16])
    O["sfcT"] = dout("sfcT", [128, NFC * 2])
    if debug:
        O["dbg_mix"] = dout("dbg_mix", [NBLK * 128, 1024], BF16)
        O["dbg_h2"] = dout("dbg_h2", [NBLK * 128, 1024])
        mixD = O["dbg_mix"]
        h2D = O["dbg_h2"]
    else:
        mixD = dscr("mixD", [NBLK * 128, 1024], BF16)
        h2D = dscr("h2D", [NBLK * 128, 1024], F32)
    h2TD = dscr("h2TD", [NBLK, 128, 1024], BF16)
    R_mixD = [Res("mixD%d" % i) for i in range(NBLK)]
    R_h2D = [Res("h2D%d" % i) for i in range(NBLK)]
    R_h2TD = [Res("h2TD%d" % i) for i in range(NBLK)]

    es = ExitStack()
    with es:
        sems = {e: es.enter_context(nc.semaphore("s_" + e)) for e in Prog.ENG}
        dsems = [es.enter_context(nc.semaphore("d%d" % i)) for i in range(32)]
        P = Prog(nc, sems, dsems)
        block = es.enter_context(nc.Block())

        def checkpoint(name):
            if stop_at is not None and name == stop_at and not P.dead:
                P.finish()
                P.flush(block)
                P.dead = True

        pb = [es.enter_context(nc.psum_tensor("pb%d" % i, [128, 512], F32)) for i in range(8)]
        R_pb = [Res("pb%d" % i, excl=True) for i in range(8)]

        class Rot:
            def __init__(self, idxs):
                self.idxs = idxs
                self.i = 0

            def next(self):
                k = self.idxs[self.i % len(self.idxs)]
                self.i += 1
                return k

        def sb(stack, name, shape, dt):
            return stack.enter_context(nc.sbuf_tensor("sb_" + name, list(shape), dt))

        ident_f = sb(es, "ident_f", [128, 128], F32)
        ident = sb(es, "ident", [128, 512], BF16)
        R_ident = Res("ident")
        P.dma("sp", ident_f[:, :], I["ident"][:, :], writes=[R_ident])
        for r in range(4):
            P.op("act", _call("activation", out=ident[:, r * 128:(r + 1) * 128], in_=ident_f[:, :], func=AF.Copy),
                 reads=[R_ident], writes=[R_ident])

        def run_interleaved(gens):
            gens = [[0.0, i, g] for i, g in enumerate(gens)]
            while gens:
                gens.sort(key=lambda x: (x[0], x[1]))
                ent = gens[0]
                try:
                    c = next(ent[2])
                    ent[0] += (c if c else 1.0)
                except StopIteration:
                    gens.remove(ent)

        with ExitStack() as sa:
            winb = sb(sa, "winb", [128, KC * NCOL], BF16)
            R_win = Res("win")
            kbi = sb(sa, "kbi", [128, 2 * 4096], BF16)
            R_kbi = [Res("kbi%d" % r) for r in range(NT)]
            R_ki = [Res("ki%d" % r) for r in range(NT)]
            vb_aug = sb(sa, "vb_aug", [128, NT * 2 * 65], BF16)
            R_vb = [Res("vb%d" % r) for r in range(NT)]
            kaT = sb(sa, "kaT", [128, 6 * 512], BF16)
            R_ka = [Res("ka%d" % s) for s in range(6)]
            va_aug = sb(sa, "va_aug", [128, 6 * 8 * 65], BF16)
            R_va = [Res("va%d" % s) for s in range(6)]
            ABb = sb(sa, "ABb", [128, 8 * ABW], BF16)
            R_AB = Res("AB")
            Bnb = sb(sa, "Bnb", [128, 8 * BNW], BF16)
            R_Bn = Res("Bn")
            Mnear = [sb(sa, "Mnear%d" % k, [128, 8 * BNW], BF16) for k in range(2)]
            R_Mnear = [Res("Mnear%d" % k) for k in range(2)]
            score = [sb(sa, "score%d" % k, [128, 4096], F32) for k in range(2)]
            R_score = [Res("score%d" % k) for k in range(2)]
            Mb = [sb(sa, "Mb%d" % k, [128, 4096], BF16) for k in range(2)]
            R_M = [Res("M%d" % k) for k in range(2)]
            relu = [sb(sa, "relu%d" % k, [128, 512], BF16) for k in range(3)]
            R_relu = [Res("relu%d" % k) for k in range(3)]
            xstg2 = [sb(sa, "xstg%d" % k, [128, 1024], F32) for k in range(2)]
            R_xstg2 = [Res("xstg%d" % k) for k in range(2)]
            xstg, R_xstg = xstg2[0], R_xstg2[0]
            xTb = [sb(sa, "xTb%d" % k, [128, 1024], BF16) for k in range(2)]
            R_xT = [Res("xT%d" % k) for k in range(2)]
            qaz = [sb(sa, "qaz%d" % k, [128, 1024], BF16) for k in range(2)]
            qbz = [sb(sa, "qbz%d" % k, [128, 1024], BF16) for k in range(3)]
            qiz = [sb(sa, "qiz%d" % k, [128, 1024], BF16) for k in range(2)]
            R_qa = [Res("qa%d" % k) for k in range(2)]
            R_qb = [Res("qb%d" % k) for k in range(3)]
            R_qi = [Res("qi%d" % k) for k in range(2)]
            coef = [sb(sa, "coef%d" % k, [128, 8], F32) for k in range(2)]
            R_coef = [Res("coef%d" % k) for k in range(2)]
            dg = [sb(sa, "dg%d" % k, [128, 1024], BF16) for k in range(2)]
            R_dg = [Res("dg%d" % k) for k in range(2)]
            PTA = [sb(sa, "PTA%d" % k, [128, 512], BF16) for k in range(3)]
            R_PTA = [Res("PTA%d" % k) for k in range(3)]
            PTB = [sb(sa, "PTB%d" % k, [128, 512], BF16) for k in range(3)]
            R_PTB = [Res("PTB%d" % k) for k in range(3)]
            mixb = [sb(sa, "mixb%d" % k, [128, 1024], BF16) for k in range(3)]
            R_mix = [Res("mix%d" % k) for k in range(3)]
            ostg = [sb(sa, "ostg%d" % k, [128, 256], F32) for k in range(2)]
            R_ostg = [Res("ostg%d" % k) for k in range(2)]
            vbstg = [sb(sa, "vbstg%d" % k, [128, 128], F32) for k in range(2)]
            R_vbstg = [Res("vbstg%d" % k) for k in range(2)]
            astg = sb(sa, "astg", [128, 1024], F32)
            R_astg = Res("astg")
            small = [sb(sa, "small%d" % k, [128, 16], F32) for k in range(2)]
            R_small = [Res("small%d" % k) for k in range(2)]
            recA = [sb(sa, "recA%d" % k, [128, 8], F32) for k in range(2)]
            R_recA = [Res("recA%d" % k) for k in range(2)]
            recB = [sb(sa, "recB%d" % k, [128, 8], F32) for k in range(2)]
            R_recB = [Res("recB%d" % k) for k in range(2)]
            colmask = sb(sa, "colmask", [128, 1], F32)
            diagm = sb(sa, "diagm", [128, 128], F32)
            kvalid = sb(sa, "kvalid", [128, NT], F32)
            c15 = sb(sa, "c15", [128, 8], F32)
            ones8 = sb(sa, "ones8", [128, 8], F32)
            R_cst = Res("cst")

            wrot = Rot([0, 1, 2])

            P.dma("sp", colmask[:, :], I["colmask"][:, :], writes=[R_cst])
            P.dma("sp", diagm[:, :], I["diagmask"][:, :], writes=[R_cst])
            P.dma("sp", kvalid[:, :], I["kvalid"][:, :], writes=[R_cst])
            P.dma("sp", c15[:, :], I["C15"][:, :], writes=[R_cst])
            P.op("pool", _call("memset", ones8[:, :], 1.0), writes=[R_cst])
            for k in range(2):
                P.op("pool", _call("memset", qaz[k][:, :], 0.0), writes=[R_qa[k]])
                P.op("pool", _call("memset", qiz[k][:, :], 0.0), writes=[R_qi[k]])
            for k in range(3):
                P.op("pool", _call("memset", qbz[k][:, :], 0.0), writes=[R_qb[k]])

            HW = NCOL // 2
            R_slot = [Res("wslot%d" % q) for q in range(4)]
            for kc in range(KC):
                for hh in range(2):
                    q = (kc * 2 + hh) % 4
                    stg = score[q // 2][:, (q % 2) * HW:(q % 2 + 1) * HW]
                    P.dma("sp", stg, I["win"][:, kc * NCOL + hh * HW: kc * NCOL + (hh + 1) * HW], writes=[R_slot[q]])
                    if hh == 0:
                        P.op("act", _call("activation", out=winb[:, kc * NCOL + hh * HW: kc * NCOL + (hh + 1) * HW], in_=stg, func=AF.Copy),
                             reads=[R_slot[q]], writes=[R_win])
                    else:
                        P.op("dve", _call("tensor_copy", out=winb[:, kc * NCOL + hh * HW: kc * NCOL + (hh + 1) * HW], in_=stg),
                             reads=[R_slot[q]], writes=[R_win])
            for hh in range(2):
                w = 4 * ABW
                P.dma("sp", score[hh][:, 0:w], I["AB"][:, hh * w:(hh + 1) * w], writes=[R_score[hh], R_slot[2 * hh], R_slot[2 * hh + 1]])
                P.op("act", _call("activation", out=ABb[:, hh * w:(hh + 1) * w], in_=score[hh][:, 0:w], func=AF.Copy),
                     reads=[R_score[hh]], writes=[R_AB])
            P.dma("sp", score[0][:, 0:8 * BNW], I["Bn"][:, :], writes=[R_score[0]])
            for h in range(8):
                P.op("dve", _call("tensor_scalar", out=Bnb[:, h * BNW:(h + 1) * BNW], in0=score[0][:, h * BNW:(h + 1) * BNW],
                                  scalar1=c15[:, h:h + 1], scalar2=None, op0=ALU.subtract),
                     reads=[R_score[0], R_cst], writes=[R_Bn])
            checkpoint('consts')

            def win_cols(kc, c0, n):
                return winb[:, kc * NCOL + c0: kc * NCOL + c0 + n]

            def fm_proj(bank, xT, R_x, N, col0, nchunks, ocol=0):
                for j in range(nchunks):
                    for kc in range(KC):
                        P.op("pe", _call("matmul", out=pb[bank][:, ocol + j * N: ocol + (j + 1) * N], lhsT=win_cols(kc, col0 + j * 128, 128),
                                         rhs=xT[:, kc * N:(kc + 1) * N], start=(kc == 0), stop=(kc == KC - 1)),
                             reads=[R_win, R_x], writes=[R_pb[bank]])

            def tm_proj(bank, xT, R_x, N, col0, ncols, ocol=0):
                for kc in range(KC):
                    P.op("pe", _call("matmul", out=pb[bank][0:N, ocol:ocol + ncols], lhsT=xT[:, kc * N:(kc + 1) * N],
                                     rhs=win_cols(kc, col0, ncols), start=(kc == 0), stop=(kc == KC - 1)),
                         reads=[R_win, R_x], writes=[R_pb[bank]])

            def load_xT(r, eng="pool"):
                s = r % 2
                P.dma("sp", xstg2[s][:, :], I["xkT"][r], writes=[R_xstg2[s]])
                P.op(eng, _call("tensor_copy", out=xTb[s][:, :], in_=xstg2[s][:, :]), reads=[R_xstg2[s]], writes=[R_xT[s]])

            def kside(r, full):
                s = r % 2
                xT, R_x = xTb[s], R_xT[s]
                so = r % 2
                bk = wrot.next()
                fm_proj(bk, xT, R_x, 128, C_KB, 1)
                fm_proj(bk, xT, R_x, 128, C_KI, 1, ocol=128)
                P.op("act", _call("activation", out=ostg[so][:, :], in_=pb[bk][:, 0:256], func=AF.Copy), reads=[R_pb[bk]], writes=[R_ostg[so]])
                P.op("pool", _call("tensor_copy", out=kbi[:, :].rearrange("p (a c) -> p a c", a=2)[:, :, r * 128:(r + 1) * 128],
                                   in_=ostg[so][:, :].rearrange("p (a c) -> p a c", a=2)),
                     reads=[R_ostg[so]], writes=[R_kbi[r], R_ki[r]])
                P.dma("sp", O["bkT"][:, r * 128:(r + 1) * 128], ostg[so][:, 0:128], reads=[R_ostg[so]], defer=True)
                P.dma("sp", O["biT"][:, r * 128:(r + 1) * 128], ostg[so][0:64, 128:256], reads=[R_ostg[so]], defer=True)
                yield 3.0
                bv_ = wrot.next()
                tm_proj(bv_, xT, R_x, 128, C_VB, 128)
                vbv = vb_aug[:, r * 130:(r + 1) * 130].rearrange("p (g d) -> p g d", d=65)
                P.op("act", _call("activation", out=vbstg[so][:, :], in_=pb[bv_][:, 0:128], func=AF.Copy), reads=[R_pb[bv_]], writes=[R_vbstg[so]])
                P.op("pool", _call("tensor_copy", out=vbv[:, :, 0:64], in_=vbstg[so][:, :].rearrange("p (g d) -> p g d", d=64)),
                     reads=[R_vbstg[so]], writes=[R_vb[r]])
                P.op("pool", _call("tensor_scalar", out=vbv[:, :, 64:65], in0=ones8[:, 0:2].rearrange("p (g o) -> p g o", o=1),
                                   scalar1=kvalid[:, r:r + 1], scalar2=None, op0=ALU.mult),
                     reads=[R_cst], writes=[R_vb[r]])
                P.dma("sp", O["bv"][r * 128:(r + 1) * 128, :], vbstg[so][:, :], reads=[R_vbstg[so]], defer=True)
                yield 3.0
                if not full:
                    return
                slot = r % 6
                ba = wrot.next()
                fm_proj(ba, xT, R_x, 128, C_KA, 4)
                P.op("act", _call("activation", out=kaT[:, slot * 512:(slot + 1) * 512], in_=pb[ba][:, :], func=AF.Copy),
                     reads=[R_pb[ba]], writes=[R_ka[slot]])
                if r >= 28:
                    P.op("dve", _call("tensor_copy", out=astg[:, 0:512], in_=pb[ba][:, :]), reads=[R_pb[ba]], writes=[R_astg])
                    P.dma("sp", O["akT"].rearrange("p (j t) -> p j t", t=512)[:, :, (r - 28) * 128:(r - 27) * 128],
                          astg[:, 0:512].rearrange("p (j t) -> p j t", t=128), reads=[R_astg], defer=True)
                yield 3.0
                bva = wrot.next()
                tm_proj(bva, xT, R_x, 128, C_VA, 512)
                vav = va_aug[:, slot * 520:(slot + 1) * 520].rearrange("p (h d) -> p h d", d=65)
                P.op("act", _call("activation", out=vav[:, :, 0:64], in_=pb[bva][:, :].rearrange("p (h d) -> p h d", d=64), func=AF.Copy),
                     reads=[R_pb[bva]], writes=[R_va[slot]])
                P.op("pool", _call("tensor_scalar", out=vav[:, :, 64:65], in0=ones8[:, :].rearrange("p (h o) -> p h o", o=1),
                                   scalar1=kvalid[:, r:r + 1], scalar2=None, op0=ALU.mult),
                     reads=[R_cst], writes=[R_va[slot]])
                if r >= 28:
                    P.op("dve", _call("tensor_copy", out=astg[:, 512:1024], in_=pb[bva][:, :]), reads=[R_pb[bva]], writes=[R_astg])
                    P.dma("sp", O["av"][(r - 28) * 128:(r - 27) * 128, :], astg[:, 512:1024], reads=[R_astg], defer=True)
                yield 3.0

            def qside(xT, R_x, qs, st, st3):
                b1 = wrot.next()
                fm_proj(b1, xT, R_x, qs, C_QA, 4)
                for hf in range(2):
                    P.op("act", _call("activation",
                                      out=qaz[st][hf * 64:(hf + 1) * 64, 0:8 * qs].rearrange("p (j two q) -> p j two q", two=2, q=qs)[:, :, hf, :],
                                      in_=pb[b1][hf * 64:(hf + 1) * 64, 0:4 * qs].rearrange("p (j q) -> p j q", q=qs), func=AF.Copy, scale=0.125),
                         reads=[R_pb[b1]], writes=[R_qa[st]])
                yield 3.0
                b2 = wrot.next()
                fm_proj(b2, xT, R_x, qs, C_QB, 4)
                for g in range(2):
                    P.op("act", _call("activation", out=qbz[st3][g * 64:(g + 1) * 64, g * 4 * qs:(g + 1) * 4 * qs],
                                      in_=pb[b2][g * 64:(g + 1) * 64, 0:4 * qs], func=AF.Copy, scale=0.125),
                         reads=[R_pb[b2]], writes=[R_qb[st3]])
                yield 3.0
                b3 = wrot.next()
                fm_proj(b3, xT, R_x, qs, C_QI, 4)
                for hf in range(2):
                    P.op("act", _call("activation",
                                      out=qiz[st][hf * 64:(hf + 1) * 64, 0:8 * qs].rearrange("p (j two q) -> p j two q", two=2, q=qs)[:, :, hf, :],
                                      in_=pb[b3][hf * 64:(hf + 1) * 64, 0:4 * qs].rearrange("p (j q) -> p j q", q=qs), func=AF.Copy),
                         reads=[R_pb[b3]], writes=[R_qi[st]])
                b4 = wrot.next()
                tm_proj(b4, xT, R_x, qs, C_WI, 8)
                P.op("dve", _call("tensor_scalar", out=coef[st][0:qs, :], in0=pb[b4][0:qs, 0:8], scalar1=float(8.0 ** -1.5), scalar2=None, op0=ALU.mult),
                     reads=[R_pb[b4]], writes=[R_coef[st]])
                for h in range(8):
                    P.op("pool", _call("tensor_scalar", out=dg[st][0:qs, h * 128: h * 128 + qs], in0=ident_f[0:qs, 0:qs],
                                       scalar1=coef[st][0:qs, h:h + 1], scalar2=None, op0=ALU.mult),
                         reads=[R_coef[st], R_ident], writes=[R_dg[st]])
                yield 3.0

            def normalize(bank, qs, mixt, R_m, col0, rec, R_rec):
                ov = pb[bank][0:qs, 0:260].rearrange("p (h d) -> p h d", d=65)
                P.op("dve", _call("tensor_scalar", out=rec[0:qs, 0:4].rearrange("p (h o) -> p h o", o=1), in0=ov[:, :, 64:65],
                                  scalar1=1e-30, scalar2=None, op0=ALU.max),
                     reads=[R_pb[bank]], writes=[R_rec])
                P.op("dve", _call("reciprocal", out=rec[0:qs, 0:4], in_=rec[0:qs, 0:4]), reads=[R_rec], writes=[R_rec])
                for hh in range(4):
                    P.op("dve", _call("tensor_scalar", out=mixt[0:qs, col0 + hh * 64: col0 + (hh + 1) * 64],
                                      in0=pb[bank][0:qs, hh * 65: hh * 65 + 64],
                                      scalar1=rec[0:qs, hh:hh + 1], scalar2=None, op0=ALU.mult),
                         reads=[R_pb[bank], R_rec], writes=[R_m])

            def pipe3(items, s1, s2, s3, D, cost=1.0):
                pend = []
                for it in items:
                    s1(it)
                    s2(it)
                    pend.append(it)
                    if len(pend) > D:
                        s3(pend.pop(0))
                    yield cost
                while pend:
                    s3(pend.pop(0))
                    yield cost

            pta_rot = Rot([0, 1, 2])
            relu_rot = Rot([0, 1, 2])
            ptb_rot = Rot([0, 1, 2])
            brot = Rot([3, 7])

            def front_attn(sn, qs, wins, btiles, prompt_masks, abw):
                st = sn % 2
                mixt, R_m = mixb[sn % 3], R_mix[sn % 3]
                nw = len(wins)

                units = []
                for h in range(8):
                    units.append({"h": h, "t0": 0, "tiles": wins[0:4]})
                    if nw > 4:
                        units.append({"h": h, "t0": 4, "tiles": wins[4:5]})

                def a1(u):
                    h = u["h"]
                    j = h // 2
                    bank = wrot.next()
                    u["bank"] = bank
                    for i, (slot, ts) in enumerate(u["tiles"]):
                        t = u["t0"] + i
                        c0 = i * qs
                        P.op("pe", _call("matmul", out=pb[bank][0:ts, c0:c0 + qs], lhsT=kaT[:, slot * 512 + j * 128: slot * 512 + j * 128 + ts],
                                         rhs=qaz[st][:, h * qs:(h + 1) * qs], start=True, stop=False),
                             reads=[R_ka[slot], R_qa[st]], writes=[R_pb[bank]])
                        P.op("pe", _call("matmul", out=pb[bank][0:ts, c0:c0 + qs], lhsT=ABb[0:qs, h * abw + t * 128: h * abw + t * 128 + ts],
                                         rhs=ident[0:qs, 0:qs], start=False, stop=True),
                             reads=[R_AB, R_ident], writes=[R_pb[bank]])

                def a2(u):
                    k = pta_rot.next()
                    u["pt"], u["R_pt"] = PTA[k], R_PTA[k]
                    bank = u["bank"]
                    tsm = max(ts for (_, ts) in u["tiles"])
                    n = len(u["tiles"])
                    P.op("act", _call("activation", out=u["pt"][0:tsm, 0:n * qs], in_=pb[bank][0:tsm, 0:n * qs], func=AF.Exp),
                         reads=[R_pb[bank]], writes=[u["R_pt"]])

                def a3(u):
                    h = u["h"]
                    last_unit = (u["t0"] + len(u["tiles"]) == nw)
                    for i, (slot, ts) in enumerate(u["tiles"]):
                        t = u["t0"] + i
                        P.op("pe", _call("matmul", out=pb[4][0:qs, (h % 4) * 65:(h % 4) * 65 + 65], lhsT=u["pt"][0:ts, i * qs:(i + 1) * qs],
                                         rhs=va_aug[0:ts, slot * 520 + h * 65: slot * 520 + h * 65 + 65],
                                         start=(h % 4 == 0 and t == 0), stop=(t == nw - 1), skip_group_check=True),
                             reads=[u["R_pt"], R_va[slot]], writes=[R_pb[4]])
                    if last_unit and h % 4 == 3:
                        normalize(4, qs, mixt, R_m, (h // 4) * 256, recA[st], R_recA[st])

                yield from pipe3(units, a1, a2, a3, 2, 0.9)

                L = btiles[-1][1] + btiles[-1][2]
                items = []
                cc = 0
                for c0 in range(0, L, 512):
                    w = min(512, L - c0)
                    rk = [R_ki[tt[0]] for tt in btiles if tt[1] >= c0 - 127 and tt[1] < c0 + w]
                    for h in range(8):
                        items.append({"c0": c0, "w": w, "h": h, "sc": (5, 4)[cc % 2], "rk": rk})
                    cc += 1

                def i1(it):
                    bank = wrot.next()
                    it["bank"] = bank
                    h, c0, w = it["h"], it["c0"], it["w"]
                    P.op("pe", _call("matmul", out=pb[bank][0:qs, 0:w], lhsT=qiz[st][:, h * qs:(h + 1) * qs],
                                     rhs=kbi[:, 4096 + c0: 4096 + c0 + w], start=True, stop=True),
                         reads=[R_qi[st]] + it["rk"], writes=[R_pb[bank]])

                def i2(it):
                    k = relu_rot.next()
                    it["rl"], it["R_rl"] = relu[k], R_relu[k]
                    w = it["w"]
                    P.op("act", _call("activation", out=it["rl"][0:qs, 0:w], in_=pb[it["bank"]][0:qs, 0:w], func=AF.Relu),
                         reads=[R_pb[it["bank"]]], writes=[it["R_rl"]])

                def i3(it):
                    h, c0, w, sc = it["h"], it["c0"], it["w"], it["sc"]
                    P.op("pe", _call("matmul", out=pb[sc][0:qs, 0:w], lhsT=dg[st][0:qs, h * 128: h * 128 + qs], rhs=it["rl"][0:qs, 0:w],
                                     start=(h == 0), stop=(h == 7)),
                         reads=[R_dg[st], it["R_rl"]], writes=[R_pb[sc]])
                    if h == 7:
                        if prompt_masks and c0 < 2048:
                            wm = min(w, 2048 - c0)
                            P.op("act", _call("activation", out=score[st][0:qs, c0:c0 + wm], in_=pb[sc][0:qs, 0:wm], func=AF.Identity,
                                              bias=colmask[0:qs, 0:1]),
                                 reads=[R_pb[sc], R_cst], writes=[R_score[st]])
                            if wm < w:
                                P.op("act", _call("activation", out=score[st][0:qs, c0 + wm:c0 + w], in_=pb[sc][0:qs, wm:w], func=AF.Copy),
                                     reads=[R_pb[sc]], writes=[R_score[st]])
                        else:
                            P.op("act", _call("activation", out=score[st][0:qs, c0:c0 + w], in_=pb[sc][0:qs, 0:w], func=AF.Copy),
                                 reads=[R_pb[sc]], writes=[R_score[st]])

                yield from pipe3(items, i1, i2, i3, 2, 0.65)
                if prompt_masks:
                    P.op("dve", _call("tensor_tensor", out=score[st][0:qs, L - 128:L], in0=score[st][0:qs, L - 128:L], in1=diagm[0:qs, :], op=ALU.add),
                         reads=[R_score[st], R_cst], writes=[R_score[st]])
                yield

            def bis_gen(sn, qs, btiles, bnw):
                st = sn % 2
                sm, R_sm = small[st], R_small[st]
                L = btiles[-1][1] + btiles[-1][2]
                P.op("dve", _call("memset", sm[0:qs, 1:2], 0.0), writes=[R_sm])
                for k in range(NIT):
                    wk = BIS_W0 / (2.0 ** k)
                    P.op("dve", _call("tensor_scalar", out=Mb[st][0:qs, 0:L], in0=score[st][0:qs, 0:L], scalar1=sm[0:qs, 1:2], scalar2=None,
                                      op0=ALU.is_ge, op1=ALU.add, accum_out=sm[0:qs, 0:1]),
                         reads=[R_score[st], R_sm], writes=[R_M[st], R_sm])
                    P.op("dve", _call("tensor_scalar", out=sm[0:qs, 2:3], in0=sm[0:qs, 0:1], scalar1=255.5, scalar2=wk,
                                      op0=ALU.is_ge, op1=ALU.mult),
                         reads=[R_sm], writes=[R_sm])
                    P.op("dve", _call("scalar_tensor_tensor", out=sm[0:qs, 1:2], in0=sm[0:qs, 2:3], scalar=-wk / 2.0,
                                      in1=sm[0:qs, 1:2], op0=ALU.add, op1=ALU.add),
                         reads=[R_sm], writes=[R_sm])
                    yield L * 1.08e-3 + 0.5
                wl = BIS_W0 / (2.0 ** (NIT - 1)) / 2.0
                P.op("dve", _call("tensor_scalar", out=sm[0:qs, 3:4], in0=sm[0:qs, 1:2], scalar1=-wl, scalar2=None, op0=ALU.add),
                     reads=[R_sm], writes=[R_sm])
                P.op("dve", _call("tensor_scalar", out=Mb[st][0:qs, 0:L], in0=score[st][0:qs, 0:L], scalar1=sm[0:qs, 3:4], scalar2=NEGM,
                                  op0=ALU.is_lt, op1=ALU.mult),
                     reads=[R_score[st], R_sm], writes=[R_M[st]])
                nearw = btiles[-2][2] + btiles[-1][2]
                for h in range(8):
                    P.op("dve", _call("tensor_tensor", out=Mnear[st][0:qs, h * bnw: h * bnw + nearw], in0=Bnb[0:qs, h * bnw: h * bnw + nearw],
                                      in1=Mb[st][0:qs, L - nearw:L], op=ALU.add),
                         reads=[R_Bn, R_M[st]], writes=[R_Mnear[st]])
                yield
            def battn_gen(sn, qs, btiles, blk, bnw):
                st = sn % 2
                st3 = sn % 3
                mixt, R_m = mixb[st3], R_mix[st3]
                nb = len(btiles)
                items = [{"g": g, "t": t, "vt": vt, "c0": c0, "ts": ts} for g in range(2) for t, (vt, c0, ts) in enumerate(btiles)]

                def b1(it):
                    g, t, vt, c0, ts = it["g"], it["t"], it["vt"], it["c0"], it["ts"]
                    bank = brot.next()
                    it["bank"] = bank
                    P.op("pe", _call("matmul", out=pb[bank][0:ts, 0:4 * qs], lhsT=kbi[:, c0:c0 + ts],
                                     rhs=qbz[st3][:, g * 4 * qs:(g + 1) * 4 * qs], start=True, stop=False),
                         reads=[R_kbi[vt], R_qb[st3]], writes=[R_pb[bank]])
                    if t < nb - 2 and qs == 128:
                        P.op("pe", _call("matmul", out=pb[bank][0:ts, 0:512], lhsT=Mb[st][0:qs, c0:c0 + ts], rhs=ident[0:128, 0:512],
                                         start=False, stop=True),
                             reads=[R_M[st], R_ident], writes=[R_pb[bank]])
                    elif t < nb - 2:
                        for r in range(4):
                            P.op("pe", _call("matmul", out=pb[bank][0:ts, r * qs:(r + 1) * qs], lhsT=Mb[st][0:qs, c0:c0 + ts],
                                             rhs=ident[0:qs, 0:qs], start=False, stop=(r == 3)),
                                 reads=[R_M[st], R_ident], writes=[R_pb[bank]])
                    else:
                        tt = t - (nb - 2)
                        for r in range(4):
                            hh = g * 4 + r
                            P.op("pe", _call("matmul", out=pb[bank][0:ts, r * qs:(r + 1) * qs],
                                             lhsT=Mnear[st][0:qs, hh * bnw + tt * 128: hh * bnw + tt * 128 + ts], rhs=ident[0:qs, 0:qs],
                                             start=False, stop=(r == 3)),
                                 reads=[R_Mnear[st], R_ident], writes=[R_pb[bank]])

                def b2(it):
                    k = ptb_rot.next()
                    it["ptb"], it["R_ptb"] = PTB[k], R_PTB[k]
                    ts = it["ts"]
                    P.op("act", _call("activation", out=it["ptb"][0:ts, 0:4 * qs], in_=pb[it["bank"]][0:ts, 0:4 * qs], func=AF.Exp),
                         reads=[R_pb[it["bank"]]], writes=[it["R_ptb"]])

                def b3(it):
                    g, t, vt, ts = it["g"], it["t"], it["vt"], it["ts"]
                    for r in range(4):
                        P.op("pe", _call("matmul", out=pb[6][0:qs, r * 65: r * 65 + 65], lhsT=it["ptb"][0:ts, r * qs:(r + 1) * qs],
                                         rhs=vb_aug[0:ts, (vt * 2 + g) * 65:(vt * 2 + g) * 65 + 65],
                                         start=(t == 0 and r == 0), stop=(t == nb - 1), skip_group_check=True),
                             reads=[it["R_ptb"], R_vb[vt]], writes=[R_pb[6]])
                    if t == nb - 1:
                        normalize(6, qs, mixt, R_m, 512 + g * 256, recB[st], R_recB[st])

                yield from pipe3(items, b1, b2, b3, 1, 0.8)
                P.dma("sp", mixD[blk * 128: blk * 128 + qs, :], mixt[0:qs, :], reads=[R_m], writes=[R_mixD[blk]], defer=True)
                yield

            load_xT(0, "dve")
            for r in range(16):
                if r + 1 < 16:
                    load_xT(r + 1, "dve")
                for _ in kside(r, full=(r >= 11)):
                    pass
            checkpoint('phase0')

            def prompt_front(sn, T):
                if T >= 16:
                    load_xT(T)
                    yield from kside(T, full=True)
                s = T % 2
                yield from qside(xTb[s], R_xT[s], 128, sn % 2, sn % 3)
                wins = [((T - 4 + t) % 6, 128) for t in range(5)]
                btiles = [(t, t * 128, 128) for t in range(T + 1)]
                yield from front_attn(sn, 128, wins, btiles, True, ABW)

            def prompt_bis(sn, T):
                btiles = [(t, t * 128, 128) for t in range(T + 1)]
                yield from bis_gen(sn, 128, btiles, BNW)

            def prompt_battn(sn, T, blk):
                btiles = [(t, t * 128, 128) for t in range(T + 1)]
                yield from battn_gen(sn, 128, btiles, blk, BNW)

            steps = [(0, 15, 16)] + [(1 + i, 16 + i, i) for i in range(16)]
            ns = len(steps)
            SN = ns
            sst = SN % 2
            s_wins = [(0, 128), (1, 128), (2, 128), (3, 128), (4, 16)]
            s_btiles = [(t, t * 128, 128) for t in range(16)] + [(16, 2048, 16)]
            xs_, R_xs = xTb[0], R_xT[0]

            def sample_front():
                stg, R_stg = score[sst], R_score[sst]
                P.dma("sp", stg[:, 0:2048], I["cbiT"][:, :], writes=[R_stg])
                P.op("act", _call("activation", out=kbi[:, 4096:4096 + 2048], in_=stg[:, 0:2048], func=AF.Copy),
                     reads=[R_stg], writes=R_ki[0:16])
                P.dma("sp", stg[:, 2048:4096], I["cakT"][:, :], writes=[R_stg])
                for s4 in range(4):
                    P.op("act", _call("activation", out=kaT[:, s4 * 512:(s4 + 1) * 512].rearrange("p (j t) -> p j t", t=128),
                                      in_=stg[:, 2048:4096].rearrange("p (j t) -> p j t", t=512)[:, :, s4 * 128:(s4 + 1) * 128], func=AF.Copy),
                         reads=[R_stg], writes=[R_ka[s4]])
                yield 3.0
                P.dma("sp", stg[:, 0:2048].rearrange("p (t c) -> p t c", c=512), I["cav"].rearrange("(t p) c -> p t c", p=128), writes=[R_stg])
                vaall = va_aug[:, 0:4 * 520].rearrange("p (t d) -> p t d", d=65)
                P.op("act", _call("activation", out=vaall[:, :, 0:64], in_=stg[:, 0:2048].rearrange("p (t d) -> p t d", d=64), func=AF.Copy),
                     reads=[R_stg], writes=R_va[0:5])
                P.op("pool", _call("memset", va_aug[:, 0:5 * 520].rearrange("p (t d) -> p t d", d=65)[:, :, 64:65], 1.0), writes=R_va[0:5])
                for hh in range(2):
                    w = 4 * 528
                    P.dma("sp", stg[0:16, 0:w], I["ABs"][:, hh * w:(hh + 1) * w], writes=[R_stg])
                    P.op("act", _call("activation", out=ABb[0:16, hh * w:(hh + 1) * w], in_=stg[0:16, 0:w], func=AF.Copy),
                         reads=[R_stg], writes=[R_AB])
                P.op("pool", _call("memset", qaz[sst][:, :], 0.0), writes=[R_qa[sst]])
                P.op("pool", _call("memset", qbz[SN % 3][:, :], 0.0), writes=[R_qb[SN % 3]])
                P.op("pool", _call("memset", qiz[sst][:, :], 0.0), writes=[R_qi[sst]])
                P.dma("sp", xstg[:, 0:128], I["xsT"][:, :], writes=[R_xstg])
                P.op("pool", _call("tensor_copy", out=xTb[0][:, 0:128], in_=xstg[:, 0:128]), reads=[R_xstg], writes=[R_xT[0]])
                yield 3.0
                bk = wrot.next()
                fm_proj(bk, xs_, R_xs, 16, C_KI, 1)
                P.op("act", _call("activation", out=kbi[:, 4096 + 2048:4096 + 2064], in_=pb[bk][:, 0:16], func=AF.Copy), reads=[R_pb[bk]], writes=[R_ki[16]])
                P.op("dve", _call("tensor_copy", out=ostg[0][:, 16:32], in_=pb[bk][:, 0:16]), reads=[R_pb[bk]], writes=[R_ostg[0]])
                P.dma("sp", O["sbiT"][:, :], ostg[0][0:64, 16:32], reads=[R_ostg[0]], defer=True)
                ba = wrot.next()
                fm_proj(ba, xs_, R_xs, 16, C_KA, 4)
                P.op("act", _call("activation", out=kaT[:, 4 * 512:5 * 512].rearrange("p (j t) -> p j t", t=128)[:, :, 0:16],
                                  in_=pb[ba][:, 0:64].rearrange("p (j t) -> p j t", t=16), func=AF.Copy),
                     reads=[R_pb[ba]], writes=[R_ka[4]])
                P.op("dve", _call("tensor_copy", out=astg[:, 0:64], in_=pb[ba][:, 0:64]), reads=[R_pb[ba]], writes=[R_astg])
                P.dma("sp", O["sakT"][:, :], astg[:, 0:64], reads=[R_astg], defer=True)
                bva = wrot.next()
                tm_proj(bva, xs_, R_xs, 16, C_VA, 512)
                vav = va_aug[0:16, 4 * 520:5 * 520].rearrange("p (h d) -> p h d", d=65)
                P.op("act", _call("activation", out=vav[:, :, 0:64], in_=pb[bva][0:16, :].rearrange("p (h d) -> p h d", d=64), func=AF.Copy),
                     reads=[R_pb[bva]], writes=[R_va[4]])
                P.op("dve", _call("tensor_copy", out=astg[0:16, 512:1024], in_=pb[bva][0:16, :]), reads=[R_pb[bva]], writes=[R_astg])
                P.dma("sp", O["sav"][:, :], astg[0:16, 512:1024], reads=[R_astg], defer=True)
                yield 3.0
                yield from qside(xs_, R_xs, 16, sst, SN % 3)
                yield from front_attn(SN, 16, s_wins, s_btiles, False, 528)

            def sample_bis():
                stg, R_stg = score[1 - sst], R_score[1 - sst]
                P.dma("sp", stg[0:16, 0:8 * 144], I["Bns"][:, :], writes=[R_stg])
                for h in range(8):
                    P.op("dve", _call("tensor_scalar", out=Bnb[0:16, h * 144:(h + 1) * 144], in0=stg[0:16, h * 144:(h + 1) * 144],
                                      scalar1=c15[0:16, h:h + 1], scalar2=None, op0=ALU.subtract),
                         reads=[R_stg, R_cst], writes=[R_Bn])
                yield 1.0
                yield from bis_gen(SN, 16, s_btiles, 144)

            def sample_battn():
                stg, R_stg = score[1 - sst], R_score[1 - sst]
                P.dma("sp", stg[:, 0:2048], I["cbkT"][:, :], writes=[R_stg])
                P.op("act", _call("activation", out=kbi[:, 0:2048], in_=stg[:, 0:2048], func=AF.Copy),
                     reads=[R_stg], writes=R_kbi[0:16])
                P.dma("sp", stg[:, 2048:4096].rearrange("p (t c) -> p t c", c=128), I["cbv"].rearrange("(t p) c -> p t c", p=128), writes=[R_stg])
                vball = vb_aug[:, 0:16 * 130].rearrange("p (t d) -> p t d", d=65)
                P.op("act", _call("activation", out=vball[:, :, 0:64], in_=stg[:, 2048:4096].rearrange("p (t d) -> p t d", d=64), func=AF.Copy),
                     reads=[R_stg], writes=R_vb[0:17])
                P.op("pool", _call("memset", vb_aug[:, 0:17 * 130].rearrange("p (t d) -> p t d", d=65)[:, :, 64:65], 1.0), writes=R_vb[0:17])
                bk = wrot.next()
                fm_proj(bk, xs_, R_xs, 16, C_KB, 1)
                P.op("act", _call("activation", out=kbi[:, 2048:2064], in_=pb[bk][:, 0:16], func=AF.Copy), reads=[R_pb[bk]], writes=[R_kbi[16]])
                P.op("dve", _call("tensor_copy", out=ostg[1][:, 0:16], in_=pb[bk][:, 0:16]), reads=[R_pb[bk]], writes=[R_ostg[1]])
                P.dma("sp", O["sbkT"][:, :], ostg[1][:, 0:16], reads=[R_ostg[1]], defer=True)
                bv_ = wrot.next()
                tm_proj(bv_, xs_, R_xs, 16, C_VB, 128)
                vbv = vb_aug[0:16, 16 * 130:17 * 130].rearrange("p (g d) -> p g d", d=65)
                P.op("act", _call("activation", out=vbv[:, :, 0:64], in_=pb[bv_][0:16, 0:128].rearrange("p (g d) -> p g d", d=64), func=AF.Copy),
                     reads=[R_pb[bv_]], writes=[R_vb[16]])
                P.op("dve", _call("tensor_copy", out=vbstg[0][0:16, :], in_=pb[bv_][0:16, 0:128]), reads=[R_pb[bv_]], writes=[R_vbstg[0]])
                P.dma("sp", O["sbv"][:, :], vbstg[0][0:16, :], reads=[R_vbstg[0]], defer=True)
                yield 3.0
                yield from battn_gen(SN, 16, s_btiles, 17, 144)

            for tick in range(ns + 3):
                gens = []
                if 0 <= tick - 2 < ns:
                    gens.append(prompt_battn(*steps[tick - 2]))
                elif tick - 2 == ns:
                    gens.append(sample_battn())
                if 0 <= tick - 1 < ns:
                    gens.append(prompt_bis(*steps[tick - 1][0:2]))
                elif tick - 1 == ns:
                    gens.append(sample_bis())
                if tick < ns:
                    gens.append(prompt_front(*steps[tick][0:2]))
                elif tick == ns:
                    gens.append(sample_front())
                run_interleaved(gens)
            checkpoint('steps')
            checkpoint('phaseA')
            P.flush(block)

        P.barrier()
        with ExitStack() as sbk:
            wob = sb(sbk, "wob", [128, 8 * 1024], BF16)
            wmqb = sb(sbk, "wmqb", [128, 8 * 512], BF16)
            wmob = sb(sbk, "wmob", [128, 4 * 1024], BF16)
            wtmp = sb(sbk, "wtmp", [128, 8 * 512], BF16)
            R_wo, R_wmq, R_wmo, R_wtmp = Res("wo"), Res("wmq"), Res("wmo"), Res("wtmp")
            wst = [sb(sbk, "wst%d" % k, [128, 2048], F32) for k in range(2)]
            R_wst = [Res("wst%d" % k) for k in range(2)]
            lnt = sb(sbk, "lnt", [128, 4 * 1024], F32)
            R_ln = Res("ln")
            memTb = sb(sbk, "memTb", [128, 8 * 256], BF16)
            R_memT = Res("memT")
            mkT = [sb(sbk, "mkT%d" % k, [128, 4 * 256], BF16) for k in range(2)]
            mva = [sb(sbk, "mva%d" % k, [128, 2 * 4 * 129], BF16) for k in range(2)]
            R_mk = [Res("mk%d" % k) for k in range(2)]
            R_mv = [Res("mv%d" % k) for k in range(2)]
            mixl = [sb(sbk, "mixl%d" % k, [128, 1024], BF16) for k in range(4)]
            R_mixl = [Res("mixl%d" % k) for k in range(4)]
            xr = [sb(sbk, "xr%d" % k, [128, 1024], F32) for k in range(4)]
            R_xr = [Res("xr%d" % k) for k in range(4)]
            NB3 = 4
            tT_l = [sb(sbk, "tT%d" % k, [128, 1024], BF16) for k in range(NB3)]
            hA_l = [sb(sbk, "hA%d" % k, [128, 1024], F32) for k in range(NB3)]
            hB_l = [sb(sbk, "hB%d" % k, [128, 1024], F32) for k in range(NB3)]
            h16_l = [sb(sbk, "h16%d" % k, [128, 1024], BF16) for k in range(NB3)]
            qmT_l = [sb(sbk, "qmT%d" % k, [128, 512], BF16) for k in range(NB3)]
            PTm_l = [sb(sbk, "PTm%d" % k, [128, 1024], BF16) for k in range(NB3)]
            o16_l = [sb(sbk, "o16%d" % k, [128, 512], BF16) for k in range(NB3)]
            oT_l = [sb(sbk, "oT%d" % k, [128, 512], BF16) for k in range(NB3)]
            stat_l = [sb(sbk, "stat%d" % k, [128, 32], F32) for k in range(NB3)]
            RB = [{n: Res(n + str(k)) for n in ("tT", "hA", "hB", "h16", "qm", "PTm", "o16", "oT", "stat")} for k in range(NB3)]
            h2T = [sb(sbk, "h2T%d" % k, [128, 1024], BF16) for k in range(4)]
            R_h2T = [Res("h2T%d" % k) for k in range(4)]
            mstg = sb(sbk, "mstg", [128, 1024], F32)
            R_mstg = Res("mstg")
            wrot = Rot([0, 1, 2, 3, 4, 5, 6, 7])

            def load_cast(dst, R_dst, src, ncols, engs=("act", "dve")):
                k = 0
                for c0 in range(0, ncols, 2048):
                    w = min(2048, ncols - c0)
                    s = k % 2
                    P.dma("sp", wst[s][:, 0:w], src[:, c0:c0 + w], writes=[R_wst[s]])
                    eng = engs[k % len(engs)]
                    if eng == "act":
                        P.op("act", _call("activation", out=dst[:, c0:c0 + w], in_=wst[s][:, 0:w], func=AF.Copy),
                             reads=[R_wst[s]], writes=[R_dst])
                    else:
                        P.op(eng, _call("tensor_copy", out=dst[:, c0:c0 + w], in_=wst[s][:, 0:w]),
                             reads=[R_wst[s]], writes=[R_dst])
                    k += 1

            load_cast(wob, R_wo, I["wo"], 8192)
            load_cast(wmqb, R_wmq, I["wmq"], 4096)
            load_cast(wmob, R_wmo, I["wmo"], 4096)
            for k in range(4):
                P.dma("sp", lnt[:, k * 1024:(k + 1) * 1024], I["lnp"][k:k + 1, :].to_broadcast([128, 1024]), writes=[R_ln])
            load_cast(memTb, R_memT, I["memT"], 2048)
            load_cast(wtmp, R_wtmp, I["wmk"], 4096)
            for h in range(4):
                bank = wrot.next()
                for kc in range(KC):
                    P.op("pe", _call("matmul",
                        out=pb[bank][:, 0:256], lhsT=wtmp[:, kc * 512 + h * 128: kc * 512 + (h + 1) * 128],
                        rhs=memTb[:, kc * 256:(kc + 1) * 256], start=(kc == 0), stop=(kc == KC - 1)),
                        reads=[R_wtmp, R_memT], writes=[R_pb[bank]])
                P.op("act", _call("activation", out=mkT[0][:, h * 256:(h + 1) * 256], in_=pb[bank][:, 0:256], func=AF.Copy),
                     reads=[R_pb[bank]], writes=[R_mk[0]])
                P.op("dve", _call("tensor_copy", out=mstg[:, h * 256:(h + 1) * 256], in_=pb[bank][:, 0:256]),
                     reads=[R_pb[bank]], writes=[R_mstg])
            P.dma("sp", O["mkT"][:, :], mstg[:, :], reads=[R_mstg], defer=True)
            load_cast(wtmp, R_wtmp, I["wmv"], 4096)
            for mt in range(2):
                bank = wrot.next()
                for kc in range(KC):
                    P.op("pe", _call("matmul",
                        out=pb[bank][:, 0:512], lhsT=memTb[:, kc * 256 + mt * 128: kc * 256 + (mt + 1) * 128],
                        rhs=wtmp[:, kc * 512:(kc + 1) * 512], start=(kc == 0), stop=(kc == KC - 1)),
                        reads=[R_wtmp, R_memT], writes=[R_pb[bank]])
                mvv = mva[0][:, mt * 516:(mt + 1) * 516].rearrange("p (h d) -> p h d", d=129)
                P.op("act", _call("activation", out=mvv[:, :, 0:128], in_=pb[bank][:, :].rearrange("p (h d) -> p h d", d=128), func=AF.Copy),
                     reads=[R_pb[bank]], writes=[R_mv[0]])
                P.op("dve", _call("tensor_copy", out=mstg[:, mt * 512:(mt + 1) * 512], in_=pb[bank][:, :]),
                     reads=[R_pb[bank]], writes=[R_mstg])
                P.dma("sp", O["mv"][mt * 128:(mt + 1) * 128, :], mstg[:, mt * 512:(mt + 1) * 512], reads=[R_mstg], defer=True)
            for k in range(2):
                P.op("pool", _call("memset", mva[k][:, :].rearrange("p (t d) -> p t d", d=129)[:, :, 128:129], 1.0), writes=[R_mv[k]])
            load_cast(mkT[1], R_mk[1], I["cmkT"], 1024)
            P.dma("sp", wst[0][:, 0:1024].rearrange("p (t c) -> p t c", c=512), I["cmv"].rearrange("(t p) c -> p t c", p=128), writes=[R_wst[0]])
            P.op("act", _call("activation", out=mva[1][:, :].rearrange("p (t d) -> p t d", d=129)[:, :, 0:128],
                                               in_=wst[0][:, 0:1024].rearrange("p (t d) -> p t d", d=128), func=AF.Copy),
                 reads=[R_wst[0]], writes=[R_mv[1]])

            checkpoint('phaseB_pre')
            def transpose_to(src16, R_src, qs, nchunk, dst, R_dst):
                bank = wrot.next()
                pbf = pb[bank][:, :].bitcast(BF16)
                for c in range(nchunk):
                    P.op("pe", _call("transpose", out=pbf[:, c * qs:(c + 1) * qs], in_=src16[0:qs, c * 128:(c + 1) * 128],
                                                                   identity=ident[0:qs, 0:qs]),
                         reads=[R_src, R_ident], writes=[R_pb[bank]])
                P.op("act", _call("activation", out=dst[:, 0:nchunk * qs], in_=pbf[:, 0:nchunk * qs], func=AF.Copy),
                     reads=[R_pb[bank]], writes=[R_dst])

            def layer_norm(hin, R_hin, qs, gcol, hout, R_hout, stat, R_stat):
                for c in range(2):
                    P.op("dve", _call("bn_stats", out=stat[0:qs, c * 6:(c + 1) * 6], in_=hin[0:qs, c * 512:(c + 1) * 512]),
                         reads=[R_hin], writes=[R_stat])
                P.op("dve", _call("bn_aggr", out=stat[0:qs, 12:14], in_=stat[0:qs, 0:12]), reads=[R_stat], writes=[R_stat])
                P.op("dve", _call("tensor_scalar", out=stat[0:qs, 14:15], in0=stat[0:qs, 13:14], scalar1=LN_EPS, scalar2=None, op0=ALU.add),
                     reads=[R_stat], writes=[R_stat])
                P.op("act", _call("activation", out=stat[0:qs, 15:16], in_=stat[0:qs, 14:15], func=AF.Sqrt), reads=[R_stat], writes=[R_stat])
                P.op("dve", _call("reciprocal", out=stat[0:qs, 16:17], in_=stat[0:qs, 15:16]), reads=[R_stat], writes=[R_stat])
                P.op("dve", _call("scalar_tensor_tensor", out=stat[0:qs, 17:18], in0=stat[0:qs, 12:13], scalar=-1.0, in1=stat[0:qs, 16:17],
                                                             op0=ALU.mult, op1=ALU.mult),
                     reads=[R_stat], writes=[R_stat])
                P.op("act", _call("activation", out=hout[0:qs, :], in_=hin[0:qs, :], func=AF.Identity, scale=stat[0:qs, 16:17], bias=stat[0:qs, 17:18]),
                     reads=[R_hin, R_stat], writes=[R_hout])
                P.op("dve", _call("tensor_tensor", out=hout[0:qs, :], in0=hout[0:qs, :], in1=lnt[0:qs, gcol * 1024:(gcol + 1) * 1024], op=ALU.mult),
                     reads=[R_hout, R_ln], writes=[R_hout])
                P.op("dve", _call("tensor_tensor", out=hout[0:qs, :], in0=hout[0:qs, :], in1=lnt[0:qs, (gcol + 1) * 1024:(gcol + 2) * 1024], op=ALU.add),
                     reads=[R_hout, R_ln], writes=[R_hout])

            def phaseB_block(blk, qs, row0, mi, k2):
                s = k2
                tT, hA, hB, h16, qmT, PTm, o16, oT, stat = (tT_l[k2], hA_l[k2], hB_l[k2], h16_l[k2], qmT_l[k2], PTm_l[k2], o16_l[k2],
                                                             oT_l[k2], stat_l[k2])
                R_tT, R_hA, R_hB, R_h16, R_qm, R_PTm, R_o16, R_oT, R_stat = (RB[k2][n] for n in ("tT", "hA", "hB", "h16", "qm", "PTm", "o16", "oT", "stat"))
                P.dma("sp", mixl[s][0:qs, :], mixD[blk * 128 + row0: blk * 128 + row0 + qs, :], reads=[R_mixD[blk]], writes=[R_mixl[s]])
                P.dma("sp", xr[s][0:qs, :], I["xres"][blk * 128: blk * 128 + qs, :], writes=[R_xr[s]])
                transpose_to(mixl[s], R_mixl[s], qs, 8, tT, R_tT)
                yield
                b0, b1 = wrot.next(), wrot.next()
                for n, bank in enumerate((b0, b1)):
                    for kc in range(KC):
                        P.op("pe", _call("matmul",
                            out=pb[bank][0:qs, :], lhsT=tT[:, kc * qs:(kc + 1) * qs], rhs=wob[:, kc * 1024 + n * 512: kc * 1024 + (n + 1) * 512],
                            start=(kc == 0), stop=(kc == KC - 1)),
                            reads=[R_tT, R_wo], writes=[R_pb[bank]])
                    P.op("dve", _call("scalar_tensor_tensor",
                        out=hA[0:qs, n * 512:(n + 1) * 512], in0=xr[s][0:qs, n * 512:(n + 1) * 512], scalar=ALPHA, in1=pb[bank][0:qs, :],
                        op0=ALU.mult, op1=ALU.add),
                        reads=[R_xr[s], R_pb[bank]], writes=[R_hA])
                yield
                layer_norm(hA, R_hA, qs, 0, hB, R_hB, stat, R_stat)
                yield
                P.op("act", _call("activation", out=h16[0:qs, :], in_=hB[0:qs, :], func=AF.Copy), reads=[R_hB], writes=[R_h16])
                transpose_to(h16, R_h16, qs, 8, tT, R_tT)
                yield
                bq = wrot.next()
                for h in range(4):
                    for kc in range(KC):
                        P.op("pe", _call("matmul",
                            out=pb[bq][:, h * qs:(h + 1) * qs], lhsT=wmqb[:, kc * 512 + h * 128: kc * 512 + (h + 1) * 128],
                            rhs=tT[:, kc * qs:(kc + 1) * qs], start=(kc == 0), stop=(kc == KC - 1)),
                            reads=[R_wmq, R_tT], writes=[R_pb[bq]])
                P.op("act", _call("activation", out=qmT[:, 0:4 * qs], in_=pb[bq][:, 0:4 * qs], func=AF.Copy, scale=float(128.0 ** -0.5)),
                     reads=[R_pb[bq]], writes=[R_qm])
                yield
                bs0, bs1 = wrot.next(), wrot.next()
                for h in range(4):
                    for mt in range(2):
                        idx = h * 2 + mt
                        bank = bs0 if idx < 4 else bs1
                        c0 = (idx % 4) * qs
                        P.op("pe", _call("matmul",
                            out=pb[bank][:, c0:c0 + qs], lhsT=mkT[mi][:, h * 256 + mt * 128: h * 256 + (mt + 1) * 128],
                            rhs=qmT[:, h * qs:(h + 1) * qs], start=True, stop=True),
                            reads=[R_mk[mi], R_qm], writes=[R_pb[bank]])
                for k, bank in enumerate((bs0, bs1)):
                    P.op("act", _call("activation", out=PTm[:, k * 4 * qs:(k + 1) * 4 * qs], in_=pb[bank][:, 0:4 * qs], func=AF.Exp),
                         reads=[R_pb[bank]], writes=[R_PTm])
                yield
                bo0, bo1 = wrot.next(), wrot.next()
                for h in range(4):
                    bank = bo0 if h < 2 else bo1
                    for mt in range(2):
                        idx = h * 2 + mt
                        P.op("pe", _call("matmul",
                            out=pb[bank][0:qs, (h % 2) * 129:(h % 2) * 129 + 129], lhsT=PTm[:, idx * qs:(idx + 1) * qs],
                            rhs=mva[mi][:, (mt * 4 + h) * 129:(mt * 4 + h) * 129 + 129],
                            start=(h % 2 == 0 and mt == 0), stop=(mt == 1), skip_group_check=True),
                            reads=[R_PTm, R_mv[mi]], writes=[R_pb[bank]])
                for k, bank in enumerate((bo0, bo1)):
                    ov = pb[bank][0:qs, 0:258].rearrange("p (h d) -> p h d", d=129)
                    P.op("dve", _call("tensor_scalar", out=stat[0:qs, 20 + 2 * k:22 + 2 * k].rearrange("p (h o) -> p h o", o=1),
                                                                      in0=ov[:, :, 128:129], scalar1=1e-30, scalar2=None, op0=ALU.max),
                         reads=[R_pb[bank]], writes=[R_stat])
                    P.op("dve", _call("reciprocal", out=stat[0:qs, 20 + 2 * k:22 + 2 * k], in_=stat[0:qs, 20 + 2 * k:22 + 2 * k]),
                         reads=[R_stat], writes=[R_stat])
                    for hh in range(2):
                        h = k * 2 + hh
                        P.op("dve", _call("tensor_scalar",
                            out=o16[0:qs, h * 128:(h + 1) * 128], in0=pb[bank][0:qs, hh * 129: hh * 129 + 128],
                            scalar1=stat[0:qs, 20 + 2 * k + hh:21 + 2 * k + hh], scalar2=None, op0=ALU.mult),
                            reads=[R_pb[bank], R_stat], writes=[R_o16])
                yield
                transpose_to(o16, R_o16, qs, 4, oT, R_oT)
                yield
                b0, b1 = wrot.next(), wrot.next()
                for n, bank in enumerate((b0, b1)):
                    for c in range(4):
                        P.op("pe", _call("matmul",
                            out=pb[bank][0:qs, :], lhsT=oT[:, c * qs:(c + 1) * qs], rhs=wmob[:, c * 1024 + n * 512: c * 1024 + (n + 1) * 512],
                            start=(c == 0), stop=(c == 3)),
                            reads=[R_oT, R_wmo], writes=[R_pb[bank]])
                    P.op("dve", _call("scalar_tensor_tensor",
                        out=hA[0:qs, n * 512:(n + 1) * 512], in0=hB[0:qs, n * 512:(n + 1) * 512], scalar=ALPHA, in1=pb[bank][0:qs, :],
                        op0=ALU.mult, op1=ALU.add),
                        reads=[R_hB, R_pb[bank]], writes=[R_hA])
                yield
                layer_norm(hA, R_hA, qs, 2, hB, R_hB, stat, R_stat)
                yield
                P.dma("sp", h2D[blk * 128: blk * 128 + qs, :], hB[0:qs, :], reads=[R_hB], writes=[R_h2D[blk]], defer=True)
                P.op("act", _call("activation", out=h16[0:qs, :], in_=hB[0:qs, :], func=AF.Copy), reads=[R_hB], writes=[R_h16])
                transpose_to(h16, R_h16, qs, 8, h2T[s], R_h2T[s])
                P.dma("sp", h2TD[blk][:, 0:8 * qs], h2T[s][:, 0:8 * qs], reads=[R_h2T[s]], writes=[R_h2TD[blk]], defer=True)
                yield

            def run_staggered(gens, lag):
                active = []
                pending = list(gens)
                tick = 0
                while active or pending:
                    if pending and (not active or tick >= lag):
                        active.append(pending.pop(0))
                        tick = 0
                    for g in list(active):
                        try:
                            next(g)
                        except StopIteration:
                            active.remove(g)
                    tick += 1

            blocks = [(16, 2, 126, 0), (17, 16, 0, 1)] + [(i, 128, 0, 0) for i in range(16)]
            run_staggered([phaseB_block(b_, q_, r_, m_, pos % 4) for pos, (b_, q_, r_, m_) in enumerate(blocks)], 3)
            checkpoint('phaseB')
            P.flush(block)

        P.barrier()
        with ExitStack() as sc:
            wdb = sb(sc, "wdb", [128, NFC * 1024], BF16)
            R_wd = Res("wd")
            wst = [sb(sc, "wstc%d" % k, [128, 2048], F32) for k in range(2)]
            R_wst = [Res("wstc%d" % k) for k in range(2)]
            wsl = [sb(sc, "wsl%d" % k, [128, 2048], BF16) for k in range(2)]
            R_wsl = [Res("wsl%d" % k) for k in range(2)]
            R_wslB = [Res("wslB%d" % k) for k in range(2)]
            hT2 = [sb(sc, "hT%d" % k, [128, NFC * 512], BF16) for k in range(2)]
            R_hT2 = [Res("hT%d" % k) for k in range(2)]
            hTm = sb(sc, "hTm", [128, NFC * 16], BF16)
            R_hTm = Res("hTm")
            h2Tg = [sb(sc, "h2Tg%d" % k, [128, 8 * 512], BF16) for k in range(2)]
            R_h2Tg = [Res("h2Tg%d" % k) for k in range(2)]
            h2Tm = sb(sc, "h2Tm", [128, 8 * 18], BF16)
            R_h2Tm = Res("h2Tm")
            Gb = [sb(sc, "Gb%d" % k, [128, 532], F32) for k in range(3)]
            R_Gb = [Res("Gb%d" % k) for k in range(3)]
            Gs = sb(sc, "Gs", [128, 18], F32)
            R_Gs = Res("Gs")
            t0b = [sb(sc, "t0b%d" % k, [128, 530], F32) for k in range(3)]
            R_t0 = [Res("t0%d" % k) for k in range(3)]
            geb = [sb(sc, "geb%d" % k, [128, 530], F32) for k in range(3)]
            R_ge = [Res("ge%d" % k) for k in range(3)]
            t1b = [sb(sc, "t1b%d" % k, [128, 530], F32) for k in range(3)]
            R_t1b = [Res("t1b%d" % k) for k in range(3)]
            t2b = [sb(sc, "t2b%d" % k, [128, 530], F32) for k in range(3)]
            R_t2b = [Res("t2b%d" % k) for k in range(3)]
            t0s = sb(sc, "t0s", [128, 16], F32)
            ges = sb(sc, "ges", [128, 16], F32)
            R_ts = Res("ts")
            carry = sb(sc, "carry", [128, NFC * 2], F32)
            R_carry = [Res("carry%d" % c) for c in range(NFC)]
            sfc = sb(sc, "sfc", [128, NFC * 2], F32)
            R_sfc = Res("sfc")
            sconv = sb(sc, "sconv", [128, NFC * 2], F32)
            wconv = sb(sc, "wconv", [128, NFC * 3], F32)
            bconv = sb(sc, "bconv", [128, NFC], F32)
            flag = sb(sc, "flag", [128, 1], F32)
            R_cc = Res("cc")
            ln3 = sb(sc, "ln3", [128, 2 * 1024], F32)
            R_ln3 = Res("ln3")
            h2r = [sb(sc, "h2r%d" % k, [128, 1024], F32) for k in range(2)]
            R_h2r = [Res("h2r%d" % k) for k in range(2)]
            yA = sb(sc, "yA", [128, 1024], F32)
            R_yA = Res("yA")
            yB = [sb(sc, "yB%d" % k, [128, 1024], F32) for k in range(2)]
            R_yB = [Res("yB%d" % k) for k in range(2)]
            stat = sb(sc, "statc", [128, 32], F32)
            R_stat = Res("statc")

            P.dma("sp", sconv[:, :], I["sconvT"][:, :], writes=[R_cc])
            P.dma("sp", wconv[:, :], I["wconvT"][:, :], writes=[R_cc])
            P.dma("sp", bconv[:, :], I["bconvT"][:, :], writes=[R_cc])
            P.dma("sp", flag[:, :], I["flag"][:, :], writes=[R_cc])
            for k in range(2):
                P.dma("sp", ln3[:, k * 1024:(k + 1) * 1024], I["lnp"][4 + k:5 + k, :].to_broadcast([128, 1024]), writes=[R_ln3])
            def wdown_piece(j):
                kq = j % 2
                P.dma("sp", yB[kq][:, :], I["wdown"][:, j * 1024:(j + 1) * 1024], writes=[R_yB[kq]])
                P.op("dve", _call("tensor_copy", out=wdb[:, j * 1024:(j + 1) * 1024], in_=yB[kq][:, :]), reads=[R_yB[kq]], writes=[R_wd])

            P.dma("sp", h2Tm[:, :].rearrange("p (c q) -> p c q", q=18)[:, :, 0:2], h2TD[16][:, 0:16].rearrange("p (c q) -> p c q", q=2),
                  reads=[R_h2TD[16]], writes=[R_h2Tm], slow=True)
            P.dma("sp", h2Tm[:, :].rearrange("p (c q) -> p c q", q=18)[:, :, 2:18], h2TD[17][:, 0:128].rearrange("p (c q) -> p c q", q=16),
                  reads=[R_h2TD[17]], writes=[R_h2Tm], slow=True)

            checkpoint('phaseC_pre')
            UB = [0, 2, 4]
            GBK = [1, 3, 5]
            MB = 7
            YB = [6, 7]
            wk = [0]

            def ln3_out(pre_banks, qs, h2src, R_h2src, dst_ap, ys, R_ys):
                for n, bank in enumerate(pre_banks):
                    P.op("dve", _call("scalar_tensor_tensor",
                        out=yA[0:qs, n * 512:(n + 1) * 512], in0=h2src[0:qs, n * 512:(n + 1) * 512], scalar=ALPHA, in1=pb[bank][0:qs, :],
                        op0=ALU.mult, op1=ALU.add),
                        reads=[R_h2src, R_pb[bank]], writes=[R_yA])
                for c in range(2):
                    P.op("dve", _call("bn_stats", out=stat[0:qs, c * 6:(c + 1) * 6], in_=yA[0:qs, c * 512:(c + 1) * 512]),
                         reads=[R_yA], writes=[R_stat])
                P.op("dve", _call("bn_aggr", out=stat[0:qs, 12:14], in_=stat[0:qs, 0:12]), reads=[R_stat], writes=[R_stat])
                P.op("dve", _call("tensor_scalar", out=stat[0:qs, 14:15], in0=stat[0:qs, 13:14], scalar1=LN_EPS, scalar2=None, op0=ALU.add),
                     reads=[R_stat], writes=[R_stat])
                P.op("act", _call("activation", out=stat[0:qs, 15:16], in_=stat[0:qs, 14:15], func=AF.Sqrt), reads=[R_stat], writes=[R_stat])
                P.op("dve", _call("reciprocal", out=stat[0:qs, 16:17], in_=stat[0:qs, 15:16]), reads=[R_stat], writes=[R_stat])
                P.op("dve", _call("scalar_tensor_tensor", out=stat[0:qs, 17:18], in0=stat[0:qs, 12:13], scalar=-1.0, in1=stat[0:qs, 16:17],
                                                             op0=ALU.mult, op1=ALU.mult),
                     reads=[R_stat], writes=[R_stat])
                P.op("act", _call("activation", out=ys[0:qs, :], in_=yA[0:qs, :], func=AF.Identity, scale=stat[0:qs, 16:17], bias=stat[0:qs, 17:18]),
                     reads=[R_yA, R_stat], writes=[R_ys])
                P.op("pool", _call("tensor_tensor", out=ys[0:qs, :], in0=ys[0:qs, :], in1=ln3[0:qs, 0:1024], op=ALU.mult),
                     reads=[R_ys, R_ln3], writes=[R_ys])
                P.op("pool", _call("tensor_tensor", out=ys[0:qs, :], in0=ys[0:qs, :], in1=ln3[0:qs, 1024:2048], op=ALU.add),
                     reads=[R_ys, R_ln3], writes=[R_ys])
                P.dma("sp", dst_ap, ys[0:qs, :], reads=[R_ys], defer=True)

            def load_h2Tg(grp):
                gs = grp % 2
                for bi in range(4):
                    blk = grp * 4 + bi
                    P.dma("sp", h2Tg[gs][:, :].rearrange("p (c q) -> p c q", q=512)[:, :, bi * 128:(bi + 1) * 128],
                          h2TD[blk][:, :].rearrange("p (c q) -> p c q", q=128), reads=[R_h2TD[blk]], writes=[R_h2Tg[gs]])

            def c_s1(grp, c):
                s = (grp * NFC + c) % 2
                P.dma("sp", wst[s][:, :], I["wup"][c], writes=[R_wst[s]])
                P.op("dve", _call("tensor_copy", out=wsl[s][:, 0:1024], in_=wst[s][:, 0:1024]), reads=[R_wst[s]], writes=[R_wsl[s]])
                P.op("dve", _call("tensor_copy", out=wsl[s][:, 1024:2048], in_=wst[s][:, 1024:2048]), reads=[R_wst[s]], writes=[R_wslB[s]])

            def c_s2(grp, c):
                s = (grp * NFC + c) % 2
                gs = grp % 2
                mo = (c % 3) * 64
                if grp == 0:
                    for part, oc in ((0, mo), (1, mo + 32)):
                        for kc in range(KC):
                            P.op("pe", _call("matmul", out=pb[MB][:, oc:oc + 18], lhsT=wsl[s][:, kc * 256 + part * 128: kc * 256 + (part + 1) * 128],
                                             rhs=h2Tm[:, kc * 18:(kc + 1) * 18], start=(kc == 0), stop=(kc == KC - 1)),
                                 reads=[R_wsl[s], R_wslB[s], R_h2Tm], writes=[R_pb[MB]])
                k3 = (grp * NFC + c) % 3
                ub, gbk = UB[k3], GBK[k3]
                for part, bank in ((0, ub), (1, gbk)):
                    for kc in range(KC):
                        P.op("pe", _call("matmul", out=pb[bank][:, :], lhsT=wsl[s][:, kc * 256 + part * 128: kc * 256 + (part + 1) * 128],
                                         rhs=h2Tg[gs][:, kc * 512:(kc + 1) * 512], start=(kc == 0), stop=(kc == KC - 1)),
                             reads=[R_wsl[s], R_wslB[s], R_h2Tg[gs]], writes=[R_pb[bank]])

            def c_s3(grp, c):
                hTg, R_hTg = hT2[grp % 2], R_hT2[grp % 2]
                mo = (c % 3) * 64
                k3 = (grp * NFC + c) % 3
                ub, gbk = UB[k3], GBK[k3]
                G, R_G = Gb[k3], R_Gb[k3]
                t0, R_t = t0b[k3], R_t0[k3]
                ge, R_g = geb[k3], R_ge[k3]
                t1, R_t1 = t1b[k3], R_t1b[k3]
                t2, R_t2 = t2b[k3], R_t2b[k3]
                W = 530 if grp == 0 else 512
                if grp == 0:
                    P.op("dve", _call("tensor_scalar", out=carry[:, c * 2:(c + 1) * 2], in0=pb[MB][:, mo + 32:mo + 34], scalar1=flag[:, 0:1],
                                      scalar2=None, op0=ALU.mult),
                         reads=[R_pb[MB], R_cc], writes=[R_carry[c]])
                P.op("act", _call("activation", out=G[:, 0:2], in_=carry[:, c * 2:(c + 1) * 2], func=AF.Copy),
                     reads=[R_carry[c]], writes=[R_G])
                P.op("act", _call("activation", out=G[:, 2:514], in_=pb[gbk][:, :], func=AF.Copy), reads=[R_pb[gbk]], writes=[R_G])
                if grp == 0:
                    P.op("act", _call("activation", out=G[:, 514:516], in_=sconv[:, c * 2:(c + 1) * 2], func=AF.Copy), reads=[R_cc], writes=[R_G])
                    P.op("act", _call("activation", out=G[:, 516:532], in_=pb[MB][:, mo + 34:mo + 50], func=AF.Copy), reads=[R_pb[MB]], writes=[R_G])
                    P.op("act", _call("activation", out=sfc[:, c * 2:(c + 1) * 2], in_=G[:, 530:532], func=AF.Copy), reads=[R_G], writes=[R_sfc])
                P.op("act", _call("activation", out=carry[:, c * 2:(c + 1) * 2], in_=G[:, 512:514], func=AF.Copy),
                     reads=[R_G], writes=[R_carry[c]])
                P.op("act", _call("activation", out=t0[:, 0:W], in_=G[:, 2:2 + W], func=AF.Identity,
                                  scale=wconv[:, c * 3 + 2:c * 3 + 3], bias=bconv[:, c:c + 1]),
                     reads=[R_G, R_cc], writes=[R_t])
                P.op("act", _call("activation", out=t1[:, 0:W], in_=G[:, 1:1 + W], func=AF.Identity, scale=wconv[:, c * 3 + 1:c * 3 + 2]),
                     reads=[R_G, R_cc], writes=[R_t1])
                P.op("act", _call("activation", out=t2[:, 0:W], in_=G[:, 0:W], func=AF.Identity, scale=wconv[:, c * 3:c * 3 + 1]),
                     reads=[R_G, R_cc], writes=[R_t2])
                P.op("dve", _call("tensor_tensor", out=t0[:, 0:W], in0=t0[:, 0:W], in1=t1[:, 0:W], op=ALU.add), reads=[R_t, R_t1], writes=[R_t])
                P.op("dve", _call("tensor_tensor", out=t0[:, 0:W], in0=t0[:, 0:W], in1=t2[:, 0:W], op=ALU.add), reads=[R_t, R_t2], writes=[R_t])

            def c_s3b(grp, c):
                hTg, R_hTg = hT2[grp % 2], R_hT2[grp % 2]
                mo = (c % 3) * 64
                k3 = (grp * NFC + c) % 3
                ub = UB[k3]
                t0, R_t = t0b[k3], R_t0[k3]
                ge, R_g = geb[k3], R_ge[k3]
                W = 530 if grp == 0 else 512
                P.op("act", _call("activation", out=ge[:, 0:W], in_=t0[:, 0:W], func=AF.Gelu_apprx_tanh), reads=[R_t], writes=[R_g])
                P.op("dve", _call("tensor_tensor", out=hTg[:, c * 512:(c + 1) * 512], in0=pb[ub][:, :], in1=ge[:, 0:512], op=ALU.mult),
                     reads=[R_pb[ub], R_g], writes=[R_hTg])
                if grp == 0:
                    P.op("dve", _call("tensor_tensor", out=hTm[:, c * 16:(c + 1) * 16], in0=pb[MB][:, mo + 2:mo + 18], in1=ge[:, 514:530], op=ALU.mult),
                         reads=[R_pb[MB], R_g], writes=[R_hTm])

            def c_down(grp):
                hTg, R_hTg = hT2[grp % 2], R_hT2[grp % 2]
                if grp == 0:
                    for n, bank in enumerate(YB):
                        for c in range(NFC):
                            P.op("pe", _call("matmul", out=pb[bank][0:16, :], lhsT=hTm[:, c * 16:(c + 1) * 16],
                                             rhs=wdb[:, c * 1024 + n * 512: c * 1024 + (n + 1) * 512], start=(c == 0), stop=(c == NFC - 1)),
                                 reads=[R_hTm, R_wd], writes=[R_pb[bank]])
                    P.dma("sp", h2r[0][0:16, :], h2D[17 * 128: 17 * 128 + 16, :], reads=[R_h2D[17]], writes=[R_h2r[0]])
                    ln3_out(YB, 16, h2r[0], R_h2r[0], O["ys"][:, :], yB[0], R_yB[0])
                    P.dma("sp", O["sfcT"][:, :], sfc[:, :], reads=[R_sfc], defer=True)
                for bi in range(4):
                    blk = grp * 4 + bi
                    hs = blk % 2
                    P.dma("sp", h2r[hs][:, :], h2D[blk * 128:(blk + 1) * 128, :], reads=[R_h2D[blk]], writes=[R_h2r[hs]])
                    for n, bank in enumerate(YB):
                        for c in range(NFC):
                            P.op("pe", _call("matmul", out=pb[bank][:, :], lhsT=hTg[:, c * 512 + bi * 128: c * 512 + (bi + 1) * 128],
                                             rhs=wdb[:, c * 1024 + n * 512: c * 1024 + (n + 1) * 512], start=(c == 0), stop=(c == NFC - 1)),
                                 reads=[R_hTg, R_wd], writes=[R_pb[bank]])
                    ln3_out(YB, 128, h2r[hs], R_h2r[hs], O["y"][blk * 128:(blk + 1) * 128, :], yB[hs], R_yB[hs])

            seq = [(grp, c) for grp in range(4) for c in range(NFC)]
            nseq = len(seq)
            load_h2Tg(0)
            load_h2Tg(1)
            for idx in range(nseq + 3):
                if 1 <= idx <= NFC:
                    wdown_piece(idx - 1)
                if idx < nseq:
                    c_s1(*seq[idx])
                if 1 <= idx <= nseq:
                    c_s2(*seq[idx - 1])
                if 3 <= idx:
                    g4, c4 = seq[idx - 3]
                    c_s3b(g4, c4)
                    if c4 == NFC - 1:
                        c_down(g4)
                        if g4 + 2 < 4:
                            load_h2Tg(g4 + 2)
                if 2 <= idx <= nseq + 1:
                    c_s3(*seq[idx - 2])
            P.dma("sp", O["fcT"][:, :], carry[:, :], reads=R_carry, defer=True)
            P.finish()
            P.flush(block)
    return nc


def _t5_bucket(rel):
    half, max_exact = 16, 8
    n = np.abs(rel)
    log_ratio = np.log(np.maximum(n, 1).astype(np.float32) / max_exact) / math.log(128 / max_exact)
    large = np.minimum(max_exact + (log_ratio * (half - max_exact)).astype(np.int32), half - 1)
    return np.where(rel < 0, half, 0) + np.where(n < max_exact, n, large)


def _host_inputs(inp):
    f32 = np.float32
    x_prompt = np.asarray(inp["x_prompt"], f32)
    x_sample = np.asarray(inp["x_sample"], f32)
    w_in = np.asarray(inp["w_in"], f32)[0]
    qa, ka, va = w_in[:, 0:512], w_in[:, 512:1024], w_in[:, 1024:1536]
    qb, kb, vb = w_in[:, 1536:2048], w_in[:, 2048:2176], w_in[:, 2176:2304]
    qi, ki, wi = w_in[:, 2304:2816], w_in[:, 2816:2880], w_in[:, 2880:2888]
    qbp = np.concatenate([np.concatenate([qb[:, r * 64:(r + 1) * 64], qb[:, (4 + r) * 64:(5 + r) * 64]], axis=1) for r in range(4)], axis=1)
    winp = np.concatenate([qa, ka, qbp, kb, qi, ki, ki, va, vb, wi], axis=1)
    assert winp.shape[1] == NCOL

    def kc_layout(w):
        n = w.shape[1]
        return np.ascontiguousarray(w.reshape(8, 128, n).transpose(1, 0, 2).reshape(128, 8 * n))

    shared = {}
    shared["win"] = kc_layout(winp)
    shared["wo"] = kc_layout(np.asarray(inp["w_o"], f32)[0])
    shared["wmq"] = kc_layout(np.asarray(inp["w_mq"], f32)[0])
    shared["wmk"] = kc_layout(np.asarray(inp["w_mk"], f32)[0])
    shared["wmv"] = kc_layout(np.asarray(inp["w_mv"], f32)[0])
    wmo = np.asarray(inp["w_mo"], f32)[0]
    shared["wmo"] = np.ascontiguousarray(wmo.reshape(4, 128, 1024).transpose(1, 0, 2).reshape(128, 4096))
    w_up = np.asarray(inp["w_up"], f32)[0]
    wu = w_up[:, :DFF].reshape(8, 128, NFC, 128)
    wg = w_up[:, DFF:].reshape(8, 128, NFC, 128)
    wup = np.stack([wu, wg], axis=3)
    shared["wup"] = np.ascontiguousarray(wup.transpose(2, 1, 0, 3, 4).reshape(NFC, 128, 8 * 256))
    w_down = np.asarray(inp["w_down"], f32)[0]
    shared["wdown"] = np.ascontiguousarray(w_down.reshape(NFC, 128, 1024).transpose(1, 0, 2).reshape(128, NFC * 1024))
    shared["lnp"] = np.ascontiguousarray(np.stack([np.asarray(inp[k], f32)[0] for k in ("ln1_g", "ln1_b", "ln2_g", "ln2_b", "ln3_g", "ln3_b")]))
    w_conv = np.asarray(inp["w_conv"], f32)[0]
    shared["wconvT"] = np.ascontiguousarray(w_conv.reshape(3, NFC, 128).transpose(2, 1, 0).reshape(128, NFC * 3))
    shared["bconvT"] = np.ascontiguousarray(np.asarray(inp["b_conv"], f32)[0].reshape(NFC, 128).T)
    shared["ident"] = np.eye(128, dtype=f32)
    tabA = np.asarray(inp["a_rel_bias"], f32)[0]
    qq = np.arange(128)[:, None]
    kk = np.arange(640)[None, :]
    kpos = kk - 512
    rel = qq - kpos
    cq = qq // 64
    kch = np.floor_divide(kpos, 64)
    allowed = (kch >= cq - 8) & (kch <= cq)
    bias = tabA[np.clip(rel, -64, 64) + 64]
    AB = np.where(allowed[:, :, None], bias, f32(NEGM)).astype(f32)
    shared["AB"] = np.ascontiguousarray(AB.transpose(0, 2, 1).reshape(128, 8 * ABW))
    js = np.arange(16)[:, None]
    ks = np.arange(528)[None, :]
    ABs = tabA[np.clip(512 + js - ks, -64, 64) + 64]
    shared["ABs"] = np.ascontiguousarray(ABs.transpose(0, 2, 1).reshape(16, 8 * 528)).astype(f32)
    t5 = np.asarray(inp["t5_bias"], f32)
    relB = np.arange(128)[:, None] - np.arange(256)[None, :] + 128
    Bn = t5[_t5_bucket(relB)]
    shared["Bn"] = np.ascontiguousarray(Bn.transpose(0, 2, 1).reshape(128, 8 * BNW)).astype(f32)
    relBs = 128 + np.arange(16)[:, None] - np.arange(144)[None, :]
    Bns = t5[_t5_bucket(relBs)]
    shared["Bns"] = np.ascontiguousarray(Bns.transpose(0, 2, 1).reshape(16, 8 * 144)).astype(f32)
    shared["C15"] = np.ascontiguousarray(np.broadcast_to(t5[15][None, :], (128, 8))).astype(f32)
    dm = np.zeros((128, 128), f32)
    dm[0:64, 64:128] = NEGM
    shared["diagmask"] = dm

    mem_prompt = np.asarray(inp["mem_prompt"], f32)
    maps = []
    for c in range(8):
        b, half = c // 2, c % 2
        m = dict(shared)
        xk = np.zeros((4096, 1024), f32)
        if half == 1:
            xk[:] = x_prompt[b]
        else:
            xk[2048:] = x_prompt[b, :2048]
        m["xkT"] = np.ascontiguousarray(xk.reshape(32, 128, 8, 128).transpose(0, 3, 2, 1).reshape(32, 128, 1024))
        xs = x_sample[c]
        m["xsT"] = np.ascontiguousarray(xs.reshape(16, 8, 128).transpose(2, 1, 0).reshape(128, 128))
        xres = np.zeros((NBLK * 128, 1024), f32)
        xres[0:2048] = xk[2048:]
        xres[2048:2050] = xk[2046:2048]
        xres[17 * 128:17 * 128 + 16] = xs
        m["xres"] = xres
        m["memT"] = np.ascontiguousarray(mem_prompt[b].reshape(256, 8, 128).transpose(2, 1, 0).reshape(128, 2048))
        cmk = np.asarray(inp["cache_mem_k"], f32)[0, c]
        m["cmkT"] = np.ascontiguousarray(cmk.transpose(2, 1, 0).reshape(128, 1024))
        m["cmv"] = np.ascontiguousarray(np.asarray(inp["cache_mem_v"], f32)[0, c].reshape(256, 512))
        cak = np.asarray(inp["cache_a_k"], f32)[0, c]
        m["cakT"] = np.ascontiguousarray(cak.reshape(512, 4, 2, 64).transpose(2, 3, 1, 0).reshape(128, 2048))
        m["cav"] = np.ascontiguousarray(np.asarray(inp["cache_a_v"], f32)[0, c].reshape(512, 512))
        cbk = np.asarray(inp["cache_b_k"], f32)[0, c]
        m["cbkT"] = np.ascontiguousarray(cbk.reshape(2048, 128).T)
        m["cbv"] = np.ascontiguousarray(np.asarray(inp["cache_b_v"], f32)[0, c].reshape(2048, 128))
        cbi = np.asarray(inp["cache_b_kidx"], f32)[0, c]
        m["cbiT"] = np.ascontiguousarray(np.concatenate([cbi.T, cbi.T], axis=0))
        sc_ = np.asarray(inp["state_ffn_conv"], f32)[0, c]
        m["sconvT"] = np.ascontiguousarray(sc_.reshape(2, NFC, 128).transpose(2, 1, 0).reshape(128, NFC * 2))
        m["colmask"] = np.full((128, 1), NEGM if half == 0 else 0.0, f32)
        kv = np.ones((128, NT), f32)
        if half == 0:
            kv[:, 0:16] = 0.0
        m["kvalid"] = kv
        m["flag"] = np.full((128, 1), float(half), f32)
        maps.append(m)
    return maps


_NC_CACHE = {}


def _run(inputs, debug=False):
    key = bool(debug)
    if key not in _NC_CACHE:
        _NC_CACHE[key] = build_program(debug=debug)
    nc = _NC_CACHE[key]
    maps = _host_inputs(inputs)
    res = run_bass_kernel_spmd(nc, maps, core_ids=list(range(8)))
    return res.results


def kernel(**inputs):
    R = _run(inputs)
    f32 = np.float32
    y = np.zeros((4, 4096, 1024), f32)
    ys = np.zeros((8, 16, 1024), f32)
    pak = np.zeros((1, 4, 512, 8, 64), f32)
    pav = np.zeros((1, 4, 512, 8, 64), f32)
    pbk = np.zeros((1, 4, 4096, 2, 64), f32)
    pbv = np.zeros((1, 4, 4096, 2, 64), f32)
    pbi = np.zeros((1, 4, 4096, 64), f32)
    pmk = np.zeros((1, 4, 256, 4, 128), f32)
    pmv = np.zeros((1, 4, 256, 4, 128), f32)
    pfc = np.zeros((1, 4, 2, DFF), f32)
    sak = np.zeros((1, 8, 16, 8, 64), f32)
    sav = np.zeros((1, 8, 16, 8, 64), f32)
    sbk = np.zeros((1, 8, 16, 2, 64), f32)
    sbv = np.zeros((1, 8, 16, 2, 64), f32)
    sbi = np.zeros((1, 8, 16, 64), f32)
    sfc = np.zeros((1, 8, 2, DFF), f32)
    for c in range(8):
        b, half = c // 2, c % 2
        r = R[c]
        y[b, half * 2048:(half + 1) * 2048] = np.asarray(r["y"], f32)
        ys[c] = np.asarray(r["ys"], f32)
        if half == 1:
            akT = np.asarray(r["akT"], f32).reshape(2, 64, 4, 512)
            pak[0, b] = akT.transpose(3, 2, 0, 1).reshape(512, 8, 64)
            pav[0, b] = np.asarray(r["av"], f32).reshape(512, 8, 64)
            pbk[0, b] = np.asarray(r["bkT"], f32).T.reshape(4096, 2, 64)
            pbv[0, b] = np.asarray(r["bv"], f32).reshape(4096, 2, 64)
            pbi[0, b] = np.asarray(r["biT"], f32).T
            pmk[0, b] = np.asarray(r["mkT"], f32).reshape(128, 4, 256).transpose(2, 1, 0)
            pmv[0, b] = np.asarray(r["mv"], f32).reshape(256, 4, 128)
            pfc[0, b] = np.asarray(r["fcT"], f32).reshape(128, NFC, 2).transpose(2, 1, 0).reshape(2, DFF)
        sakT = np.asarray(r["sakT"], f32).reshape(2, 64, 4, 16)
        sak[0, c] = sakT.transpose(3, 2, 0, 1).reshape(16, 8, 64)
        sav[0, c] = np.asarray(r["sav"], f32).reshape(16, 8, 64)
        sbk[0, c] = np.asarray(r["sbkT"], f32).T.reshape(16, 2, 64)
        sbv[0, c] = np.asarray(r["sbv"], f32).reshape(16, 2, 64)
        sbi[0, c] = np.asarray(r["sbiT"], f32).T
        sfc[0, c] = np.asarray(r["sfcT"], f32).reshape(128, NFC, 2).transpose(2, 1, 0).reshape(2, DFF)
    return (y, ys, pak, pav, pbk, pbv, pbi, pmk, pmv, pfc, sak, sav, sbk, sbv, sbi, sfc)
```

```python
import math
from contextlib import ExitStack

import numpy as np
import concourse.bass as bass
import concourse.mybir as mybir
from concourse.bass_utils import run_bass_kernel_spmd

F32 = mybir.dt.float32
BF16 = mybir.dt.bfloat16
AF = mybir.ActivationFunctionType
ALU = mybir.AluOpType

D = 1024
KC = 8
NT = 32
NCOL = 2952
C_QA, C_KA, C_QB, C_KB, C_QI, C_KI, C_VA, C_VB, C_WI = 0, 512, 1024, 1536, 1664, 2176, 2304, 2816, 2944
DFF = 2816
NFC = 22
ALPHA = 2.0 ** 0.25
LN_EPS = 1e-5
NEGM = -30000.0
NIT = 16
BIS_W0 = 16.0
ABW = 640
BNW = 256
NBLK = 18


class Res:
    __slots__ = ("lw", "rd", "name", "excl")

    def __init__(self, name="", excl=False):
        self.lw = None
        self.rd = {}
        self.name = name
        self.excl = excl


def _call(name, *args, **kw):
    return lambda e: getattr(e, name)(*args, **kw)


class Prog:
    ENG = ("pe", "act", "dve", "pool", "sp")

    def __init__(self, nc, sems, dma_sems):
        self.nc = nc
        self.streams = {e: [] for e in self.ENG}
        self.sem = sems
        self.cnt = {e: 0 for e in self.ENG}
        self.seen = {e: {} for e in self.ENG}
        self.dsems = dma_sems
        self.dval = [0] * len(dma_sems)
        self.dnext = 0
        self.semh = dict(sems)
        for i, h in enumerate(dma_sems):
            self.semh[("d", i)] = h
        self.ninst = 0
        self.dead = False
        self.deferred = []
        self.defer_lag = 48

    def _deps(self, reads, writes, eng=None):
        d = {}
        for r in reads:
            if r.lw is not None:
                k, v = r.lw
                if d.get(k, 0) < v:
                    d[k] = v
            if r.excl:
                for k, v in r.rd.items():
                    if k != eng and d.get(k, 0) < v:
                        d[k] = v
        for w in writes:
            if w.lw is not None:
                k, v = w.lw
                if d.get(k, 0) < v:
                    d[k] = v
            for k, v in w.rd.items():
                if d.get(k, 0) < v:
                    d[k] = v
        return d

    def _wait(self, eng, deps):
        for k, v in deps.items():
            if k == "pe" and eng == "pe":
                continue
            if self.seen[eng].get(k, 0) >= v:
                continue
            self.seen[eng][k] = v
            h = self.semh[k]
            self.streams[eng].append(lambda e, h=h, v=v: e.wait_ge(h, v))

    def _flush_deferred(self, force=False, reads=(), writes=()):
        if not self.deferred:
            return
        conflict = force
        if not conflict:
            ws = set(id(w) for w in writes)
            rs = set(id(r) for r in reads)
            for d in self.deferred:
                dr = set(id(x) for x in d[3])
                dw = set(id(x) for x in d[4])
                if (ws & dr) or (ws & dw) or (rs & dw):
                    conflict = True
                    break
        if conflict:
            pend, self.deferred = self.deferred, []
            for d in pend:
                self._dma_now(d[0], d[1], d[2], d[3], d[4], d[5])
            return
        while self.deferred and self.ninst - self.deferred[0][6] >= self.defer_lag:
            d = self.deferred.pop(0)
            self._dma_now(d[0], d[1], d[2], d[3], d[4], d[5])

    def op(self, eng, fn, reads=(), writes=()):
        if self.dead:
            return
        self._flush_deferred(False, reads, writes)
        self._wait(eng, self._deps(reads, writes, eng))
        self.cnt[eng] += 1
        n = self.cnt[eng]
        h = self.sem[eng]
        self.streams[eng].append(lambda e, fn=fn, h=h: fn(e).then_inc(h, 1))
        self.ninst += 1
        for r in reads:
            if r.rd.get(eng, 0) < n:
                r.rd[eng] = n
        for w in writes:
            w.lw = (eng, n)
            w.rd = {}

    def dma(self, q, out, in_, reads=(), writes=(), slow=False, defer=False):
        if self.dead:
            return
        if defer:
            self._flush_deferred(False, reads, writes)
            self.deferred.append((q, out, in_, list(reads), list(writes), slow, self.ninst))
            return
        self._flush_deferred(False, reads, writes)
        self._dma_now(q, out, in_, reads, writes, slow)

    def _dma_now(self, q, out, in_, reads=(), writes=(), slow=False):
        deps = self._deps(reads, writes)
        i = self.dnext
        self.dnext = (i + 1) % len(self.dsems)
        k = ("d", i)
        if self.dval[i] > 0 and deps.get(k, 0) < self.dval[i]:
            deps[k] = self.dval[i]
        self._wait(q, deps)
        self.dval[i] += 16
        v = self.dval[i]
        h = self.dsems[i]
        if slow:
            self.streams[q].append(
                lambda e, out=out, in_=in_, h=h: e.dma_start(out=out, in_=in_, allow_slow_non_contiguous=True).then_inc(h, 16))
        else:
            self.streams[q].append(lambda e, out=out, in_=in_, h=h: e.dma_start(out=out, in_=in_).then_inc(h, 16))
        self.ninst += 1
        for r in reads:
            if r.rd.get(k, 0) < v:
                r.rd[k] = v
        for w in writes:
            w.lw = (k, v)
            w.rd = {}

    def barrier(self):
        if self.dead:
            return
        self._flush_deferred(True)
        deps = {e: self.cnt[e] for e in self.ENG if self.cnt[e] > 0}
        for i, v in enumerate(self.dval):
            if v > 0:
                deps[("d", i)] = v
        for e in self.ENG:
            self._wait(e, dict(deps))

    def finish(self):
        self._flush_deferred(True)
        deps = {("d", i): v for i, v in enumerate(self.dval) if v > 0}
        self._wait("sp", deps)

    def flush(self, block):
        self._flush_deferred(True)
        s = self.streams
        self.streams = {e: [] for e in self.ENG}

        def mk(lst):
            def body(e):
                for f in lst:
                    f(e)
            return body

        block.tensor(mk(s["pe"]))
        block.scalar(mk(s["act"]))
        block.vector(mk(s["dve"]))
        block.gpsimd(mk(s["pool"]))
        block.sync(mk(s["sp"]))


def build_program(debug=False, stop_at=None):
    nc = bass.Bass("TRN2", target_bir_lowering=False)

    def din(name, shape, dt=F32):
        return nc.dram_tensor(name, list(shape), dt, kind="ExternalInput").ap()

    def dout(name, shape, dt=F32):
        return nc.dram_tensor(name, list(shape), dt, kind="ExternalOutput").ap()

    def dscr(name, shape, dt):
        return nc.dram_tensor(name, list(shape), dt, kind="Internal").ap()

    I = {}
    I["xkT"] = din("xkT", [NT, 128, 1024])
    I["xsT"] = din("xsT", [128, 8 * 16])
    I["xres"] = din("xres", [NBLK * 128, 1024])
    I["win"] = din("win", [128, KC * NCOL])
    I["wo"] = din("wo", [128, 8 * 1024])
    I["wmq"] = din("wmq", [128, 8 * 512])
    I["wmk"] = din("wmk", [128, 8 * 512])
    I["wmv"] = din("wmv", [128, 8 * 512])
    I["wmo"] = din("wmo", [128, 4 * 1024])
    I["wup"] = din("wup", [NFC, 128, 8 * 256])
    I["wdown"] = din("wdown", [128, NFC * 1024])
    I["lnp"] = din("lnp", [6, 1024])
    I["wconvT"] = din("wconvT", [128, NFC * 3])
    I["bconvT"] = din("bconvT", [128, NFC])
    I["memT"] = din("memT", [128, 8 * 256])
    I["cmkT"] = din("cmkT", [128, 4 * 256])
    I["cmv"] = din("cmv", [256, 512])
    I["cakT"] = din("cakT", [128, 4 * 512])
    I["cav"] = din("cav", [512, 512])
    I["cbkT"] = din("cbkT", [128, 2048])
    I["cbv"] = din("cbv", [2048, 128])
    I["cbiT"] = din("cbiT", [128, 2048])
    I["sconvT"] = din("sconvT", [128, NFC * 2])
    I["ident"] = din("ident", [128, 128])
    I["AB"] = din("AB", [128, 8 * ABW])
    I["ABs"] = din("ABs", [16, 8 * 528])
    I["Bn"] = din("Bn", [128, 8 * BNW])
    I["Bns"] = din("Bns", [16, 8 * 144])
    I["C15"] = din("C15", [128, 8])
    I["colmask"] = din("colmask", [128, 1])
    I["diagmask"] = din("diagmask", [128, 128])
    I["kvalid"] = din("kvalid", [128, NT])
    I["flag"] = din("flag", [128, 1])

    O = {}
    O["y"] = dout("y", [2048, 1024])
    O["ys"] = dout("ys", [16, 1024])
    O["akT"] = dout("akT", [128, 4 * 512])
    O["av"] = dout("av", [512, 512])
    O["bkT"] = dout("bkT", [128, 4096])
    O["bv"] = dout("bv", [4096, 128])
    O["biT"] = dout("biT", [64, 4096])
    O["mkT"] = dout("mkT", [128, 4 * 256])
    O["mv"] = dout("mv", [256, 512])
    O["fcT"] = dout("fcT", [128, NFC * 2])
    O["sakT"] = dout("sakT", [128, 4 * 16])
    O["sav"] = dout("sav", [16, 512])
    O["sbkT"] = dout("sbkT", [128, 16])
    O["sbv"] = dout("sbv", [16, 128])
    O["sbiT"] = dout("sbiT", [64, 16])
    O["sfcT"] = dout("sfcT", [128, NFC * 2])
    if debug:
        O["dbg_mix"] = dout("dbg_mix", [NBLK * 128, 1024], BF16)
        O["dbg_h2"] = dout("dbg_h2", [NBLK * 128, 1024])
        mixD = O["dbg_mix"]
        h2D = O["dbg_h2"]
    else:
        mixD = dscr("mixD", [NBLK * 128, 1024], BF16)
        h2D = dscr("h2D", [NBLK * 128, 1024], F32)
    h2TD = dscr("h2TD", [NBLK, 128, 1024], BF16)
    R_mixD = [Res("mixD%d" % i) for i in range(NBLK)]
    R_h2D = [Res("h2D%d" % i) for i in range(NBLK)]
    R_h2TD = [Res("h2TD%d" % i) for i in range(NBLK)]

    es = ExitStack()
    with es:
        sems = {e: es.enter_context(nc.semaphore("s_" + e)) for e in Prog.ENG}
        dsems = [es.enter_context(nc.semaphore("d%d" % i)) for i in range(32)]
        P = Prog(nc, sems, dsems)
        block = es.enter_context(nc.Block())

        def checkpoint(name):
            if stop_at is not None and name == stop_at and not P.dead:
                P.finish()
                P.flush(block)
                P.dead = True

        pb = [es.enter_context(nc.psum_tensor("pb%d" % i, [128, 512], F32)) for i in range(8)]
        R_pb = [Res("pb%d" % i, excl=True) for i in range(8)]

        class Rot:
            def __init__(self, idxs):
                self.idxs = idxs
                self.i = 0

            def next(self):
                k = self.idxs[self.i % len(self.idxs)]
                self.i += 1
                return k

        def sb(stack, name, shape, dt):
            return stack.enter_context(nc.sbuf_tensor("sb_" + name, list(shape), dt))

        ident_f = sb(es, "ident_f", [128, 128], F32)
        ident = sb(es, "ident", [128, 512], BF16)
        R_ident = Res("ident")
        P.dma("sp", ident_f[:, :], I["ident"][:, :], writes=[R_ident])
        for r in range(4):
            P.op("act", _call("activation", out=ident[:, r * 128:(r + 1) * 128], in_=ident_f[:, :], func=AF.Copy),
                 reads=[R_ident], writes=[R_ident])

        def run_interleaved(gens):
            gens = [[0.0, i, g] for i, g in enumerate(gens)]
            while gens:
                gens.sort(key=lambda x: (x[0], x[1]))
                ent = gens[0]
                try:
                    c = next(ent[2])
                    ent[0] += (c if c else 1.0)
                except StopIteration:
                    gens.remove(ent)

        with ExitStack() as sa:
            winb = sb(sa, "winb", [128, KC * NCOL], BF16)
            R_win = Res("win")
            kbi = sb(sa, "kbi", [128, 2 * 4096], BF16)
            R_kbi = [Res("kbi%d" % r) for r in range(NT)]
            R_ki = [Res("ki%d" % r) for r in range(NT)]
            vb_aug = sb(sa, "vb_aug", [128, NT * 2 * 65], BF16)
            R_vb = [Res("vb%d" % r) for r in range(NT)]
            kaT = sb(sa, "kaT", [128, 6 * 512], BF16)
            R_ka = [Res("ka%d" % s) for s in range(6)]
            va_aug = sb(sa, "va_aug", [128, 6 * 8 * 65], BF16)
            R_va = [Res("va%d" % s) for s in range(6)]
            ABb = sb(sa, "ABb", [128, 8 * ABW], BF16)
            R_AB = Res("AB")
            Bnb = sb(sa, "Bnb", [128, 8 * BNW], BF16)
            R_Bn = Res("Bn")
            Mnear = [sb(sa, "Mnear%d" % k, [128, 8 * BNW], BF16) for k in range(2)]
            R_Mnear = [Res("Mnear%d" % k) for k in range(2)]
            score = [sb(sa, "score%d" % k, [128, 4096], F32) for k in range(2)]
            R_score = [Res("score%d" % k) for k in range(2)]
            Mb = [sb(sa, "Mb%d" % k, [128, 4096], BF16) for k in range(2)]
            R_M = [Res("M%d" % k) for k in range(2)]
            relu = [sb(sa, "relu%d" % k, [128, 512], BF16) for k in range(3)]
            R_relu = [Res("relu%d" % k) for k in range(3)]
            xstg2 = [sb(sa, "xstg%d" % k, [128, 1024], F32) for k in range(2)]
            R_xstg2 = [Res("xstg%d" % k) for k in range(2)]
            xstg, R_xstg = xstg2[0], R_xstg2[0]
            xTb = [sb(sa, "xTb%d" % k, [128, 1024], BF16) for k in range(2)]
            R_xT = [Res("xT%d" % k) for k in range(2)]
            qaz = [sb(sa, "qaz%d" % k, [128, 1024], BF16) for k in range(2)]
            qbz = [sb(sa, "qbz%d" % k, [128, 1024], BF16) for k in range(3)]
            qiz = [sb(sa, "qiz%d" % k, [128, 1024], BF16) for k in range(2)]
            R_qa = [Res("qa%d" % k) for k in range(2)]
            R_qb = [Res("qb%d" % k) for k in range(3)]
            R_qi = [Res("qi%d" % k) for k in range(2)]
            coef = [sb(sa, "coef%d" % k, [128, 8], F32) for k in range(2)]
            R_coef = [Res("coef%d" % k) for k in range(2)]
            dg = [sb(sa, "dg%d" % k, [128, 1024], BF16) for k in range(2)]
            R_dg = [Res("dg%d" % k) for k in range(2)]
            PTA = [sb(sa, "PTA%d" % k, [128, 512], BF16) for k in range(3)]
            R_PTA = [Res("PTA%d" % k) for k in range(3)]
            PTB = [sb(sa, "PTB%d" % k, [128, 512], BF16) for k in range(3)]
            R_PTB = [Res("PTB%d" % k) for k in range(3)]
            mixb = [sb(sa, "mixb%d" % k, [128, 1024], BF16) for k in range(3)]
            R_mix = [Res("mix%d" % k) for k in range(3)]
            ostg = [sb(sa, "ostg%d" % k, [128, 256], F32) for k in range(2)]
            R_ostg = [Res("ostg%d" % k) for k in range(2)]
            vbstg = [sb(sa, "vbstg%d" % k, [128, 128], F32) for k in range(2)]
            R_vbstg = [Res("vbstg%d" % k) for k in range(2)]
            astg = sb(sa, "astg", [128, 1024], F32)
            R_astg = Res("astg")
            small = [sb(sa, "small%d" % k, [128, 16], F32) for k in range(2)]
            R_small = [Res("small%d" % k) for k in range(2)]
            recA = [sb(sa, "recA%d" % k, [128, 8], F32) for k in range(2)]
            R_recA = [Res("recA%d" % k) for k in range(2)]
            recB = [sb(sa, "recB%d" % k, [128, 8], F32) for k in range(2)]
            R_recB = [Res("recB%d" % k) for k in range(2)]
            colmask = sb(sa, "colmask", [128, 1], F32)
            diagm = sb(sa, "diagm", [128, 128], F32)
            kvalid = sb(sa, "kvalid", [128, NT], F32)
            c15 = sb(sa, "c15", [128, 8], F32)
            ones8 = sb(sa, "ones8", [128, 8], F32)
            R_cst = Res("cst")

            wrot = Rot([0, 1, 2])

            P.dma("sp", colmask[:, :], I["colmask"][:, :], writes=[R_cst])
            P.dma("sp", diagm[:, :], I["diagmask"][:, :], writes=[R_cst])
            P.dma("sp", kvalid[:, :], I["kvalid"][:, :], writes=[R_cst])
            P.dma("sp", c15[:, :], I["C15"][:, :], writes=[R_cst])
            P.op("pool", _call("memset", ones8[:, :], 1.0), writes=[R_cst])
            for k in range(2):
                P.op("pool", _call("memset", qaz[k][:, :], 0.0), writes=[R_qa[k]])
                P.op("pool", _call("memset", qiz[k][:, :], 0.0), writes=[R_qi[k]])
            for k in range(3):
                P.op("pool", _call("memset", qbz[k][:, :], 0.0), writes=[R_qb[k]])

            HW = NCOL // 2
            R_slot = [Res("wslot%d" % q) for q in range(4)]
            for kc in range(KC):
                for hh in range(2):
                    q = (kc * 2 + hh) % 4
                    stg = score[q // 2][:, (q % 2) * HW:(q % 2 + 1) * HW]
                    P.dma("sp", stg, I["win"][:, kc * NCOL + hh * HW: kc * NCOL + (hh + 1) * HW], writes=[R_slot[q]])
                    if hh == 0:
                        P.op("act", _call("activation", out=winb[:, kc * NCOL + hh * HW: kc * NCOL + (hh + 1) * HW], in_=stg, func=AF.Copy),
                             reads=[R_slot[q]], writes=[R_win])
                    else:
                        P.op("dve", _call("tensor_copy", out=winb[:, kc * NCOL + hh * HW: kc * NCOL + (hh + 1) * HW], in_=stg),
                             reads=[R_slot[q]], writes=[R_win])
            for hh in range(2):
                w = 4 * ABW
                P.dma("sp", score[hh][:, 0:w], I["AB"][:, hh * w:(hh + 1) * w], writes=[R_score[hh], R_slot[2 * hh], R_slot[2 * hh + 1]])
                P.op("act", _call("activation", out=ABb[:, hh * w:(hh + 1) * w], in_=score[hh][:, 0:w], func=AF.Copy),
                     reads=[R_score[hh]], writes=[R_AB])
            P.dma("sp", score[0][:, 0:8 * BNW], I["Bn"][:, :], writes=[R_score[0]])
            for h in range(8):
                P.op("dve", _call("tensor_scalar", out=Bnb[:, h * BNW:(h + 1) * BNW], in0=score[0][:, h * BNW:(h + 1) * BNW],
                                  scalar1=c15[:, h:h + 1], scalar2=None, op0=ALU.subtract),
                     reads=[R_score[0], R_cst], writes=[R_Bn])
            checkpoint('consts')

            def win_cols(kc, c0, n):
                return winb[:, kc * NCOL + c0: kc * NCOL + c0 + n]

            def fm_proj(bank, xT, R_x, N, col0, nchunks, ocol=0):
                for j in range(nchunks):
                    for kc in range(KC):
                        P.op("pe", _call("matmul", out=pb[bank][:, ocol + j * N: ocol + (j + 1) * N], lhsT=win_cols(kc, col0 + j * 128, 128),
                                         rhs=xT[:, kc * N:(kc + 1) * N], start=(kc == 0), stop=(kc == KC - 1)),
                             reads=[R_win, R_x], writes=[R_pb[bank]])

            def tm_proj(bank, xT, R_x, N, col0, ncols, ocol=0):
                for kc in range(KC):
                    P.op("pe", _call("matmul", out=pb[bank][0:N, ocol:ocol + ncols], lhsT=xT[:, kc * N:(kc + 1) * N],
                                     rhs=win_cols(kc, col0, ncols), start=(kc == 0), stop=(kc == KC - 1)),
                         reads=[R_win, R_x], writes=[R_pb[bank]])

            def load_xT(r, eng="pool"):
                s = r % 2
                P.dma("sp", xstg2[s][:, :], I["xkT"][r], writes=[R_xstg2[s]])
                P.op(eng, _call("tensor_copy", out=xTb[s][:, :], in_=xstg2[s][:, :]), reads=[R_xstg2[s]], writes=[R_xT[s]])

            def kside(r, full):
                s = r % 2
                xT, R_x = xTb[s], R_xT[s]
                so = r % 2
                bk = wrot.next()
                fm_proj(bk, xT, R_x, 128, C_KB, 1)
                fm_proj(bk, xT, R_x, 128, C_KI, 1, ocol=128)
                P.op("act", _call("activation", out=ostg[so][:, :], in_=pb[bk][:, 0:256], func=AF.Copy), reads=[R_pb[bk]], writes=[R_ostg[so]])
                P.op("pool", _call("tensor_copy", out=kbi[:, :].rearrange("p (a c) -> p a c", a=2)[:, :, r * 128:(r + 1) * 128],
                                   in_=ostg[so][:, :].rearrange("p (a c) -> p a c", a=2)),
                     reads=[R_ostg[so]], writes=[R_kbi[r], R_ki[r]])
                P.dma("sp", O["bkT"][:, r * 128:(r + 1) * 128], ostg[so][:, 0:128], reads=[R_ostg[so]], defer=True)
                P.dma("sp", O["biT"][:, r * 128:(r + 1) * 128], ostg[so][0:64, 128:256], reads=[R_ostg[so]], defer=True)
                yield 3.0
                bv_ = wrot.next()
                tm_proj(bv_, xT, R_x, 128, C_VB, 128)
                vbv = vb_aug[:, r * 130:(r + 1) * 130].rearrange("p (g d) -> p g d", d=65)
                P.op("act", _call("activation", out=vbstg[so][:, :], in_=pb[bv_][:, 0:128], func=AF.Copy), reads=[R_pb[bv_]], writes=[R_vbstg[so]])
                P.op("pool", _call("tensor_copy", out=vbv[:, :, 0:64], in_=vbstg[so][:, :].rearrange("p (g d) -> p g d", d=64)),
                     reads=[R_vbstg[so]], writes=[R_vb[r]])
                P.op("pool", _call("tensor_scalar", out=vbv[:, :, 64:65], in0=ones8[:, 0:2].rearrange("p (g o) -> p g o", o=1),
                                   scalar1=kvalid[:, r:r + 1], scalar2=None, op0=ALU.mult),
                     reads=[R_cst], writes=[R_vb[r]])
                P.dma("sp", O["bv"][r * 128:(r + 1) * 128, :], vbstg[so][:, :], reads=[R_vbstg[so]], defer=True)
                yield 3.0
                if not full:
                    return
                slot = r % 6
                ba = wrot.next()
                fm_proj(ba, xT, R_x, 128, C_KA, 4)
                P.op("act", _call("activation", out=kaT[:, slot * 512:(slot + 1) * 512], in_=pb[ba][:, :], func=AF.Copy),
                     reads=[R_pb[ba]], writes=[R_ka[slot]])
                if r >= 28:
                    P.op("dve", _call("tensor_copy", out=astg[:, 0:512], in_=pb[ba][:, :]), reads=[R_pb[ba]], writes=[R_astg])
                    P.dma("sp", O["akT"].rearrange("p (j t) -> p j t", t=512)[:, :, (r - 28) * 128:(r - 27) * 128],
                          astg[:, 0:512].rearrange("p (j t) -> p j t", t=128), reads=[R_astg], defer=True)
                yield 3.0
                bva = wrot.next()
                tm_proj(bva, xT, R_x, 128, C_VA, 512)
                vav = va_aug[:, slot * 520:(slot + 1) * 520].rearrange("p (h d) -> p h d", d=65)
                P.op("act", _call("activation", out=vav[:, :, 0:64], in_=pb[bva][:, :].rearrange("p (h d) -> p h d", d=64), func=AF.Copy),
                     reads=[R_pb[bva]], writes=[R_va[slot]])
                P.op("pool", _call("tensor_scalar", out=vav[:, :, 64:65], in0=ones8[:, :].rearrange("p (h o) -> p h o", o=1),
                                   scalar1=kvalid[:, r:r + 1], scalar2=None, op0=ALU.mult),
                     reads=[R_cst], writes=[R_va[slot]])
                if r >= 28:
                    P.op("dve", _call("tensor_copy", out=astg[:, 512:1024], in_=pb[bva][:, :]), reads=[R_pb[bva]], writes=[R_astg])
                    P.dma("sp", O["av"][(r - 28) * 128:(r - 27) * 128, :], astg[:, 512:1024], reads=[R_astg], defer=True)
                yield 3.0

            def qside(xT, R_x, qs, st, st3):
                b1 = wrot.next()
                fm_proj(b1, xT, R_x, qs, C_QA, 4)
                for hf in range(2):
                    P.op("act", _call("activation",
                                      out=qaz[st][hf * 64:(hf + 1) * 64, 0:8 * qs].rearrange("p (j two q) -> p j two q", two=2, q=qs)[:, :, hf, :],
                                      in_=pb[b1][hf * 64:(hf + 1) * 64, 0:4 * qs].rearrange("p (j q) -> p j q", q=qs), func=AF.Copy, scale=0.125),
                         reads=[R_pb[b1]], writes=[R_qa[st]])
                yield 3.0
                b2 = wrot.next()
                fm_proj(b2, xT, R_x, qs, C_QB, 4)
                for g in range(2):
                    P.op("act", _call("activation", out=qbz[st3][g * 64:(g + 1) * 64, g * 4 * qs:(g + 1) * 4 * qs],
                                      in_=pb[b2][g * 64:(g + 1) * 64, 0:4 * qs], func=AF.Copy, scale=0.125),
                         reads=[R_pb[b2]], writes=[R_qb[st3]])
                yield 3.0
                b3 = wrot.next()
                fm_proj(b3, xT, R_x, qs, C_QI, 4)
                for hf in range(2):
                    P.op("act", _call("activation",
                                      out=qiz[st][hf * 64:(hf + 1) * 64, 0:8 * qs].rearrange("p (j two q) -> p j two q", two=2, q=qs)[:, :, hf, :],
                                      in_=pb[b3][hf * 64:(hf + 1) * 64, 0:4 * qs].rearrange("p (j q) -> p j q", q=qs), func=AF.Copy),
                         reads=[R_pb[b3]], writes=[R_qi[st]])
                b4 = wrot.next()
                tm_proj(b4, xT, R_x, qs, C_WI, 8)
                P.op("dve", _call("tensor_scalar", out=coef[st][0:qs, :], in0=pb[b4][0:qs, 0:8], scalar1=float(8.0 ** -1.5), scalar2=None, op0=ALU.mult),
                     reads=[R_pb[b4]], writes=[R_coef[st]])
                for h in range(8):
                    P.op("pool", _call("tensor_scalar", out=dg[st][0:qs, h * 128: h * 128 + qs], in0=ident_f[0:qs, 0:qs],
                                       scalar1=coef[st][0:qs, h:h + 1], scalar2=None, op0=ALU.mult),
                         reads=[R_coef[st], R_ident], writes=[R_dg[st]])
                yield 3.0

            def normalize(bank, qs, mixt, R_m, col0, rec, R_rec):
                ov = pb[bank][0:qs, 0:260].rearrange("p (h d) -> p h d", d=65)
                P.op("dve", _call("tensor_scalar", out=rec[0:qs, 0:4].rearrange("p (h o) -> p h o", o=1), in0=ov[:, :, 64:65],
                                  scalar1=1e-30, scalar2=None, op0=ALU.max),
                     reads=[R_pb[bank]], writes=[R_rec])
                P.op("dve", _call("reciprocal", out=rec[0:qs, 0:4], in_=rec[0:qs, 0:4]), reads=[R_rec], writes=[R_rec])
                for hh in range(4):
                    P.op("dve", _call("tensor_scalar", out=mixt[0:qs, col0 + hh * 64: col0 + (hh + 1) * 64],
                                      in0=pb[bank][0:qs, hh * 65: hh * 65 + 64],
                                      scalar1=rec[0:qs, hh:hh + 1], scalar2=None, op0=ALU.mult),
                         reads=[R_pb[bank], R_rec], writes=[R_m])

            def pipe3(items, s1, s2, s3, D, cost=1.0):
                pend = []
                for it in items:
                    s1(it)
                    s2(it)
                    pend.append(it)
                    if len(pend) > D:
                        s3(pend.pop(0))
                    yield cost
                while pend:
                    s3(pend.pop(0))
                    yield cost

            pta_rot = Rot([0, 1, 2])
            relu_rot = Rot([0, 1, 2])
            ptb_rot = Rot([0, 1, 2])
            brot = Rot([3, 7])

            def front_attn(sn, qs, wins, btiles, prompt_masks, abw):
                st = sn % 2
                mixt, R_m = mixb[sn % 3], R_mix[sn % 3]
                nw = len(wins)

                units = []
                for h in range(8):
                    units.append({"h": h, "t0": 0, "tiles": wins[0:4]})
                    if nw > 4:
                        units.append({"h": h, "t0": 4, "tiles": wins[4:5]})

                def a1(u):
                    h = u["h"]
                    j = h // 2
                    bank = wrot.next()
                    u["bank"] = bank
                    for i, (slot, ts) in enumerate(u["tiles"]):
                        t = u["t0"] + i
                        c0 = i * qs
                        P.op("pe", _call("matmul", out=pb[bank][0:ts, c0:c0 + qs], lhsT=kaT[:, slot * 512 + j * 128: slot * 512 + j * 128 + ts],
                                         rhs=qaz[st][:, h * qs:(h + 1) * qs], start=True, stop=False),
                             reads=[R_ka[slot], R_qa[st]], writes=[R_pb[bank]])
                        P.op("pe", _call("matmul", out=pb[bank][0:ts, c0:c0 + qs], lhsT=ABb[0:qs, h * abw + t * 128: h * abw + t * 128 + ts],
                                         rhs=ident[0:qs, 0:qs], start=False, stop=True),
                             reads=[R_AB, R_ident], writes=[R_pb[bank]])

                def a2(u):
                    k = pta_rot.next()
                    u["pt"], u["R_pt"] = PTA[k], R_PTA[k]
                    bank = u["bank"]
                    tsm = max(ts for (_, ts) in u["tiles"])
                    n = len(u["tiles"])
                    P.op("act", _call("activation", out=u["pt"][0:tsm, 0:n * qs], in_=pb[bank][0:tsm, 0:n * qs], func=AF.Exp),
                         reads=[R_pb[bank]], writes=[u["R_pt"]])

                def a3(u):
                    h = u["h"]
                    last_unit = (u["t0"] + len(u["tiles"]) == nw)
                    for i, (slot, ts) in enumerate(u["tiles"]):
                        t = u["t0"] + i
                        P.op("pe", _call("matmul", out=pb[4][0:qs, (h % 4) * 65:(h % 4) * 65 + 65], lhsT=u["pt"][0:ts, i * qs:(i + 1) * qs],
                                         rhs=va_aug[0:ts, slot * 520 + h * 65: slot * 520 + h * 65 + 65],
                                         start=(h % 4 == 0 and t == 0), stop=(t == nw - 1), skip_group_check=True),
                             reads=[u["R_pt"], R_va[slot]], writes=[R_pb[4]])
                    if last_unit and h % 4 == 3:
                        normalize(4, qs, mixt, R_m, (h // 4) * 256, recA[st], R_recA[st])

                yield from pipe3(units, a1, a2, a3, 2, 0.9)

                L = btiles[-1][1] + btiles[-1][2]
                items = []
                cc = 0
                for c0 in range(0, L, 512):
                    w = min(512, L - c0)
                    rk = [R_ki[tt[0]] for tt in btiles if tt[1] >= c0 - 127 and tt[1] < c0 + w]
                    for h in range(8):
                        items.append({"c0": c0, "w": w, "h": h, "sc": (5, 4)[cc % 2], "rk": rk})
                    cc += 1

                def i1(it):
                    bank = wrot.next()
                    it["bank"] = bank
                    h, c0, w = it["h"], it["c0"], it["w"]
                    P.op("pe", _call("matmul", out=pb[bank][0:qs, 0:w], lhsT=qiz[st][:, h * qs:(h + 1) * qs],
                                     rhs=kbi[:, 4096 + c0: 4096 + c0 + w], start=True, stop=True),
                         reads=[R_qi[st]] + it["rk"], writes=[R_pb[bank]])

                def i2(it):
                    k = relu_rot.next()
                    it["rl"], it["R_rl"] = relu[k], R_relu[k]
                    w = it["w"]
                    P.op("act", _call("activation", out=it["rl"][0:qs, 0:w], in_=pb[it["bank"]][0:qs, 0:w], func=AF.Relu),
                         reads=[R_pb[it["bank"]]], writes=[it["R_rl"]])

                def i3(it):
                    h, c0, w, sc = it["h"], it["c0"], it["w"], it["sc"]
                    P.op("pe", _call("matmul", out=pb[sc][0:qs, 0:w], lhsT=dg[st][0:qs, h * 128: h * 128 + qs], rhs=it["rl"][0:qs, 0:w],
                                     start=(h == 0), stop=(h == 7)),
                         reads=[R_dg[st], it["R_rl"]], writes=[R_pb[sc]])
                    if h == 7:
                        if prompt_masks and c0 < 2048:
                            wm = min(w, 2048 - c0)
                            P.op("act", _call("activation", out=score[st][0:qs, c0:c0 + wm], in_=pb[sc][0:qs, 0:wm], func=AF.Identity,
                                              bias=colmask[0:qs, 0:1]),
                                 reads=[R_pb[sc], R_cst], writes=[R_score[st]])
                            if wm < w:
                                P.op("act", _call("activation", out=score[st][0:qs, c0 + wm:c0 + w], in_=pb[sc][0:qs, wm:w], func=AF.Copy),
                                     reads=[R_pb[sc]], writes=[R_score[st]])
                        else:
                            P.op("act", _call("activation", out=score[st][0:qs, c0:c0 + w], in_=pb[sc][0:qs, 0:w], func=AF.Copy),
                                 reads=[R_pb[sc]], writes=[R_score[st]])

                yield from pipe3(items, i1, i2, i3, 2, 0.65)
                if prompt_masks:
                    P.op("dve", _call("tensor_tensor", out=score[st][0:qs, L - 128:L], in0=score[st][0:qs, L - 128:L], in1=diagm[0:qs, :], op=ALU.add),
                         reads=[R_score[st], R_cst], writes=[R_score[st]])
                yield

            def bis_gen(sn, qs, btiles, bnw):
                st = sn % 2
                sm, R_sm = small[st], R_small[st]
                L = btiles[-1][1] + btiles[-1][2]
                P.op("dve", _call("memset", sm[0:qs, 1:2], 0.0), writes=[R_sm])
                for k in range(NIT):
                    wk = BIS_W0 / (2.0 ** k)
                    P.op("dve", _call("tensor_scalar", out=Mb[st][0:qs, 0:L], in0=score[st][0:qs, 0:L], scalar1=sm[0:qs, 1:2], scalar2=None,
                                      op0=ALU.is_ge, op1=ALU.add, accum_out=sm[0:qs, 0:1]),
                         reads=[R_score[st], R_sm], writes=[R_M[st], R_sm])
                    P.op("dve", _call("tensor_scalar", out=sm[0:qs, 2:3], in0=sm[0:qs, 0:1], scalar1=255.5, scalar2=wk,
                                      op0=ALU.is_ge, op1=ALU.mult),
                         reads=[R_sm], writes=[R_sm])
                    P.op("dve", _call("scalar_tensor_tensor", out=sm[0:qs, 1:2], in0=sm[0:qs, 2:3], scalar=-wk / 2.0,
                                      in1=sm[0:qs, 1:2], op0=ALU.add, op1=ALU.add),
                         reads=[R_sm], writes=[R_sm])
                    yield L * 1.08e-3 + 0.5
                wl = BIS_W0 / (2.0 ** (NIT - 1)) / 2.0
                P.op("dve", _call("tensor_scalar", out=sm[0:qs, 3:4], in0=sm[0:qs, 1:2], scalar1=-wl, scalar2=None, op0=ALU.add),
                     reads=[R_sm], writes=[R_sm])
                P.op("dve", _call("tensor_scalar", out=Mb[st][0:qs, 0:L], in0=score[st][0:qs, 0:L], scalar1=sm[0:qs, 3:4], scalar2=NEGM,
                                  op0=ALU.is_lt, op1=ALU.mult),
                     reads=[R_score[st], R_sm], writes=[R_M[st]])
                nearw = btiles[-2][2] + btiles[-1][2]
                for h in range(8):
                    P.op("dve", _call("tensor_tensor", out=Mnear[st][0:qs, h * bnw: h * bnw + nearw], in0=Bnb[0:qs, h * bnw: h * bnw + nearw],
                                      in1=Mb[st][0:qs, L - nearw:L], op=ALU.add),
                         reads=[R_Bn, R_M[st]], writes=[R_Mnear[st]])
                yield
            def battn_gen(sn, qs, btiles, blk, bnw):
                st = sn % 2
                st3 = sn % 3
                mixt, R_m = mixb[st3], R_mix[st3]
                nb = len(btiles)
                items = [{"g": g, "t": t, "vt": vt, "c0": c0, "ts": ts} for g in range(2) for t, (vt, c0, ts) in enumerate(btiles)]

                def b1(it):
                    g, t, vt, c0, ts = it["g"], it["t"], it["vt"], it["c0"], it["ts"]
                    bank = brot.next()
                    it["bank"] = bank
                    P.op("pe", _call("matmul", out=pb[bank][0:ts, 0:4 * qs], lhsT=kbi[:, c0:c0 + ts],
                                     rhs=qbz[st3][:, g * 4 * qs:(g + 1) * 4 * qs], start=True, stop=False),
                         reads=[R_kbi[vt], R_qb[st3]], writes=[R_pb[bank]])
                    if t < nb - 2 and qs == 128:
                        P.op("pe", _call("matmul", out=pb[bank][0:ts, 0:512], lhsT=Mb[st][0:qs, c0:c0 + ts], rhs=ident[0:128, 0:512],
                                         start=False, stop=True),
                             reads=[R_M[st], R_ident], writes=[R_pb[bank]])
                    elif t < nb - 2:
                        for r in range(4):
                            P.op("pe", _call("matmul", out=pb[bank][0:ts, r * qs:(r + 1) * qs], lhsT=Mb[st][0:qs, c0:c0 + ts],
                                             rhs=ident[0:qs, 0:qs], start=False, stop=(r == 3)),
                                 reads=[R_M[st], R_ident], writes=[R_pb[bank]])
                    else:
                        tt = t - (nb - 2)
                        for r in range(4):
                            hh = g * 4 + r
                            P.op("pe", _call("matmul", out=pb[bank][0:ts, r * qs:(r + 1) * qs],
                                             lhsT=Mnear[st][0:qs, hh * bnw + tt * 128: hh * bnw + tt * 128 + ts], rhs=ident[0:qs, 0:qs],
                                             start=False, stop=(r == 3)),
                                 reads=[R_Mnear[st], R_ident], writes=[R_pb[bank]])

                def b2(it):
                    k = ptb_rot.next()
                    it["ptb"], it["R_ptb"] = PTB[k], R_PTB[k]
                    ts = it["ts"]
                    P.op("act", _call("activation", out=it["ptb"][0:ts, 0:4 * qs], in_=pb[it["bank"]][0:ts, 0:4 * qs], func=AF.Exp),
                         reads=[R_pb[it["bank"]]], writes=[it["R_ptb"]])

                def b3(it):
                    g, t, vt, ts = it["g"], it["t"], it["vt"], it["ts"]
                    for r in range(4):
                        P.op("pe", _call("matmul", out=pb[6][0:qs, r * 65: r * 65 + 65], lhsT=it["ptb"][0:ts, r * qs:(r + 1) * qs],
                                         rhs=vb_aug[0:ts, (vt * 2 + g) * 65:(vt * 2 + g) * 65 + 65],
                                         start=(t == 0 and r == 0), stop=(t == nb - 1), skip_group_check=True),
                             reads=[it["R_ptb"], R_vb[vt]], writes=[R_pb[6]])
                    if t == nb - 1:
                        normalize(6, qs, mixt, R_m, 512 + g * 256, recB[st], R_recB[st])

                yield from pipe3(items, b1, b2, b3, 1, 0.8)
                P.dma("sp", mixD[blk * 128: blk * 128 + qs, :], mixt[0:qs, :], reads=[R_m], writes=[R_mixD[blk]], defer=True)
                yield

            load_xT(0, "dve")
            for r in range(16):
                if r + 1 < 16:
                    load_xT(r + 1, "dve")
                for _ in kside(r, full=(r >= 11)):
                    pass
            checkpoint('phase0')

            def prompt_front(sn, T):
                if T + 1 <= 31:
                    load_xT(T + 1)
                if T >= 16:
                    yield from kside(T, full=True)
                s = T % 2
                yield from qside(xTb[s], R_xT[s], 128, sn % 2, sn % 3)
                wins = [((T - 4 + t) % 6, 128) for t in range(5)]
                btiles = [(t, t * 128, 128) for t in range(T + 1)]
                yield from front_attn(sn, 128, wins, btiles, True, ABW)

            def prompt_bis(sn, T):
                btiles = [(t, t * 128, 128) for t in range(T + 1)]
                yield from bis_gen(sn, 128, btiles, BNW)

            def prompt_battn(sn, T, blk):
                btiles = [(t, t * 128, 128) for t in range(T + 1)]
                yield from battn_gen(sn, 128, btiles, blk, BNW)

            steps = [(0, 15, 16)] + [(1 + i, 16 + i, i) for i in range(16)]
            ns = len(steps)
            SN = ns
            sst = SN % 2
            s_wins = [(0, 128), (1, 128), (2, 128), (3, 128), (4, 16)]
            s_btiles = [(t, t * 128, 128) for t in range(16)] + [(16, 2048, 16)]
            xs_, R_xs = xTb[0], R_xT[0]

            def sample_front():
                stg, R_stg = score[sst], R_score[sst]
                P.dma("sp", stg[:, 0:2048], I["cbiT"][:, :], writes=[R_stg])
                P.op("act", _call("activation", out=kbi[:, 4096:4096 + 2048], in_=stg[:, 0:2048], func=AF.Copy),
                     reads=[R_stg], writes=R_ki[0:16])
                P.dma("sp", stg[:, 2048:4096], I["cakT"][:, :], writes=[R_stg])
                for s4 in range(4):
                    P.op("act", _call("activation", out=kaT[:, s4 * 512:(s4 + 1) * 512].rearrange("p (j t) -> p j t", t=128),
                                      in_=stg[:, 2048:4096].rearrange("p (j t) -> p j t", t=512)[:, :, s4 * 128:(s4 + 1) * 128], func=AF.Copy),
                         reads=[R_stg], writes=[R_ka[s4]])
                yield 3.0
                P.dma("sp", stg[:, 0:2048].rearrange("p (t c) -> p t c", c=512), I["cav"].rearrange("(t p) c -> p t c", p=128), writes=[R_stg])
                vaall = va_aug[:, 0:4 * 520].rearrange("p (t d) -> p t d", d=65)
                P.op("act", _call("activation", out=vaall[:, :, 0:64], in_=stg[:, 0:2048].rearrange("p (t d) -> p t d", d=64), func=AF.Copy),
                     reads=[R_stg], writes=R_va[0:5])
                P.op("pool", _call("memset", va_aug[:, 0:5 * 520].rearrange("p (t d) -> p t d", d=65)[:, :, 64:65], 1.0), writes=R_va[0:5])
                for hh in range(2):
                    w = 4 * 528
                    P.dma("sp", stg[0:16, 0:w], I["ABs"][:, hh * w:(hh + 1) * w], writes=[R_stg])
                    P.op("act", _call("activation", out=ABb[0:16, hh * w:(hh + 1) * w], in_=stg[0:16, 0:w], func=AF.Copy),
                         reads=[R_stg], writes=[R_AB])
                P.op("pool", _call("memset", qaz[sst][:, :], 0.0), writes=[R_qa[sst]])
                P.op("pool", _call("memset", qbz[SN % 3][:, :], 0.0), writes=[R_qb[SN % 3]])
                P.op("pool", _call("memset", qiz[sst][:, :], 0.0), writes=[R_qi[sst]])
                P.dma("sp", xstg[:, 0:128], I["xsT"][:, :], writes=[R_xstg])
                P.op("pool", _call("tensor_copy", out=xTb[0][:, 0:128], in_=xstg[:, 0:128]), reads=[R_xstg], writes=[R_xT[0]])
                yield 3.0
                bk = wrot.next()
                fm_proj(bk, xs_, R_xs, 16, C_KI, 1)
                P.op("act", _call("activation", out=kbi[:, 4096 + 2048:4096 + 2064], in_=pb[bk][:, 0:16], func=AF.Copy), reads=[R_pb[bk]], writes=[R_ki[16]])
                P.op("dve", _call("tensor_copy", out=ostg[0][:, 16:32], in_=pb[bk][:, 0:16]), reads=[R_pb[bk]], writes=[R_ostg[0]])
                P.dma("sp", O["sbiT"][:, :], ostg[0][0:64, 16:32], reads=[R_ostg[0]], defer=True)
                ba = wrot.next()
                fm_proj(ba, xs_, R_xs, 16, C_KA, 4)
                P.op("act", _call("activation", out=kaT[:, 4 * 512:5 * 512].rearrange("p (j t) -> p j t", t=128)[:, :, 0:16],
                                  in_=pb[ba][:, 0:64].rearrange("p (j t) -> p j t", t=16), func=AF.Copy),
                     reads=[R_pb[ba]], writes=[R_ka[4]])
                P.op("dve", _call("tensor_copy", out=astg[:, 0:64], in_=pb[ba][:, 0:64]), reads=[R_pb[ba]], writes=[R_astg])
                P.dma("sp", O["sakT"][:, :], astg[:, 0:64], reads=[R_astg], defer=True)
                bva = wrot.next()
                tm_proj(bva, xs_, R_xs, 16, C_VA, 512)
                vav = va_aug[0:16, 4 * 520:5 * 520].rearrange("p (h d) -> p h d", d=65)
                P.op("act", _call("activation", out=vav[:, :, 0:64], in_=pb[bva][0:16, :].rearrange("p (h d) -> p h d", d=64), func=AF.Copy),
                     reads=[R_pb[bva]], writes=[R_va[4]])
                P.op("dve", _call("tensor_copy", out=astg[0:16, 512:1024], in_=pb[bva][0:16, :]), reads=[R_pb[bva]], writes=[R_astg])
                P.dma("sp", O["sav"][:, :], astg[0:16, 512:1024], reads=[R_astg], defer=True)
                yield 3.0
                yield from qside(xs_, R_xs, 16, sst, SN % 3)
                yield from front_attn(SN, 16, s_wins, s_btiles, False, 528)

            def sample_bis():
                stg, R_stg = score[1 - sst], R_score[1 - sst]
                P.dma("sp", stg[0:16, 0:8 * 144], I["Bns"][:, :], writes=[R_stg])
                for h in range(8):
                    P.op("dve", _call("tensor_scalar", out=Bnb[0:16, h * 144:(h + 1) * 144], in0=stg[0:16, h * 144:(h + 1) * 144],
                                      scalar1=c15[0:16, h:h + 1], scalar2=None, op0=ALU.subtract),
                         reads=[R_stg, R_cst], writes=[R_Bn])
                yield 1.0
                yield from bis_gen(SN, 16, s_btiles, 144)

            def sample_battn():
                stg, R_stg = score[1 - sst], R_score[1 - sst]
                P.dma("sp", stg[:, 0:2048], I["cbkT"][:, :], writes=[R_stg])
                P.op("act", _call("activation", out=kbi[:, 0:2048], in_=stg[:, 0:2048], func=AF.Copy),
                     reads=[R_stg], writes=R_kbi[0:16])
                P.dma("sp", stg[:, 2048:4096].rearrange("p (t c) -> p t c", c=128), I["cbv"].rearrange("(t p) c -> p t c", p=128), writes=[R_stg])
                vball = vb_aug[:, 0:16 * 130].rearrange("p (t d) -> p t d", d=65)
                P.op("act", _call("activation", out=vball[:, :, 0:64], in_=stg[:, 2048:4096].rearrange("p (t d) -> p t d", d=64), func=AF.Copy),
                     reads=[R_stg], writes=R_vb[0:17])
                P.op("pool", _call("memset", vb_aug[:, 0:17 * 130].rearrange("p (t d) -> p t d", d=65)[:, :, 64:65], 1.0), writes=R_vb[0:17])
                bk = wrot.next()
                fm_proj(bk, xs_, R_xs, 16, C_KB, 1)
                P.op("act", _call("activation", out=kbi[:, 2048:2064], in_=pb[bk][:, 0:16], func=AF.Copy), reads=[R_pb[bk]], writes=[R_kbi[16]])
                P.op("dve", _call("tensor_copy", out=ostg[1][:, 0:16], in_=pb[bk][:, 0:16]), reads=[R_pb[bk]], writes=[R_ostg[1]])
                P.dma("sp", O["sbkT"][:, :], ostg[1][:, 0:16], reads=[R_ostg[1]], defer=True)
                bv_ = wrot.next()
                tm_proj(bv_, xs_, R_xs, 16, C_VB, 128)
                vbv = vb_aug[0:16, 16 * 130:17 * 130].rearrange("p (g d) -> p g d", d=65)
                P.op("act", _call("activation", out=vbv[:, :, 0:64], in_=pb[bv_][0:16, 0:128].rearrange("p (g d) -> p g d", d=64), func=AF.Copy),
                     reads=[R_pb[bv_]], writes=[R_vb[16]])
                P.op("dve", _call("tensor_copy", out=vbstg[0][0:16, :], in_=pb[bv_][0:16, 0:128]), reads=[R_pb[bv_]], writes=[R_vbstg[0]])
                P.dma("sp", O["sbv"][:, :], vbstg[0][0:16, :], reads=[R_vbstg[0]], defer=True)
                yield 3.0
                yield from battn_gen(SN, 16, s_btiles, 17, 144)

            for tick in range(ns + 3):
                gens = []
                if 0 <= tick - 2 < ns:
                    gens.append(prompt_battn(*steps[tick - 2]))
                elif tick - 2 == ns:
                    gens.append(sample_battn())
                if 0 <= tick - 1 < ns:
                    gens.append(prompt_bis(*steps[tick - 1][0:2]))
                elif tick - 1 == ns:
                    gens.append(sample_bis())
                if tick < ns:
                    gens.append(prompt_front(*steps[tick][0:2]))
                elif tick == ns:
                    gens.append(sample_front())
                run_interleaved(gens)
            checkpoint('steps')
            checkpoint('phaseA')
            P.flush(block)

        P.barrier()
        with ExitStack() as sbk:
            wob = sb(sbk, "wob", [128, 8 * 1024], BF16)
            wmqb = sb(sbk, "wmqb", [128, 8 * 512], BF16)
            wmob = sb(sbk, "wmob", [128, 4 * 1024], BF16)
            wtmp = sb(sbk, "wtmp", [128, 8 * 512], BF16)
            R_wo, R_wmq, R_wmo, R_wtmp = Res("wo"), Res("wmq"), Res("wmo"), Res("wtmp")
            wst = [sb(sbk, "wst%d" % k, [128, 2048], F32) for k in range(2)]
            R_wst = [Res("wst%d" % k) for k in range(2)]
            lnt = sb(sbk, "lnt", [128, 4 * 1024], F32)
            R_ln = Res("ln")
            memTb = sb(sbk, "memTb", [128, 8 * 256], BF16)
            R_memT = Res("memT")
            mkT = [sb(sbk, "mkT%d" % k, [128, 4 * 256], BF16) for k in range(2)]
            mva = [sb(sbk, "mva%d" % k, [128, 2 * 4 * 129], BF16) for k in range(2)]
            R_mk = [Res("mk%d" % k) for k in range(2)]
            R_mv = [Res("mv%d" % k) for k in range(2)]
            mixl = [sb(sbk, "mixl%d" % k, [128, 1024], BF16) for k in range(4)]
            R_mixl = [Res("mixl%d" % k) for k in range(4)]
            xr = [sb(sbk, "xr%d" % k, [128, 1024], F32) for k in range(4)]
            R_xr = [Res("xr%d" % k) for k in range(4)]
            NB3 = 4
            tT_l = [sb(sbk, "tT%d" % k, [128, 1024], BF16) for k in range(NB3)]
            hA_l = [sb(sbk, "hA%d" % k, [128, 1024], F32) for k in range(NB3)]
            hB_l = [sb(sbk, "hB%d" % k, [128, 1024], F32) for k in range(NB3)]
            h16_l = [sb(sbk, "h16%d" % k, [128, 1024], BF16) for k in range(NB3)]
            qmT_l = [sb(sbk, "qmT%d" % k, [128, 512], BF16) for k in range(NB3)]
            PTm_l = [sb(sbk, "PTm%d" % k, [128, 1024], BF16) for k in range(NB3)]
            o16_l = [sb(sbk, "o16%d" % k, [128, 512], BF16) for k in range(NB3)]
            oT_l = [sb(sbk, "oT%d" % k, [128, 512], BF16) for k in range(NB3)]
            stat_l = [sb(sbk, "stat%d" % k, [128, 32], F32) for k in range(NB3)]
            RB = [{n: Res(n + str(k)) for n in ("tT", "hA", "hB", "h16", "qm", "PTm", "o16", "oT", "stat")} for k in range(NB3)]
            h2T = [sb(sbk, "h2T%d" % k, [128, 1024], BF16) for k in range(4)]
            R_h2T = [Res("h2T%d" % k) for k in range(4)]
            mstg = sb(sbk, "mstg", [128, 1024], F32)
            R_mstg = Res("mstg")
            wrot = Rot([0, 1, 2, 3, 4, 5, 6, 7])

            def load_cast(dst, R_dst, src, ncols, engs=("act", "dve")):
                k = 0
                for c0 in range(0, ncols, 2048):
                    w = min(2048, ncols - c0)
                    s = k % 2
                    P.dma("sp", wst[s][:, 0:w], src[:, c0:c0 + w], writes=[R_wst[s]])
                    eng = engs[k % len(engs)]
                    if eng == "act":
                        P.op("act", _call("activation", out=dst[:, c0:c0 + w], in_=wst[s][:, 0:w], func=AF.Copy),
                             reads=[R_wst[s]], writes=[R_dst])
                    else:
                        P.op(eng, _call("tensor_copy", out=dst[:, c0:c0 + w], in_=wst[s][:, 0:w]),
                             reads=[R_wst[s]], writes=[R_dst])
                    k += 1

            load_cast(wob, R_wo, I["wo"], 8192)
            load_cast(wmqb, R_wmq, I["wmq"], 4096)
            load_cast(wmob, R_wmo, I["wmo"], 4096)
            for k in range(4):
                P.dma("sp", lnt[:, k * 1024:(k + 1) * 1024], I["lnp"][k:k + 1, :].to_broadcast([128, 1024]), writes=[R_ln])
            load_cast(memTb, R_memT, I["memT"], 2048)
            load_cast(wtmp, R_wtmp, I["wmk"], 4096)
            for h in range(4):
                bank = wrot.next()
                for kc in range(KC):
                    P.op("pe", _call("matmul",
                        out=pb[bank][:, 0:256], lhsT=wtmp[:, kc * 512 + h * 128: kc * 512 + (h + 1) * 128],
                        rhs=memTb[:, kc * 256:(kc + 1) * 256], start=(kc == 0), stop=(kc == KC - 1)),
                        reads=[R_wtmp, R_memT], writes=[R_pb[bank]])
                P.op("act", _call("activation", out=mkT[0][:, h * 256:(h + 1) * 256], in_=pb[bank][:, 0:256], func=AF.Copy),
                     reads=[R_pb[bank]], writes=[R_mk[0]])
                P.op("dve", _call("tensor_copy", out=mstg[:, h * 256:(h + 1) * 256], in_=pb[bank][:, 0:256]),
                     reads=[R_pb[bank]], writes=[R_mstg])
            P.dma("sp", O["mkT"][:, :], mstg[:, :], reads=[R_mstg], defer=True)
            load_cast(wtmp, R_wtmp, I["wmv"], 4096)
            for mt in range(2):
                bank = wrot.next()
                for kc in range(KC):
                    P.op("pe", _call("matmul",
                        out=pb[bank][:, 0:512], lhsT=memTb[:, kc * 256 + mt * 128: kc * 256 + (mt + 1) * 128],
                        rhs=wtmp[:, kc * 512:(kc + 1) * 512], start=(kc == 0), stop=(kc == KC - 1)),
                        reads=[R_wtmp, R_memT], writes=[R_pb[bank]])
                mvv = mva[0][:, mt * 516:(mt + 1) * 516].rearrange("p (h d) -> p h d", d=129)
                P.op("act", _call("activation", out=mvv[:, :, 0:128], in_=pb[bank][:, :].rearrange("p (h d) -> p h d", d=128), func=AF.Copy),
                     reads=[R_pb[bank]], writes=[R_mv[0]])
                P.op("dve", _call("tensor_copy", out=mstg[:, mt * 512:(mt + 1) * 512], in_=pb[bank][:, :]),
                     reads=[R_pb[bank]], writes=[R_mstg])
                P.dma("sp", O["mv"][mt * 128:(mt + 1) * 128, :], mstg[:, mt * 512:(mt + 1) * 512], reads=[R_mstg], defer=True)
            for k in range(2):
                P.op("pool", _call("memset", mva[k][:, :].rearrange("p (t d) -> p t d", d=129)[:, :, 128:129], 1.0), writes=[R_mv[k]])
            load_cast(mkT[1], R_mk[1], I["cmkT"], 1024)
            P.dma("sp", wst[0][:, 0:1024].rearrange("p (t c) -> p t c", c=512), I["cmv"].rearrange("(t p) c -> p t c", p=128), writes=[R_wst[0]])
            P.op("act", _call("activation", out=mva[1][:, :].rearrange("p (t d) -> p t d", d=129)[:, :, 0:128],
                                               in_=wst[0][:, 0:1024].rearrange("p (t d) -> p t d", d=128), func=AF.Copy),
                 reads=[R_wst[0]], writes=[R_mv[1]])

            checkpoint('phaseB_pre')
            def transpose_to(src16, R_src, qs, nchunk, dst, R_dst):
                bank = wrot.next()
                pbf = pb[bank][:, :].bitcast(BF16)
                for c in range(nchunk):
                    P.op("pe", _call("transpose", out=pbf[:, c * qs:(c + 1) * qs], in_=src16[0:qs, c * 128:(c + 1) * 128],
                                                                   identity=ident[0:qs, 0:qs]),
                         reads=[R_src, R_ident], writes=[R_pb[bank]])
                P.op("act", _call("activation", out=dst[:, 0:nchunk * qs], in_=pbf[:, 0:nchunk * qs], func=AF.Copy),
                     reads=[R_pb[bank]], writes=[R_dst])

            def layer_norm(hin, R_hin, qs, gcol, hout, R_hout, stat, R_stat):
                for c in range(2):
                    P.op("dve", _call("bn_stats", out=stat[0:qs, c * 6:(c + 1) * 6], in_=hin[0:qs, c * 512:(c + 1) * 512]),
                         reads=[R_hin], writes=[R_stat])
                P.op("dve", _call("bn_aggr", out=stat[0:qs, 12:14], in_=stat[0:qs, 0:12]), reads=[R_stat], writes=[R_stat])
                P.op("dve", _call("tensor_scalar", out=stat[0:qs, 14:15], in0=stat[0:qs, 13:14], scalar1=LN_EPS, scalar2=None, op0=ALU.add),
                     reads=[R_stat], writes=[R_stat])
                P.op("act", _call("activation", out=stat[0:qs, 15:16], in_=stat[0:qs, 14:15], func=AF.Sqrt), reads=[R_stat], writes=[R_stat])
                P.op("dve", _call("reciprocal", out=stat[0:qs, 16:17], in_=stat[0:qs, 15:16]), reads=[R_stat], writes=[R_stat])
                P.op("dve", _call("tensor_scalar", out=hout[0:qs, :], in0=hin[0:qs, :], scalar1=stat[0:qs, 12:13], scalar2=stat[0:qs, 16:17],
                                  op0=ALU.subtract, op1=ALU.mult),
                     reads=[R_hin, R_stat], writes=[R_hout])
                P.op("dve", _call("tensor_tensor", out=hout[0:qs, :], in0=hout[0:qs, :], in1=lnt[0:qs, gcol * 1024:(gcol + 1) * 1024], op=ALU.mult),
                     reads=[R_hout, R_ln], writes=[R_hout])
                P.op("dve", _call("tensor_tensor", out=hout[0:qs, :], in0=hout[0:qs, :], in1=lnt[0:qs, (gcol + 1) * 1024:(gcol + 2) * 1024], op=ALU.add),
                     reads=[R_hout, R_ln], writes=[R_hout])

            def phaseB_block(blk, qs, row0, mi, k2):
                s = k2
                tT, hA, hB, h16, qmT, PTm, o16, oT, stat = (tT_l[k2], hA_l[k2], hB_l[k2], h16_l[k2], qmT_l[k2], PTm_l[k2], o16_l[k2],
                                                             oT_l[k2], stat_l[k2])
                R_tT, R_hA, R_hB, R_h16, R_qm, R_PTm, R_o16, R_oT, R_stat = (RB[k2][n] for n in ("tT", "hA", "hB", "h16", "qm", "PTm", "o16", "oT", "stat"))
                P.dma("sp", mixl[s][0:qs, :], mixD[blk * 128 + row0: blk * 128 + row0 + qs, :], reads=[R_mixD[blk]], writes=[R_mixl[s]])
                P.dma("sp", xr[s][0:qs, :], I["xres"][blk * 128: blk * 128 + qs, :], writes=[R_xr[s]])
                transpose_to(mixl[s], R_mixl[s], qs, 8, tT, R_tT)
                yield
                b0, b1 = wrot.next(), wrot.next()
                for n, bank in enumerate((b0, b1)):
                    for kc in range(KC):
                        P.op("pe", _call("matmul",
                            out=pb[bank][0:qs, :], lhsT=tT[:, kc * qs:(kc + 1) * qs], rhs=wob[:, kc * 1024 + n * 512: kc * 1024 + (n + 1) * 512],
                            start=(kc == 0), stop=(kc == KC - 1)),
                            reads=[R_tT, R_wo], writes=[R_pb[bank]])
                    P.op("dve", _call("scalar_tensor_tensor",
                        out=hA[0:qs, n * 512:(n + 1) * 512], in0=xr[s][0:qs, n * 512:(n + 1) * 512], scalar=ALPHA, in1=pb[bank][0:qs, :],
                        op0=ALU.mult, op1=ALU.add),
                        reads=[R_xr[s], R_pb[bank]], writes=[R_hA])
                yield
                layer_norm(hA, R_hA, qs, 0, hB, R_hB, stat, R_stat)
                yield
                P.op("act", _call("activation", out=h16[0:qs, :], in_=hB[0:qs, :], func=AF.Copy), reads=[R_hB], writes=[R_h16])
                transpose_to(h16, R_h16, qs, 8, tT, R_tT)
                yield
                bq = wrot.next()
                for h in range(4):
                    for kc in range(KC):
                        P.op("pe", _call("matmul",
                            out=pb[bq][:, h * qs:(h + 1) * qs], lhsT=wmqb[:, kc * 512 + h * 128: kc * 512 + (h + 1) * 128],
                            rhs=tT[:, kc * qs:(kc + 1) * qs], start=(kc == 0), stop=(kc == KC - 1)),
                            reads=[R_wmq, R_tT], writes=[R_pb[bq]])
                P.op("act", _call("activation", out=qmT[:, 0:4 * qs], in_=pb[bq][:, 0:4 * qs], func=AF.Copy, scale=float(128.0 ** -0.5)),
                     reads=[R_pb[bq]], writes=[R_qm])
                yield
                bs0, bs1 = wrot.next(), wrot.next()
                for h in range(4):
                    for mt in range(2):
                        idx = h * 2 + mt
                        bank = bs0 if idx < 4 else bs1
                        c0 = (idx % 4) * qs
                        P.op("pe", _call("matmul",
                            out=pb[bank][:, c0:c0 + qs], lhsT=mkT[mi][:, h * 256 + mt * 128: h * 256 + (mt + 1) * 128],
                            rhs=qmT[:, h * qs:(h + 1) * qs], start=True, stop=True),
                            reads=[R_mk[mi], R_qm], writes=[R_pb[bank]])
                for k, bank in enumerate((bs0, bs1)):
                    P.op("act", _call("activation", out=PTm[:, k * 4 * qs:(k + 1) * 4 * qs], in_=pb[bank][:, 0:4 * qs], func=AF.Exp),
                         reads=[R_pb[bank]], writes=[R_PTm])
                yield
                bo0, bo1 = wrot.next(), wrot.next()
                for h in range(4):
                    bank = bo0 if h < 2 else bo1
                    for mt in range(2):
                        idx = h * 2 + mt
                        P.op("pe", _call("matmul",
                            out=pb[bank][0:qs, (h % 2) * 129:(h % 2) * 129 + 129], lhsT=PTm[:, idx * qs:(idx + 1) * qs],
                            rhs=mva[mi][:, (mt * 4 + h) * 129:(mt * 4 + h) * 129 + 129],
                            start=(h % 2 == 0 and mt == 0), stop=(mt == 1), skip_group_check=True),
                            reads=[R_PTm, R_mv[mi]], writes=[R_pb[bank]])
                for k, bank in enumerate((bo0, bo1)):
                    ov = pb[bank][0:qs, 0:258].rearrange("p (h d) -> p h d", d=129)
                    P.op("dve", _call("tensor_scalar", out=stat[0:qs, 20 + 2 * k:22 + 2 * k].rearrange("p (h o) -> p h o", o=1),
                                                                      in0=ov[:, :, 128:129], scalar1=1e-30, scalar2=None, op0=ALU.max),
                         reads=[R_pb[bank]], writes=[R_stat])
                    P.op("dve", _call("reciprocal", out=stat[0:qs, 20 + 2 * k:22 + 2 * k], in_=stat[0:qs, 20 + 2 * k:22 + 2 * k]),
                         reads=[R_stat], writes=[R_stat])
                    for hh in range(2):
                        h = k * 2 + hh
                        P.op("dve", _call("tensor_scalar",
                            out=o16[0:qs, h * 128:(h + 1) * 128], in0=pb[bank][0:qs, hh * 129: hh * 129 + 128],
                            scalar1=stat[0:qs, 20 + 2 * k + hh:21 + 2 * k + hh], scalar2=None, op0=ALU.mult),
                            reads=[R_pb[bank], R_stat], writes=[R_o16])
                yield
                transpose_to(o16, R_o16, qs, 4, oT, R_oT)
                yield
                b0, b1 = wrot.next(), wrot.next()
                for n, bank in enumerate((b0, b1)):
                    for c in range(4):
                        P.op("pe", _call("matmul",
                            out=pb[bank][0:qs, :], lhsT=oT[:, c * qs:(c + 1) * qs], rhs=wmob[:, c * 1024 + n * 512: c * 1024 + (n + 1) * 512],
                            start=(c == 0), stop=(c == 3)),
                            reads=[R_oT, R_wmo], writes=[R_pb[bank]])
                    P.op("dve", _call("scalar_tensor_tensor",
                        out=hA[0:qs, n * 512:(n + 1) * 512], in0=hB[0:qs, n * 512:(n + 1) * 512], scalar=ALPHA, in1=pb[bank][0:qs, :],
                        op0=ALU.mult, op1=ALU.add),
                        reads=[R_hB, R_pb[bank]], writes=[R_hA])
                yield
                layer_norm(hA, R_hA, qs, 2, hB, R_hB, stat, R_stat)
                yield
                P.dma("sp", h2D[blk * 128: blk * 128 + qs, :], hB[0:qs, :], reads=[R_hB], writes=[R_h2D[blk]], defer=True)
                P.op("act", _call("activation", out=h16[0:qs, :], in_=hB[0:qs, :], func=AF.Copy), reads=[R_hB], writes=[R_h16])
                transpose_to(h16, R_h16, qs, 8, h2T[s], R_h2T[s])
                P.dma("sp", h2TD[blk][:, 0:8 * qs], h2T[s][:, 0:8 * qs], reads=[R_h2T[s]], writes=[R_h2TD[blk]], defer=True)
                yield

            def run_staggered(gens, lag):
                active = []
                pending = list(gens)
                tick = 0
                while active or pending:
                    if pending and (not active or tick >= lag):
                        active.append(pending.pop(0))
                        tick = 0
                    for g in list(active):
                        try:
                            next(g)
                        except StopIteration:
                            active.remove(g)
                    tick += 1

            blocks = [(16, 2, 126, 0), (17, 16, 0, 1)] + [(i, 128, 0, 0) for i in range(16)]
            run_staggered([phaseB_block(b_, q_, r_, m_, pos % 4) for pos, (b_, q_, r_, m_) in enumerate(blocks)], 3)
            checkpoint('phaseB')
            P.flush(block)

        P.barrier()
        with ExitStack() as sc:
            wdb = sb(sc, "wdb", [128, NFC * 1024], BF16)
            R_wd = Res("wd")
            wst = [sb(sc, "wstc%d" % k, [128, 2048], F32) for k in range(2)]
            R_wst = [Res("wstc%d" % k) for k in range(2)]
            wsl = [sb(sc, "wsl%d" % k, [128, 2048], BF16) for k in range(2)]
            R_wsl = [Res("wsl%d" % k) for k in range(2)]
            R_wslB = [Res("wslB%d" % k) for k in range(2)]
            hT2 = [sb(sc, "hT%d" % k, [128, NFC * 512], BF16) for k in range(2)]
            R_hT2 = [Res("hT%d" % k) for k in range(2)]
            hTm = sb(sc, "hTm", [128, NFC * 16], BF16)
            R_hTm = Res("hTm")
            h2Tg = [sb(sc, "h2Tg%d" % k, [128, 8 * 512], BF16) for k in range(2)]
            R_h2Tg = [Res("h2Tg%d" % k) for k in range(2)]
            h2Tm = sb(sc, "h2Tm", [128, 8 * 18], BF16)
            R_h2Tm = Res("h2Tm")
            Gb = [sb(sc, "Gb%d" % k, [128, 532], F32) for k in range(3)]
            R_Gb = [Res("Gb%d" % k) for k in range(3)]
            Gs = sb(sc, "Gs", [128, 18], F32)
            R_Gs = Res("Gs")
            t0b = [sb(sc, "t0b%d" % k, [128, 530], F32) for k in range(3)]
            R_t0 = [Res("t0%d" % k) for k in range(3)]
            geb = [sb(sc, "geb%d" % k, [128, 530], F32) for k in range(3)]
            R_ge = [Res("ge%d" % k) for k in range(3)]
            t1b = [sb(sc, "t1b%d" % k, [128, 530], F32) for k in range(3)]
            R_t1b = [Res("t1b%d" % k) for k in range(3)]
            t2b = [sb(sc, "t2b%d" % k, [128, 530], F32) for k in range(3)]
            R_t2b = [Res("t2b%d" % k) for k in range(3)]
            t0s = sb(sc, "t0s", [128, 16], F32)
            ges = sb(sc, "ges", [128, 16], F32)
            R_ts = Res("ts")
            carry = sb(sc, "carry", [128, NFC * 2], F32)
            R_carry = [Res("carry%d" % c) for c in range(NFC)]
            sfc = sb(sc, "sfc", [128, NFC * 2], F32)
            R_sfc = Res("sfc")
            sconv = sb(sc, "sconv", [128, NFC * 2], F32)
            wconv = sb(sc, "wconv", [128, NFC * 3], F32)
            bconv = sb(sc, "bconv", [128, NFC], F32)
            flag = sb(sc, "flag", [128, 1], F32)
            R_cc = Res("cc")
            ln3 = sb(sc, "ln3", [128, 2 * 1024], F32)
            R_ln3 = Res("ln3")
            h2r = [sb(sc, "h2r%d" % k, [128, 1024], F32) for k in range(2)]
            R_h2r = [Res("h2r%d" % k) for k in range(2)]
            yA = sb(sc, "yA", [128, 1024], F32)
            R_yA = Res("yA")
            yB = [sb(sc, "yB%d" % k, [128, 1024], F32) for k in range(2)]
            R_yB = [Res("yB%d" % k) for k in range(2)]
            stat = sb(sc, "statc", [128, 32], F32)
            R_stat = Res("statc")

            P.dma("sp", sconv[:, :], I["sconvT"][:, :], writes=[R_cc])
            P.dma("sp", wconv[:, :], I["wconvT"][:, :], writes=[R_cc])
            P.dma("sp", bconv[:, :], I["bconvT"][:, :], writes=[R_cc])
            P.dma("sp", flag[:, :], I["flag"][:, :], writes=[R_cc])
            for k in range(2):
                P.dma("sp", ln3[:, k * 1024:(k + 1) * 1024], I["lnp"][4 + k:5 + k, :].to_broadcast([128, 1024]), writes=[R_ln3])
            def wdown_piece(j):
                kq = j % 2
                P.dma("sp", yB[kq][:, :], I["wdown"][:, j * 1024:(j + 1) * 1024], writes=[R_yB[kq]])
                P.op("dve", _call("tensor_copy", out=wdb[:, j * 1024:(j + 1) * 1024], in_=yB[kq][:, :]), reads=[R_yB[kq]], writes=[R_wd])

            P.dma("sp", h2Tm[:, :].rearrange("p (c q) -> p c q", q=18)[:, :, 0:2], h2TD[16][:, 0:16].rearrange("p (c q) -> p c q", q=2),
                  reads=[R_h2TD[16]], writes=[R_h2Tm], slow=True)
            P.dma("sp", h2Tm[:, :].rearrange("p (c q) -> p c q", q=18)[:, :, 2:18], h2TD[17][:, 0:128].rearrange("p (c q) -> p c q", q=16),
                  reads=[R_h2TD[17]], writes=[R_h2Tm], slow=True)

            checkpoint('phaseC_pre')
            UB = [0, 2, 4]
            GBK = [1, 3, 5]
            MB = 7
            YB = [6, 7]
            wk = [0]

            def ln3_out(pre_banks, qs, h2src, R_h2src, dst_ap, ys, R_ys):
                for n, bank in enumerate(pre_banks):
                    P.op("dve", _call("scalar_tensor_tensor",
                        out=yA[0:qs, n * 512:(n + 1) * 512], in0=h2src[0:qs, n * 512:(n + 1) * 512], scalar=ALPHA, in1=pb[bank][0:qs, :],
                        op0=ALU.mult, op1=ALU.add),
                        reads=[R_h2src, R_pb[bank]], writes=[R_yA])
                for c in range(2):
                    P.op("dve", _call("bn_stats", out=stat[0:qs, c * 6:(c + 1) * 6], in_=yA[0:qs, c * 512:(c + 1) * 512]),
                         reads=[R_yA], writes=[R_stat])
                P.op("dve", _call("bn_aggr", out=stat[0:qs, 12:14], in_=stat[0:qs, 0:12]), reads=[R_stat], writes=[R_stat])
                P.op("dve", _call("tensor_scalar", out=stat[0:qs, 14:15], in0=stat[0:qs, 13:14], scalar1=LN_EPS, scalar2=None, op0=ALU.add),
                     reads=[R_stat], writes=[R_stat])
                P.op("act", _call("activation", out=stat[0:qs, 15:16], in_=stat[0:qs, 14:15], func=AF.Sqrt), reads=[R_stat], writes=[R_stat])
                P.op("dve", _call("reciprocal", out=stat[0:qs, 16:17], in_=stat[0:qs, 15:16]), reads=[R_stat], writes=[R_stat])
                P.op("dve", _call("scalar_tensor_tensor", out=stat[0:qs, 17:18], in0=stat[0:qs, 12:13], scalar=-1.0, in1=stat[0:qs, 16:17],
                                                             op0=ALU.mult, op1=ALU.mult),
                     reads=[R_stat], writes=[R_stat])
                P.op("act", _call("activation", out=ys[0:qs, :], in_=yA[0:qs, :], func=AF.Identity, scale=stat[0:qs, 16:17], bias=stat[0:qs, 17:18]),
                     reads=[R_yA, R_stat], writes=[R_ys])
                P.op("pool", _call("tensor_tensor", out=ys[0:qs, :], in0=ys[0:qs, :], in1=ln3[0:qs, 0:1024], op=ALU.mult),
                     reads=[R_ys, R_ln3], writes=[R_ys])
                P.op("pool", _call("tensor_tensor", out=ys[0:qs, :], in0=ys[0:qs, :], in1=ln3[0:qs, 1024:2048], op=ALU.add),
                     reads=[R_ys, R_ln3], writes=[R_ys])
                P.dma("sp", dst_ap, ys[0:qs, :], reads=[R_ys], defer=True)

            def load_h2Tg(grp):
                gs = grp % 2
                for bi in range(4):
                    blk = grp * 4 + bi
                    P.dma("sp", h2Tg[gs][:, :].rearrange("p (c q) -> p c q", q=512)[:, :, bi * 128:(bi + 1) * 128],
                          h2TD[blk][:, :].rearrange("p (c q) -> p c q", q=128), reads=[R_h2TD[blk]], writes=[R_h2Tg[gs]])

            def c_s1(grp, c):
                s = (grp * NFC + c) % 2
                P.dma("sp", wst[s][:, :], I["wup"][c], writes=[R_wst[s]])
                P.op("dve", _call("tensor_copy", out=wsl[s][:, 0:1024], in_=wst[s][:, 0:1024]), reads=[R_wst[s]], writes=[R_wsl[s]])
                P.op("dve", _call("tensor_copy", out=wsl[s][:, 1024:2048], in_=wst[s][:, 1024:2048]), reads=[R_wst[s]], writes=[R_wslB[s]])

            def c_s2(grp, c):
                s = (grp * NFC + c) % 2
                gs = grp % 2
                mo = (c % 3) * 64
                if grp == 0:
                    for part, oc in ((0, mo), (1, mo + 32)):
                        for kc in range(KC):
                            P.op("pe", _call("matmul", out=pb[MB][:, oc:oc + 18], lhsT=wsl[s][:, kc * 256 + part * 128: kc * 256 + (part + 1) * 128],
                                             rhs=h2Tm[:, kc * 18:(kc + 1) * 18], start=(kc == 0), stop=(kc == KC - 1)),
                                 reads=[R_wsl[s], R_wslB[s], R_h2Tm], writes=[R_pb[MB]])
                k3 = (grp * NFC + c) % 3
                ub, gbk = UB[k3], GBK[k3]
                for part, bank in ((0, ub), (1, gbk)):
                    for kc in range(KC):
                        P.op("pe", _call("matmul", out=pb[bank][:, :], lhsT=wsl[s][:, kc * 256 + part * 128: kc * 256 + (part + 1) * 128],
                                         rhs=h2Tg[gs][:, kc * 512:(kc + 1) * 512], start=(kc == 0), stop=(kc == KC - 1)),
                             reads=[R_wsl[s], R_wslB[s], R_h2Tg[gs]], writes=[R_pb[bank]])

            def c_s3(grp, c):
                hTg, R_hTg = hT2[grp % 2], R_hT2[grp % 2]
                mo = (c % 3) * 64
                k3 = (grp * NFC + c) % 3
                ub, gbk = UB[k3], GBK[k3]
                G, R_G = Gb[k3], R_Gb[k3]
                t0, R_t = t0b[k3], R_t0[k3]
                ge, R_g = geb[k3], R_ge[k3]
                t1, R_t1 = t1b[k3], R_t1b[k3]
                t2, R_t2 = t2b[k3], R_t2b[k3]
                W = 530 if grp == 0 else 512
                if grp == 0:
                    P.op("dve", _call("tensor_scalar", out=carry[:, c * 2:(c + 1) * 2], in0=pb[MB][:, mo + 32:mo + 34], scalar1=flag[:, 0:1],
                                      scalar2=None, op0=ALU.mult),
                         reads=[R_pb[MB], R_cc], writes=[R_carry[c]])
                P.op("act", _call("activation", out=G[:, 0:2], in_=carry[:, c * 2:(c + 1) * 2], func=AF.Copy),
                     reads=[R_carry[c]], writes=[R_G])
                P.op("act", _call("activation", out=G[:, 2:514], in_=pb[gbk][:, :], func=AF.Copy), reads=[R_pb[gbk]], writes=[R_G])
                if grp == 0:
                    P.op("act", _call("activation", out=G[:, 514:516], in_=sconv[:, c * 2:(c + 1) * 2], func=AF.Copy), reads=[R_cc], writes=[R_G])
                    P.op("act", _call("activation", out=G[:, 516:532], in_=pb[MB][:, mo + 34:mo + 50], func=AF.Copy), reads=[R_pb[MB]], writes=[R_G])
                    P.op("act", _call("activation", out=sfc[:, c * 2:(c + 1) * 2], in_=G[:, 530:532], func=AF.Copy), reads=[R_G], writes=[R_sfc])
                P.op("act", _call("activation", out=carry[:, c * 2:(c + 1) * 2], in_=G[:, 512:514], func=AF.Copy),
                     reads=[R_G], writes=[R_carry[c]])
                P.op("act", _call("activation", out=t0[:, 0:W], in_=G[:, 2:2 + W], func=AF.Identity,
                                  scale=wconv[:, c * 3 + 2:c * 3 + 3], bias=bconv[:, c:c + 1]),
                     reads=[R_G, R_cc], writes=[R_t])
                P.op("act", _call("activation", out=t1[:, 0:W], in_=G[:, 1:1 + W], func=AF.Identity, scale=wconv[:, c * 3 + 1:c * 3 + 2]),
                     reads=[R_G, R_cc], writes=[R_t1])
                P.op("act", _call("activation", out=t2[:, 0:W], in_=G[:, 0:W], func=AF.Identity, scale=wconv[:, c * 3:c * 3 + 1]),
                     reads=[R_G, R_cc], writes=[R_t2])
                P.op("dve", _call("tensor_tensor", out=t0[:, 0:W], in0=t0[:, 0:W], in1=t1[:, 0:W], op=ALU.add), reads=[R_t, R_t1], writes=[R_t])
                P.op("dve", _call("tensor_tensor", out=t0[:, 0:W], in0=t0[:, 0:W], in1=t2[:, 0:W], op=ALU.add), reads=[R_t, R_t2], writes=[R_t])

            def c_s3b(grp, c):
                hTg, R_hTg = hT2[grp % 2], R_hT2[grp % 2]
                mo = (c % 3) * 64
                k3 = (grp * NFC + c) % 3
                ub = UB[k3]
                t0, R_t = t0b[k3], R_t0[k3]
                ge, R_g = geb[k3], R_ge[k3]
                W = 530 if grp == 0 else 512
                P.op("act", _call("activation", out=ge[:, 0:W], in_=t0[:, 0:W], func=AF.Gelu_apprx_tanh), reads=[R_t], writes=[R_g])
                P.op("dve", _call("tensor_tensor", out=hTg[:, c * 512:(c + 1) * 512], in0=pb[ub][:, :], in1=ge[:, 0:512], op=ALU.mult),
                     reads=[R_pb[ub], R_g], writes=[R_hTg])
                if grp == 0:
                    P.op("dve", _call("tensor_tensor", out=hTm[:, c * 16:(c + 1) * 16], in0=pb[MB][:, mo + 2:mo + 18], in1=ge[:, 514:530], op=ALU.mult),
                         reads=[R_pb[MB], R_g], writes=[R_hTm])

            def c_down(grp):
                hTg, R_hTg = hT2[grp % 2], R_hT2[grp % 2]
                if grp == 0:
                    for n, bank in enumerate(YB):
                        for c in range(NFC):
                            P.op("pe", _call("matmul", out=pb[bank][0:16, :], lhsT=hTm[:, c * 16:(c + 1) * 16],
                                             rhs=wdb[:, c * 1024 + n * 512: c * 1024 + (n + 1) * 512], start=(c == 0), stop=(c == NFC - 1)),
                                 reads=[R_hTm, R_wd], writes=[R_pb[bank]])
                    P.dma("sp", h2r[0][0:16, :], h2D[17 * 128: 17 * 128 + 16, :], reads=[R_h2D[17]], writes=[R_h2r[0]])
                    ln3_out(YB, 16, h2r[0], R_h2r[0], O["ys"][:, :], yB[0], R_yB[0])
                    P.dma("sp", O["sfcT"][:, :], sfc[:, :], reads=[R_sfc], defer=True)
                for bi in range(4):
                    blk = grp * 4 + bi
                    hs = blk % 2
                    P.dma("sp", h2r[hs][:, :], h2D[blk * 128:(blk + 1) * 128, :], reads=[R_h2D[blk]], writes=[R_h2r[hs]])
                    for n, bank in enumerate(YB):
                        for c in range(NFC):
                            P.op("pe", _call("matmul", out=pb[bank][:, :], lhsT=hTg[:, c * 512 + bi * 128: c * 512 + (bi + 1) * 128],
                                             rhs=wdb[:, c * 1024 + n * 512: c * 1024 + (n + 1) * 512], start=(c == 0), stop=(c == NFC - 1)),
                                 reads=[R_hTg, R_wd], writes=[R_pb[bank]])
                    ln3_out(YB, 128, h2r[hs], R_h2r[hs], O["y"][blk * 128:(blk + 1) * 128, :], yB[hs], R_yB[hs])

            seq = [(grp, c) for grp in range(4) for c in range(NFC)]
            nseq = len(seq)
            load_h2Tg(0)
            load_h2Tg(1)
            for idx in range(nseq + 3):
                if 1 <= idx <= NFC:
                    wdown_piece(idx - 1)
                if idx < nseq:
                    c_s1(*seq[idx])
                if 1 <= idx <= nseq:
                    c_s2(*seq[idx - 1])
                if 3 <= idx:
                    g4, c4 = seq[idx - 3]
                    c_s3b(g4, c4)
                    if c4 == NFC - 1:
                        c_down(g4)
                        if g4 + 2 < 4:
                            load_h2Tg(g4 + 2)
                if 2 <= idx <= nseq + 1:
                    c_s3(*seq[idx - 2])
            P.dma("sp", O["fcT"][:, :], carry[:, :], reads=R_carry, defer=True)
            P.finish()
            P.flush(block)
    return nc


def _t5_bucket(rel):
    half, max_exact = 16, 8
    n = np.abs(rel)
    log_ratio = np.log(np.maximum(n, 1).astype(np.float32) / max_exact) / math.log(128 / max_exact)
    large = np.minimum(max_exact + (log_ratio * (half - max_exact)).astype(np.int32), half - 1)
    return np.where(rel < 0, half, 0) + np.where(n < max_exact, n, large)


def _host_inputs(inp):
    f32 = np.float32
    x_prompt = np.asarray(inp["x_prompt"], f32)
    x_sample = np.asarray(inp["x_sample"], f32)
    w_in = np.asarray(inp["w_in"], f32)[0]
    qa, ka, va = w_in[:, 0:512], w_in[:, 512:1024], w_in[:, 1024:1536]
    qb, kb, vb = w_in[:, 1536:2048], w_in[:, 2048:2176], w_in[:, 2176:2304]
    qi, ki, wi = w_in[:, 2304:2816], w_in[:, 2816:2880], w_in[:, 2880:2888]
    qbp = np.concatenate([np.concatenate([qb[:, r * 64:(r + 1) * 64], qb[:, (4 + r) * 64:(5 + r) * 64]], axis=1) for r in range(4)], axis=1)
    winp = np.concatenate([qa, ka, qbp, kb, qi, ki, ki, va, vb, wi], axis=1)
    assert winp.shape[1] == NCOL

    def kc_layout(w):
        n = w.shape[1]
        return np.ascontiguousarray(w.reshape(8, 128, n).transpose(1, 0, 2).reshape(128, 8 * n))

    shared = {}
    shared["win"] = kc_layout(winp)
    shared["wo"] = kc_layout(np.asarray(inp["w_o"], f32)[0])
    shared["wmq"] = kc_layout(np.asarray(inp["w_mq"], f32)[0])
    shared["wmk"] = kc_layout(np.asarray(inp["w_mk"], f32)[0])
    shared["wmv"] = kc_layout(np.asarray(inp["w_mv"], f32)[0])
    wmo = np.asarray(inp["w_mo"], f32)[0]
    shared["wmo"] = np.ascontiguousarray(wmo.reshape(4, 128, 1024).transpose(1, 0, 2).reshape(128, 4096))
    w_up = np.asarray(inp["w_up"], f32)[0]
    wu = w_up[:, :DFF].reshape(8, 128, NFC, 128)
    wg = w_up[:, DFF:].reshape(8, 128, NFC, 128)
    wup = np.stack([wu, wg], axis=3)
    shared["wup"] = np.ascontiguousarray(wup.transpose(2, 1, 0, 3, 4).reshape(NFC, 128, 8 * 256))
    w_down = np.asarray(inp["w_down"], f32)[0]
    shared["wdown"] = np.ascontiguousarray(w_down.reshape(NFC, 128, 1024).transpose(1, 0, 2).reshape(128, NFC * 1024))
    shared["lnp"] = np.ascontiguousarray(np.stack([np.asarray(inp[k], f32)[0] for k in ("ln1_g", "ln1_b", "ln2_g", "ln2_b", "ln3_g", "ln3_b")]))
    w_conv = np.asarray(inp["w_conv"], f32)[0]
    shared["wconvT"] = np.ascontiguousarray(w_conv.reshape(3, NFC, 128).transpose(2, 1, 0).reshape(128, NFC * 3))
    shared["bconvT"] = np.ascontiguousarray(np.asarray(inp["b_conv"], f32)[0].reshape(NFC, 128).T)
    shared["ident"] = np.eye(128, dtype=f32)
    tabA = np.asarray(inp["a_rel_bias"], f32)[0]
    qq = np.arange(128)[:, None]
    kk = np.arange(640)[None, :]
    kpos = kk - 512
    rel = qq - kpos
    cq = qq // 64
    kch = np.floor_divide(kpos, 64)
    allowed = (kch >= cq - 8) & (kch <= cq)
    bias = tabA[np.clip(rel, -64, 64) + 64]
    AB = np.where(allowed[:, :, None], bias, f32(NEGM)).astype(f32)
    shared["AB"] = np.ascontiguousarray(AB.transpose(0, 2, 1).reshape(128, 8 * ABW))
    js = np.arange(16)[:, None]
    ks = np.arange(528)[None, :]
    ABs = tabA[np.clip(512 + js - ks, -64, 64) + 64]
    shared["ABs"] = np.ascontiguousarray(ABs.transpose(0, 2, 1).reshape(16, 8 * 528)).astype(f32)
    t5 = np.asarray(inp["t5_bias"], f32)
    relB = np.arange(128)[:, None] - np.arange(256)[None, :] + 128
    Bn = t5[_t5_bucket(relB)]
    shared["Bn"] = np.ascontiguousarray(Bn.transpose(0, 2, 1).reshape(128, 8 * BNW)).astype(f32)
    relBs = 128 + np.arange(16)[:, None] - np.arange(144)[None, :]
    Bns = t5[_t5_bucket(relBs)]
    shared["Bns"] = np.ascontiguousarray(Bns.transpose(0, 2, 1).reshape(16, 8 * 144)).astype(f32)
    shared["C15"] = np.ascontiguousarray(np.broadcast_to(t5[15][None, :], (128, 8))).astype(f32)
    dm = np.zeros((128, 128), f32)
    dm[0:64, 64:128] = NEGM
    shared["diagmask"] = dm

    mem_prompt = np.asarray(inp["mem_prompt"], f32)
    maps = []
    for c in range(8):
        b, half = c // 2, c % 2
        m = dict(shared)
        xk = np.zeros((4096, 1024), f32)
        if half == 1:
            xk[:] = x_prompt[b]
        else:
            xk[2048:] = x_prompt[b, :2048]
        m["xkT"] = np.ascontiguousarray(xk.reshape(32, 128, 8, 128).transpose(0, 3, 2, 1).reshape(32, 128, 1024))
        xs = x_sample[c]
        m["xsT"] = np.ascontiguousarray(xs.reshape(16, 8, 128).transpose(2, 1, 0).reshape(128, 128))
        xres = np.zeros((NBLK * 128, 1024), f32)
        xres[0:2048] = xk[2048:]
        xres[2048:2050] = xk[2046:2048]
        xres[17 * 128:17 * 128 + 16] = xs
        m["xres"] = xres
        m["memT"] = np.ascontiguousarray(mem_prompt[b].reshape(256, 8, 128).transpose(2, 1, 0).reshape(128, 2048))
        cmk = np.asarray(inp["cache_mem_k"], f32)[0, c]
        m["cmkT"] = np.ascontiguousarray(cmk.transpose(2, 1, 0).reshape(128, 1024))
        m["cmv"] = np.ascontiguousarray(np.asarray(inp["cache_mem_v"], f32)[0, c].reshape(256, 512))
        cak = np.asarray(inp["cache_a_k"], f32)[0, c]
        m["cakT"] = np.ascontiguousarray(cak.reshape(512, 4, 2, 64).transpose(2, 3, 1, 0).reshape(128, 2048))
        m["cav"] = np.ascontiguousarray(np.asarray(inp["cache_a_v"], f32)[0, c].reshape(512, 512))
        cbk = np.asarray(inp["cache_b_k"], f32)[0, c]
        m["cbkT"] = np.ascontiguousarray(cbk.reshape(2048, 128).T)
        m["cbv"] = np.ascontiguousarray(np.asarray(inp["cache_b_v"], f32)[0, c].reshape(2048, 128))
        cbi = np.asarray(inp["cache_b_kidx"], f32)[0, c]
        m["cbiT"] = np.ascontiguousarray(np.concatenate([cbi.T, cbi.T], axis=0))
        sc_ = np.asarray(inp["state_ffn_conv"], f32)[0, c]
        m["sconvT"] = np.ascontiguousarray(sc_.reshape(2, NFC, 128).transpose(2, 1, 0).reshape(128, NFC * 2))
        m["colmask"] = np.full((128, 1), NEGM if half == 0 else 0.0, f32)
        kv = np.ones((128, NT), f32)
        if half == 0:
            kv[:, 0:16] = 0.0
        m["kvalid"] = kv
        m["flag"] = np.full((128, 1), float(half), f32)
        maps.append(m)
    return maps


_NC_CACHE = {}


def _run(inputs, debug=False):
    key = bool(debug)
    if key not in _NC_CACHE:
        _NC_CACHE[key] = build_program(debug=debug)
    nc = _NC_CACHE[key]
    maps = _host_inputs(inputs)
    res = run_bass_kernel_spmd(nc, maps, core_ids=list(range(8)))
    return res.results


def kernel(**inputs):
    R = _run(inputs)
    f32 = np.float32
    y = np.zeros((4, 4096, 1024), f32)
    ys = np.zeros((8, 16, 1024), f32)
    pak = np.zeros((1, 4, 512, 8, 64), f32)
    pav = np.zeros((1, 4, 512, 8, 64), f32)
    pbk = np.zeros((1, 4, 4096, 2, 64), f32)
    pbv = np.zeros((1, 4, 4096, 2, 64), f32)
    pbi = np.zeros((1, 4, 4096, 64), f32)
    pmk = np.zeros((1, 4, 256, 4, 128), f32)
    pmv = np.zeros((1, 4, 256, 4, 128), f32)
    pfc = np.zeros((1, 4, 2, DFF), f32)
    sak = np.zeros((1, 8, 16, 8, 64), f32)
    sav = np.zeros((1, 8, 16, 8, 64), f32)
    sbk = np.zeros((1, 8, 16, 2, 64), f32)
    sbv = np.zeros((1, 8, 16, 2, 64), f32)
    sbi = np.zeros((1, 8, 16, 64), f32)
    sfc = np.zeros((1, 8, 2, DFF), f32)
    for c in range(8):
        b, half = c // 2, c % 2
        r = R[c]
        y[b, half * 2048:(half + 1) * 2048] = np.asarray(r["y"], f32)
        ys[c] = np.asarray(r["ys"], f32)
        if half == 1:
            akT = np.asarray(r["akT"], f32).reshape(2, 64, 4, 512)
            pak[0, b] = akT.transpose(3, 2, 0, 1).reshape(512, 8, 64)
            pav[0, b] = np.asarray(r["av"], f32).reshape(512, 8, 64)
            pbk[0, b] = np.asarray(r["bkT"], f32).T.reshape(4096, 2, 64)
            pbv[0, b] = np.asarray(r["bv"], f32).reshape(4096, 2, 64)
            pbi[0, b] = np.asarray(r["biT"], f32).T
            pmk[0, b] = np.asarray(r["mkT"], f32).reshape(128, 4, 256).transpose(2, 1, 0)
            pmv[0, b] = np.asarray(r["mv"], f32).reshape(256, 4, 128)
            pfc[0, b] = np.asarray(r["fcT"], f32).reshape(128, NFC, 2).transpose(2, 1, 0).reshape(2, DFF)
        sakT = np.asarray(r["sakT"], f32).reshape(2, 64, 4, 16)
        sak[0, c] = sakT.transpose(3, 2, 0, 1).reshape(16, 8, 64)
        sav[0, c] = np.asarray(r["sav"], f32).reshape(16, 8, 64)
        sbk[0, c] = np.asarray(r["sbkT"], f32).T.reshape(16, 2, 64)
        sbv[0, c] = np.asarray(r["sbv"], f32).reshape(16, 2, 64)
        sbi[0, c] = np.asarray(r["sbiT"], f32).T
        sfc[0, c] = np.asarray(r["sfcT"], f32).reshape(128, NFC, 2).transpose(2, 1, 0).reshape(2, DFF)
    return (y, ys, pak, pav, pbk, pbv, pbi, pmk, pmv, pfc, sak, sav, sbk, sbv, sbi, sfc)
```

```python
import math
from contextlib import ExitStack

import numpy as np
import concourse.bass as bass
import concourse.mybir as mybir
from concourse.bass_utils import run_bass_kernel_spmd

F32 = mybir.dt.float32
BF16 = mybir.dt.bfloat16
AF = mybir.ActivationFunctionType
ALU = mybir.AluOpType

D = 1024
KC = 8
NT = 32
NCOL = 2952
C_QA, C_KA, C_QB, C_KB, C_QI, C_KI, C_VA, C_VB, C_WI = 0, 512, 1024, 1536, 1664, 2176, 2304, 2816, 2944
DFF = 2816
NFC = 22
ALPHA = 2.0 ** 0.25
LN_EPS = 1e-5
NEGM = -30000.0
NIT = 16
BIS_W0 = 16.0
ABW = 640
BNW = 256
NBLK = 18


class Res:
    __slots__ = ("lw", "rd", "name", "excl")

    def __init__(self, name="", excl=False):
        self.lw = None
        self.rd = {}
        self.name = name
        self.excl = excl


def _call(name, *args, **kw):
    return lambda e: getattr(e, name)(*args, **kw)


class Prog:
    ENG = ("pe", "act", "dve", "pool", "sp")

    def __init__(self, nc, sems, dma_sems):
        self.nc = nc
        self.streams = {e: [] for e in self.ENG}
        self.sem = sems
        self.cnt = {e: 0 for e in self.ENG}
        self.seen = {e: {} for e in self.ENG}
        self.dsems = dma_sems
        self.dval = [0] * len(dma_sems)
        self.dnext = 0
        self.semh = dict(sems)
        for i, h in enumerate(dma_sems):
            self.semh[("d", i)] = h
        self.ninst = 0
        self.dead = False
        self.deferred = []
        self.defer_lag = 48

    def _deps(self, reads, writes, eng=None):
        d = {}
        for r in reads:
            if r.lw is not None:
                k, v = r.lw
                if d.get(k, 0) < v:
                    d[k] = v
            if r.excl:
                for k, v in r.rd.items():
                    if k != eng and d.get(k, 0) < v:
                        d[k] = v
        for w in writes:
            if w.lw is not None:
                k, v = w.lw
                if d.get(k, 0) < v:
                    d[k] = v
            for k, v in w.rd.items():
                if d.get(k, 0) < v:
                    d[k] = v
        return d

    def _wait(self, eng, deps):
        for k, v in deps.items():
            if k == "pe" and eng == "pe":
                continue
            if self.seen[eng].get(k, 0) >= v:
                continue
            self.seen[eng][k] = v
            h = self.semh[k]
            self.streams[eng].append(lambda e, h=h, v=v: e.wait_ge(h, v))

    def _flush_deferred(self, force=False, reads=(), writes=()):
        if not self.deferred:
            return
        conflict = force
        if not conflict:
            ws = set(id(w) for w in writes)
            rs = set(id(r) for r in reads)
            for d in self.deferred:
                dr = set(id(x) for x in d[3])
                dw = set(id(x) for x in d[4])
                if (ws & dr) or (ws & dw) or (rs & dw):
                    conflict = True
                    break
        if conflict:
            pend, self.deferred = self.deferred, []
            for d in pend:
                self._dma_now(d[0], d[1], d[2], d[3], d[4], d[5])
            return
        while self.deferred and self.ninst - self.deferred[0][6] >= self.defer_lag:
            d = self.deferred.pop(0)
            self._dma_now(d[0], d[1], d[2], d[3], d[4], d[5])

    def op(self, eng, fn, reads=(), writes=()):
        if self.dead:
            return
        self._flush_deferred(False, reads, writes)
        self._wait(eng, self._deps(reads, writes, eng))
        self.cnt[eng] += 1
        n = self.cnt[eng]
        h = self.sem[eng]
        self.streams[eng].append(lambda e, fn=fn, h=h: fn(e).then_inc(h, 1))
        self.ninst += 1
        for r in reads:
            if r.rd.get(eng, 0) < n:
                r.rd[eng] = n
        for w in writes:
            w.lw = (eng, n)
            w.rd = {}

    def dma(self, q, out, in_, reads=(), writes=(), slow=False, defer=False):
        if self.dead:
            return
        if defer:
            self._flush_deferred(False, reads, writes)
            self.deferred.append((q, out, in_, list(reads), list(writes), slow, self.ninst))
            return
        self._flush_deferred(False, reads, writes)
        self._dma_now(q, out, in_, reads, writes, slow)

    def _dma_now(self, q, out, in_, reads=(), writes=(), slow=False):
        deps = self._deps(reads, writes)
        i = self.dnext
        self.dnext = (i + 1) % len(self.dsems)
        k = ("d", i)
        if self.dval[i] > 0 and deps.get(k, 0) < self.dval[i]:
            deps[k] = self.dval[i]
        self._wait(q, deps)
        self.dval[i] += 16
        v = self.dval[i]
        h = self.dsems[i]
        if slow:
            self.streams[q].append(
                lambda e, out=out, in_=in_, h=h: e.dma_start(out=out, in_=in_, allow_slow_non_contiguous=True).then_inc(h, 16))
        else:
            self.streams[q].append(lambda e, out=out, in_=in_, h=h: e.dma_start(out=out, in_=in_).then_inc(h, 16))
        self.ninst += 1
        for r in reads:
            if r.rd.get(k, 0) < v:
                r.rd[k] = v
        for w in writes:
            w.lw = (k, v)
            w.rd = {}

    def barrier(self):
        if self.dead:
            return
        self._flush_deferred(True)
        deps = {e: self.cnt[e] for e in self.ENG if self.cnt[e] > 0}
        for i, v in enumerate(self.dval):
            if v > 0:
                deps[("d", i)] = v
        for e in self.ENG:
            self._wait(e, dict(deps))

    def finish(self):
        self._flush_deferred(True)
        deps = {("d", i): v for i, v in enumerate(self.dval) if v > 0}
        self._wait("sp", deps)

    def flush(self, block):
        self._flush_deferred(True)
        s = self.streams
        self.streams = {e: [] for e in self.ENG}

        def mk(lst):
            def body(e):
                for f in lst:
                    f(e)
            return body

        block.tensor(mk(s["pe"]))
        block.scalar(mk(s["act"]))
        block.vector(mk(s["dve"]))
        block.gpsimd(mk(s["pool"]))
        block.sync(mk(s["sp"]))


def build_program(debug=False, stop_at=None):
    nc = bass.Bass("TRN2", target_bir_lowering=False)

    def din(name, shape, dt=F32):
        return nc.dram_tensor(name, list(shape), dt, kind="ExternalInput").ap()

    def dout(name, shape, dt=F32):
        return nc.dram_tensor(name, list(shape), dt, kind="ExternalOutput").ap()

    def dscr(name, shape, dt):
        return nc.dram_tensor(name, list(shape), dt, kind="Internal").ap()

    I = {}
    I["xkT"] = din("xkT", [NT, 128, 1024])
    I["xsT"] = din("xsT", [128, 8 * 16])
    I["xres"] = din("xres", [NBLK * 128, 1024])
    I["win"] = din("win", [128, KC * NCOL])
    I["wo"] = din("wo", [128, 8 * 1024])
    I["wmq"] = din("wmq", [128, 8 * 512])
    I["wmk"] = din("wmk", [128, 8 * 512])
    I["wmv"] = din("wmv", [128, 8 * 512])
    I["wmo"] = din("wmo", [128, 4 * 1024])
    I["wup"] = din("wup", [NFC, 128, 8 * 256])
    I["wdown"] = din("wdown", [128, NFC * 1024])
    I["lnp"] = din("lnp", [6, 1024])
    I["wconvT"] = din("wconvT", [128, NFC * 3])
    I["bconvT"] = din("bconvT", [128, NFC])
    I["memT"] = din("memT", [128, 8 * 256])
    I["cmkT"] = din("cmkT", [128, 4 * 256])
    I["cmv"] = din("cmv", [256, 512])
    I["cakT"] = din("cakT", [128, 4 * 512])
    I["cav"] = din("cav", [512, 512])
    I["cbkT"] = din("cbkT", [128, 2048])
    I["cbv"] = din("cbv", [2048, 128])
    I["cbiT"] = din("cbiT", [128, 2048])
    I["sconvT"] = din("sconvT", [128, NFC * 2])
    I["ident"] = din("ident", [128, 128])
    I["AB"] = din("AB", [128, 8 * ABW])
    I["ABs"] = din("ABs", [16, 8 * 528])
    I["Bn"] = din("Bn", [128, 8 * BNW])
    I["Bns"] = din("Bns", [16, 8 * 144])
    I["C15"] = din("C15", [128, 8])
    I["colmask"] = din("colmask", [128, 1])
    I["diagmask"] = din("diagmask", [128, 128])
    I["kvalid"] = din("kvalid", [128, NT])
    I["flag"] = din("flag", [128, 1])

    O = {}
    O["y"] = dout("y", [2048, 1024])
    O["ys"] = dout("ys", [16, 1024])
    O["akT"] = dout("akT", [128, 4 * 512])
    O["av"] = dout("av", [512, 512])
    O["bkT"] = dout("bkT", [128, 4096])
    O["bv"] = dout("bv", [4096, 128])
    O["biT"] = dout("biT", [64, 4096])
    O["mkT"] = dout("mkT", [128, 4 * 256])
    O["mv"] = dout("mv", [256, 512])
    O["fcT"] = dout("fcT", [128, NFC * 2])
    O["sakT"] = dout("sakT", [128, 4 * 16])
    O["sav"] = dout("sav", [16, 512])
    O["sbkT"] = dout("sbkT", [128, 16])
    O["sbv"] = dout("sbv", [16, 128])
    O["sbiT"] = dout("sbiT", [64, 16])
    O["sfcT"] = dout("sfcT", [128, NFC * 2])
    if debug:
        O["dbg_mix"] = dout("dbg_mix", [NBLK * 128, 1024], BF16)
        O["dbg_h2"] = dout("dbg_h2", [NBLK * 128, 1024])
        mixD = O["dbg_mix"]
        h2D = O["dbg_h2"]
    else:
        mixD = dscr("mixD", [NBLK * 128, 1024], BF16)
        h2D = dscr("h2D", [NBLK * 128, 1024], F32)
    h2TD = dscr("h2TD", [NBLK, 128, 1024], BF16)
    R_mixD = [Res("mixD%d" % i) for i in range(NBLK)]
    R_h2D = [Res("h2D%d" % i) for i in range(NBLK)]
    R_h2TD = [Res("h2TD%d" % i) for i in range(NBLK)]

    es = ExitStack()
    with es:
        sems = {e: es.enter_context(nc.semaphore("s_" + e)) for e in Prog.ENG}
        dsems = [es.enter_context(nc.semaphore("d%d" % i)) for i in range(32)]
        P = Prog(nc, sems, dsems)
        block = es.enter_context(nc.Block())

        def checkpoint(name):
            if stop_at is not None and name == stop_at and not P.dead:
                P.finish()
                P.flush(block)
                P.dead = True

        pb = [es.enter_context(nc.psum_tensor("pb%d" % i, [128, 512], F32)) for i in range(8)]
        R_pb = [Res("pb%d" % i, excl=True) for i in range(8)]

        class Rot:
            def __init__(self, idxs):
                self.idxs = idxs
                self.i = 0

            def next(self):
                k = self.idxs[self.i % len(self.idxs)]
                self.i += 1
                return k

        def sb(stack, name, shape, dt):
            return stack.enter_context(nc.sbuf_tensor("sb_" + name, list(shape), dt))

        ident_f = sb(es, "ident_f", [128, 128], F32)
        ident = sb(es, "ident", [128, 512], BF16)
        R_ident = Res("ident")
        P.dma("sp", ident_f[:, :], I["ident"][:, :], writes=[R_ident])
        for r in range(4):
            P.op("act", _call("activation", out=ident[:, r * 128:(r + 1) * 128], in_=ident_f[:, :], func=AF.Copy),
                 reads=[R_ident], writes=[R_ident])

        def run_interleaved(gens):
            gens = [[0.0, i, g] for i, g in enumerate(gens)]
            while gens:
                gens.sort(key=lambda x: (x[0], x[1]))
                ent = gens[0]
                try:
                    c = next(ent[2])
                    ent[0] += (c if c else 1.0)
                except StopIteration:
                    gens.remove(ent)

        with ExitStack() as sa:
            winb = sb(sa, "winb", [128, KC * NCOL], BF16)
            R_win = Res("win")
            kbi = sb(sa, "kbi", [128, 2 * 4096], BF16)
            R_kbi = [Res("kbi%d" % r) for r in range(NT)]
            R_ki = [Res("ki%d" % r) for r in range(NT)]
            vb_aug = sb(sa, "vb_aug", [128, NT * 2 * 65], BF16)
            R_vb = [Res("vb%d" % r) for r in range(NT)]
            kaT = sb(sa, "kaT", [128, 6 * 512], BF16)
            R_ka = [Res("ka%d" % s) for s in range(6)]
            va_aug = sb(sa, "va_aug", [128, 6 * 8 * 65], BF16)
            R_va = [Res("va%d" % s) for s in range(6)]
            ABb = sb(sa, "ABb", [128, 8 * ABW], BF16)
            R_AB = Res("AB")
            Bnb = sb(sa, "Bnb", [128, 8 * BNW], BF16)
            R_Bn = Res("Bn")
            Mnear = [sb(sa, "Mnear%d" % k, [128, 8 * BNW], BF16) for k in range(2)]
            R_Mnear = [Res("Mnear%d" % k) for k in range(2)]
            score = [sb(sa, "score%d" % k, [128, 4096], F32) for k in range(2)]
            R_score = [Res("score%d" % k) for k in range(2)]
            Mb = [sb(sa, "Mb%d" % k, [128, 4096], BF16) for k in range(2)]
            R_M = [Res("M%d" % k) for k in range(2)]
            relu = [sb(sa, "relu%d" % k, [128, 512], BF16) for k in range(3)]
            R_relu = [Res("relu%d" % k) for k in range(3)]
            xstg2 = [sb(sa, "xstg%d" % k, [128, 1024], F32) for k in range(2)]
            R_xstg2 = [Res("xstg%d" % k) for k in range(2)]
            xstg, R_xstg = xstg2[0], R_xstg2[0]
            xTb = [sb(sa, "xTb%d" % k, [128, 1024], BF16) for k in range(2)]
            R_xT = [Res("xT%d" % k) for k in range(2)]
            qaz = [sb(sa, "qaz%d" % k, [128, 1024], BF16) for k in range(2)]
            qbz = [sb(sa, "qbz%d" % k, [128, 1024], BF16) for k in range(3)]
            qiz = [sb(sa, "qiz%d" % k, [128, 1024], BF16) for k in range(2)]
            R_qa = [Res("qa%d" % k) for k in range(2)]
            R_qb = [Res("qb%d" % k) for k in range(3)]
            R_qi = [Res("qi%d" % k) for k in range(2)]
            coef = [sb(sa, "coef%d" % k, [128, 8], F32) for k in range(2)]
            R_coef = [Res("coef%d" % k) for k in range(2)]
            dg = [sb(sa, "dg%d" % k, [128, 1024], BF16) for k in range(2)]
            R_dg = [Res("dg%d" % k) for k in range(2)]
            PTA = [sb(sa, "PTA%d" % k, [128, 512], BF16) for k in range(3)]
            R_PTA = [Res("PTA%d" % k) for k in range(3)]
            PTB = [sb(sa, "PTB%d" % k, [128, 512], BF16) for k in range(3)]
            R_PTB = [Res("PTB%d" % k) for k in range(3)]
            mixb = [sb(sa, "mixb%d" % k, [128, 1024], BF16) for k in range(3)]
            R_mix = [Res("mix%d" % k) for k in range(3)]
            ostg = [sb(sa, "ostg%d" % k, [128, 256], F32) for k in range(2)]
            R_ostg = [Res("ostg%d" % k) for k in range(2)]
            vbstg = [sb(sa, "vbstg%d" % k, [128, 128], F32) for k in range(2)]
            R_vbstg = [Res("vbstg%d" % k) for k in range(2)]
            astg = sb(sa, "astg", [128, 1024], F32)
            R_astg = Res("astg")
            small = [sb(sa, "small%d" % k, [128, 16], F32) for k in range(2)]
            R_small = [Res("small%d" % k) for k in range(2)]
            recA = [sb(sa, "recA%d" % k, [128, 8], F32) for k in range(2)]
            R_recA = [Res("recA%d" % k) for k in range(2)]
            recB = [sb(sa, "recB%d" % k, [128, 8], F32) for k in range(2)]
            R_recB = [Res("recB%d" % k) for k in range(2)]
            colmask = sb(sa, "colmask", [128, 1], F32)
            diagm = sb(sa, "diagm", [128, 128], F32)
            kvalid = sb(sa, "kvalid", [128, NT], F32)
            c15 = sb(sa, "c15", [128, 8], F32)
            ones8 = sb(sa, "ones8", [128, 8], F32)
            R_cst = Res("cst")

            wrot = Rot([0, 1, 2])

            P.dma("sp", colmask[:, :], I["colmask"][:, :], writes=[R_cst])
            P.dma("sp", diagm[:, :], I["diagmask"][:, :], writes=[R_cst])
            P.dma("sp", kvalid[:, :], I["kvalid"][:, :], writes=[R_cst])
            P.dma("sp", c15[:, :], I["C15"][:, :], writes=[R_cst])
            P.op("pool", _call("memset", ones8[:, :], 1.0), writes=[R_cst])
            for k in range(2):
                P.op("pool", _call("memset", qaz[k][:, :], 0.0), writes=[R_qa[k]])
                P.op("pool", _call("memset", qiz[k][:, :], 0.0), writes=[R_qi[k]])
            for k in range(3):
                P.op("pool", _call("memset", qbz[k][:, :], 0.0), writes=[R_qb[k]])

            HW = NCOL // 2
            R_slot = [Res("wslot%d" % q) for q in range(4)]
            for kc in range(KC):
                for hh in range(2):
                    q = (kc * 2 + hh) % 4
                    stg = score[q // 2][:, (q % 2) * HW:(q % 2 + 1) * HW]
                    P.dma("sp", stg, I["win"][:, kc * NCOL + hh * HW: kc * NCOL + (hh + 1) * HW], writes=[R_slot[q]])
                    if hh == 0:
                        P.op("act", _call("activation", out=winb[:, kc * NCOL + hh * HW: kc * NCOL + (hh + 1) * HW], in_=stg, func=AF.Copy),
                             reads=[R_slot[q]], writes=[R_win])
                    else:
                        P.op("dve", _call("tensor_copy", out=winb[:, kc * NCOL + hh * HW: kc * NCOL + (hh + 1) * HW], in_=stg),
                             reads=[R_slot[q]], writes=[R_win])
            for hh in range(2):
                w = 4 * ABW
                P.dma("sp", score[hh][:, 0:w], I["AB"][:, hh * w:(hh + 1) * w], writes=[R_score[hh], R_slot[2 * hh], R_slot[2 * hh + 1]])
                P.op("act", _call("activation", out=ABb[:, hh * w:(hh + 1) * w], in_=score[hh][:, 0:w], func=AF.Copy),
                     reads=[R_score[hh]], writes=[R_AB])
            P.dma("sp", score[0][:, 0:8 * BNW], I["Bn"][:, :], writes=[R_score[0]])
            for h in range(8):
                P.op("dve", _call("tensor_scalar", out=Bnb[:, h * BNW:(h + 1) * BNW], in0=score[0][:, h * BNW:(h + 1) * BNW],
                                  scalar1=c15[:, h:h + 1], scalar2=None, op0=ALU.subtract),
                     reads=[R_score[0], R_cst], writes=[R_Bn])
            checkpoint('consts')

            def win_cols(kc, c0, n):
                return winb[:, kc * NCOL + c0: kc * NCOL + c0 + n]

            def fm_proj(bank, xT, R_x, N, col0, nchunks, ocol=0):
                for j in range(nchunks):
                    for kc in range(KC):
                        P.op("pe", _call("matmul", out=pb[bank][:, ocol + j * N: ocol + (j + 1) * N], lhsT=win_cols(kc, col0 + j * 128, 128),
                                         rhs=xT[:, kc * N:(kc + 1) * N], start=(kc == 0), stop=(kc == KC - 1)),
                             reads=[R_win, R_x], writes=[R_pb[bank]])

            def tm_proj(bank, xT, R_x, N, col0, ncols, ocol=0):
                for kc in range(KC):
                    P.op("pe", _call("matmul", out=pb[bank][0:N, ocol:ocol + ncols], lhsT=xT[:, kc * N:(kc + 1) * N],
                                     rhs=win_cols(kc, col0, ncols), start=(kc == 0), stop=(kc == KC - 1)),
                         reads=[R_win, R_x], writes=[R_pb[bank]])

            def load_xT(r, eng="pool"):
                s = r % 2
                P.dma("sp", xstg2[s][:, :], I["xkT"][r], writes=[R_xstg2[s]])
                P.op(eng, _call("tensor_copy", out=xTb[s][:, :], in_=xstg2[s][:, :]), reads=[R_xstg2[s]], writes=[R_xT[s]])

            def kside(r, full):
                s = r % 2
                xT, R_x = xTb[s], R_xT[s]
                so = r % 2
                bk = wrot.next()
                fm_proj(bk, xT, R_x, 128, C_KB, 1)
                fm_proj(bk, xT, R_x, 128, C_KI, 1, ocol=128)
                P.op("act", _call("activation", out=ostg[so][:, :], in_=pb[bk][:, 0:256], func=AF.Copy), reads=[R_pb[bk]], writes=[R_ostg[so]])
                P.op("pool", _call("tensor_copy", out=kbi[:, :].rearrange("p (a c) -> p a c", a=2)[:, :, r * 128:(r + 1) * 128],
                                   in_=ostg[so][:, :].rearrange("p (a c) -> p a c", a=2)),
                     reads=[R_ostg[so]], writes=[R_kbi[r], R_ki[r]])
                P.dma("sp", O["bkT"][:, r * 128:(r + 1) * 128], ostg[so][:, 0:128], reads=[R_ostg[so]], defer=True)
                P.dma("sp", O["biT"][:, r * 128:(r + 1) * 128], ostg[so][0:64, 128:256], reads=[R_ostg[so]], defer=True)
                yield 3.0
                bv_ = wrot.next()
                tm_proj(bv_, xT, R_x, 128, C_VB, 128)
                vbv = vb_aug[:, r * 130:(r + 1) * 130].rearrange("p (g d) -> p g d", d=65)
                P.op("act", _call("activation", out=vbstg[so][:, :], in_=pb[bv_][:, 0:128], func=AF.Copy), reads=[R_pb[bv_]], writes=[R_vbstg[so]])
                P.op("pool", _call("tensor_copy", out=vbv[:, :, 0:64], in_=vbstg[so][:, :].rearrange("p (g d) -> p g d", d=64)),
                     reads=[R_vbstg[so]], writes=[R_vb[r]])
                P.op("pool", _call("tensor_scalar", out=vbv[:, :, 64:65], in0=ones8[:, 0:2].rearrange("p (g o) -> p g o", o=1),
                                   scalar1=kvalid[:, r:r + 1], scalar2=None, op0=ALU.mult),
                     reads=[R_cst], writes=[R_vb[r]])
                P.dma("sp", O["bv"][r * 128:(r + 1) * 128, :], vbstg[so][:, :], reads=[R_vbstg[so]], defer=True)
                yield 3.0
                if not full:
                    return
                slot = r % 6
                ba = wrot.next()
                fm_proj(ba, xT, R_x, 128, C_KA, 4)
                P.op("act", _call("activation", out=kaT[:, slot * 512:(slot + 1) * 512], in_=pb[ba][:, :], func=AF.Copy),
                     reads=[R_pb[ba]], writes=[R_ka[slot]])
                if r >= 28:
                    P.op("dve", _call("tensor_copy", out=astg[:, 0:512], in_=pb[ba][:, :]), reads=[R_pb[ba]], writes=[R_astg])
                    P.dma("sp", O["akT"].rearrange("p (j t) -> p j t", t=512)[:, :, (r - 28) * 128:(r - 27) * 128],
                          astg[:, 0:512].rearrange("p (j t) -> p j t", t=128), reads=[R_astg], defer=True)
                yield 3.0
                bva = wrot.next()
                tm_proj(bva, xT, R_x, 128, C_VA, 512)
                vav = va_aug[:, slot * 520:(slot + 1) * 520].rearrange("p (h d) -> p h d", d=65)
                P.op("act", _call("activation", out=vav[:, :, 0:64], in_=pb[bva][:, :].rearrange("p (h d) -> p h d", d=64), func=AF.Copy),
                     reads=[R_pb[bva]], writes=[R_va[slot]])
                P.op("pool", _call("tensor_scalar", out=vav[:, :, 64:65], in0=ones8[:, :].rearrange("p (h o) -> p h o", o=1),
                                   scalar1=kvalid[:, r:r + 1], scalar2=None, op0=ALU.mult),
                     reads=[R_cst], writes=[R_va[slot]])
                if r >= 28:
                    P.op("dve", _call("tensor_copy", out=astg[:, 512:1024], in_=pb[bva][:, :]), reads=[R_pb[bva]], writes=[R_astg])
                    P.dma("sp", O["av"][(r - 28) * 128:(r - 27) * 128, :], astg[:, 512:1024], reads=[R_astg], defer=True)
                yield 3.0

            def qside(xT, R_x, qs, st, st3):
                b1 = wrot.next()
                fm_proj(b1, xT, R_x, qs, C_QA, 4)
                for hf in range(2):
                    P.op("act", _call("activation",
                                      out=qaz[st][hf * 64:(hf + 1) * 64, 0:8 * qs].rearrange("p (j two q) -> p j two q", two=2, q=qs)[:, :, hf, :],
                                      in_=pb[b1][hf * 64:(hf + 1) * 64, 0:4 * qs].rearrange("p (j q) -> p j q", q=qs), func=AF.Copy, scale=0.125),
                         reads=[R_pb[b1]], writes=[R_qa[st]])
                yield 3.0
                b2 = wrot.next()
                fm_proj(b2, xT, R_x, qs, C_QB, 4)
                for g in range(2):
                    P.op("act", _call("activation", out=qbz[st3][g * 64:(g + 1) * 64, g * 4 * qs:(g + 1) * 4 * qs],
                                      in_=pb[b2][g * 64:(g + 1) * 64, 0:4 * qs], func=AF.Copy, scale=0.125),
                         reads=[R_pb[b2]], writes=[R_qb[st3]])
                yield 3.0
                b3 = wrot.next()
                fm_proj(b3, xT, R_x, qs, C_QI, 4)
                for hf in range(2):
                    P.op("act", _call("activation",
                                      out=qiz[st][hf * 64:(hf + 1) * 64, 0:8 * qs].rearrange("p (j two q) -> p j two q", two=2, q=qs)[:, :, hf, :],
                                      in_=pb[b3][hf * 64:(hf + 1) * 64, 0:4 * qs].rearrange("p (j q) -> p j q", q=qs), func=AF.Copy),
                         reads=[R_pb[b3]], writes=[R_qi[st]])
                b4 = wrot.next()
                tm_proj(b4, xT, R_x, qs, C_WI, 8)
                P.op("dve", _call("tensor_scalar", out=coef[st][0:qs, :], in0=pb[b4][0:qs, 0:8], scalar1=float(8.0 ** -1.5), scalar2=None, op0=ALU.mult),
                     reads=[R_pb[b4]], writes=[R_coef[st]])
                for h in range(8):
                    P.op("pool", _call("tensor_scalar", out=dg[st][0:qs, h * 128: h * 128 + qs], in0=ident_f[0:qs, 0:qs],
                                       scalar1=coef[st][0:qs, h:h + 1], scalar2=None, op0=ALU.mult),
                         reads=[R_coef[st], R_ident], writes=[R_dg[st]])
                yield 3.0

            def normalize(bank, qs, mixt, R_m, col0, rec, R_rec):
                ov = pb[bank][0:qs, 0:260].rearrange("p (h d) -> p h d", d=65)
                P.op("dve", _call("tensor_scalar", out=rec[0:qs, 0:4].rearrange("p (h o) -> p h o", o=1), in0=ov[:, :, 64:65],
                                  scalar1=1e-30, scalar2=None, op0=ALU.max),
                     reads=[R_pb[bank]], writes=[R_rec])
                P.op("dve", _call("reciprocal", out=rec[0:qs, 0:4], in_=rec[0:qs, 0:4]), reads=[R_rec], writes=[R_rec])
                for hh in range(4):
                    P.op("dve", _call("tensor_scalar", out=mixt[0:qs, col0 + hh * 64: col0 + (hh + 1) * 64],
                                      in0=pb[bank][0:qs, hh * 65: hh * 65 + 64],
                                      scalar1=rec[0:qs, hh:hh + 1], scalar2=None, op0=ALU.mult),
                         reads=[R_pb[bank], R_rec], writes=[R_m])

            def pipe3(items, s1, s2, s3, D, cost=1.0):
                pend = []
                for it in items:
                    s1(it)
                    s2(it)
                    pend.append(it)
                    if len(pend) > D:
                        s3(pend.pop(0))
                    yield cost
                while pend:
                    s3(pend.pop(0))
                    yield cost

            pta_rot = Rot([0, 1, 2])
            relu_rot = Rot([0, 1, 2])
            ptb_rot = Rot([0, 1, 2])
            brot = Rot([3, 7])

            def front_attn(sn, qs, wins, btiles, prompt_masks, abw):
                st = sn % 2
                mixt, R_m = mixb[sn % 3], R_mix[sn % 3]
                nw = len(wins)

                units = []
                for h in range(8):
                    units.append({"h": h, "t0": 0, "tiles": wins[0:4]})
                    if nw > 4:
                        units.append({"h": h, "t0": 4, "tiles": wins[4:5]})

                def a1(u):
                    h = u["h"]
                    j = h // 2
                    bank = wrot.next()
                    u["bank"] = bank
                    for i, (slot, ts) in enumerate(u["tiles"]):
                        t = u["t0"] + i
                        c0 = i * qs
                        P.op("pe", _call("matmul", out=pb[bank][0:ts, c0:c0 + qs], lhsT=kaT[:, slot * 512 + j * 128: slot * 512 + j * 128 + ts],
                                         rhs=qaz[st][:, h * qs:(h + 1) * qs], start=True, stop=False),
                             reads=[R_ka[slot], R_qa[st]], writes=[R_pb[bank]])
                        P.op("pe", _call("matmul", out=pb[bank][0:ts, c0:c0 + qs], lhsT=ABb[0:qs, h * abw + t * 128: h * abw + t * 128 + ts],
                                         rhs=ident[0:qs, 0:qs], start=False, stop=True),
                             reads=[R_AB, R_ident], writes=[R_pb[bank]])

                def a2(u):
                    k = pta_rot.next()
                    u["pt"], u["R_pt"] = PTA[k], R_PTA[k]
                    bank = u["bank"]
                    tsm = max(ts for (_, ts) in u["tiles"])
                    n = len(u["tiles"])
                    P.op("act", _call("activation", out=u["pt"][0:tsm, 0:n * qs], in_=pb[bank][0:tsm, 0:n * qs], func=AF.Exp),
                         reads=[R_pb[bank]], writes=[u["R_pt"]])

                def a3(u):
                    h = u["h"]
                    last_unit = (u["t0"] + len(u["tiles"]) == nw)
                    for i, (slot, ts) in enumerate(u["tiles"]):
                        t = u["t0"] + i
                        P.op("pe", _call("matmul", out=pb[4][0:qs, (h % 4) * 65:(h % 4) * 65 + 65], lhsT=u["pt"][0:ts, i * qs:(i + 1) * qs],
                                         rhs=va_aug[0:ts, slot * 520 + h * 65: slot * 520 + h * 65 + 65],
                                         start=(h % 4 == 0 and t == 0), stop=(t == nw - 1), skip_group_check=True),
                             reads=[u["R_pt"], R_va[slot]], writes=[R_pb[4]])
                    if last_unit and h % 4 == 3:
                        normalize(4, qs, mixt, R_m, (h // 4) * 256, recA[st], R_recA[st])

                yield from pipe3(units, a1, a2, a3, 2, 0.9)

                L = btiles[-1][1] + btiles[-1][2]
                items = []
                cc = 0
                for c0 in range(0, L, 512):
                    w = min(512, L - c0)
                    rk = [R_ki[tt[0]] for tt in btiles if tt[1] >= c0 - 127 and tt[1] < c0 + w]
                    for h in range(8):
                        items.append({"c0": c0, "w": w, "h": h, "sc": (5, 4)[cc % 2], "rk": rk})
                    cc += 1

                def i1(it):
                    bank = wrot.next()
                    it["bank"] = bank
                    h, c0, w = it["h"], it["c0"], it["w"]
                    P.op("pe", _call("matmul", out=pb[bank][0:qs, 0:w], lhsT=qiz[st][:, h * qs:(h + 1) * qs],
                                     rhs=kbi[:, 4096 + c0: 4096 + c0 + w], start=True, stop=True),
                         reads=[R_qi[st]] + it["rk"], writes=[R_pb[bank]])

                def i2(it):
                    k = relu_rot.next()
                    it["rl"], it["R_rl"] = relu[k], R_relu[k]
                    w = it["w"]
                    P.op("act", _call("activation", out=it["rl"][0:qs, 0:w], in_=pb[it["bank"]][0:qs, 0:w], func=AF.Relu),
                         reads=[R_pb[it["bank"]]], writes=[it["R_rl"]])

                def i3(it):
                    h, c0, w, sc = it["h"], it["c0"], it["w"], it["sc"]
                    P.op("pe", _call("matmul", out=pb[sc][0:qs, 0:w], lhsT=dg[st][0:qs, h * 128: h * 128 + qs], rhs=it["rl"][0:qs, 0:w],
                                     start=(h == 0), stop=(h == 7)),
                         reads=[R_dg[st], it["R_rl"]], writes=[R_pb[sc]])
                    if h == 7:
                        if prompt_masks and c0 < 2048:
                            wm = min(w, 2048 - c0)
                            P.op("act", _call("activation", out=score[st][0:qs, c0:c0 + wm], in_=pb[sc][0:qs, 0:wm], func=AF.Identity,
                                              bias=colmask[0:qs, 0:1]),
                                 reads=[R_pb[sc], R_cst], writes=[R_score[st]])
                            if wm < w:
                                P.op("act", _call("activation", out=score[st][0:qs, c0 + wm:c0 + w], in_=pb[sc][0:qs, wm:w], func=AF.Copy),
                                     reads=[R_pb[sc]], writes=[R_score[st]])
                        else:
                            P.op("act", _call("activation", out=score[st][0:qs, c0:c0 + w], in_=pb[sc][0:qs, 0:w], func=AF.Copy),
                                 reads=[R_pb[sc]], writes=[R_score[st]])

                yield from pipe3(items, i1, i2, i3, 2, 0.65)
                if prompt_masks:
                    P.op("dve", _call("tensor_tensor", out=score[st][0:qs, L - 128:L], in0=score[st][0:qs, L - 128:L], in1=diagm[0:qs, :], op=ALU.add),
                         reads=[R_score[st], R_cst], writes=[R_score[st]])
                yield

            def bis_gen(sn, qs, btiles, bnw):
                st = sn % 2
                sm, R_sm = small[st], R_small[st]
                L = btiles[-1][1] + btiles[-1][2]
                P.op("dve", _call("memset", sm[0:qs, 1:2], 0.0), writes=[R_sm])
                for k in range(NIT):
                    wk = BIS_W0 / (2.0 ** k)
                    P.op("dve", _call("tensor_scalar", out=Mb[st][0:qs, 0:L], in0=score[st][0:qs, 0:L], scalar1=sm[0:qs, 1:2], scalar2=None,
                                      op0=ALU.is_ge, op1=ALU.add, accum_out=sm[0:qs, 0:1]),
                         reads=[R_score[st], R_sm], writes=[R_M[st], R_sm])
                    P.op("dve", _call("tensor_scalar", out=sm[0:qs, 2:3], in0=sm[0:qs, 0:1], scalar1=255.5, scalar2=wk,
                                      op0=ALU.is_ge, op1=ALU.mult),
                         reads=[R_sm], writes=[R_sm])
                    P.op("dve", _call("scalar_tensor_tensor", out=sm[0:qs, 1:2], in0=sm[0:qs, 2:3], scalar=-wk / 2.0,
                                      in1=sm[0:qs, 1:2], op0=ALU.add, op1=ALU.add),
                         reads=[R_sm], writes=[R_sm])
                    yield L * 1.08e-3 + 0.5
                wl = BIS_W0 / (2.0 ** (NIT - 1)) / 2.0
                P.op("dve", _call("tensor_scalar", out=sm[0:qs, 3:4], in0=sm[0:qs, 1:2], scalar1=-wl, scalar2=None, op0=ALU.add),
                     reads=[R_sm], writes=[R_sm])
                P.op("dve", _call("tensor_scalar", out=Mb[st][0:qs, 0:L], in0=score[st][0:qs, 0:L], scalar1=sm[0:qs, 3:4], scalar2=NEGM,
                                  op0=ALU.is_lt, op1=ALU.mult),
                     reads=[R_score[st], R_sm], writes=[R_M[st]])
                nearw = btiles[-2][2] + btiles[-1][2]
                for h in range(8):
                    P.op("dve", _call("tensor_tensor", out=Mnear[st][0:qs, h * bnw: h * bnw + nearw], in0=Bnb[0:qs, h * bnw: h * bnw + nearw],
                                      in1=Mb[st][0:qs, L - nearw:L], op=ALU.add),
                         reads=[R_Bn, R_M[st]], writes=[R_Mnear[st]])
                yield
            def battn_gen(sn, qs, btiles, blk, bnw):
                st = sn % 2
                st3 = sn % 3
                mixt, R_m = mixb[st3], R_mix[st3]
                nb = len(btiles)
                items = [{"g": g, "t": t, "vt": vt, "c0": c0, "ts": ts} for g in range(2) for t, (vt, c0, ts) in enumerate(btiles)]

                def b1(it):
                    g, t, vt, c0, ts = it["g"], it["t"], it["vt"], it["c0"], it["ts"]
                    bank = brot.next()
                    it["bank"] = bank
                    P.op("pe", _call("matmul", out=pb[bank][0:ts, 0:4 * qs], lhsT=kbi[:, c0:c0 + ts],
                                     rhs=qbz[st3][:, g * 4 * qs:(g + 1) * 4 * qs], start=True, stop=False),
                         reads=[R_kbi[vt], R_qb[st3]], writes=[R_pb[bank]])
                    if t < nb - 2 and qs == 128:
                        P.op("pe", _call("matmul", out=pb[bank][0:ts, 0:512], lhsT=Mb[st][0:qs, c0:c0 + ts], rhs=ident[0:128, 0:512],
                                         start=False, stop=True),
                             reads=[R_M[st], R_ident], writes=[R_pb[bank]])
                    elif t < nb - 2:
                        for r in range(4):
                            P.op("pe", _call("matmul", out=pb[bank][0:ts, r * qs:(r + 1) * qs], lhsT=Mb[st][0:qs, c0:c0 + ts],
                                             rhs=ident[0:qs, 0:qs], start=False, stop=(r == 3)),
                                 reads=[R_M[st], R_ident], writes=[R_pb[bank]])
                    else:
                        tt = t - (nb - 2)
                        for r in range(4):
                            hh = g * 4 + r
                            P.op("pe", _call("matmul", out=pb[bank][0:ts, r * qs:(r + 1) * qs],
                                             lhsT=Mnear[st][0:qs, hh * bnw + tt * 128: hh * bnw + tt * 128 + ts], rhs=ident[0:qs, 0:qs],
                                             start=False, stop=(r == 3)),
                                 reads=[R_Mnear[st], R_ident], writes=[R_pb[bank]])

                def b2(it):
                    k = ptb_rot.next()
                    it["ptb"], it["R_ptb"] = PTB[k], R_PTB[k]
                    ts = it["ts"]
                    P.op("act", _call("activation", out=it["ptb"][0:ts, 0:4 * qs], in_=pb[it["bank"]][0:ts, 0:4 * qs], func=AF.Exp),
                         reads=[R_pb[it["bank"]]], writes=[it["R_ptb"]])

                def b3(it):
                    g, t, vt, ts = it["g"], it["t"], it["vt"], it["ts"]
                    for r in range(4):
                        P.op("pe", _call("matmul", out=pb[6][0:qs, r * 65: r * 65 + 65], lhsT=it["ptb"][0:ts, r * qs:(r + 1) * qs],
                                         rhs=vb_aug[0:ts, (vt * 2 + g) * 65:(vt * 2 + g) * 65 + 65],
                                         start=(t == 0 and r == 0), stop=(t == nb - 1), skip_group_check=True),
                             reads=[it["R_ptb"], R_vb[vt]], writes=[R_pb[6]])
                    if t == nb - 1:
                        normalize(6, qs, mixt, R_m, 512 + g * 256, recB[st], R_recB[st])

                yield from pipe3(items, b1, b2, b3, 1, 0.8)
                P.dma("sp", mixD[blk * 128: blk * 128 + qs, :], mixt[0:qs, :], reads=[R_m], writes=[R_mixD[blk]], defer=True)
                yield

            load_xT(0, "dve")
            for r in range(16):
                if r + 1 < 16:
                    load_xT(r + 1, "dve")
                for _ in kside(r, full=(r >= 11)):
                    pass
            checkpoint('phase0')

            def prompt_front(sn, T):
                if T + 1 <= 31:
                    load_xT(T + 1)
                if T >= 16:
                    yield from kside(T, full=True)
                s = T % 2
                yield from qside(xTb[s], R_xT[s], 128, sn % 2, sn % 3)
                wins = [((T - 4 + t) % 6, 128) for t in range(5)]
                btiles = [(t, t * 128, 128) for t in range(T + 1)]
                yield from front_attn(sn, 128, wins, btiles, True, ABW)

            def prompt_bis(sn, T):
                btiles = [(t, t * 128, 128) for t in range(T + 1)]
                yield from bis_gen(sn, 128, btiles, BNW)

            def prompt_battn(sn, T, blk):
                btiles = [(t, t * 128, 128) for t in range(T + 1)]
                yield from battn_gen(sn, 128, btiles, blk, BNW)

            steps = [(0, 15, 16)] + [(1 + i, 16 + i, i) for i in range(16)]
            ns = len(steps)
            SN = ns
            sst = SN % 2
            s_wins = [(0, 128), (1, 128), (2, 128), (3, 128), (4, 16)]
            s_btiles = [(t, t * 128, 128) for t in range(16)] + [(16, 2048, 16)]
            xs_, R_xs = xTb[0], R_xT[0]

            def sample_front():
                stg, R_stg = score[sst], R_score[sst]
                P.dma("sp", stg[:, 0:2048], I["cbiT"][:, :], writes=[R_stg])
                P.op("act", _call("activation", out=kbi[:, 4096:4096 + 2048], in_=stg[:, 0:2048], func=AF.Copy),
                     reads=[R_stg], writes=R_ki[0:16])
                P.dma("sp", stg[:, 2048:4096], I["cakT"][:, :], writes=[R_stg])
                for s4 in range(4):
                    P.op("act", _call("activation", out=kaT[:, s4 * 512:(s4 + 1) * 512].rearrange("p (j t) -> p j t", t=128),
                                      in_=stg[:, 2048:4096].rearrange("p (j t) -> p j t", t=512)[:, :, s4 * 128:(s4 + 1) * 128], func=AF.Copy),
                         reads=[R_stg], writes=[R_ka[s4]])
                yield 3.0
                P.dma("sp", stg[:, 0:2048].rearrange("p (t c) -> p t c", c=512), I["cav"].rearrange("(t p) c -> p t c", p=128), writes=[R_stg])
                vaall = va_aug[:, 0:4 * 520].rearrange("p (t d) -> p t d", d=65)
                P.op("act", _call("activation", out=vaall[:, :, 0:64], in_=stg[:, 0:2048].rearrange("p (t d) -> p t d", d=64), func=AF.Copy),
                     reads=[R_stg], writes=R_va[0:5])
                P.op("pool", _call("memset", va_aug[:, 0:5 * 520].rearrange("p (t d) -> p t d", d=65)[:, :, 64:65], 1.0), writes=R_va[0:5])
                for hh in range(2):
                    w = 4 * 528
                    P.dma("sp", stg[0:16, 0:w], I["ABs"][:, hh * w:(hh + 1) * w], writes=[R_stg])
                    P.op("act", _call("activation", out=ABb[0:16, hh * w:(hh + 1) * w], in_=stg[0:16, 0:w], func=AF.Copy),
                         reads=[R_stg], writes=[R_AB])
                P.op("pool", _call("memset", qaz[sst][:, :], 0.0), writes=[R_qa[sst]])
                P.op("pool", _call("memset", qbz[SN % 3][:, :], 0.0), writes=[R_qb[SN % 3]])
                P.op("pool", _call("memset", qiz[sst][:, :], 0.0), writes=[R_qi[sst]])
                P.dma("sp", xstg[:, 0:128], I["xsT"][:, :], writes=[R_xstg])
                P.op("pool", _call("tensor_copy", out=xTb[0][:, 0:128], in_=xstg[:, 0:128]), reads=[R_xstg], writes=[R_xT[0]])
                yield 3.0
                bk = wrot.next()
                fm_proj(bk, xs_, R_xs, 16, C_KI, 1)
                P.op("act", _call("activation", out=kbi[:, 4096 + 2048:4096 + 2064], in_=pb[bk][:, 0:16], func=AF.Copy), reads=[R_pb[bk]], writes=[R_ki[16]])
                P.op("dve", _call("tensor_copy", out=ostg[0][:, 16:32], in_=pb[bk][:, 0:16]), reads=[R_pb[bk]], writes=[R_ostg[0]])
                P.dma("sp", O["sbiT"][:, :], ostg[0][0:64, 16:32], reads=[R_ostg[0]], defer=True)
                ba = wrot.next()
                fm_proj(ba, xs_, R_xs, 16, C_KA, 4)
                P.op("act", _call("activation", out=kaT[:, 4 * 512:5 * 512].rearrange("p (j t) -> p j t", t=128)[:, :, 0:16],
                                  in_=pb[ba][:, 0:64].rearrange("p (j t) -> p j t", t=16), func=AF.Copy),
                     reads=[R_pb[ba]], writes=[R_ka[4]])
                P.op("dve", _call("tensor_copy", out=astg[:, 0:64], in_=pb[ba][:, 0:64]), reads=[R_pb[ba]], writes=[R_astg])
                P.dma("sp", O["sakT"][:, :], astg[:, 0:64], reads=[R_astg], defer=True)
                bva = wrot.next()
                tm_proj(bva, xs_, R_xs, 16, C_VA, 512)
                vav = va_aug[0:16, 4 * 520:5 * 520].rearrange("p (h d) -> p h d", d=65)
                P.op("act", _call("activation", out=vav[:, :, 0:64], in_=pb[bva][0:16, :].rearrange("p (h d) -> p h d", d=64), func=AF.Copy),
                     reads=[R_pb[bva]], writes=[R_va[4]])
                P.op("dve", _call("tensor_copy", out=astg[0:16, 512:1024], in_=pb[bva][0:16, :]), reads=[R_pb[bva]], writes=[R_astg])
                P.dma("sp", O["sav"][:, :], astg[0:16, 512:1024], reads=[R_astg], defer=True)
                yield 3.0
                yield from qside(xs_, R_xs, 16, sst, SN % 3)
                yield from front_attn(SN, 16, s_wins, s_btiles, False, 528)

            def sample_bis():
                stg, R_stg = score[1 - sst], R_score[1 - sst]
                P.dma("sp", stg[0:16, 0:8 * 144], I["Bns"][:, :], writes=[R_stg])
                for h in range(8):
                    P.op("dve", _call("tensor_scalar", out=Bnb[0:16, h * 144:(h + 1) * 144], in0=stg[0:16, h * 144:(h + 1) * 144],
                                      scalar1=c15[0:16, h:h + 1], scalar2=None, op0=ALU.subtract),
                         reads=[R_stg, R_cst], writes=[R_Bn])
                yield 1.0
                yield from bis_gen(SN, 16, s_btiles, 144)

            def sample_battn():
                stg, R_stg = score[1 - sst], R_score[1 - sst]
                P.dma("sp", stg[:, 0:2048], I["cbkT"][:, :], writes=[R_stg])
                P.op("act", _call("activation", out=kbi[:, 0:2048], in_=stg[:, 0:2048], func=AF.Copy),
                     reads=[R_stg], writes=R_kbi[0:16])
                P.dma("sp", stg[:, 2048:4096].rearrange("p (t c) -> p t c", c=128), I["cbv"].rearrange("(t p) c -> p t c", p=128), writes=[R_stg])
                vball = vb_aug[:, 0:16 * 130].rearrange("p (t d) -> p t d", d=65)
                P.op("act", _call("activation", out=vball[:, :, 0:64], in_=stg[:, 2048:4096].rearrange("p (t d) -> p t d", d=64), func=AF.Copy),
                     reads=[R_stg], writes=R_vb[0:17])
                P.op("pool", _call("memset", vb_aug[:, 0:17 * 130].rearrange("p (t d) -> p t d", d=65)[:, :, 64:65], 1.0), writes=R_vb[0:17])
                bk = wrot.next()
                fm_proj(bk, xs_, R_xs, 16, C_KB, 1)
                P.op("act", _call("activation", out=kbi[:, 2048:2064], in_=pb[bk][:, 0:16], func=AF.Copy), reads=[R_pb[bk]], writes=[R_kbi[16]])
                P.op("dve", _call("tensor_copy", out=ostg[1][:, 0:16], in_=pb[bk][:, 0:16]), reads=[R_pb[bk]], writes=[R_ostg[1]])
                P.dma("sp", O["sbkT"][:, :], ostg[1][:, 0:16], reads=[R_ostg[1]], defer=True)
                bv_ = wrot.next()
                tm_proj(bv_, xs_, R_xs, 16, C_VB, 128)
                vbv = vb_aug[0:16, 16 * 130:17 * 130].rearrange("p (g d) -> p g d", d=65)
                P.op("act", _call("activation", out=vbv[:, :, 0:64], in_=pb[bv_][0:16, 0:128].rearrange("p (g d) -> p g d", d=64), func=AF.Copy),
                     reads=[R_pb[bv_]], writes=[R_vb[16]])
                P.op("dve", _call("tensor_copy", out=vbstg[0][0:16, :], in_=pb[bv_][0:16, 0:128]), reads=[R_pb[bv_]], writes=[R_vbstg[0]])
                P.dma("sp", O["sbv"][:, :], vbstg[0][0:16, :], reads=[R_vbstg[0]], defer=True)
                yield 3.0
                yield from battn_gen(SN, 16, s_btiles, 17, 144)

            for tick in range(ns + 3):
                gens = []
                if 0 <= tick - 2 < ns:
                    gens.append(prompt_battn(*steps[tick - 2]))
                elif tick - 2 == ns:
                    gens.append(sample_battn())
                if 0 <= tick - 1 < ns:
                    gens.append(prompt_bis(*steps[tick - 1][0:2]))
                elif tick - 1 == ns:
                    gens.append(sample_bis())
                if tick < ns:
                    gens.append(prompt_front(*steps[tick][0:2]))
                elif tick == ns:
                    gens.append(sample_front())
                run_interleaved(gens)
            checkpoint('steps')
            checkpoint('phaseA')
            P.flush(block)

        P.barrier()
        with ExitStack() as sbk:
            wob = sb(sbk, "wob", [128, 8 * 1024], BF16)
            wmqb = sb(sbk, "wmqb", [128, 8 * 512], BF16)
            wmob = sb(sbk, "wmob", [128, 4 * 1024], BF16)
            wtmp = sb(sbk, "wtmp", [128, 8 * 512], BF16)
            R_wo, R_wmq, R_wmo, R_wtmp = Res("wo"), Res("wmq"), Res("wmo"), Res("wtmp")
            wst = [sb(sbk, "wst%d" % k, [128, 2048], F32) for k in range(2)]
            R_wst = [Res("wst%d" % k) for k in range(2)]
            lnt = sb(sbk, "lnt", [128, 4 * 1024], F32)
            R_ln = Res("ln")
            memTb = sb(sbk, "memTb", [128, 8 * 256], BF16)
            R_memT = Res("memT")
            mkT = [sb(sbk, "mkT%d" % k, [128, 4 * 256], BF16) for k in range(2)]
            mva = [sb(sbk, "mva%d" % k, [128, 2 * 4 * 129], BF16) for k in range(2)]
            R_mk = [Res("mk%d" % k) for k in range(2)]
            R_mv = [Res("mv%d" % k) for k in range(2)]
            mixl = [sb(sbk, "mixl%d" % k, [128, 1024], BF16) for k in range(4)]
            R_mixl = [Res("mixl%d" % k) for k in range(4)]
            xr = [sb(sbk, "xr%d" % k, [128, 1024], F32) for k in range(4)]
            R_xr = [Res("xr%d" % k) for k in range(4)]
            NB3 = 4
            tT_l = [sb(sbk, "tT%d" % k, [128, 1024], BF16) for k in range(NB3)]
            hA_l = [sb(sbk, "hA%d" % k, [128, 1024], F32) for k in range(NB3)]
            hB_l = [sb(sbk, "hB%d" % k, [128, 1024], F32) for k in range(NB3)]
            h16_l = [sb(sbk, "h16%d" % k, [128, 1024], BF16) for k in range(NB3)]
            qmT_l = [sb(sbk, "qmT%d" % k, [128, 512], BF16) for k in range(NB3)]
            PTm_l = [sb(sbk, "PTm%d" % k, [128, 1024], BF16) for k in range(NB3)]
            o16_l = [sb(sbk, "o16%d" % k, [128, 512], BF16) for k in range(NB3)]
            oT_l = [sb(sbk, "oT%d" % k, [128, 512], BF16) for k in range(NB3)]
            stat_l = [sb(sbk, "stat%d" % k, [128, 32], F32) for k in range(NB3)]
            RB = [{n: Res(n + str(k)) for n in ("tT", "hA", "hB", "h16", "qm", "PTm", "o16", "oT", "stat")} for k in range(NB3)]
            h2T = [sb(sbk, "h2T%d" % k, [128, 1024], BF16) for k in range(4)]
            R_h2T = [Res("h2T%d" % k) for k in range(4)]
            mstg = sb(sbk, "mstg", [128, 1024], F32)
            R_mstg = Res("mstg")
            wrot = Rot([0, 1, 2, 3, 4, 5, 6, 7])

            def load_cast(dst, R_dst, src, ncols, engs=("act", "dve")):
                k = 0
                for c0 in range(0, ncols, 2048):
                    w = min(2048, ncols - c0)
                    s = k % 2
                    P.dma("sp", wst[s][:, 0:w], src[:, c0:c0 + w], writes=[R_wst[s]])
                    eng = engs[k % len(engs)]
                    if eng == "act":
                        P.op("act", _call("activation", out=dst[:, c0:c0 + w], in_=wst[s][:, 0:w], func=AF.Copy),
                             reads=[R_wst[s]], writes=[R_dst])
                    else:
                        P.op(eng, _call("tensor_copy", out=dst[:, c0:c0 + w], in_=wst[s][:, 0:w]),
                             reads=[R_wst[s]], writes=[R_dst])
                    k += 1

            load_cast(wob, R_wo, I["wo"], 8192)
            load_cast(wmqb, R_wmq, I["wmq"], 4096)
            load_cast(wmob, R_wmo, I["wmo"], 4096)
            for k in range(4):
                P.dma("sp", lnt[:, k * 1024:(k + 1) * 1024], I["lnp"][k:k + 1, :].to_broadcast([128, 1024]), writes=[R_ln])
            load_cast(memTb, R_memT, I["memT"], 2048)
            load_cast(wtmp, R_wtmp, I["wmk"], 4096)
            for h in range(4):
                bank = wrot.next()
                for kc in range(KC):
                    P.op("pe", _call("matmul",
                        out=pb[bank][:, 0:256], lhsT=wtmp[:, kc * 512 + h * 128: kc * 512 + (h + 1) * 128],
                        rhs=memTb[:, kc * 256:(kc + 1) * 256], start=(kc == 0), stop=(kc == KC - 1)),
                        reads=[R_wtmp, R_memT], writes=[R_pb[bank]])
                P.op("act", _call("activation", out=mkT[0][:, h * 256:(h + 1) * 256], in_=pb[bank][:, 0:256], func=AF.Copy),
                     reads=[R_pb[bank]], writes=[R_mk[0]])
                P.op("dve", _call("tensor_copy", out=mstg[:, h * 256:(h + 1) * 256], in_=pb[bank][:, 0:256]),
                     reads=[R_pb[bank]], writes=[R_mstg])
            P.dma("sp", O["mkT"][:, :], mstg[:, :], reads=[R_mstg], defer=True)
            load_cast(wtmp, R_wtmp, I["wmv"], 4096)
            for mt in range(2):
                bank = wrot.next()
                for kc in range(KC):
                    P.op("pe", _call("matmul",
                        out=pb[bank][:, 0:512], lhsT=memTb[:, kc * 256 + mt * 128: kc * 256 + (mt + 1) * 128],
                        rhs=wtmp[:, kc * 512:(kc + 1) * 512], start=(kc == 0), stop=(kc == KC - 1)),
                        reads=[R_wtmp, R_memT], writes=[R_pb[bank]])
                mvv = mva[0][:, mt * 516:(mt + 1) * 516].rearrange("p (h d) -> p h d", d=129)
                P.op("act", _call("activation", out=mvv[:, :, 0:128], in_=pb[bank][:, :].rearrange("p (h d) -> p h d", d=128), func=AF.Copy),
                     reads=[R_pb[bank]], writes=[R_mv[0]])
                P.op("dve", _call("tensor_copy", out=mstg[:, mt * 512:(mt + 1) * 512], in_=pb[bank][:, :]),
                     reads=[R_pb[bank]], writes=[R_mstg])
                P.dma("sp", O["mv"][mt * 128:(mt + 1) * 128, :], mstg[:, mt * 512:(mt + 1) * 512], reads=[R_mstg], defer=True)
            for k in range(2):
                P.op("pool", _call("memset", mva[k][:, :].rearrange("p (t d) -> p t d", d=129)[:, :, 128:129], 1.0), writes=[R_mv[k]])
            load_cast(mkT[1], R_mk[1], I["cmkT"], 1024)
            P.dma("sp", wst[0][:, 0:1024].rearrange("p (t c) -> p t c", c=512), I["cmv"].rearrange("(t p) c -> p t c", p=128), writes=[R_wst[0]])
            P.op("act", _call("activation", out=mva[1][:, :].rearrange("p (t d) -> p t d", d=129)[:, :, 0:128],
                                               in_=wst[0][:, 0:1024].rearrange("p (t d) -> p t d", d=128), func=AF.Copy),
                 reads=[R_wst[0]], writes=[R_mv[1]])

            checkpoint('phaseB_pre')
            def transpose_to(src16, R_src, qs, nchunk, dst, R_dst):
                bank = wrot.next()
                pbf = pb[bank][:, :].bitcast(BF16)
                for c in range(nchunk):
                    P.op("pe", _call("transpose", out=pbf[:, c * qs:(c + 1) * qs], in_=src16[0:qs, c * 128:(c + 1) * 128],
                                                                   identity=ident[0:qs, 0:qs]),
                         reads=[R_src, R_ident], writes=[R_pb[bank]])
                P.op("act", _call("activation", out=dst[:, 0:nchunk * qs], in_=pbf[:, 0:nchunk * qs], func=AF.Copy),
                     reads=[R_pb[bank]], writes=[R_dst])

            def layer_norm(hin, R_hin, qs, gcol, hout, R_hout, stat, R_stat):
                for c in range(2):
                    P.op("dve", _call("bn_stats", out=stat[0:qs, c * 6:(c + 1) * 6], in_=hin[0:qs, c * 512:(c + 1) * 512]),
                         reads=[R_hin], writes=[R_stat])
                P.op("dve", _call("bn_aggr", out=stat[0:qs, 12:14], in_=stat[0:qs, 0:12]), reads=[R_stat], writes=[R_stat])
                P.op("dve", _call("tensor_scalar", out=stat[0:qs, 14:15], in0=stat[0:qs, 13:14], scalar1=LN_EPS, scalar2=None, op0=ALU.add),
                     reads=[R_stat], writes=[R_stat])
                P.op("act", _call("activation", out=stat[0:qs, 15:16], in_=stat[0:qs, 14:15], func=AF.Sqrt), reads=[R_stat], writes=[R_stat])
                P.op("dve", _call("reciprocal", out=stat[0:qs, 16:17], in_=stat[0:qs, 15:16]), reads=[R_stat], writes=[R_stat])
                P.op("dve", _call("tensor_scalar", out=hout[0:qs, :], in0=hin[0:qs, :], scalar1=stat[0:qs, 12:13], scalar2=stat[0:qs, 16:17],
                                  op0=ALU.subtract, op1=ALU.mult),
                     reads=[R_hin, R_stat], writes=[R_hout])
                P.op("dve", _call("tensor_tensor", out=hout[0:qs, :], in0=hout[0:qs, :], in1=lnt[0:qs, gcol * 1024:(gcol + 1) * 1024], op=ALU.mult),
                     reads=[R_hout, R_ln], writes=[R_hout])
                P.op("dve", _call("tensor_tensor", out=hout[0:qs, :], in0=hout[0:qs, :], in1=lnt[0:qs, (gcol + 1) * 1024:(gcol + 2) * 1024], op=ALU.add),
                     reads=[R_hout, R_ln], writes=[R_hout])

            def phaseB_block(blk, qs, row0, mi, k2):
                s = k2
                tT, hA, hB, h16, qmT, PTm, o16, oT, stat = (tT_l[k2], hA_l[k2], hB_l[k2], h16_l[k2], qmT_l[k2], PTm_l[k2], o16_l[k2],
                                                             oT_l[k2], stat_l[k2])
                R_tT, R_hA, R_hB, R_h16, R_qm, R_PTm, R_o16, R_oT, R_stat = (RB[k2][n] for n in ("tT", "hA", "hB", "h16", "qm", "PTm", "o16", "oT", "stat"))
                P.dma("sp", mixl[s][0:qs, :], mixD[blk * 128 + row0: blk * 128 + row0 + qs, :], reads=[R_mixD[blk]], writes=[R_mixl[s]])
                P.dma("sp", xr[s][0:qs, :], I["xres"][blk * 128: blk * 128 + qs, :], writes=[R_xr[s]])
                transpose_to(mixl[s], R_mixl[s], qs, 8, tT, R_tT)
                yield
                b0, b1 = wrot.next(), wrot.next()
                for n, bank in enumerate((b0, b1)):
                    for kc in range(KC):
                        P.op("pe", _call("matmul",
                            out=pb[bank][0:qs, :], lhsT=tT[:, kc * qs:(kc + 1) * qs], rhs=wob[:, kc * 1024 + n * 512: kc * 1024 + (n + 1) * 512],
                            start=(kc == 0), stop=(kc == KC - 1)),
                            reads=[R_tT, R_wo], writes=[R_pb[bank]])
                    P.op("dve", _call("scalar_tensor_tensor",
                        out=hA[0:qs, n * 512:(n + 1) * 512], in0=xr[s][0:qs, n * 512:(n + 1) * 512], scalar=ALPHA, in1=pb[bank][0:qs, :],
                        op0=ALU.mult, op1=ALU.add),
                        reads=[R_xr[s], R_pb[bank]], writes=[R_hA])
                yield
                layer_norm(hA, R_hA, qs, 0, hB, R_hB, stat, R_stat)
                yield
                P.op("act", _call("activation", out=h16[0:qs, :], in_=hB[0:qs, :], func=AF.Copy), reads=[R_hB], writes=[R_h16])
                transpose_to(h16, R_h16, qs, 8, tT, R_tT)
                yield
                bq = wrot.next()
                for h in range(4):
                    for kc in range(KC):
                        P.op("pe", _call("matmul",
                            out=pb[bq][:, h * qs:(h + 1) * qs], lhsT=wmqb[:, kc * 512 + h * 128: kc * 512 + (h + 1) * 128],
                            rhs=tT[:, kc * qs:(kc + 1) * qs], start=(kc == 0), stop=(kc == KC - 1)),
                            reads=[R_wmq, R_tT], writes=[R_pb[bq]])
                P.op("act", _call("activation", out=qmT[:, 0:4 * qs], in_=pb[bq][:, 0:4 * qs], func=AF.Copy, scale=float(128.0 ** -0.5)),
                     reads=[R_pb[bq]], writes=[R_qm])
                yield
                bs0, bs1 = wrot.next(), wrot.next()
                for h in range(4):
                    for mt in range(2):
                        idx = h * 2 + mt
                        bank = bs0 if idx < 4 else bs1
                        c0 = (idx % 4) * qs
                        P.op("pe", _call("matmul",
                            out=pb[bank][:, c0:c0 + qs], lhsT=mkT[mi][:, h * 256 + mt * 128: h * 256 + (mt + 1) * 128],
                            rhs=qmT[:, h * qs:(h + 1) * qs], start=True, stop=True),
                            reads=[R_mk[mi], R_qm], writes=[R_pb[bank]])
                for k, bank in enumerate((bs0, bs1)):
                    P.op("act", _call("activation", out=PTm[:, k * 4 * qs:(k + 1) * 4 * qs], in_=pb[bank][:, 0:4 * qs], func=AF.Exp),
                         reads=[R_pb[bank]], writes=[R_PTm])
                yield
                bo0, bo1 = wrot.next(), wrot.next()
                for h in range(4):
                    bank = bo0 if h < 2 else bo1
                    for mt in range(2):
                        idx = h * 2 + mt
                        P.op("pe", _call("matmul",
                            out=pb[bank][0:qs, (h % 2) * 129:(h % 2) * 129 + 129], lhsT=PTm[:, idx * qs:(idx + 1) * qs],
                            rhs=mva[mi][:, (mt * 4 + h) * 129:(mt * 4 + h) * 129 + 129],
                            start=(h % 2 == 0 and mt == 0), stop=(mt == 1), skip_group_check=True),
                            reads=[R_PTm, R_mv[mi]], writes=[R_pb[bank]])
                for k, bank in enumerate((bo0, bo1)):
                    ov = pb[bank][0:qs, 0:258].rearrange("p (h d) -> p h d", d=129)
                    P.op("dve", _call("tensor_scalar", out=stat[0:qs, 20 + 2 * k:22 + 2 * k].rearrange("p (h o) -> p h o", o=1),
                                                                      in0=ov[:, :, 128:129], scalar1=1e-30, scalar2=None, op0=ALU.max),
                         reads=[R_pb[bank]], writes=[R_stat])
                    P.op("dve", _call("reciprocal", out=stat[0:qs, 20 + 2 * k:22 + 2 * k], in_=stat[0:qs, 20 + 2 * k:22 + 2 * k]),
                         reads=[R_stat], writes=[R_stat])
                    for hh in range(2):
                        h = k * 2 + hh
                        P.op("dve", _call("tensor_scalar",
                            out=o16[0:qs, h * 128:(h + 1) * 128], in0=pb[bank][0:qs, hh * 129: hh * 129 + 128],
                            scalar1=stat[0:qs, 20 + 2 * k + hh:21 + 2 * k + hh], scalar2=None, op0=ALU.mult),
                            reads=[R_pb[bank], R_stat], writes=[R_o16])
                yield
                transpose_to(o16, R_o16, qs, 4, oT, R_oT)
                yield
                b0, b1 = wrot.next(), wrot.next()
                for n, bank in enumerate((b0, b1)):
                    for c in range(4):
                        P.op("pe", _call("matmul",
                            out=pb[bank][0:qs, :], lhsT=oT[:, c * qs:(c + 1) * qs], rhs=wmob[:, c * 1024 + n * 512: c * 1024 + (n + 1) * 512],
                            start=(c == 0), stop=(c == 3)),
                            reads=[R_oT, R_wmo], writes=[R_pb[bank]])
                    P.op("dve", _call("scalar_tensor_tensor",
                        out=hA[0:qs, n * 512:(n + 1) * 512], in0=hB[0:qs, n * 512:(n + 1) * 512], scalar=ALPHA, in1=pb[bank][0:qs, :],
                        op0=ALU.mult, op1=ALU.add),
                        reads=[R_hB, R_pb[bank]], writes=[R_hA])
                yield
                layer_norm(hA, R_hA, qs, 2, hB, R_hB, stat, R_stat)
                yield
                P.dma("sp", h2D[blk * 128: blk * 128 + qs, :], hB[0:qs, :], reads=[R_hB], writes=[R_h2D[blk]], defer=True)
                P.op("act", _call("activation", out=h16[0:qs, :], in_=hB[0:qs, :], func=AF.Copy), reads=[R_hB], writes=[R_h16])
                transpose_to(h16, R_h16, qs, 8, h2T[s], R_h2T[s])
                P.dma("sp", h2TD[blk][:, 0:8 * qs], h2T[s][:, 0:8 * qs], reads=[R_h2T[s]], writes=[R_h2TD[blk]], defer=True)
                yield

            def run_staggered(gens, lag):
                active = []
                pending = list(gens)
                tick = 0
                while active or pending:
                    if pending and (not active or tick >= lag):
                        active.append(pending.pop(0))
                        tick = 0
                    for g in list(active):
                        try:
                            next(g)
                        except StopIteration:
                            active.remove(g)
                    tick += 1

            blocks = [(16, 2, 126, 0), (17, 16, 0, 1)] + [(i, 128, 0, 0) for i in range(16)]
            run_staggered([phaseB_block(b_, q_, r_, m_, pos % 4) for pos, (b_, q_, r_, m_) in enumerate(blocks)], 3)
            checkpoint('phaseB')
            P.flush(block)

        P.barrier()
        with ExitStack() as sc:
            wdb = sb(sc, "wdb", [128, NFC * 1024], BF16)
            R_wd = Res("wd")
            wst = [sb(sc, "wstc%d" % k, [128, 2048], F32) for k in range(2)]
            R_wst = [Res("wstc%d" % k) for k in range(2)]
            wsl = [sb(sc, "wsl%d" % k, [128, 2048], BF16) for k in range(2)]
            R_wsl = [Res("wsl%d" % k) for k in range(2)]
            R_wslB = [Res("wslB%d" % k) for k in range(2)]
            hT2 = [sb(sc, "hT%d" % k, [128, NFC * 512], BF16) for k in range(2)]
            R_hT2 = [Res("hT%d" % k) for k in range(2)]
            hTm = sb(sc, "hTm", [128, NFC * 16], BF16)
            R_hTm = Res("hTm")
            h2Tg = [sb(sc, "h2Tg%d" % k, [128, 8 * 512], BF16) for k in range(2)]
            R_h2Tg = [Res("h2Tg%d" % k) for k in range(2)]
            h2Tm = sb(sc, "h2Tm", [128, 8 * 18], BF16)
            R_h2Tm = Res("h2Tm")
            Gb = [sb(sc, "Gb%d" % k, [128, 532], F32) for k in range(3)]
            R_Gb = [Res("Gb%d" % k) for k in range(3)]
            Gs = sb(sc, "Gs", [128, 18], F32)
            R_Gs = Res("Gs")
            t0b = [sb(sc, "t0b%d" % k, [128, 530], F32) for k in range(3)]
            R_t0 = [Res("t0%d" % k) for k in range(3)]
            geb = [sb(sc, "geb%d" % k, [128, 530], F32) for k in range(3)]
            R_ge = [Res("ge%d" % k) for k in range(3)]
            t1b = [sb(sc, "t1b%d" % k, [128, 530], F32) for k in range(3)]
            R_t1b = [Res("t1b%d" % k) for k in range(3)]
            t2b = [sb(sc, "t2b%d" % k, [128, 530], F32) for k in range(3)]
            R_t2b = [Res("t2b%d" % k) for k in range(3)]
            t0s = sb(sc, "t0s", [128, 16], F32)
            ges = sb(sc, "ges", [128, 16], F32)
            R_ts = Res("ts")
            carry = sb(sc, "carry", [128, NFC * 2], F32)
            R_carry = [Res("carry%d" % c) for c in range(NFC)]
            sfc = sb(sc, "sfc", [128, NFC * 2], F32)
            R_sfc = Res("sfc")
            sconv = sb(sc, "sconv", [128, NFC * 2], F32)
            wconv = sb(sc, "wconv", [128, NFC * 3], F32)
            bconv = sb(sc, "bconv", [128, NFC], F32)
            flag = sb(sc, "flag", [128, 1], F32)
            R_cc = Res("cc")
            ln3 = sb(sc, "ln3", [128, 2 * 1024], F32)
            R_ln3 = Res("ln3")
            h2r = [sb(sc, "h2r%d" % k, [128, 1024], F32) for k in range(2)]
            R_h2r = [Res("h2r%d" % k) for k in range(2)]
            yA = sb(sc, "yA", [128, 1024], F32)
            R_yA = Res("yA")
            yB = [sb(sc, "yB%d" % k, [128, 1024], F32) for k in range(2)]
            R_yB = [Res("yB%d" % k) for k in range(2)]
            stat = sb(sc, "statc", [128, 32], F32)
            R_stat = Res("statc")

            P.dma("sp", sconv[:, :], I["sconvT"][:, :], writes=[R_cc])
            P.dma("sp", wconv[:, :], I["wconvT"][:, :], writes=[R_cc])
            P.dma("sp", bconv[:, :], I["bconvT"][:, :], writes=[R_cc])
            P.dma("sp", flag[:, :], I["flag"][:, :], writes=[R_cc])
            for k in range(2):
                P.dma("sp", ln3[:, k * 1024:(k + 1) * 1024], I["lnp"][4 + k:5 + k, :].to_broadcast([128, 1024]), writes=[R_ln3])
            def wdown_piece(j):
                kq = j % 2
                P.dma("sp", yB[kq][:, :], I["wdown"][:, j * 1024:(j + 1) * 1024], writes=[R_yB[kq]])
                P.op("dve", _call("tensor_copy", out=wdb[:, j * 1024:(j + 1) * 1024], in_=yB[kq][:, :]), reads=[R_yB[kq]], writes=[R_wd])

            P.dma("sp", h2Tm[:, :].rearrange("p (c q) -> p c q", q=18)[:, :, 0:2], h2TD[16][:, 0:16].rearrange("p (c q) -> p c q", q=2),
                  reads=[R_h2TD[16]], writes=[R_h2Tm], slow=True)
            P.dma("sp", h2Tm[:, :].rearrange("p (c q) -> p c q", q=18)[:, :, 2:18], h2TD[17][:, 0:128].rearrange("p (c q) -> p c q", q=16),
                  reads=[R_h2TD[17]], writes=[R_h2Tm], slow=True)

            checkpoint('phaseC_pre')
            UB = [0, 2, 4]
            GBK = [1, 3, 5]
            MB = 7
            YB = [6, 7]
            wk = [0]

            def ln3_out(pre_banks, qs, h2src, R_h2src, dst_ap, ys, R_ys):
                for n, bank in enumerate(pre_banks):
                    P.op("dve", _call("scalar_tensor_tensor",
                        out=yA[0:qs, n * 512:(n + 1) * 512], in0=h2src[0:qs, n * 512:(n + 1) * 512], scalar=ALPHA, in1=pb[bank][0:qs, :],
                        op0=ALU.mult, op1=ALU.add),
                        reads=[R_h2src, R_pb[bank]], writes=[R_yA])
                for c in range(2):
                    P.op("dve", _call("bn_stats", out=stat[0:qs, c * 6:(c + 1) * 6], in_=yA[0:qs, c * 512:(c + 1) * 512]),
                         reads=[R_yA], writes=[R_stat])
                P.op("dve", _call("bn_aggr", out=stat[0:qs, 12:14], in_=stat[0:qs, 0:12]), reads=[R_stat], writes=[R_stat])
                P.op("dve", _call("tensor_scalar", out=stat[0:qs, 14:15], in0=stat[0:qs, 13:14], scalar1=LN_EPS, scalar2=None, op0=ALU.add),
                     reads=[R_stat], writes=[R_stat])
                P.op("act", _call("activation", out=stat[0:qs, 15:16], in_=stat[0:qs, 14:15], func=AF.Sqrt), reads=[R_stat], writes=[R_stat])
                P.op("dve", _call("reciprocal", out=stat[0:qs, 16:17], in_=stat[0:qs, 15:16]), reads=[R_stat], writes=[R_stat])
                P.op("dve", _call("tensor_scalar", out=ys[0:qs, :], in0=yA[0:qs, :], scalar1=stat[0:qs, 12:13], scalar2=stat[0:qs, 16:17],
                                  op0=ALU.subtract, op1=ALU.mult),
                     reads=[R_yA, R_stat], writes=[R_ys])
                P.op("dve", _call("tensor_tensor", out=ys[0:qs, :], in0=ys[0:qs, :], in1=ln3[0:qs, 0:1024], op=ALU.mult),
                     reads=[R_ys, R_ln3], writes=[R_ys])
                P.op("dve", _call("tensor_tensor", out=ys[0:qs, :], in0=ys[0:qs, :], in1=ln3[0:qs, 1024:2048], op=ALU.add),
                     reads=[R_ys, R_ln3], writes=[R_ys])
                P.dma("sp", dst_ap, ys[0:qs, :], reads=[R_ys], defer=True)

            def load_h2Tg(grp):
                gs = grp % 2
                for bi in range(4):
                    blk = grp * 4 + bi
                    P.dma("sp", h2Tg[gs][:, :].rearrange("p (c q) -> p c q", q=512)[:, :, bi * 128:(bi + 1) * 128],
                          h2TD[blk][:, :].rearrange("p (c q) -> p c q", q=128), reads=[R_h2TD[blk]], writes=[R_h2Tg[gs]])

            def c_s1(grp, c):
                s = (grp * NFC + c) % 2
                P.dma("sp", wst[s][:, :], I["wup"][c], writes=[R_wst[s]])
                P.op("dve", _call("tensor_copy", out=wsl[s][:, 0:1024], in_=wst[s][:, 0:1024]), reads=[R_wst[s]], writes=[R_wsl[s]])
                P.op("dve", _call("tensor_copy", out=wsl[s][:, 1024:2048], in_=wst[s][:, 1024:2048]), reads=[R_wst[s]], writes=[R_wslB[s]])

            def c_s2(grp, c):
                s = (grp * NFC + c) % 2
                gs = grp % 2
                mo = (c % 3) * 64
                if grp == 0:
                    for part, oc in ((0, mo), (1, mo + 32)):
                        for kc in range(KC):
                            P.op("pe", _call("matmul", out=pb[MB][:, oc:oc + 18], lhsT=wsl[s][:, kc * 256 + part * 128: kc * 256 + (part + 1) * 128],
                                             rhs=h2Tm[:, kc * 18:(kc + 1) * 18], start=(kc == 0), stop=(kc == KC - 1)),
                                 reads=[R_wsl[s], R_wslB[s], R_h2Tm], writes=[R_pb[MB]])
                k3 = (grp * NFC + c) % 3
                ub, gbk = UB[k3], GBK[k3]
                for part, bank in ((0, ub), (1, gbk)):
                    for kc in range(KC):
                        P.op("pe", _call("matmul", out=pb[bank][:, :], lhsT=wsl[s][:, kc * 256 + part * 128: kc * 256 + (part + 1) * 128],
                                         rhs=h2Tg[gs][:, kc * 512:(kc + 1) * 512], start=(kc == 0), stop=(kc == KC - 1)),
                             reads=[R_wsl[s], R_wslB[s], R_h2Tg[gs]], writes=[R_pb[bank]])

            def c_s3(grp, c):
                hTg, R_hTg = hT2[grp % 2], R_hT2[grp % 2]
                mo = (c % 3) * 64
                k3 = (grp * NFC + c) % 3
                ub, gbk = UB[k3], GBK[k3]
                G, R_G = Gb[k3], R_Gb[k3]
                t0, R_t = t0b[k3], R_t0[k3]
                ge, R_g = geb[k3], R_ge[k3]
                t1, R_t1 = t1b[k3], R_t1b[k3]
                t2, R_t2 = t2b[k3], R_t2b[k3]
                W = 530 if grp == 0 else 512
                if grp == 0:
                    P.op("dve", _call("tensor_scalar", out=carry[:, c * 2:(c + 1) * 2], in0=pb[MB][:, mo + 32:mo + 34], scalar1=flag[:, 0:1],
                                      scalar2=None, op0=ALU.mult),
                         reads=[R_pb[MB], R_cc], writes=[R_carry[c]])
                P.op("act", _call("activation", out=G[:, 0:2], in_=carry[:, c * 2:(c + 1) * 2], func=AF.Copy),
                     reads=[R_carry[c]], writes=[R_G])
                P.op("act", _call("activation", out=G[:, 2:514], in_=pb[gbk][:, :], func=AF.Copy), reads=[R_pb[gbk]], writes=[R_G])
                if grp == 0:
                    P.op("act", _call("activation", out=G[:, 514:516], in_=sconv[:, c * 2:(c + 1) * 2], func=AF.Copy), reads=[R_cc], writes=[R_G])
                    P.op("act", _call("activation", out=G[:, 516:532], in_=pb[MB][:, mo + 34:mo + 50], func=AF.Copy), reads=[R_pb[MB]], writes=[R_G])
                    P.op("act", _call("activation", out=sfc[:, c * 2:(c + 1) * 2], in_=G[:, 530:532], func=AF.Copy), reads=[R_G], writes=[R_sfc])
                P.op("act", _call("activation", out=carry[:, c * 2:(c + 1) * 2], in_=G[:, 512:514], func=AF.Copy),
                     reads=[R_G], writes=[R_carry[c]])
                P.op("act", _call("activation", out=t0[:, 0:W], in_=G[:, 2:2 + W], func=AF.Identity,
                                  scale=wconv[:, c * 3 + 2:c * 3 + 3], bias=bconv[:, c:c + 1]),
                     reads=[R_G, R_cc], writes=[R_t])
                P.op("act", _call("activation", out=t1[:, 0:W], in_=G[:, 1:1 + W], func=AF.Identity, scale=wconv[:, c * 3 + 1:c * 3 + 2]),
                     reads=[R_G, R_cc], writes=[R_t1])
                P.op("act", _call("activation", out=t2[:, 0:W], in_=G[:, 0:W], func=AF.Identity, scale=wconv[:, c * 3:c * 3 + 1]),
                     reads=[R_G, R_cc], writes=[R_t2])
                P.op("dve", _call("tensor_tensor", out=t0[:, 0:W], in0=t0[:, 0:W], in1=t1[:, 0:W], op=ALU.add), reads=[R_t, R_t1], writes=[R_t])
                P.op("dve", _call("tensor_tensor", out=t0[:, 0:W], in0=t0[:, 0:W], in1=t2[:, 0:W], op=ALU.add), reads=[R_t, R_t2], writes=[R_t])

            def c_s3b(grp, c):
                hTg, R_hTg = hT2[grp % 2], R_hT2[grp % 2]
                mo = (c % 3) * 64
                k3 = (grp * NFC + c) % 3
                ub = UB[k3]
                t0, R_t = t0b[k3], R_t0[k3]
                ge, R_g = geb[k3], R_ge[k3]
                W = 530 if grp == 0 else 512
                P.op("act", _call("activation", out=ge[:, 0:W], in_=t0[:, 0:W], func=AF.Gelu_apprx_tanh), reads=[R_t], writes=[R_g])
                P.op("dve", _call("tensor_tensor", out=hTg[:, c * 512:(c + 1) * 512], in0=pb[ub][:, :], in1=ge[:, 0:512], op=ALU.mult),
                     reads=[R_pb[ub], R_g], writes=[R_hTg])
                if grp == 0:
                    P.op("dve", _call("tensor_tensor", out=hTm[:, c * 16:(c + 1) * 16], in0=pb[MB][:, mo + 2:mo + 18], in1=ge[:, 514:530], op=ALU.mult),
                         reads=[R_pb[MB], R_g], writes=[R_hTm])

            def c_down(grp):
                hTg, R_hTg = hT2[grp % 2], R_hT2[grp % 2]
                if grp == 0:
                    for n, bank in enumerate(YB):
                        for c in range(NFC):
                            P.op("pe", _call("matmul", out=pb[bank][0:16, :], lhsT=hTm[:, c * 16:(c + 1) * 16],
                                             rhs=wdb[:, c * 1024 + n * 512: c * 1024 + (n + 1) * 512], start=(c == 0), stop=(c == NFC - 1)),
                                 reads=[R_hTm, R_wd], writes=[R_pb[bank]])
                    P.dma("sp", h2r[0][0:16, :], h2D[17 * 128: 17 * 128 + 16, :], reads=[R_h2D[17]], writes=[R_h2r[0]])
                    ln3_out(YB, 16, h2r[0], R_h2r[0], O["ys"][:, :], yB[0], R_yB[0])
                    P.dma("sp", O["sfcT"][:, :], sfc[:, :], reads=[R_sfc], defer=True)
                for bi in range(4):
                    blk = grp * 4 + bi
                    hs = blk % 2
                    P.dma("sp", h2r[hs][:, :], h2D[blk * 128:(blk + 1) * 128, :], reads=[R_h2D[blk]], writes=[R_h2r[hs]])
                    for n, bank in enumerate(YB):
                        for c in range(NFC):
                            P.op("pe", _call("matmul", out=pb[bank][:, :], lhsT=hTg[:, c * 512 + bi * 128: c * 512 + (bi + 1) * 128],
                                             rhs=wdb[:, c * 1024 + n * 512: c * 1024 + (n + 1) * 512], start=(c == 0), stop=(c == NFC - 1)),
                                 reads=[R_hTg, R_wd], writes=[R_pb[bank]])
                    ln3_out(YB, 128, h2r[hs], R_h2r[hs], O["y"][blk * 128:(blk + 1) * 128, :], yB[hs], R_yB[hs])

            seq = [(grp, c) for grp in range(4) for c in range(NFC)]
            nseq = len(seq)
            load_h2Tg(0)
            load_h2Tg(1)
            for idx in range(nseq + 3):
                if 1 <= idx <= NFC:
                    wdown_piece(idx - 1)
                if idx < nseq:
                    c_s1(*seq[idx])
                if 1 <= idx <= nseq:
                    c_s2(*seq[idx - 1])
                if 3 <= idx:
                    g4, c4 = seq[idx - 3]
                    c_s3b(g4, c4)
                    if c4 == NFC - 1:
                        c_down(g4)
                        if g4 + 2 < 4:
                            load_h2Tg(g4 + 2)
                if 2 <= idx <= nseq + 1:
                    c_s3(*seq[idx - 2])
            P.dma("sp", O["fcT"][:, :], carry[:, :], reads=R_carry, defer=True)
            P.finish()
            P.flush(block)
    return nc


def _t5_bucket(rel):
    half, max_exact = 16, 8
    n = np.abs(rel)
    log_ratio = np.log(np.maximum(n, 1).astype(np.float32) / max_exact) / math.log(128 / max_exact)
    large = np.minimum(max_exact + (log_ratio * (half - max_exact)).astype(np.int32), half - 1)
    return np.where(rel < 0, half, 0) + np.where(n < max_exact, n, large)


def _host_inputs(inp):
    f32 = np.float32
    x_prompt = np.asarray(inp["x_prompt"], f32)
    x_sample = np.asarray(inp["x_sample"], f32)
    w_in = np.asarray(inp["w_in"], f32)[0]
    qa, ka, va = w_in[:, 0:512], w_in[:, 512:1024], w_in[:, 1024:1536]
    qb, kb, vb = w_in[:, 1536:2048], w_in[:, 2048:2176], w_in[:, 2176:2304]
    qi, ki, wi = w_in[:, 2304:2816], w_in[:, 2816:2880], w_in[:, 2880:2888]
    qbp = np.concatenate([np.concatenate([qb[:, r * 64:(r + 1) * 64], qb[:, (4 + r) * 64:(5 + r) * 64]], axis=1) for r in range(4)], axis=1)
    winp = np.concatenate([qa, ka, qbp, kb, qi, ki, ki, va, vb, wi], axis=1)
    assert winp.shape[1] == NCOL

    def kc_layout(w):
        n = w.shape[1]
        return np.ascontiguousarray(w.reshape(8, 128, n).transpose(1, 0, 2).reshape(128, 8 * n))

    shared = {}
    shared["win"] = kc_layout(winp)
    shared["wo"] = kc_layout(np.asarray(inp["w_o"], f32)[0])
    shared["wmq"] = kc_layout(np.asarray(inp["w_mq"], f32)[0])
    shared["wmk"] = kc_layout(np.asarray(inp["w_mk"], f32)[0])
    shared["wmv"] = kc_layout(np.asarray(inp["w_mv"], f32)[0])
    wmo = np.asarray(inp["w_mo"], f32)[0]
    shared["wmo"] = np.ascontiguousarray(wmo.reshape(4, 128, 1024).transpose(1, 0, 2).reshape(128, 4096))
    w_up = np.asarray(inp["w_up"], f32)[0]
    wu = w_up[:, :DFF].reshape(8, 128, NFC, 128)
    wg = w_up[:, DFF:].reshape(8, 128, NFC, 128)
    wup = np.stack([wu, wg], axis=3)
    shared["wup"] = np.ascontiguousarray(wup.transpose(2, 1, 0, 3, 4).reshape(NFC, 128, 8 * 256))
    w_down = np.asarray(inp["w_down"], f32)[0]
    shared["wdown"] = np.ascontiguousarray(w_down.reshape(NFC, 128, 1024).transpose(1, 0, 2).reshape(128, NFC * 1024))
    shared["lnp"] = np.ascontiguousarray(np.stack([np.asarray(inp[k], f32)[0] for k in ("ln1_g", "ln1_b", "ln2_g", "ln2_b", "ln3_g", "ln3_b")]))
    w_conv = np.asarray(inp["w_conv"], f32)[0]
    shared["wconvT"] = np.ascontiguousarray(w_conv.reshape(3, NFC, 128).transpose(2, 1, 0).reshape(128, NFC * 3))
    shared["bconvT"] = np.ascontiguousarray(np.asarray(inp["b_conv"], f32)[0].reshape(NFC, 128).T)
    shared["ident"] = np.eye(128, dtype=f32)
    tabA = np.asarray(inp["a_rel_bias"], f32)[0]
    qq = np.arange(128)[:, None]
    kk = np.arange(640)[None, :]
    kpos = kk - 512
    rel = qq - kpos
    cq = qq // 64
    kch = np.floor_divide(kpos, 64)
    allowed = (kch >= cq - 8) & (kch <= cq)
    bias = tabA[np.clip(rel, -64, 64) + 64]
    AB = np.where(allowed[:, :, None], bias, f32(NEGM)).astype(f32)
    shared["AB"] = np.ascontiguousarray(AB.transpose(0, 2, 1).reshape(128, 8 * ABW))
    js = np.arange(16)[:, None]
    ks = np.arange(528)[None, :]
    ABs = tabA[np.clip(512 + js - ks, -64, 64) + 64]
    shared["ABs"] = np.ascontiguousarray(ABs.transpose(0, 2, 1).reshape(16, 8 * 528)).astype(f32)
    t5 = np.asarray(inp["t5_bias"], f32)
    relB = np.arange(128)[:, None] - np.arange(256)[None, :] + 128
    Bn = t5[_t5_bucket(relB)]
    shared["Bn"] = np.ascontiguousarray(Bn.transpose(0, 2, 1).reshape(128, 8 * BNW)).astype(f32)
    relBs = 128 + np.arange(16)[:, None] - np.arange(144)[None, :]
    Bns = t5[_t5_bucket(relBs)]
    shared["Bns"] = np.ascontiguousarray(Bns.transpose(0, 2, 1).reshape(16, 8 * 144)).astype(f32)
    shared["C15"] = np.ascontiguousarray(np.broadcast_to(t5[15][None, :], (128, 8))).astype(f32)
    dm = np.zeros((128, 128), f32)
    dm[0:64, 64:128] = NEGM
    shared["diagmask"] = dm

    mem_prompt = np.asarray(inp["mem_prompt"], f32)
    maps = []
    for c in range(8):
        b, half = c // 2, c % 2
        m = dict(shared)
        xk = np.zeros((4096, 1024), f32)
        if half == 1:
            xk[:] = x_prompt[b]
        else:
            xk[2048:] = x_prompt[b, :2048]
        m["xkT"] = np.ascontiguousarray(xk.reshape(32, 128, 8, 128).transpose(0, 3, 2, 1).reshape(32, 128, 1024))
        xs = x_sample[c]
        m["xsT"] = np.ascontiguousarray(xs.reshape(16, 8, 128).transpose(2, 1, 0).reshape(128, 128))
        xres = np.zeros((NBLK * 128, 1024), f32)
        xres[0:2048] = xk[2048:]
        xres[2048:2050] = xk[2046:2048]
        xres[17 * 128:17 * 128 + 16] = xs
        m["xres"] = xres
        m["memT"] = np.ascontiguousarray(mem_prompt[b].reshape(256, 8, 128).transpose(2, 1, 0).reshape(128, 2048))
        cmk = np.asarray(inp["cache_mem_k"], f32)[0, c]
        m["cmkT"] = np.ascontiguousarray(cmk.transpose(2, 1, 0).reshape(128, 1024))
        m["cmv"] = np.ascontiguousarray(np.asarray(inp["cache_mem_v"], f32)[0, c].reshape(256, 512))
        cak = np.asarray(inp["cache_a_k"], f32)[0, c]
        m["cakT"] = np.ascontiguousarray(cak.reshape(512, 4, 2, 64).transpose(2, 3, 1, 0).reshape(128, 2048))
        m["cav"] = np.ascontiguousarray(np.asarray(inp["cache_a_v"], f32)[0, c].reshape(512, 512))
        cbk = np.asarray(inp["cache_b_k"], f32)[0, c]
        m["cbkT"] = np.ascontiguousarray(cbk.reshape(2048, 128).T)
        m["cbv"] = np.ascontiguousarray(np.asarray(inp["cache_b_v"], f32)[0, c].reshape(2048, 128))
        cbi = np.asarray(inp["cache_b_kidx"], f32)[0, c]
        m["cbiT"] = np.ascontiguousarray(np.concatenate([cbi.T, cbi.T], axis=0))
        sc_ = np.asarray(inp["state_ffn_conv"], f32)[0, c]
        m["sconvT"] = np.ascontiguousarray(sc_.reshape(2, NFC, 128).transpose(2, 1, 0).reshape(128, NFC * 2))
        m["colmask"] = np.full((128, 1), NEGM if half == 0 else 0.0, f32)
        kv = np.ones((128, NT), f32)
        if half == 0:
            kv[:, 0:16] = 0.0
        m["kvalid"] = kv
        m["flag"] = np.full((128, 1), float(half), f32)
        maps.append(m)
    return maps


_NC_CACHE = {}


def _run(inputs, debug=False):
    key = bool(debug)
    if key not in _NC_CACHE:
        _NC_CACHE[key] = build_program(debug=debug)
    nc = _NC_CACHE[key]
    maps = _host_inputs(inputs)
    res = run_bass_kernel_spmd(nc, maps, core_ids=list(range(8)))
    return res.results


def kernel(**inputs):
    R = _run(inputs)
    f32 = np.float32
    y = np.zeros((4, 4096, 1024), f32)
    ys = np.zeros((8, 16, 1024), f32)
    pak = np.zeros((1, 4, 512, 8, 64), f32)
    pav = np.zeros((1, 4, 512, 8, 64), f32)
    pbk = np.zeros((1, 4, 4096, 2, 64), f32)
    pbv = np.zeros((1, 4, 4096, 2, 64), f32)
    pbi = np.zeros((1, 4, 4096, 64), f32)
    pmk = np.zeros((1, 4, 256, 4, 128), f32)
    pmv = np.zeros((1, 4, 256, 4, 128), f32)
    pfc = np.zeros((1, 4, 2, DFF), f32)
    sak = np.zeros((1, 8, 16, 8, 64), f32)
    sav = np.zeros((1, 8, 16, 8, 64), f32)
    sbk = np.zeros((1, 8, 16, 2, 64), f32)
    sbv = np.zeros((1, 8, 16, 2, 64), f32)
    sbi = np.zeros((1, 8, 16, 64), f32)
    sfc = np.zeros((1, 8, 2, DFF), f32)
    for c in range(8):
        b, half = c // 2, c % 2
        r = R[c]
        y[b, half * 2048:(half + 1) * 2048] = np.asarray(r["y"], f32)
        ys[c] = np.asarray(r["ys"], f32)
        if half == 1:
            akT = np.asarray(r["akT"], f32).reshape(2, 64, 4, 512)
            pak[0, b] = akT.transpose(3, 2, 0, 1).reshape(512, 8, 64)
            pav[0, b] = np.asarray(r["av"], f32).reshape(512, 8, 64)
            pbk[0, b] = np.asarray(r["bkT"], f32).T.reshape(4096, 2, 64)
            pbv[0, b] = np.asarray(r["bv"], f32).reshape(4096, 2, 64)
            pbi[0, b] = np.asarray(r["biT"], f32).T
            pmk[0, b] = np.asarray(r["mkT"], f32).reshape(128, 4, 256).transpose(2, 1, 0)
            pmv[0, b] = np.asarray(r["mv"], f32).reshape(256, 4, 128)
            pfc[0, b] = np.asarray(r["fcT"], f32).reshape(128, NFC, 2).transpose(2, 1, 0).reshape(2, DFF)
        sakT = np.asarray(r["sakT"], f32).reshape(2, 64, 4, 16)
        sak[0, c] = sakT.transpose(3, 2, 0, 1).reshape(16, 8, 64)
        sav[0, c] = np.asarray(r["sav"], f32).reshape(16, 8, 64)
        sbk[0, c] = np.asarray(r["sbkT"], f32).T.reshape(16, 2, 64)
        sbv[0, c] = np.asarray(r["sbv"], f32).reshape(16, 2, 64)
        sbi[0, c] = np.asarray(r["sbiT"], f32).T
        sfc[0, c] = np.asarray(r["sfcT"], f32).reshape(128, NFC, 2).transpose(2, 1, 0).reshape(2, DFF)
    return (y, ys, pak, pav, pbk, pbv, pbi, pmk, pmv, pfc, sak, sav, sbk, sbv, sbi, sfc)
```

```python
import math
from contextlib import ExitStack

import numpy as np
import concourse.bass as bass
import concourse.mybir as mybir
from concourse.bass_utils import run_bass_kernel_spmd

F32 = mybir.dt.float32
BF16 = mybir.dt.bfloat16
AF = mybir.ActivationFunctionType
ALU = mybir.AluOpType

D = 1024
KC = 8
NT = 32
NCOL = 2952
C_QA, C_KA, C_QB, C_KB, C_QI, C_KI, C_VA, C_VB, C_WI = 0, 512, 1024, 1536, 1664, 2176, 2304, 2816, 2944
DFF = 2816
NFC = 22
ALPHA = 2.0 ** 0.25
LN_EPS = 1e-5
NEGM = -30000.0
NIT = 16
BIS_W0 = 16.0
ABW = 640
BNW = 256
NBLK = 18


class Res:
    __slots__ = ("lw", "rd", "name", "excl")

    def __init__(self, name="", excl=False):
        self.lw = None
        self.rd = {}
        self.name = name
        self.excl = excl


def _call(name, *args, **kw):
    return lambda e: getattr(e, name)(*args, **kw)


class Prog:
    ENG = ("pe", "act", "dve", "pool", "sp")

    def __init__(self, nc, sems, dma_sems):
        self.nc = nc
        self.streams = {e: [] for e in self.ENG}
        self.sem = sems
        self.cnt = {e: 0 for e in self.ENG}
        self.seen = {e: {} for e in self.ENG}
        self.dsems = dma_sems
        self.dval = [0] * len(dma_sems)
        self.dnext = 0
        self.semh = dict(sems)
        for i, h in enumerate(dma_sems):
            self.semh[("d", i)] = h
        self.ninst = 0
        self.dead = False
        self.deferred = []
        self.defer_lag = 48

    def _deps(self, reads, writes, eng=None):
        d = {}
        for r in reads:
            if r.lw is not None:
                k, v = r.lw
                if d.get(k, 0) < v:
                    d[k] = v
            if r.excl:
                for k, v in r.rd.items():
                    if k != eng and d.get(k, 0) < v:
                        d[k] = v
        for w in writes:
            if w.lw is not None:
                k, v = w.lw
                if d.get(k, 0) < v:
                    d[k] = v
            for k, v in w.rd.items():
                if d.get(k, 0) < v:
                    d[k] = v
        return d

    def _wait(self, eng, deps):
        for k, v in deps.items():
            if k == "pe" and eng == "pe":
                continue
            if self.seen[eng].get(k, 0) >= v:
                continue
            self.seen[eng][k] = v
            h = self.semh[k]
            self.streams[eng].append(lambda e, h=h, v=v: e.wait_ge(h, v))

    def _flush_deferred(self, force=False, reads=(), writes=()):
        if not self.deferred:
            return
        conflict = force
        if not conflict:
            ws = set(id(w) for w in writes)
            rs = set(id(r) for r in reads)
            for d in self.deferred:
                dr = set(id(x) for x in d[3])
                dw = set(id(x) for x in d[4])
                if (ws & dr) or (ws & dw) or (rs & dw):
                    conflict = True
                    break
        if conflict:
            pend, self.deferred = self.deferred, []
            for d in pend:
                self._dma_now(d[0], d[1], d[2], d[3], d[4], d[5])
            return
        while self.deferred and self.ninst - self.deferred[0][6] >= self.defer_lag:
            d = self.deferred.pop(0)
            self._dma_now(d[0], d[1], d[2], d[3], d[4], d[5])

    def op(self, eng, fn, reads=(), writes=()):
        if self.dead:
            return
        self._flush_deferred(False, reads, writes)
        self._wait(eng, self._deps(reads, writes, eng))
        self.cnt[eng] += 1
        n = self.cnt[eng]
        h = self.sem[eng]
        self.streams[eng].append(lambda e, fn=fn, h=h: fn(e).then_inc(h, 1))
        self.ninst += 1
        for r in reads:
            if r.rd.get(eng, 0) < n:
                r.rd[eng] = n
        for w in writes:
            w.lw = (eng, n)
            w.rd = {}

    def dma(self, q, out, in_, reads=(), writes=(), slow=False, defer=False):
        if self.dead:
            return
        if defer:
            self._flush_deferred(False, reads, writes)
            self.deferred.append((q, out, in_, list(reads), list(writes), slow, self.ninst))
            return
        self._flush_deferred(False, reads, writes)
        self._dma_now(q, out, in_, reads, writes, slow)

    def _dma_now(self, q, out, in_, reads=(), writes=(), slow=False):
        deps = self._deps(reads, writes)
        i = self.dnext
        self.dnext = (i + 1) % len(self.dsems)
        k = ("d", i)
        if self.dval[i] > 0 and deps.get(k, 0) < self.dval[i]:
            deps[k] = self.dval[i]
        self._wait(q, deps)
        self.dval[i] += 16
        v = self.dval[i]
        h = self.dsems[i]
        if slow:
            self.streams[q].append(
                lambda e, out=out, in_=in_, h=h: e.dma_start(out=out, in_=in_, allow_slow_non_contiguous=True).then_inc(h, 16))
        else:
            self.streams[q].append(lambda e, out=out, in_=in_, h=h: e.dma_start(out=out, in_=in_).then_inc(h, 16))
        self.ninst += 1
        for r in reads:
            if r.rd.get(k, 0) < v:
                r.rd[k] = v
        for w in writes:
            w.lw = (k, v)
            w.rd = {}

    def barrier(self):
        if self.dead:
            return
        self._flush_deferred(True)
        deps = {e: self.cnt[e] for e in self.ENG if self.cnt[e] > 0}
        for i, v in enumerate(self.dval):
            if v > 0:
                deps[("d", i)] = v
        for e in self.ENG:
            self._wait(e, dict(deps))

    def finish(self):
        self._flush_deferred(True)
        deps = {("d", i): v for i, v in enumerate(self.dval) if v > 0}
        self._wait("sp", deps)

    def flush(self, block):
        self._flush_deferred(True)
        s = self.streams
        self.streams = {e: [] for e in self.ENG}

        def mk(lst):
            def body(e):
                for f in lst:
                    f(e)
            return body

        block.tensor(mk(s["pe"]))
        block.scalar(mk(s["act"]))
        block.vector(mk(s["dve"]))
        block.gpsimd(mk(s["pool"]))
        block.sync(mk(s["sp"]))


def build_program(debug=False, stop_at=None):
    nc = bass.Bass("TRN2", target_bir_lowering=False)

    def din(name, shape, dt=F32):
        return nc.dram_tensor(name, list(shape), dt, kind="ExternalInput").ap()

    def dout(name, shape, dt=F32):
        return nc.dram_tensor(name, list(shape), dt, kind="ExternalOutput").ap()

    def dscr(name, shape, dt):
        return nc.dram_tensor(name, list(shape), dt, kind="Internal").ap()

    I = {}
    I["xkT"] = din("xkT", [NT, 128, 1024])
    I["xsT"] = din("xsT", [128, 8 * 16])
    I["xres"] = din("xres", [NBLK * 128, 1024])
    I["win"] = din("win", [128, KC * NCOL])
    I["wo"] = din("wo", [128, 8 * 1024])
    I["wmq"] = din("wmq", [128, 8 * 512])
    I["wmk"] = din("wmk", [128, 8 * 512])
    I["wmv"] = din("wmv", [128, 8 * 512])
    I["wmo"] = din("wmo", [128, 4 * 1024])
    I["wup"] = din("wup", [NFC, 128, 8 * 256])
    I["wdown"] = din("wdown", [128, NFC * 1024])
    I["lnp"] = din("lnp", [6, 1024])
    I["wconvT"] = din("wconvT", [128, NFC * 3])
    I["bconvT"] = din("bconvT", [128, NFC])
    I["memT"] = din("memT", [128, 8 * 256])
    I["cmkT"] = din("cmkT", [128, 4 * 256])
    I["cmv"] = din("cmv", [256, 512])
    I["cakT"] = din("cakT", [128, 4 * 512])
    I["cav"] = din("cav", [512, 512])
    I["cbkT"] = din("cbkT", [128, 2048])
    I["cbv"] = din("cbv", [2048, 128])
    I["cbiT"] = din("cbiT", [128, 2048])
    I["sconvT"] = din("sconvT", [128, NFC * 2])
    I["ident"] = din("ident", [128, 128])
    I["AB"] = din("AB", [128, 8 * ABW])
    I["ABs"] = din("ABs", [16, 8 * 528])
    I["Bn"] = din("Bn", [128, 8 * BNW])
    I["Bns"] = din("Bns", [16, 8 * 144])
    I["C15"] = din("C15", [128, 8])
    I["colmask"] = din("colmask", [128, 1])
    I["diagmask"] = din("diagmask", [128, 128])
    I["kvalid"] = din("kvalid", [128, NT])
    I["flag"] = din("flag", [128, 1])

    O = {}
    O["y"] = dout("y", [2048, 1024])
    O["ys"] = dout("ys", [16, 1024])
    O["akT"] = dout("akT", [128, 4 * 512])
    O["av"] = dout("av", [512, 512])
    O["bkT"] = dout("bkT", [128, 4096])
    O["bv"] = dout("bv", [4096, 128])
    O["biT"] = dout("biT", [64, 4096])
    O["mkT"] = dout("mkT", [128, 4 * 256])
    O["mv"] = dout("mv", [256, 512])
    O["fcT"] = dout("fcT", [128, NFC * 2])
    O["sakT"] = dout("sakT", [128, 4 * 16])
    O["sav"] = dout("sav", [16, 512])
    O["sbkT"] = dout("sbkT", [128, 16])
    O["sbv"] = dout("sbv", [16, 128])
    O["sbiT"] = dout("sbiT", [64, 16])
    O["sfcT"] = dout("sfcT", [128, NFC * 2])
    if debug:
        O["dbg_mix"] = dout("dbg_mix", [NBLK * 128, 1024], BF16)
        O["dbg_h2"] = dout("dbg_h2", [NBLK * 128, 1024])
        mixD = O["dbg_mix"]
        h2D = O["dbg_h2"]
    else:
        mixD = dscr("mixD", [NBLK * 128, 1024], BF16)
        h2D = dscr("h2D", [NBLK * 128, 1024], F32)
    h2TD = dscr("h2TD", [NBLK, 128, 1024], BF16)
    R_mixD = [Res("mixD%d" % i) for i in range(NBLK)]
    R_h2D = [Res("h2D%d" % i) for i in range(NBLK)]
    R_h2TD = [Res("h2TD%d" % i) for i in range(NBLK)]

    es = ExitStack()
    with es:
        sems = {e: es.enter_context(nc.semaphore("s_" + e)) for e in Prog.ENG}
        dsems = [es.enter_context(nc.semaphore("d%d" % i)) for i in range(32)]
        P = Prog(nc, sems, dsems)
        block = es.enter_context(nc.Block())

        def checkpoint(name):
            if stop_at is not None and name == stop_at and not P.dead:
                P.finish()
                P.flush(block)
                P.dead = True

        pb = [es.enter_context(nc.psum_tensor("pb%d" % i, [128, 512], F32)) for i in range(8)]
        R_pb = [Res("pb%d" % i, excl=True) for i in range(8)]

        class Rot:
            def __init__(self, idxs):
                self.idxs = idxs
                self.i = 0

            def next(self):
                k = self.idxs[self.i % len(self.idxs)]
                self.i += 1
                return k

        def sb(stack, name, shape, dt):
            return stack.enter_context(nc.sbuf_tensor("sb_" + name, list(shape), dt))

        ident_f = sb(es, "ident_f", [128, 128], F32)
        ident = sb(es, "ident", [128, 512], BF16)
        R_ident = Res("ident")
        P.dma("sp", ident_f[:, :], I["ident"][:, :], writes=[R_ident])
        for r in range(4):
            P.op("act", _call("activation", out=ident[:, r * 128:(r + 1) * 128], in_=ident_f[:, :], func=AF.Copy),
                 reads=[R_ident], writes=[R_ident])

        def run_interleaved(gens):
            gens = [[0.0, i, g] for i, g in enumerate(gens)]
            while gens:
                gens.sort(key=lambda x: (x[0], x[1]))
                ent = gens[0]
                try:
                    c = next(ent[2])
                    ent[0] += (c if c else 1.0)
                except StopIteration:
                    gens.remove(ent)

        with ExitStack() as sa:
            winb = sb(sa, "winb", [128, KC * NCOL], BF16)
            R_win = Res("win")
            kbi = sb(sa, "kbi", [128, 2 * 4096], BF16)
            R_kbi = [Res("kbi%d" % r) for r in range(NT)]
            R_ki = [Res("ki%d" % r) for r in range(NT)]
            vb_aug = sb(sa, "vb_aug", [128, NT * 2 * 65], BF16)
            R_vb = [Res("vb%d" % r) for r in range(NT)]
            kaT = sb(sa, "kaT", [128, 6 * 512], BF16)
            R_ka = [Res("ka%d" % s) for s in range(6)]
            va_aug = sb(sa, "va_aug", [128, 6 * 8 * 65], BF16)
            R_va = [Res("va%d" % s) for s in range(6)]
            ABb = sb(sa, "ABb", [128, 8 * ABW], BF16)
            R_AB = Res("AB")
            Bnb = sb(sa, "Bnb", [128, 8 * BNW], BF16)
            R_Bn = Res("Bn")
            Mnear = [sb(sa, "Mnear%d" % k, [128, 8 * BNW], BF16) for k in range(2)]
            R_Mnear = [Res("Mnear%d" % k) for k in range(2)]
            score = [sb(sa, "score%d" % k, [128, 4096], F32) for k in range(2)]
            R_score = [Res("score%d" % k) for k in range(2)]
            Mb = [sb(sa, "Mb%d" % k, [128, 4096], BF16) for k in range(2)]
            R_M = [Res("M%d" % k) for k in range(2)]
            relu = [sb(sa, "relu%d" % k, [128, 512], BF16) for k in range(3)]
            R_relu = [Res("relu%d" % k) for k in range(3)]
            xstg2 = [sb(sa, "xstg%d" % k, [128, 1024], F32) for k in range(2)]
            R_xstg2 = [Res("xstg%d" % k) for k in range(2)]
            xstg, R_xstg = xstg2[0], R_xstg2[0]
            xTb = [sb(sa, "xTb%d" % k, [128, 1024], BF16) for k in range(2)]
            R_xT = [Res("xT%d" % k) for k in range(2)]
            qaz = [sb(sa, "qaz%d" % k, [128, 1024], BF16) for k in range(2)]
            qbz = [sb(sa, "qbz%d" % k, [128, 1024], BF16) for k in range(3)]
            qiz = [sb(sa, "qiz%d" % k, [128, 1024], BF16) for k in range(2)]
            R_qa = [Res("qa%d" % k) for k in range(2)]
            R_qb = [Res("qb%d" % k) for k in range(3)]
            R_qi = [Res("qi%d" % k) for k in range(2)]
            coef = [sb(sa, "coef%d" % k, [128, 8], F32) for k in range(2)]
            R_coef = [Res("coef%d" % k) for k in range(2)]
            dg = [sb(sa, "dg%d" % k, [128, 1024], BF16) for k in range(2)]
            R_dg = [Res("dg%d" % k) for k in range(2)]
            PTA = [sb(sa, "PTA%d" % k, [128, 512], BF16) for k in range(3)]
            R_PTA = [Res("PTA%d" % k) for k in range(3)]
            PTB = [sb(sa, "PTB%d" % k, [128, 512], BF16) for k in range(3)]
            R_PTB = [Res("PTB%d" % k) for k in range(3)]
            mixb = [sb(sa, "mixb%d" % k, [128, 1024], BF16) for k in range(3)]
            R_mix = [Res("mix%d" % k) for k in range(3)]
            ostg = [sb(sa, "ostg%d" % k, [128, 256], F32) for k in range(2)]
            R_ostg = [Res("ostg%d" % k) for k in range(2)]
            vbstg = [sb(sa, "vbstg%d" % k, [128, 128], F32) for k in range(2)]
            R_vbstg = [Res("vbstg%d" % k) for k in range(2)]
            astg = sb(sa, "astg", [128, 1024], F32)
            R_astg = Res("astg")
            small = [sb(sa, "small%d" % k, [128, 16], F32) for k in range(2)]
            R_small = [Res("small%d" % k) for k in range(2)]
            recA = [sb(sa, "recA%d" % k, [128, 8], F32) for k in range(2)]
            R_recA = [Res("recA%d" % k) for k in range(2)]
            recB = [sb(sa, "recB%d" % k, [128, 8], F32) for k in range(2)]
            R_recB = [Res("recB%d" % k) for k in range(2)]
            colmask = sb(sa, "colmask", [128, 1], F32)
            diagm = sb(sa, "diagm", [128, 128], F32)
            kvalid = sb(sa, "kvalid", [128, NT], F32)
            c15 = sb(sa, "c15", [128, 8], F32)
            ones8 = sb(sa, "ones8", [128, 8], F32)
            R_cst = Res("cst")

            wrot = Rot([0, 1, 2])

            P.dma("sp", colmask[:, :], I["colmask"][:, :], writes=[R_cst])
            P.dma("sp", diagm[:, :], I["diagmask"][:, :], writes=[R_cst])
            P.dma("sp", kvalid[:, :], I["kvalid"][:, :], writes=[R_cst])
            P.dma("sp", c15[:, :], I["C15"][:, :], writes=[R_cst])
            P.op("pool", _call("memset", ones8[:, :], 1.0), writes=[R_cst])
            for k in range(2):
                P.op("pool", _call("memset", qaz[k][:, :], 0.0), writes=[R_qa[k]])
                P.op("pool", _call("memset", qiz[k][:, :], 0.0), writes=[R_qi[k]])
            for k in range(3):
                P.op("pool", _call("memset", qbz[k][:, :], 0.0), writes=[R_qb[k]])

            HW = NCOL // 2
            R_slot = [Res("wslot%d" % q) for q in range(4)]
            for kc in range(KC):
                for hh in range(2):
                    q = (kc * 2 + hh) % 4
                    stg = score[q // 2][:, (q % 2) * HW:(q % 2 + 1) * HW]
                    P.dma("sp", stg, I["win"][:, kc * NCOL + hh * HW: kc * NCOL + (hh + 1) * HW], writes=[R_slot[q]])
                    if hh == 0:
                        P.op("act", _call("activation", out=winb[:, kc * NCOL + hh * HW: kc * NCOL + (hh + 1) * HW], in_=stg, func=AF.Copy),
                             reads=[R_slot[q]], writes=[R_win])
                    else:
                        P.op("dve", _call("tensor_copy", out=winb[:, kc * NCOL + hh * HW: kc * NCOL + (hh + 1) * HW], in_=stg),
                             reads=[R_slot[q]], writes=[R_win])
            for hh in range(2):
                w = 4 * ABW
                P.dma("sp", score[hh][:, 0:w], I["AB"][:, hh * w:(hh + 1) * w], writes=[R_score[hh], R_slot[2 * hh], R_slot[2 * hh + 1]])
                P.op("act", _call("activation", out=ABb[:, hh * w:(hh + 1) * w], in_=score[hh][:, 0:w], func=AF.Copy),
                     reads=[R_score[hh]], writes=[R_AB])
            P.dma("sp", score[0][:, 0:8 * BNW], I["Bn"][:, :], writes=[R_score[0]])
            for h in range(8):
                P.op("dve", _call("tensor_scalar", out=Bnb[:, h * BNW:(h + 1) * BNW], in0=score[0][:, h * BNW:(h + 1) * BNW],
                                  scalar1=c15[:, h:h + 1], scalar2=None, op0=ALU.subtract),
                     reads=[R_score[0], R_cst], writes=[R_Bn])
            checkpoint('consts')

            def win_cols(kc, c0, n):
                return winb[:, kc * NCOL + c0: kc * NCOL + c0 + n]

            def fm_proj(bank, xT, R_x, N, col0, nchunks, ocol=0):
                for j in range(nchunks):
                    for kc in range(KC):
                        P.op("pe", _call("matmul", out=pb[bank][:, ocol + j * N: ocol + (j + 1) * N], lhsT=win_cols(kc, col0 + j * 128, 128),
                                         rhs=xT[:, kc * N:(kc + 1) * N], start=(kc == 0), stop=(kc == KC - 1)),
                             reads=[R_win, R_x], writes=[R_pb[bank]])

            def tm_proj(bank, xT, R_x, N, col0, ncols, ocol=0):
                for kc in range(KC):
                    P.op("pe", _call("matmul", out=pb[bank][0:N, ocol:ocol + ncols], lhsT=xT[:, kc * N:(kc + 1) * N],
                                     rhs=win_cols(kc, col0, ncols), start=(kc == 0), stop=(kc == KC - 1)),
                         reads=[R_win, R_x], writes=[R_pb[bank]])

            def load_xT(r, eng="pool"):
                s = r % 2
                P.dma("sp", xstg2[s][:, :], I["xkT"][r], writes=[R_xstg2[s]])
                P.op(eng, _call("tensor_copy", out=xTb[s][:, :], in_=xstg2[s][:, :]), reads=[R_xstg2[s]], writes=[R_xT[s]])

            def kside(r, full):
                s = r % 2
                xT, R_x = xTb[s], R_xT[s]
                so = r % 2
                bk = wrot.next()
                fm_proj(bk, xT, R_x, 128, C_KB, 1)
                fm_proj(bk, xT, R_x, 128, C_KI, 1, ocol=128)
                P.op("act", _call("activation", out=ostg[so][:, :], in_=pb[bk][:, 0:256], func=AF.Copy), reads=[R_pb[bk]], writes=[R_ostg[so]])
                P.op("pool", _call("tensor_copy", out=kbi[:, :].rearrange("p (a c) -> p a c", a=2)[:, :, r * 128:(r + 1) * 128],
                                   in_=ostg[so][:, :].rearrange("p (a c) -> p a c", a=2)),
                     reads=[R_ostg[so]], writes=[R_kbi[r], R_ki[r]])
                P.dma("sp", O["bkT"][:, r * 128:(r + 1) * 128], ostg[so][:, 0:128], reads=[R_ostg[so]], defer=True)
                P.dma("sp", O["biT"][:, r * 128:(r + 1) * 128], ostg[so][0:64, 128:256], reads=[R_ostg[so]], defer=True)
                yield 3.0
                bv_ = wrot.next()
                tm_proj(bv_, xT, R_x, 128, C_VB, 128)
                vbv = vb_aug[:, r * 130:(r + 1) * 130].rearrange("p (g d) -> p g d", d=65)
                P.op("act", _call("activation", out=vbstg[so][:, :], in_=pb[bv_][:, 0:128], func=AF.Copy), reads=[R_pb[bv_]], writes=[R_vbstg[so]])
                P.op("pool", _call("tensor_copy", out=vbv[:, :, 0:64], in_=vbstg[so][:, :].rearrange("p (g d) -> p g d", d=64)),
                     reads=[R_vbstg[so]], writes=[R_vb[r]])
                P.op("pool", _call("tensor_scalar", out=vbv[:, :, 64:65], in0=ones8[:, 0:2].rearrange("p (g o) -> p g o", o=1),
                                   scalar1=kvalid[:, r:r + 1], scalar2=None, op0=ALU.mult),
                     reads=[R_cst], writes=[R_vb[r]])
                P.dma("sp", O["bv"][r * 128:(r + 1) * 128, :], vbstg[so][:, :], reads=[R_vbstg[so]], defer=True)
                yield 3.0
                if not full:
                    return
                slot = r % 6
                ba = wrot.next()
                fm_proj(ba, xT, R_x, 128, C_KA, 4)
                P.op("act", _call("activation", out=kaT[:, slot * 512:(slot + 1) * 512], in_=pb[ba][:, :], func=AF.Copy),
                     reads=[R_pb[ba]], writes=[R_ka[slot]])
                if r >= 28:
                    P.op("dve", _call("tensor_copy", out=astg[:, 0:512], in_=pb[ba][:, :]), reads=[R_pb[ba]], writes=[R_astg])
                    P.dma("sp", O["akT"].rearrange("p (j t) -> p j t", t=512)[:, :, (r - 28) * 128:(r - 27) * 128],
                          astg[:, 0:512].rearrange("p (j t) -> p j t", t=128), reads=[R_astg], defer=True)
                yield 3.0
                bva = wrot.next()
                tm_proj(bva, xT, R_x, 128, C_VA, 512)
                vav = va_aug[:, slot * 520:(slot + 1) * 520].rearrange("p (h d) -> p h d", d=65)
                P.op("act", _call("activation", out=vav[:, :, 0:64], in_=pb[bva][:, :].rearrange("p (h d) -> p h d", d=64), func=AF.Copy),
                     reads=[R_pb[bva]], writes=[R_va[slot]])
                P.op("pool", _call("tensor_scalar", out=vav[:, :, 64:65], in0=ones8[:, :].rearrange("p (h o) -> p h o", o=1),
                                   scalar1=kvalid[:, r:r + 1], scalar2=None, op0=ALU.mult),
                     reads=[R_cst], writes=[R_va[slot]])
                if r >= 28:
                    P.op("dve", _call("tensor_copy", out=astg[:, 512:1024], in_=pb[bva][:, :]), reads=[R_pb[bva]], writes=[R_astg])
                    P.dma("sp", O["av"][(r - 28) * 128:(r - 27) * 128, :], astg[:, 512:1024], reads=[R_astg], defer=True)
                yield 3.0

            def qside(xT, R_x, qs, st, st3):
                b1 = wrot.next()
                fm_proj(b1, xT, R_x, qs, C_QA, 4)
                for hf in range(2):
                    P.op("act", _call("activation",
                                      out=qaz[st][hf * 64:(hf + 1) * 64, 0:8 * qs].rearrange("p (j two q) -> p j two q", two=2, q=qs)[:, :, hf, :],
                                      in_=pb[b1][hf * 64:(hf + 1) * 64, 0:4 * qs].rearrange("p (j q) -> p j q", q=qs), func=AF.Copy, scale=0.125),
                         reads=[R_pb[b1]], writes=[R_qa[st]])
                yield 3.0
                b2 = wrot.next()
                fm_proj(b2, xT, R_x, qs, C_QB, 4)
                for g in range(2):
                    P.op("act", _call("activation", out=qbz[st3][g * 64:(g + 1) * 64, g * 4 * qs:(g + 1) * 4 * qs],
                                      in_=pb[b2][g * 64:(g + 1) * 64, 0:4 * qs], func=AF.Copy, scale=0.125),
                         reads=[R_pb[b2]], writes=[R_qb[st3]])
                yield 3.0
                b3 = wrot.next()
                fm_proj(b3, xT, R_x, qs, C_QI, 4)
                for hf in range(2):
                    P.op("act", _call("activation",
                                      out=qiz[st][hf * 64:(hf + 1) * 64, 0:8 * qs].rearrange("p (j two q) -> p j two q", two=2, q=qs)[:, :, hf, :],
                                      in_=pb[b3][hf * 64:(hf + 1) * 64, 0:4 * qs].rearrange("p (j q) -> p j q", q=qs), func=AF.Copy),
                         reads=[R_pb[b3]], writes=[R_qi[st]])
                b4 = wrot.next()
                tm_proj(b4, xT, R_x, qs, C_WI, 8)
                P.op("dve", _call("tensor_scalar", out=coef[st][0:qs, :], in0=pb[b4][0:qs, 0:8], scalar1=float(8.0 ** -1.5), scalar2=None, op0=ALU.mult),
                     reads=[R_pb[b4]], writes=[R_coef[st]])
                for h in range(8):
                    P.op("pool", _call("tensor_scalar", out=dg[st][0:qs, h * 128: h * 128 + qs], in0=ident_f[0:qs, 0:qs],
                                       scalar1=coef[st][0:qs, h:h + 1], scalar2=None, op0=ALU.mult),
                         reads=[R_coef[st], R_ident], writes=[R_dg[st]])
                yield 3.0

            def normalize(bank, qs, mixt, R_m, col0, rec, R_rec):
                ov = pb[bank][0:qs, 0:260].rearrange("p (h d) -> p h d", d=65)
                P.op("dve", _call("tensor_scalar", out=rec[0:qs, 0:4].rearrange("p (h o) -> p h o", o=1), in0=ov[:, :, 64:65],
                                  scalar1=1e-30, scalar2=None, op0=ALU.max),
                     reads=[R_pb[bank]], writes=[R_rec])
                P.op("dve", _call("reciprocal", out=rec[0:qs, 0:4], in_=rec[0:qs, 0:4]), reads=[R_rec], writes=[R_rec])
                for hh in range(4):
                    P.op("dve", _call("tensor_scalar", out=mixt[0:qs, col0 + hh * 64: col0 + (hh + 1) * 64],
                                      in0=pb[bank][0:qs, hh * 65: hh * 65 + 64],
                                      scalar1=rec[0:qs, hh:hh + 1], scalar2=None, op0=ALU.mult),
                         reads=[R_pb[bank], R_rec], writes=[R_m])

            def pipe3(items, s1, s2, s3, D, cost=1.0):
                pend = []
                for it in items:
                    s1(it)
                    s2(it)
                    pend.append(it)
                    if len(pend) > D:
                        s3(pend.pop(0))
                    yield cost
                while pend:
                    s3(pend.pop(0))
                    yield cost

            pta_rot = Rot([0, 1, 2])
            relu_rot = Rot([0, 1, 2])
            ptb_rot = Rot([0, 1, 2])
            brot = Rot([3, 7])

            def front_attn(sn, qs, wins, btiles, prompt_masks, abw):
                st = sn % 2
                mixt, R_m = mixb[sn % 3], R_mix[sn % 3]
                nw = len(wins)

                units = []
                for h in range(8):
                    units.append({"h": h, "t0": 0, "tiles": wins[0:4]})
                    if nw > 4:
                        units.append({"h": h, "t0": 4, "tiles": wins[4:5]})

                def a1(u):
                    h = u["h"]
                    j = h // 2
                    bank = wrot.next()
                    u["bank"] = bank
                    for i, (slot, ts) in enumerate(u["tiles"]):
                        t = u["t0"] + i
                        c0 = i * qs
                        P.op("pe", _call("matmul", out=pb[bank][0:ts, c0:c0 + qs], lhsT=kaT[:, slot * 512 + j * 128: slot * 512 + j * 128 + ts],
                                         rhs=qaz[st][:, h * qs:(h + 1) * qs], start=True, stop=False),
                             reads=[R_ka[slot], R_qa[st]], writes=[R_pb[bank]])
                        P.op("pe", _call("matmul", out=pb[bank][0:ts, c0:c0 + qs], lhsT=ABb[0:qs, h * abw + t * 128: h * abw + t * 128 + ts],
                                         rhs=ident[0:qs, 0:qs], start=False, stop=True),
                             reads=[R_AB, R_ident], writes=[R_pb[bank]])

                def a2(u):
                    k = pta_rot.next()
                    u["pt"], u["R_pt"] = PTA[k], R_PTA[k]
                    bank = u["bank"]
                    tsm = max(ts for (_, ts) in u["tiles"])
                    n = len(u["tiles"])
                    P.op("act", _call("activation", out=u["pt"][0:tsm, 0:n * qs], in_=pb[bank][0:tsm, 0:n * qs], func=AF.Exp),
                         reads=[R_pb[bank]], writes=[u["R_pt"]])

                def a3(u):
                    h = u["h"]
                    last_unit = (u["t0"] + len(u["tiles"]) == nw)
                    for i, (slot, ts) in enumerate(u["tiles"]):
                        t = u["t0"] + i
                        P.op("pe", _call("matmul", out=pb[4][0:qs, (h % 4) * 65:(h % 4) * 65 + 65], lhsT=u["pt"][0:ts, i * qs:(i + 1) * qs],
                                         rhs=va_aug[0:ts, slot * 520 + h * 65: slot * 520 + h * 65 + 65],
                                         start=(h % 4 == 0 and t == 0), stop=(t == nw - 1), skip_group_check=True),
                             reads=[u["R_pt"], R_va[slot]], writes=[R_pb[4]])
                    if last_unit and h % 4 == 3:
                        normalize(4, qs, mixt, R_m, (h // 4) * 256, recA[st], R_recA[st])

                yield from pipe3(units, a1, a2, a3, 2, 0.9)

                L = btiles[-1][1] + btiles[-1][2]
                items = []
                cc = 0
                for c0 in range(0, L, 512):
                    w = min(512, L - c0)
                    rk = [R_ki[tt[0]] for tt in btiles if tt[1] >= c0 - 127 and tt[1] < c0 + w]
                    for h in range(8):
                        items.append({"c0": c0, "w": w, "h": h, "sc": (5, 4)[cc % 2], "rk": rk})
                    cc += 1

                def i1(it):
                    bank = wrot.next()
                    it["bank"] = bank
                    h, c0, w = it["h"], it["c0"], it["w"]
                    P.op("pe", _call("matmul", out=pb[bank][0:qs, 0:w], lhsT=qiz[st][:, h * qs:(h + 1) * qs],
                                     rhs=kbi[:, 4096 + c0: 4096 + c0 + w], start=True, stop=True),
                         reads=[R_qi[st]] + it["rk"], writes=[R_pb[bank]])

                def i2(it):
                    k = relu_rot.next()
                    it["rl"], it["R_rl"] = relu[k], R_relu[k]
                    w = it["w"]
                    P.op("act", _call("activation", out=it["rl"][0:qs, 0:w], in_=pb[it["bank"]][0:qs, 0:w], func=AF.Relu),
                         reads=[R_pb[it["bank"]]], writes=[it["R_rl"]])

                def i3(it):
                    h, c0, w, sc = it["h"], it["c0"], it["w"], it["sc"]
                    P.op("pe", _call("matmul", out=pb[sc][0:qs, 0:w], lhsT=dg[st][0:qs, h * 128: h * 128 + qs], rhs=it["rl"][0:qs, 0:w],
                                     start=(h == 0), stop=(h == 7)),
                         reads=[R_dg[st], it["R_rl"]], writes=[R_pb[sc]])
                    if h == 7:
                        if prompt_masks and c0 < 2048:
                            wm = min(w, 2048 - c0)
                            P.op("act", _call("activation", out=score[st][0:qs, c0:c0 + wm], in_=pb[sc][0:qs, 0:wm], func=AF.Identity,
                                              bias=colmask[0:qs, 0:1]),
                                 reads=[R_pb[sc], R_cst], writes=[R_score[st]])
                            if wm < w:
                                P.op("act", _call("activation", out=score[st][0:qs, c0 + wm:c0 + w], in_=pb[sc][0:qs, wm:w], func=AF.Copy),
                                     reads=[R_pb[sc]], writes=[R_score[st]])
                        else:
                            P.op("act", _call("activation", out=score[st][0:qs, c0:c0 + w], in_=pb[sc][0:qs, 0:w], func=AF.Copy),
                                 reads=[R_pb[sc]], writes=[R_score[st]])

                yield from pipe3(items, i1, i2, i3, 2, 0.65)
                if prompt_masks:
                    P.op("dve", _call("tensor_tensor", out=score[st][0:qs, L - 128:L], in0=score[st][0:qs, L - 128:L], in1=diagm[0:qs, :], op=ALU.add),
                         reads=[R_score[st], R_cst], writes=[R_score[st]])
                yield

            def bis_gen(sn, qs, btiles, bnw):
                st = sn % 2
                sm, R_sm = small[st], R_small[st]
                L = btiles[-1][1] + btiles[-1][2]
                P.op("dve", _call("memset", sm[0:qs, 1:2], 0.0), writes=[R_sm])
                for k in range(NIT):
                    wk = BIS_W0 / (2.0 ** k)
                    P.op("dve", _call("tensor_scalar", out=Mb[st][0:qs, 0:L], in0=score[st][0:qs, 0:L], scalar1=sm[0:qs, 1:2], scalar2=None,
                                      op0=ALU.is_ge, op1=ALU.add, accum_out=sm[0:qs, 0:1]),
                         reads=[R_score[st], R_sm], writes=[R_M[st], R_sm])
                    P.op("dve", _call("tensor_scalar", out=sm[0:qs, 2:3], in0=sm[0:qs, 0:1], scalar1=255.5, scalar2=wk,
                                      op0=ALU.is_ge, op1=ALU.mult),
                         reads=[R_sm], writes=[R_sm])
                    P.op("dve", _call("scalar_tensor_tensor", out=sm[0:qs, 1:2], in0=sm[0:qs, 2:3], scalar=-wk / 2.0,
                                      in1=sm[0:qs, 1:2], op0=ALU.add, op1=ALU.add),
                         reads=[R_sm], writes=[R_sm])
                    yield L * 1.08e-3 + 0.5
                wl = BIS_W0 / (2.0 ** (NIT - 1)) / 2.0
                P.op("dve", _call("tensor_scalar", out=sm[0:qs, 3:4], in0=sm[0:qs, 1:2], scalar1=-wl, scalar2=None, op0=ALU.add),
                     reads=[R_sm], writes=[R_sm])
                P.op("dve", _call("tensor_scalar", out=Mb[st][0:qs, 0:L], in0=score[st][0:qs, 0:L], scalar1=sm[0:qs, 3:4], scalar2=NEGM,
                                  op0=ALU.is_lt, op1=ALU.mult),
                     reads=[R_score[st], R_sm], writes=[R_M[st]])
                nearw = btiles[-2][2] + btiles[-1][2]
                for h in range(8):
                    P.op("dve", _call("tensor_tensor", out=Mnear[st][0:qs, h * bnw: h * bnw + nearw], in0=Bnb[0:qs, h * bnw: h * bnw + nearw],
                                      in1=Mb[st][0:qs, L - nearw:L], op=ALU.add),
                         reads=[R_Bn, R_M[st]], writes=[R_Mnear[st]])
                yield
            def battn_gen(sn, qs, btiles, blk, bnw):
                st = sn % 2
                st3 = sn % 3
                mixt, R_m = mixb[st3], R_mix[st3]
                nb = len(btiles)
                items = [{"g": g, "t": t, "vt": vt, "c0": c0, "ts": ts} for g in range(2) for t, (vt, c0, ts) in enumerate(btiles)]

                def b1(it):
                    g, t, vt, c0, ts = it["g"], it["t"], it["vt"], it["c0"], it["ts"]
                    bank = brot.next()
                    it["bank"] = bank
                    P.op("pe", _call("matmul", out=pb[bank][0:ts, 0:4 * qs], lhsT=kbi[:, c0:c0 + ts],
                                     rhs=qbz[st3][:, g * 4 * qs:(g + 1) * 4 * qs], start=True, stop=False),
                         reads=[R_kbi[vt], R_qb[st3]], writes=[R_pb[bank]])
                    if t < nb - 2 and qs == 128:
                        P.op("pe", _call("matmul", out=pb[bank][0:ts, 0:512], lhsT=Mb[st][0:qs, c0:c0 + ts], rhs=ident[0:128, 0:512],
                                         start=False, stop=True),
                             reads=[R_M[st], R_ident], writes=[R_pb[bank]])
                    elif t < nb - 2:
                        for r in range(4):
                            P.op("pe", _call("matmul", out=pb[bank][0:ts, r * qs:(r + 1) * qs], lhsT=Mb[st][0:qs, c0:c0 + ts],
                                             rhs=ident[0:qs, 0:qs], start=False, stop=(r == 3)),
                                 reads=[R_M[st], R_ident], writes=[R_pb[bank]])
                    else:
                        tt = t - (nb - 2)
                        for r in range(4):
                            hh = g * 4 + r
                            P.op("pe", _call("matmul", out=pb[bank][0:ts, r * qs:(r + 1) * qs],
                                             lhsT=Mnear[st][0:qs, hh * bnw + tt * 128: hh * bnw + tt * 128 + ts], rhs=ident[0:qs, 0:qs],
                                             start=False, stop=(r == 3)),
                                 reads=[R_Mnear[st], R_ident], writes=[R_pb[bank]])

                def b2(it):
                    k = ptb_rot.next()
                    it["ptb"], it["R_ptb"] = PTB[k], R_PTB[k]
                    ts = it["ts"]
                    P.op("act", _call("activation", out=it["ptb"][0:ts, 0:4 * qs], in_=pb[it["bank"]][0:ts, 0:4 * qs], func=AF.Exp),
                         reads=[R_pb[it["bank"]]], writes=[it["R_ptb"]])

                def b3(it):
                    g, t, vt, ts = it["g"], it["t"], it["vt"], it["ts"]
                    for r in range(4):
                        P.op("pe", _call("matmul", out=pb[6][0:qs, r * 65: r * 65 + 65], lhsT=it["ptb"][0:ts, r * qs:(r + 1) * qs],
                                         rhs=vb_aug[0:ts, (vt * 2 + g) * 65:(vt * 2 + g) * 65 + 65],
                                         start=(t == 0 and r == 0), stop=(t == nb - 1), skip_group_check=True),
                             reads=[it["R_ptb"], R_vb[vt]], writes=[R_pb[6]])
                    if t == nb - 1:
                        normalize(6, qs, mixt, R_m, 512 + g * 256, recB[st], R_recB[st])

                yield from pipe3(items, b1, b2, b3, 1, 0.8)
                P.dma("sp", mixD[blk * 128: blk * 128 + qs, :], mixt[0:qs, :], reads=[R_m], writes=[R_mixD[blk]], defer=True)
                yield

            load_xT(0, "dve")
            for r in range(16):
                if r + 1 < 16:
                    load_xT(r + 1, "dve")
                for _ in kside(r, full=(r >= 11)):
                    pass
            checkpoint('phase0')

            def prompt_front(sn, T):
                if T + 1 <= 31:
                    load_xT(T + 1)
                if T >= 16:
                    yield from kside(T, full=True)
                s = T % 2
                yield from qside(xTb[s], R_xT[s], 128, sn % 2, sn % 3)
                wins = [((T - 4 + t) % 6, 128) for t in range(5)]
                btiles = [(t, t * 128, 128) for t in range(T + 1)]
                yield from front_attn(sn, 128, wins, btiles, True, ABW)

            def prompt_bis(sn, T):
                btiles = [(t, t * 128, 128) for t in range(T + 1)]
                yield from bis_gen(sn, 128, btiles, BNW)

            def prompt_battn(sn, T, blk):
                btiles = [(t, t * 128, 128) for t in range(T + 1)]
                yield from battn_gen(sn, 128, btiles, blk, BNW)

            steps = [(0, 15, 16)] + [(1 + i, 16 + i, i) for i in range(16)]
            ns = len(steps)
            SN = ns
            sst = SN % 2
            s_wins = [(0, 128), (1, 128), (2, 128), (3, 128), (4, 16)]
            s_btiles = [(t, t * 128, 128) for t in range(16)] + [(16, 2048, 16)]
            xs_, R_xs = xTb[0], R_xT[0]

            def sample_front():
                stg, R_stg = score[sst], R_score[sst]
                P.dma("sp", stg[:, 0:2048], I["cbiT"][:, :], writes=[R_stg])
                P.op("act", _call("activation", out=kbi[:, 4096:4096 + 2048], in_=stg[:, 0:2048], func=AF.Copy),
                     reads=[R_stg], writes=R_ki[0:16])
                P.dma("sp", stg[:, 2048:4096], I["cakT"][:, :], writes=[R_stg])
                for s4 in range(4):
                    P.op("act", _call("activation", out=kaT[:, s4 * 512:(s4 + 1) * 512].rearrange("p (j t) -> p j t", t=128),
                                      in_=stg[:, 2048:4096].rearrange("p (j t) -> p j t", t=512)[:, :, s4 * 128:(s4 + 1) * 128], func=AF.Copy),
                         reads=[R_stg], writes=[R_ka[s4]])
                yield 3.0
                P.dma("sp", stg[:, 0:2048].rearrange("p (t c) -> p t c", c=512), I["cav"].rearrange("(t p) c -> p t c", p=128), writes=[R_stg])
                vaall = va_aug[:, 0:4 * 520].rearrange("p (t d) -> p t d", d=65)
                P.op("act", _call("activation", out=vaall[:, :, 0:64], in_=stg[:, 0:2048].rearrange("p (t d) -> p t d", d=64), func=AF.Copy),
                     reads=[R_stg], writes=R_va[0:5])
                P.op("pool", _call("memset", va_aug[:, 0:5 * 520].rearrange("p (t d) -> p t d", d=65)[:, :, 64:65], 1.0), writes=R_va[0:5])
                for hh in range(2):
                    w = 4 * 528
                    P.dma("sp", stg[0:16, 0:w], I["ABs"][:, hh * w:(hh + 1) * w], writes=[R_stg])
                    P.op("act", _call("activation", out=ABb[0:16, hh * w:(hh + 1) * w], in_=stg[0:16, 0:w], func=AF.Copy),
                         reads=[R_stg], writes=[R_AB])
                P.op("pool", _call("memset", qaz[sst][:, :], 0.0), writes=[R_qa[sst]])
                P.op("pool", _call("memset", qbz[SN % 3][:, :], 0.0), writes=[R_qb[SN % 3]])
                P.op("pool", _call("memset", qiz[sst][:, :], 0.0), writes=[R_qi[sst]])
                P.dma("sp", xstg[:, 0:128], I["xsT"][:, :], writes=[R_xstg])
                P.op("pool", _call("tensor_copy", out=xTb[0][:, 0:128], in_=xstg[:, 0:128]), reads=[R_xstg], writes=[R_xT[0]])
                yield 3.0
                bk = wrot.next()
                fm_proj(bk, xs_, R_xs, 16, C_KI, 1)
                P.op("act", _call("activation", out=kbi[:, 4096 + 2048:4096 + 2064], in_=pb[bk][:, 0:16], func=AF.Copy), reads=[R_pb[bk]], writes=[R_ki[16]])
                P.op("dve", _call("tensor_copy", out=ostg[0][:, 16:32], in_=pb[bk][:, 0:16]), reads=[R_pb[bk]], writes=[R_ostg[0]])
                P.dma("sp", O["sbiT"][:, :], ostg[0][0:64, 16:32], reads=[R_ostg[0]], defer=True)
                ba = wrot.next()
                fm_proj(ba, xs_, R_xs, 16, C_KA, 4)
                P.op("act", _call("activation", out=kaT[:, 4 * 512:5 * 512].rearrange("p (j t) -> p j t", t=128)[:, :, 0:16],
                                  in_=pb[ba][:, 0:64].rearrange("p (j t) -> p j t", t=16), func=AF.Copy),
                     reads=[R_pb[ba]], writes=[R_ka[4]])
                P.op("dve", _call("tensor_copy", out=astg[:, 0:64], in_=pb[ba][:, 0:64]), reads=[R_pb[ba]], writes=[R_astg])
                P.dma("sp", O["sakT"][:, :], astg[:, 0:64], reads=[R_astg], defer=True)
                bva = wrot.next()
                tm_proj(bva, xs_, R_xs, 16, C_VA, 512)
                vav = va_aug[0:16, 4 * 520:5 * 520].rearrange("p (h d) -> p h d", d=65)
                P.op("act", _call("activation", out=vav[:, :, 0:64], in_=pb[bva][0:16, :].rearrange("p (h d) -> p h d", d=64), func=AF.Copy),
                     reads=[R_pb[bva]], writes=[R_va[4]])
                P.op("dve", _call("tensor_copy", out=astg[0:16, 512:1024], in_=pb[bva][0:16, :]), reads=[R_pb[bva]], writes=[R_astg])
                P.dma("sp", O["sav"][:, :], astg[0:16, 512:1024], reads=[R_astg], defer=True)
                yield 3.0
                yield from qside(xs_, R_xs, 16, sst, SN % 3)
                yield from front_attn(SN, 16, s_wins, s_btiles, False, 528)

            def sample_bis():
                stg, R_stg = score[1 - sst], R_score[1 - sst]
                P.dma("sp", stg[0:16, 0:8 * 144], I["Bns"][:, :], writes=[R_stg])
                for h in range(8):
                    P.op("dve", _call("tensor_scalar", out=Bnb[0:16, h * 144:(h + 1) * 144], in0=stg[0:16, h * 144:(h + 1) * 144],
                                      scalar1=c15[0:16, h:h + 1], scalar2=None, op0=ALU.subtract),
                         reads=[R_stg, R_cst], writes=[R_Bn])
                yield 1.0
                yield from bis_gen(SN, 16, s_btiles, 144)

            def sample_battn():
                stg, R_stg = score[1 - sst], R_score[1 - sst]
                P.dma("sp", stg[:, 0:2048], I["cbkT"][:, :], writes=[R_stg])
                P.op("act", _call("activation", out=kbi[:, 0:2048], in_=stg[:, 0:2048], func=AF.Copy),
                     reads=[R_stg], writes=R_kbi[0:16])
                P.dma("sp", stg[:, 2048:4096].rearrange("p (t c) -> p t c", c=128), I["cbv"].rearrange("(t p) c -> p t c", p=128), writes=[R_stg])
                vball = vb_aug[:, 0:16 * 130].rearrange("p (t d) -> p t d", d=65)
                P.op("act", _call("activation", out=vball[:, :, 0:64], in_=stg[:, 2048:4096].rearrange("p (t d) -> p t d", d=64), func=AF.Copy),
                     reads=[R_stg], writes=R_vb[0:17])
                P.op("pool", _call("memset", vb_aug[:, 0:17 * 130].rearrange("p (t d) -> p t d", d=65)[:, :, 64:65], 1.0), writes=R_vb[0:17])
                bk = wrot.next()
                fm_proj(bk, xs_, R_xs, 16, C_KB, 1)
                P.op("act", _call("activation", out=kbi[:, 2048:2064], in_=pb[bk][:, 0:16], func=AF.Copy), reads=[R_pb[bk]], writes=[R_kbi[16]])
                P.op("dve", _call("tensor_copy", out=ostg[1][:, 0:16], in_=pb[bk][:, 0:16]), reads=[R_pb[bk]], writes=[R_ostg[1]])
                P.dma("sp", O["sbkT"][:, :], ostg[1][:, 0:16], reads=[R_ostg[1]], defer=True)
                bv_ = wrot.next()
                tm_proj(bv_, xs_, R_xs, 16, C_VB, 128)
                vbv = vb_aug[0:16, 16 * 130:17 * 130].rearrange("p (g d) -> p g d", d=65)
                P.op("act", _call("activation", out=vbv[:, :, 0:64], in_=pb[bv_][0:16, 0:128].rearrange("p (g d) -> p g d", d=64), func=AF.Copy),
                     reads=[R_pb[bv_]], writes=[R_vb[16]])
                P.op("dve", _call("tensor_copy", out=vbstg[0][0:16, :], in_=pb[bv_][0:16, 0:128]), reads=[R_pb[bv_]], writes=[R_vbstg[0]])
                P.dma("sp", O["sbv"][:, :], vbstg[0][0:16, :], reads=[R_vbstg[0]], defer=True)
                yield 3.0
                yield from battn_gen(SN, 16, s_btiles, 17, 144)

            for tick in range(ns + 3):
                gens = []
                if 0 <= tick - 2 < ns:
                    gens.append(prompt_battn(*steps[tick - 2]))
                elif tick - 2 == ns:
                    gens.append(sample_battn())
                if 0 <= tick - 1 < ns:
                    gens.append(prompt_bis(*steps[tick - 1][0:2]))
                elif tick - 1 == ns:
                    gens.append(sample_bis())
                if tick < ns:
                    gens.append(prompt_front(*steps[tick][0:2]))
                elif tick == ns:
                    gens.append(sample_front())
                run_interleaved(gens)
            checkpoint('steps')
            checkpoint('phaseA')
            P.flush(block)

        P.barrier()
        with ExitStack() as sbk:
            wob = sb(sbk, "wob", [128, 8 * 1024], BF16)
            wmqb = sb(sbk, "wmqb", [128, 8 * 512], BF16)
            wmob = sb(sbk, "wmob", [128, 4 * 1024], BF16)
            wtmp = sb(sbk, "wtmp", [128, 8 * 512], BF16)
            R_wo, R_wmq, R_wmo, R_wtmp = Res("wo"), Res("wmq"), Res("wmo"), Res("wtmp")
            wst = [sb(sbk, "wst%d" % k, [128, 2048], F32) for k in range(2)]
            R_wst = [Res("wst%d" % k) for k in range(2)]
            lnt = sb(sbk, "lnt", [128, 4 * 1024], F32)
            R_ln = Res("ln")
            memTb = sb(sbk, "memTb", [128, 8 * 256], BF16)
            R_memT = Res("memT")
            mkT = [sb(sbk, "mkT%d" % k, [128, 4 * 256], BF16) for k in range(2)]
            mva = [sb(sbk, "mva%d" % k, [128, 2 * 4 * 129], BF16) for k in range(2)]
            R_mk = [Res("mk%d" % k) for k in range(2)]
            R_mv = [Res("mv%d" % k) for k in range(2)]
            mixl = [sb(sbk, "mixl%d" % k, [128, 1024], BF16) for k in range(4)]
            R_mixl = [Res("mixl%d" % k) for k in range(4)]
            xr = [sb(sbk, "xr%d" % k, [128, 1024], F32) for k in range(4)]
            R_xr = [Res("xr%d" % k) for k in range(4)]
            NB3 = 4
            tT_l = [sb(sbk, "tT%d" % k, [128, 1024], BF16) for k in range(NB3)]
            hA_l = [sb(sbk, "hA%d" % k, [128, 1024], F32) for k in range(NB3)]
            hB_l = [sb(sbk, "hB%d" % k, [128, 1024], F32) for k in range(NB3)]
            h16_l = [sb(sbk, "h16%d" % k, [128, 1024], BF16) for k in range(NB3)]
            qmT_l = [sb(sbk, "qmT%d" % k, [128, 512], BF16) for k in range(NB3)]
            PTm_l = [sb(sbk, "PTm%d" % k, [128, 1024], BF16) for k in range(NB3)]
            o16_l = [sb(sbk, "o16%d" % k, [128, 512], BF16) for k in range(NB3)]
            oT_l = [sb(sbk, "oT%d" % k, [128, 512], BF16) for k in range(NB3)]
            stat_l = [sb(sbk, "stat%d" % k, [128, 32], F32) for k in range(NB3)]
            RB = [{n: Res(n + str(k)) for n in ("tT", "hA", "hB", "h16", "qm", "PTm", "o16", "oT", "stat")} for k in range(NB3)]
            h2T = [sb(sbk, "h2T%d" % k, [128, 1024], BF16) for k in range(4)]
            R_h2T = [Res("h2T%d" % k) for k in range(4)]
            mstg = sb(sbk, "mstg", [128, 1024], F32)
            R_mstg = Res("mstg")
            neghalf = sb(sbk, "neghalf", [128, 1], F32)
            R_nh = Res("neghalf")
            P.op("pool", _call("memset", neghalf[:, :], -0.5), writes=[R_nh])
            wrot = Rot([0, 1, 2, 3, 4, 5, 6, 7])

            def load_cast(dst, R_dst, src, ncols, engs=("act", "dve")):
                k = 0
                for c0 in range(0, ncols, 2048):
                    w = min(2048, ncols - c0)
                    s = k % 2
                    P.dma("sp", wst[s][:, 0:w], src[:, c0:c0 + w], writes=[R_wst[s]])
                    eng = engs[k % len(engs)]
                    if eng == "act":
                        P.op("act", _call("activation", out=dst[:, c0:c0 + w], in_=wst[s][:, 0:w], func=AF.Copy),
                             reads=[R_wst[s]], writes=[R_dst])
                    else:
                        P.op(eng, _call("tensor_copy", out=dst[:, c0:c0 + w], in_=wst[s][:, 0:w]),
                             reads=[R_wst[s]], writes=[R_dst])
                    k += 1

            load_cast(wob, R_wo, I["wo"], 8192)
            load_cast(wmqb, R_wmq, I["wmq"], 4096)
            load_cast(wmob, R_wmo, I["wmo"], 4096)
            for k in range(4):
                P.dma("sp", lnt[:, k * 1024:(k + 1) * 1024], I["lnp"][k:k + 1, :].to_broadcast([128, 1024]), writes=[R_ln])
            load_cast(memTb, R_memT, I["memT"], 2048)
            load_cast(wtmp, R_wtmp, I["wmk"], 4096)
            for h in range(4):
                bank = wrot.next()
                for kc in range(KC):
                    P.op("pe", _call("matmul",
                        out=pb[bank][:, 0:256], lhsT=wtmp[:, kc * 512 + h * 128: kc * 512 + (h + 1) * 128],
                        rhs=memTb[:, kc * 256:(kc + 1) * 256], start=(kc == 0), stop=(kc == KC - 1)),
                        reads=[R_wtmp, R_memT], writes=[R_pb[bank]])
                P.op("act", _call("activation", out=mkT[0][:, h * 256:(h + 1) * 256], in_=pb[bank][:, 0:256], func=AF.Copy),
                     reads=[R_pb[bank]], writes=[R_mk[0]])
                P.op("dve", _call("tensor_copy", out=mstg[:, h * 256:(h + 1) * 256], in_=pb[bank][:, 0:256]),
                     reads=[R_pb[bank]], writes=[R_mstg])
            P.dma("sp", O["mkT"][:, :], mstg[:, :], reads=[R_mstg], defer=True)
            load_cast(wtmp, R_wtmp, I["wmv"], 4096)
            for mt in range(2):
                bank = wrot.next()
                for kc in range(KC):
                    P.op("pe", _call("matmul",
                        out=pb[bank][:, 0:512], lhsT=memTb[:, kc * 256 + mt * 128: kc * 256 + (mt + 1) * 128],
                        rhs=wtmp[:, kc * 512:(kc + 1) * 512], start=(kc == 0), stop=(kc == KC - 1)),
                        reads=[R_wtmp, R_memT], writes=[R_pb[bank]])
                mvv = mva[0][:, mt * 516:(mt + 1) * 516].rearrange("p (h d) -> p h d", d=129)
                P.op("act", _call("activation", out=mvv[:, :, 0:128], in_=pb[bank][:, :].rearrange("p (h d) -> p h d", d=128), func=AF.Copy),
                     reads=[R_pb[bank]], writes=[R_mv[0]])
                P.op("dve", _call("tensor_copy", out=mstg[:, mt * 512:(mt + 1) * 512], in_=pb[bank][:, :]),
                     reads=[R_pb[bank]], writes=[R_mstg])
                P.dma("sp", O["mv"][mt * 128:(mt + 1) * 128, :], mstg[:, mt * 512:(mt + 1) * 512], reads=[R_mstg], defer=True)
            for k in range(2):
                P.op("pool", _call("memset", mva[k][:, :].rearrange("p (t d) -> p t d", d=129)[:, :, 128:129], 1.0), writes=[R_mv[k]])
            load_cast(mkT[1], R_mk[1], I["cmkT"], 1024)
            P.dma("sp", wst[0][:, 0:1024].rearrange("p (t c) -> p t c", c=512), I["cmv"].rearrange("(t p) c -> p t c", p=128), writes=[R_wst[0]])
            P.op("act", _call("activation", out=mva[1][:, :].rearrange("p (t d) -> p t d", d=129)[:, :, 0:128],
                                               in_=wst[0][:, 0:1024].rearrange("p (t d) -> p t d", d=128), func=AF.Copy),
                 reads=[R_wst[0]], writes=[R_mv[1]])

            checkpoint('phaseB_pre')
            def transpose_to(src16, R_src, qs, nchunk, dst, R_dst):
                bank = wrot.next()
                pbf = pb[bank][:, :].bitcast(BF16)
                for c in range(nchunk):
                    P.op("pe", _call("transpose", out=pbf[:, c * qs:(c + 1) * qs], in_=src16[0:qs, c * 128:(c + 1) * 128],
                                                                   identity=ident[0:qs, 0:qs]),
                         reads=[R_src, R_ident], writes=[R_pb[bank]])
                P.op("act", _call("activation", out=dst[:, 0:nchunk * qs], in_=pbf[:, 0:nchunk * qs], func=AF.Copy),
                     reads=[R_pb[bank]], writes=[R_dst])

            def layer_norm(hin, R_hin, qs, gcol, hout, R_hout, stat, R_stat):
                for c in range(2):
                    P.op("dve", _call("bn_stats", out=stat[0:qs, c * 6:(c + 1) * 6], in_=hin[0:qs, c * 512:(c + 1) * 512]),
                         reads=[R_hin], writes=[R_stat])
                P.op("dve", _call("bn_aggr", out=stat[0:qs, 12:14], in_=stat[0:qs, 0:12]), reads=[R_stat], writes=[R_stat])
                P.op("dve", _call("tensor_scalar", out=stat[0:qs, 14:15], in0=stat[0:qs, 13:14], scalar1=LN_EPS, scalar2=None, op0=ALU.add),
                     reads=[R_stat], writes=[R_stat])
                P.op("pool", _call("tensor_tensor", out=stat[0:qs, 16:17], in0=stat[0:qs, 14:15], in1=neghalf[0:qs, 0:1], op=ALU.pow),
                     reads=[R_stat, R_nh], writes=[R_stat])
                P.op("dve", _call("tensor_scalar", out=hout[0:qs, :], in0=hin[0:qs, :], scalar1=stat[0:qs, 12:13], scalar2=stat[0:qs, 16:17],
                                  op0=ALU.subtract, op1=ALU.mult),
                     reads=[R_hin, R_stat], writes=[R_hout])
                P.op("dve", _call("tensor_tensor", out=hout[0:qs, :], in0=hout[0:qs, :], in1=lnt[0:qs, gcol * 1024:(gcol + 1) * 1024], op=ALU.mult),
                     reads=[R_hout, R_ln], writes=[R_hout])
                P.op("dve", _call("tensor_tensor", out=hout[0:qs, :], in0=hout[0:qs, :], in1=lnt[0:qs, (gcol + 1) * 1024:(gcol + 2) * 1024], op=ALU.add),
                     reads=[R_hout, R_ln], writes=[R_hout])

            def phaseB_block(blk, qs, row0, mi, k2):
                s = k2
                tT, hA, hB, h16, qmT, PTm, o16, oT, stat = (tT_l[k2], hA_l[k2], hB_l[k2], h16_l[k2], qmT_l[k2], PTm_l[k2], o16_l[k2],
                                                             oT_l[k2], stat_l[k2])
                R_tT, R_hA, R_hB, R_h16, R_qm, R_PTm, R_o16, R_oT, R_stat = (RB[k2][n] for n in ("tT", "hA", "hB", "h16", "qm", "PTm", "o16", "oT", "stat"))
                P.dma("sp", mixl[s][0:qs, :], mixD[blk * 128 + row0: blk * 128 + row0 + qs, :], reads=[R_mixD[blk]], writes=[R_mixl[s]])
                P.dma("sp", xr[s][0:qs, :], I["xres"][blk * 128: blk * 128 + qs, :], writes=[R_xr[s]])
                transpose_to(mixl[s], R_mixl[s], qs, 8, tT, R_tT)
                yield
                b0, b1 = wrot.next(), wrot.next()
                for n, bank in enumerate((b0, b1)):
                    for kc in range(KC):
                        P.op("pe", _call("matmul",
                            out=pb[bank][0:qs, :], lhsT=tT[:, kc * qs:(kc + 1) * qs], rhs=wob[:, kc * 1024 + n * 512: kc * 1024 + (n + 1) * 512],
                            start=(kc == 0), stop=(kc == KC - 1)),
                            reads=[R_tT, R_wo], writes=[R_pb[bank]])
                    P.op("dve", _call("scalar_tensor_tensor",
                        out=hA[0:qs, n * 512:(n + 1) * 512], in0=xr[s][0:qs, n * 512:(n + 1) * 512], scalar=ALPHA, in1=pb[bank][0:qs, :],
                        op0=ALU.mult, op1=ALU.add),
                        reads=[R_xr[s], R_pb[bank]], writes=[R_hA])
                yield
                layer_norm(hA, R_hA, qs, 0, hB, R_hB, stat, R_stat)
                yield
                P.op("act", _call("activation", out=h16[0:qs, :], in_=hB[0:qs, :], func=AF.Copy), reads=[R_hB], writes=[R_h16])
                transpose_to(h16, R_h16, qs, 8, tT, R_tT)
                yield
                bq = wrot.next()
                for h in range(4):
                    for kc in range(KC):
                        P.op("pe", _call("matmul",
                            out=pb[bq][:, h * qs:(h + 1) * qs], lhsT=wmqb[:, kc * 512 + h * 128: kc * 512 + (h + 1) * 128],
                            rhs=tT[:, kc * qs:(kc + 1) * qs], start=(kc == 0), stop=(kc == KC - 1)),
                            reads=[R_wmq, R_tT], writes=[R_pb[bq]])
                P.op("act", _call("activation", out=qmT[:, 0:4 * qs], in_=pb[bq][:, 0:4 * qs], func=AF.Copy, scale=float(128.0 ** -0.5)),
                     reads=[R_pb[bq]], writes=[R_qm])
                yield
                bs0, bs1 = wrot.next(), wrot.next()
                for h in range(4):
                    for mt in range(2):
                        idx = h * 2 + mt
                        bank = bs0 if idx < 4 else bs1
                        c0 = (idx % 4) * qs
                        P.op("pe", _call("matmul",
                            out=pb[bank][:, c0:c0 + qs], lhsT=mkT[mi][:, h * 256 + mt * 128: h * 256 + (mt + 1) * 128],
                            rhs=qmT[:, h * qs:(h + 1) * qs], start=True, stop=True),
                            reads=[R_mk[mi], R_qm], writes=[R_pb[bank]])
                for k, bank in enumerate((bs0, bs1)):
                    P.op("act", _call("activation", out=PTm[:, k * 4 * qs:(k + 1) * 4 * qs], in_=pb[bank][:, 0:4 * qs], func=AF.Exp),
                         reads=[R_pb[bank]], writes=[R_PTm])
                yield
                bo0, bo1 = wrot.next(), wrot.next()
                for h in range(4):
                    bank = bo0 if h < 2 else bo1
                    for mt in range(2):
                        idx = h * 2 + mt
                        P.op("pe", _call("matmul",
                            out=pb[bank][0:qs, (h % 2) * 129:(h % 2) * 129 + 129], lhsT=PTm[:, idx * qs:(idx + 1) * qs],
                            rhs=mva[mi][:, (mt * 4 + h) * 129:(mt * 4 + h) * 129 + 129],
                            start=(h % 2 == 0 and mt == 0), stop=(mt == 1), skip_group_check=True),
                            reads=[R_PTm, R_mv[mi]], writes=[R_pb[bank]])
                for k, bank in enumerate((bo0, bo1)):
                    ov = pb[bank][0:qs, 0:258].rearrange("p (h d) -> p h d", d=129)
                    P.op("dve", _call("tensor_scalar", out=stat[0:qs, 20 + 2 * k:22 + 2 * k].rearrange("p (h o) -> p h o", o=1),
                                                                      in0=ov[:, :, 128:129], scalar1=1e-30, scalar2=None, op0=ALU.max),
                         reads=[R_pb[bank]], writes=[R_stat])
                    P.op("dve", _call("reciprocal", out=stat[0:qs, 20 + 2 * k:22 + 2 * k], in_=stat[0:qs, 20 + 2 * k:22 + 2 * k]),
                         reads=[R_stat], writes=[R_stat])
                    for hh in range(2):
                        h = k * 2 + hh
                        P.op("dve", _call("tensor_scalar",
                            out=o16[0:qs, h * 128:(h + 1) * 128], in0=pb[bank][0:qs, hh * 129: hh * 129 + 128],
                            scalar1=stat[0:qs, 20 + 2 * k + hh:21 + 2 * k + hh], scalar2=None, op0=ALU.mult),
                            reads=[R_pb[bank], R_stat], writes=[R_o16])
                yield
                transpose_to(o16, R_o16, qs, 4, oT, R_oT)
                yield
                b0, b1 = wrot.next(), wrot.next()
                for n, bank in enumerate((b0, b1)):
                    for c in range(4):
                        P.op("pe", _call("matmul",
                            out=pb[bank][0:qs, :], lhsT=oT[:, c * qs:(c + 1) * qs], rhs=wmob[:, c * 1024 + n * 512: c * 1024 + (n + 1) * 512],
                            start=(c == 0), stop=(c == 3)),
                            reads=[R_oT, R_wmo], writes=[R_pb[bank]])
                    P.op("dve", _call("scalar_tensor_tensor",
                        out=hA[0:qs, n * 512:(n + 1) * 512], in0=hB[0:qs, n * 512:(n + 1) * 512], scalar=ALPHA, in1=pb[bank][0:qs, :],
                        op0=ALU.mult, op1=ALU.add),
                        reads=[R_hB, R_pb[bank]], writes=[R_hA])
                yield
                layer_norm(hA, R_hA, qs, 2, hB, R_hB, stat, R_stat)
                yield
                P.dma("sp", h2D[blk * 128: blk * 128 + qs, :], hB[0:qs, :], reads=[R_hB], writes=[R_h2D[blk]], defer=True)
                P.op("act", _call("activation", out=h16[0:qs, :], in_=hB[0:qs, :], func=AF.Copy), reads=[R_hB], writes=[R_h16])
                transpose_to(h16, R_h16, qs, 8, h2T[s], R_h2T[s])
                P.dma("sp", h2TD[blk][:, 0:8 * qs], h2T[s][:, 0:8 * qs], reads=[R_h2T[s]], writes=[R_h2TD[blk]], defer=True)
                yield

            def run_staggered(gens, lag):
                active = []
                pending = list(gens)
                tick = 0
                while active or pending:
                    if pending and (not active or tick >= lag):
                        active.append(pending.pop(0))
                        tick = 0
                    for g in list(active):
                        try:
                            next(g)
                        except StopIteration:
                            active.remove(g)
                    tick += 1

            blocks = [(16, 2, 126, 0), (17, 16, 0, 1)] + [(i, 128, 0, 0) for i in range(16)]
            run_staggered([phaseB_block(b_, q_, r_, m_, pos % 4) for pos, (b_, q_, r_, m_) in enumerate(blocks)], 3)
            checkpoint('phaseB')
            P.flush(block)

        P.barrier()
        with ExitStack() as sc:
            wdb = sb(sc, "wdb", [128, NFC * 1024], BF16)
            R_wd = Res("wd")
            wst = [sb(sc, "wstc%d" % k, [128, 2048], F32) for k in range(2)]
            R_wst = [Res("wstc%d" % k) for k in range(2)]
            wsl = [sb(sc, "wsl%d" % k, [128, 2048], BF16) for k in range(2)]
            R_wsl = [Res("wsl%d" % k) for k in range(2)]
            R_wslB = [Res("wslB%d" % k) for k in range(2)]
            hT2 = [sb(sc, "hT%d" % k, [128, NFC * 512], BF16) for k in range(2)]
            R_hT2 = [Res("hT%d" % k) for k in range(2)]
            hTm = sb(sc, "hTm", [128, NFC * 16], BF16)
            R_hTm = Res("hTm")
            h2Tg = [sb(sc, "h2Tg%d" % k, [128, 8 * 512], BF16) for k in range(2)]
            R_h2Tg = [Res("h2Tg%d" % k) for k in range(2)]
            h2Tm = sb(sc, "h2Tm", [128, 8 * 18], BF16)
            R_h2Tm = Res("h2Tm")
            Gb = [sb(sc, "Gb%d" % k, [128, 532], F32) for k in range(3)]
            R_Gb = [Res("Gb%d" % k) for k in range(3)]
            Gs = sb(sc, "Gs", [128, 18], F32)
            R_Gs = Res("Gs")
            t0b = [sb(sc, "t0b%d" % k, [128, 530], F32) for k in range(3)]
            R_t0 = [Res("t0%d" % k) for k in range(3)]
            geb = [sb(sc, "geb%d" % k, [128, 530], F32) for k in range(3)]
            R_ge = [Res("ge%d" % k) for k in range(3)]
            t1b = [sb(sc, "t1b%d" % k, [128, 530], F32) for k in range(3)]
            R_t1b = [Res("t1b%d" % k) for k in range(3)]
            t2b = [sb(sc, "t2b%d" % k, [128, 530], F32) for k in range(3)]
            R_t2b = [Res("t2b%d" % k) for k in range(3)]
            t0s = sb(sc, "t0s", [128, 16], F32)
            ges = sb(sc, "ges", [128, 16], F32)
            R_ts = Res("ts")
            carry = sb(sc, "carry", [128, NFC * 2], F32)
            R_carry = [Res("carry%d" % c) for c in range(NFC)]
            sfc = sb(sc, "sfc", [128, NFC * 2], F32)
            R_sfc = Res("sfc")
            sconv = sb(sc, "sconv", [128, NFC * 2], F32)
            wconv = sb(sc, "wconv", [128, NFC * 3], F32)
            bconv = sb(sc, "bconv", [128, NFC], F32)
            flag = sb(sc, "flag", [128, 1], F32)
            R_cc = Res("cc")
            ln3 = sb(sc, "ln3", [128, 2 * 1024], F32)
            R_ln3 = Res("ln3")
            h2r = [sb(sc, "h2r%d" % k, [128, 1024], F32) for k in range(2)]
            R_h2r = [Res("h2r%d" % k) for k in range(2)]
            yA = sb(sc, "yA", [128, 1024], F32)
            R_yA = Res("yA")
            yB = [sb(sc, "yB%d" % k, [128, 1024], F32) for k in range(2)]
            R_yB = [Res("yB%d" % k) for k in range(2)]
            stat = sb(sc, "statc", [128, 32], F32)
            R_stat = Res("statc")

            P.dma("sp", sconv[:, :], I["sconvT"][:, :], writes=[R_cc])
            P.dma("sp", wconv[:, :], I["wconvT"][:, :], writes=[R_cc])
            P.dma("sp", bconv[:, :], I["bconvT"][:, :], writes=[R_cc])
            P.dma("sp", flag[:, :], I["flag"][:, :], writes=[R_cc])
            for k in range(2):
                P.dma("sp", ln3[:, k * 1024:(k + 1) * 1024], I["lnp"][4 + k:5 + k, :].to_broadcast([128, 1024]), writes=[R_ln3])
            def wdown_piece(j):
                kq = j % 2
                P.dma("sp", yB[kq][:, :], I["wdown"][:, j * 1024:(j + 1) * 1024], writes=[R_yB[kq]])
                P.op("dve", _call("tensor_copy", out=wdb[:, j * 1024:(j + 1) * 1024], in_=yB[kq][:, :]), reads=[R_yB[kq]], writes=[R_wd])

            P.dma("sp", h2Tm[:, :].rearrange("p (c q) -> p c q", q=18)[:, :, 0:2], h2TD[16][:, 0:16].rearrange("p (c q) -> p c q", q=2),
                  reads=[R_h2TD[16]], writes=[R_h2Tm], slow=True)
            P.dma("sp", h2Tm[:, :].rearrange("p (c q) -> p c q", q=18)[:, :, 2:18], h2TD[17][:, 0:128].rearrange("p (c q) -> p c q", q=16),
                  reads=[R_h2TD[17]], writes=[R_h2Tm], slow=True)

            checkpoint('phaseC_pre')
            UB = [0, 2, 4]
            GBK = [1, 3, 5]
            MB = 7
            YB = [6, 7]
            wk = [0]

            def ln3_out(pre_banks, qs, h2src, R_h2src, dst_ap, ys, R_ys):
                for n, bank in enumerate(pre_banks):
                    P.op("dve", _call("scalar_tensor_tensor",
                        out=yA[0:qs, n * 512:(n + 1) * 512], in0=h2src[0:qs, n * 512:(n + 1) * 512], scalar=ALPHA, in1=pb[bank][0:qs, :],
                        op0=ALU.mult, op1=ALU.add),
                        reads=[R_h2src, R_pb[bank]], writes=[R_yA])
                for c in range(2):
                    P.op("dve", _call("bn_stats", out=stat[0:qs, c * 6:(c + 1) * 6], in_=yA[0:qs, c * 512:(c + 1) * 512]),
                         reads=[R_yA], writes=[R_stat])
                P.op("dve", _call("bn_aggr", out=stat[0:qs, 12:14], in_=stat[0:qs, 0:12]), reads=[R_stat], writes=[R_stat])
                P.op("dve", _call("tensor_scalar", out=stat[0:qs, 14:15], in0=stat[0:qs, 13:14], scalar1=LN_EPS, scalar2=None, op0=ALU.add),
                     reads=[R_stat], writes=[R_stat])
                P.op("act", _call("activation", out=stat[0:qs, 15:16], in_=stat[0:qs, 14:15], func=AF.Sqrt), reads=[R_stat], writes=[R_stat])
                P.op("dve", _call("reciprocal", out=stat[0:qs, 16:17], in_=stat[0:qs, 15:16]), reads=[R_stat], writes=[R_stat])
                P.op("dve", _call("scalar_tensor_tensor", out=stat[0:qs, 17:18], in0=stat[0:qs, 12:13], scalar=-1.0, in1=stat[0:qs, 16:17],
                                                             op0=ALU.mult, op1=ALU.mult),
                     reads=[R_stat], writes=[R_stat])
                P.op("act", _call("activation", out=ys[0:qs, :], in_=yA[0:qs, :], func=AF.Identity, scale=stat[0:qs, 16:17], bias=stat[0:qs, 17:18]),
                     reads=[R_yA, R_stat], writes=[R_ys])
                P.op("pool", _call("tensor_tensor", out=ys[0:qs, :], in0=ys[0:qs, :], in1=ln3[0:qs, 0:1024], op=ALU.mult),
                     reads=[R_ys, R_ln3], writes=[R_ys])
                P.op("pool", _call("tensor_tensor", out=ys[0:qs, :], in0=ys[0:qs, :], in1=ln3[0:qs, 1024:2048], op=ALU.add),
                     reads=[R_ys, R_ln3], writes=[R_ys])
                P.dma("sp", dst_ap, ys[0:qs, :], reads=[R_ys], defer=True)

            def load_h2Tg(grp):
                gs = grp % 2
                for bi in range(4):
                    blk = grp * 4 + bi
                    P.dma("sp", h2Tg[gs][:, :].rearrange("p (c q) -> p c q", q=512)[:, :, bi * 128:(bi + 1) * 128],
                          h2TD[blk][:, :].rearrange("p (c q) -> p c q", q=128), reads=[R_h2TD[blk]], writes=[R_h2Tg[gs]])

            def c_s1(grp, c):
                s = (grp * NFC + c) % 2
                P.dma("sp", wst[s][:, :], I["wup"][c], writes=[R_wst[s]])
                P.op("dve", _call("tensor_copy", out=wsl[s][:, 0:1024], in_=wst[s][:, 0:1024]), reads=[R_wst[s]], writes=[R_wsl[s]])
                P.op("dve", _call("tensor_copy", out=wsl[s][:, 1024:2048], in_=wst[s][:, 1024:2048]), reads=[R_wst[s]], writes=[R_wslB[s]])

            def c_s2(grp, c):
                s = (grp * NFC + c) % 2
                gs = grp % 2
                mo = (c % 3) * 64
                if grp == 0:
                    for part, oc in ((0, mo), (1, mo + 32)):
                        for kc in range(KC):
                            P.op("pe", _call("matmul", out=pb[MB][:, oc:oc + 18], lhsT=wsl[s][:, kc * 256 + part * 128: kc * 256 + (part + 1) * 128],
                                             rhs=h2Tm[:, kc * 18:(kc + 1) * 18], start=(kc == 0), stop=(kc == KC - 1)),
                                 reads=[R_wsl[s], R_wslB[s], R_h2Tm], writes=[R_pb[MB]])
                k3 = (grp * NFC + c) % 3
                ub, gbk = UB[k3], GBK[k3]
                for part, bank in ((0, ub), (1, gbk)):
                    for kc in range(KC):
                        P.op("pe", _call("matmul", out=pb[bank][:, :], lhsT=wsl[s][:, kc * 256 + part * 128: kc * 256 + (part + 1) * 128],
                                         rhs=h2Tg[gs][:, kc * 512:(kc + 1) * 512], start=(kc == 0), stop=(kc == KC - 1)),
                             reads=[R_wsl[s], R_wslB[s], R_h2Tg[gs]], writes=[R_pb[bank]])

            def c_s3(grp, c):
                hTg, R_hTg = hT2[grp % 2], R_hT2[grp % 2]
                mo = (c % 3) * 64
                k3 = (grp * NFC + c) % 3
                ub, gbk = UB[k3], GBK[k3]
                G, R_G = Gb[k3], R_Gb[k3]
                t0, R_t = t0b[k3], R_t0[k3]
                ge, R_g = geb[k3], R_ge[k3]
                t1, R_t1 = t1b[k3], R_t1b[k3]
                t2, R_t2 = t2b[k3], R_t2b[k3]
                W = 530 if grp == 0 else 512
                if grp == 0:
                    P.op("dve", _call("tensor_scalar", out=carry[:, c * 2:(c + 1) * 2], in0=pb[MB][:, mo + 32:mo + 34], scalar1=flag[:, 0:1],
                                      scalar2=None, op0=ALU.mult),
                         reads=[R_pb[MB], R_cc], writes=[R_carry[c]])
                P.op("act", _call("activation", out=G[:, 0:2], in_=carry[:, c * 2:(c + 1) * 2], func=AF.Copy),
                     reads=[R_carry[c]], writes=[R_G])
                P.op("act", _call("activation", out=G[:, 2:514], in_=pb[gbk][:, :], func=AF.Copy), reads=[R_pb[gbk]], writes=[R_G])
                if grp == 0:
                    P.op("act", _call("activation", out=G[:, 514:516], in_=sconv[:, c * 2:(c + 1) * 2], func=AF.Copy), reads=[R_cc], writes=[R_G])
                    P.op("act", _call("activation", out=G[:, 516:532], in_=pb[MB][:, mo + 34:mo + 50], func=AF.Copy), reads=[R_pb[MB]], writes=[R_G])
                    P.op("act", _call("activation", out=sfc[:, c * 2:(c + 1) * 2], in_=G[:, 530:532], func=AF.Copy), reads=[R_G], writes=[R_sfc])
                P.op("act", _call("activation", out=carry[:, c * 2:(c + 1) * 2], in_=G[:, 512:514], func=AF.Copy),
                     reads=[R_G], writes=[R_carry[c]])
                P.op("act", _call("activation", out=t0[:, 0:W], in_=G[:, 2:2 + W], func=AF.Identity,
                                  scale=wconv[:, c * 3 + 2:c * 3 + 3], bias=bconv[:, c:c + 1]),
                     reads=[R_G, R_cc], writes=[R_t])
                P.op("act", _call("activation", out=t1[:, 0:W], in_=G[:, 1:1 + W], func=AF.Identity, scale=wconv[:, c * 3 + 1:c * 3 + 2]),
                     reads=[R_G, R_cc], writes=[R_t1])
                P.op("act", _call("activation", out=t2[:, 0:W], in_=G[:, 0:W], func=AF.Identity, scale=wconv[:, c * 3:c * 3 + 1]),
                     reads=[R_G, R_cc], writes=[R_t2])
                P.op("dve", _call("tensor_tensor", out=t0[:, 0:W], in0=t0[:, 0:W], in1=t1[:, 0:W], op=ALU.add), reads=[R_t, R_t1], writes=[R_t])
                P.op("dve", _call("tensor_tensor", out=t0[:, 0:W], in0=t0[:, 0:W], in1=t2[:, 0:W], op=ALU.add), reads=[R_t, R_t2], writes=[R_t])

            def c_s3b(grp, c):
                hTg, R_hTg = hT2[grp % 2], R_hT2[grp % 2]
                mo = (c % 3) * 64
                k3 = (grp * NFC + c) % 3
                ub = UB[k3]
                t0, R_t = t0b[k3], R_t0[k3]
                ge, R_g = geb[k3], R_ge[k3]
                W = 530 if grp == 0 else 512
                P.op("act", _call("activation", out=ge[:, 0:W], in_=t0[:, 0:W], func=AF.Gelu_apprx_tanh), reads=[R_t], writes=[R_g])
                P.op("dve", _call("tensor_tensor", out=hTg[:, c * 512:(c + 1) * 512], in0=pb[ub][:, :], in1=ge[:, 0:512], op=ALU.mult),
                     reads=[R_pb[ub], R_g], writes=[R_hTg])
                if grp == 0:
                    P.op("dve", _call("tensor_tensor", out=hTm[:, c * 16:(c + 1) * 16], in0=pb[MB][:, mo + 2:mo + 18], in1=ge[:, 514:530], op=ALU.mult),
                         reads=[R_pb[MB], R_g], writes=[R_hTm])

            def c_down(grp):
                hTg, R_hTg = hT2[grp % 2], R_hT2[grp % 2]
                if grp == 0:
                    for n, bank in enumerate(YB):
                        for c in range(NFC):
                            P.op("pe", _call("matmul", out=pb[bank][0:16, :], lhsT=hTm[:, c * 16:(c + 1) * 16],
                                             rhs=wdb[:, c * 1024 + n * 512: c * 1024 + (n + 1) * 512], start=(c == 0), stop=(c == NFC - 1)),
                                 reads=[R_hTm, R_wd], writes=[R_pb[bank]])
                    P.dma("sp", h2r[0][0:16, :], h2D[17 * 128: 17 * 128 + 16, :], reads=[R_h2D[17]], writes=[R_h2r[0]])
                    ln3_out(YB, 16, h2r[0], R_h2r[0], O["ys"][:, :], yB[0], R_yB[0])
                    P.dma("sp", O["sfcT"][:, :], sfc[:, :], reads=[R_sfc], defer=True)
                for bi in range(4):
                    blk = grp * 4 + bi
                    hs = blk % 2
                    P.dma("sp", h2r[hs][:, :], h2D[blk * 128:(blk + 1) * 128, :], reads=[R_h2D[blk]], writes=[R_h2r[hs]])
                    for n, bank in enumerate(YB):
                        for c in range(NFC):
                            P.op("pe", _call("matmul", out=pb[bank][:, :], lhsT=hTg[:, c * 512 + bi * 128: c * 512 + (bi + 1) * 128],
                                             rhs=wdb[:, c * 1024 + n * 512: c * 1024 + (n + 1) * 512], start=(c == 0), stop=(c == NFC - 1)),
                                 reads=[R_hTg, R_wd], writes=[R_pb[bank]])
                    ln3_out(YB, 128, h2r[hs], R_h2r[hs], O["y"][blk * 128:(blk + 1) * 128, :], yB[hs], R_yB[hs])

            seq = [(grp, c) for grp in range(4) for c in range(NFC)]
            nseq = len(seq)
            load_h2Tg(0)
            load_h2Tg(1)
            for idx in range(nseq + 3):
                if 1 <= idx <= NFC:
                    wdown_piece(idx - 1)
                if idx < nseq:
                    c_s1(*seq[idx])
                if 1 <= idx <= nseq:
                    c_s2(*seq[idx - 1])
                if 3 <= idx:
                    g4, c4 = seq[idx - 3]
                    c_s3b(g4, c4)
                    if c4 == NFC - 1:
                        c_down(g4)
                        if g4 + 2 < 4:
                            load_h2Tg(g4 + 2)
                if 2 <= idx <= nseq + 1:
                    c_s3(*seq[idx - 2])
            P.dma("sp", O["fcT"][:, :], carry[:, :], reads=R_carry, defer=True)
            P.finish()
            P.flush(block)
    return nc


def _t5_bucket(rel):
    half, max_exact = 16, 8
    n = np.abs(rel)
    log_ratio = np.log(np.maximum(n, 1).astype(np.float32) / max_exact) / math.log(128 / max_exact)
    large = np.minimum(max_exact + (log_ratio * (half - max_exact)).astype(np.int32), half - 1)
    return np.where(rel < 0, half, 0) + np.where(n < max_exact, n, large)


def _host_inputs(inp):
    f32 = np.float32
    x_prompt = np.asarray(inp["x_prompt"], f32)
    x_sample = np.asarray(inp["x_sample"], f32)
    w_in = np.asarray(inp["w_in"], f32)[0]
    qa, ka, va = w_in[:, 0:512], w_in[:, 512:1024], w_in[:, 1024:1536]
    qb, kb, vb = w_in[:, 1536:2048], w_in[:, 2048:2176], w_in[:, 2176:2304]
    qi, ki, wi = w_in[:, 2304:2816], w_in[:, 2816:2880], w_in[:, 2880:2888]
    qbp = np.concatenate([np.concatenate([qb[:, r * 64:(r + 1) * 64], qb[:, (4 + r) * 64:(5 + r) * 64]], axis=1) for r in range(4)], axis=1)
    winp = np.concatenate([qa, ka, qbp, kb, qi, ki, ki, va, vb, wi], axis=1)
    assert winp.shape[1] == NCOL

    def kc_layout(w):
        n = w.shape[1]
        return np.ascontiguousarray(w.reshape(8, 128, n).transpose(1, 0, 2).reshape(128, 8 * n))

    shared = {}
    shared["win"] = kc_layout(winp)
    shared["wo"] = kc_layout(np.asarray(inp["w_o"], f32)[0])
    shared["wmq"] = kc_layout(np.asarray(inp["w_mq"], f32)[0])
    shared["wmk"] = kc_layout(np.asarray(inp["w_mk"], f32)[0])
    shared["wmv"] = kc_layout(np.asarray(inp["w_mv"], f32)[0])
    wmo = np.asarray(inp["w_mo"], f32)[0]
    shared["wmo"] = np.ascontiguousarray(wmo.reshape(4, 128, 1024).transpose(1, 0, 2).reshape(128, 4096))
    w_up = np.asarray(inp["w_up"], f32)[0]
    wu = w_up[:, :DFF].reshape(8, 128, NFC, 128)
    wg = w_up[:, DFF:].reshape(8, 128, NFC, 128)
    wup = np.stack([wu, wg], axis=3)
    shared["wup"] = np.ascontiguousarray(wup.transpose(2, 1, 0, 3, 4).reshape(NFC, 128, 8 * 256))
    w_down = np.asarray(inp["w_down"], f32)[0]
    shared["wdown"] = np.ascontiguousarray(w_down.reshape(NFC, 128, 1024).transpose(1, 0, 2).reshape(128, NFC * 1024))
    shared["lnp"] = np.ascontiguousarray(np.stack([np.asarray(inp[k], f32)[0] for k in ("ln1_g", "ln1_b", "ln2_g", "ln2_b", "ln3_g", "ln3_b")]))
    w_conv = np.asarray(inp["w_conv"], f32)[0]
    shared["wconvT"] = np.ascontiguousarray(w_conv.reshape(3, NFC, 128).transpose(2, 1, 0).reshape(128, NFC * 3))
    shared["bconvT"] = np.ascontiguousarray(np.asarray(inp["b_conv"], f32)[0].reshape(NFC, 128).T)
    shared["ident"] = np.eye(128, dtype=f32)
    tabA = np.asarray(inp["a_rel_bias"], f32)[0]
    qq = np.arange(128)[:, None]
    kk = np.arange(640)[None, :]
    kpos = kk - 512
    rel = qq - kpos
    cq = qq // 64
    kch = np.floor_divide(kpos, 64)
    allowed = (kch >= cq - 8) & (kch <= cq)
    bias = tabA[np.clip(rel, -64, 64) + 64]
    AB = np.where(allowed[:, :, None], bias, f32(NEGM)).astype(f32)
    shared["AB"] = np.ascontiguousarray(AB.transpose(0, 2, 1).reshape(128, 8 * ABW))
    js = np.arange(16)[:, None]
    ks = np.arange(528)[None, :]
    ABs = tabA[np.clip(512 + js - ks, -64, 64) + 64]
    shared["ABs"] = np.ascontiguousarray(ABs.transpose(0, 2, 1).reshape(16, 8 * 528)).astype(f32)
    t5 = np.asarray(inp["t5_bias"], f32)
    relB = np.arange(128)[:, None] - np.arange(256)[None, :] + 128
    Bn = t5[_t5_bucket(relB)]
    shared["Bn"] = np.ascontiguousarray(Bn.transpose(0, 2, 1).reshape(128, 8 * BNW)).astype(f32)
    relBs = 128 + np.arange(16)[:, None] - np.arange(144)[None, :]
    Bns = t5[_t5_bucket(relBs)]
    shared["Bns"] = np.ascontiguousarray(Bns.transpose(0, 2, 1).reshape(16, 8 * 144)).astype(f32)
    shared["C15"] = np.ascontiguousarray(np.broadcast_to(t5[15][None, :], (128, 8))).astype(f32)
    dm = np.zeros((128, 128), f32)
    dm[0:64, 64:128] = NEGM
    shared["diagmask"] = dm

    mem_prompt = np.asarray(inp["mem_prompt"], f32)
    maps = []
    for c in range(8):
        b, half = c // 2, c % 2
        m = dict(shared)
        xk = np.zeros((4096, 1024), f32)
        if half == 1:
            xk[:] = x_prompt[b]
        else:
            xk[2048:] = x_prompt[b, :2048]
        m["xkT"] = np.ascontiguousarray(xk.reshape(32, 128, 8, 128).transpose(0, 3, 2, 1).reshape(32, 128, 1024))
        xs = x_sample[c]
        m["xsT"] = np.ascontiguousarray(xs.reshape(16, 8, 128).transpose(2, 1, 0).reshape(128, 128))
        xres = np.zeros((NBLK * 128, 1024), f32)
        xres[0:2048] = xk[2048:]
        xres[2048:2050] = xk[2046:2048]
        xres[17 * 128:17 * 128 + 16] = xs
        m["xres"] = xres
        m["memT"] = np.ascontiguousarray(mem_prompt[b].reshape(256, 8, 128).transpose(2, 1, 0).reshape(128, 2048))
        cmk = np.asarray(inp["cache_mem_k"], f32)[0, c]
        m["cmkT"] = np.ascontiguousarray(cmk.transpose(2, 1, 0).reshape(128, 1024))
        m["cmv"] = np.ascontiguousarray(np.asarray(inp["cache_mem_v"], f32)[0, c].reshape(256, 512))
        cak = np.asarray(inp["cache_a_k"], f32)[0, c]
        m["cakT"] = np.ascontiguousarray(cak.reshape(512, 4, 2, 64).transpose(2, 3, 1, 0).reshape(128, 2048))
        m["cav"] = np.ascontiguousarray(np.asarray(inp["cache_a_v"], f32)[0, c].reshape(512, 512))
        cbk = np.asarray(inp["cache_b_k"], f32)[0, c]
        m["cbkT"] = np.ascontiguousarray(cbk.reshape(2048, 128).T)
        m["cbv"] = np.ascontiguousarray(np.asarray(inp["cache_b_v"], f32)[0, c].reshape(2048, 128))
        cbi = np.asarray(inp["cache_b_kidx"], f32)[0, c]
        m["cbiT"] = np.ascontiguousarray(np.concatenate([cbi.T, cbi.T], axis=0))
        sc_ = np.asarray(inp["state_ffn_conv"], f32)[0, c]
        m["sconvT"] = np.ascontiguousarray(sc_.reshape(2, NFC, 128).transpose(2, 1, 0).reshape(128, NFC * 2))
        m["colmask"] = np.full((128, 1), NEGM if half == 0 else 0.0, f32)
        kv = np.ones((128, NT), f32)
        if half == 0:
            kv[:, 0:16] = 0.0
        m["kvalid"] = kv
        m["flag"] = np.full((128, 1), float(half), f32)
        maps.append(m)
    return maps


_NC_CACHE = {}


def _run(inputs, debug=False):
    key = bool(debug)
    if key not in _NC_CACHE:
        _NC_CACHE[key] = build_program(debug=debug)
    nc = _NC_CACHE[key]
    maps = _host_inputs(inputs)
    res = run_bass_kernel_spmd(nc, maps, core_ids=list(range(8)))
    return res.results


def kernel(**inputs):
    R = _run(inputs)
    f32 = np.float32
    y = np.zeros((4, 4096, 1024), f32)
    ys = np.zeros((8, 16, 1024), f32)
    pak = np.zeros((1, 4, 512, 8, 64), f32)
    pav = np.zeros((1, 4, 512, 8, 64), f32)
    pbk = np.zeros((1, 4, 4096, 2, 64), f32)
    pbv = np.zeros((1, 4, 4096, 2, 64), f32)
    pbi = np.zeros((1, 4, 4096, 64), f32)
    pmk = np.zeros((1, 4, 256, 4, 128), f32)
    pmv = np.zeros((1, 4, 256, 4, 128), f32)
    pfc = np.zeros((1, 4, 2, DFF), f32)
    sak = np.zeros((1, 8, 16, 8, 64), f32)
    sav = np.zeros((1, 8, 16, 8, 64), f32)
    sbk = np.zeros((1, 8, 16, 2, 64), f32)
    sbv = np.zeros((1, 8, 16, 2, 64), f32)
    sbi = np.zeros((1, 8, 16, 64), f32)
    sfc = np.zeros((1, 8, 2, DFF), f32)
    for c in range(8):
        b, half = c // 2, c % 2
        r = R[c]
        y[b, half * 2048:(half + 1) * 2048] = np.asarray(r["y"], f32)
        ys[c] = np.asarray(r["ys"], f32)
        if half == 1:
            akT = np.asarray(r["akT"], f32).reshape(2, 64, 4, 512)
            pak[0, b] = akT.transpose(3, 2, 0, 1).reshape(512, 8, 64)
            pav[0, b] = np.asarray(r["av"], f32).reshape(512, 8, 64)
            pbk[0, b] = np.asarray(r["bkT"], f32).T.reshape(4096, 2, 64)
            pbv[0, b] = np.asarray(r["bv"], f32).reshape(4096, 2, 64)
            pbi[0, b] = np.asarray(r["biT"], f32).T
            pmk[0, b] = np.asarray(r["mkT"], f32).reshape(128, 4, 256).transpose(2, 1, 0)
            pmv[0, b] = np.asarray(r["mv"], f32).reshape(256, 4, 128)
            pfc[0, b] = np.asarray(r["fcT"], f32).reshape(128, NFC, 2).transpose(2, 1, 0).reshape(2, DFF)
        sakT = np.asarray(r["sakT"], f32).reshape(2, 64, 4, 16)
        sak[0, c] = sakT.transpose(3, 2, 0, 1).reshape(16, 8, 64)
        sav[0, c] = np.asarray(r["sav"], f32).reshape(16, 8, 64)
        sbk[0, c] = np.asarray(r["sbkT"], f32).T.reshape(16, 2, 64)
        sbv[0, c] = np.asarray(r["sbv"], f32).reshape(16, 2, 64)
        sbi[0, c] = np.asarray(r["sbiT"], f32).T
        sfc[0, c] = np.asarray(r["sfcT"], f32).reshape(128, NFC, 2).transpose(2, 1, 0).reshape(2, DFF)
    return (y, ys, pak, pav, pbk, pbv, pbi, pmk, pmv, pfc, sak, sav, sbk, sbv, sbi, sfc)
```
